# Optimizing a Trainium2 kernel written in Bass

```python
import jax, jax.numpy as jnp
from jax import lax
import numpy as np

D_MODEL = 1024
BATCH = 4
SEQ = 4096
DEPTH = 2

CHUNK = 64
N_MIXERS = 2
HEAD_DIM = 64
N_HEADS = D_MODEL // HEAD_DIM
Q_BLOCK = 128
FOX_IN = 4 * D_MODEL + N_HEADS
D_FF = -(-8 * D_MODEL // (3 * 256)) * 256
DECAY_LORA = 64
AAA_LORA = 64
GATE_LORA = 128
N_FOX = (DEPTH + N_MIXERS - 1) // N_MIXERS
N_RWKV = DEPTH // N_MIXERS
DEEPNORM_ALPHA = (2 * DEPTH) ** 0.25
DEEPNORM_BETA = (8 * DEPTH) ** -0.25
LN_EPS = 1e-5
QK_EPS = 1e-6
GN_EPS = HEAD_DIM * 1e-5
FORGET_BIAS_INIT = 2.0

kernel_name = "fox_rwkv7_deepnorm_adaln_trunk"


def layer_norm(x, g, b):
    xf = x.astype(jnp.float32)
    mu = jnp.mean(xf, axis=-1, keepdims=True)
    var = jnp.mean(jnp.square(xf - mu), axis=-1, keepdims=True)
    return ((xf - mu) * lax.rsqrt(var + LN_EPS) * g + b).astype(x.dtype)


def rms_norm(x, g):
    xf = x.astype(jnp.float32)
    return (xf * lax.rsqrt(jnp.mean(jnp.square(xf), axis=-1, keepdims=True) + QK_EPS) * g).astype(x.dtype)


def fox_block_attention(q, k, v, F):
    T = q.shape[2]
    scale = HEAD_DIM ** -0.5
    outs = []
    for blk in range(T // Q_BLOCK):
        q0 = blk * Q_BLOCK
        L = q0 + Q_BLOCK
        s = jnp.einsum('bhqd,bhkd->bhqk', q[:, :, q0:L], k[:, :, :L]).astype(jnp.float32) * scale
        s = s + F[:, :, q0:L, None] - F[:, :, None, :L]
        causal = jnp.arange(L)[None, :] <= (q0 + jnp.arange(Q_BLOCK))[:, None]
        p = jax.nn.softmax(jnp.where(causal, s, -jnp.inf), axis=-1)
        outs.append(jnp.einsum('bhqk,bhkd->bhqd', p.astype(v.dtype), v[:, :, :L]))
    return jnp.concatenate(outs, axis=2)


def fox_mixer(h, w_in, b_f, q_g, k_g, w_o):
    B, T, D = h.shape
    proj = h @ w_in
    q, k, v, f_logit, o = jnp.split(proj, [D, 2 * D, 3 * D, 3 * D + N_HEADS], axis=-1)
    heads = lambda t: t.reshape(B, T, N_HEADS, HEAD_DIM).transpose(0, 2, 1, 3)
    q = rms_norm(heads(q), q_g)
    k = rms_norm(heads(k), k_g)
    v = heads(v)
    log_f = jax.nn.log_sigmoid(f_logit.astype(jnp.float32) + b_f)
    F = jnp.cumsum(log_f, axis=1).transpose(0, 2, 1)
    y = fox_block_attention(q, k, v, F)
    y = y.transpose(0, 2, 1, 3).reshape(B, T, D)
    return (y * jax.nn.sigmoid(o)) @ w_o


def rwkv7_step(S, inp):
    r, w, k, v, kk, a = inp
    sa = jnp.einsum('bhvk,bhk->bhv', S, -kk)
    S = S * w[:, :, None, :] + sa[..., None] * (kk * a)[:, :, None, :] + v[..., None] * k[:, :, None, :]
    return S, jnp.einsum('bhvk,bhk->bhv', S, r)


def rwkv7_mixer(h, mu, w_rkv, w0, w1, w2, a0, a1, a2, g1, g2, k_k, k_a, r_k, lnx_g, lnx_b, w_o):
    B, T, D = h.shape
    f32 = jnp.float32
    dx = jnp.pad(h, ((0, 0), (1, 0), (0, 0)))[:, :-1] - h
    xs = h[:, :, None, :] + dx[:, :, None, :] * mu
    rkv = jnp.einsum('btnd,nde->btne', xs[:, :, :3], w_rkv).astype(f32)
    r, k, v = rkv[:, :, 0], rkv[:, :, 1], rkv[:, :, 2]
    xw, xa, xg = xs[:, :, 3], xs[:, :, 4], xs[:, :, 5]
    w_log = -jax.nn.softplus(-(w0 + jnp.tanh(xw @ w1) @ w2).astype(f32)) - 0.5
    decay = jnp.exp(-jnp.exp(w_log))
    a = jax.nn.sigmoid((a0 + (xa @ a1) @ a2).astype(f32))
    g = (jax.nn.sigmoid(xg @ g1) @ g2).astype(f32)
    kk = k * k_k
    k = k * (1.0 + (a - 1.0) * k_a)
    heads = lambda t: t.reshape(B, T, N_HEADS, HEAD_DIM)
    r, k, v, decay, a, kk = heads(r), heads(k), heads(v), heads(decay), heads(a), heads(kk)
    kk = kk / jnp.maximum(jnp.linalg.norm(kk, axis=-1, keepdims=True), 1e-12)
    tm = lambda t: jnp.moveaxis(t, 1, 0)
    S0 = jnp.zeros((B, N_HEADS, HEAD_DIM, HEAD_DIM), f32)
    _, y = lax.scan(rwkv7_step, S0, (tm(r), tm(decay), tm(k), tm(v), tm(kk), tm(a)))
    y = jnp.moveaxis(y, 0, 1)
    mean = jnp.mean(y, axis=-1, keepdims=True)
    var = jnp.mean(jnp.square(y - mean), axis=-1, keepdims=True)
    y = ((y - mean) * lax.rsqrt(var + GN_EPS)).reshape(B, T, D) * lnx_g + lnx_b
    bonus = jnp.sum(r * k * r_k, axis=-1, keepdims=True) * v
    y = (y + bonus.reshape(B, T, D)) * g
    return y.astype(h.dtype) @ w_o


def swiglu(h, w_in, w_out):
    gate, up = jnp.split(h @ w_in, 2, axis=-1)
    return (jax.nn.silu(gate) * up) @ w_out


def setup_inputs(seed: int = 0) -> dict:
    key = jax.random.key(seed)
    ks = iter(jax.random.split(key, 40))
    f32 = jnp.float32
    nrm = lambda shape, s: jax.random.normal(next(ks), shape, f32) * s
    D = D_MODEL
    fan = D ** -0.5
    fox_col_scale = jnp.ones((FOX_IN,), f32).at[2 * D:3 * D].set(DEEPNORM_BETA)
    rkv_scale = jnp.array([1.0, 1.0, DEEPNORM_BETA], f32).reshape(1, 3, 1, 1)
    return {
        "x": nrm((BATCH, SEQ, D), 1.0),
        "c": nrm((BATCH, D), 1.0),
        "ada_w": nrm((DEPTH, D, 6 * D), 0.5 * fan),
        "ada_b": nrm((DEPTH, 6 * D), 0.02),
        "ln_g": 1.0 + nrm((DEPTH, 2, D), 0.02),
        "ln_b": nrm((DEPTH, 2, D), 0.02),
        "ffn_w_in": nrm((DEPTH, D, 2 * D_FF), fan),
        "ffn_w_out": nrm((DEPTH, D_FF, D), D_FF ** -0.5 * DEEPNORM_BETA),
        "fox_w_in": nrm((N_FOX, D, FOX_IN), fan) * fox_col_scale,
        "fox_b_f": FORGET_BIAS_INIT + nrm((N_FOX, N_HEADS), 0.1),
        "fox_q_g": 1.0 + nrm((N_FOX, HEAD_DIM), 0.02),
        "fox_k_g": 1.0 + nrm((N_FOX, HEAD_DIM), 0.02),
        "fox_w_o": nrm((N_FOX, D, D), fan * DEEPNORM_BETA),
        "rwkv_mu": jax.random.uniform(next(ks), (N_RWKV, 6, D), f32, 0.0, 1.0),
        "rwkv_w_rkv": nrm((N_RWKV, 3, D, D), fan) * rkv_scale,
        "rwkv_w0": nrm((N_RWKV, D), 0.5),
        "rwkv_w1": nrm((N_RWKV, D, DECAY_LORA), fan),
        "rwkv_w2": nrm((N_RWKV, DECAY_LORA, D), 0.1 * DECAY_LORA ** -0.5),
        "rwkv_a0": nrm((N_RWKV, D), 0.1),
        "rwkv_a1": nrm((N_RWKV, D, AAA_LORA), fan),
        "rwkv_a2": nrm((N_RWKV, AAA_LORA, D), 0.1 * AAA_LORA ** -0.5),
        "rwkv_g1": nrm((N_RWKV, D, GATE_LORA), fan),
        "rwkv_g2": nrm((N_RWKV, GATE_LORA, D), GATE_LORA ** -0.5),
        "rwkv_k_k": 0.85 + nrm((N_RWKV, D), 0.05),
        "rwkv_k_a": 1.0 + nrm((N_RWKV, D), 0.05),
        "rwkv_r_k": nrm((N_RWKV, N_HEADS, HEAD_DIM), 0.1),
        "rwkv_lnx_g": 1.0 + nrm((N_RWKV, D), 0.02),
        "rwkv_lnx_b": nrm((N_RWKV, D), 0.02),
        "rwkv_w_o": nrm((N_RWKV, D, D), fan * DEEPNORM_BETA),
    }


def reference(x, c, ada_w, ada_b, ln_g, ln_b, ffn_w_in, ffn_w_out,
              fox_w_in, fox_b_f, fox_q_g, fox_k_g, fox_w_o,
              rwkv_mu, rwkv_w_rkv, rwkv_w0, rwkv_w1, rwkv_w2, rwkv_a0, rwkv_a1, rwkv_a2,
              rwkv_g1, rwkv_g2, rwkv_k_k, rwkv_k_a, rwkv_r_k, rwkv_lnx_g, rwkv_lnx_b, rwkv_w_o):
    B, D = c.shape
    c_act = jax.nn.silu(c)
    for i in range(DEPTH):
        mods = (c_act @ ada_w[i] + ada_b[i]).reshape(B, 6, D)[:, :, None, :]
        shift1, scale1, gate1, shift2, scale2, gate2 = [mods[:, n] for n in range(6)]
        j = i // N_MIXERS
        h = x * (1.0 + scale1) + shift1
        if i % N_MIXERS == 0:
            y = fox_mixer(h, fox_w_in[j], fox_b_f[j], fox_q_g[j], fox_k_g[j], fox_w_o[j])
        else:
            y = rwkv7_mixer(h, rwkv_mu[j], rwkv_w_rkv[j], rwkv_w0[j], rwkv_w1[j], rwkv_w2[j],
                            rwkv_a0[j], rwkv_a1[j], rwkv_a2[j], rwkv_g1[j], rwkv_g2[j],
                            rwkv_k_k[j], rwkv_k_a[j], rwkv_r_k[j], rwkv_lnx_g[j], rwkv_lnx_b[j],
                            rwkv_w_o[j])
        x = layer_norm(DEEPNORM_ALPHA * x + gate1 * y, ln_g[i, 0], ln_b[i, 0])
        h = x * (1.0 + scale2) + shift2
        x = layer_norm(DEEPNORM_ALPHA * x + gate2 * swiglu(h, ffn_w_in[i], ffn_w_out[i]), ln_g[i, 1], ln_b[i, 1])
    return x
```

```python
import numpy as np
from contextlib import ExitStack
import concourse.bass as bass
import concourse.mybir as mybir
from concourse.bass_utils import run_bass_kernel_spmd

F32 = mybir.dt.float32
BF16 = mybir.dt.bfloat16
AF = mybir.ActivationFunctionType
ALU = mybir.AluOpType
AX = mybir.AxisListType

ENGS = ("pe", "act", "dve", "pool", "sp")
LAST_SEMS = []
SEM_CTR = [0]


def phase_end(nc):
    nc.all_engine_barrier()
    nc.clear_and_free_semaphores(list(LAST_SEMS))
    LAST_SEMS[:] = []
    nc.all_engine_barrier()


class Res:
    __slots__ = ("name", "last_w", "readers", "excl")

    def __init__(self, name, excl=False):
        self.name = name
        self.excl = excl
        self.last_w = None
        self.readers = []


class Op:
    __slots__ = ("eng", "fn", "deps", "sig", "count", "is_dma", "key", "semval", "vc", "waits", "gid")


class Sched:
    def __init__(self, nc):
        self.nc = nc
        self.ops = []
        self.dma_counts = {}

    def res(self, name, excl=False):
        return Res(name, excl)

    def _add(self, op, r, w, extra=()):
        deps = [(d, "raw") for d in extra]
        for x in r:
            if x.last_w is not None:
                deps.append((x.last_w, "raw"))
            if x.excl:
                for rd in x.readers:
                    deps.append((rd, "war"))
        for x in w:
            if x.last_w is not None:
                deps.append((x.last_w, "waw"))
            for rd in x.readers:
                deps.append((rd, "war"))
        dd = []
        for d, kind in deps:
            if d is op:
                continue
            if (not d.is_dma) and (not op.is_dma) and d.eng == op.eng:
                if op.eng == "pe" or kind == "war":
                    continue
            dd.append(d)
        op.deps = dd
        for d in dd:
            d.sig = True
        for x in w:
            x.last_w = op
            x.readers = []
        for x in r:
            if x.last_w is not op:
                if not op.is_dma:
                    x.readers = [q for q in x.readers if q.is_dma or q.eng != op.eng]
                x.readers.append(op)
        op.gid = len(self.ops)
        self.ops.append(op)
        return op

    def op(self, eng, fn, r=(), w=()):
        o = Op()
        o.eng = eng
        o.fn = fn
        o.is_dma = False
        o.sig = False
        o.key = eng
        return self._add(o, r, w)

    def dma(self, eng, out, in_, r=(), w=(), key=None, **kw):
        o = Op()
        o.eng = eng
        o.is_dma = True
        o.sig = True
        assert key is not None
        o.key = "dma:" + key
        o.fn = lambda e: e.dma_start(out=out, in_=in_, **kw)
        return self._add(o, r, w)

    def cc(self, kind, groups, in_ap, out_ap, r=(), w=(), key=None, extra=()):
        o = Op()
        o.eng = "pool"
        o.is_dma = True
        o.sig = True
        o.key = "cc:" + key
        o.fn = lambda e: e.collective_compute(kind, ALU.bypass, replica_groups=groups, ins=[in_ap], outs=[out_ap])
        return self._add(o, r, w, extra)

    def finalize(self, final_wait_eng="sp"):
        nc = self.nc
        counts = {e: 0 for e in ENGS}
        dmac = {}
        for o in self.ops:
            if o.is_dma:
                dmac[o.key] = dmac.get(o.key, 0) + (1 if o.key.startswith("cc:") else 16)
                o.semval = dmac[o.key]
            elif o.sig:
                counts[o.eng] += 1
                o.semval = counts[o.eng]
        seen = {e: {} for e in ENGS}
        for o in self.ops:
            s = seen[o.eng]
            need = {}
            for d in o.deps:
                if s.get(d.key, 0) < d.semval:
                    if d.key not in need or need[d.key].semval < d.semval:
                        need[d.key] = d
            o.waits = [(d.key, d.semval) for d in need.values()]
            for d in need.values():
                for k, v in d.vc.items():
                    if s.get(k, 0) < v:
                        s[k] = v
            if o.sig:
                vc = dict(s)
                vc[o.key] = o.semval
                o.vc = vc
            else:
                o.vc = None
        return counts, dmac

    def emit(self, stack, tail_waits=()):
        nc = self.nc
        counts, dmac = self.finalize()
        sems = {}
        for k in list(ENGS) + list(dmac):
            SEM_CTR[0] += 1
            sems[k] = nc.alloc_semaphore(name="sem%d" % SEM_CTR[0])
        self.nsems = len(sems)
        LAST_SEMS[:] = list(sems.values())
        per = {e: [o for o in self.ops if o.eng == e] for e in ENGS}
        block = stack.enter_context(nc.Block())

        def run(eng_name, eng):
            for o in per[eng_name]:
                for k, v in o.waits:
                    eng.wait_ge(sems[k], v)
                ins = o.fn(eng)
                if o.sig:
                    if o.is_dma:
                        ins.then_inc(sems[o.key], 1 if o.key.startswith("cc:") else 16)
                    else:
                        ins.then_inc(sems[o.key], 1)
            if eng_name == "sp":
                done = {}
                for o in tail_waits:
                    done[o.key] = max(done.get(o.key, 0), o.semval)
                for k, v in done.items():
                    eng.wait_ge(sems[k], v)

        @block.tensor
        def _(e):
            run("pe", e)

        @block.scalar
        def _(e):
            run("act", e)

        @block.vector
        def _(e):
            run("dve", e)

        @block.gpsimd
        def _(e):
            run("pool", e)

        @block.sync
        def _(e):
            run("sp", e)


D = 1024
DFF = 2816
NFC = 22
ALPHA = 4.0 ** 0.25
LN_EPS = 1e-5
EPS_P = LN_EPS / (ALPHA * ALPHA)


class Ctx:
    def __init__(self, nc, st, pfx=""):
        self.nc = nc
        self.st = st
        self.S = Sched(nc)
        self.n = 0
        self.pfx = pfx

    def sb(self, name, shape, dt):
        return self.st.enter_context(self.nc.sbuf_tensor(self.pfx + "sb_" + name, list(shape), dt))

    def ps(self, name, shape, dt=F32):
        return self.st.enter_context(self.nc.psum_tensor(self.pfx + "ps_" + name, list(shape), dt))

    def R(self, name, excl=False):
        return self.S.res(name, excl)


def make_consts(C):
    S = C.S
    C.ones = C.sb("ones", [128, 128], F32)
    C.r_ones = C.R("ones")
    S.op("pool", lambda e: e.memset(C.ones[:], 1.0), w=[C.r_ones])
    C.ident = C.sb("ident", [128, 128], F32)
    C.r_ident = C.R("ident")
    S.op("pool", lambda e: e.memset(C.ident[:], 1.0), w=[C.r_ident])
    S.op("pool", lambda e: e.affine_select(out=C.ident[:], in_=C.ident[:], pattern=[[-1, 128]], compare_op=ALU.is_equal, fill=0.0, base=0, channel_multiplier=1), r=[C.r_ident], w=[C.r_ident])


def compute_mods(C, cT_d, adaw_d, adab_d, nm, wbufs, psm, r_psm):
    S = C.S
    cT = C.sb("cT", [128, 8], F32)
    r_cT = C.R("cT")
    S.dma("sp", cT[:], cT_d, w=[r_cT], key="cT")
    S.op("act", lambda e: e.activation(out=cT[:], in_=cT[:], func=AF.Silu), r=[r_cT], w=[r_cT])
    adab = C.sb("adab", [128, nm * 8], F32)
    r_adab = C.R("adab")
    S.dma("sp", adab[:], adab_d, w=[r_adab], key="adab")
    mods = C.sb("mods", [128, nm * 8], F32)
    r_mods = C.R("mods")
    src = adaw_d.rearrange("(kc p) n -> p kc n", p=128)
    for m in range(nm):
        for q in range(4):
            bi = (m * 4 + q) % 2
            wt, r_wt = wbufs[bi]
            c0 = m * 1024 + q * 256
            S.dma("sp", wt[:], src[:, :, c0:c0 + 256], w=[r_wt], key="adaw%d" % bi)
            for j in range(2):
                cc = q * 2 + j
                for kc in range(8):
                    S.op("pe", lambda e, m=m, cc=cc, kc=kc, j=j, wt=wt: e.matmul(psm[:, m * 8 + cc:m * 8 + cc + 1], wt[:, kc, j * 128:(j + 1) * 128], cT[:, kc:kc + 1], start=(kc == 0), stop=(kc == 7)), r=[r_wt, r_cT], w=[r_psm])
    S.op("dve", lambda e: e.tensor_tensor(out=mods[:], in0=psm[:, 0:nm * 8], in1=adab[:], op=ALU.add), r=[r_psm, r_adab], w=[r_mods])
    return mods, r_mods


def ln_group(C, zT, r_z, g, n0, colA, colB, r_cols, outs, P):
    S = C.S
    ps_s, r_ps_s = P["st0"]
    ps_q, r_ps_q = P["st1"]
    sq, r_sq = P["sq"]
    for cc in range(8):
        S.op("act", lambda e, cc=cc: e.activation(out=sq[cc % 2][:], in_=zT[:, cc, n0:n0 + 512], func=AF.Square), r=[r_z[cc]], w=[r_sq[cc % 2]])
        S.op("pe", lambda e, cc=cc: e.matmul(ps_s[:], C.ones[:], zT[:, cc, n0:n0 + 512], start=(cc == 0), stop=(cc == 7)), r=[C.r_ones, r_z[cc]], w=[r_ps_s])
        S.op("pe", lambda e, cc=cc: e.matmul(ps_q[:], C.ones[:], sq[cc % 2][:], start=(cc == 0), stop=(cc == 7)), r=[C.r_ones, r_sq[cc % 2]], w=[r_ps_q])
    mean, r_mean = P["mean"]
    rstd, r_rstd = P["rstd"]
    S.op("act", lambda e: e.mul(out=mean[:], in_=ps_s[:], mul=1.0 / D), r=[r_ps_s], w=[r_mean])
    S.op("dve", lambda e: e.tensor_tensor(out=rstd[:], in0=mean[:], in1=mean[:], op=ALU.mult), r=[r_mean], w=[r_rstd])
    S.op("dve", lambda e: e.scalar_tensor_tensor(out=rstd[:], in0=ps_q[:], scalar=1.0 / D, in1=rstd[:], op0=ALU.mult, op1=ALU.subtract), r=[r_ps_q, r_rstd], w=[r_rstd])
    S.op("dve", lambda e: e.tensor_scalar(out=rstd[:], in0=rstd[:], scalar1=P["eps"], scalar2=None, op0=ALU.add), r=[r_rstd], w=[r_rstd])
    S.op("act", lambda e: e.activation(out=rstd[:], in_=rstd[:], func=AF.Sqrt), r=[r_rstd], w=[r_rstd])
    S.op("dve", lambda e: e.reciprocal(out=rstd[:], in_=rstd[:]), r=[r_rstd], w=[r_rstd])
    S.op("dve", lambda e: e.scalar_tensor_tensor(out=mean[:], in0=mean[:], scalar=-1.0, in1=rstd[:], op0=ALU.mult, op1=ALU.mult), r=[r_mean, r_rstd], w=[r_mean])
    tt, r_tt = P["tt"]
    for cc in range(8):
        k = cc % 2
        S.op("dve", lambda e, cc=cc, k=k: e.tensor_tensor(out=tt[k][:], in0=zT[:, cc, n0:n0 + 512], in1=rstd[:], op=ALU.mult), r=[r_z[cc], r_rstd], w=[r_tt[k]])
        S.op("dve", lambda e, cc=cc, k=k: e.tensor_tensor(out=tt[k][:], in0=tt[k][:], in1=mean[:], op=ALU.add), r=[r_tt[k], r_mean], w=[r_tt[k]])
        for (dst, r_dst, gt, go, bt, bo, eng) in outs:
            if eng == "act":
                S.op("act", lambda e, cc=cc, k=k, dst=dst, gt=gt, go=go, bt=bt, bo=bo: e.activation(out=dst[:, cc, n0:n0 + 512], in_=tt[k][:], func=AF.Identity, scale=gt[:, go + cc:go + cc + 1], bias=bt[:, bo + cc:bo + cc + 1]), r=[r_tt[k]] + r_cols, w=[r_dst[cc]])
            else:
                S.op(eng, lambda e, cc=cc, k=k, dst=dst, gt=gt, go=go, bt=bt, bo=bo: e.tensor_scalar(out=dst[:, cc, n0:n0 + 512], in0=tt[k][:], scalar1=gt[:, go + cc:go + cc + 1], scalar2=bt[:, bo + cc:bo + cc + 1], op0=ALU.mult, op1=ALU.add), r=[r_tt[k]] + r_cols, w=[r_dst[cc]])


def build_tl(NT=2048, HALF=1024, debug=False):
    nc = bass.Bass("TRN2", target_bir_lowering=False)
    dt = lambda n, s, d, k="ExternalInput": nc.dram_tensor(n, list(s), d, kind=k).ap()
    yT_d = dt("yT", [D, NT], BF16)
    xT_d = dt("xT", [D, NT], F32)
    cT_d = dt("cT", [128, 8], F32)
    adaw_d = dt("adaw", [D, 4 * D], F32)
    adab_d = dt("adab", [128, 32], F32)
    lng_d = dt("lng", [128, 16], F32)
    lnb_d = dt("lnb", [128, 16], F32)
    wo_d = dt("wo", [D, D], F32)
    win_d = dt("win", [D, 2 * DFF], F32)
    wout_d = dt("wout", [DFF, D], F32)
    oT_d = dt("oT", [D, NT], F32, "ExternalOutput")
    tl_phase(nc, "", yT_d, xT_d, cT_d, adaw_d, adab_d, lng_d, lnb_d, wo_d, win_d, wout_d, oT_d, NT, HALF, debug=debug)
    return nc


def tl_phase(nc, pfx, yT_d, xT_d, cT_d, adaw_d, adab_d, lng_d, lnb_d, wo_d, win_d, wout_d, oT_d, NT=2048, HALF=1024, debug=False, sel_d=None, after=None):
    dt = lambda n, s, d, k="ExternalInput": nc.dram_tensor(n, list(s), d, kind=k).ap()
    NG = HALF // 512
    if debug:
        dbg_cols = dt("dbg_cols", [128, 48], F32, "ExternalOutput")
        dbg_mods = dt("dbg_mods", [128, 32], F32, "ExternalOutput")
        dbg_x1 = dt("dbg_x1", [D, HALF], F32, "ExternalOutput")
        dbg_h = dt("dbg_h", [D, HALF], BF16, "ExternalOutput")
        dbg_a = dt("dbg_a", [DFF, HALF], BF16, "ExternalOutput")
    with ExitStack() as st:
        C = Ctx(nc, st, pfx)
        S = C.S
        make_consts(C)
        PB = [(C.ps("pb%d" % i, [128, 512]), C.R("pb%d" % i, True)) for i in range(8)]
        if sel_d is not None:
            selt = C.sb("selt", [128, 2], F32)
            r_selt = C.R("selt")
            S.dma("sp", selt[:], sel_d, w=[r_selt], key="selt")
            ystg = [[(C.sb("ystg%d_%d" % (i, j), [128, HALF], BF16), C.R("ystg%d_%d" % (i, j))) for j in range(2)] for i in range(2)]
        xT = C.sb("xT", [128, 8, HALF], F32)
        r_x = [C.R("xT%d" % cc) for cc in range(8)]
        yT = C.sb("yT", [128, 8, HALF], BF16)
        r_y = [C.R("yT%d" % cc) for cc in range(8)]
        hT = yT
        r_h = r_y
        aT = C.sb("aT", [128, NFC, HALF], BF16)
        r_a = [C.R("aT%d" % f) for f in range(NFC)]
        wo = C.sb("wo", [128, 8, D], BF16)
        r_wo = C.R("wo")
        wA = [(C.sb("wA%d" % i, [128, 8, 256], F32), C.R("wA%d" % i)) for i in range(2)]
        wi = [(C.sb("wi%d" % i, [128, 8, 512], BF16), C.R("wi%d" % i)) for i in range(2)]
        wo2 = [(C.sb("wo2%d" % i, [128, NFC, 256], BF16), C.R("wo2%d" % i)) for i in range(2)]
        P = {
            "st0": PB[6], "st1": PB[7],
            "sq": ([C.sb("sq%d" % i, [128, 512], F32) for i in range(2)], [C.R("sq%d" % i) for i in range(2)]),
            "mean": (C.sb("mean", [128, 512], F32), C.R("mean")),
            "rstd": (C.sb("rstd", [128, 512], F32), C.R("rstd")),
            "tt": ([C.sb("tt%d" % i, [128, 512], F32) for i in range(2)], [C.R("tt%d" % i) for i in range(2)]),
            "eps": EPS_P,
        }
        sl = [C.sb("sl%d" % i, [128, 512], F32) for i in range(2)]
        r_sl = [C.R("sl%d" % i) for i in range(2)]
        mods, r_mods = compute_mods(C, cT_d, adaw_d, adab_d, 4, wA, PB[5][0], PB[5][1])
        lng = C.sb("lng", [128, 16], F32)
        lnb = C.sb("lnb", [128, 16], F32)
        r_ln = C.R("ln")
        S.dma("sp", lng[:], lng_d, w=[r_ln], key="lng")
        S.dma("sp", lnb[:], lnb_d, w=[r_ln], key="lnb")
        cols = C.sb("cols", [128, 48], F32)
        r_cols = C.R("cols")
        S.op("dve", lambda e: e.tensor_scalar(out=cols[:, 0:8], in0=mods[:, 0:8], scalar1=1.0 / ALPHA, scalar2=None, op0=ALU.mult), r=[r_mods], w=[r_cols])
        S.op("dve", lambda e: e.tensor_scalar(out=cols[:, 8:16], in0=mods[:, 24:32], scalar1=1.0 / ALPHA, scalar2=None, op0=ALU.mult), r=[r_mods], w=[r_cols])
        S.op("dve", lambda e: e.tensor_scalar(out=cols[:, 32:40], in0=mods[:, 16:24], scalar1=1.0, scalar2=None, op0=ALU.add), r=[r_mods], w=[r_cols])
        S.op("dve", lambda e: e.tensor_tensor(out=cols[:, 16:24], in0=lng[:, 0:8], in1=cols[:, 32:40], op=ALU.mult), r=[r_ln, r_cols], w=[r_cols])
        S.op("dve", lambda e: e.tensor_tensor(out=cols[:, 24:32], in0=lnb[:, 0:8], in1=cols[:, 32:40], op=ALU.mult), r=[r_ln, r_cols], w=[r_cols])
        S.op("dve", lambda e: e.tensor_tensor(out=cols[:, 24:32], in0=cols[:, 24:32], in1=mods[:, 8:16], op=ALU.add), r=[r_mods, r_cols], w=[r_cols])
        rc = [r_cols, r_ln]
        S.dma("pool", wo[:], wo_d.rearrange("(kc p) n -> p kc n", p=128), w=[r_wo], key="wo")
        xsrc = xT_d.rearrange("(kc p) t -> p kc t", p=128)
        if sel_d is None:
            ysrc_ = yT_d.rearrange("(kc p) t -> p kc t", p=128)
            ysrc_fn = lambda cc, a, n: ysrc_[:, cc, a:a + n]
        else:
            ysrc_fn = lambda cc, a, n: yT_d[cc % 4, cc // 4, :, a:a + n]
        osrc = oT_d.rearrange("(kc p) t -> p kc t", p=128)
        winr = win_d.rearrange("(kc p) n -> p kc n", p=128)
        woutr = wout_d.rearrange("(fc p) n -> p fc n", p=128)
        tails = []
        pbi = 0
        for half in range(NT // HALF):
            t0 = half * HALF
            for cc in range(8):
                if sel_d is None:
                    S.dma("sp", yT[:, cc, :], ysrc_fn(cc, t0, HALF), w=[r_y[cc]], key="yT%d" % cc)
                else:
                    (ya, r_ya), (yb, r_yb) = ystg[cc % 2]
                    S.dma("sp", ya[:], ysrc_fn(cc, t0, HALF), w=[r_ya], key="ysa%d" % (cc % 2))
                    S.dma("sp", yb[:], ysrc_fn(cc, NT + t0, HALF), w=[r_yb], key="ysb%d" % (cc % 2))
                    S.op("dve", lambda e, ya=ya: e.tensor_scalar(out=ya[:], in0=ya[:], scalar1=selt[:, 0:1], scalar2=None, op0=ALU.mult), r=[r_ya, r_selt], w=[r_ya])
                    S.op("dve", lambda e, ya=ya, yb=yb, cc=cc: e.scalar_tensor_tensor(out=yT[:, cc, :], in0=yb[:], scalar=selt[:, 1:2], in1=ya[:], op0=ALU.mult, op1=ALU.add), r=[r_ya, r_yb, r_selt], w=[r_y[cc]])
                S.dma("sp", xT[:, cc, :], xsrc[:, cc, t0:t0 + HALF], w=[r_x[cc]], key="xT%d" % cc)
            for g in range(NG):
                n0 = g * 512
                for cc in range(8):
                    pb, r_pb = PB[pbi % 4]
                    pbi += 1
                    for kc in range(8):
                        S.op("pe", lambda e, pb=pb, cc=cc, kc=kc, n0=n0: e.matmul(pb[:], wo[:, kc, cc * 128:(cc + 1) * 128], yT[:, kc, n0:n0 + 512], start=(kc == 0), stop=(kc == 7)), r=[r_wo] + r_y, w=[r_pb])
                    S.op("dve", lambda e, pb=pb, cc=cc, n0=n0: e.scalar_tensor_tensor(out=xT[:, cc, n0:n0 + 512], in0=pb[:], scalar=cols[:, cc:cc + 1], in1=xT[:, cc, n0:n0 + 512], op0=ALU.mult, op1=ALU.add), r=[r_pb, r_x[cc], r_cols], w=[r_x[cc]])
            for g in range(NG):
                n0 = g * 512
                ln_group(C, xT, r_x, g, n0, None, None, rc,
                         [(xT, r_x, lng, 0, lnb, 0, "act"), (hT, r_h, cols, 16, cols, 24, "act")], P)
            if debug and half == 0:
                tails.append(S.dma("sp", dbg_cols, cols[:], r=[r_cols], key="dbg0"))
                tails.append(S.dma("sp", dbg_mods, mods[:], r=[r_mods], key="dbg1"))
                tails.append(S.dma("sp", dbg_x1.rearrange("(kc p) t -> p kc t", p=128), xT[:], r=r_x, key="dbg2"))
                tails.append(S.dma("sp", dbg_h.rearrange("(kc p) t -> p kc t", p=128), hT[:], r=r_h, key="dbg3"))
            for blk in range(NFC // 2):
                wt, r_wt = wi[blk % 2]
                c0 = blk * 256
                S.dma("pool", wt[:, :, 0:256], winr[:, :, c0:c0 + 256], w=[r_wt], key="wi%d" % (blk % 2))
                S.dma("pool", wt[:, :, 256:512], winr[:, :, DFF + c0:DFF + c0 + 256], w=[r_wt], key="wi%d" % (blk % 2))
                for j in range(2):
                    fc = blk * 2 + j
                    for g in range(NG):
                        n0 = g * 512
                        pg, r_pg = PB[pbi % 4]
                        pu, r_pu = PB[(pbi + 1) % 4]
                        pbi += 2
                        for kc in range(8):
                            S.op("pe", lambda e, pg=pg, wt=wt, j=j, kc=kc, n0=n0: e.matmul(pg[:], wt[:, kc, j * 128:(j + 1) * 128], hT[:, kc, n0:n0 + 512], start=(kc == 0), stop=(kc == 7)), r=[r_wt] + r_h, w=[r_pg])
                        for kc in range(8):
                            S.op("pe", lambda e, pu=pu, wt=wt, j=j, kc=kc, n0=n0: e.matmul(pu[:], wt[:, kc, 256 + j * 128:256 + (j + 1) * 128], hT[:, kc, n0:n0 + 512], start=(kc == 0), stop=(kc == 7)), r=[r_wt] + r_h, w=[r_pu])
                        k = (fc * NG + g) % 2
                        S.op("act", lambda e, pg=pg, k=k: e.activation(out=sl[k][:], in_=pg[:], func=AF.Silu), r=[r_pg], w=[r_sl[k]])
                        S.op("dve", lambda e, pu=pu, k=k, fc=fc, n0=n0: e.tensor_tensor(out=aT[:, fc, n0:n0 + 512], in0=pu[:], in1=sl[k][:], op=ALU.mult), r=[r_pu, r_sl[k]], w=[r_a[fc]])
            if debug and half == 0:
                tails.append(S.dma("sp", dbg_a.rearrange("(kc p) t -> p kc t", p=128), aT[:], r=r_a, key="dbg4"))
            for blk in range(4):
                wt, r_wt = wo2[blk % 2]
                S.dma("pool", wt[:], woutr[:, :, blk * 256:(blk + 1) * 256], w=[r_wt], key="wo2%d" % (blk % 2))
                for j in range(2):
                    cc = blk * 2 + j
                    for g in range(NG):
                        n0 = g * 512
                        pb, r_pb = PB[pbi % 4]
                        pbi += 1
                        for fc in range(NFC):
                            S.op("pe", lambda e, pb=pb, wt=wt, j=j, fc=fc, n0=n0: e.matmul(pb[:], wt[:, fc, j * 128:(j + 1) * 128], aT[:, fc, n0:n0 + 512], start=(fc == 0), stop=(fc == NFC - 1)), r=[r_wt, r_a[fc]], w=[r_pb])
                        S.op("dve", lambda e, pb=pb, cc=cc, n0=n0: e.scalar_tensor_tensor(out=xT[:, cc, n0:n0 + 512], in0=pb[:], scalar=cols[:, 8 + cc:9 + cc], in1=xT[:, cc, n0:n0 + 512], op0=ALU.mult, op1=ALU.add), r=[r_pb, r_x[cc], r_cols], w=[r_x[cc]])
            for g in range(NG):
                n0 = g * 512
                ln_group(C, xT, r_x, g, n0, None, None, rc, [(xT, r_x, lng, 8, lnb, 8, "act")], P)
            for cc in range(8):
                o = S.dma("sp", osrc[:, cc, t0:t0 + HALF], xT[:, cc, :], r=[r_x[cc]], w=[], key="oT%d" % cc)
                tails.append(o)
        if after is not None:
            after(C, tails)
        S.emit(st, tail_waits=tails)


T = 4096
NH = 8
HD = 64
QK_EPS = 1e-6
NEG = -30000.0


def build_fox(TT=T, debug=False):
    nc = bass.Bass("TRN2", target_bir_lowering=False)
    dt = lambda n, s, d, k="ExternalInput": nc.dram_tensor(n, list(s), d, kind=k).ap()
    ins = fox_decl(dt, TT)
    yg_d = dt("ygT", [512, TT], BF16, "ExternalOutput")
    fox_phase(nc, "", ins, yg_d, TT, debug=debug)
    return nc


def fox_decl(dt, TT=T, sfx=""):
    return dict(
        x=dt("x" + sfx, [TT, D], F32), cT=dt("cT" + sfx, [128, 8], F32), adaw=dt("adaw" + sfx, [D, 2 * D], F32), adab=dt("adab" + sfx, [128, 16], F32),
        wq=dt("wq" + sfx, [D, 512], F32), wk=dt("wk" + sfx, [D, 512], F32), wv=dt("wv" + sfx, [D, 512], F32), wf=dt("wf" + sfx, [D, 8], F32), wg=dt("wg" + sfx, [D, 512], F32),
        bfb=dt("bfb" + sfx, [128, 8], F32), qgb=dt("qgb" + sfx, [128, 512], F32), kgb=dt("kgb" + sfx, [128, 512], F32))


def fox_phase(nc, pfx, ins, yg_d, TT=T, debug=False, after=None):
    dt = lambda n, s, d, k="ExternalInput": nc.dram_tensor(n, list(s), d, kind=k).ap()
    x_d, cT_d, adaw_d, adab_d = ins["x"], ins["cT"], ins["adaw"], ins["adab"]
    wq_d, wk_d, wv_d, wf_d, wg_d = ins["wq"], ins["wk"], ins["wv"], ins["wf"], ins["wg"]
    bf_d, qg_d, kg_d = ins["bfb"], ins["qgb"], ins["kgb"]
    NGR = TT // 512
    NTL = TT // 128
    if debug:
        dbg_h = dt("dbg_h", [D, 512], BF16, "ExternalOutput")
        dbg_q = dt("dbg_q", [128, 4, 512], BF16, "ExternalOutput")
        dbg_k = dt("dbg_k", [128, 4, TT], BF16, "ExternalOutput")
        dbg_F = dt("dbg_F", [128, NTL, 8], F32, "ExternalOutput")
        dbg_v = dt("dbg_v", [128, NTL, 8, 65], BF16, "ExternalOutput")
    with ExitStack() as st:
        C = Ctx(nc, st, pfx)
        S = C.S
        make_consts(C)
        identb = C.sb("identb", [128, 128], BF16)
        r_identb = C.R("identb")
        S.op("dve", lambda e: e.tensor_copy(out=identb[:], in_=C.ident[:]), r=[C.r_ident], w=[r_identb])
        tri = C.sb("tri", [128, 128], F32)
        r_tri = C.R("tri")
        S.op("pool", lambda e: e.memset(tri[:], 1.0), w=[r_tri])
        S.op("pool", lambda e: e.affine_select(out=tri[:], in_=tri[:], pattern=[[1, 128]], compare_op=ALU.is_ge, fill=0.0, base=0, channel_multiplier=-1), r=[r_tri], w=[r_tri])
        selA = C.sb("selA", [16, 8, 128], F32)
        selB = C.sb("selB", [16, 8, 128], F32)
        sel = C.sb("sel", [16, 8, 128], BF16)
        r_sel = C.R("sel")
        S.op("pool", lambda e: e.memset(selA[:], 1.0), w=[r_sel])
        S.op("pool", lambda e: e.memset(selB[:], 1.0), w=[r_sel])
        S.op("pool", lambda e: e.affine_select(out=selA[:], in_=selA[:], pattern=[[-1, 8], [0, 128]], compare_op=ALU.is_equal, fill=0.0, base=0, channel_multiplier=1), r=[r_sel], w=[r_sel])
        S.op("pool", lambda e: e.affine_select(out=selB[:], in_=selB[:], pattern=[[-1, 8], [0, 128]], compare_op=ALU.is_equal, fill=0.0, base=-8, channel_multiplier=1), r=[r_sel], w=[r_sel])
        S.op("dve", lambda e: e.tensor_tensor(out=sel[:], in0=selA[:], in1=selB[:], op=ALU.add), r=[r_sel], w=[r_sel])
        maskf = C.sb("maskf", [128, 4, 512], F32)
        maskb = C.sb("maskb", [128, 4, 512], BF16)
        r_mask = C.R("mask")
        S.op("pool", lambda e: e.memset(maskf[:], 0.0), w=[r_mask])
        for j in range(4):
            S.op("pool", lambda e, j=j: e.affine_select(out=maskf[:, j, :], in_=maskf[:, j, :], pattern=[[1, 512]], compare_op=ALU.is_ge, fill=NEG, base=-128 * j, channel_multiplier=-1), r=[r_mask], w=[r_mask])
        S.op("dve", lambda e: e.tensor_copy(out=maskb[:], in_=maskf[:]), r=[r_mask], w=[r_mask])
        PB = [(C.ps("pb%d" % i, [128, 512]), C.R("pb%d" % i, True)) for i in range(7)]
        pbb = C.ps("pbb", [128, 1024], BF16)
        r_pbb = C.R("pbb", True)
        wA = [(C.sb("wA%d" % i, [128, 8, 256], F32), C.R("wA%d" % i)) for i in range(2)]
        mods, r_mods = compute_mods(C, cT_d, adaw_d, adab_d, 2, wA, PB[6][0], PB[6][1])
        cols = C.sb("cols", [128, 8], F32)
        r_cols = C.R("cols")
        S.op("dve", lambda e: e.tensor_scalar(out=cols[:], in0=mods[:, 8:16], scalar1=1.0, scalar2=None, op0=ALU.add), r=[r_mods], w=[r_cols])
        wts = {}
        r_w = C.R("wts")
        for nm, d_ in (("wq", wq_d), ("wk", wk_d), ("wv", wv_d), ("wg", wg_d)):
            wts[nm] = C.sb(nm, [128, 8, 512], BF16)
            S.dma("pool", wts[nm][:], d_.rearrange("(kc p) n -> p kc n", p=128), w=[r_w], key=nm)
        wf = C.sb("wf", [128, 8, 8], BF16)
        S.dma("pool", wf[:], wf_d.rearrange("(kc p) n -> p kc n", p=128), w=[r_w], key="wf")
        bfb = C.sb("bfb", [128, 8], F32)
        qgb = C.sb("qgb", [128, 512], F32)
        kgb = C.sb("kgb", [128, 512], F32)
        r_par = C.R("par")
        S.dma("sp", bfb[:], bf_d, w=[r_par], key="bfb")
        S.dma("sp", qgb[:], qg_d, w=[r_par], key="qgb")
        S.dma("sp", kgb[:], kg_d, w=[r_par], key="kgb")
        xt = C.sb("xt", [128, 4, D], F32)
        r_xt = C.R("xt")
        hT = C.sb("hT", [128, 8, 512], BF16)
        r_hT = [C.R("hT%d" % k) for k in range(8)]
        kT = C.sb("kT", [128, 4, TT], BF16)
        r_kT = [C.R("kT%d" % g) for g in range(NGR)]
        qT = C.sb("qT", [128, 4, 512], BF16)
        r_qT = C.R("qT")
        Va = C.sb("Va", [128, NTL, NH, 65], BF16)
        r_V = [C.R("V%d" % g) for g in range(NGR)]
        r_Vones = C.R("Vones")
        S.op("pool", lambda e: e.memset(Va[:, :, :, 64:65], 1.0), w=[r_Vones])
        eg = C.sb("eg", [64, NH, 512], F32)
        r_eg = [C.R("eg%d" % h) for h in range(NH)]
        Fcol = C.sb("Fcol", [128, NTL, NH], F32)
        r_F = C.R("Fcol")
        carry = C.sb("carry", [128, NTL + 1, NH], F32)
        r_carry = C.R("carry")
        S.op("dve", lambda e: e.memset(carry[:, 0, :], 0.0), w=[r_carry])
        biasG = C.sb("biasG", [128, NTL, NH], F32)
        r_bias = C.R("biasG")
        FqT = C.sb("FqT", [16, 512], BF16)
        r_FqT = C.R("FqT")
        sq = C.sb("sq", [128, 512], F32); r_sq = C.R("sq")
        tmp = C.sb("tmp", [128, 512], F32); r_tmp = C.R("tmp")
        qtok = C.sb("qtok", [128, 512], BF16); r_qtok = C.R("qtok")
        ktok = C.sb("ktok", [128, 512], BF16); r_ktok = C.R("ktok")
        ssq = C.sb("ssq", [128, 16], F32); r_ssq = C.R("ssq")
        zf = C.sb("zf", [128, 8], F32); r_zf = C.R("zf")
        fq = C.sb("fq", [128, 8], F32); r_fq = C.R("fq")
        fq16 = C.sb("fq16", [128, 16], BF16); r_fq16 = C.R("fq16")
        PT = [(C.sb("PT%d" % i, [128, 512], BF16), C.R("PT%d" % i)) for i in range(3)]
        oT = C.sb("oT", [64, 512], F32); r_oT = C.R("oT")
        dn = C.sb("dn", [128, 512], F32); r_dn = C.R("dn")
        wv_ = C.sb("wv_", [64, 512], F32); r_wv_ = C.R("wv_")
        yg = [(C.sb("yg%d" % i, [64, 512], BF16), C.R("yg%d" % i)) for i in range(2)]
        tails = []
        xsrc = x_d.rearrange("(i p) d -> p i d", p=128)
        ps_i = 0
        for G in range(NGR):
            S.dma("sp", xt[:], xsrc[:, 4 * G:4 * G + 4, :], w=[r_xt], key="xt")
            for kc in range(8):
                pb, r_pb = PB[kc % 2]
                for i in range(4):
                    S.op("pe", lambda e, pb=pb, i=i, kc=kc: e.transpose(pb[:, i * 128:(i + 1) * 128], xt[:, i, kc * 128:(kc + 1) * 128], C.ident[:]), r=[r_xt, C.r_ident], w=[r_pb])
                if kc % 2 == 0:
                    S.op("act", lambda e, pb=pb, kc=kc: e.activation(out=hT[:, kc, :], in_=pb[:], func=AF.Identity, scale=cols[:, kc:kc + 1], bias=mods[:, kc:kc + 1]), r=[r_pb, r_cols, r_mods], w=[r_hT[kc]])
                else:
                    S.op("dve", lambda e, pb=pb, kc=kc: e.tensor_scalar(out=hT[:, kc, :], in0=pb[:], scalar1=cols[:, kc:kc + 1], scalar2=mods[:, kc:kc + 1], op0=ALU.mult, op1=ALU.add), r=[r_pb, r_cols, r_mods], w=[r_hT[kc]])
            if debug and G == 0:
                tails.append(S.dma("sp", dbg_h.rearrange("(kc p) t -> p kc t", p=128), hT[:], r=r_hT, key="dbg0"))
            for h in range(NH):
                pb, r_pb = PB[2 + h % 2]
                for kc in range(8):
                    S.op("pe", lambda e, pb=pb, h=h, kc=kc: e.matmul(pb[0:64, :], wts["wg"][:, kc, h * 64:(h + 1) * 64], hT[:, kc, :], start=(kc == 0), stop=(kc == 7)), r=[r_w, r_hT[kc]], w=[r_pb])
                S.op("act", lambda e, pb=pb, h=h: e.activation(out=eg[:, h, :], in_=pb[0:64, :], func=AF.Exp, scale=-1.0), r=[r_pb], w=[r_eg[h]])
            for i in range(4):
                tl_ = 4 * G + i
                (pq, r_pq), (pk, r_pk), (pv, r_pv), (pf, r_pf) = PB[2], PB[3], PB[4], PB[5]
                for kc in range(8):
                    lw = hT[:, kc, i * 128:(i + 1) * 128]
                    S.op("pe", lambda e, lw=lw, kc=kc: e.matmul(pq[:], lw, wts["wq"][:, kc, :], start=(kc == 0), stop=(kc == 7)), r=[r_w, r_hT[kc]], w=[r_pq])
                    S.op("pe", lambda e, lw=lw, kc=kc: e.matmul(pk[:], lw, wts["wk"][:, kc, :], start=(kc == 0), stop=(kc == 7)), r=[r_w, r_hT[kc]], w=[r_pk])
                    S.op("pe", lambda e, lw=lw, kc=kc: e.matmul(pv[:], lw, wts["wv"][:, kc, :], start=(kc == 0), stop=(kc == 7)), r=[r_w, r_hT[kc]], w=[r_pv])
                    S.op("pe", lambda e, lw=lw, kc=kc: e.matmul(pf[:, 0:8], lw, wf[:, kc, :], start=(kc == 0), stop=(kc == 7)), r=[r_w, r_hT[kc]], w=[r_pf])
                for which, (pp, r_pp), gb, tok, r_tok, eps_, mul_ in (("q", (pq, r_pq), qgb, qtok, r_qtok, 64.0 * QK_EPS, 1.0), ("k", (pk, r_pk), kgb, ktok, r_ktok, QK_EPS, 1.0 / 64.0)):
                    o8 = 0 if which == "q" else 8
                    S.op("act", lambda e, pp=pp: e.activation(out=sq[:], in_=pp[:], func=AF.Square), r=[r_pp], w=[r_sq])
                    S.op("dve", lambda e, o8=o8: e.tensor_reduce(out=ssq[:, o8:o8 + 8], in_=sq[:].rearrange("p (h d) -> p h d", d=64), axis=AX.X, op=ALU.add), r=[r_sq], w=[r_ssq])
                    S.op("dve", lambda e, o8=o8, eps_=eps_, mul_=mul_: e.tensor_scalar(out=ssq[:, o8:o8 + 8], in0=ssq[:, o8:o8 + 8], scalar1=mul_, scalar2=eps_, op0=ALU.mult, op1=ALU.add), r=[r_ssq], w=[r_ssq])
                    S.op("act", lambda e, o8=o8: e.activation(out=ssq[:, o8:o8 + 8], in_=ssq[:, o8:o8 + 8], func=AF.Sqrt), r=[r_ssq], w=[r_ssq])
                    S.op("dve", lambda e, o8=o8: e.reciprocal(out=ssq[:, o8:o8 + 8], in_=ssq[:, o8:o8 + 8]), r=[r_ssq], w=[r_ssq])
                    S.op("dve", lambda e, pp=pp, o8=o8: e.tensor_tensor(out=tmp[:].rearrange("p (h d) -> p h d", d=64), in0=pp[:].rearrange("p (h d) -> p h d", d=64), in1=ssq[:, o8:o8 + 8].unsqueeze(2).to_broadcast([128, 8, 64]), op=ALU.mult), r=[r_pp, r_ssq], w=[r_tmp])
                    S.op("dve", lambda e, gb=gb, tok=tok: e.tensor_tensor(out=tok[:], in0=tmp[:], in1=gb[:], op=ALU.mult), r=[r_tmp, r_par], w=[r_tok])
                S.op("act", lambda e, tl_=tl_: e.copy(out=Va[:, tl_, :, 0:64], in_=pv[:].rearrange("p (h d) -> p h d", d=64)), r=[r_pv, r_Vones], w=[r_V[G]])
                S.op("dve", lambda e: e.tensor_tensor(out=zf[:], in0=pf[:, 0:8], in1=bfb[:], op=ALU.add), r=[r_pf, r_par], w=[r_zf])
                S.op("act", lambda e: e.activation(out=zf[:], in_=zf[:], func=AF.Exp, scale=-1.0), r=[r_zf], w=[r_zf])
                S.op("act", lambda e: e.activation(out=zf[:], in_=zf[:], func=AF.Ln, bias=1.0), r=[r_zf], w=[r_zf])
                pc, r_pc = PB[6]
                S.op("pe", lambda e: e.matmul(pc[:, 0:8], tri[:], zf[:], start=True, stop=True), r=[r_tri, r_zf], w=[r_pc])
                S.op("pe", lambda e: e.matmul(pc[:, 8:16], C.ones[:], zf[:], start=True, stop=True), r=[C.r_ones, r_zf], w=[r_pc])
                S.op("dve", lambda e, tl_=tl_: e.tensor_tensor(out=Fcol[:, tl_, :], in0=carry[:, tl_, :], in1=pc[:, 0:8], op=ALU.subtract), r=[r_carry, r_pc], w=[r_F])
                S.op("dve", lambda e, tl_=tl_: e.tensor_tensor(out=carry[:, tl_ + 1, :], in0=carry[:, tl_, :], in1=pc[:, 8:16], op=ALU.subtract), r=[r_carry, r_pc], w=[r_carry])
                S.op("dve", lambda e, tl_=tl_, G=G: e.tensor_tensor(out=fq[:], in0=Fcol[:, tl_, :], in1=carry[:, 4 * G, :], op=ALU.subtract), r=[r_F, r_carry], w=[r_fq])
                S.op("dve", lambda e: e.tensor_copy(out=fq16[:, 0:8], in_=fq[:]), r=[r_fq], w=[r_fq16])
                S.op("dve", lambda e: e.tensor_tensor(out=fq[:], in0=fq[:], in1=fq16[:, 0:8], op=ALU.subtract), r=[r_fq, r_fq16], w=[r_fq])
                S.op("dve", lambda e: e.tensor_copy(out=fq16[:, 8:16], in_=fq[:]), r=[r_fq], w=[r_fq16])
                for p in range(4):
                    S.op("pe", lambda e, p=p: e.transpose(pbb[:, p * 128:(p + 1) * 128], qtok[:, p * 128:(p + 1) * 128], identb[:]), r=[r_qtok, r_identb], w=[r_pbb])
                for p in range(4):
                    S.op("pe", lambda e, p=p: e.transpose(pbb[:, 512 + p * 128:512 + (p + 1) * 128], ktok[:, p * 128:(p + 1) * 128], identb[:]), r=[r_ktok, r_identb], w=[r_pbb])
                S.op("act", lambda e, i=i: e.copy(out=qT[:, :, i * 128:(i + 1) * 128], in_=pbb[:, 0:512].rearrange("p (a t) -> p a t", t=128)), r=[r_pbb], w=[r_qT])
                S.op("dve", lambda e, tl_=tl_: e.tensor_copy(out=kT[:, :, tl_ * 128:(tl_ + 1) * 128], in_=pbb[:, 512:1024].rearrange("p (a t) -> p a t", t=128)), r=[r_pbb], w=[r_kT[G]])
                S.op("pe", lambda e: e.transpose(pbb[0:16, 0:128], fq16[:, 0:16], identb[:]), r=[r_fq16, r_identb], w=[r_pbb])
                S.op("dve", lambda e, i=i: e.tensor_copy(out=FqT[:, i * 128:(i + 1) * 128], in_=pbb[0:16, 0:128]), r=[r_pbb], w=[r_FqT])
            nkb = 4 * G + 4
            for h in range(NH):
                S.op("dve", lambda e, h=h, nkb=nkb, G=G: e.tensor_scalar(out=biasG[:, 0:nkb, h], in0=Fcol[:, 0:nkb, h], scalar1=-1.0, scalar2=carry[:, 4 * G, h:h + 1], op0=ALU.mult, op1=ALU.add), r=[r_F, r_carry], w=[r_bias])
            for h in range(NH):
                p_, e_ = h // 2, h % 2
                pO, r_pO = PB[3 + h % 2]
                for kb in range(nkb):
                    pS, r_pS = PB[ps_i % 3]
                    PTt, r_PT = PT[ps_i % 3]
                    ps_i += 1
                    diag = kb >= 4 * G
                    gk = kb // 4
                    S.op("pe", lambda e, pS=pS, p_=p_, e_=e_, kb=kb: e.matmul(pS[:], kT[e_ * 64:(e_ + 1) * 64, p_, kb * 128:(kb + 1) * 128], qT[e_ * 64:(e_ + 1) * 64, p_, :], start=True, stop=False), r=[r_kT[gk], r_qT], w=[r_pS])
                    S.op("pe", lambda e, pS=pS, h=h, diag=diag: e.matmul(pS[:], sel[:, h, :], FqT[:], start=False, stop=(not diag)), r=[r_sel, r_FqT], w=[r_pS])
                    if diag:
                        S.op("pe", lambda e, pS=pS, kb=kb, G=G: e.matmul(pS[:], identb[:], maskb[:, kb - 4 * G, :], start=False, stop=True), r=[r_identb, r_mask], w=[r_pS])
                    S.op("act", lambda e, pS=pS, PTt=PTt, kb=kb, h=h: e.activation(out=PTt[:], in_=pS[:], func=AF.Exp, bias=biasG[:, kb, h:h + 1], scale=1.0), r=[r_pS, r_bias], w=[r_PT])
                    S.op("pe", lambda e, pO=pO, PTt=PTt, kb=kb, h=h, nkb=nkb: e.matmul(pO[0:65, :], Va[:, kb, h, :], PTt[:], start=(kb == 0), stop=(kb == nkb - 1)), r=[r_V[gk], r_Vones, r_PT], w=[r_pO])
                pB_, r_pB = PB[5]
                ygt, r_yg = yg[h % 2]
                S.op("act", lambda e, pO=pO: e.copy(out=oT[:], in_=pO[0:64, :]), r=[r_pO], w=[r_oT])
                S.op("dve", lambda e, pO=pO: e.tensor_copy(out=dn[64:65, :], in_=pO[64:65, :]), r=[r_pO], w=[r_dn])
                S.op("pe", lambda e: e.matmul(pB_[0:64, :], C.ones[64:65, 0:64], dn[64:65, :], start=True, stop=True), r=[C.r_ones, r_dn], w=[r_pB])
                S.op("dve", lambda e, h=h: e.scalar_tensor_tensor(out=wv_[:], in0=eg[:, h, :], scalar=1.0, in1=pB_[0:64, :], op0=ALU.add, op1=ALU.mult), r=[r_eg[h], r_pB], w=[r_wv_])
                S.op("dve", lambda e: e.reciprocal(out=wv_[:], in_=wv_[:]), r=[r_wv_], w=[r_wv_])
                S.op("dve", lambda e, ygt=ygt: e.tensor_tensor(out=ygt[:], in0=oT[:], in1=wv_[:], op=ALU.mult), r=[r_oT, r_wv_], w=[r_yg])
                tails.append(S.dma("sp", yg_d[h * 64:(h + 1) * 64, G * 512:(G + 1) * 512], ygt[:], r=[r_yg], key="yg%d" % (h % 2)))
            if debug and G == 0:
                tails.append(S.dma("sp", dbg_q, qT[:], r=[r_qT], key="dbg1"))
        if debug:
            tails.append(S.dma("sp", dbg_k, kT[:], r=r_kT, key="dbg2"))
            tails.append(S.dma("sp", dbg_F, Fcol[:], r=[r_F], key="dbg3"))
            tails.append(S.dma("sp", dbg_v, Va[:], r=r_V + [r_Vones], key="dbg4"))
        if after is not None:
            after(C, tails)
        S.emit(st, tail_waits=tails)

import math, os
SEQ_VAR = 0

NH = 8
HD = 64
C0 = math.exp(-0.5)
GN_EPS = 64 * 1e-5


class TR:
    def __init__(self, C, name, shape, dt, psum=False):
        self.t = C.ps(name, shape, dt) if psum else C.sb(name, shape, dt)
        self.r = C.R(name, excl=psum)


class _Stop(Exception):
    pass


def build_rwkv(TT=4096, debug=False, stop=None):
    nc = bass.Bass("TRN2", target_bir_lowering=False)
    dt = lambda n, s, d, k="ExternalInput": nc.dram_tensor(n, list(s), d, kind=k).ap()
    xT_d = dt("xT", [D, TT], F32)
    ins = rwkv_decl(dt)
    yT_d = dt("yT", [512, TT], BF16, "ExternalOutput")
    xv = xT_d.rearrange("(kc p) t -> p kc t", p=128)
    rwkv_phase(nc, "", lambda t0, n: xv[:, :, t0:t0 + n], ins, yT_d, TT, debug=debug, stop=stop)
    return nc


RW_BC = ["w0b", "a0b", "kkb", "kab", "rkb", "lngb", "lnbb"]


def rwkv_decl(dt, sfx=""):
    d_ = dict(cT=dt("cT" + sfx, [128, 8], F32), adaw=dt("adaw" + sfx, [D, 2 * D], F32), adab=dt("adab" + sfx, [128, 16], F32), mu=dt("mu" + sfx, [128, 48], F32),
              wr=dt("wr" + sfx, [D, 512], F32), wk=dt("wk" + sfx, [D, 512], F32), wv=dt("wv" + sfx, [D, 512], F32),
              w1=dt("w1" + sfx, [D, 64], F32), a1=dt("a1" + sfx, [D, 64], F32), g1=dt("g1" + sfx, [D, 128], F32),
              w2=dt("w2" + sfx, [64, 512], F32), a2=dt("a2" + sfx, [64, 512], F32), g2=dt("g2" + sfx, [128, 512], F32))
    for n in RW_BC:
        d_[n] = dt(n + sfx, [128, 512], F32)
    return d_


def rwkv_phase(nc, pfx, xsrc_fn, ins, yT_d, TT=4096, debug=False, stop=None, after=None):
    dt = lambda n, s, d, k="ExternalInput": nc.dram_tensor(n, list(s), d, kind=k).ap()
    cT_d, adaw_d, adab_d, mu_d = ins["cT"], ins["adaw"], ins["adab"], ins["mu"]
    wr_d, wk_d, wv_d, w1_d, a1_d, g1_d, w2_d, a2_d, g2_d = [ins[k] for k in ("wr", "wk", "wv", "w1", "a1", "g1", "w2", "a2", "g2")]
    bc_names = RW_BC
    bc_d = {n: ins[n] for n in bc_names}
    GS = 256
    NTG = GS // 128
    NGR = TT // GS
    dbg = {}
    if debug:
        for n in ["r", "kp", "kkn", "a", "sw", "Y", "bonus", "Gm", "yn"]:
            dbg[n] = dt("dbg_" + n, [128, 512], F32, "ExternalOutput")
        dbg["S"] = dt("dbg_S", [128, 4, 64], F32, "ExternalOutput")
        for n, shp in (("Arb", [128, 8, 128]), ("Aak", [128, 8, 128]), ("NT0", [128, 8, 64]), ("Z0", [128, 8, 128]), ("Zf", [128, 8, 128]), ("Y0", [128, 512]), ("RbT", [128, 4, 128]), ("Qm", [128, 4, 128]), ("Hm", [128, 4, 128]), ("gcol", [128, 8]), ("v", [128, 512]), ("ART", [128, 4, 2, 128]), ("KT", [128, 4, 128]), ("BT", [128, 4, 128]), ("Bh", [128, 512]), ("Kh", [128, 512])):
            dbg[n] = dt("dbg_" + n, shp, F32, "ExternalOutput")
    with ExitStack() as st:
        C = Ctx(nc, st, pfx)
        S = C.S
        make_consts(C)
        ident, ones = C.ident, C.ones
        r_ident, r_ones = C.r_ident, C.r_ones
        tails = []
        identb = TR(C, "identb", [128, 128], BF16)
        S.op("dve", lambda e: e.tensor_copy(out=identb.t[:], in_=ident[:]), r=[r_ident], w=[identb.r])
        U = TR(C, "U", [128, 128], F32)
        IU = TR(C, "IUf", [128, 128], F32)
        S.op("pool", lambda e: e.memset(U.t[:], 1.0), w=[U.r])
        S.op("pool", lambda e: e.affine_select(out=U.t[:], in_=U.t[:], pattern=[[1, 128]], compare_op=ALU.is_gt, fill=0.0, base=0, channel_multiplier=-1), r=[U.r], w=[U.r])
        S.op("pool", lambda e: e.memset(IU.t[:], 1.0), w=[IU.r])
        S.op("pool", lambda e: e.affine_select(out=IU.t[:], in_=IU.t[:], pattern=[[1, 128]], compare_op=ALU.is_ge, fill=0.0, base=0, channel_multiplier=-1), r=[IU.r], w=[IU.r])
        SLf = TR(C, "SLf", [128, 128], F32)
        S.op("pool", lambda e: e.memset(SLf.t[:], 1.0), w=[SLf.r])
        S.op("pool", lambda e: e.affine_select(out=SLf.t[:], in_=SLf.t[:], pattern=[[-1, 128]], compare_op=ALU.is_gt, fill=0.0, base=0, channel_multiplier=1), r=[SLf.r], w=[SLf.r])
        mask1 = TR(C, "mask1", [128, 128], F32)
        maskL = TR(C, "maskL", [128, 64], F32)
        idl = TR(C, "idl", [128, 64], F32)
        for ch in range(2):
            ps_ = slice(ch * 64, ch * 64 + 64)
            S.op("dve", lambda e, ps_=ps_: e.tensor_copy(out=mask1.t[ps_, 0:64], in_=U.t[ps_, ps_]), r=[U.r], w=[mask1.r])
            S.op("dve", lambda e, ps_=ps_: e.tensor_copy(out=mask1.t[ps_, 64:128], in_=IU.t[ps_, ps_]), r=[IU.r], w=[mask1.r])
            S.op("dve", lambda e, ps_=ps_: e.tensor_copy(out=maskL.t[ps_, :], in_=SLf.t[ps_, ps_]), r=[SLf.r], w=[maskL.r])
            S.op("dve", lambda e, ps_=ps_: e.tensor_copy(out=idl.t[ps_, :], in_=ident[ps_, ps_]), r=[r_ident], w=[idl.r])
        triC = TR(C, "triC", [128, 128], F32)
        blkC = TR(C, "blkC", [128, 128], F32)
        S.op("dve", lambda e: e.tensor_scalar(out=triC.t[:], in0=IU.t[:], scalar1=-C0, scalar2=None, op0=ALU.mult), r=[IU.r], w=[triC.r])
        S.op("dve", lambda e: e.memset(triC.t[0:64, 64:128], 0.0), w=[triC.r])
        S.op("dve", lambda e: e.memset(blkC.t[:], -C0), w=[blkC.r])
        S.op("dve", lambda e: e.memset(blkC.t[0:64, 64:128], 0.0), w=[blkC.r])
        S.op("dve", lambda e: e.memset(blkC.t[64:128, 0:64], 0.0), w=[blkC.r])
        avg = TR(C, "avg", [128, 1], F32)
        S.op("dve", lambda e: e.memset(avg.t[:], 1.0 / 64.0), w=[avg.r])
        PB = [TR(C, "pb%d" % i, [128, 512], F32, psum=True) for i in range(7)]
        pbb = TR(C, "pbb", [128, 1024], BF16, psum=True)
        wA = [(C.sb("wA%d" % i, [128, 8, 256], F32), C.R("wA%d" % i)) for i in range(2)]
        mods, r_mods = compute_mods(C, cT_d, adaw_d, adab_d, 2, wA, PB[6].t, PB[6].r)
        cols = TR(C, "cols", [128, 8], F32)
        S.op("dve", lambda e: e.tensor_scalar(out=cols.t[:], in0=mods[:, 8:16], scalar1=1.0, scalar2=None, op0=ALU.add), r=[r_mods], w=[cols.r])
        mu = TR(C, "mu", [128, 48], F32)
        S.dma("sp", mu.t[:], mu_d, w=[mu.r], key="mu")
        W = {}
        r_w = C.R("wts")
        for nm, d_, shp in (("wr", wr_d, [128, 8, 512]), ("wk", wk_d, [128, 8, 512]), ("wv", wv_d, [128, 8, 512]), ("w1", w1_d, [128, 8, 64]), ("a1", a1_d, [128, 8, 64]), ("g1", g1_d, [128, 8, 128])):
            W[nm] = C.sb(nm, shp, BF16)
            S.dma("pool", W[nm][:], d_.rearrange("(kc p) n -> p kc n", p=128), w=[r_w], key=nm)
        for nm, d_, shp in (("w2", w2_d, [64, 512]), ("a2", a2_d, [64, 512]), ("g2", g2_d, [128, 512])):
            W[nm] = C.sb(nm, shp, BF16)
            S.dma("pool", W[nm][:], d_, w=[r_w], key=nm)
        BC = {}
        r_bc = C.R("bc")
        for n in bc_names:
            BC[n] = C.sb(n, [128, 512], F32)
            S.dma("sp", BC[n][:], bc_d[n], w=[r_bc], key=n)
        xg = TR(C, "xg", [128, 8, GS + 1], F32)
        S.op("dve", lambda e: e.memset(xg.t[:, :, 0:1], 0.0), w=[xg.r])
        dx = TR(C, "dx", [128, 8, GS], F32)
        xs = [TR(C, "xs%d" % i, [128, 8, GS], BF16) for i in range(2)]
        l1 = [TR(C, "l1_%d" % i, [128, GS], BF16) for i in range(3)]
        l1f = TR(C, "l1f", [128, GS], F32)
        G_ = {n: TR(C, "G_" + n, [128, NTG, 512], F32) for n in ("r", "k", "v")}
        names = ["sw", "Gm", "Gi", "Gp", "GC", "a", "kkn", "kp", "t1", "t2", "Bt", "Kt", "Bh", "Kh", "bonus", "Y", "yn", "gg"]
        X = {n: TR(C, "X_" + n, [128, 512], F32) for n in names}
        sm = TR(C, "sm", [128, 64], F32)
        Z = [TR(C, "Z%d" % i, [128, 8, 128], F32) for i in range(2)]
        Rt = TR(C, "Rt", [128, 512], F32)
        ART = TR(C, "ART", [128, 4, 2, 128], F32)
        KT = TR(C, "KT", [128, 4, 128], F32)
        BT = TR(C, "BT", [128, 4, 128], F32)
        Nm = [TR(C, "Nm%d" % i, [128, 8, 64], F32) for i in range(2)]
        NTm = [TR(C, "NTm%d" % i, [128, 8, 64], F32) for i in range(2)]
        Arb = TR(C, "Arb", [128, 8, 128], F32)
        Aak = TR(C, "Aak", [128, 8, 128], F32)
        RbT = TR(C, "RbT", [128, 4, 128], F32)
        Qm = TR(C, "Qm", [128, 4, 128], F32)
        Hm = TR(C, "Hm", [128, 4, 128], F32)
        gcol = TR(C, "gcol", [128, 8], F32)
        Y0 = TR(C, "Y0", [128, 512], F32)
        ST = [TR(C, "ST%d" % i, [128, 4, 64], F32) for i in range(2)]
        S.op("dve", lambda e: e.memset(ST[0].t[:], 0.0), w=[ST[0].r])
        ybf = TR(C, "ybf", [128, 512], BF16)
        yTo = [TR(C, "yTo%d" % i, [128, 4, 128], BF16) for i in range(2)]
        ysrc = yT_d.rearrange("(a p) t -> p a t", p=128)

        def hv(t):
            return t.rearrange("p (h d) -> p h d", d=64)

        def bc8(ap):
            return ap.unsqueeze(2).to_broadcast([128, 8, 64])

        st_i = 0
        try:
          for G in range(NGR):
              if G > 0:
                  S.op("dve", lambda e: e.tensor_copy(out=xg.t[:, :, 0:1], in_=xg.t[:, :, GS:GS + 1]), r=[xg.r], w=[xg.r])
              S.dma("sp", xg.t[:, :, 1:GS + 1], xsrc_fn(G * GS, GS), r=[xg.r], w=[xg.r], key="xg")
              for kc in range(8):
                  S.op("dve", lambda e, kc=kc: e.tensor_scalar(out=xg.t[:, kc, 1:GS + 1], in0=xg.t[:, kc, 1:GS + 1], scalar1=cols.t[:, kc:kc + 1], scalar2=mods[:, kc:kc + 1], op0=ALU.mult, op1=ALU.add), r=[xg.r, cols.r, r_mods], w=[xg.r])
              S.op("dve", lambda e: e.tensor_tensor(out=dx.t[:], in0=xg.t[:, :, 0:GS], in1=xg.t[:, :, 1:GS + 1], op=ALU.subtract), r=[xg.r], w=[dx.r])
              for n, nm in enumerate(("r", "k", "v", "w", "a", "g")):
                  xb = xs[n % 2]
                  for kc in range(8):
                      eng = "dve"
                      S.op(eng, lambda e, kc=kc, n=n, xb=xb: e.scalar_tensor_tensor(out=xb.t[:, kc, :], in0=dx.t[:, kc, :], scalar=mu.t[:, n * 8 + kc:n * 8 + kc + 1], in1=xg.t[:, kc, 1:GS + 1], op0=ALU.mult, op1=ALU.add), r=[dx.r, mu.r, xg.r], w=[xb.r])
                  if n < 3:
                      wt = W[("wr", "wk", "wv")[n]]
                      for i in range(NTG):
                          pb = PB[(n * NTG + i) % 2]
                          for kc in range(8):
                              S.op("pe", lambda e, pb=pb, xb=xb, wt=wt, i=i, kc=kc: e.matmul(pb.t[:], xb.t[:, kc, i * 128:(i + 1) * 128], wt[:, kc, :], start=(kc == 0), stop=(kc == 7)), r=[xb.r, r_w], w=[pb.r])
                          S.op("act", lambda e, pb=pb, nm=nm, i=i: e.copy(out=G_[nm].t[:, i, :], in_=pb.t[:]), r=[pb.r], w=[G_[nm].r])
                  else:
                      w1n, w2n, rows = (("w1", "w2", 64), ("a1", "a2", 64), ("g1", "g2", 128))[n - 3]
                      pb = PB[2]
                      for kc in range(8):
                          S.op("pe", lambda e, pb=pb, xb=xb, w1n=w1n, rows=rows, kc=kc: e.matmul(pb.t[0:rows, 0:GS], W[w1n][:, kc, :], xb.t[:, kc, :], start=(kc == 0), stop=(kc == 7)), r=[xb.r, r_w], w=[pb.r])
                      lt = l1[n - 3]
                      if nm == "w":
                          S.op("act", lambda e, pb=pb: e.activation(out=l1f.t[0:64, :], in_=pb.t[0:64, 0:GS], func=AF.Exp, scale=-2.0), r=[pb.r, l1f.r], w=[l1f.r])
                          S.op("dve", lambda e: e.tensor_scalar(out=l1f.t[0:64, :], in0=l1f.t[0:64, :], scalar1=1.0, scalar2=None, op0=ALU.add), r=[l1f.r], w=[l1f.r])
                          S.op("dve", lambda e: e.reciprocal(out=l1f.t[0:64, :], in_=l1f.t[0:64, :]), r=[l1f.r], w=[l1f.r])
                          S.op("dve", lambda e, lt=lt: e.tensor_scalar(out=lt.t[0:64, :], in0=l1f.t[0:64, :], scalar1=2.0, scalar2=-1.0, op0=ALU.mult, op1=ALU.add), r=[l1f.r], w=[lt.r])
                      elif nm == "a":
                          S.op("act", lambda e, pb=pb, lt=lt: e.copy(out=lt.t[0:64, :], in_=pb.t[0:64, 0:GS]), r=[pb.r], w=[lt.r])
                      else:
                          S.op("act", lambda e, pb=pb: e.activation(out=l1f.t[:], in_=pb.t[:, 0:GS], func=AF.Exp, scale=-1.0), r=[pb.r, l1f.r], w=[l1f.r])
                          S.op("dve", lambda e: e.tensor_scalar(out=l1f.t[:], in0=l1f.t[:], scalar1=1.0, scalar2=None, op0=ALU.add), r=[l1f.r], w=[l1f.r])
                          S.op("dve", lambda e: e.reciprocal(out=l1f.t[:], in_=l1f.t[:]), r=[l1f.r], w=[l1f.r])
                          S.op("act", lambda e, lt=lt: e.copy(out=lt.t[:], in_=l1f.t[:]), r=[l1f.r], w=[lt.r])
              if stop == 1:
                  raise _Stop()
              for i in range(NTG):
                  tl_ = G * NTG + i
                  r_ = G_["r"].t[:, i, :]; k_ = G_["k"].t[:, i, :]; v_ = G_["v"].t[:, i, :]
                  rG = [G_[n].r for n in G_]
                  sw, Gm, Gi, Gp, GC, a_, kkn, kp, t1, t2 = [X[n] for n in ("sw", "Gm", "Gi", "Gp", "GC", "a", "kkn", "kp", "t1", "t2")]
                  Bt, Kt, Bh, Kh, bonus, Yt, yn = [X[n] for n in ("Bt", "Kt", "Bh", "Kh", "bonus", "Y", "yn")]
                  pzw, pza, pgg = PB[4], PB[5], PB[6]
                  for (pz_, lt, w2n, rows) in ((pzw, l1[0], "w2", 64), (pza, l1[1], "a2", 64), (pgg, l1[2], "g2", 128)):
                      S.op("pe", lambda e, pz_=pz_, lt=lt, w2n=w2n, rows=rows, i=i: e.matmul(pz_.t[:], lt.t[0:rows, i * 128:(i + 1) * 128], W[w2n][0:rows, :], start=True, stop=True), r=[lt.r, r_w], w=[pz_.r])
                  S.op("act", lambda e: e.copy(out=X["gg"].t[:], in_=pgg.t[:]), r=[pgg.r], w=[X["gg"].r])
                  S.op("dve", lambda e: e.tensor_tensor(out=t1.t[:], in0=pzw.t[:], in1=BC["w0b"][:], op=ALU.add), r=[pzw.r, r_bc], w=[t1.r])
                  S.op("act", lambda e: e.activation(out=t1.t[:], in_=t1.t[:], func=AF.Exp, scale=-1.0), r=[t1.r], w=[t1.r])
                  S.op("dve", lambda e: e.tensor_scalar(out=t1.t[:], in0=t1.t[:], scalar1=1.0, scalar2=None, op0=ALU.add), r=[t1.r], w=[t1.r])
                  S.op("dve", lambda e: e.reciprocal(out=sw.t[:], in_=t1.t[:]), r=[t1.r], w=[sw.r])
                  pL, pLC = PB[2], PB[3]
                  S.op("pe", lambda e: e.matmul(pL.t[:], triC.t[:], sw.t[:], start=True, stop=True), r=[triC.r, sw.r], w=[pL.r])
                  S.op("pe", lambda e: e.matmul(pLC.t[:], blkC.t[:], sw.t[:], start=True, stop=True), r=[blkC.r, sw.r], w=[pLC.r])
                  S.op("act", lambda e: e.activation(out=Gm.t[:], in_=pL.t[:], func=AF.Exp), r=[pL.r], w=[Gm.r])
                  S.op("act", lambda e: e.activation(out=Gi.t[:], in_=pL.t[:], func=AF.Exp, scale=-1.0), r=[pL.r], w=[Gi.r])
                  S.op("dve", lambda e: e.scalar_tensor_tensor(out=Gp.t[:], in0=sw.t[:], scalar=C0, in1=pL.t[:], op0=ALU.mult, op1=ALU.add), r=[sw.r, pL.r], w=[Gp.r])
                  S.op("act", lambda e: e.activation(out=Gp.t[:], in_=Gp.t[:], func=AF.Exp), r=[Gp.r], w=[Gp.r])
                  S.op("act", lambda e: e.activation(out=GC.t[:], in_=pLC.t[:], func=AF.Exp), r=[pLC.r], w=[GC.r])
                  S.op("dve", lambda e: e.tensor_tensor(out=t1.t[:], in0=pza.t[:], in1=BC["a0b"][:], op=ALU.add), r=[pza.r, r_bc, t1.r], w=[t1.r])
                  S.op("act", lambda e: e.activation(out=t1.t[:], in_=t1.t[:], func=AF.Exp, scale=-1.0), r=[t1.r], w=[t1.r])
                  S.op("dve", lambda e: e.tensor_scalar(out=t1.t[:], in0=t1.t[:], scalar1=1.0, scalar2=None, op0=ALU.add), r=[t1.r], w=[t1.r])
                  S.op("dve", lambda e: e.reciprocal(out=a_.t[:], in_=t1.t[:]), r=[t1.r], w=[a_.r])
                  S.op("dve", lambda e, k_=k_: e.tensor_tensor(out=kkn.t[:], in0=k_, in1=BC["kkb"][:], op=ALU.mult), r=rG + [r_bc], w=[kkn.r])
                  S.op("act", lambda e: e.activation(out=t2.t[:], in_=kkn.t[:], func=AF.Square), r=[kkn.r], w=[t2.r])
                  S.op("dve", lambda e: e.tensor_reduce(out=sm.t[:, 0:8], in_=hv(t2.t[:]), axis=AX.X, op=ALU.add), r=[t2.r], w=[sm.r])
                  S.op("dve", lambda e: e.tensor_scalar(out=sm.t[:, 0:8], in0=sm.t[:, 0:8], scalar1=1e-24, scalar2=None, op0=ALU.max), r=[sm.r], w=[sm.r])
                  S.op("act", lambda e: e.activation(out=sm.t[:, 0:8], in_=sm.t[:, 0:8], func=AF.Sqrt), r=[sm.r], w=[sm.r])
                  S.op("dve", lambda e: e.reciprocal(out=sm.t[:, 0:8], in_=sm.t[:, 0:8]), r=[sm.r], w=[sm.r])
                  S.op("dve", lambda e: e.tensor_tensor(out=hv(kkn.t[:]), in0=hv(kkn.t[:]), in1=bc8(sm.t[:, 0:8]), op=ALU.mult), r=[kkn.r, sm.r], w=[kkn.r])
                  S.op("dve", lambda e: e.scalar_tensor_tensor(out=t2.t[:], in0=a_.t[:], scalar=-1.0, in1=BC["kab"][:], op0=ALU.add, op1=ALU.mult), r=[a_.r, r_bc, t2.r], w=[t2.r])
                  S.op("dve", lambda e, k_=k_: e.scalar_tensor_tensor(out=kp.t[:], in0=t2.t[:], scalar=1.0, in1=k_, op0=ALU.add, op1=ALU.mult), r=[t2.r] + rG, w=[kp.r])
                  S.op("dve", lambda e: e.scalar_tensor_tensor(out=Z[0].t[:, :, 0:64], in0=hv(kkn.t[:]), scalar=-1.0, in1=hv(Gp.t[:]), op0=ALU.mult, op1=ALU.mult), r=[kkn.r, Gp.r], w=[Z[0].r])
                  S.op("pool", lambda e: e.tensor_tensor(out=Bt.t[:], in0=kkn.t[:], in1=a_.t[:], op=ALU.mult), r=[kkn.r, a_.r], w=[Bt.r])
                  S.op("pool", lambda e: e.tensor_tensor(out=Bt.t[:], in0=Bt.t[:], in1=Gi.t[:], op=ALU.mult), r=[Bt.r, Gi.r], w=[Bt.r])
                  S.op("dve", lambda e: e.tensor_tensor(out=Kt.t[:], in0=kp.t[:], in1=Gi.t[:], op=ALU.mult), r=[kp.r, Gi.r], w=[Kt.r])
                  S.op("dve", lambda e, r_=r_: e.tensor_tensor(out=Rt.t[:], in0=r_, in1=Gm.t[:], op=ALU.mult), r=rG + [Gm.r], w=[Rt.r])
                  S.op("pool", lambda e: e.tensor_tensor(out=Bh.t[:], in0=Bt.t[:], in1=GC.t[:], op=ALU.mult), r=[Bt.r, GC.r], w=[Bh.r])
                  S.op("pool", lambda e: e.tensor_tensor(out=Kh.t[:], in0=Kt.t[:], in1=GC.t[:], op=ALU.mult), r=[Kt.r, GC.r], w=[Kh.r])
                  S.op("dve", lambda e, r_=r_: e.tensor_tensor(out=t2.t[:], in0=r_, in1=kp.t[:], op=ALU.mult), r=rG + [kp.r, t2.r], w=[t2.r])
                  S.op("dve", lambda e: e.tensor_tensor(out=t2.t[:], in0=t2.t[:], in1=BC["rkb"][:], op=ALU.mult), r=[t2.r, r_bc], w=[t2.r])
                  S.op("dve", lambda e: e.tensor_reduce(out=sm.t[:, 8:16], in_=hv(t2.t[:]), axis=AX.X, op=ALU.add), r=[t2.r], w=[sm.r])
                  S.op("dve", lambda e, v_=v_: e.tensor_tensor(out=hv(bonus.t[:]), in0=hv(v_), in1=bc8(sm.t[:, 8:16]), op=ALU.mult), r=rG + [sm.r], w=[bonus.r])
                  if stop == 2:
                      raise _Stop()
                  S.op("pool", lambda e: e.tensor_copy(out=hv(t1.t[:]), in_=Z[0].t[:, :, 0:64]), r=[Z[0].r, t1.r], w=[t1.r])
                  for p in range(4):
                      pT = PB[p % 2]
                      for q_, src in enumerate((t1, Rt, Kt, Bt)):
                          S.op("pe", lambda e, pT=pT, p=p, q_=q_, src=src: e.transpose(pT.t[:, q_ * 128:(q_ + 1) * 128], src.t[:, p * 128:(p + 1) * 128], ident[:]), r=[src.r, r_ident], w=[pT.r])
                      S.op("dve", lambda e, pT=pT, p=p: e.tensor_copy(out=ART.t[:, p, :, 0:64], in_=pT.t[:, 0:128].rearrange("p (c t) -> p c t", t=64)), r=[pT.r], w=[ART.r])
                      S.op("act", lambda e, pT=pT, p=p: e.copy(out=ART.t[:, p, :, 64:128], in_=pT.t[:, 128:256].rearrange("p (c t) -> p c t", t=64)), r=[pT.r], w=[ART.r])
                      S.op("act", lambda e, pT=pT, p=p: e.copy(out=KT.t[:, p, :], in_=pT.t[:, 256:384]), r=[pT.r], w=[KT.r])
                      S.op("dve", lambda e, pT=pT, p=p: e.tensor_copy(out=BT.t[:, p, :], in_=pT.t[:, 384:512]), r=[pT.r], w=[BT.r])
                  if stop == 3:
                      raise _Stop()
                  pS1a, pS1b, pS2a, pS2b, pS3 = PB[0], PB[1], PB[2], PB[3], PB[4]
                  for h in range(8):
                      p, e_ = h // 2, h % 2
                      fe = slice(e_ * 64, e_ * 64 + 64)
                      pa, pb_ = (pS1a, pS2a) if h < 4 else (pS1b, pS2b)
                      hc = (h % 4) * 128
                      for ch in range(2):
                          tc_ = slice(ch * 64, ch * 64 + 64)
                          S.op("pe", lambda e, pa=pa, fe=fe, p=p, ch=ch, tc_=tc_, hc=hc: e.matmul(pa.t[tc_, hc:hc + 128], BT.t[fe, p, tc_], ART.t[fe, p, ch, :], start=True, stop=True), r=[BT.r, ART.r], w=[pa.r])
                          S.op("pe", lambda e, pb_=pb_, fe=fe, p=p, ch=ch, tc_=tc_, hc=hc: e.matmul(pb_.t[tc_, hc:hc + 128], KT.t[fe, p, tc_], ART.t[fe, p, ch, :], start=True, stop=True), r=[KT.r, ART.r], w=[pb_.r])
                          S.op("pe", lambda e, fe=fe, p=p, ch=ch, tc_=tc_, h=h: e.matmul(pS3.t[tc_, h * 64:(h + 1) * 64], ART.t[fe, p, ch, 0:64], BT.t[fe, p, tc_], start=True, stop=True), r=[BT.r, ART.r], w=[pS3.r])
                  m1b = mask1.t[:].unsqueeze(1).to_broadcast([128, 4, 128])
                  for half, (pa, pb_) in enumerate(((pS1a, pS2a), (pS1b, pS2b))):
                      hs = slice(half * 4, half * 4 + 4)
                      S.op("dve", lambda e, pa=pa, hs=hs: e.tensor_tensor(out=Arb.t[:, hs, :], in0=pa.t[:].rearrange("p (h t) -> p h t", t=128), in1=m1b, op=ALU.mult), r=[pa.r, mask1.r], w=[Arb.r])
                      S.op("dve", lambda e, pb_=pb_, hs=hs: e.tensor_tensor(out=Aak.t[:, hs, :], in0=pb_.t[:].rearrange("p (h t) -> p h t", t=128), in1=m1b, op=ALU.mult), r=[pb_.r, mask1.r], w=[Aak.r])
                  S.op("dve", lambda e: e.tensor_tensor(out=NTm[0].t[:], in0=hv(pS3.t[:]), in1=maskL.t[:].unsqueeze(1).to_broadcast([128, 8, 64]), op=ALU.mult), r=[pS3.r, maskL.r], w=[NTm[0].r])
                  S.op("act", lambda e: e.copy(out=Nm[0].t[:], in_=Arb.t[:, :, 0:64]), r=[Arb.r], w=[Nm[0].r])
                  if stop == 4:
                      raise _Stop()
                  if debug and tl_ == 0:
                      tails.append(S.dma("sp", dbg["Arb"], Arb.t[:], r=[Arb.r], key="dbg_Arb"))
                  if debug and tl_ == 0:
                      tails.append(S.dma("sp", dbg["Aak"], Aak.t[:], r=[Aak.r], key="dbg_Aak"))
                  if debug and tl_ == 0:
                      tails.append(S.dma("sp", dbg["NT0"], NTm[0].t[:], r=[NTm[0].r], key="dbg_NT0"))
                  if debug and tl_ == 0:
                      tails.append(S.dma("sp", dbg["ART"], ART.t[:], r=[ART.r], key="dbg_ART"))
                  if debug and tl_ == 0:
                      tails.append(S.dma("sp", dbg["KT"], KT.t[:], r=[KT.r], key="dbg_KT"))
                  if debug and tl_ == 0:
                      tails.append(S.dma("sp", dbg["BT"], BT.t[:], r=[BT.r], key="dbg_BT"))
                  if debug and tl_ == 0:
                      tails.append(S.dma("sp", dbg["Bh"], Bh.t[:], r=[Bh.r], key="dbg_Bh"))
                  if debug and tl_ == 0:
                      tails.append(S.dma("sp", dbg["Kh"], Kh.t[:], r=[Kh.r], key="dbg_Kh"))
                  pX = PB[5]
                  for h in range(8):
                      for ch in range(2):
                          tc_ = slice(ch * 64, ch * 64 + 64)
                          S.op("pe", lambda e, h=h, tc_=tc_, v_=v_: e.matmul(pX.t[tc_, h * 64:(h + 1) * 64], Aak.t[tc_, h, 0:64], v_[tc_, h * 64:(h + 1) * 64], start=True, stop=True), r=[Aak.r] + rG, w=[pX.r])
                  S.op("act", lambda e: e.copy(out=Z[0].t[:, :, 64:128], in_=hv(pX.t[:])), r=[pX.r], w=[Z[0].r])
                  if stop == 5:
                      raise _Stop()
                  if debug and tl_ == 0:
                      tails.append(S.dma("sp", dbg["Z0"], Z[0].t[:], r=[Z[0].r], key="dbg_Z0"))
                  zi = 0
                  for lev in range(6):
                      Nc, NTc = Nm[lev % 2], NTm[lev % 2]
                      Nn, NTn = Nm[(lev + 1) % 2], NTm[(lev + 1) % 2]
                      pZa, pZb = PB[0], PB[1]
                      Zc, Zn = Z[zi], Z[1 - zi]
                      for h in range(8):
                          pz = pZa if h < 4 else pZb
                          hc = (h % 4) * 128
                          for ch in range(2):
                              tc_ = slice(ch * 64, ch * 64 + 64)
                              S.op("pe", lambda e, pz=pz, Nc=Nc, Zc=Zc, h=h, hc=hc, tc_=tc_: e.matmul(pz.t[tc_, hc:hc + 128], Nc.t[tc_, h, :], Zc.t[tc_, h, :], start=True, stop=True), r=[Nc.r, Zc.r], w=[pz.r])
                      S.op("dve", lambda e, Zc=Zc, Zn=Zn: e.tensor_tensor(out=Zn.t[:, 0:4, :], in0=Zc.t[:, 0:4, :], in1=pZa.t[:].rearrange("p (h t) -> p h t", t=128), op=ALU.add), r=[Zc.r, pZa.r], w=[Zn.r])
                      S.op("dve", lambda e, Zc=Zc, Zn=Zn: e.tensor_tensor(out=Zn.t[:, 4:8, :], in0=Zc.t[:, 4:8, :], in1=pZb.t[:].rearrange("p (h t) -> p h t", t=128), op=ALU.add), r=[Zc.r, pZb.r], w=[Zn.r])
                      zi = 1 - zi
                      if lev < 5:
                          pN, pNT = PB[2], PB[3]
                          for h in range(8):
                              for ch in range(2):
                                  tc_ = slice(ch * 64, ch * 64 + 64)
                                  S.op("pe", lambda e, NTc=NTc, Nc=Nc, h=h, tc_=tc_: e.matmul(pN.t[tc_, h * 64:(h + 1) * 64], NTc.t[tc_, h, :], Nc.t[tc_, h, :], start=True, stop=True), r=[Nc.r, NTc.r], w=[pN.r])
                                  if lev < 4:
                                      S.op("pe", lambda e, NTc=NTc, Nc=Nc, h=h, tc_=tc_: e.matmul(pNT.t[tc_, h * 64:(h + 1) * 64], Nc.t[tc_, h, :], NTc.t[tc_, h, :], start=True, stop=True), r=[Nc.r, NTc.r], w=[pNT.r])
                          S.op("act", lambda e, Nn=Nn: e.copy(out=Nn.t[:], in_=hv(pN.t[:])), r=[pN.r], w=[Nn.r])
                          if lev < 4:
                              S.op("act", lambda e, NTn=NTn: e.copy(out=NTn.t[:], in_=hv(pNT.t[:])), r=[pNT.r], w=[NTn.r])
                  if stop == 6:
                      raise _Stop()
                  Zf = Z[zi]
                  pY0 = PB[4]
                  for h in range(8):
                      for ch in range(2):
                          tc_ = slice(ch * 64, ch * 64 + 64)
                          S.op("pe", lambda e, h=h, tc_=tc_: e.matmul(pY0.t[tc_, h * 64:(h + 1) * 64], Arb.t[tc_, h, 64:128], Zf.t[tc_, h, 64:128], start=True, stop=False), r=[Arb.r, Zf.r], w=[pY0.r])
                          S.op("pe", lambda e, h=h, tc_=tc_, v_=v_: e.matmul(pY0.t[tc_, h * 64:(h + 1) * 64], Aak.t[tc_, h, 64:128], v_[tc_, h * 64:(h + 1) * 64], start=False, stop=True), r=[Aak.r] + rG, w=[pY0.r])
                  S.op("act", lambda e: e.copy(out=Y0.t[:], in_=pY0.t[:]), r=[pY0.r], w=[Y0.r])
                  if stop == 7:
                      raise _Stop()
                  if debug and tl_ == 0:
                      tails.append(S.dma("sp", dbg["Zf"], Zf.t[:], r=[Zf.r], key="dbg_Zf"))
                  if debug and tl_ == 0:
                      tails.append(S.dma("sp", dbg["Y0"], Y0.t[:], r=[Y0.r], key="dbg_Y0"))
                  pR, pQ, pH, pg = PB[5], PB[6], PB[0], PB[1]
                  for h in range(8):
                      p, e_ = h // 2, h % 2
                      fe = slice(e_ * 64, e_ * 64 + 64)
                      for ch in range(2):
                          tc_ = slice(ch * 64, ch * 64 + 64)
                          cs = slice(p * 128 + ch * 64, p * 128 + ch * 64 + 64)
                          hs_ = slice(h * 64, h * 64 + 64)
                          S.op("pe", lambda e, fe=fe, cs=cs, tc_=tc_, h=h: e.matmul(pR.t[fe, cs], Zf.t[tc_, h, 0:64], Arb.t[tc_, h, 64:128], start=True, stop=True), r=[Zf.r, Arb.r], w=[pR.r])
                          S.op("pe", lambda e, fe=fe, cs=cs, tc_=tc_, h=h, hs_=hs_: e.matmul(pQ.t[fe, cs], Zf.t[tc_, h, 0:64], Bh.t[tc_, hs_], start=True, stop=True), r=[Zf.r, Bh.r], w=[pQ.r])
                          S.op("pe", lambda e, fe=fe, cs=cs, tc_=tc_, h=h, hs_=hs_: e.matmul(pH.t[fe, cs], Bh.t[tc_, hs_], Zf.t[tc_, h, 64:128], start=True, stop=False), r=[Zf.r, Bh.r], w=[pH.r])
                          S.op("pe", lambda e, fe=fe, cs=cs, tc_=tc_, hs_=hs_, v_=v_: e.matmul(pH.t[fe, cs], Kh.t[tc_, hs_], v_[tc_, hs_], start=False, stop=True), r=[Kh.r] + rG, w=[pH.r])
                          S.op("pe", lambda e, fe=fe, tc_=tc_, hs_=hs_, p=p, ch=ch: e.matmul(pg.t[fe, p * 2 + ch:p * 2 + ch + 1], GC.t[tc_, hs_], avg.t[tc_, :], start=True, stop=True), r=[GC.r, avg.r], w=[pg.r])
                  for ch in range(2):
                      S.op("dve", lambda e, ch=ch: e.tensor_tensor(out=RbT.t[:, :, ch * 64:(ch + 1) * 64], in0=pR.t[:].rearrange("p (a t) -> p a t", t=128)[:, :, ch * 64:(ch + 1) * 64], in1=ART.t[:, :, ch, 64:128], op=ALU.add), r=[pR.r, ART.r], w=[RbT.r])
                  S.op("act", lambda e: e.copy(out=Qm.t[:], in_=pQ.t[:].rearrange("p (a t) -> p a t", t=128)), r=[pQ.r], w=[Qm.r])
                  S.op("act", lambda e: e.copy(out=Hm.t[:], in_=pH.t[:].rearrange("p (a t) -> p a t", t=128)), r=[pH.r], w=[Hm.r])
                  S.op("act", lambda e: e.copy(out=gcol.t[:], in_=pg.t[:, 0:8]), r=[pg.r], w=[gcol.r])
                  if stop == 8:
                      raise _Stop()
                  if debug and tl_ == 0:
                      tails.append(S.dma("sp", dbg["RbT"], RbT.t[:], r=[RbT.r], key="dbg_RbT"))
                  if debug and tl_ == 0:
                      tails.append(S.dma("sp", dbg["Qm"], Qm.t[:], r=[Qm.r], key="dbg_Qm"))
                  if debug and tl_ == 0:
                      tails.append(S.dma("sp", dbg["Hm"], Hm.t[:], r=[Hm.r], key="dbg_Hm"))
                  if debug and tl_ == 0:
                      tails.append(S.dma("sp", dbg["gcol"], gcol.t[:], r=[gcol.r], key="dbg_gcol"))
                  for ch in range(2):
                      tc_ = slice(ch * 64, ch * 64 + 64)
                      Sc, Sn = ST[st_i], ST[1 - st_i]
                      pYe, pS_ = (PB[2], PB[4]), PB[3]
                      for h in range(8):
                          p, e_ = h // 2, h % 2
                          fe = slice(e_ * 64, e_ * 64 + 64)
                          cs = slice(ch * 64, ch * 64 + 64)
                          pY = pYe[e_]
                          S.op("pe", lambda e, fe=fe, p=p, cs=cs, tc_=tc_, h=h, Sc=Sc, pY=pY: e.matmul(pY.t[tc_, h * 64:(h + 1) * 64], RbT.t[fe, p, cs], Sc.t[fe, p, :], start=True, stop=True), r=[RbT.r, Sc.r], w=[pY.r])
                          S.op("pe", lambda e, fe=fe, p=p, cs=cs, Sc=Sc: e.matmul(pS_.t[fe, p * 64:(p + 1) * 64], Qm.t[fe, p, cs], Sc.t[fe, p, :], start=True, stop=True), r=[Qm.r, Sc.r], w=[pS_.r])
                      v4 = lambda ap: ap.rearrange("p (a e d) -> p a e d", e=2, d=64)
                      for e_ in range(2):
                          S.op("dve", lambda e, tc_=tc_, e_=e_: e.tensor_tensor(out=v4(Yt.t[tc_, :])[:, :, e_, :], in0=v4(Y0.t[tc_, :])[:, :, e_, :], in1=v4(pYe[e_].t[tc_, :])[:, :, e_, :], op=ALU.add), r=[Y0.r, pYe[e_].r], w=[Yt.r])
                      for p in range(4):
                          S.op("dve", lambda e, p=p, ch=ch, Sc=Sc, Sn=Sn: e.scalar_tensor_tensor(out=Sn.t[:, p, :], in0=Sc.t[:, p, :], scalar=gcol.t[:, p * 2 + ch:p * 2 + ch + 1], in1=pS_.t[:, p * 64:(p + 1) * 64], op0=ALU.mult, op1=ALU.add), r=[Sc.r, gcol.r, pS_.r], w=[Sn.r])
                      S.op("dve", lambda e, ch=ch, Sn=Sn: e.tensor_tensor(out=Sn.t[:], in0=Sn.t[:], in1=Hm.t[:, :, ch * 64:(ch + 1) * 64], op=ALU.add), r=[Sn.r, Hm.r], w=[Sn.r])
                      st_i = 1 - st_i
                  if stop == 9:
                      raise _Stop()
                  S.op("dve", lambda e: e.tensor_reduce(out=sm.t[:, 16:24], in_=hv(Yt.t[:]), axis=AX.X, op=ALU.add), r=[Yt.r], w=[sm.r])
                  S.op("act", lambda e: e.activation(out=t2.t[:], in_=Yt.t[:], func=AF.Square), r=[Yt.r, t2.r], w=[t2.r])
                  S.op("dve", lambda e: e.tensor_reduce(out=sm.t[:, 24:32], in_=hv(t2.t[:]), axis=AX.X, op=ALU.add), r=[t2.r], w=[sm.r])
                  S.op("dve", lambda e: e.tensor_scalar(out=sm.t[:, 16:24], in0=sm.t[:, 16:24], scalar1=1.0 / 64, scalar2=None, op0=ALU.mult), r=[sm.r], w=[sm.r])
                  S.op("dve", lambda e: e.tensor_tensor(out=sm.t[:, 32:40], in0=sm.t[:, 16:24], in1=sm.t[:, 16:24], op=ALU.mult), r=[sm.r], w=[sm.r])
                  S.op("dve", lambda e: e.scalar_tensor_tensor(out=sm.t[:, 24:32], in0=sm.t[:, 24:32], scalar=1.0 / 64, in1=sm.t[:, 32:40], op0=ALU.mult, op1=ALU.subtract), r=[sm.r], w=[sm.r])
                  S.op("dve", lambda e: e.tensor_scalar(out=sm.t[:, 24:32], in0=sm.t[:, 24:32], scalar1=GN_EPS, scalar2=None, op0=ALU.add), r=[sm.r], w=[sm.r])
                  S.op("act", lambda e: e.activation(out=sm.t[:, 24:32], in_=sm.t[:, 24:32], func=AF.Sqrt), r=[sm.r], w=[sm.r])
                  S.op("dve", lambda e: e.reciprocal(out=sm.t[:, 24:32], in_=sm.t[:, 24:32]), r=[sm.r], w=[sm.r])
                  S.op("dve", lambda e: e.tensor_tensor(out=hv(yn.t[:]), in0=hv(Yt.t[:]), in1=bc8(sm.t[:, 16:24]), op=ALU.subtract), r=[Yt.r, sm.r], w=[yn.r])
                  S.op("dve", lambda e: e.tensor_tensor(out=hv(yn.t[:]), in0=hv(yn.t[:]), in1=bc8(sm.t[:, 24:32]), op=ALU.mult), r=[yn.r, sm.r], w=[yn.r])
                  S.op("pool", lambda e: e.tensor_tensor(out=yn.t[:], in0=yn.t[:], in1=BC["lngb"][:], op=ALU.mult), r=[yn.r, r_bc], w=[yn.r])
                  S.op("pool", lambda e: e.tensor_tensor(out=yn.t[:], in0=yn.t[:], in1=BC["lnbb"][:], op=ALU.add), r=[yn.r, r_bc], w=[yn.r])
                  S.op("dve", lambda e: e.tensor_tensor(out=yn.t[:], in0=yn.t[:], in1=bonus.t[:], op=ALU.add), r=[yn.r, bonus.r], w=[yn.r])
                  S.op("dve", lambda e: e.tensor_tensor(out=ybf.t[:], in0=yn.t[:], in1=X["gg"].t[:], op=ALU.mult), r=[yn.r, X["gg"].r], w=[ybf.r])
                  for p in range(4):
                      S.op("pe", lambda e, p=p: e.transpose(pbb.t[:, p * 128:(p + 1) * 128], ybf.t[:, p * 128:(p + 1) * 128], identb.t[:]), r=[ybf.r, identb.r], w=[pbb.r])
                  yo = yTo[tl_ % 2]
                  S.op("act", lambda e, yo=yo: e.copy(out=yo.t[:], in_=pbb.t[:, 0:512].rearrange("p (a t) -> p a t", t=128)), r=[pbb.r], w=[yo.r])
                  tails.append(S.dma("sp", ysrc[:, :, tl_ * 128:(tl_ + 1) * 128], yo.t[:], r=[yo.r], key="yTo%d" % (tl_ % 2)))
                  if debug and tl_ == 0:
                      S.op("act", lambda e, r_=r_: e.copy(out=t1.t[:], in_=r_), r=rG + [t1.r], w=[t1.r])
                      S.op("act", lambda e, v_=v_: e.copy(out=t2.t[:], in_=v_), r=rG + [t2.r], w=[t2.r])
                      tails.append(S.dma("sp", dbg["v"], t2.t[:], r=[t2.r], key="dbg_v"))
                      for n, src in (("r", t1), ("kp", kp), ("kkn", kkn), ("a", a_), ("sw", sw), ("Y", Yt), ("bonus", bonus), ("Gm", Gm), ("yn", yn)):
                          tails.append(S.dma("sp", dbg[n], src.t[:], r=[src.r], key="dbg_" + n))
                      tails.append(S.dma("sp", dbg["S"], ST[st_i].t[:], r=[ST[st_i].r], key="dbg_S"))
        except _Stop:
            pass
        if after is not None:
            after(C, tails)
        S.emit(st, tail_waits=tails)


PAIRS = [[0, 1], [2, 3], [4, 5], [6, 7]]
TL_KEYS = ("cT", "adaw", "adab", "lng", "lnb", "wo", "win", "wout")


def tl_decl(dt, sfx):
    return dict(cT=dt("cT" + sfx, [128, 8], F32), adaw=dt("adaw" + sfx, [D, 4 * D], F32), adab=dt("adab" + sfx, [128, 32], F32),
                lng=dt("lng" + sfx, [128, 16], F32), lnb=dt("lnb" + sfx, [128, 16], F32),
                wo=dt("wo" + sfx, [D, D], F32), win=dt("win" + sfx, [D, 2 * DFF], F32), wout=dt("wout" + sfx, [DFF, D], F32))


def build_fused(stop=99):
    nc = bass.Bass("TRN2", target_bir_lowering=False)
    dt = lambda n, s, d, k="ExternalInput": nc.dram_tensor(n, list(s), d, kind=k).ap()
    it = lambda n, s, d: nc.dram_tensor(n, list(s), d).ap()
    fox_in = fox_decl(dt, 4096, "_f")
    oT_d = dt("oT", [D, 2048], F32, "ExternalOutput")
    cin1, cout1 = it("cin1", [4, 128, 4096], BF16), it("cout1", [4, 2, 128, 4096], BF16)
    cin2, cout2 = it("cin2", [8, 128, 2048], F32), it("cout2", [8, 2, 128, 2048], F32)
    cin3, cout3 = it("cin3", [4, 128, 4096], BF16), it("cout3", [4, 2, 128, 4096], BF16)

    def gather(cin, cout, key, nblk):
        def after(C, tails):
            prev = list(tails)
            for k in range(nblk):
                o = C.S.cc("AllGather", PAIRS, cin[k], cout[k].rearrange("r p t -> (r p) t"), key=key, extra=prev)
                prev = [o]
            tails.append(o)
        return after

    cin1_v = cin1.rearrange("k p t -> (k p) t")
    cin3_v = cin3.rearrange("k p t -> (k p) t")
    cin2_v = cin2.rearrange("k p t -> (k p) t")
    y1_v = cout1
    y3_v = cout3
    fox_phase(nc, "f_", fox_in, cin1_v, 4096, after=gather(cin1, cout1, "g1", 4))
    if stop <= 1:
        return nc
    phase_end(nc)
    sel_d = dt("sel", [128, 2], F32)
    xTh_d = dt("xTh", [D, 2048], F32)
    t = tl_decl(dt, "_t0")
    tl_phase(nc, "t0_", y1_v, xTh_d, t["cT"], t["adaw"], t["adab"], t["lng"], t["lnb"], t["wo"], t["win"], t["wout"], cin2_v, 2048, 1024,
             sel_d=sel_d, after=gather(cin2, cout2, "g2", 8))
    if stop <= 2:
        return nc
    phase_end(nc)
    rw_in = rwkv_decl(dt, "_r")
    c2v = cout2.rearrange("kc r p t -> r p kc t")
    rwkv_phase(nc, "r_", lambda t0, n: c2v[t0 // 2048, :, :, (t0 % 2048):(t0 % 2048) + n], rw_in, cin3_v, 4096, after=gather(cin3, cout3, "g3", 4))
    if stop <= 3:
        return nc
    phase_end(nc)
    t = tl_decl(dt, "_t1")
    tl_phase(nc, "t1_", y3_v, cin2_v, t["cT"], t["adaw"], t["adab"], t["lng"], t["lnb"], t["wo"], t["win"], t["wout"], oT_d, 2048, 1024, sel_d=sel_d)
    return nc


NCORES = 8
_PROGS = {}


def _prog(name, fn):
    if name not in _PROGS:
        _PROGS[name] = fn()
    return _PROGS[name]


def _col_layout(v):
    return np.ascontiguousarray(np.asarray(v).reshape(-1, 128).T)


def _bc(v, n=512):
    v = np.asarray(v, dtype=np.float32).reshape(-1)
    return np.ascontiguousarray(np.broadcast_to(v[None, :], (128, v.shape[0])))


def _run(nc, in_maps):
    res = run_bass_kernel_spmd(nc, in_maps, core_ids=list(range(NCORES)))
    return res.results


def _tl_maps(yT_list, xT_list, c, ada_w_i, ada_b_i, ln_g_i, ln_b_i, wo, win, wout):
    maps = []
    adaw = np.ascontiguousarray(ada_w_i[:, 2 * D:])
    adab = _col_layout(ada_b_i[2 * D:])
    lng = _col_layout(ln_g_i.reshape(-1))
    lnb = _col_layout(ln_b_i.reshape(-1))
    for core in range(NCORES):
        b, th = core // 2, core % 2
        ts = slice(th * 2048, (th + 1) * 2048)
        maps.append({
            "yT": np.ascontiguousarray(yT_list[b][:, ts]),
            "xT": np.ascontiguousarray(xT_list[b][:, ts]),
            "cT": _col_layout(c[b]),
            "adaw": adaw, "adab": adab, "lng": lng, "lnb": lnb,
            "wo": wo, "win": win, "wout": wout,
        })
    return maps


def kernel_unfused(x, c, ada_w, ada_b, ln_g, ln_b, ffn_w_in, ffn_w_out,
           fox_w_in, fox_b_f, fox_q_g, fox_k_g, fox_w_o,
           rwkv_mu, rwkv_w_rkv, rwkv_w0, rwkv_w1, rwkv_w2, rwkv_a0, rwkv_a1, rwkv_a2,
           rwkv_g1, rwkv_g2, rwkv_k_k, rwkv_k_a, rwkv_r_k, rwkv_lnx_g, rwkv_lnx_b, rwkv_w_o):
    f = lambda a: np.ascontiguousarray(np.asarray(a, dtype=np.float32))
    x, c, ada_w, ada_b, ln_g, ln_b = f(x), f(c), f(ada_w), f(ada_b), f(ln_g), f(ln_b)
    ffn_w_in, ffn_w_out = f(ffn_w_in), f(ffn_w_out)
    B = x.shape[0]
    w_in = f(fox_w_in)[0]
    b_f, q_g, k_g = f(fox_b_f)[0], f(fox_q_g)[0], f(fox_k_g)[0]
    maps = []
    for core in range(NCORES):
        b, hh = core // 2, core % 2
        sl = slice(hh * 512, (hh + 1) * 512)
        maps.append({
            "x": x[b],
            "cT": _col_layout(c[b]),
            "adaw": np.ascontiguousarray(ada_w[0][:, 0:2 * D]),
            "adab": _col_layout(ada_b[0][0:2 * D]),
            "wq": np.ascontiguousarray(w_in[:, 0:D][:, sl]),
            "wk": np.ascontiguousarray(w_in[:, D:2 * D][:, sl]),
            "wv": np.ascontiguousarray(w_in[:, 2 * D:3 * D][:, sl]),
            "wf": np.ascontiguousarray(w_in[:, 3 * D + hh * 8:3 * D + hh * 8 + 8]),
            "wg": np.ascontiguousarray(w_in[:, 3 * D + 16:][:, sl]),
            "bfb": _bc(b_f[hh * 8:(hh + 1) * 8]),
            "qgb": _bc(np.tile(q_g, 8)),
            "kgb": _bc(np.tile(k_g, 8)),
        })
    r1 = _run(_prog("fox", build_fox), maps)
    yT = [np.concatenate([r1[2 * b]["ygT"], r1[2 * b + 1]["ygT"]], axis=0) for b in range(B)]
    xT = [np.ascontiguousarray(x[b].T) for b in range(B)]
    tlp = _prog("tl", build_tl)
    r2 = _run(tlp, _tl_maps(yT, xT, c, ada_w[0], ada_b[0], ln_g[0], ln_b[0], f(fox_w_o)[0], ffn_w_in[0], ffn_w_out[0]))
    x1T = [np.concatenate([r2[2 * b]["oT"], r2[2 * b + 1]["oT"]], axis=1) for b in range(B)]
    mu = f(rwkv_mu)[0]
    w_rkv = f(rwkv_w_rkv)[0]
    P = dict(w0=f(rwkv_w0)[0], w1=f(rwkv_w1)[0], w2=f(rwkv_w2)[0], a0=f(rwkv_a0)[0], a1=f(rwkv_a1)[0], a2=f(rwkv_a2)[0],
             g1=f(rwkv_g1)[0], g2=f(rwkv_g2)[0], k_k=f(rwkv_k_k)[0], k_a=f(rwkv_k_a)[0], r_k=f(rwkv_r_k)[0].reshape(-1),
             lnx_g=f(rwkv_lnx_g)[0], lnx_b=f(rwkv_lnx_b)[0])
    mu_l = np.concatenate([_col_layout(mu[n]) for n in range(6)], axis=1)
    maps = []
    for core in range(NCORES):
        b, hh = core // 2, core % 2
        sl = slice(hh * 512, (hh + 1) * 512)
        maps.append({
            "xT": x1T[b],
            "cT": _col_layout(c[b]),
            "adaw": np.ascontiguousarray(ada_w[1][:, 0:2 * D]),
            "adab": _col_layout(ada_b[1][0:2 * D]),
            "mu": mu_l,
            "wr": np.ascontiguousarray(w_rkv[0][:, sl]),
            "wk": np.ascontiguousarray(w_rkv[1][:, sl]),
            "wv": np.ascontiguousarray(w_rkv[2][:, sl]),
            "w1": P["w1"], "a1": P["a1"], "g1": P["g1"],
            "w2": np.ascontiguousarray(P["w2"][:, sl]),
            "a2": np.ascontiguousarray(P["a2"][:, sl]),
            "g2": np.ascontiguousarray(P["g2"][:, sl]),
            "w0b": _bc(P["w0"][sl]), "a0b": _bc(P["a0"][sl]), "kkb": _bc(P["k_k"][sl]), "kab": _bc(P["k_a"][sl]),
            "rkb": _bc(P["r_k"][sl]), "lngb": _bc(P["lnx_g"][sl]), "lnbb": _bc(P["lnx_b"][sl]),
        })
    r3 = _run(_prog("rwkv", build_rwkv), maps)
    y2T = [np.concatenate([r3[2 * b]["yT"], r3[2 * b + 1]["yT"]], axis=0) for b in range(B)]
    r4 = _run(tlp, _tl_maps(y2T, x1T, c, ada_w[1], ada_b[1], ln_g[1], ln_b[1], f(rwkv_w_o)[0], ffn_w_in[1], ffn_w_out[1]))
    out = np.empty(x.shape, np.float32)
    for core in range(NCORES):
        b, th = core // 2, core % 2
        out[b, th * 2048:(th + 1) * 2048, :] = r4[core]["oT"].T
    return out


def kernel(x, c, ada_w, ada_b, ln_g, ln_b, ffn_w_in, ffn_w_out,
           fox_w_in, fox_b_f, fox_q_g, fox_k_g, fox_w_o,
           rwkv_mu, rwkv_w_rkv, rwkv_w0, rwkv_w1, rwkv_w2, rwkv_a0, rwkv_a1, rwkv_a2,
           rwkv_g1, rwkv_g2, rwkv_k_k, rwkv_k_a, rwkv_r_k, rwkv_lnx_g, rwkv_lnx_b, rwkv_w_o):
    f = lambda a: np.ascontiguousarray(np.asarray(a, dtype=np.float32))
    x, c, ada_w, ada_b, ln_g, ln_b = f(x), f(c), f(ada_w), f(ada_b), f(ln_g), f(ln_b)
    ffn_w_in, ffn_w_out = f(ffn_w_in), f(ffn_w_out)
    w_in = f(fox_w_in)[0]
    b_f, q_g, k_g = f(fox_b_f)[0], f(fox_q_g)[0], f(fox_k_g)[0]
    mu = f(rwkv_mu)[0]
    w_rkv = f(rwkv_w_rkv)[0]
    P = dict(w0=f(rwkv_w0)[0], w1=f(rwkv_w1)[0], w2=f(rwkv_w2)[0], a0=f(rwkv_a0)[0], a1=f(rwkv_a1)[0], a2=f(rwkv_a2)[0],
             g1=f(rwkv_g1)[0], g2=f(rwkv_g2)[0], k_k=f(rwkv_k_k)[0], k_a=f(rwkv_k_a)[0], r_k=f(rwkv_r_k)[0].reshape(-1),
             lnx_g=f(rwkv_lnx_g)[0], lnx_b=f(rwkv_lnx_b)[0])
    mu_l = np.concatenate([_col_layout(mu[n]) for n in range(6)], axis=1)
    wos = [f(fox_w_o)[0], f(rwkv_w_o)[0]]
    tl_common = []
    for L in range(2):
        tl_common.append({
            "adaw_t%d" % L: np.ascontiguousarray(ada_w[L][:, 2 * D:]), "adab_t%d" % L: _col_layout(ada_b[L][2 * D:]),
            "lng_t%d" % L: _col_layout(ln_g[L].reshape(-1)), "lnb_t%d" % L: _col_layout(ln_b[L].reshape(-1)),
            "wo_t%d" % L: wos[L], "win_t%d" % L: ffn_w_in[L], "wout_t%d" % L: ffn_w_out[L]})
    adaw_f = np.ascontiguousarray(ada_w[0][:, 0:2 * D])
    adaw_r = np.ascontiguousarray(ada_w[1][:, 0:2 * D])
    maps = []
    for core in range(NCORES):
        b, j = core // 2, core % 2
        sl = slice(j * 512, (j + 1) * 512)
        cT = _col_layout(c[b])
        m = {
            "x_f": x[b], "cT_f": cT, "adaw_f": adaw_f, "adab_f": _col_layout(ada_b[0][0:2 * D]),
            "wq_f": np.ascontiguousarray(w_in[:, 0:D][:, sl]), "wk_f": np.ascontiguousarray(w_in[:, D:2 * D][:, sl]),
            "wv_f": np.ascontiguousarray(w_in[:, 2 * D:3 * D][:, sl]), "wf_f": np.ascontiguousarray(w_in[:, 3 * D + j * 8:3 * D + j * 8 + 8]),
            "wg_f": np.ascontiguousarray(w_in[:, 3 * D + 16:][:, sl]),
            "bfb_f": _bc(b_f[j * 8:(j + 1) * 8]), "qgb_f": _bc(np.tile(q_g, 8)), "kgb_f": _bc(np.tile(k_g, 8)),
            "cT_r": cT, "adaw_r": adaw_r, "adab_r": _col_layout(ada_b[1][0:2 * D]), "mu_r": mu_l,
            "wr_r": np.ascontiguousarray(w_rkv[0][:, sl]), "wk_r": np.ascontiguousarray(w_rkv[1][:, sl]), "wv_r": np.ascontiguousarray(w_rkv[2][:, sl]),
            "w1_r": P["w1"], "a1_r": P["a1"], "g1_r": P["g1"],
            "w2_r": np.ascontiguousarray(P["w2"][:, sl]), "a2_r": np.ascontiguousarray(P["a2"][:, sl]), "g2_r": np.ascontiguousarray(P["g2"][:, sl]),
            "w0b_r": _bc(P["w0"][sl]), "a0b_r": _bc(P["a0"][sl]), "kkb_r": _bc(P["k_k"][sl]), "kab_r": _bc(P["k_a"][sl]),
            "rkb_r": _bc(P["r_k"][sl]), "lngb_r": _bc(P["lnx_g"][sl]), "lnbb_r": _bc(P["lnx_b"][sl]),
            "cT_t0": cT, "cT_t1": cT,
            "sel": np.ascontiguousarray(np.broadcast_to(np.array([1.0 - j, float(j)], np.float32)[None, :], (128, 2))),
            "xTh": np.ascontiguousarray(x[b, j * 2048:(j + 1) * 2048, :].T),
        }
        m.update(tl_common[0])
        m.update(tl_common[1])
        maps.append(m)
    res = _run(_prog("fused", build_fused), maps)
    out = np.empty(x.shape, np.float32)
    for core in range(NCORES):
        b, j = core // 2, core % 2
        out[b, j * 2048:(j + 1) * 2048, :] = res[core]["oT"].T
    return out
```

```python
import numpy as np
from contextlib import ExitStack
import concourse.bass as bass
import concourse.mybir as mybir
from concourse.bass_utils import run_bass_kernel_spmd

F32 = mybir.dt.float32
BF16 = mybir.dt.bfloat16
AF = mybir.ActivationFunctionType
ALU = mybir.AluOpType
AX = mybir.AxisListType

ENGS = ("pe", "act", "dve", "pool", "sp")
LAST_SEMS = []
SEM_CTR = [0]


def phase_end(nc):
    nc.all_engine_barrier()
    nc.clear_and_free_semaphores(list(LAST_SEMS))
    LAST_SEMS[:] = []
    nc.all_engine_barrier()


class Res:
    __slots__ = ("name", "last_w", "readers", "excl")

    def __init__(self, name, excl=False):
        self.name = name
        self.excl = excl
        self.last_w = None
        self.readers = []


class Op:
    __slots__ = ("eng", "fn", "deps", "sig", "count", "is_dma", "key", "semval", "vc", "waits", "gid")


class Sched:
    def __init__(self, nc):
        self.nc = nc
        self.ops = []
        self.dma_counts = {}

    def res(self, name, excl=False):
        return Res(name, excl)

    def _add(self, op, r, w, extra=()):
        deps = [(d, "raw") for d in extra]
        for x in r:
            if x.last_w is not None:
                deps.append((x.last_w, "raw"))
            if x.excl:
                for rd in x.readers:
                    deps.append((rd, "war"))
        for x in w:
            if x.last_w is not None:
                deps.append((x.last_w, "waw"))
            for rd in x.readers:
                deps.append((rd, "war"))
        dd = []
        for d, kind in deps:
            if d is op:
                continue
            if (not d.is_dma) and (not op.is_dma) and d.eng == op.eng:
                if op.eng == "pe" or kind == "war":
                    continue
            dd.append(d)
        op.deps = dd
        for d in dd:
            d.sig = True
        for x in w:
            x.last_w = op
            x.readers = []
        for x in r:
            if x.last_w is not op:
                if not op.is_dma:
                    x.readers = [q for q in x.readers if q.is_dma or q.eng != op.eng]
                x.readers.append(op)
        op.gid = len(self.ops)
        self.ops.append(op)
        return op

    def op(self, eng, fn, r=(), w=()):
        o = Op()
        o.eng = eng
        o.fn = fn
        o.is_dma = False
        o.sig = False
        o.key = eng
        return self._add(o, r, w)

    def dma(self, eng, out, in_, r=(), w=(), key=None, **kw):
        o = Op()
        o.eng = eng
        o.is_dma = True
        o.sig = True
        assert key is not None
        o.key = "dma:" + key
        o.fn = lambda e: e.dma_start(out=out, in_=in_, **kw)
        return self._add(o, r, w)

    def cc(self, kind, groups, in_ap, out_ap, r=(), w=(), key=None, extra=()):
        o = Op()
        o.eng = "pool"
        o.is_dma = True
        o.sig = True
        o.key = "cc:" + key
        o.fn = lambda e: e.collective_compute(kind, ALU.bypass, replica_groups=groups, ins=[in_ap], outs=[out_ap])
        return self._add(o, r, w, extra)

    def finalize(self, final_wait_eng="sp"):
        nc = self.nc
        counts = {e: 0 for e in ENGS}
        dmac = {}
        for o in self.ops:
            if o.is_dma:
                dmac[o.key] = dmac.get(o.key, 0) + (1 if o.key.startswith("cc:") else 16)
                o.semval = dmac[o.key]
            elif o.sig:
                counts[o.eng] += 1
                o.semval = counts[o.eng]
        seen = {e: {} for e in ENGS}
        for o in self.ops:
            s = seen[o.eng]
            need = {}
            for d in o.deps:
                if s.get(d.key, 0) < d.semval:
                    if d.key not in need or need[d.key].semval < d.semval:
                        need[d.key] = d
            o.waits = [(d.key, d.semval) for d in need.values()]
            for d in need.values():
                for k, v in d.vc.items():
                    if s.get(k, 0) < v:
                        s[k] = v
            if o.sig:
                vc = dict(s)
                vc[o.key] = o.semval
                o.vc = vc
            else:
                o.vc = None
        return counts, dmac

    def emit(self, stack, tail_waits=()):
        nc = self.nc
        counts, dmac = self.finalize()
        sems = {}
        for k in list(ENGS) + list(dmac):
            SEM_CTR[0] += 1
            sems[k] = nc.alloc_semaphore(name="sem%d" % SEM_CTR[0])
        self.nsems = len(sems)
        LAST_SEMS[:] = list(sems.values())
        per = {e: [o for o in self.ops if o.eng == e] for e in ENGS}
        block = stack.enter_context(nc.Block())

        def run(eng_name, eng):
            for o in per[eng_name]:
                for k, v in o.waits:
                    eng.wait_ge(sems[k], v)
                ins = o.fn(eng)
                if o.sig:
                    if o.is_dma:
                        ins.then_inc(sems[o.key], 1 if o.key.startswith("cc:") else 16)
                    else:
                        ins.then_inc(sems[o.key], 1)
            if eng_name == "sp":
                done = {}
                for o in tail_waits:
                    done[o.key] = max(done.get(o.key, 0), o.semval)
                for k, v in done.items():
                    eng.wait_ge(sems[k], v)

        @block.tensor
        def _(e):
            run("pe", e)

        @block.scalar
        def _(e):
            run("act", e)

        @block.vector
        def _(e):
            run("dve", e)

        @block.gpsimd
        def _(e):
            run("pool", e)

        @block.sync
        def _(e):
            run("sp", e)


D = 1024
DFF = 2816
NFC = 22
ALPHA = 4.0 ** 0.25
LN_EPS = 1e-5
EPS_P = LN_EPS / (ALPHA * ALPHA)


class Ctx:
    def __init__(self, nc, st, pfx=""):
        self.nc = nc
        self.st = st
        self.S = Sched(nc)
        self.n = 0
        self.pfx = pfx

    def sb(self, name, shape, dt):
        return self.st.enter_context(self.nc.sbuf_tensor(self.pfx + "sb_" + name, list(shape), dt))

    def ps(self, name, shape, dt=F32):
        return self.st.enter_context(self.nc.psum_tensor(self.pfx + "ps_" + name, list(shape), dt))

    def R(self, name, excl=False):
        return self.S.res(name, excl)


def make_consts(C):
    S = C.S
    C.ones = C.sb("ones", [128, 128], F32)
    C.r_ones = C.R("ones")
    S.op("pool", lambda e: e.memset(C.ones[:], 1.0), w=[C.r_ones])
    C.ident = C.sb("ident", [128, 128], F32)
    C.r_ident = C.R("ident")
    S.op("pool", lambda e: e.memset(C.ident[:], 1.0), w=[C.r_ident])
    S.op("pool", lambda e: e.affine_select(out=C.ident[:], in_=C.ident[:], pattern=[[-1, 128]], compare_op=ALU.is_equal, fill=0.0, base=0, channel_multiplier=1), r=[C.r_ident], w=[C.r_ident])


def compute_mods(C, cT_d, adaw_d, adab_d, nm, wbufs, psm, r_psm):
    S = C.S
    cT = C.sb("cT", [128, 8], F32)
    r_cT = C.R("cT")
    S.dma("sp", cT[:], cT_d, w=[r_cT], key="cT")
    S.op("act", lambda e: e.activation(out=cT[:], in_=cT[:], func=AF.Silu), r=[r_cT], w=[r_cT])
    adab = C.sb("adab", [128, nm * 8], F32)
    r_adab = C.R("adab")
    S.dma("sp", adab[:], adab_d, w=[r_adab], key="adab")
    mods = C.sb("mods", [128, nm * 8], F32)
    r_mods = C.R("mods")
    src = adaw_d.rearrange("(kc p) n -> p kc n", p=128)
    for m in range(nm):
        for q in range(4):
            bi = (m * 4 + q) % 2
            wt, r_wt = wbufs[bi]
            c0 = m * 1024 + q * 256
            S.dma("sp", wt[:], src[:, :, c0:c0 + 256], w=[r_wt], key="adaw%d" % bi)
            for j in range(2):
                cc = q * 2 + j
                for kc in range(8):
                    S.op("pe", lambda e, m=m, cc=cc, kc=kc, j=j, wt=wt: e.matmul(psm[:, m * 8 + cc:m * 8 + cc + 1], wt[:, kc, j * 128:(j + 1) * 128], cT[:, kc:kc + 1], start=(kc == 0), stop=(kc == 7)), r=[r_wt, r_cT], w=[r_psm])
    S.op("dve", lambda e: e.tensor_tensor(out=mods[:], in0=psm[:, 0:nm * 8], in1=adab[:], op=ALU.add), r=[r_psm, r_adab], w=[r_mods])
    return mods, r_mods


def ln_group(C, zT, r_z, g, n0, colA, colB, r_cols, outs, P):
    S = C.S
    ps_s, r_ps_s = P["st0"]
    ps_q, r_ps_q = P["st1"]
    sq, r_sq = P["sq"]
    for cc in range(8):
        S.op("act", lambda e, cc=cc: e.activation(out=sq[cc % 2][:], in_=zT[:, cc, n0:n0 + 512], func=AF.Square), r=[r_z[cc]], w=[r_sq[cc % 2]])
        S.op("pe", lambda e, cc=cc: e.matmul(ps_s[:], C.ones[:], zT[:, cc, n0:n0 + 512], start=(cc == 0), stop=(cc == 7)), r=[C.r_ones, r_z[cc]], w=[r_ps_s])
        S.op("pe", lambda e, cc=cc: e.matmul(ps_q[:], C.ones[:], sq[cc % 2][:], start=(cc == 0), stop=(cc == 7)), r=[C.r_ones, r_sq[cc % 2]], w=[r_ps_q])
    mean, r_mean = P["mean"]
    rstd, r_rstd = P["rstd"]
    S.op("act", lambda e: e.mul(out=mean[:], in_=ps_s[:], mul=1.0 / D), r=[r_ps_s], w=[r_mean])
    S.op("dve", lambda e: e.tensor_tensor(out=rstd[:], in0=mean[:], in1=mean[:], op=ALU.mult), r=[r_mean], w=[r_rstd])
    S.op("dve", lambda e: e.scalar_tensor_tensor(out=rstd[:], in0=ps_q[:], scalar=1.0 / D, in1=rstd[:], op0=ALU.mult, op1=ALU.subtract), r=[r_ps_q, r_rstd], w=[r_rstd])
    S.op("dve", lambda e: e.tensor_scalar(out=rstd[:], in0=rstd[:], scalar1=P["eps"], scalar2=None, op0=ALU.add), r=[r_rstd], w=[r_rstd])
    S.op("act", lambda e: e.activation(out=rstd[:], in_=rstd[:], func=AF.Sqrt), r=[r_rstd], w=[r_rstd])
    S.op("dve", lambda e: e.reciprocal(out=rstd[:], in_=rstd[:]), r=[r_rstd], w=[r_rstd])
    S.op("dve", lambda e: e.scalar_tensor_tensor(out=mean[:], in0=mean[:], scalar=-1.0, in1=rstd[:], op0=ALU.mult, op1=ALU.mult), r=[r_mean, r_rstd], w=[r_mean])
    tt, r_tt = P["tt"]
    for cc in range(8):
        k = cc % 2
        S.op("dve", lambda e, cc=cc, k=k: e.tensor_tensor(out=tt[k][:], in0=zT[:, cc, n0:n0 + 512], in1=rstd[:], op=ALU.mult), r=[r_z[cc], r_rstd], w=[r_tt[k]])
        S.op("dve", lambda e, cc=cc, k=k: e.tensor_tensor(out=tt[k][:], in0=tt[k][:], in1=mean[:], op=ALU.add), r=[r_tt[k], r_mean], w=[r_tt[k]])
        for (dst, r_dst, gt, go, bt, bo, eng) in outs:
            if eng == "act":
                S.op("act", lambda e, cc=cc, k=k, dst=dst, gt=gt, go=go, bt=bt, bo=bo: e.activation(out=dst[:, cc, n0:n0 + 512], in_=tt[k][:], func=AF.Identity, scale=gt[:, go + cc:go + cc + 1], bias=bt[:, bo + cc:bo + cc + 1]), r=[r_tt[k]] + r_cols, w=[r_dst[cc]])
            else:
                S.op(eng, lambda e, cc=cc, k=k, dst=dst, gt=gt, go=go, bt=bt, bo=bo: e.tensor_scalar(out=dst[:, cc, n0:n0 + 512], in0=tt[k][:], scalar1=gt[:, go + cc:go + cc + 1], scalar2=bt[:, bo + cc:bo + cc + 1], op0=ALU.mult, op1=ALU.add), r=[r_tt[k]] + r_cols, w=[r_dst[cc]])


def build_tl(NT=2048, HALF=1024, debug=False):
    nc = bass.Bass("TRN2", target_bir_lowering=False)
    dt = lambda n, s, d, k="ExternalInput": nc.dram_tensor(n, list(s), d, kind=k).ap()
    yT_d = dt("yT", [D, NT], BF16)
    xT_d = dt("xT", [D, NT], F32)
    cT_d = dt("cT", [128, 8], F32)
    adaw_d = dt("adaw", [D, 4 * D], F32)
    adab_d = dt("adab", [128, 32], F32)
    lng_d = dt("lng", [128, 16], F32)
    lnb_d = dt("lnb", [128, 16], F32)
    wo_d = dt("wo", [D, D], F32)
    win_d = dt("win", [D, 2 * DFF], F32)
    wout_d = dt("wout", [DFF, D], F32)
    oT_d = dt("oT", [D, NT], F32, "ExternalOutput")
    tl_phase(nc, "", yT_d, xT_d, cT_d, adaw_d, adab_d, lng_d, lnb_d, wo_d, win_d, wout_d, oT_d, NT, HALF, debug=debug)
    return nc


def tl_phase(nc, pfx, yT_d, xT_d, cT_d, adaw_d, adab_d, lng_d, lnb_d, wo_d, win_d, wout_d, oT_d, NT=2048, HALF=1024, debug=False, sel_d=None, after=None):
    dt = lambda n, s, d, k="ExternalInput": nc.dram_tensor(n, list(s), d, kind=k).ap()
    NG = HALF // 512
    if debug:
        dbg_cols = dt("dbg_cols", [128, 48], F32, "ExternalOutput")
        dbg_mods = dt("dbg_mods", [128, 32], F32, "ExternalOutput")
        dbg_x1 = dt("dbg_x1", [D, HALF], F32, "ExternalOutput")
        dbg_h = dt("dbg_h", [D, HALF], BF16, "ExternalOutput")
        dbg_a = dt("dbg_a", [DFF, HALF], BF16, "ExternalOutput")
    with ExitStack() as st:
        C = Ctx(nc, st, pfx)
        S = C.S
        make_consts(C)
        PB = [(C.ps("pb%d" % i, [128, 512]), C.R("pb%d" % i, True)) for i in range(8)]
        if sel_d is not None:
            selt = C.sb("selt", [128, 2], F32)
            r_selt = C.R("selt")
            S.dma("sp", selt[:], sel_d, w=[r_selt], key="selt")
            ystg = [[(C.sb("ystg%d_%d" % (i, j), [128, HALF], BF16), C.R("ystg%d_%d" % (i, j))) for j in range(2)] for i in range(2)]
        xT = C.sb("xT", [128, 8, HALF], F32)
        r_x = [C.R("xT%d" % cc) for cc in range(8)]
        yT = C.sb("yT", [128, 8, HALF], BF16)
        r_y = [C.R("yT%d" % cc) for cc in range(8)]
        hT = yT
        r_h = r_y
        aT = C.sb("aT", [128, NFC, HALF], BF16)
        r_a = [C.R("aT%d" % f) for f in range(NFC)]
        wo = C.sb("wo", [128, 8, D], BF16)
        r_wo = C.R("wo")
        wA = [(C.sb("wA%d" % i, [128, 8, 256], F32), C.R("wA%d" % i)) for i in range(2)]
        wi = [(C.sb("wi%d" % i, [128, 8, 512], BF16), C.R("wi%d" % i)) for i in range(2)]
        wo2 = [(C.sb("wo2%d" % i, [128, NFC, 256], BF16), C.R("wo2%d" % i)) for i in range(2)]
        P = {
            "st0": PB[6], "st1": PB[7],
            "sq": ([C.sb("sq%d" % i, [128, 512], F32) for i in range(2)], [C.R("sq%d" % i) for i in range(2)]),
            "mean": (C.sb("mean", [128, 512], F32), C.R("mean")),
            "rstd": (C.sb("rstd", [128, 512], F32), C.R("rstd")),
            "tt": ([C.sb("tt%d" % i, [128, 512], F32) for i in range(2)], [C.R("tt%d" % i) for i in range(2)]),
            "eps": EPS_P,
        }
        sl = [C.sb("sl%d" % i, [128, 512], F32) for i in range(2)]
        r_sl = [C.R("sl%d" % i) for i in range(2)]
        mods, r_mods = compute_mods(C, cT_d, adaw_d, adab_d, 4, wA, PB[5][0], PB[5][1])
        lng = C.sb("lng", [128, 16], F32)
        lnb = C.sb("lnb", [128, 16], F32)
        r_ln = C.R("ln")
        S.dma("sp", lng[:], lng_d, w=[r_ln], key="lng")
        S.dma("sp", lnb[:], lnb_d, w=[r_ln], key="lnb")
        cols = C.sb("cols", [128, 48], F32)
        r_cols = C.R("cols")
        S.op("dve", lambda e: e.tensor_scalar(out=cols[:, 0:8], in0=mods[:, 0:8], scalar1=1.0 / ALPHA, scalar2=None, op0=ALU.mult), r=[r_mods], w=[r_cols])
        S.op("dve", lambda e: e.tensor_scalar(out=cols[:, 8:16], in0=mods[:, 24:32], scalar1=1.0 / ALPHA, scalar2=None, op0=ALU.mult), r=[r_mods], w=[r_cols])
        S.op("dve", lambda e: e.tensor_scalar(out=cols[:, 32:40], in0=mods[:, 16:24], scalar1=1.0, scalar2=None, op0=ALU.add), r=[r_mods], w=[r_cols])
        S.op("dve", lambda e: e.tensor_tensor(out=cols[:, 16:24], in0=lng[:, 0:8], in1=cols[:, 32:40], op=ALU.mult), r=[r_ln, r_cols], w=[r_cols])
        S.op("dve", lambda e: e.tensor_tensor(out=cols[:, 24:32], in0=lnb[:, 0:8], in1=cols[:, 32:40], op=ALU.mult), r=[r_ln, r_cols], w=[r_cols])
        S.op("dve", lambda e: e.tensor_tensor(out=cols[:, 24:32], in0=cols[:, 24:32], in1=mods[:, 8:16], op=ALU.add), r=[r_mods, r_cols], w=[r_cols])
        rc = [r_cols, r_ln]
        S.dma("pool", wo[:], wo_d.rearrange("(kc p) n -> p kc n", p=128), w=[r_wo], key="wo")
        xsrc = xT_d.rearrange("(kc p) t -> p kc t", p=128)
        if sel_d is None:
            ysrc_ = yT_d.rearrange("(kc p) t -> p kc t", p=128)
            ysrc_fn = lambda cc, a, n: ysrc_[:, cc, a:a + n]
        else:
            ysrc_fn = lambda cc, a, n: yT_d[cc % 4, cc // 4, :, a:a + n]
        osrc = oT_d.rearrange("(kc p) t -> p kc t", p=128)
        winr = win_d.rearrange("(kc p) n -> p kc n", p=128)
        woutr = wout_d.rearrange("(fc p) n -> p fc n", p=128)
        tails = []
        pbi = 0
        for half in range(NT // HALF):
            t0 = half * HALF
            for cc in range(8):
                if sel_d is None:
                    S.dma("sp", yT[:, cc, :], ysrc_fn(cc, t0, HALF), w=[r_y[cc]], key="yT%d" % cc)
                else:
                    (ya, r_ya), (yb, r_yb) = ystg[cc % 2]
                    S.dma("sp", ya[:], ysrc_fn(cc, t0, HALF), w=[r_ya], key="ysa%d" % (cc % 2))
                    S.dma("sp", yb[:], ysrc_fn(cc, NT + t0, HALF), w=[r_yb], key="ysb%d" % (cc % 2))
                    S.op("dve", lambda e, ya=ya: e.tensor_scalar(out=ya[:], in0=ya[:], scalar1=selt[:, 0:1], scalar2=None, op0=ALU.mult), r=[r_ya, r_selt], w=[r_ya])
                    S.op("dve", lambda e, ya=ya, yb=yb, cc=cc: e.scalar_tensor_tensor(out=yT[:, cc, :], in0=yb[:], scalar=selt[:, 1:2], in1=ya[:], op0=ALU.mult, op1=ALU.add), r=[r_ya, r_yb, r_selt], w=[r_y[cc]])
                S.dma("sp", xT[:, cc, :], xsrc[:, cc, t0:t0 + HALF], w=[r_x[cc]], key="xT%d" % cc)
            for g in range(NG):
                n0 = g * 512
                for cc in range(8):
                    pb, r_pb = PB[pbi % 4]
                    pbi += 1
                    for kc in range(8):
                        S.op("pe", lambda e, pb=pb, cc=cc, kc=kc, n0=n0: e.matmul(pb[:], wo[:, kc, cc * 128:(cc + 1) * 128], yT[:, kc, n0:n0 + 512], start=(kc == 0), stop=(kc == 7)), r=[r_wo] + r_y, w=[r_pb])
                    S.op("dve", lambda e, pb=pb, cc=cc, n0=n0: e.scalar_tensor_tensor(out=xT[:, cc, n0:n0 + 512], in0=pb[:], scalar=cols[:, cc:cc + 1], in1=xT[:, cc, n0:n0 + 512], op0=ALU.mult, op1=ALU.add), r=[r_pb, r_x[cc], r_cols], w=[r_x[cc]])
            for g in range(NG):
                n0 = g * 512
                ln_group(C, xT, r_x, g, n0, None, None, rc,
                         [(xT, r_x, lng, 0, lnb, 0, "act"), (hT, r_h, cols, 16, cols, 24, "act")], P)
            if debug and half == 0:
                tails.append(S.dma("sp", dbg_cols, cols[:], r=[r_cols], key="dbg0"))
                tails.append(S.dma("sp", dbg_mods, mods[:], r=[r_mods], key="dbg1"))
                tails.append(S.dma("sp", dbg_x1.rearrange("(kc p) t -> p kc t", p=128), xT[:], r=r_x, key="dbg2"))
                tails.append(S.dma("sp", dbg_h.rearrange("(kc p) t -> p kc t", p=128), hT[:], r=r_h, key="dbg3"))
            for blk in range(NFC // 2):
                wt, r_wt = wi[blk % 2]
                c0 = blk * 256
                S.dma("pool", wt[:, :, 0:256], winr[:, :, c0:c0 + 256], w=[r_wt], key="wi%d" % (blk % 2))
                S.dma("pool", wt[:, :, 256:512], winr[:, :, DFF + c0:DFF + c0 + 256], w=[r_wt], key="wi%d" % (blk % 2))
                for j in range(2):
                    fc = blk * 2 + j
                    for g in range(NG):
                        n0 = g * 512
                        pg, r_pg = PB[pbi % 4]
                        pu, r_pu = PB[(pbi + 1) % 4]
                        pbi += 2
                        for kc in range(8):
                            S.op("pe", lambda e, pg=pg, wt=wt, j=j, kc=kc, n0=n0: e.matmul(pg[:], wt[:, kc, j * 128:(j + 1) * 128], hT[:, kc, n0:n0 + 512], start=(kc == 0), stop=(kc == 7)), r=[r_wt] + r_h, w=[r_pg])
                        for kc in range(8):
                            S.op("pe", lambda e, pu=pu, wt=wt, j=j, kc=kc, n0=n0: e.matmul(pu[:], wt[:, kc, 256 + j * 128:256 + (j + 1) * 128], hT[:, kc, n0:n0 + 512], start=(kc == 0), stop=(kc == 7)), r=[r_wt] + r_h, w=[r_pu])
                        k = (fc * NG + g) % 2
                        S.op("act", lambda e, pg=pg, k=k: e.activation(out=sl[k][:], in_=pg[:], func=AF.Silu), r=[r_pg], w=[r_sl[k]])
                        S.op("dve", lambda e, pu=pu, k=k, fc=fc, n0=n0: e.tensor_tensor(out=aT[:, fc, n0:n0 + 512], in0=pu[:], in1=sl[k][:], op=ALU.mult), r=[r_pu, r_sl[k]], w=[r_a[fc]])
            if debug and half == 0:
                tails.append(S.dma("sp", dbg_a.rearrange("(kc p) t -> p kc t", p=128), aT[:], r=r_a, key="dbg4"))
            for blk in range(4):
                wt, r_wt = wo2[blk % 2]
                S.dma("pool", wt[:], woutr[:, :, blk * 256:(blk + 1) * 256], w=[r_wt], key="wo2%d" % (blk % 2))
                for j in range(2):
                    cc = blk * 2 + j
                    for g in range(NG):
                        n0 = g * 512
                        pb, r_pb = PB[pbi % 4]
                        pbi += 1
                        for fc in range(NFC):
                            S.op("pe", lambda e, pb=pb, wt=wt, j=j, fc=fc, n0=n0: e.matmul(pb[:], wt[:, fc, j * 128:(j + 1) * 128], aT[:, fc, n0:n0 + 512], start=(fc == 0), stop=(fc == NFC - 1)), r=[r_wt, r_a[fc]], w=[r_pb])
                        S.op("dve", lambda e, pb=pb, cc=cc, n0=n0: e.scalar_tensor_tensor(out=xT[:, cc, n0:n0 + 512], in0=pb[:], scalar=cols[:, 8 + cc:9 + cc], in1=xT[:, cc, n0:n0 + 512], op0=ALU.mult, op1=ALU.add), r=[r_pb, r_x[cc], r_cols], w=[r_x[cc]])
            for g in range(NG):
                n0 = g * 512
                ln_group(C, xT, r_x, g, n0, None, None, rc, [(xT, r_x, lng, 8, lnb, 8, "act")], P)
            for cc in range(8):
                o = S.dma("sp", osrc[:, cc, t0:t0 + HALF], xT[:, cc, :], r=[r_x[cc]], w=[], key="oT%d" % cc)
                tails.append(o)
        if after is not None:
            after(C, tails)
        S.emit(st, tail_waits=tails)


T = 4096
NH = 8
HD = 64
QK_EPS = 1e-6
NEG = -30000.0
import os
LAG = int(os.environ.get("FOX_LAG", "2"))
FOX_DVE = int(os.environ.get("FOX_DVE", "1"))


def build_fox(TT=T, debug=False):
    nc = bass.Bass("TRN2", target_bir_lowering=False)
    dt = lambda n, s, d, k="ExternalInput": nc.dram_tensor(n, list(s), d, kind=k).ap()
    ins = fox_decl(dt, TT)
    yg_d = dt("ygT", [512, TT], BF16, "ExternalOutput")
    fox_phase(nc, "", ins, yg_d, TT, debug=debug)
    return nc


def fox_decl(dt, TT=T, sfx=""):
    return dict(
        x=dt("x" + sfx, [TT, D], F32), cT=dt("cT" + sfx, [128, 8], F32), adaw=dt("adaw" + sfx, [D, 2 * D], F32), adab=dt("adab" + sfx, [128, 16], F32),
        wq=dt("wq" + sfx, [D, 512], F32), wk=dt("wk" + sfx, [D, 512], F32), wv=dt("wv" + sfx, [D, 512], F32), wf=dt("wf" + sfx, [D, 8], F32), wg=dt("wg" + sfx, [D, 512], F32),
        bfb=dt("bfb" + sfx, [128, 8], F32), qgb=dt("qgb" + sfx, [128, 512], F32), kgb=dt("kgb" + sfx, [128, 512], F32))


def fox_phase(nc, pfx, ins, yg_d, TT=T, debug=False, after=None):
    dt = lambda n, s, d, k="ExternalInput": nc.dram_tensor(n, list(s), d, kind=k).ap()
    x_d, cT_d, adaw_d, adab_d = ins["x"], ins["cT"], ins["adaw"], ins["adab"]
    wq_d, wk_d, wv_d, wf_d, wg_d = ins["wq"], ins["wk"], ins["wv"], ins["wf"], ins["wg"]
    bf_d, qg_d, kg_d = ins["bfb"], ins["qgb"], ins["kgb"]
    NGR = TT // 512
    NTL = TT // 128
    if debug:
        dbg_h = dt("dbg_h", [D, 512], BF16, "ExternalOutput")
        dbg_q = dt("dbg_q", [128, 4, 512], BF16, "ExternalOutput")
        dbg_k = dt("dbg_k", [128, 4, TT], BF16, "ExternalOutput")
        dbg_F = dt("dbg_F", [128, NTL, 8], F32, "ExternalOutput")
        dbg_v = dt("dbg_v", [128, NTL, 8, 65], BF16, "ExternalOutput")
    with ExitStack() as st:
        C = Ctx(nc, st, pfx)
        S = C.S
        make_consts(C)
        identb = C.sb("identb", [128, 128], BF16)
        r_identb = C.R("identb")
        S.op("dve", lambda e: e.tensor_copy(out=identb[:], in_=C.ident[:]), r=[C.r_ident], w=[r_identb])
        tri = C.sb("tri", [128, 128], F32)
        r_tri = C.R("tri")
        S.op("pool", lambda e: e.memset(tri[:], 1.0), w=[r_tri])
        S.op("pool", lambda e: e.affine_select(out=tri[:], in_=tri[:], pattern=[[1, 128]], compare_op=ALU.is_ge, fill=0.0, base=0, channel_multiplier=-1), r=[r_tri], w=[r_tri])
        selA = C.sb("selA", [16, 8, 128], F32)
        selB = C.sb("selB", [16, 8, 128], F32)
        sel = C.sb("sel", [16, 8, 128], BF16)
        r_sel = C.R("sel")
        S.op("pool", lambda e: e.memset(selA[:], 1.0), w=[r_sel])
        S.op("pool", lambda e: e.memset(selB[:], 1.0), w=[r_sel])
        S.op("pool", lambda e: e.affine_select(out=selA[:], in_=selA[:], pattern=[[-1, 8], [0, 128]], compare_op=ALU.is_equal, fill=0.0, base=0, channel_multiplier=1), r=[r_sel], w=[r_sel])
        S.op("pool", lambda e: e.affine_select(out=selB[:], in_=selB[:], pattern=[[-1, 8], [0, 128]], compare_op=ALU.is_equal, fill=0.0, base=-8, channel_multiplier=1), r=[r_sel], w=[r_sel])
        S.op("dve", lambda e: e.tensor_tensor(out=sel[:], in0=selA[:], in1=selB[:], op=ALU.add), r=[r_sel], w=[r_sel])
        maskf = C.sb("maskf", [128, 4, 512], F32)
        maskb = C.sb("maskb", [128, 4, 512], BF16)
        r_mask = C.R("mask")
        S.op("pool", lambda e: e.memset(maskf[:], 0.0), w=[r_mask])
        for j in range(4):
            S.op("pool", lambda e, j=j: e.affine_select(out=maskf[:, j, :], in_=maskf[:, j, :], pattern=[[1, 512]], compare_op=ALU.is_ge, fill=NEG, base=-128 * j, channel_multiplier=-1), r=[r_mask], w=[r_mask])
        S.op("dve", lambda e: e.tensor_copy(out=maskb[:], in_=maskf[:]), r=[r_mask], w=[r_mask])
        PB = [(C.ps("pb%d" % i, [128, 512]), C.R("pb%d" % i, True)) for i in range(7)]
        pbb = C.ps("pbb", [128, 1024], BF16)
        r_pbb = C.R("pbb", True)
        wA = [(C.sb("wA%d" % i, [128, 8, 256], F32), C.R("wA%d" % i)) for i in range(2)]
        mods, r_mods = compute_mods(C, cT_d, adaw_d, adab_d, 2, wA, PB[6][0], PB[6][1])
        cols = C.sb("cols", [128, 8], F32)
        r_cols = C.R("cols")
        S.op("dve", lambda e: e.tensor_scalar(out=cols[:], in0=mods[:, 8:16], scalar1=1.0, scalar2=None, op0=ALU.add), r=[r_mods], w=[r_cols])
        wts = {}
        r_w = C.R("wts")
        for nm, d_ in (("wq", wq_d), ("wk", wk_d), ("wv", wv_d), ("wg", wg_d)):
            wts[nm] = C.sb(nm, [128, 8, 512], BF16)
            S.dma("pool", wts[nm][:], d_.rearrange("(kc p) n -> p kc n", p=128), w=[r_w], key=nm)
        wf = C.sb("wf", [128, 8, 8], BF16)
        S.dma("pool", wf[:], wf_d.rearrange("(kc p) n -> p kc n", p=128), w=[r_w], key="wf")
        bfb = C.sb("bfb", [128, 8], F32)
        qgb = C.sb("qgb", [128, 512], F32)
        kgb = C.sb("kgb", [128, 512], F32)
        r_par = C.R("par")
        S.dma("sp", bfb[:], bf_d, w=[r_par], key="bfb")
        S.dma("sp", qgb[:], qg_d, w=[r_par], key="qgb")
        S.dma("sp", kgb[:], kg_d, w=[r_par], key="kgb")
        xt = C.sb("xt", [128, 4, D], F32)
        r_xt = C.R("xt")
        hT = C.sb("hT", [128, 8, 512], BF16)
        r_hT = [C.R("hT%d" % k) for k in range(8)]
        kT = C.sb("kT", [128, 4, TT], BF16)
        r_kT = [C.R("kT%d" % g) for g in range(NGR)]
        qT = C.sb("qT", [128, 4, 512], BF16)
        r_qT = C.R("qT")
        Va = C.sb("Va", [128, NTL, NH, 65], BF16)
        r_V = [C.R("V%d" % g) for g in range(NGR)]
        r_Vones = C.R("Vones")
        S.op("pool", lambda e: e.memset(Va[:, :, :, 64:65], 1.0), w=[r_Vones])
        eg = C.sb("eg", [64, NH, 512], F32)
        r_eg = [C.R("eg%d" % h) for h in range(NH)]
        Fcol = C.sb("Fcol", [128, NTL, NH], F32)
        r_F = C.R("Fcol")
        carry = C.sb("carry", [128, NTL + 1, NH], F32)
        r_carry = C.R("carry")
        S.op("dve", lambda e: e.memset(carry[:, 0, :], 0.0), w=[r_carry])
        biasG = C.sb("biasG", [128, NTL, NH], F32)
        r_bias = C.R("biasG")
        FqT = C.sb("FqT", [16, 512], BF16)
        r_FqT = C.R("FqT")
        sq = C.sb("sq", [128, 512], F32); r_sq = C.R("sq")
        tmp = C.sb("tmp", [128, 512], F32); r_tmp = C.R("tmp")
        qtok = C.sb("qtok", [128, 512], BF16); r_qtok = C.R("qtok")
        ktok = C.sb("ktok", [128, 512], BF16); r_ktok = C.R("ktok")
        ssq = C.sb("ssq", [128, 16], F32); r_ssq = C.R("ssq")
        zf = C.sb("zf", [128, 8], F32); r_zf = C.R("zf")
        fq = C.sb("fq", [128, 8], F32); r_fq = C.R("fq")
        fq16 = C.sb("fq16", [128, 16], BF16); r_fq16 = C.R("fq16")
        PT = [(C.sb("PT%d" % i, [128, 512], BF16), C.R("PT%d" % i)) for i in range(3)]
        oT = C.sb("oT", [64, 512], F32); r_oT = C.R("oT")
        dn = C.sb("dn", [128, 512], F32); r_dn = C.R("dn")
        wv_ = C.sb("wv_", [64, 512], F32); r_wv_ = C.R("wv_")
        yg = [(C.sb("yg%d" % i, [64, 512], BF16), C.R("yg%d" % i)) for i in range(2)]
        tails = []
        xsrc = x_d.rearrange("(i p) d -> p i d", p=128)
        ps_i = 0
        for G in range(NGR):
            S.dma("sp", xt[:], xsrc[:, 4 * G:4 * G + 4, :], w=[r_xt], key="xt")
            for kc in range(8):
                pb, r_pb = PB[kc % 2]
                for i in range(4):
                    S.op("pe", lambda e, pb=pb, i=i, kc=kc: e.transpose(pb[:, i * 128:(i + 1) * 128], xt[:, i, kc * 128:(kc + 1) * 128], C.ident[:]), r=[r_xt, C.r_ident], w=[r_pb])
                if kc % 2 == 0:
                    S.op("act", lambda e, pb=pb, kc=kc: e.activation(out=hT[:, kc, :], in_=pb[:], func=AF.Identity, scale=cols[:, kc:kc + 1], bias=mods[:, kc:kc + 1]), r=[r_pb, r_cols, r_mods], w=[r_hT[kc]])
                else:
                    S.op("dve", lambda e, pb=pb, kc=kc: e.tensor_scalar(out=hT[:, kc, :], in0=pb[:], scalar1=cols[:, kc:kc + 1], scalar2=mods[:, kc:kc + 1], op0=ALU.mult, op1=ALU.add), r=[r_pb, r_cols, r_mods], w=[r_hT[kc]])
            if debug and G == 0:
                tails.append(S.dma("sp", dbg_h.rearrange("(kc p) t -> p kc t", p=128), hT[:], r=r_hT, key="dbg0"))
            for h in range(NH):
                pb, r_pb = PB[2 + h % 2]
                for kc in range(8):
                    S.op("pe", lambda e, pb=pb, h=h, kc=kc: e.matmul(pb[0:64, :], wts["wg"][:, kc, h * 64:(h + 1) * 64], hT[:, kc, :], start=(kc == 0), stop=(kc == 7)), r=[r_w, r_hT[kc]], w=[r_pb])
                S.op("act", lambda e, pb=pb, h=h: e.activation(out=eg[:, h, :], in_=pb[0:64, :], func=AF.Exp, scale=-1.0), r=[r_pb], w=[r_eg[h]])
            for i in range(4):
                tl_ = 4 * G + i
                (pq, r_pq), (pk, r_pk), (pv, r_pv), (pf, r_pf) = PB[2], PB[3], PB[4], PB[5]
                for kc in range(8):
                    lw = hT[:, kc, i * 128:(i + 1) * 128]
                    S.op("pe", lambda e, lw=lw, kc=kc: e.matmul(pq[:], lw, wts["wq"][:, kc, :], start=(kc == 0), stop=(kc == 7)), r=[r_w, r_hT[kc]], w=[r_pq])
                    S.op("pe", lambda e, lw=lw, kc=kc: e.matmul(pk[:], lw, wts["wk"][:, kc, :], start=(kc == 0), stop=(kc == 7)), r=[r_w, r_hT[kc]], w=[r_pk])
                    S.op("pe", lambda e, lw=lw, kc=kc: e.matmul(pv[:], lw, wts["wv"][:, kc, :], start=(kc == 0), stop=(kc == 7)), r=[r_w, r_hT[kc]], w=[r_pv])
                    S.op("pe", lambda e, lw=lw, kc=kc: e.matmul(pf[:, 0:8], lw, wf[:, kc, :], start=(kc == 0), stop=(kc == 7)), r=[r_w, r_hT[kc]], w=[r_pf])
                for which, (pp, r_pp), gb, tok, r_tok, eps_, mul_ in (("q", (pq, r_pq), qgb, qtok, r_qtok, 64.0 * QK_EPS, 1.0), ("k", (pk, r_pk), kgb, ktok, r_ktok, QK_EPS, 1.0 / 64.0)):
                    o8 = 0 if which == "q" else 8
                    S.op("act", lambda e, pp=pp: e.activation(out=sq[:], in_=pp[:], func=AF.Square), r=[r_pp], w=[r_sq])
                    S.op("dve", lambda e, o8=o8: e.tensor_reduce(out=ssq[:, o8:o8 + 8], in_=sq[:].rearrange("p (h d) -> p h d", d=64), axis=AX.X, op=ALU.add), r=[r_sq], w=[r_ssq])
                    S.op("dve", lambda e, o8=o8, eps_=eps_, mul_=mul_: e.tensor_scalar(out=ssq[:, o8:o8 + 8], in0=ssq[:, o8:o8 + 8], scalar1=mul_, scalar2=eps_, op0=ALU.mult, op1=ALU.add), r=[r_ssq], w=[r_ssq])
                    S.op("act", lambda e, o8=o8: e.activation(out=ssq[:, o8:o8 + 8], in_=ssq[:, o8:o8 + 8], func=AF.Sqrt), r=[r_ssq], w=[r_ssq])
                    S.op("dve", lambda e, o8=o8: e.reciprocal(out=ssq[:, o8:o8 + 8], in_=ssq[:, o8:o8 + 8]), r=[r_ssq], w=[r_ssq])
                    S.op("dve", lambda e, pp=pp, o8=o8: e.tensor_tensor(out=tmp[:].rearrange("p (h d) -> p h d", d=64), in0=pp[:].rearrange("p (h d) -> p h d", d=64), in1=ssq[:, o8:o8 + 8].unsqueeze(2).to_broadcast([128, 8, 64]), op=ALU.mult), r=[r_pp, r_ssq], w=[r_tmp])
                    S.op("dve", lambda e, gb=gb, tok=tok: e.tensor_tensor(out=tok[:], in0=tmp[:], in1=gb[:], op=ALU.mult), r=[r_tmp, r_par], w=[r_tok])
                S.op("act", lambda e, tl_=tl_: e.copy(out=Va[:, tl_, :, 0:64], in_=pv[:].rearrange("p (h d) -> p h d", d=64)), r=[r_pv, r_Vones], w=[r_V[G]])
                S.op("dve", lambda e: e.tensor_tensor(out=zf[:], in0=pf[:, 0:8], in1=bfb[:], op=ALU.add), r=[r_pf, r_par], w=[r_zf])
                S.op("act", lambda e: e.activation(out=zf[:], in_=zf[:], func=AF.Exp, scale=-1.0), r=[r_zf], w=[r_zf])
                S.op("act", lambda e: e.activation(out=zf[:], in_=zf[:], func=AF.Ln, bias=1.0), r=[r_zf], w=[r_zf])
                pc, r_pc = PB[6]
                S.op("pe", lambda e: e.matmul(pc[:, 0:8], tri[:], zf[:], start=True, stop=True), r=[r_tri, r_zf], w=[r_pc])
                S.op("pe", lambda e: e.matmul(pc[:, 8:16], C.ones[:], zf[:], start=True, stop=True), r=[C.r_ones, r_zf], w=[r_pc])
                S.op("dve", lambda e, tl_=tl_: e.tensor_tensor(out=Fcol[:, tl_, :], in0=carry[:, tl_, :], in1=pc[:, 0:8], op=ALU.subtract), r=[r_carry, r_pc], w=[r_F])
                S.op("dve", lambda e, tl_=tl_: e.tensor_tensor(out=carry[:, tl_ + 1, :], in0=carry[:, tl_, :], in1=pc[:, 8:16], op=ALU.subtract), r=[r_carry, r_pc], w=[r_carry])
                S.op("dve", lambda e, tl_=tl_, G=G: e.tensor_tensor(out=fq[:], in0=Fcol[:, tl_, :], in1=carry[:, 4 * G, :], op=ALU.subtract), r=[r_F, r_carry], w=[r_fq])
                S.op("dve", lambda e: e.tensor_copy(out=fq16[:, 0:8], in_=fq[:]), r=[r_fq], w=[r_fq16])
                S.op("dve", lambda e: e.tensor_tensor(out=fq[:], in0=fq[:], in1=fq16[:, 0:8], op=ALU.subtract), r=[r_fq, r_fq16], w=[r_fq])
                S.op("dve", lambda e: e.tensor_copy(out=fq16[:, 8:16], in_=fq[:]), r=[r_fq], w=[r_fq16])
                for p in range(4):
                    S.op("pe", lambda e, p=p: e.transpose(pbb[:, p * 128:(p + 1) * 128], qtok[:, p * 128:(p + 1) * 128], identb[:]), r=[r_qtok, r_identb], w=[r_pbb])
                for p in range(4):
                    S.op("pe", lambda e, p=p: e.transpose(pbb[:, 512 + p * 128:512 + (p + 1) * 128], ktok[:, p * 128:(p + 1) * 128], identb[:]), r=[r_ktok, r_identb], w=[r_pbb])
                S.op("act", lambda e, i=i: e.copy(out=qT[:, :, i * 128:(i + 1) * 128], in_=pbb[:, 0:512].rearrange("p (a t) -> p a t", t=128)), r=[r_pbb], w=[r_qT])
                S.op("dve", lambda e, tl_=tl_: e.tensor_copy(out=kT[:, :, tl_ * 128:(tl_ + 1) * 128], in_=pbb[:, 512:1024].rearrange("p (a t) -> p a t", t=128)), r=[r_pbb], w=[r_kT[G]])
                S.op("pe", lambda e: e.transpose(pbb[0:16, 0:128], fq16[:, 0:16], identb[:]), r=[r_fq16, r_identb], w=[r_pbb])
                S.op("dve", lambda e, i=i: e.tensor_copy(out=FqT[:, i * 128:(i + 1) * 128], in_=pbb[0:16, 0:128]), r=[r_pbb], w=[r_FqT])
            nkb = 4 * G + 4
            for h in range(NH):
                S.op("dve", lambda e, h=h, nkb=nkb, G=G: e.tensor_scalar(out=biasG[:, 0:nkb, h], in0=Fcol[:, 0:nkb, h], scalar1=-1.0, scalar2=carry[:, 4 * G, h:h + 1], op0=ALU.mult, op1=ALU.add), r=[r_F, r_carry], w=[r_bias])
            tiles = [(h, kb) for h in range(NH) for kb in range(nkb)]

            def emit_pv(h, kb, PTt, r_PT, nkb=nkb, G=G):
                pO, r_pO = PB[3 + h % 2]
                gk = kb // 4
                S.op("pe", lambda e, pO=pO, PTt=PTt, kb=kb, h=h, nkb=nkb: e.matmul(pO[0:65, :], Va[:, kb, h, :], PTt[:], start=(kb == 0), stop=(kb == nkb - 1)), r=[r_V[gk], r_Vones, r_PT], w=[r_pO])
                if kb == nkb - 1:
                    pB_, r_pB = PB[5]
                    ygt, r_yg = yg[h % 2]
                    S.op("act", lambda e, pO=pO: e.copy(out=oT[:], in_=pO[0:64, :]), r=[r_pO], w=[r_oT])
                    S.op("dve", lambda e, pO=pO: e.tensor_copy(out=dn[64:65, :], in_=pO[64:65, :]), r=[r_pO], w=[r_dn])
                    S.op("pe", lambda e: e.matmul(pB_[0:64, :], C.ones[64:65, 0:64], dn[64:65, :], start=True, stop=True), r=[C.r_ones, r_dn], w=[r_pB])
                    S.op("dve", lambda e, h=h: e.scalar_tensor_tensor(out=wv_[:], in0=eg[:, h, :], scalar=1.0, in1=pB_[0:64, :], op0=ALU.add, op1=ALU.mult), r=[r_eg[h], r_pB], w=[r_wv_])
                    S.op("dve", lambda e: e.reciprocal(out=wv_[:], in_=wv_[:]), r=[r_wv_], w=[r_wv_])
                    S.op("dve", lambda e, ygt=ygt: e.tensor_tensor(out=ygt[:], in0=oT[:], in1=wv_[:], op=ALU.mult), r=[r_oT, r_wv_], w=[r_yg])
                    tails.append(S.dma("sp", yg_d[h * 64:(h + 1) * 64, G * 512:(G + 1) * 512], ygt[:], r=[r_yg], key="yg%d" % (h % 2)))

            pendq = []
            for (h, kb) in tiles:
                p_, e_ = h // 2, h % 2
                pS, r_pS = PB[ps_i % 3]
                PTt, r_PT = PT[ps_i % 3]
                ps_i += 1
                diag = kb >= 4 * G
                gk = kb // 4
                if FOX_DVE:
                    fqb, r_fqb = ((tmp, r_tmp), (sq, r_sq))[h % 2]
                    if kb == 0:
                        pB_, r_pB = PB[5]
                        S.op("pe", lambda e, h=h: e.matmul(pB_[:], sel[:, h, :], FqT[:], start=True, stop=True), r=[r_sel, r_FqT], w=[r_pB])
                        S.op("dve", lambda e, fqb=fqb: e.tensor_copy(out=fqb[:], in_=pB_[:]), r=[r_pB], w=[r_fqb])
                    S.op("pe", lambda e, pS=pS, p_=p_, e_=e_, kb=kb, diag=diag: e.matmul(pS[:], kT[e_ * 64:(e_ + 1) * 64, p_, kb * 128:(kb + 1) * 128], qT[e_ * 64:(e_ + 1) * 64, p_, :], start=True, stop=(not diag)), r=[r_kT[gk], r_qT], w=[r_pS])
                    if diag:
                        S.op("pe", lambda e, pS=pS, kb=kb, G=G: e.matmul(pS[:], identb[:], maskb[:, kb - 4 * G, :], start=False, stop=True), r=[r_identb, r_mask], w=[r_pS])
                    S.op("dve", lambda e, pS=pS, fqb=fqb: e.tensor_tensor(out=pS[:], in0=pS[:], in1=fqb[:], op=ALU.add), r=[r_pS, r_fqb], w=[r_pS])
                else:
                    S.op("pe", lambda e, pS=pS, p_=p_, e_=e_, kb=kb: e.matmul(pS[:], kT[e_ * 64:(e_ + 1) * 64, p_, kb * 128:(kb + 1) * 128], qT[e_ * 64:(e_ + 1) * 64, p_, :], start=True, stop=False), r=[r_kT[gk], r_qT], w=[r_pS])
                    S.op("pe", lambda e, pS=pS, h=h, diag=diag: e.matmul(pS[:], sel[:, h, :], FqT[:], start=False, stop=(not diag)), r=[r_sel, r_FqT], w=[r_pS])
                    if diag:
                        S.op("pe", lambda e, pS=pS, kb=kb, G=G: e.matmul(pS[:], identb[:], maskb[:, kb - 4 * G, :], start=False, stop=True), r=[r_identb, r_mask], w=[r_pS])
                S.op("act", lambda e, pS=pS, PTt=PTt, kb=kb, h=h: e.activation(out=PTt[:], in_=pS[:], func=AF.Exp, bias=biasG[:, kb, h:h + 1], scale=1.0), r=[r_pS, r_bias], w=[r_PT])
                pendq.append((h, kb, PTt, r_PT))
                if len(pendq) > LAG:
                    emit_pv(*pendq.pop(0))
            while pendq:
                emit_pv(*pendq.pop(0))
            if debug and G == 0:
                tails.append(S.dma("sp", dbg_q, qT[:], r=[r_qT], key="dbg1"))
        if debug:
            tails.append(S.dma("sp", dbg_k, kT[:], r=r_kT, key="dbg2"))
            tails.append(S.dma("sp", dbg_F, Fcol[:], r=[r_F], key="dbg3"))
            tails.append(S.dma("sp", dbg_v, Va[:], r=r_V + [r_Vones], key="dbg4"))
        if after is not None:
            after(C, tails)
        S.emit(st, tail_waits=tails)

import math, os
SEQ_VAR = 0

NH = 8
HD = 64
C0 = math.exp(-0.5)
GN_EPS = 64 * 1e-5


class TR:
    def __init__(self, C, name, shape, dt, psum=False):
        self.t = C.ps(name, shape, dt) if psum else C.sb(name, shape, dt)
        self.r = C.R(name, excl=psum)


class _Stop(Exception):
    pass


def build_rwkv(TT=4096, debug=False, stop=None):
    nc = bass.Bass("TRN2", target_bir_lowering=False)
    dt = lambda n, s, d, k="ExternalInput": nc.dram_tensor(n, list(s), d, kind=k).ap()
    xT_d = dt("xT", [D, TT], F32)
    ins = rwkv_decl(dt)
    yT_d = dt("yT", [512, TT], BF16, "ExternalOutput")
    xv = xT_d.rearrange("(kc p) t -> p kc t", p=128)
    rwkv_phase(nc, "", lambda t0, n: xv[:, :, t0:t0 + n], ins, yT_d, TT, debug=debug, stop=stop)
    return nc


RW_BC = ["w0b", "a0b", "kkb", "kab", "rkb", "lngb", "lnbb"]


def rwkv_decl(dt, sfx=""):
    d_ = dict(cT=dt("cT" + sfx, [128, 8], F32), adaw=dt("adaw" + sfx, [D, 2 * D], F32), adab=dt("adab" + sfx, [128, 16], F32), mu=dt("mu" + sfx, [128, 48], F32),
              wr=dt("wr" + sfx, [D, 512], F32), wk=dt("wk" + sfx, [D, 512], F32), wv=dt("wv" + sfx, [D, 512], F32),
              w1=dt("w1" + sfx, [D, 64], F32), a1=dt("a1" + sfx, [D, 64], F32), g1=dt("g1" + sfx, [D, 128], F32),
              w2=dt("w2" + sfx, [64, 512], F32), a2=dt("a2" + sfx, [64, 512], F32), g2=dt("g2" + sfx, [128, 512], F32))
    for n in RW_BC:
        d_[n] = dt(n + sfx, [128, 512], F32)
    return d_


def rwkv_phase(nc, pfx, xsrc_fn, ins, yT_d, TT=4096, debug=False, stop=None, after=None):
    dt = lambda n, s, d, k="ExternalInput": nc.dram_tensor(n, list(s), d, kind=k).ap()
    cT_d, adaw_d, adab_d, mu_d = ins["cT"], ins["adaw"], ins["adab"], ins["mu"]
    wr_d, wk_d, wv_d, w1_d, a1_d, g1_d, w2_d, a2_d, g2_d = [ins[k] for k in ("wr", "wk", "wv", "w1", "a1", "g1", "w2", "a2", "g2")]
    bc_names = RW_BC
    bc_d = {n: ins[n] for n in bc_names}
    GS = 256
    NTG = GS // 128
    NGR = TT // GS
    dbg = {}
    if debug:
        for n in ["r", "kp", "kkn", "a", "sw", "Y", "bonus", "Gm", "yn"]:
            dbg[n] = dt("dbg_" + n, [128, 512], F32, "ExternalOutput")
        dbg["S"] = dt("dbg_S", [128, 4, 64], F32, "ExternalOutput")
        for n, shp in (("Arb", [128, 8, 128]), ("Aak", [128, 8, 128]), ("NT0", [128, 8, 64]), ("Z0", [128, 8, 128]), ("Zf", [128, 8, 128]), ("Y0", [128, 512]), ("RbT", [128, 4, 128]), ("Qm", [128, 4, 128]), ("Hm", [128, 4, 128]), ("gcol", [128, 8]), ("v", [128, 512]), ("ART", [128, 4, 2, 128]), ("KT", [128, 4, 128]), ("BT", [128, 4, 128]), ("Bh", [128, 512]), ("Kh", [128, 512])):
            dbg[n] = dt("dbg_" + n, shp, F32, "ExternalOutput")
    with ExitStack() as st:
        C = Ctx(nc, st, pfx)
        S = C.S
        make_consts(C)
        ident, ones = C.ident, C.ones
        r_ident, r_ones = C.r_ident, C.r_ones
        tails = []
        identb = TR(C, "identb", [128, 128], BF16)
        S.op("dve", lambda e: e.tensor_copy(out=identb.t[:], in_=ident[:]), r=[r_ident], w=[identb.r])
        U = TR(C, "U", [128, 128], F32)
        IU = TR(C, "IUf", [128, 128], F32)
        S.op("pool", lambda e: e.memset(U.t[:], 1.0), w=[U.r])
        S.op("pool", lambda e: e.affine_select(out=U.t[:], in_=U.t[:], pattern=[[1, 128]], compare_op=ALU.is_gt, fill=0.0, base=0, channel_multiplier=-1), r=[U.r], w=[U.r])
        S.op("pool", lambda e: e.memset(IU.t[:], 1.0), w=[IU.r])
        S.op("pool", lambda e: e.affine_select(out=IU.t[:], in_=IU.t[:], pattern=[[1, 128]], compare_op=ALU.is_ge, fill=0.0, base=0, channel_multiplier=-1), r=[IU.r], w=[IU.r])
        SLf = TR(C, "SLf", [128, 128], F32)
        S.op("pool", lambda e: e.memset(SLf.t[:], 1.0), w=[SLf.r])
        S.op("pool", lambda e: e.affine_select(out=SLf.t[:], in_=SLf.t[:], pattern=[[-1, 128]], compare_op=ALU.is_gt, fill=0.0, base=0, channel_multiplier=1), r=[SLf.r], w=[SLf.r])
        mask1 = TR(C, "mask1", [128, 128], F32)
        maskL = TR(C, "maskL", [128, 64], F32)
        idl = TR(C, "idl", [128, 64], F32)
        for ch in range(2):
            ps_ = slice(ch * 64, ch * 64 + 64)
            S.op("dve", lambda e, ps_=ps_: e.tensor_copy(out=mask1.t[ps_, 0:64], in_=U.t[ps_, ps_]), r=[U.r], w=[mask1.r])
            S.op("dve", lambda e, ps_=ps_: e.tensor_copy(out=mask1.t[ps_, 64:128], in_=IU.t[ps_, ps_]), r=[IU.r], w=[mask1.r])
            S.op("dve", lambda e, ps_=ps_: e.tensor_copy(out=maskL.t[ps_, :], in_=SLf.t[ps_, ps_]), r=[SLf.r], w=[maskL.r])
            S.op("dve", lambda e, ps_=ps_: e.tensor_copy(out=idl.t[ps_, :], in_=ident[ps_, ps_]), r=[r_ident], w=[idl.r])
        triC = TR(C, "triC", [128, 128], F32)
        blkC = TR(C, "blkC", [128, 128], F32)
        S.op("dve", lambda e: e.tensor_scalar(out=triC.t[:], in0=IU.t[:], scalar1=-C0, scalar2=None, op0=ALU.mult), r=[IU.r], w=[triC.r])
        S.op("dve", lambda e: e.memset(triC.t[0:64, 64:128], 0.0), w=[triC.r])
        S.op("dve", lambda e: e.memset(blkC.t[:], -C0), w=[blkC.r])
        S.op("dve", lambda e: e.memset(blkC.t[0:64, 64:128], 0.0), w=[blkC.r])
        S.op("dve", lambda e: e.memset(blkC.t[64:128, 0:64], 0.0), w=[blkC.r])
        avg = TR(C, "avg", [128, 1], F32)
        S.op("dve", lambda e: e.memset(avg.t[:], 1.0 / 64.0), w=[avg.r])
        PB = [TR(C, "pb%d" % i, [128, 512], F32, psum=True) for i in range(7)]
        pbb = TR(C, "pbb", [128, 1024], BF16, psum=True)
        wA = [(C.sb("wA%d" % i, [128, 8, 256], F32), C.R("wA%d" % i)) for i in range(2)]
        mods, r_mods = compute_mods(C, cT_d, adaw_d, adab_d, 2, wA, PB[6].t, PB[6].r)
        cols = TR(C, "cols", [128, 8], F32)
        S.op("dve", lambda e: e.tensor_scalar(out=cols.t[:], in0=mods[:, 8:16], scalar1=1.0, scalar2=None, op0=ALU.add), r=[r_mods], w=[cols.r])
        mu = TR(C, "mu", [128, 48], F32)
        S.dma("sp", mu.t[:], mu_d, w=[mu.r], key="mu")
        W = {}
        r_w = C.R("wts")
        for nm, d_, shp in (("wr", wr_d, [128, 8, 512]), ("wk", wk_d, [128, 8, 512]), ("wv", wv_d, [128, 8, 512]), ("w1", w1_d, [128, 8, 64]), ("a1", a1_d, [128, 8, 64]), ("g1", g1_d, [128, 8, 128])):
            W[nm] = C.sb(nm, shp, BF16)
            S.dma("pool", W[nm][:], d_.rearrange("(kc p) n -> p kc n", p=128), w=[r_w], key=nm)
        for nm, d_, shp in (("w2", w2_d, [64, 512]), ("a2", a2_d, [64, 512]), ("g2", g2_d, [128, 512])):
            W[nm] = C.sb(nm, shp, BF16)
            S.dma("pool", W[nm][:], d_, w=[r_w], key=nm)
        BC = {}
        r_bc = C.R("bc")
        for n in bc_names:
            BC[n] = C.sb(n, [128, 512], F32)
            S.dma("sp", BC[n][:], bc_d[n], w=[r_bc], key=n)
        xg = TR(C, "xg", [128, 8, GS + 1], F32)
        S.op("dve", lambda e: e.memset(xg.t[:, :, 0:1], 0.0), w=[xg.r])
        dx = TR(C, "dx", [128, 8, GS], F32)
        xs = [TR(C, "xs%d" % i, [128, 8, GS], BF16) for i in range(2)]
        l1 = [TR(C, "l1_%d" % i, [128, GS], BF16) for i in range(3)]
        l1f = TR(C, "l1f", [128, GS], F32)
        G_ = {n: TR(C, "G_" + n, [128, NTG, 512], F32) for n in ("r", "k", "v")}
        names = ["sw", "Gm", "Gi", "Gp", "GC", "a", "kkn", "kp", "t1", "t2", "Bt", "Kt", "Bh", "Kh", "bonus", "Y", "yn", "gg", "At"]
        X = {n: TR(C, "X_" + n, [128, 512], F32) for n in names}
        sm = TR(C, "sm", [128, 64], F32)
        Z = [TR(C, "Z%d" % i, [128, 8, 128], BF16) for i in range(2)]
        Zf32 = TR(C, "Zf32", [128, 8, 128], F32)
        Rt = TR(C, "Rt", [128, 512], F32)
        ART = TR(C, "ART", [128, 4, 2, 128], F32)
        KT = TR(C, "KT", [128, 4, 128], F32)
        BT = TR(C, "BT", [128, 4, 128], F32)
        Nm = [TR(C, "Nm%d" % i, [128, 8, 64], BF16) for i in range(2)]
        NTm = [TR(C, "NTm%d" % i, [128, 8, 64], BF16) for i in range(2)]
        Arb = TR(C, "Arb", [128, 8, 128], F32)
        Aak = TR(C, "Aak", [128, 8, 128], F32)
        RbT = TR(C, "RbT", [128, 4, 128], F32)
        Qm = TR(C, "Qm", [128, 4, 128], F32)
        Hm = TR(C, "Hm", [128, 4, 128], F32)
        gcol = TR(C, "gcol", [128, 8], F32)
        Y0 = TR(C, "Y0", [128, 512], F32)
        ST = [TR(C, "ST%d" % i, [128, 4, 64], F32) for i in range(2)]
        S.op("dve", lambda e: e.memset(ST[0].t[:], 0.0), w=[ST[0].r])
        ybf = TR(C, "ybf", [128, 512], BF16)
        yTo = [TR(C, "yTo%d" % i, [128, 4, 128], BF16) for i in range(2)]
        ysrc = yT_d.rearrange("(a p) t -> p a t", p=128)

        def hv(t):
            return t.rearrange("p (h d) -> p h d", d=64)

        def bc8(ap):
            return ap.unsqueeze(2).to_broadcast([128, 8, 64])

        st_i = 0
        try:
          for G in range(NGR):
              if G > 0:
                  S.op("dve", lambda e: e.tensor_copy(out=xg.t[:, :, 0:1], in_=xg.t[:, :, GS:GS + 1]), r=[xg.r], w=[xg.r])
              S.dma("sp", xg.t[:, :, 1:GS + 1], xsrc_fn(G * GS, GS), r=[xg.r], w=[xg.r], key="xg")
              for kc in range(8):
                  S.op("dve", lambda e, kc=kc: e.tensor_scalar(out=xg.t[:, kc, 1:GS + 1], in0=xg.t[:, kc, 1:GS + 1], scalar1=cols.t[:, kc:kc + 1], scalar2=mods[:, kc:kc + 1], op0=ALU.mult, op1=ALU.add), r=[xg.r, cols.r, r_mods], w=[xg.r])
              S.op("dve", lambda e: e.tensor_tensor(out=dx.t[:], in0=xg.t[:, :, 0:GS], in1=xg.t[:, :, 1:GS + 1], op=ALU.subtract), r=[xg.r], w=[dx.r])
              for n, nm in enumerate(("r", "k", "v", "w", "a", "g")):
                  xb = xs[n % 2]
                  for kc in range(8):
                      eng = "dve"
                      S.op(eng, lambda e, kc=kc, n=n, xb=xb: e.scalar_tensor_tensor(out=xb.t[:, kc, :], in0=dx.t[:, kc, :], scalar=mu.t[:, n * 8 + kc:n * 8 + kc + 1], in1=xg.t[:, kc, 1:GS + 1], op0=ALU.mult, op1=ALU.add), r=[dx.r, mu.r, xg.r], w=[xb.r])
                  if n < 3:
                      wt = W[("wr", "wk", "wv")[n]]
                      for i in range(NTG):
                          pb = PB[(n * NTG + i) % 2]
                          for kc in range(8):
                              S.op("pe", lambda e, pb=pb, xb=xb, wt=wt, i=i, kc=kc: e.matmul(pb.t[:], xb.t[:, kc, i * 128:(i + 1) * 128], wt[:, kc, :], start=(kc == 0), stop=(kc == 7)), r=[xb.r, r_w], w=[pb.r])
                          S.op("act", lambda e, pb=pb, nm=nm, i=i: e.copy(out=G_[nm].t[:, i, :], in_=pb.t[:]), r=[pb.r], w=[G_[nm].r])
                  else:
                      w1n, w2n, rows = (("w1", "w2", 64), ("a1", "a2", 64), ("g1", "g2", 128))[n - 3]
                      pb = PB[2]
                      for kc in range(8):
                          S.op("pe", lambda e, pb=pb, xb=xb, w1n=w1n, rows=rows, kc=kc: e.matmul(pb.t[0:rows, 0:GS], W[w1n][:, kc, :], xb.t[:, kc, :], start=(kc == 0), stop=(kc == 7)), r=[xb.r, r_w], w=[pb.r])
                      lt = l1[n - 3]
                      if nm == "w":
                          S.op("act", lambda e, pb=pb: e.activation(out=l1f.t[0:64, :], in_=pb.t[0:64, 0:GS], func=AF.Exp, scale=-2.0), r=[pb.r, l1f.r], w=[l1f.r])
                          S.op("dve", lambda e: e.tensor_scalar(out=l1f.t[0:64, :], in0=l1f.t[0:64, :], scalar1=1.0, scalar2=None, op0=ALU.add), r=[l1f.r], w=[l1f.r])
                          S.op("dve", lambda e: e.reciprocal(out=l1f.t[0:64, :], in_=l1f.t[0:64, :]), r=[l1f.r], w=[l1f.r])
                          S.op("dve", lambda e, lt=lt: e.tensor_scalar(out=lt.t[0:64, :], in0=l1f.t[0:64, :], scalar1=2.0, scalar2=-1.0, op0=ALU.mult, op1=ALU.add), r=[l1f.r], w=[lt.r])
                      elif nm == "a":
                          S.op("act", lambda e, pb=pb, lt=lt: e.copy(out=lt.t[0:64, :], in_=pb.t[0:64, 0:GS]), r=[pb.r], w=[lt.r])
                      else:
                          S.op("act", lambda e, pb=pb: e.activation(out=l1f.t[:], in_=pb.t[:, 0:GS], func=AF.Exp, scale=-1.0), r=[pb.r, l1f.r], w=[l1f.r])
                          S.op("dve", lambda e: e.tensor_scalar(out=l1f.t[:], in0=l1f.t[:], scalar1=1.0, scalar2=None, op0=ALU.add), r=[l1f.r], w=[l1f.r])
                          S.op("dve", lambda e: e.reciprocal(out=l1f.t[:], in_=l1f.t[:]), r=[l1f.r], w=[l1f.r])
                          S.op("act", lambda e, lt=lt: e.copy(out=lt.t[:], in_=l1f.t[:]), r=[l1f.r], w=[lt.r])
              if stop == 1:
                  raise _Stop()
              for i in range(NTG):
                  tl_ = G * NTG + i
                  r_ = G_["r"].t[:, i, :]; k_ = G_["k"].t[:, i, :]; v_ = G_["v"].t[:, i, :]
                  rG = [G_[n].r for n in G_]
                  sw, Gm, Gi, Gp, GC, a_, kkn, kp, t1, t2 = [X[n] for n in ("sw", "Gm", "Gi", "Gp", "GC", "a", "kkn", "kp", "t1", "t2")]
                  Bt, Kt, Bh, Kh, bonus, Yt, yn = [X[n] for n in ("Bt", "Kt", "Bh", "Kh", "bonus", "Y", "yn")]
                  pzw, pza, pgg = PB[4], PB[5], PB[6]
                  for (pz_, lt, w2n, rows) in ((pzw, l1[0], "w2", 64), (pza, l1[1], "a2", 64), (pgg, l1[2], "g2", 128)):
                      S.op("pe", lambda e, pz_=pz_, lt=lt, w2n=w2n, rows=rows, i=i: e.matmul(pz_.t[:], lt.t[0:rows, i * 128:(i + 1) * 128], W[w2n][0:rows, :], start=True, stop=True), r=[lt.r, r_w], w=[pz_.r])
                  S.op("act", lambda e: e.copy(out=X["gg"].t[:], in_=pgg.t[:]), r=[pgg.r], w=[X["gg"].r])
                  S.op("dve", lambda e: e.tensor_tensor(out=t1.t[:], in0=pzw.t[:], in1=BC["w0b"][:], op=ALU.add), r=[pzw.r, r_bc], w=[t1.r])
                  S.op("act", lambda e: e.activation(out=t1.t[:], in_=t1.t[:], func=AF.Exp, scale=-1.0), r=[t1.r], w=[t1.r])
                  S.op("dve", lambda e: e.tensor_scalar(out=t1.t[:], in0=t1.t[:], scalar1=1.0, scalar2=None, op0=ALU.add), r=[t1.r], w=[t1.r])
                  S.op("dve", lambda e: e.reciprocal(out=sw.t[:], in_=t1.t[:]), r=[t1.r], w=[sw.r])
                  pL, pLC = PB[2], PB[3]
                  S.op("pe", lambda e: e.matmul(pL.t[:], triC.t[:], sw.t[:], start=True, stop=True), r=[triC.r, sw.r], w=[pL.r])
                  S.op("pe", lambda e: e.matmul(pLC.t[:], blkC.t[:], sw.t[:], start=True, stop=True), r=[blkC.r, sw.r], w=[pLC.r])
                  S.op("act", lambda e: e.activation(out=Gm.t[:], in_=pL.t[:], func=AF.Exp), r=[pL.r], w=[Gm.r])
                  S.op("act", lambda e: e.activation(out=Gi.t[:], in_=pL.t[:], func=AF.Exp, scale=-1.0), r=[pL.r], w=[Gi.r])
                  S.op("dve", lambda e: e.scalar_tensor_tensor(out=Gp.t[:], in0=sw.t[:], scalar=C0, in1=pL.t[:], op0=ALU.mult, op1=ALU.add), r=[sw.r, pL.r], w=[Gp.r])
                  S.op("act", lambda e: e.activation(out=Gp.t[:], in_=Gp.t[:], func=AF.Exp), r=[Gp.r], w=[Gp.r])
                  S.op("act", lambda e: e.activation(out=GC.t[:], in_=pLC.t[:], func=AF.Exp), r=[pLC.r], w=[GC.r])
                  S.op("dve", lambda e: e.tensor_tensor(out=t1.t[:], in0=pza.t[:], in1=BC["a0b"][:], op=ALU.add), r=[pza.r, r_bc, t1.r], w=[t1.r])
                  S.op("act", lambda e: e.activation(out=t1.t[:], in_=t1.t[:], func=AF.Exp, scale=-1.0), r=[t1.r], w=[t1.r])
                  S.op("dve", lambda e: e.tensor_scalar(out=t1.t[:], in0=t1.t[:], scalar1=1.0, scalar2=None, op0=ALU.add), r=[t1.r], w=[t1.r])
                  S.op("dve", lambda e: e.reciprocal(out=a_.t[:], in_=t1.t[:]), r=[t1.r], w=[a_.r])
                  S.op("dve", lambda e, k_=k_: e.tensor_tensor(out=kkn.t[:], in0=k_, in1=BC["kkb"][:], op=ALU.mult), r=rG + [r_bc], w=[kkn.r])
                  S.op("act", lambda e: e.activation(out=t2.t[:], in_=kkn.t[:], func=AF.Square), r=[kkn.r], w=[t2.r])
                  S.op("dve", lambda e: e.tensor_reduce(out=sm.t[:, 0:8], in_=hv(t2.t[:]), axis=AX.X, op=ALU.add), r=[t2.r], w=[sm.r])
                  S.op("dve", lambda e: e.tensor_scalar(out=sm.t[:, 0:8], in0=sm.t[:, 0:8], scalar1=1e-24, scalar2=None, op0=ALU.max), r=[sm.r], w=[sm.r])
                  S.op("act", lambda e: e.activation(out=sm.t[:, 0:8], in_=sm.t[:, 0:8], func=AF.Sqrt), r=[sm.r], w=[sm.r])
                  S.op("dve", lambda e: e.reciprocal(out=sm.t[:, 0:8], in_=sm.t[:, 0:8]), r=[sm.r], w=[sm.r])
                  S.op("dve", lambda e: e.tensor_tensor(out=hv(kkn.t[:]), in0=hv(kkn.t[:]), in1=bc8(sm.t[:, 0:8]), op=ALU.mult), r=[kkn.r, sm.r], w=[kkn.r])
                  S.op("dve", lambda e: e.scalar_tensor_tensor(out=t2.t[:], in0=a_.t[:], scalar=-1.0, in1=BC["kab"][:], op0=ALU.add, op1=ALU.mult), r=[a_.r, r_bc, t2.r], w=[t2.r])
                  S.op("dve", lambda e, k_=k_: e.scalar_tensor_tensor(out=kp.t[:], in0=t2.t[:], scalar=1.0, in1=k_, op0=ALU.add, op1=ALU.mult), r=[t2.r] + rG, w=[kp.r])
                  S.op("dve", lambda e: e.scalar_tensor_tensor(out=X["At"].t[:], in0=kkn.t[:], scalar=-1.0, in1=Gp.t[:], op0=ALU.mult, op1=ALU.mult), r=[kkn.r, Gp.r], w=[X["At"].r])
                  S.op("pool", lambda e: e.tensor_copy(out=Z[0].t[:, :, 0:64], in_=hv(X["At"].t[:])), r=[X["At"].r], w=[Z[0].r])
                  S.op("pool", lambda e: e.tensor_tensor(out=Bt.t[:], in0=kkn.t[:], in1=a_.t[:], op=ALU.mult), r=[kkn.r, a_.r], w=[Bt.r])
                  S.op("pool", lambda e: e.tensor_tensor(out=Bt.t[:], in0=Bt.t[:], in1=Gi.t[:], op=ALU.mult), r=[Bt.r, Gi.r], w=[Bt.r])
                  S.op("dve", lambda e: e.tensor_tensor(out=Kt.t[:], in0=kp.t[:], in1=Gi.t[:], op=ALU.mult), r=[kp.r, Gi.r], w=[Kt.r])
                  S.op("dve", lambda e, r_=r_: e.tensor_tensor(out=Rt.t[:], in0=r_, in1=Gm.t[:], op=ALU.mult), r=rG + [Gm.r], w=[Rt.r])
                  S.op("pool", lambda e: e.tensor_tensor(out=Bh.t[:], in0=Bt.t[:], in1=GC.t[:], op=ALU.mult), r=[Bt.r, GC.r], w=[Bh.r])
                  S.op("pool", lambda e: e.tensor_tensor(out=Kh.t[:], in0=Kt.t[:], in1=GC.t[:], op=ALU.mult), r=[Kt.r, GC.r], w=[Kh.r])
                  S.op("dve", lambda e, r_=r_: e.tensor_tensor(out=t2.t[:], in0=r_, in1=kp.t[:], op=ALU.mult), r=rG + [kp.r, t2.r], w=[t2.r])
                  S.op("dve", lambda e: e.tensor_tensor(out=t2.t[:], in0=t2.t[:], in1=BC["rkb"][:], op=ALU.mult), r=[t2.r, r_bc], w=[t2.r])
                  S.op("dve", lambda e: e.tensor_reduce(out=sm.t[:, 8:16], in_=hv(t2.t[:]), axis=AX.X, op=ALU.add), r=[t2.r], w=[sm.r])
                  S.op("dve", lambda e, v_=v_: e.tensor_tensor(out=hv(bonus.t[:]), in0=hv(v_), in1=bc8(sm.t[:, 8:16]), op=ALU.mult), r=rG + [sm.r], w=[bonus.r])
                  if stop == 2:
                      raise _Stop()
                  for p in range(4):
                      pT = PB[p % 2]
                      for q_, src in enumerate((X["At"], Rt, Kt, Bt)):
                          S.op("pe", lambda e, pT=pT, p=p, q_=q_, src=src: e.transpose(pT.t[:, q_ * 128:(q_ + 1) * 128], src.t[:, p * 128:(p + 1) * 128], ident[:]), r=[src.r, r_ident], w=[pT.r])
                      S.op("dve", lambda e, pT=pT, p=p: e.tensor_copy(out=ART.t[:, p, :, 0:64], in_=pT.t[:, 0:128].rearrange("p (c t) -> p c t", t=64)), r=[pT.r], w=[ART.r])
                      S.op("act", lambda e, pT=pT, p=p: e.copy(out=ART.t[:, p, :, 64:128], in_=pT.t[:, 128:256].rearrange("p (c t) -> p c t", t=64)), r=[pT.r], w=[ART.r])
                      S.op("act", lambda e, pT=pT, p=p: e.copy(out=KT.t[:, p, :], in_=pT.t[:, 256:384]), r=[pT.r], w=[KT.r])
                      S.op("dve", lambda e, pT=pT, p=p: e.tensor_copy(out=BT.t[:, p, :], in_=pT.t[:, 384:512]), r=[pT.r], w=[BT.r])
                  if stop == 3:
                      raise _Stop()
                  pS1a, pS1b, pS2a, pS2b, pS3 = PB[0], PB[1], PB[2], PB[3], PB[4]
                  for h in range(8):
                      p, e_ = h // 2, h % 2
                      fe = slice(e_ * 64, e_ * 64 + 64)
                      pa, pb_ = (pS1a, pS2a) if h < 4 else (pS1b, pS2b)
                      hc = (h % 4) * 128
                      for ch in range(2):
                          tc_ = slice(ch * 64, ch * 64 + 64)
                          S.op("pe", lambda e, pa=pa, fe=fe, p=p, ch=ch, tc_=tc_, hc=hc: e.matmul(pa.t[tc_, hc:hc + 128], BT.t[fe, p, tc_], ART.t[fe, p, ch, :], start=True, stop=True), r=[BT.r, ART.r], w=[pa.r])
                          S.op("pe", lambda e, pb_=pb_, fe=fe, p=p, ch=ch, tc_=tc_, hc=hc: e.matmul(pb_.t[tc_, hc:hc + 128], KT.t[fe, p, tc_], ART.t[fe, p, ch, :], start=True, stop=True), r=[KT.r, ART.r], w=[pb_.r])
                          S.op("pe", lambda e, fe=fe, p=p, ch=ch, tc_=tc_, h=h: e.matmul(pS3.t[tc_, h * 64:(h + 1) * 64], ART.t[fe, p, ch, 0:64], BT.t[fe, p, tc_], start=True, stop=True), r=[BT.r, ART.r], w=[pS3.r])
                  m1b = mask1.t[:].unsqueeze(1).to_broadcast([128, 4, 128])
                  for half, (pa, pb_) in enumerate(((pS1a, pS2a), (pS1b, pS2b))):
                      hs = slice(half * 4, half * 4 + 4)
                      S.op("dve", lambda e, pa=pa, hs=hs: e.tensor_tensor(out=Arb.t[:, hs, :], in0=pa.t[:].rearrange("p (h t) -> p h t", t=128), in1=m1b, op=ALU.mult), r=[pa.r, mask1.r], w=[Arb.r])
                      S.op("dve", lambda e, pb_=pb_, hs=hs: e.tensor_tensor(out=Aak.t[:, hs, :], in0=pb_.t[:].rearrange("p (h t) -> p h t", t=128), in1=m1b, op=ALU.mult), r=[pb_.r, mask1.r], w=[Aak.r])
                  S.op("dve", lambda e: e.tensor_tensor(out=NTm[0].t[:], in0=hv(pS3.t[:]), in1=maskL.t[:].unsqueeze(1).to_broadcast([128, 8, 64]), op=ALU.mult), r=[pS3.r, maskL.r], w=[NTm[0].r])
                  S.op("act", lambda e: e.copy(out=Nm[0].t[:], in_=Arb.t[:, :, 0:64]), r=[Arb.r], w=[Nm[0].r])
                  if stop == 4:
                      raise _Stop()
                  if debug and tl_ == 0:
                      tails.append(S.dma("sp", dbg["Arb"], Arb.t[:], r=[Arb.r], key="dbg_Arb"))
                  if debug and tl_ == 0:
                      tails.append(S.dma("sp", dbg["Aak"], Aak.t[:], r=[Aak.r], key="dbg_Aak"))
                  if debug and tl_ == 0:
                      tails.append(S.dma("sp", dbg["NT0"], NTm[0].t[:], r=[NTm[0].r], key="dbg_NT0"))
                  if debug and tl_ == 0:
                      tails.append(S.dma("sp", dbg["ART"], ART.t[:], r=[ART.r], key="dbg_ART"))
                  if debug and tl_ == 0:
                      tails.append(S.dma("sp", dbg["KT"], KT.t[:], r=[KT.r], key="dbg_KT"))
                  if debug and tl_ == 0:
                      tails.append(S.dma("sp", dbg["BT"], BT.t[:], r=[BT.r], key="dbg_BT"))
                  if debug and tl_ == 0:
                      tails.append(S.dma("sp", dbg["Bh"], Bh.t[:], r=[Bh.r], key="dbg_Bh"))
                  if debug and tl_ == 0:
                      tails.append(S.dma("sp", dbg["Kh"], Kh.t[:], r=[Kh.r], key="dbg_Kh"))
                  pX = PB[5]
                  for h in range(8):
                      for ch in range(2):
                          tc_ = slice(ch * 64, ch * 64 + 64)
                          S.op("pe", lambda e, h=h, tc_=tc_, v_=v_: e.matmul(pX.t[tc_, h * 64:(h + 1) * 64], Aak.t[tc_, h, 0:64], v_[tc_, h * 64:(h + 1) * 64], start=True, stop=True), r=[Aak.r] + rG, w=[pX.r])
                  S.op("act", lambda e: e.copy(out=Z[0].t[:, :, 64:128], in_=hv(pX.t[:])), r=[pX.r], w=[Z[0].r])
                  if stop == 5:
                      raise _Stop()
                  if debug and tl_ == 0:
                      tails.append(S.dma("sp", dbg["Z0"], Z[0].t[:], r=[Z[0].r], key="dbg_Z0"))
                  zi = 0
                  for lev in range(6):
                      Nc, NTc = Nm[lev % 2], NTm[lev % 2]
                      Nn, NTn = Nm[(lev + 1) % 2], NTm[(lev + 1) % 2]
                      pZa, pZb = PB[0], PB[1]
                      Zc, Zn = Z[zi], Z[1 - zi]
                      for h in range(8):
                          pz = pZa if h < 4 else pZb
                          hc = (h % 4) * 128
                          for ch in range(2):
                              tc_ = slice(ch * 64, ch * 64 + 64)
                              S.op("pe", lambda e, pz=pz, Nc=Nc, Zc=Zc, h=h, hc=hc, tc_=tc_: e.matmul(pz.t[tc_, hc:hc + 128], Nc.t[tc_, h, :], Zc.t[tc_, h, :], start=True, stop=True), r=[Nc.r, Zc.r], w=[pz.r])
                      if lev == 5:
                          Zn = Zf32
                      S.op("dve", lambda e, Zc=Zc, Zn=Zn: e.tensor_tensor(out=Zn.t[:, 0:4, :], in0=Zc.t[:, 0:4, :], in1=pZa.t[:].rearrange("p (h t) -> p h t", t=128), op=ALU.add), r=[Zc.r, pZa.r], w=[Zn.r])
                      S.op("dve", lambda e, Zc=Zc, Zn=Zn: e.tensor_tensor(out=Zn.t[:, 4:8, :], in0=Zc.t[:, 4:8, :], in1=pZb.t[:].rearrange("p (h t) -> p h t", t=128), op=ALU.add), r=[Zc.r, pZb.r], w=[Zn.r])
                      zi = 1 - zi
                      if lev < 5:
                          pN, pNT = PB[2], PB[3]
                          for h in range(8):
                              for ch in range(2):
                                  tc_ = slice(ch * 64, ch * 64 + 64)
                                  S.op("pe", lambda e, NTc=NTc, Nc=Nc, h=h, tc_=tc_: e.matmul(pN.t[tc_, h * 64:(h + 1) * 64], NTc.t[tc_, h, :], Nc.t[tc_, h, :], start=True, stop=True), r=[Nc.r, NTc.r], w=[pN.r])
                                  if lev < 4:
                                      S.op("pe", lambda e, NTc=NTc, Nc=Nc, h=h, tc_=tc_: e.matmul(pNT.t[tc_, h * 64:(h + 1) * 64], Nc.t[tc_, h, :], NTc.t[tc_, h, :], start=True, stop=True), r=[Nc.r, NTc.r], w=[pNT.r])
                          S.op("act", lambda e, Nn=Nn: e.copy(out=Nn.t[:], in_=hv(pN.t[:])), r=[pN.r], w=[Nn.r])
                          if lev < 4:
                              S.op("act", lambda e, NTn=NTn: e.copy(out=NTn.t[:], in_=hv(pNT.t[:])), r=[pNT.r], w=[NTn.r])
                  if stop == 6:
                      raise _Stop()
                  Zf = Zf32
                  pY0 = PB[4]
                  for h in range(8):
                      for ch in range(2):
                          tc_ = slice(ch * 64, ch * 64 + 64)
                          S.op("pe", lambda e, h=h, tc_=tc_: e.matmul(pY0.t[tc_, h * 64:(h + 1) * 64], Arb.t[tc_, h, 64:128], Zf.t[tc_, h, 64:128], start=True, stop=False), r=[Arb.r, Zf.r], w=[pY0.r])
                          S.op("pe", lambda e, h=h, tc_=tc_, v_=v_: e.matmul(pY0.t[tc_, h * 64:(h + 1) * 64], Aak.t[tc_, h, 64:128], v_[tc_, h * 64:(h + 1) * 64], start=False, stop=True), r=[Aak.r] + rG, w=[pY0.r])
                  S.op("act", lambda e: e.copy(out=Y0.t[:], in_=pY0.t[:]), r=[pY0.r], w=[Y0.r])
                  if stop == 7:
                      raise _Stop()
                  if debug and tl_ == 0:
                      tails.append(S.dma("sp", dbg["Zf"], Zf.t[:], r=[Zf.r], key="dbg_Zf"))
                  if debug and tl_ == 0:
                      tails.append(S.dma("sp", dbg["Y0"], Y0.t[:], r=[Y0.r], key="dbg_Y0"))
                  pR, pQ, pH, pg = PB[5], PB[6], PB[0], PB[1]
                  for h in range(8):
                      p, e_ = h // 2, h % 2
                      fe = slice(e_ * 64, e_ * 64 + 64)
                      for ch in range(2):
                          tc_ = slice(ch * 64, ch * 64 + 64)
                          cs = slice(p * 128 + ch * 64, p * 128 + ch * 64 + 64)
                          hs_ = slice(h * 64, h * 64 + 64)
                          S.op("pe", lambda e, fe=fe, cs=cs, tc_=tc_, h=h: e.matmul(pR.t[fe, cs], Zf.t[tc_, h, 0:64], Arb.t[tc_, h, 64:128], start=True, stop=True), r=[Zf.r, Arb.r], w=[pR.r])
                          S.op("pe", lambda e, fe=fe, cs=cs, tc_=tc_, h=h, hs_=hs_: e.matmul(pQ.t[fe, cs], Zf.t[tc_, h, 0:64], Bh.t[tc_, hs_], start=True, stop=True), r=[Zf.r, Bh.r], w=[pQ.r])
                          S.op("pe", lambda e, fe=fe, cs=cs, tc_=tc_, h=h, hs_=hs_: e.matmul(pH.t[fe, cs], Bh.t[tc_, hs_], Zf.t[tc_, h, 64:128], start=True, stop=False), r=[Zf.r, Bh.r], w=[pH.r])
                          S.op("pe", lambda e, fe=fe, cs=cs, tc_=tc_, hs_=hs_, v_=v_: e.matmul(pH.t[fe, cs], Kh.t[tc_, hs_], v_[tc_, hs_], start=False, stop=True), r=[Kh.r] + rG, w=[pH.r])
                          S.op("pe", lambda e, fe=fe, tc_=tc_, hs_=hs_, p=p, ch=ch: e.matmul(pg.t[fe, p * 2 + ch:p * 2 + ch + 1], GC.t[tc_, hs_], avg.t[tc_, :], start=True, stop=True), r=[GC.r, avg.r], w=[pg.r])
                  for ch in range(2):
                      S.op("dve", lambda e, ch=ch: e.tensor_tensor(out=RbT.t[:, :, ch * 64:(ch + 1) * 64], in0=pR.t[:].rearrange("p (a t) -> p a t", t=128)[:, :, ch * 64:(ch + 1) * 64], in1=ART.t[:, :, ch, 64:128], op=ALU.add), r=[pR.r, ART.r], w=[RbT.r])
                  S.op("act", lambda e: e.copy(out=Qm.t[:], in_=pQ.t[:].rearrange("p (a t) -> p a t", t=128)), r=[pQ.r], w=[Qm.r])
                  S.op("act", lambda e: e.copy(out=Hm.t[:], in_=pH.t[:].rearrange("p (a t) -> p a t", t=128)), r=[pH.r], w=[Hm.r])
                  S.op("act", lambda e: e.copy(out=gcol.t[:], in_=pg.t[:, 0:8]), r=[pg.r], w=[gcol.r])
                  if stop == 8:
                      raise _Stop()
                  if debug and tl_ == 0:
                      tails.append(S.dma("sp", dbg["RbT"], RbT.t[:], r=[RbT.r], key="dbg_RbT"))
                  if debug and tl_ == 0:
                      tails.append(S.dma("sp", dbg["Qm"], Qm.t[:], r=[Qm.r], key="dbg_Qm"))
                  if debug and tl_ == 0:
                      tails.append(S.dma("sp", dbg["Hm"], Hm.t[:], r=[Hm.r], key="dbg_Hm"))
                  if debug and tl_ == 0:
                      tails.append(S.dma("sp", dbg["gcol"], gcol.t[:], r=[gcol.r], key="dbg_gcol"))
                  for ch in range(2):
                      tc_ = slice(ch * 64, ch * 64 + 64)
                      Sc, Sn = ST[st_i], ST[1 - st_i]
                      pYe, pS_ = (PB[2], PB[4]), PB[3]
                      for h in range(8):
                          p, e_ = h // 2, h % 2
                          fe = slice(e_ * 64, e_ * 64 + 64)
                          cs = slice(ch * 64, ch * 64 + 64)
                          pY = pYe[e_]
                          S.op("pe", lambda e, fe=fe, p=p, cs=cs, tc_=tc_, h=h, Sc=Sc, pY=pY: e.matmul(pY.t[tc_, h * 64:(h + 1) * 64], RbT.t[fe, p, cs], Sc.t[fe, p, :], start=True, stop=True), r=[RbT.r, Sc.r], w=[pY.r])
                          S.op("pe", lambda e, fe=fe, p=p, cs=cs, Sc=Sc: e.matmul(pS_.t[fe, p * 64:(p + 1) * 64], Qm.t[fe, p, cs], Sc.t[fe, p, :], start=True, stop=True), r=[Qm.r, Sc.r], w=[pS_.r])
                      v4 = lambda ap: ap.rearrange("p (a e d) -> p a e d", e=2, d=64)
                      for e_ in range(2):
                          S.op("dve", lambda e, tc_=tc_, e_=e_: e.tensor_tensor(out=v4(Yt.t[tc_, :])[:, :, e_, :], in0=v4(Y0.t[tc_, :])[:, :, e_, :], in1=v4(pYe[e_].t[tc_, :])[:, :, e_, :], op=ALU.add), r=[Y0.r, pYe[e_].r], w=[Yt.r])
                      for p in range(4):
                          S.op("dve", lambda e, p=p, ch=ch, Sc=Sc, Sn=Sn: e.scalar_tensor_tensor(out=Sn.t[:, p, :], in0=Sc.t[:, p, :], scalar=gcol.t[:, p * 2 + ch:p * 2 + ch + 1], in1=pS_.t[:, p * 64:(p + 1) * 64], op0=ALU.mult, op1=ALU.add), r=[Sc.r, gcol.r, pS_.r], w=[Sn.r])
                      S.op("dve", lambda e, ch=ch, Sn=Sn: e.tensor_tensor(out=Sn.t[:], in0=Sn.t[:], in1=Hm.t[:, :, ch * 64:(ch + 1) * 64], op=ALU.add), r=[Sn.r, Hm.r], w=[Sn.r])
                      st_i = 1 - st_i
                  if stop == 9:
                      raise _Stop()
                  S.op("dve", lambda e: e.tensor_reduce(out=sm.t[:, 16:24], in_=hv(Yt.t[:]), axis=AX.X, op=ALU.add), r=[Yt.r], w=[sm.r])
                  S.op("act", lambda e: e.activation(out=t2.t[:], in_=Yt.t[:], func=AF.Square), r=[Yt.r, t2.r], w=[t2.r])
                  S.op("dve", lambda e: e.tensor_reduce(out=sm.t[:, 24:32], in_=hv(t2.t[:]), axis=AX.X, op=ALU.add), r=[t2.r], w=[sm.r])
                  S.op("dve", lambda e: e.tensor_scalar(out=sm.t[:, 16:24], in0=sm.t[:, 16:24], scalar1=1.0 / 64, scalar2=None, op0=ALU.mult), r=[sm.r], w=[sm.r])
                  S.op("dve", lambda e: e.tensor_tensor(out=sm.t[:, 32:40], in0=sm.t[:, 16:24], in1=sm.t[:, 16:24], op=ALU.mult), r=[sm.r], w=[sm.r])
                  S.op("dve", lambda e: e.scalar_tensor_tensor(out=sm.t[:, 24:32], in0=sm.t[:, 24:32], scalar=1.0 / 64, in1=sm.t[:, 32:40], op0=ALU.mult, op1=ALU.subtract), r=[sm.r], w=[sm.r])
                  S.op("dve", lambda e: e.tensor_scalar(out=sm.t[:, 24:32], in0=sm.t[:, 24:32], scalar1=GN_EPS, scalar2=None, op0=ALU.add), r=[sm.r], w=[sm.r])
                  S.op("act", lambda e: e.activation(out=sm.t[:, 24:32], in_=sm.t[:, 24:32], func=AF.Sqrt), r=[sm.r], w=[sm.r])
                  S.op("dve", lambda e: e.reciprocal(out=sm.t[:, 24:32], in_=sm.t[:, 24:32]), r=[sm.r], w=[sm.r])
                  S.op("dve", lambda e: e.tensor_tensor(out=hv(yn.t[:]), in0=hv(Yt.t[:]), in1=bc8(sm.t[:, 16:24]), op=ALU.subtract), r=[Yt.r, sm.r], w=[yn.r])
                  S.op("dve", lambda e: e.tensor_tensor(out=hv(yn.t[:]), in0=hv(yn.t[:]), in1=bc8(sm.t[:, 24:32]), op=ALU.mult), r=[yn.r, sm.r], w=[yn.r])
                  S.op("pool", lambda e: e.tensor_tensor(out=yn.t[:], in0=yn.t[:], in1=BC["lngb"][:], op=ALU.mult), r=[yn.r, r_bc], w=[yn.r])
                  S.op("pool", lambda e: e.tensor_tensor(out=yn.t[:], in0=yn.t[:], in1=BC["lnbb"][:], op=ALU.add), r=[yn.r, r_bc], w=[yn.r])
                  S.op("dve", lambda e: e.tensor_tensor(out=yn.t[:], in0=yn.t[:], in1=bonus.t[:], op=ALU.add), r=[yn.r, bonus.r], w=[yn.r])
                  S.op("dve", lambda e: e.tensor_tensor(out=ybf.t[:], in0=yn.t[:], in1=X["gg"].t[:], op=ALU.mult), r=[yn.r, X["gg"].r], w=[ybf.r])
                  for p in range(4):
                      S.op("pe", lambda e, p=p: e.transpose(pbb.t[:, p * 128:(p + 1) * 128], ybf.t[:, p * 128:(p + 1) * 128], identb.t[:]), r=[ybf.r, identb.r], w=[pbb.r])
                  yo = yTo[tl_ % 2]
                  S.op("act", lambda e, yo=yo: e.copy(out=yo.t[:], in_=pbb.t[:, 0:512].rearrange("p (a t) -> p a t", t=128)), r=[pbb.r], w=[yo.r])
                  tails.append(S.dma("sp", ysrc[:, :, tl_ * 128:(tl_ + 1) * 128], yo.t[:], r=[yo.r], key="yTo%d" % (tl_ % 2)))
                  if debug and tl_ == 0:
                      S.op("act", lambda e, r_=r_: e.copy(out=t1.t[:], in_=r_), r=rG + [t1.r], w=[t1.r])
                      S.op("act", lambda e, v_=v_: e.copy(out=t2.t[:], in_=v_), r=rG + [t2.r], w=[t2.r])
                      tails.append(S.dma("sp", dbg["v"], t2.t[:], r=[t2.r], key="dbg_v"))
                      for n, src in (("r", t1), ("kp", kp), ("kkn", kkn), ("a", a_), ("sw", sw), ("Y", Yt), ("bonus", bonus), ("Gm", Gm), ("yn", yn)):
                          tails.append(S.dma("sp", dbg[n], src.t[:], r=[src.r], key="dbg_" + n))
                      tails.append(S.dma("sp", dbg["S"], ST[st_i].t[:], r=[ST[st_i].r], key="dbg_S"))
        except _Stop:
            pass
        if after is not None:
            after(C, tails)
        S.emit(st, tail_waits=tails)


PAIRS = [[0, 1], [2, 3], [4, 5], [6, 7]]
TL_KEYS = ("cT", "adaw", "adab", "lng", "lnb", "wo", "win", "wout")


def tl_decl(dt, sfx):
    return dict(cT=dt("cT" + sfx, [128, 8], F32), adaw=dt("adaw" + sfx, [D, 4 * D], F32), adab=dt("adab" + sfx, [128, 32], F32),
                lng=dt("lng" + sfx, [128, 16], F32), lnb=dt("lnb" + sfx, [128, 16], F32),
                wo=dt("wo" + sfx, [D, D], F32), win=dt("win" + sfx, [D, 2 * DFF], F32), wout=dt("wout" + sfx, [DFF, D], F32))


def build_fused(stop=99):
    nc = bass.Bass("TRN2", target_bir_lowering=False)
    dt = lambda n, s, d, k="ExternalInput": nc.dram_tensor(n, list(s), d, kind=k).ap()
    it = lambda n, s, d: nc.dram_tensor(n, list(s), d).ap()
    fox_in = fox_decl(dt, 4096, "_f")
    oT_d = dt("oT", [D, 2048], F32, "ExternalOutput")
    cin1, cout1 = it("cin1", [4, 128, 4096], BF16), it("cout1", [4, 2, 128, 4096], BF16)
    cin2, cout2 = it("cin2", [8, 128, 2048], F32), it("cout2", [8, 2, 128, 2048], F32)
    cin3, cout3 = it("cin3", [4, 128, 4096], BF16), it("cout3", [4, 2, 128, 4096], BF16)

    def gather(cin, cout, key, nblk):
        def after(C, tails):
            prev = list(tails)
            for k in range(nblk):
                o = C.S.cc("AllGather", PAIRS, cin[k], cout[k].rearrange("r p t -> (r p) t"), key=key, extra=prev)
                prev = [o]
            tails.append(o)
        return after

    cin1_v = cin1.rearrange("k p t -> (k p) t")
    cin3_v = cin3.rearrange("k p t -> (k p) t")
    cin2_v = cin2.rearrange("k p t -> (k p) t")
    y1_v = cout1
    y3_v = cout3
    fox_phase(nc, "f_", fox_in, cin1_v, 4096, after=gather(cin1, cout1, "g1", 4))
    if stop <= 1:
        return nc
    phase_end(nc)
    sel_d = dt("sel", [128, 2], F32)
    xTh_d = dt("xTh", [D, 2048], F32)
    t = tl_decl(dt, "_t0")
    tl_phase(nc, "t0_", y1_v, xTh_d, t["cT"], t["adaw"], t["adab"], t["lng"], t["lnb"], t["wo"], t["win"], t["wout"], cin2_v, 2048, 1024,
             sel_d=sel_d, after=gather(cin2, cout2, "g2", 8))
    if stop <= 2:
        return nc
    phase_end(nc)
    rw_in = rwkv_decl(dt, "_r")
    c2v = cout2.rearrange("kc r p t -> r p kc t")
    rwkv_phase(nc, "r_", lambda t0, n: c2v[t0 // 2048, :, :, (t0 % 2048):(t0 % 2048) + n], rw_in, cin3_v, 4096, after=gather(cin3, cout3, "g3", 4))
    if stop <= 3:
        return nc
    phase_end(nc)
    t = tl_decl(dt, "_t1")
    tl_phase(nc, "t1_", y3_v, cin2_v, t["cT"], t["adaw"], t["adab"], t["lng"], t["lnb"], t["wo"], t["win"], t["wout"], oT_d, 2048, 1024, sel_d=sel_d)
    return nc


NCORES = 8
_PROGS = {}


def _prog(name, fn):
    if name not in _PROGS:
        _PROGS[name] = fn()
    return _PROGS[name]


def _col_layout(v):
    return np.ascontiguousarray(np.asarray(v).reshape(-1, 128).T)


def _bc(v, n=512):
    v = np.asarray(v, dtype=np.float32).reshape(-1)
    return np.ascontiguousarray(np.broadcast_to(v[None, :], (128, v.shape[0])))


def _run(nc, in_maps):
    res = run_bass_kernel_spmd(nc, in_maps, core_ids=list(range(NCORES)))
    return res.results


def _tl_maps(yT_list, xT_list, c, ada_w_i, ada_b_i, ln_g_i, ln_b_i, wo, win, wout):
    maps = []
    adaw = np.ascontiguousarray(ada_w_i[:, 2 * D:])
    adab = _col_layout(ada_b_i[2 * D:])
    lng = _col_layout(ln_g_i.reshape(-1))
    lnb = _col_layout(ln_b_i.reshape(-1))
    for core in range(NCORES):
        b, th = core // 2, core % 2
        ts = slice(th * 2048, (th + 1) * 2048)
        maps.append({
            "yT": np.ascontiguousarray(yT_list[b][:, ts]),
            "xT": np.ascontiguousarray(xT_list[b][:, ts]),
            "cT": _col_layout(c[b]),
            "adaw": adaw, "adab": adab, "lng": lng, "lnb": lnb,
            "wo": wo, "win": win, "wout": wout,
        })
    return maps


def kernel_unfused(x, c, ada_w, ada_b, ln_g, ln_b, ffn_w_in, ffn_w_out,
           fox_w_in, fox_b_f, fox_q_g, fox_k_g, fox_w_o,
           rwkv_mu, rwkv_w_rkv, rwkv_w0, rwkv_w1, rwkv_w2, rwkv_a0, rwkv_a1, rwkv_a2,
           rwkv_g1, rwkv_g2, rwkv_k_k, rwkv_k_a, rwkv_r_k, rwkv_lnx_g, rwkv_lnx_b, rwkv_w_o):
    f = lambda a: np.ascontiguousarray(np.asarray(a, dtype=np.float32))
    x, c, ada_w, ada_b, ln_g, ln_b = f(x), f(c), f(ada_w), f(ada_b), f(ln_g), f(ln_b)
    ffn_w_in, ffn_w_out = f(ffn_w_in), f(ffn_w_out)
    B = x.shape[0]
    w_in = f(fox_w_in)[0]
    b_f, q_g, k_g = f(fox_b_f)[0], f(fox_q_g)[0], f(fox_k_g)[0]
    maps = []
    for core in range(NCORES):
        b, hh = core // 2, core % 2
        sl = slice(hh * 512, (hh + 1) * 512)
        maps.append({
            "x": x[b],
            "cT": _col_layout(c[b]),
            "adaw": np.ascontiguousarray(ada_w[0][:, 0:2 * D]),
            "adab": _col_layout(ada_b[0][0:2 * D]),
            "wq": np.ascontiguousarray(w_in[:, 0:D][:, sl]),
            "wk": np.ascontiguousarray(w_in[:, D:2 * D][:, sl]),
            "wv": np.ascontiguousarray(w_in[:, 2 * D:3 * D][:, sl]),
            "wf": np.ascontiguousarray(w_in[:, 3 * D + hh * 8:3 * D + hh * 8 + 8]),
            "wg": np.ascontiguousarray(w_in[:, 3 * D + 16:][:, sl]),
            "bfb": _bc(b_f[hh * 8:(hh + 1) * 8]),
            "qgb": _bc(np.tile(q_g, 8)),
            "kgb": _bc(np.tile(k_g, 8)),
        })
    r1 = _run(_prog("fox", build_fox), maps)
    yT = [np.concatenate([r1[2 * b]["ygT"], r1[2 * b + 1]["ygT"]], axis=0) for b in range(B)]
    xT = [np.ascontiguousarray(x[b].T) for b in range(B)]
    tlp = _prog("tl", build_tl)
    r2 = _run(tlp, _tl_maps(yT, xT, c, ada_w[0], ada_b[0], ln_g[0], ln_b[0], f(fox_w_o)[0], ffn_w_in[0], ffn_w_out[0]))
    x1T = [np.concatenate([r2[2 * b]["oT"], r2[2 * b + 1]["oT"]], axis=1) for b in range(B)]
    mu = f(rwkv_mu)[0]
    w_rkv = f(rwkv_w_rkv)[0]
    P = dict(w0=f(rwkv_w0)[0], w1=f(rwkv_w1)[0], w2=f(rwkv_w2)[0], a0=f(rwkv_a0)[0], a1=f(rwkv_a1)[0], a2=f(rwkv_a2)[0],
             g1=f(rwkv_g1)[0], g2=f(rwkv_g2)[0], k_k=f(rwkv_k_k)[0], k_a=f(rwkv_k_a)[0], r_k=f(rwkv_r_k)[0].reshape(-1),
             lnx_g=f(rwkv_lnx_g)[0], lnx_b=f(rwkv_lnx_b)[0])
    mu_l = np.concatenate([_col_layout(mu[n]) for n in range(6)], axis=1)
    maps = []
    for core in range(NCORES):
        b, hh = core // 2, core % 2
        sl = slice(hh * 512, (hh + 1) * 512)
        maps.append({
            "xT": x1T[b],
            "cT": _col_layout(c[b]),
            "adaw": np.ascontiguousarray(ada_w[1][:, 0:2 * D]),
            "adab": _col_layout(ada_b[1][0:2 * D]),
            "mu": mu_l,
            "wr": np.ascontiguousarray(w_rkv[0][:, sl]),
            "wk": np.ascontiguousarray(w_rkv[1][:, sl]),
            "wv": np.ascontiguousarray(w_rkv[2][:, sl]),
            "w1": P["w1"], "a1": P["a1"], "g1": P["g1"],
            "w2": np.ascontiguousarray(P["w2"][:, sl]),
            "a2": np.ascontiguousarray(P["a2"][:, sl]),
            "g2": np.ascontiguousarray(P["g2"][:, sl]),
            "w0b": _bc(P["w0"][sl]), "a0b": _bc(P["a0"][sl]), "kkb": _bc(P["k_k"][sl]), "kab": _bc(P["k_a"][sl]),
            "rkb": _bc(P["r_k"][sl]), "lngb": _bc(P["lnx_g"][sl]), "lnbb": _bc(P["lnx_b"][sl]),
        })
    r3 = _run(_prog("rwkv", build_rwkv), maps)
    y2T = [np.concatenate([r3[2 * b]["yT"], r3[2 * b + 1]["yT"]], axis=0) for b in range(B)]
    r4 = _run(tlp, _tl_maps(y2T, x1T, c, ada_w[1], ada_b[1], ln_g[1], ln_b[1], f(rwkv_w_o)[0], ffn_w_in[1], ffn_w_out[1]))
    out = np.empty(x.shape, np.float32)
    for core in range(NCORES):
        b, th = core // 2, core % 2
        out[b, th * 2048:(th + 1) * 2048, :] = r4[core]["oT"].T
    return out


def kernel(x, c, ada_w, ada_b, ln_g, ln_b, ffn_w_in, ffn_w_out,
           fox_w_in, fox_b_f, fox_q_g, fox_k_g, fox_w_o,
           rwkv_mu, rwkv_w_rkv, rwkv_w0, rwkv_w1, rwkv_w2, rwkv_a0, rwkv_a1, rwkv_a2,
           rwkv_g1, rwkv_g2, rwkv_k_k, rwkv_k_a, rwkv_r_k, rwkv_lnx_g, rwkv_lnx_b, rwkv_w_o):
    f = lambda a: np.ascontiguousarray(np.asarray(a, dtype=np.float32))
    x, c, ada_w, ada_b, ln_g, ln_b = f(x), f(c), f(ada_w), f(ada_b), f(ln_g), f(ln_b)
    ffn_w_in, ffn_w_out = f(ffn_w_in), f(ffn_w_out)
    w_in = f(fox_w_in)[0]
    b_f, q_g, k_g = f(fox_b_f)[0], f(fox_q_g)[0], f(fox_k_g)[0]
    mu = f(rwkv_mu)[0]
    w_rkv = f(rwkv_w_rkv)[0]
    P = dict(w0=f(rwkv_w0)[0], w1=f(rwkv_w1)[0], w2=f(rwkv_w2)[0], a0=f(rwkv_a0)[0], a1=f(rwkv_a1)[0], a2=f(rwkv_a2)[0],
             g1=f(rwkv_g1)[0], g2=f(rwkv_g2)[0], k_k=f(rwkv_k_k)[0], k_a=f(rwkv_k_a)[0], r_k=f(rwkv_r_k)[0].reshape(-1),
             lnx_g=f(rwkv_lnx_g)[0], lnx_b=f(rwkv_lnx_b)[0])
    mu_l = np.concatenate([_col_layout(mu[n]) for n in range(6)], axis=1)
    wos = [f(fox_w_o)[0], f(rwkv_w_o)[0]]
    tl_common = []
    for L in range(2):
        tl_common.append({
            "adaw_t%d" % L: np.ascontiguousarray(ada_w[L][:, 2 * D:]), "adab_t%d" % L: _col_layout(ada_b[L][2 * D:]),
            "lng_t%d" % L: _col_layout(ln_g[L].reshape(-1)), "lnb_t%d" % L: _col_layout(ln_b[L].reshape(-1)),
            "wo_t%d" % L: wos[L], "win_t%d" % L: ffn_w_in[L], "wout_t%d" % L: ffn_w_out[L]})
    adaw_f = np.ascontiguousarray(ada_w[0][:, 0:2 * D])
    adaw_r = np.ascontiguousarray(ada_w[1][:, 0:2 * D])
    maps = []
    for core in range(NCORES):
        b, j = core // 2, core % 2
        sl = slice(j * 512, (j + 1) * 512)
        cT = _col_layout(c[b])
        m = {
            "x_f": x[b], "cT_f": cT, "adaw_f": adaw_f, "adab_f": _col_layout(ada_b[0][0:2 * D]),
            "wq_f": np.ascontiguousarray(w_in[:, 0:D][:, sl]), "wk_f": np.ascontiguousarray(w_in[:, D:2 * D][:, sl]),
            "wv_f": np.ascontiguousarray(w_in[:, 2 * D:3 * D][:, sl]), "wf_f": np.ascontiguousarray(w_in[:, 3 * D + j * 8:3 * D + j * 8 + 8]),
            "wg_f": np.ascontiguousarray(w_in[:, 3 * D + 16:][:, sl]),
            "bfb_f": _bc(b_f[j * 8:(j + 1) * 8]), "qgb_f": _bc(np.tile(q_g, 8)), "kgb_f": _bc(np.tile(k_g, 8)),
            "cT_r": cT, "adaw_r": adaw_r, "adab_r": _col_layout(ada_b[1][0:2 * D]), "mu_r": mu_l,
            "wr_r": np.ascontiguousarray(w_rkv[0][:, sl]), "wk_r": np.ascontiguousarray(w_rkv[1][:, sl]), "wv_r": np.ascontiguousarray(w_rkv[2][:, sl]),
            "w1_r": P["w1"], "a1_r": P["a1"], "g1_r": P["g1"],
            "w2_r": np.ascontiguousarray(P["w2"][:, sl]), "a2_r": np.ascontiguousarray(P["a2"][:, sl]), "g2_r": np.ascontiguousarray(P["g2"][:, sl]),
            "w0b_r": _bc(P["w0"][sl]), "a0b_r": _bc(P["a0"][sl]), "kkb_r": _bc(P["k_k"][sl]), "kab_r": _bc(P["k_a"][sl]),
            "rkb_r": _bc(P["r_k"][sl]), "lngb_r": _bc(P["lnx_g"][sl]), "lnbb_r": _bc(P["lnx_b"][sl]),
            "cT_t0": cT, "cT_t1": cT,
            "sel": np.ascontiguousarray(np.broadcast_to(np.array([1.0 - j, float(j)], np.float32)[None, :], (128, 2))),
            "xTh": np.ascontiguousarray(x[b, j * 2048:(j + 1) * 2048, :].T),
        }
        m.update(tl_common[0])
        m.update(tl_common[1])
        maps.append(m)
    res = _run(_prog("fused", build_fused), maps)
    out = np.empty(x.shape, np.float32)
    for core in range(NCORES):
        b, j = core // 2, core % 2
        out[b, j * 2048:(j + 1) * 2048, :] = res[core]["oT"].T
    return out
```

```python
import numpy as np
from contextlib import ExitStack
import concourse.bass as bass
import concourse.mybir as mybir
from concourse.bass_utils import run_bass_kernel_spmd

F32 = mybir.dt.float32
BF16 = mybir.dt.bfloat16
AF = mybir.ActivationFunctionType
ALU = mybir.AluOpType
AX = mybir.AxisListType

ENGS = ("pe", "act", "dve", "pool", "sp")
LAST_SEMS = []
SEM_CTR = [0]


def phase_end(nc):
    nc.all_engine_barrier()
    nc.clear_and_free_semaphores(list(LAST_SEMS))
    LAST_SEMS[:] = []
    nc.all_engine_barrier()


class Res:
    __slots__ = ("name", "last_w", "readers", "excl")

    def __init__(self, name, excl=False):
        self.name = name
        self.excl = excl
        self.last_w = None
        self.readers = []


class Op:
    __slots__ = ("eng", "fn", "deps", "sig", "count", "is_dma", "key", "semval", "vc", "waits", "gid")


class Sched:
    def __init__(self, nc):
        self.nc = nc
        self.ops = []
        self.dma_counts = {}

    def res(self, name, excl=False):
        return Res(name, excl)

    def _add(self, op, r, w, extra=()):
        deps = [(d, "raw") for d in extra]
        for x in r:
            if x.last_w is not None:
                deps.append((x.last_w, "raw"))
            if x.excl:
                for rd in x.readers:
                    deps.append((rd, "war"))
        for x in w:
            if x.last_w is not None:
                deps.append((x.last_w, "waw"))
            for rd in x.readers:
                deps.append((rd, "war"))
        dd = []
        for d, kind in deps:
            if d is op:
                continue
            if (not d.is_dma) and (not op.is_dma) and d.eng == op.eng:
                if op.eng == "pe" or kind == "war":
                    continue
            dd.append(d)
        op.deps = dd
        for d in dd:
            d.sig = True
        for x in w:
            x.last_w = op
            x.readers = []
        for x in r:
            if x.last_w is not op:
                if not op.is_dma:
                    x.readers = [q for q in x.readers if q.is_dma or q.eng != op.eng]
                x.readers.append(op)
        op.gid = len(self.ops)
        self.ops.append(op)
        return op

    def op(self, eng, fn, r=(), w=()):
        o = Op()
        o.eng = eng
        o.fn = fn
        o.is_dma = False
        o.sig = False
        o.key = eng
        return self._add(o, r, w)

    def dma(self, eng, out, in_, r=(), w=(), key=None, **kw):
        o = Op()
        o.eng = eng
        o.is_dma = True
        o.sig = True
        assert key is not None
        o.key = "dma:" + key
        o.fn = lambda e: e.dma_start(out=out, in_=in_, **kw)
        return self._add(o, r, w)

    def cc(self, kind, groups, in_ap, out_ap, r=(), w=(), key=None, extra=()):
        o = Op()
        o.eng = "pool"
        o.is_dma = True
        o.sig = True
        o.key = "cc:" + key
        o.fn = lambda e: e.collective_compute(kind, ALU.bypass, replica_groups=groups, ins=[in_ap], outs=[out_ap])
        return self._add(o, r, w, extra)

    def finalize(self, final_wait_eng="sp"):
        nc = self.nc
        counts = {e: 0 for e in ENGS}
        dmac = {}
        for o in self.ops:
            if o.is_dma:
                dmac[o.key] = dmac.get(o.key, 0) + (1 if o.key.startswith("cc:") else 16)
                o.semval = dmac[o.key]
            elif o.sig:
                counts[o.eng] += 1
                o.semval = counts[o.eng]
        seen = {e: {} for e in ENGS}
        for o in self.ops:
            s = seen[o.eng]
            need = {}
            for d in o.deps:
                if s.get(d.key, 0) < d.semval:
                    if d.key not in need or need[d.key].semval < d.semval:
                        need[d.key] = d
            o.waits = [(d.key, d.semval) for d in need.values()]
            for d in need.values():
                for k, v in d.vc.items():
                    if s.get(k, 0) < v:
                        s[k] = v
            if o.sig:
                vc = dict(s)
                vc[o.key] = o.semval
                o.vc = vc
            else:
                o.vc = None
        return counts, dmac

    def emit(self, stack, tail_waits=()):
        nc = self.nc
        counts, dmac = self.finalize()
        sems = {}
        for k in list(ENGS) + list(dmac):
            SEM_CTR[0] += 1
            sems[k] = nc.alloc_semaphore(name="sem%d" % SEM_CTR[0])
        self.nsems = len(sems)
        LAST_SEMS[:] = list(sems.values())
        per = {e: [o for o in self.ops if o.eng == e] for e in ENGS}
        block = stack.enter_context(nc.Block())

        def run(eng_name, eng):
            for o in per[eng_name]:
                for k, v in o.waits:
                    eng.wait_ge(sems[k], v)
                ins = o.fn(eng)
                if o.sig:
                    if o.is_dma:
                        ins.then_inc(sems[o.key], 1 if o.key.startswith("cc:") else 16)
                    else:
                        ins.then_inc(sems[o.key], 1)
            if eng_name == "sp":
                done = {}
                for o in tail_waits:
                    done[o.key] = max(done.get(o.key, 0), o.semval)
                for k, v in done.items():
                    eng.wait_ge(sems[k], v)

        @block.tensor
        def _(e):
            run("pe", e)

        @block.scalar
        def _(e):
            run("act", e)

        @block.vector
        def _(e):
            run("dve", e)

        @block.gpsimd
        def _(e):
            run("pool", e)

        @block.sync
        def _(e):
            run("sp", e)


D = 1024
DFF = 2816
NFC = 22
ALPHA = 4.0 ** 0.25
LN_EPS = 1e-5
EPS_P = LN_EPS / (ALPHA * ALPHA)


class Ctx:
    def __init__(self, nc, st, pfx=""):
        self.nc = nc
        self.st = st
        self.S = Sched(nc)
        self.n = 0
        self.pfx = pfx

    def sb(self, name, shape, dt):
        return self.st.enter_context(self.nc.sbuf_tensor(self.pfx + "sb_" + name, list(shape), dt))

    def ps(self, name, shape, dt=F32):
        return self.st.enter_context(self.nc.psum_tensor(self.pfx + "ps_" + name, list(shape), dt))

    def R(self, name, excl=False):
        return self.S.res(name, excl)


def make_consts(C):
    S = C.S
    C.ones = C.sb("ones", [128, 128], F32)
    C.r_ones = C.R("ones")
    S.op("pool", lambda e: e.memset(C.ones[:], 1.0), w=[C.r_ones])
    C.ident = C.sb("ident", [128, 128], F32)
    C.r_ident = C.R("ident")
    S.op("pool", lambda e: e.memset(C.ident[:], 1.0), w=[C.r_ident])
    S.op("pool", lambda e: e.affine_select(out=C.ident[:], in_=C.ident[:], pattern=[[-1, 128]], compare_op=ALU.is_equal, fill=0.0, base=0, channel_multiplier=1), r=[C.r_ident], w=[C.r_ident])


def compute_mods(C, cT_d, adaw_d, adab_d, nm, wbufs, psm, r_psm):
    S = C.S
    cT = C.sb("cT", [128, 8], F32)
    r_cT = C.R("cT")
    S.dma("sp", cT[:], cT_d, w=[r_cT], key="cT")
    S.op("act", lambda e: e.activation(out=cT[:], in_=cT[:], func=AF.Silu), r=[r_cT], w=[r_cT])
    adab = C.sb("adab", [128, nm * 8], F32)
    r_adab = C.R("adab")
    S.dma("sp", adab[:], adab_d, w=[r_adab], key="adab")
    mods = C.sb("mods", [128, nm * 8], F32)
    r_mods = C.R("mods")
    src = adaw_d.rearrange("(kc p) n -> p kc n", p=128)
    for m in range(nm):
        for q in range(4):
            bi = (m * 4 + q) % 2
            wt, r_wt = wbufs[bi]
            c0 = m * 1024 + q * 256
            S.dma("sp", wt[:], src[:, :, c0:c0 + 256], w=[r_wt], key="adaw%d" % bi)
            for j in range(2):
                cc = q * 2 + j
                for kc in range(8):
                    S.op("pe", lambda e, m=m, cc=cc, kc=kc, j=j, wt=wt: e.matmul(psm[:, m * 8 + cc:m * 8 + cc + 1], wt[:, kc, j * 128:(j + 1) * 128], cT[:, kc:kc + 1], start=(kc == 0), stop=(kc == 7)), r=[r_wt, r_cT], w=[r_psm])
    S.op("dve", lambda e: e.tensor_tensor(out=mods[:], in0=psm[:, 0:nm * 8], in1=adab[:], op=ALU.add), r=[r_psm, r_adab], w=[r_mods])
    return mods, r_mods


def ln_group(C, zT, r_z, g, n0, colA, colB, r_cols, outs, P):
    S = C.S
    ps_s, r_ps_s = P["st0"]
    ps_q, r_ps_q = P["st1"]
    sq, r_sq = P["sq"]
    for cc in range(8):
        S.op("act", lambda e, cc=cc: e.activation(out=sq[cc % 2][:], in_=zT[:, cc, n0:n0 + 512], func=AF.Square), r=[r_z[cc]], w=[r_sq[cc % 2]])
        S.op("pe", lambda e, cc=cc: e.matmul(ps_s[:], C.ones[:], zT[:, cc, n0:n0 + 512], start=(cc == 0), stop=(cc == 7)), r=[C.r_ones, r_z[cc]], w=[r_ps_s])
        S.op("pe", lambda e, cc=cc: e.matmul(ps_q[:], C.ones[:], sq[cc % 2][:], start=(cc == 0), stop=(cc == 7)), r=[C.r_ones, r_sq[cc % 2]], w=[r_ps_q])
    mean, r_mean = P["mean"]
    rstd, r_rstd = P["rstd"]
    S.op("act", lambda e: e.mul(out=mean[:], in_=ps_s[:], mul=1.0 / D), r=[r_ps_s], w=[r_mean])
    S.op("dve", lambda e: e.tensor_tensor(out=rstd[:], in0=mean[:], in1=mean[:], op=ALU.mult), r=[r_mean], w=[r_rstd])
    S.op("dve", lambda e: e.scalar_tensor_tensor(out=rstd[:], in0=ps_q[:], scalar=1.0 / D, in1=rstd[:], op0=ALU.mult, op1=ALU.subtract), r=[r_ps_q, r_rstd], w=[r_rstd])
    S.op("dve", lambda e: e.tensor_scalar(out=rstd[:], in0=rstd[:], scalar1=P["eps"], scalar2=None, op0=ALU.add), r=[r_rstd], w=[r_rstd])
    S.op("act", lambda e: e.activation(out=rstd[:], in_=rstd[:], func=AF.Sqrt), r=[r_rstd], w=[r_rstd])
    S.op("dve", lambda e: e.reciprocal(out=rstd[:], in_=rstd[:]), r=[r_rstd], w=[r_rstd])
    S.op("dve", lambda e: e.scalar_tensor_tensor(out=mean[:], in0=mean[:], scalar=-1.0, in1=rstd[:], op0=ALU.mult, op1=ALU.mult), r=[r_mean, r_rstd], w=[r_mean])
    tt, r_tt = P["tt"]
    for cc in range(8):
        k = cc % 2
        S.op("dve", lambda e, cc=cc, k=k: e.tensor_tensor(out=tt[k][:], in0=zT[:, cc, n0:n0 + 512], in1=rstd[:], op=ALU.mult), r=[r_z[cc], r_rstd], w=[r_tt[k]])
        S.op("dve", lambda e, cc=cc, k=k: e.tensor_tensor(out=tt[k][:], in0=tt[k][:], in1=mean[:], op=ALU.add), r=[r_tt[k], r_mean], w=[r_tt[k]])
        for (dst, r_dst, gt, go, bt, bo, eng) in outs:
            if eng == "act":
                S.op("act", lambda e, cc=cc, k=k, dst=dst, gt=gt, go=go, bt=bt, bo=bo: e.activation(out=dst[:, cc, n0:n0 + 512], in_=tt[k][:], func=AF.Identity, scale=gt[:, go + cc:go + cc + 1], bias=bt[:, bo + cc:bo + cc + 1]), r=[r_tt[k]] + r_cols, w=[r_dst[cc]])
            else:
                S.op(eng, lambda e, cc=cc, k=k, dst=dst, gt=gt, go=go, bt=bt, bo=bo: e.tensor_scalar(out=dst[:, cc, n0:n0 + 512], in0=tt[k][:], scalar1=gt[:, go + cc:go + cc + 1], scalar2=bt[:, bo + cc:bo + cc + 1], op0=ALU.mult, op1=ALU.add), r=[r_tt[k]] + r_cols, w=[r_dst[cc]])


def build_tl(NT=2048, HALF=1024, debug=False):
    nc = bass.Bass("TRN2", target_bir_lowering=False)
    dt = lambda n, s, d, k="ExternalInput": nc.dram_tensor(n, list(s), d, kind=k).ap()
    yT_d = dt("yT", [D, NT], BF16)
    xT_d = dt("xT", [D, NT], F32)
    cT_d = dt("cT", [128, 8], F32)
    adaw_d = dt("adaw", [D, 4 * D], F32)
    adab_d = dt("adab", [128, 32], F32)
    lng_d = dt("lng", [128, 16], F32)
    lnb_d = dt("lnb", [128, 16], F32)
    wo_d = dt("wo", [D, D], F32)
    win_d = dt("win", [D, 2 * DFF], F32)
    wout_d = dt("wout", [DFF, D], F32)
    oT_d = dt("oT", [D, NT], F32, "ExternalOutput")
    tl_phase(nc, "", yT_d, xT_d, cT_d, adaw_d, adab_d, lng_d, lnb_d, wo_d, win_d, wout_d, oT_d, NT, HALF, debug=debug)
    return nc


def tl_phase(nc, pfx, yT_d, xT_d, cT_d, adaw_d, adab_d, lng_d, lnb_d, wo_d, win_d, wout_d, oT_d, NT=2048, HALF=1024, debug=False, sel_d=None, after=None):
    dt = lambda n, s, d, k="ExternalInput": nc.dram_tensor(n, list(s), d, kind=k).ap()
    NG = HALF // 512
    if debug:
        dbg_cols = dt("dbg_cols", [128, 48], F32, "ExternalOutput")
        dbg_mods = dt("dbg_mods", [128, 32], F32, "ExternalOutput")
        dbg_x1 = dt("dbg_x1", [D, HALF], F32, "ExternalOutput")
        dbg_h = dt("dbg_h", [D, HALF], BF16, "ExternalOutput")
        dbg_a = dt("dbg_a", [DFF, HALF], BF16, "ExternalOutput")
    with ExitStack() as st:
        C = Ctx(nc, st, pfx)
        S = C.S
        make_consts(C)
        PB = [(C.ps("pb%d" % i, [128, 512]), C.R("pb%d" % i, True)) for i in range(8)]
        if sel_d is not None:
            selt = C.sb("selt", [128, 2], F32)
            r_selt = C.R("selt")
            S.dma("sp", selt[:], sel_d, w=[r_selt], key="selt")
            ystg = [[(C.sb("ystg%d_%d" % (i, j), [128, HALF], BF16), C.R("ystg%d_%d" % (i, j))) for j in range(2)] for i in range(2)]
        xT = C.sb("xT", [128, 8, HALF], F32)
        r_x = [C.R("xT%d" % cc) for cc in range(8)]
        yT = C.sb("yT", [128, 8, HALF], BF16)
        r_y = [C.R("yT%d" % cc) for cc in range(8)]
        hT = yT
        r_h = r_y
        aT = C.sb("aT", [128, NFC, HALF], BF16)
        r_a = [C.R("aT%d" % f) for f in range(NFC)]
        wo = C.sb("wo", [128, 8, D], BF16)
        r_wo = C.R("wo")
        wA = [(C.sb("wA%d" % i, [128, 8, 256], F32), C.R("wA%d" % i)) for i in range(2)]
        wi = [(C.sb("wi%d" % i, [128, 8, 512], BF16), C.R("wi%d" % i)) for i in range(2)]
        wo2 = [(C.sb("wo2%d" % i, [128, NFC, 256], BF16), C.R("wo2%d" % i)) for i in range(2)]
        P = {
            "st0": PB[6], "st1": PB[7],
            "sq": ([C.sb("sq%d" % i, [128, 512], F32) for i in range(2)], [C.R("sq%d" % i) for i in range(2)]),
            "mean": (C.sb("mean", [128, 512], F32), C.R("mean")),
            "rstd": (C.sb("rstd", [128, 512], F32), C.R("rstd")),
            "tt": ([C.sb("tt%d" % i, [128, 512], F32) for i in range(2)], [C.R("tt%d" % i) for i in range(2)]),
            "eps": EPS_P,
        }
        sl = [C.sb("sl%d" % i, [128, 512], F32) for i in range(2)]
        r_sl = [C.R("sl%d" % i) for i in range(2)]
        mods, r_mods = compute_mods(C, cT_d, adaw_d, adab_d, 4, wA, PB[5][0], PB[5][1])
        lng = C.sb("lng", [128, 16], F32)
        lnb = C.sb("lnb", [128, 16], F32)
        r_ln = C.R("ln")
        S.dma("sp", lng[:], lng_d, w=[r_ln], key="lng")
        S.dma("sp", lnb[:], lnb_d, w=[r_ln], key="lnb")
        cols = C.sb("cols", [128, 48], F32)
        r_cols = C.R("cols")
        S.op("dve", lambda e: e.tensor_scalar(out=cols[:, 0:8], in0=mods[:, 0:8], scalar1=1.0 / ALPHA, scalar2=None, op0=ALU.mult), r=[r_mods], w=[r_cols])
        S.op("dve", lambda e: e.tensor_scalar(out=cols[:, 8:16], in0=mods[:, 24:32], scalar1=1.0 / ALPHA, scalar2=None, op0=ALU.mult), r=[r_mods], w=[r_cols])
        S.op("dve", lambda e: e.tensor_scalar(out=cols[:, 32:40], in0=mods[:, 16:24], scalar1=1.0, scalar2=None, op0=ALU.add), r=[r_mods], w=[r_cols])
        S.op("dve", lambda e: e.tensor_tensor(out=cols[:, 16:24], in0=lng[:, 0:8], in1=cols[:, 32:40], op=ALU.mult), r=[r_ln, r_cols], w=[r_cols])
        S.op("dve", lambda e: e.tensor_tensor(out=cols[:, 24:32], in0=lnb[:, 0:8], in1=cols[:, 32:40], op=ALU.mult), r=[r_ln, r_cols], w=[r_cols])
        S.op("dve", lambda e: e.tensor_tensor(out=cols[:, 24:32], in0=cols[:, 24:32], in1=mods[:, 8:16], op=ALU.add), r=[r_mods, r_cols], w=[r_cols])
        rc = [r_cols, r_ln]
        S.dma("pool", wo[:], wo_d.rearrange("(kc p) n -> p kc n", p=128), w=[r_wo], key="wo")
        xsrc = xT_d.rearrange("(kc p) t -> p kc t", p=128)
        if sel_d is None:
            ysrc_ = yT_d.rearrange("(kc p) t -> p kc t", p=128)
            ysrc_fn = lambda cc, a, n: ysrc_[:, cc, a:a + n]
        else:
            ysrc_fn = lambda cc, a, n: yT_d[cc % 4, cc // 4, :, a:a + n]
        osrc = oT_d.rearrange("(kc p) t -> p kc t", p=128)
        winr = win_d.rearrange("(kc p) n -> p kc n", p=128)
        woutr = wout_d.rearrange("(fc p) n -> p fc n", p=128)
        tails = []
        pbi = 0
        for half in range(NT // HALF):
            t0 = half * HALF
            for cc in range(8):
                if sel_d is None:
                    S.dma("sp", yT[:, cc, :], ysrc_fn(cc, t0, HALF), w=[r_y[cc]], key="yT%d" % cc)
                else:
                    (ya, r_ya), (yb, r_yb) = ystg[cc % 2]
                    S.dma("sp", ya[:], ysrc_fn(cc, t0, HALF), w=[r_ya], key="ysa%d" % (cc % 2))
                    S.dma("sp", yb[:], ysrc_fn(cc, NT + t0, HALF), w=[r_yb], key="ysb%d" % (cc % 2))
                    S.op("dve", lambda e, ya=ya: e.tensor_scalar(out=ya[:], in0=ya[:], scalar1=selt[:, 0:1], scalar2=None, op0=ALU.mult), r=[r_ya, r_selt], w=[r_ya])
                    S.op("dve", lambda e, ya=ya, yb=yb, cc=cc: e.scalar_tensor_tensor(out=yT[:, cc, :], in0=yb[:], scalar=selt[:, 1:2], in1=ya[:], op0=ALU.mult, op1=ALU.add), r=[r_ya, r_yb, r_selt], w=[r_y[cc]])
                S.dma("sp", xT[:, cc, :], xsrc[:, cc, t0:t0 + HALF], w=[r_x[cc]], key="xT%d" % cc)
            for g in range(NG):
                n0 = g * 512
                for cc in range(8):
                    pb, r_pb = PB[pbi % 4]
                    pbi += 1
                    for kc in range(8):
                        S.op("pe", lambda e, pb=pb, cc=cc, kc=kc, n0=n0: e.matmul(pb[:], wo[:, kc, cc * 128:(cc + 1) * 128], yT[:, kc, n0:n0 + 512], start=(kc == 0), stop=(kc == 7)), r=[r_wo] + r_y, w=[r_pb])
                    S.op("dve", lambda e, pb=pb, cc=cc, n0=n0: e.scalar_tensor_tensor(out=xT[:, cc, n0:n0 + 512], in0=pb[:], scalar=cols[:, cc:cc + 1], in1=xT[:, cc, n0:n0 + 512], op0=ALU.mult, op1=ALU.add), r=[r_pb, r_x[cc], r_cols], w=[r_x[cc]])
            for g in range(NG):
                n0 = g * 512
                ln_group(C, xT, r_x, g, n0, None, None, rc,
                         [(xT, r_x, lng, 0, lnb, 0, "act"), (hT, r_h, cols, 16, cols, 24, "act")], P)
            if debug and half == 0:
                tails.append(S.dma("sp", dbg_cols, cols[:], r=[r_cols], key="dbg0"))
                tails.append(S.dma("sp", dbg_mods, mods[:], r=[r_mods], key="dbg1"))
                tails.append(S.dma("sp", dbg_x1.rearrange("(kc p) t -> p kc t", p=128), xT[:], r=r_x, key="dbg2"))
                tails.append(S.dma("sp", dbg_h.rearrange("(kc p) t -> p kc t", p=128), hT[:], r=r_h, key="dbg3"))
            for blk in range(NFC // 2):
                wt, r_wt = wi[blk % 2]
                c0 = blk * 256
                S.dma("pool", wt[:, :, 0:256], winr[:, :, c0:c0 + 256], w=[r_wt], key="wi%d" % (blk % 2))
                S.dma("pool", wt[:, :, 256:512], winr[:, :, DFF + c0:DFF + c0 + 256], w=[r_wt], key="wi%d" % (blk % 2))
                for j in range(2):
                    fc = blk * 2 + j
                    for g in range(NG):
                        n0 = g * 512
                        pg, r_pg = PB[pbi % 4]
                        pu, r_pu = PB[(pbi + 1) % 4]
                        pbi += 2
                        for kc in range(8):
                            S.op("pe", lambda e, pg=pg, wt=wt, j=j, kc=kc, n0=n0: e.matmul(pg[:], wt[:, kc, j * 128:(j + 1) * 128], hT[:, kc, n0:n0 + 512], start=(kc == 0), stop=(kc == 7)), r=[r_wt] + r_h, w=[r_pg])
                        for kc in range(8):
                            S.op("pe", lambda e, pu=pu, wt=wt, j=j, kc=kc, n0=n0: e.matmul(pu[:], wt[:, kc, 256 + j * 128:256 + (j + 1) * 128], hT[:, kc, n0:n0 + 512], start=(kc == 0), stop=(kc == 7)), r=[r_wt] + r_h, w=[r_pu])
                        k = (fc * NG + g) % 2
                        S.op("act", lambda e, pg=pg, k=k: e.activation(out=sl[k][:], in_=pg[:], func=AF.Silu), r=[r_pg], w=[r_sl[k]])
                        S.op("dve", lambda e, pu=pu, k=k, fc=fc, n0=n0: e.tensor_tensor(out=aT[:, fc, n0:n0 + 512], in0=pu[:], in1=sl[k][:], op=ALU.mult), r=[r_pu, r_sl[k]], w=[r_a[fc]])
            if debug and half == 0:
                tails.append(S.dma("sp", dbg_a.rearrange("(kc p) t -> p kc t", p=128), aT[:], r=r_a, key="dbg4"))
            for blk in range(4):
                wt, r_wt = wo2[blk % 2]
                S.dma("pool", wt[:], woutr[:, :, blk * 256:(blk + 1) * 256], w=[r_wt], key="wo2%d" % (blk % 2))
                for j in range(2):
                    cc = blk * 2 + j
                    for g in range(NG):
                        n0 = g * 512
                        pb, r_pb = PB[pbi % 4]
                        pbi += 1
                        for fc in range(NFC):
                            S.op("pe", lambda e, pb=pb, wt=wt, j=j, fc=fc, n0=n0: e.matmul(pb[:], wt[:, fc, j * 128:(j + 1) * 128], aT[:, fc, n0:n0 + 512], start=(fc == 0), stop=(fc == NFC - 1)), r=[r_wt, r_a[fc]], w=[r_pb])
                        S.op("dve", lambda e, pb=pb, cc=cc, n0=n0: e.scalar_tensor_tensor(out=xT[:, cc, n0:n0 + 512], in0=pb[:], scalar=cols[:, 8 + cc:9 + cc], in1=xT[:, cc, n0:n0 + 512], op0=ALU.mult, op1=ALU.add), r=[r_pb, r_x[cc], r_cols], w=[r_x[cc]])
            for g in range(NG):
                n0 = g * 512
                ln_group(C, xT, r_x, g, n0, None, None, rc, [(xT, r_x, lng, 8, lnb, 8, "act")], P)
            for cc in range(8):
                o = S.dma("sp", osrc[:, cc, t0:t0 + HALF], xT[:, cc, :], r=[r_x[cc]], w=[], key="oT%d" % cc)
                tails.append(o)
        if after is not None:
            after(C, tails)
        S.emit(st, tail_waits=tails)


T = 4096
NH = 8
HD = 64
QK_EPS = 1e-6
NEG = -30000.0
import os
LAG = int(os.environ.get("FOX_LAG", "2"))
FOX_DVE = int(os.environ.get("FOX_DVE", "1"))


def build_fox(TT=T, debug=False):
    nc = bass.Bass("TRN2", target_bir_lowering=False)
    dt = lambda n, s, d, k="ExternalInput": nc.dram_tensor(n, list(s), d, kind=k).ap()
    ins = fox_decl(dt, TT)
    yg_d = dt("ygT", [512, TT], BF16, "ExternalOutput")
    fox_phase(nc, "", ins, yg_d, TT, debug=debug)
    return nc


def fox_decl(dt, TT=T, sfx=""):
    return dict(
        x=dt("x" + sfx, [TT, D], F32), cT=dt("cT" + sfx, [128, 8], F32), adaw=dt("adaw" + sfx, [D, 2 * D], F32), adab=dt("adab" + sfx, [128, 16], F32),
        wq=dt("wq" + sfx, [D, 512], F32), wk=dt("wk" + sfx, [D, 512], F32), wv=dt("wv" + sfx, [D, 512], F32), wf=dt("wf" + sfx, [D, 8], F32), wg=dt("wg" + sfx, [D, 512], F32),
        bfb=dt("bfb" + sfx, [128, 8], F32), qgb=dt("qgb" + sfx, [128, 512], F32), kgb=dt("kgb" + sfx, [128, 512], F32))


def fox_phase(nc, pfx, ins, yg_d, TT=T, debug=False, after=None):
    dt = lambda n, s, d, k="ExternalInput": nc.dram_tensor(n, list(s), d, kind=k).ap()
    x_d, cT_d, adaw_d, adab_d = ins["x"], ins["cT"], ins["adaw"], ins["adab"]
    wq_d, wk_d, wv_d, wf_d, wg_d = ins["wq"], ins["wk"], ins["wv"], ins["wf"], ins["wg"]
    bf_d, qg_d, kg_d = ins["bfb"], ins["qgb"], ins["kgb"]
    NGR = TT // 512
    NTL = TT // 128
    if debug:
        dbg_h = dt("dbg_h", [D, 512], BF16, "ExternalOutput")
        dbg_q = dt("dbg_q", [128, 4, 512], BF16, "ExternalOutput")
        dbg_k = dt("dbg_k", [128, 4, TT], BF16, "ExternalOutput")
        dbg_F = dt("dbg_F", [128, NTL, 8], F32, "ExternalOutput")
        dbg_v = dt("dbg_v", [128, NTL, 8, 65], BF16, "ExternalOutput")
    with ExitStack() as st:
        C = Ctx(nc, st, pfx)
        S = C.S
        make_consts(C)
        identb = C.sb("identb", [128, 128], BF16)
        r_identb = C.R("identb")
        S.op("dve", lambda e: e.tensor_copy(out=identb[:], in_=C.ident[:]), r=[C.r_ident], w=[r_identb])
        tri = C.sb("tri", [128, 128], F32)
        r_tri = C.R("tri")
        S.op("pool", lambda e: e.memset(tri[:], 1.0), w=[r_tri])
        S.op("pool", lambda e: e.affine_select(out=tri[:], in_=tri[:], pattern=[[1, 128]], compare_op=ALU.is_ge, fill=0.0, base=0, channel_multiplier=-1), r=[r_tri], w=[r_tri])
        selA = C.sb("selA", [16, 8, 128], F32)
        selB = C.sb("selB", [16, 8, 128], F32)
        sel = C.sb("sel", [16, 8, 128], BF16)
        r_sel = C.R("sel")
        S.op("pool", lambda e: e.memset(selA[:], 1.0), w=[r_sel])
        S.op("pool", lambda e: e.memset(selB[:], 1.0), w=[r_sel])
        S.op("pool", lambda e: e.affine_select(out=selA[:], in_=selA[:], pattern=[[-1, 8], [0, 128]], compare_op=ALU.is_equal, fill=0.0, base=0, channel_multiplier=1), r=[r_sel], w=[r_sel])
        S.op("pool", lambda e: e.affine_select(out=selB[:], in_=selB[:], pattern=[[-1, 8], [0, 128]], compare_op=ALU.is_equal, fill=0.0, base=-8, channel_multiplier=1), r=[r_sel], w=[r_sel])
        S.op("dve", lambda e: e.tensor_tensor(out=sel[:], in0=selA[:], in1=selB[:], op=ALU.add), r=[r_sel], w=[r_sel])
        maskf = C.sb("maskf", [128, 4, 512], F32)
        maskb = C.sb("maskb", [128, 4, 512], BF16)
        r_mask = C.R("mask")
        S.op("pool", lambda e: e.memset(maskf[:], 0.0), w=[r_mask])
        for j in range(4):
            S.op("pool", lambda e, j=j: e.affine_select(out=maskf[:, j, :], in_=maskf[:, j, :], pattern=[[1, 512]], compare_op=ALU.is_ge, fill=NEG, base=-128 * j, channel_multiplier=-1), r=[r_mask], w=[r_mask])
        S.op("dve", lambda e: e.tensor_copy(out=maskb[:], in_=maskf[:]), r=[r_mask], w=[r_mask])
        PB = [(C.ps("pb%d" % i, [128, 512]), C.R("pb%d" % i, True)) for i in range(7)]
        pbb = C.ps("pbb", [128, 1024], BF16)
        r_pbb = C.R("pbb", True)
        wA = [(C.sb("wA%d" % i, [128, 8, 256], F32), C.R("wA%d" % i)) for i in range(2)]
        mods, r_mods = compute_mods(C, cT_d, adaw_d, adab_d, 2, wA, PB[6][0], PB[6][1])
        cols = C.sb("cols", [128, 8], F32)
        r_cols = C.R("cols")
        S.op("dve", lambda e: e.tensor_scalar(out=cols[:], in0=mods[:, 8:16], scalar1=1.0, scalar2=None, op0=ALU.add), r=[r_mods], w=[r_cols])
        wts = {}
        r_w = C.R("wts")
        for nm, d_ in (("wq", wq_d), ("wk", wk_d), ("wv", wv_d), ("wg", wg_d)):
            wts[nm] = C.sb(nm, [128, 8, 512], BF16)
            S.dma("pool", wts[nm][:], d_.rearrange("(kc p) n -> p kc n", p=128), w=[r_w], key=nm)
        wf = C.sb("wf", [128, 8, 8], BF16)
        S.dma("pool", wf[:], wf_d.rearrange("(kc p) n -> p kc n", p=128), w=[r_w], key="wf")
        bfb = C.sb("bfb", [128, 8], F32)
        qgb = C.sb("qgb", [128, 512], F32)
        kgb = C.sb("kgb", [128, 512], F32)
        r_par = C.R("par")
        S.dma("sp", bfb[:], bf_d, w=[r_par], key="bfb")
        S.dma("sp", qgb[:], qg_d, w=[r_par], key="qgb")
        S.dma("sp", kgb[:], kg_d, w=[r_par], key="kgb")
        xt = C.sb("xt", [128, 4, D], F32)
        r_xt = C.R("xt")
        hT = C.sb("hT", [128, 8, 512], BF16)
        r_hT = [C.R("hT%d" % k) for k in range(8)]
        kT = C.sb("kT", [128, 4, TT], BF16)
        r_kT = [C.R("kT%d" % g) for g in range(NGR)]
        qT = C.sb("qT", [128, 4, 512], BF16)
        r_qT = C.R("qT")
        Va = C.sb("Va", [128, NTL, NH, 65], BF16)
        r_V = [C.R("V%d" % g) for g in range(NGR)]
        r_Vones = C.R("Vones")
        S.op("pool", lambda e: e.memset(Va[:, :, :, 64:65], 1.0), w=[r_Vones])
        eg = C.sb("eg", [64, NH, 512], F32)
        r_eg = [C.R("eg%d" % h) for h in range(NH)]
        Fcol = C.sb("Fcol", [128, NTL, NH], F32)
        r_F = C.R("Fcol")
        carry = C.sb("carry", [128, NTL + 1, NH], F32)
        r_carry = C.R("carry")
        S.op("dve", lambda e: e.memset(carry[:, 0, :], 0.0), w=[r_carry])
        biasG = C.sb("biasG", [128, NTL, NH], F32)
        r_bias = C.R("biasG")
        FqT = C.sb("FqT", [16, 512], BF16)
        r_FqT = C.R("FqT")
        sq = C.sb("sq", [128, 512], F32); r_sq = C.R("sq")
        tmp = C.sb("tmp", [128, 512], F32); r_tmp = C.R("tmp")
        qtok = C.sb("qtok", [128, 512], BF16); r_qtok = C.R("qtok")
        ktok = C.sb("ktok", [128, 512], BF16); r_ktok = C.R("ktok")
        ssq = C.sb("ssq", [128, 16], F32); r_ssq = C.R("ssq")
        zf = C.sb("zf", [128, 8], F32); r_zf = C.R("zf")
        fq = C.sb("fq", [128, 8], F32); r_fq = C.R("fq")
        fq16 = C.sb("fq16", [128, 16], BF16); r_fq16 = C.R("fq16")
        PT = [(C.sb("PT%d" % i, [128, 512], BF16), C.R("PT%d" % i)) for i in range(3)]
        oT = C.sb("oT", [64, 512], F32); r_oT = C.R("oT")
        dn = C.sb("dn", [128, 512], F32); r_dn = C.R("dn")
        wv_ = C.sb("wv_", [64, 512], F32); r_wv_ = C.R("wv_")
        yg = [(C.sb("yg%d" % i, [64, 512], BF16), C.R("yg%d" % i)) for i in range(2)]
        tails = []
        xsrc = x_d.rearrange("(i p) d -> p i d", p=128)
        ps_i = 0
        for G in range(NGR):
            S.dma("sp", xt[:], xsrc[:, 4 * G:4 * G + 4, :], w=[r_xt], key="xt")
            for kc in range(8):
                pb, r_pb = PB[kc % 2]
                for i in range(4):
                    S.op("pe", lambda e, pb=pb, i=i, kc=kc: e.transpose(pb[:, i * 128:(i + 1) * 128], xt[:, i, kc * 128:(kc + 1) * 128], C.ident[:]), r=[r_xt, C.r_ident], w=[r_pb])
                if kc % 2 == 0:
                    S.op("act", lambda e, pb=pb, kc=kc: e.activation(out=hT[:, kc, :], in_=pb[:], func=AF.Identity, scale=cols[:, kc:kc + 1], bias=mods[:, kc:kc + 1]), r=[r_pb, r_cols, r_mods], w=[r_hT[kc]])
                else:
                    S.op("dve", lambda e, pb=pb, kc=kc: e.tensor_scalar(out=hT[:, kc, :], in0=pb[:], scalar1=cols[:, kc:kc + 1], scalar2=mods[:, kc:kc + 1], op0=ALU.mult, op1=ALU.add), r=[r_pb, r_cols, r_mods], w=[r_hT[kc]])
            if debug and G == 0:
                tails.append(S.dma("sp", dbg_h.rearrange("(kc p) t -> p kc t", p=128), hT[:], r=r_hT, key="dbg0"))
            for h in range(NH):
                pb, r_pb = PB[2 + h % 2]
                for kc in range(8):
                    S.op("pe", lambda e, pb=pb, h=h, kc=kc: e.matmul(pb[0:64, :], wts["wg"][:, kc, h * 64:(h + 1) * 64], hT[:, kc, :], start=(kc == 0), stop=(kc == 7)), r=[r_w, r_hT[kc]], w=[r_pb])
                S.op("act", lambda e, pb=pb, h=h: e.activation(out=eg[:, h, :], in_=pb[0:64, :], func=AF.Exp, scale=-1.0), r=[r_pb], w=[r_eg[h]])
            for i in range(4):
                tl_ = 4 * G + i
                (pq, r_pq), (pk, r_pk), (pv, r_pv), (pf, r_pf) = PB[2], PB[3], PB[4], PB[5]
                for kc in range(8):
                    lw = hT[:, kc, i * 128:(i + 1) * 128]
                    S.op("pe", lambda e, lw=lw, kc=kc: e.matmul(pq[:], lw, wts["wq"][:, kc, :], start=(kc == 0), stop=(kc == 7)), r=[r_w, r_hT[kc]], w=[r_pq])
                    S.op("pe", lambda e, lw=lw, kc=kc: e.matmul(pk[:], lw, wts["wk"][:, kc, :], start=(kc == 0), stop=(kc == 7)), r=[r_w, r_hT[kc]], w=[r_pk])
                    S.op("pe", lambda e, lw=lw, kc=kc: e.matmul(pv[:], lw, wts["wv"][:, kc, :], start=(kc == 0), stop=(kc == 7)), r=[r_w, r_hT[kc]], w=[r_pv])
                    S.op("pe", lambda e, lw=lw, kc=kc: e.matmul(pf[:, 0:8], lw, wf[:, kc, :], start=(kc == 0), stop=(kc == 7)), r=[r_w, r_hT[kc]], w=[r_pf])
                for which, (pp, r_pp), gb, tok, r_tok, eps_, mul_ in (("q", (pq, r_pq), qgb, qtok, r_qtok, 64.0 * QK_EPS, 1.0), ("k", (pk, r_pk), kgb, ktok, r_ktok, QK_EPS, 1.0 / 64.0)):
                    o8 = 0 if which == "q" else 8
                    S.op("act", lambda e, pp=pp: e.activation(out=sq[:], in_=pp[:], func=AF.Square), r=[r_pp], w=[r_sq])
                    S.op("dve", lambda e, o8=o8: e.tensor_reduce(out=ssq[:, o8:o8 + 8], in_=sq[:].rearrange("p (h d) -> p h d", d=64), axis=AX.X, op=ALU.add), r=[r_sq], w=[r_ssq])
                    S.op("dve", lambda e, o8=o8, eps_=eps_, mul_=mul_: e.tensor_scalar(out=ssq[:, o8:o8 + 8], in0=ssq[:, o8:o8 + 8], scalar1=mul_, scalar2=eps_, op0=ALU.mult, op1=ALU.add), r=[r_ssq], w=[r_ssq])
                    S.op("act", lambda e, o8=o8: e.activation(out=ssq[:, o8:o8 + 8], in_=ssq[:, o8:o8 + 8], func=AF.Sqrt), r=[r_ssq], w=[r_ssq])
                    S.op("dve", lambda e, o8=o8: e.reciprocal(out=ssq[:, o8:o8 + 8], in_=ssq[:, o8:o8 + 8]), r=[r_ssq], w=[r_ssq])
                    S.op("dve", lambda e, pp=pp, o8=o8: e.tensor_tensor(out=tmp[:].rearrange("p (h d) -> p h d", d=64), in0=pp[:].rearrange("p (h d) -> p h d", d=64), in1=ssq[:, o8:o8 + 8].unsqueeze(2).to_broadcast([128, 8, 64]), op=ALU.mult), r=[r_pp, r_ssq], w=[r_tmp])
                    S.op("dve", lambda e, gb=gb, tok=tok: e.tensor_tensor(out=tok[:], in0=tmp[:], in1=gb[:], op=ALU.mult), r=[r_tmp, r_par], w=[r_tok])
                S.op("act", lambda e, tl_=tl_: e.copy(out=Va[:, tl_, :, 0:64], in_=pv[:].rearrange("p (h d) -> p h d", d=64)), r=[r_pv, r_Vones], w=[r_V[G]])
                S.op("dve", lambda e: e.tensor_tensor(out=zf[:], in0=pf[:, 0:8], in1=bfb[:], op=ALU.add), r=[r_pf, r_par], w=[r_zf])
                S.op("act", lambda e: e.activation(out=zf[:], in_=zf[:], func=AF.Exp, scale=-1.0), r=[r_zf], w=[r_zf])
                S.op("act", lambda e: e.activation(out=zf[:], in_=zf[:], func=AF.Ln, bias=1.0), r=[r_zf], w=[r_zf])
                pc, r_pc = PB[6]
                S.op("pe", lambda e: e.matmul(pc[:, 0:8], tri[:], zf[:], start=True, stop=True), r=[r_tri, r_zf], w=[r_pc])
                S.op("pe", lambda e: e.matmul(pc[:, 8:16], C.ones[:], zf[:], start=True, stop=True), r=[C.r_ones, r_zf], w=[r_pc])
                S.op("dve", lambda e, tl_=tl_: e.tensor_tensor(out=Fcol[:, tl_, :], in0=carry[:, tl_, :], in1=pc[:, 0:8], op=ALU.subtract), r=[r_carry, r_pc], w=[r_F])
                S.op("dve", lambda e, tl_=tl_: e.tensor_tensor(out=carry[:, tl_ + 1, :], in0=carry[:, tl_, :], in1=pc[:, 8:16], op=ALU.subtract), r=[r_carry, r_pc], w=[r_carry])
                S.op("dve", lambda e, tl_=tl_, G=G: e.tensor_tensor(out=fq[:], in0=Fcol[:, tl_, :], in1=carry[:, 4 * G, :], op=ALU.subtract), r=[r_F, r_carry], w=[r_fq])
                S.op("dve", lambda e: e.tensor_copy(out=fq16[:, 0:8], in_=fq[:]), r=[r_fq], w=[r_fq16])
                S.op("dve", lambda e: e.tensor_tensor(out=fq[:], in0=fq[:], in1=fq16[:, 0:8], op=ALU.subtract), r=[r_fq, r_fq16], w=[r_fq])
                S.op("dve", lambda e: e.tensor_copy(out=fq16[:, 8:16], in_=fq[:]), r=[r_fq], w=[r_fq16])
                for p in range(4):
                    S.op("pe", lambda e, p=p: e.transpose(pbb[:, p * 128:(p + 1) * 128], qtok[:, p * 128:(p + 1) * 128], identb[:]), r=[r_qtok, r_identb], w=[r_pbb])
                for p in range(4):
                    S.op("pe", lambda e, p=p: e.transpose(pbb[:, 512 + p * 128:512 + (p + 1) * 128], ktok[:, p * 128:(p + 1) * 128], identb[:]), r=[r_ktok, r_identb], w=[r_pbb])
                S.op("act", lambda e, i=i: e.copy(out=qT[:, :, i * 128:(i + 1) * 128], in_=pbb[:, 0:512].rearrange("p (a t) -> p a t", t=128)), r=[r_pbb], w=[r_qT])
                S.op("dve", lambda e, tl_=tl_: e.tensor_copy(out=kT[:, :, tl_ * 128:(tl_ + 1) * 128], in_=pbb[:, 512:1024].rearrange("p (a t) -> p a t", t=128)), r=[r_pbb], w=[r_kT[G]])
                S.op("pe", lambda e: e.transpose(pbb[0:16, 0:128], fq16[:, 0:16], identb[:]), r=[r_fq16, r_identb], w=[r_pbb])
                S.op("dve", lambda e, i=i: e.tensor_copy(out=FqT[:, i * 128:(i + 1) * 128], in_=pbb[0:16, 0:128]), r=[r_pbb], w=[r_FqT])
            nkb = 4 * G + 4
            for h in range(NH):
                S.op("dve", lambda e, h=h, nkb=nkb, G=G: e.tensor_scalar(out=biasG[:, 0:nkb, h], in0=Fcol[:, 0:nkb, h], scalar1=-1.0, scalar2=carry[:, 4 * G, h:h + 1], op0=ALU.mult, op1=ALU.add), r=[r_F, r_carry], w=[r_bias])
            tiles = [(h, kb) for h in range(NH) for kb in range(nkb)]

            def emit_pv(h, kb, PTt, r_PT, nkb=nkb, G=G):
                pO, r_pO = PB[3 + h % 2]
                gk = kb // 4
                S.op("pe", lambda e, pO=pO, PTt=PTt, kb=kb, h=h, nkb=nkb: e.matmul(pO[0:65, :], Va[:, kb, h, :], PTt[:], start=(kb == 0), stop=(kb == nkb - 1)), r=[r_V[gk], r_Vones, r_PT], w=[r_pO])
                if kb == nkb - 1:
                    pB_, r_pB = PB[5]
                    ygt, r_yg = yg[h % 2]
                    S.op("act", lambda e, pO=pO: e.copy(out=oT[:], in_=pO[0:64, :]), r=[r_pO], w=[r_oT])
                    S.op("dve", lambda e, pO=pO: e.tensor_copy(out=dn[64:65, :], in_=pO[64:65, :]), r=[r_pO], w=[r_dn])
                    S.op("pe", lambda e: e.matmul(pB_[0:64, :], C.ones[64:65, 0:64], dn[64:65, :], start=True, stop=True), r=[C.r_ones, r_dn], w=[r_pB])
                    S.op("dve", lambda e, h=h: e.scalar_tensor_tensor(out=wv_[:], in0=eg[:, h, :], scalar=1.0, in1=pB_[0:64, :], op0=ALU.add, op1=ALU.mult), r=[r_eg[h], r_pB], w=[r_wv_])
                    S.op("dve", lambda e: e.reciprocal(out=wv_[:], in_=wv_[:]), r=[r_wv_], w=[r_wv_])
                    S.op("dve", lambda e, ygt=ygt: e.tensor_tensor(out=ygt[:], in0=oT[:], in1=wv_[:], op=ALU.mult), r=[r_oT, r_wv_], w=[r_yg])
                    tails.append(S.dma("sp", yg_d[h * 64:(h + 1) * 64, G * 512:(G + 1) * 512], ygt[:], r=[r_yg], key="yg%d" % (h % 2)))

            pendq = []
            for (h, kb) in tiles:
                p_, e_ = h // 2, h % 2
                pS, r_pS = PB[ps_i % 3]
                PTt, r_PT = PT[ps_i % 3]
                ps_i += 1
                diag = kb >= 4 * G
                gk = kb // 4
                if FOX_DVE:
                    fqb, r_fqb = ((tmp, r_tmp), (sq, r_sq))[h % 2]
                    if kb == 0:
                        pB_, r_pB = PB[5]
                        S.op("pe", lambda e, h=h: e.matmul(pB_[:], sel[:, h, :], FqT[:], start=True, stop=True), r=[r_sel, r_FqT], w=[r_pB])
                        S.op("dve", lambda e, fqb=fqb: e.tensor_copy(out=fqb[:], in_=pB_[:]), r=[r_pB], w=[r_fqb])
                    S.op("pe", lambda e, pS=pS, p_=p_, e_=e_, kb=kb, diag=diag: e.matmul(pS[:], kT[e_ * 64:(e_ + 1) * 64, p_, kb * 128:(kb + 1) * 128], qT[e_ * 64:(e_ + 1) * 64, p_, :], start=True, stop=(not diag)), r=[r_kT[gk], r_qT], w=[r_pS])
                    if diag:
                        S.op("pe", lambda e, pS=pS, kb=kb, G=G: e.matmul(pS[:], identb[:], maskb[:, kb - 4 * G, :], start=False, stop=True), r=[r_identb, r_mask], w=[r_pS])
                    S.op("dve", lambda e, pS=pS, fqb=fqb: e.tensor_tensor(out=pS[:], in0=pS[:], in1=fqb[:], op=ALU.add), r=[r_pS, r_fqb], w=[r_pS])
                else:
                    S.op("pe", lambda e, pS=pS, p_=p_, e_=e_, kb=kb: e.matmul(pS[:], kT[e_ * 64:(e_ + 1) * 64, p_, kb * 128:(kb + 1) * 128], qT[e_ * 64:(e_ + 1) * 64, p_, :], start=True, stop=False), r=[r_kT[gk], r_qT], w=[r_pS])
                    S.op("pe", lambda e, pS=pS, h=h, diag=diag: e.matmul(pS[:], sel[:, h, :], FqT[:], start=False, stop=(not diag)), r=[r_sel, r_FqT], w=[r_pS])
                    if diag:
                        S.op("pe", lambda e, pS=pS, kb=kb, G=G: e.matmul(pS[:], identb[:], maskb[:, kb - 4 * G, :], start=False, stop=True), r=[r_identb, r_mask], w=[r_pS])
                S.op("act", lambda e, pS=pS, PTt=PTt, kb=kb, h=h: e.activation(out=PTt[:], in_=pS[:], func=AF.Exp, bias=biasG[:, kb, h:h + 1], scale=1.0), r=[r_pS, r_bias], w=[r_PT])
                pendq.append((h, kb, PTt, r_PT))
                if len(pendq) > LAG:
                    emit_pv(*pendq.pop(0))
            while pendq:
                emit_pv(*pendq.pop(0))
            if debug and G == 0:
                tails.append(S.dma("sp", dbg_q, qT[:], r=[r_qT], key="dbg1"))
        if debug:
            tails.append(S.dma("sp", dbg_k, kT[:], r=r_kT, key="dbg2"))
            tails.append(S.dma("sp", dbg_F, Fcol[:], r=[r_F], key="dbg3"))
            tails.append(S.dma("sp", dbg_v, Va[:], r=r_V + [r_Vones], key="dbg4"))
        if after is not None:
            after(C, tails)
        S.emit(st, tail_waits=tails)

import math, os
SEQ_VAR = 0

NH = 8
HD = 64
C0 = math.exp(-0.5)
GN_EPS = 64 * 1e-5


class TR:
    def __init__(self, C, name, shape, dt, psum=False):
        self.t = C.ps(name, shape, dt) if psum else C.sb(name, shape, dt)
        self.r = C.R(name, excl=psum)


class _Stop(Exception):
    pass


def build_rwkv(TT=4096, debug=False, stop=None):
    nc = bass.Bass("TRN2", target_bir_lowering=False)
    dt = lambda n, s, d, k="ExternalInput": nc.dram_tensor(n, list(s), d, kind=k).ap()
    xT_d = dt("xT", [D, TT], F32)
    ins = rwkv_decl(dt)
    yT_d = dt("yT", [512, TT], BF16, "ExternalOutput")
    xv = xT_d.rearrange("(kc p) t -> p kc t", p=128)
    rwkv_phase(nc, "", lambda t0, n: xv[:, :, t0:t0 + n], ins, yT_d, TT, debug=debug, stop=stop)
    return nc


RW_BC = ["w0b", "a0b", "kkb", "kab", "rkb", "lngb", "lnbb"]


def rwkv_decl(dt, sfx=""):
    d_ = dict(cT=dt("cT" + sfx, [128, 8], F32), adaw=dt("adaw" + sfx, [D, 2 * D], F32), adab=dt("adab" + sfx, [128, 16], F32), mu=dt("mu" + sfx, [128, 48], F32),
              wr=dt("wr" + sfx, [D, 512], F32), wk=dt("wk" + sfx, [D, 512], F32), wv=dt("wv" + sfx, [D, 512], F32),
              w1=dt("w1" + sfx, [D, 64], F32), a1=dt("a1" + sfx, [D, 64], F32), g1=dt("g1" + sfx, [D, 128], F32),
              w2=dt("w2" + sfx, [64, 512], F32), a2=dt("a2" + sfx, [64, 512], F32), g2=dt("g2" + sfx, [128, 512], F32))
    for n in RW_BC:
        d_[n] = dt(n + sfx, [128, 512], F32)
    return d_


def rwkv_phase(nc, pfx, xsrc_fn, ins, yT_d, TT=4096, debug=False, stop=None, after=None):
    dt = lambda n, s, d, k="ExternalInput": nc.dram_tensor(n, list(s), d, kind=k).ap()
    cT_d, adaw_d, adab_d, mu_d = ins["cT"], ins["adaw"], ins["adab"], ins["mu"]
    wr_d, wk_d, wv_d, w1_d, a1_d, g1_d, w2_d, a2_d, g2_d = [ins[k] for k in ("wr", "wk", "wv", "w1", "a1", "g1", "w2", "a2", "g2")]
    bc_names = RW_BC
    bc_d = {n: ins[n] for n in bc_names}
    GS = 256
    NTG = GS // 128
    NGR = TT // GS
    dbg = {}
    if debug:
        for n in ["r", "kp", "kkn", "a", "sw", "Y", "bonus", "Gm", "yn"]:
            dbg[n] = dt("dbg_" + n, [128, 512], F32, "ExternalOutput")
        dbg["S"] = dt("dbg_S", [128, 4, 64], F32, "ExternalOutput")
        for n, shp in (("Arb", [128, 8, 128]), ("Aak", [128, 8, 128]), ("NT0", [128, 8, 128]), ("Z0", [128, 8, 128]), ("Zf", [128, 8, 128]), ("Y0", [128, 512]), ("RbT", [128, 4, 128]), ("Qm", [128, 4, 128]), ("Hm", [128, 4, 128]), ("gcol", [128, 8]), ("v", [128, 512]), ("ART", [128, 4, 2, 128]), ("KT", [128, 4, 128]), ("BT", [128, 4, 128]), ("Bh", [128, 512]), ("Kh", [128, 512])):
            dbg[n] = dt("dbg_" + n, shp, F32, "ExternalOutput")
    with ExitStack() as st:
        C = Ctx(nc, st, pfx)
        S = C.S
        make_consts(C)
        ident, ones = C.ident, C.ones
        r_ident, r_ones = C.r_ident, C.r_ones
        tails = []
        identb = TR(C, "identb", [128, 128], BF16)
        S.op("dve", lambda e: e.tensor_copy(out=identb.t[:], in_=ident[:]), r=[r_ident], w=[identb.r])
        U = TR(C, "U", [128, 128], F32)
        IU = TR(C, "IUf", [128, 128], F32)
        S.op("pool", lambda e: e.memset(U.t[:], 1.0), w=[U.r])
        S.op("pool", lambda e: e.affine_select(out=U.t[:], in_=U.t[:], pattern=[[1, 128]], compare_op=ALU.is_gt, fill=0.0, base=0, channel_multiplier=-1), r=[U.r], w=[U.r])
        S.op("pool", lambda e: e.memset(IU.t[:], 1.0), w=[IU.r])
        S.op("pool", lambda e: e.affine_select(out=IU.t[:], in_=IU.t[:], pattern=[[1, 128]], compare_op=ALU.is_ge, fill=0.0, base=0, channel_multiplier=-1), r=[IU.r], w=[IU.r])
        SLf = TR(C, "SLf", [128, 128], F32)
        S.op("pool", lambda e: e.memset(SLf.t[:], 1.0), w=[SLf.r])
        S.op("pool", lambda e: e.affine_select(out=SLf.t[:], in_=SLf.t[:], pattern=[[-1, 128]], compare_op=ALU.is_gt, fill=0.0, base=0, channel_multiplier=1), r=[SLf.r], w=[SLf.r])
        mask1 = TR(C, "mask1", [128, 128], F32)
        maskL = TR(C, "maskL", [128, 64], F32)
        idl = TR(C, "idl", [128, 64], F32)
        for ch in range(2):
            ps_ = slice(ch * 64, ch * 64 + 64)
            S.op("dve", lambda e, ps_=ps_: e.tensor_copy(out=mask1.t[ps_, 0:64], in_=U.t[ps_, ps_]), r=[U.r], w=[mask1.r])
            S.op("dve", lambda e, ps_=ps_: e.tensor_copy(out=mask1.t[ps_, 64:128], in_=IU.t[ps_, ps_]), r=[IU.r], w=[mask1.r])
            S.op("dve", lambda e, ps_=ps_: e.tensor_copy(out=maskL.t[ps_, :], in_=SLf.t[ps_, ps_]), r=[SLf.r], w=[maskL.r])
            S.op("dve", lambda e, ps_=ps_: e.tensor_copy(out=idl.t[ps_, :], in_=ident[ps_, ps_]), r=[r_ident], w=[idl.r])
        triC = TR(C, "triC", [128, 128], F32)
        blkC = TR(C, "blkC", [128, 128], F32)
        S.op("dve", lambda e: e.tensor_scalar(out=triC.t[:], in0=IU.t[:], scalar1=-C0, scalar2=None, op0=ALU.mult), r=[IU.r], w=[triC.r])
        S.op("dve", lambda e: e.memset(triC.t[0:64, 64:128], 0.0), w=[triC.r])
        S.op("dve", lambda e: e.memset(blkC.t[:], -C0), w=[blkC.r])
        S.op("dve", lambda e: e.memset(blkC.t[0:64, 64:128], 0.0), w=[blkC.r])
        S.op("dve", lambda e: e.memset(blkC.t[64:128, 0:64], 0.0), w=[blkC.r])
        avg = TR(C, "avg", [128, 1], F32)
        S.op("dve", lambda e: e.memset(avg.t[:], 1.0 / 64.0), w=[avg.r])
        PB = [TR(C, "pb%d" % i, [128, 512], F32, psum=True) for i in range(7)]
        pbb = TR(C, "pbb", [128, 1024], BF16, psum=True)
        wA = [(C.sb("wA%d" % i, [128, 8, 256], F32), C.R("wA%d" % i)) for i in range(2)]
        mods, r_mods = compute_mods(C, cT_d, adaw_d, adab_d, 2, wA, PB[6].t, PB[6].r)
        cols = TR(C, "cols", [128, 8], F32)
        S.op("dve", lambda e: e.tensor_scalar(out=cols.t[:], in0=mods[:, 8:16], scalar1=1.0, scalar2=None, op0=ALU.add), r=[r_mods], w=[cols.r])
        mu = TR(C, "mu", [128, 48], F32)
        S.dma("sp", mu.t[:], mu_d, w=[mu.r], key="mu")
        W = {}
        r_w = C.R("wts")
        for nm, d_, shp in (("wr", wr_d, [128, 8, 512]), ("wk", wk_d, [128, 8, 512]), ("wv", wv_d, [128, 8, 512]), ("w1", w1_d, [128, 8, 64]), ("a1", a1_d, [128, 8, 64]), ("g1", g1_d, [128, 8, 128])):
            W[nm] = C.sb(nm, shp, BF16)
            S.dma("pool", W[nm][:], d_.rearrange("(kc p) n -> p kc n", p=128), w=[r_w], key=nm)
        for nm, d_, shp in (("w2", w2_d, [64, 512]), ("a2", a2_d, [64, 512]), ("g2", g2_d, [128, 512])):
            W[nm] = C.sb(nm, shp, BF16)
            S.dma("pool", W[nm][:], d_, w=[r_w], key=nm)
        BC = {}
        r_bc = C.R("bc")
        for n in bc_names:
            BC[n] = C.sb(n, [128, 512], F32)
            S.dma("sp", BC[n][:], bc_d[n], w=[r_bc], key=n)
        xg = TR(C, "xg", [128, 8, GS + 1], F32)
        S.op("dve", lambda e: e.memset(xg.t[:, :, 0:1], 0.0), w=[xg.r])
        dx = TR(C, "dx", [128, 8, GS], F32)
        xs = [TR(C, "xs%d" % i, [128, 8, GS], BF16) for i in range(2)]
        l1 = [TR(C, "l1_%d" % i, [128, GS], BF16) for i in range(3)]
        l1f = TR(C, "l1f", [128, GS], F32)
        G_ = {n: TR(C, "G_" + n, [128, NTG, 512], F32) for n in ("r", "k", "v")}
        names = ["sw", "Gm", "Gi", "Gp", "GC", "a", "kkn", "kp", "t1", "t2", "Bt", "Kt", "Bh", "Kh", "bonus", "Y", "yn", "gg", "At"]
        X = {n: TR(C, "X_" + n, [128, 512], F32) for n in names}
        sm = TR(C, "sm", [128, 64], F32)
        Z = [TR(C, "Z%d" % i, [128, 8, 128], BF16) for i in range(2)]
        Zf32 = TR(C, "Zf32", [128, 8, 128], F32)
        Rt = TR(C, "Rt", [128, 512], F32)
        ART = TR(C, "ART", [128, 4, 2, 128], F32)
        KT = TR(C, "KT", [128, 4, 128], F32)
        BT = TR(C, "BT", [128, 4, 128], F32)
        Nm = [TR(C, "Nm%d" % i, [128, 8, 128], BF16) for i in range(2)]
        NTm = [TR(C, "NTm%d" % i, [128, 8, 128], BF16) for i in range(2)]
        for t_ in Nm + NTm:
            S.op("pool", lambda e, t_=t_: e.memset(t_.t[:], 0.0), w=[t_.r])
        Arb = TR(C, "Arb", [128, 8, 128], F32)
        Aak = TR(C, "Aak", [128, 8, 128], F32)
        RbT = TR(C, "RbT", [128, 4, 128], F32)
        Qm = TR(C, "Qm", [128, 4, 128], F32)
        Hm = TR(C, "Hm", [128, 4, 128], F32)
        gcol = TR(C, "gcol", [128, 8], F32)
        Y0 = TR(C, "Y0", [128, 512], F32)
        ST = [TR(C, "ST%d" % i, [128, 4, 64], F32) for i in range(2)]
        S.op("dve", lambda e: e.memset(ST[0].t[:], 0.0), w=[ST[0].r])
        ybf = TR(C, "ybf", [128, 512], BF16)
        yTo = [TR(C, "yTo%d" % i, [128, 4, 128], BF16) for i in range(2)]
        ysrc = yT_d.rearrange("(a p) t -> p a t", p=128)

        def hv(t):
            return t.rearrange("p (h d) -> p h d", d=64)

        def bc8(ap):
            return ap.unsqueeze(2).to_broadcast([128, 8, 64])

        st_i = 0
        try:
          for G in range(NGR):
              if G > 0:
                  S.op("dve", lambda e: e.tensor_copy(out=xg.t[:, :, 0:1], in_=xg.t[:, :, GS:GS + 1]), r=[xg.r], w=[xg.r])
              S.dma("sp", xg.t[:, :, 1:GS + 1], xsrc_fn(G * GS, GS), r=[xg.r], w=[xg.r], key="xg")
              for kc in range(8):
                  S.op("dve", lambda e, kc=kc: e.tensor_scalar(out=xg.t[:, kc, 1:GS + 1], in0=xg.t[:, kc, 1:GS + 1], scalar1=cols.t[:, kc:kc + 1], scalar2=mods[:, kc:kc + 1], op0=ALU.mult, op1=ALU.add), r=[xg.r, cols.r, r_mods], w=[xg.r])
              S.op("dve", lambda e: e.tensor_tensor(out=dx.t[:], in0=xg.t[:, :, 0:GS], in1=xg.t[:, :, 1:GS + 1], op=ALU.subtract), r=[xg.r], w=[dx.r])
              for n, nm in enumerate(("r", "k", "v", "w", "a", "g")):
                  xb = xs[n % 2]
                  for kc in range(8):
                      eng = "dve"
                      S.op(eng, lambda e, kc=kc, n=n, xb=xb: e.scalar_tensor_tensor(out=xb.t[:, kc, :], in0=dx.t[:, kc, :], scalar=mu.t[:, n * 8 + kc:n * 8 + kc + 1], in1=xg.t[:, kc, 1:GS + 1], op0=ALU.mult, op1=ALU.add), r=[dx.r, mu.r, xg.r], w=[xb.r])
                  if n < 3:
                      wt = W[("wr", "wk", "wv")[n]]
                      for i in range(NTG):
                          pb = PB[(n * NTG + i) % 2]
                          for kc in range(8):
                              S.op("pe", lambda e, pb=pb, xb=xb, wt=wt, i=i, kc=kc: e.matmul(pb.t[:], xb.t[:, kc, i * 128:(i + 1) * 128], wt[:, kc, :], start=(kc == 0), stop=(kc == 7)), r=[xb.r, r_w], w=[pb.r])
                          S.op("act", lambda e, pb=pb, nm=nm, i=i: e.copy(out=G_[nm].t[:, i, :], in_=pb.t[:]), r=[pb.r], w=[G_[nm].r])
                  else:
                      w1n, w2n, rows = (("w1", "w2", 64), ("a1", "a2", 64), ("g1", "g2", 128))[n - 3]
                      pb = PB[2]
                      for kc in range(8):
                          S.op("pe", lambda e, pb=pb, xb=xb, w1n=w1n, rows=rows, kc=kc: e.matmul(pb.t[0:rows, 0:GS], W[w1n][:, kc, :], xb.t[:, kc, :], start=(kc == 0), stop=(kc == 7)), r=[xb.r, r_w], w=[pb.r])
                      lt = l1[n - 3]
                      if nm == "w":
                          S.op("act", lambda e, pb=pb: e.activation(out=l1f.t[0:64, :], in_=pb.t[0:64, 0:GS], func=AF.Exp, scale=-2.0), r=[pb.r, l1f.r], w=[l1f.r])
                          S.op("dve", lambda e: e.tensor_scalar(out=l1f.t[0:64, :], in0=l1f.t[0:64, :], scalar1=1.0, scalar2=None, op0=ALU.add), r=[l1f.r], w=[l1f.r])
                          S.op("dve", lambda e: e.reciprocal(out=l1f.t[0:64, :], in_=l1f.t[0:64, :]), r=[l1f.r], w=[l1f.r])
                          S.op("dve", lambda e, lt=lt: e.tensor_scalar(out=lt.t[0:64, :], in0=l1f.t[0:64, :], scalar1=2.0, scalar2=-1.0, op0=ALU.mult, op1=ALU.add), r=[l1f.r], w=[lt.r])
                      elif nm == "a":
                          S.op("act", lambda e, pb=pb, lt=lt: e.copy(out=lt.t[0:64, :], in_=pb.t[0:64, 0:GS]), r=[pb.r], w=[lt.r])
                      else:
                          S.op("act", lambda e, pb=pb: e.activation(out=l1f.t[:], in_=pb.t[:, 0:GS], func=AF.Exp, scale=-1.0), r=[pb.r, l1f.r], w=[l1f.r])
                          S.op("dve", lambda e: e.tensor_scalar(out=l1f.t[:], in0=l1f.t[:], scalar1=1.0, scalar2=None, op0=ALU.add), r=[l1f.r], w=[l1f.r])
                          S.op("dve", lambda e: e.reciprocal(out=l1f.t[:], in_=l1f.t[:]), r=[l1f.r], w=[l1f.r])
                          S.op("act", lambda e, lt=lt: e.copy(out=lt.t[:], in_=l1f.t[:]), r=[l1f.r], w=[lt.r])
              if stop == 1:
                  raise _Stop()
              for i in range(NTG):
                  tl_ = G * NTG + i
                  r_ = G_["r"].t[:, i, :]; k_ = G_["k"].t[:, i, :]; v_ = G_["v"].t[:, i, :]
                  rG = [G_[n].r for n in G_]
                  sw, Gm, Gi, Gp, GC, a_, kkn, kp, t1, t2 = [X[n] for n in ("sw", "Gm", "Gi", "Gp", "GC", "a", "kkn", "kp", "t1", "t2")]
                  Bt, Kt, Bh, Kh, bonus, Yt, yn = [X[n] for n in ("Bt", "Kt", "Bh", "Kh", "bonus", "Y", "yn")]
                  pzw, pza, pgg = PB[4], PB[5], PB[6]
                  for (pz_, lt, w2n, rows) in ((pzw, l1[0], "w2", 64), (pza, l1[1], "a2", 64), (pgg, l1[2], "g2", 128)):
                      S.op("pe", lambda e, pz_=pz_, lt=lt, w2n=w2n, rows=rows, i=i: e.matmul(pz_.t[:], lt.t[0:rows, i * 128:(i + 1) * 128], W[w2n][0:rows, :], start=True, stop=True), r=[lt.r, r_w], w=[pz_.r])
                  S.op("act", lambda e: e.copy(out=X["gg"].t[:], in_=pgg.t[:]), r=[pgg.r], w=[X["gg"].r])
                  S.op("dve", lambda e: e.tensor_tensor(out=t1.t[:], in0=pzw.t[:], in1=BC["w0b"][:], op=ALU.add), r=[pzw.r, r_bc], w=[t1.r])
                  S.op("dve", lambda e: e.tensor_tensor(out=t2.t[:], in0=pza.t[:], in1=BC["a0b"][:], op=ALU.add), r=[pza.r, r_bc, t2.r], w=[t2.r])
                  S.op("act", lambda e: e.activation(out=sw.t[:], in_=t1.t[:], func=AF.Sigmoid), r=[t1.r], w=[sw.r])
                  S.op("act", lambda e: e.activation(out=a_.t[:], in_=t2.t[:], func=AF.Sigmoid), r=[t2.r], w=[a_.r])
                  pL, pLC = PB[2], PB[3]
                  S.op("pe", lambda e: e.matmul(pL.t[:], triC.t[:], sw.t[:], start=True, stop=True), r=[triC.r, sw.r], w=[pL.r])
                  S.op("pe", lambda e: e.matmul(pLC.t[:], blkC.t[:], sw.t[:], start=True, stop=True), r=[blkC.r, sw.r], w=[pLC.r])
                  S.op("act", lambda e: e.activation(out=Gm.t[:], in_=pL.t[:], func=AF.Exp), r=[pL.r], w=[Gm.r])
                  S.op("act", lambda e: e.activation(out=Gi.t[:], in_=pL.t[:], func=AF.Exp, scale=-1.0), r=[pL.r], w=[Gi.r])
                  S.op("dve", lambda e: e.scalar_tensor_tensor(out=Gp.t[:], in0=sw.t[:], scalar=C0, in1=pL.t[:], op0=ALU.mult, op1=ALU.add), r=[sw.r, pL.r], w=[Gp.r])
                  S.op("act", lambda e: e.activation(out=Gp.t[:], in_=Gp.t[:], func=AF.Exp), r=[Gp.r], w=[Gp.r])
                  S.op("act", lambda e: e.activation(out=GC.t[:], in_=pLC.t[:], func=AF.Exp), r=[pLC.r], w=[GC.r])
                  S.op("dve", lambda e, k_=k_: e.tensor_tensor(out=kkn.t[:], in0=k_, in1=BC["kkb"][:], op=ALU.mult), r=rG + [r_bc], w=[kkn.r])
                  S.op("act", lambda e: e.activation(out=t2.t[:], in_=kkn.t[:], func=AF.Square), r=[kkn.r], w=[t2.r])
                  S.op("dve", lambda e: e.tensor_reduce(out=sm.t[:, 0:8], in_=hv(t2.t[:]), axis=AX.X, op=ALU.add), r=[t2.r], w=[sm.r])
                  S.op("dve", lambda e: e.tensor_scalar(out=sm.t[:, 0:8], in0=sm.t[:, 0:8], scalar1=1e-24, scalar2=None, op0=ALU.max), r=[sm.r], w=[sm.r])
                  S.op("act", lambda e: e.activation(out=sm.t[:, 0:8], in_=sm.t[:, 0:8], func=AF.Sqrt), r=[sm.r], w=[sm.r])
                  S.op("dve", lambda e: e.reciprocal(out=sm.t[:, 0:8], in_=sm.t[:, 0:8]), r=[sm.r], w=[sm.r])
                  S.op("dve", lambda e: e.tensor_tensor(out=hv(kkn.t[:]), in0=hv(kkn.t[:]), in1=bc8(sm.t[:, 0:8]), op=ALU.mult), r=[kkn.r, sm.r], w=[kkn.r])
                  S.op("dve", lambda e: e.scalar_tensor_tensor(out=t2.t[:], in0=a_.t[:], scalar=-1.0, in1=BC["kab"][:], op0=ALU.add, op1=ALU.mult), r=[a_.r, r_bc, t2.r], w=[t2.r])
                  S.op("dve", lambda e, k_=k_: e.scalar_tensor_tensor(out=kp.t[:], in0=t2.t[:], scalar=1.0, in1=k_, op0=ALU.add, op1=ALU.mult), r=[t2.r] + rG, w=[kp.r])
                  S.op("dve", lambda e: e.scalar_tensor_tensor(out=X["At"].t[:], in0=kkn.t[:], scalar=-1.0, in1=Gp.t[:], op0=ALU.mult, op1=ALU.mult), r=[kkn.r, Gp.r], w=[X["At"].r])
                  S.op("pool", lambda e: e.tensor_copy(out=Z[0].t[:, :, 0:64], in_=hv(X["At"].t[:])), r=[X["At"].r], w=[Z[0].r])
                  S.op("pool", lambda e: e.tensor_tensor(out=Bt.t[:], in0=kkn.t[:], in1=a_.t[:], op=ALU.mult), r=[kkn.r, a_.r], w=[Bt.r])
                  S.op("pool", lambda e: e.tensor_tensor(out=Bt.t[:], in0=Bt.t[:], in1=Gi.t[:], op=ALU.mult), r=[Bt.r, Gi.r], w=[Bt.r])
                  S.op("dve", lambda e: e.tensor_tensor(out=Kt.t[:], in0=kp.t[:], in1=Gi.t[:], op=ALU.mult), r=[kp.r, Gi.r], w=[Kt.r])
                  S.op("dve", lambda e, r_=r_: e.tensor_tensor(out=Rt.t[:], in0=r_, in1=Gm.t[:], op=ALU.mult), r=rG + [Gm.r], w=[Rt.r])
                  S.op("pool", lambda e: e.tensor_tensor(out=Bh.t[:], in0=Bt.t[:], in1=GC.t[:], op=ALU.mult), r=[Bt.r, GC.r], w=[Bh.r])
                  S.op("pool", lambda e: e.tensor_tensor(out=Kh.t[:], in0=Kt.t[:], in1=GC.t[:], op=ALU.mult), r=[Kt.r, GC.r], w=[Kh.r])
                  S.op("dve", lambda e, r_=r_: e.tensor_tensor(out=t2.t[:], in0=r_, in1=kp.t[:], op=ALU.mult), r=rG + [kp.r, t2.r], w=[t2.r])
                  S.op("dve", lambda e: e.tensor_tensor(out=t2.t[:], in0=t2.t[:], in1=BC["rkb"][:], op=ALU.mult), r=[t2.r, r_bc], w=[t2.r])
                  S.op("dve", lambda e: e.tensor_reduce(out=sm.t[:, 8:16], in_=hv(t2.t[:]), axis=AX.X, op=ALU.add), r=[t2.r], w=[sm.r])
                  S.op("dve", lambda e, v_=v_: e.tensor_tensor(out=hv(bonus.t[:]), in0=hv(v_), in1=bc8(sm.t[:, 8:16]), op=ALU.mult), r=rG + [sm.r], w=[bonus.r])
                  if stop == 2:
                      raise _Stop()
                  for p in range(4):
                      pT = PB[p % 2]
                      for q_, src in enumerate((X["At"], Rt, Kt, Bt)):
                          S.op("pe", lambda e, pT=pT, p=p, q_=q_, src=src: e.transpose(pT.t[:, q_ * 128:(q_ + 1) * 128], src.t[:, p * 128:(p + 1) * 128], ident[:]), r=[src.r, r_ident], w=[pT.r])
                      S.op("dve", lambda e, pT=pT, p=p: e.tensor_copy(out=ART.t[:, p, :, 0:64], in_=pT.t[:, 0:128].rearrange("p (c t) -> p c t", t=64)), r=[pT.r], w=[ART.r])
                      S.op("act", lambda e, pT=pT, p=p: e.copy(out=ART.t[:, p, :, 64:128], in_=pT.t[:, 128:256].rearrange("p (c t) -> p c t", t=64)), r=[pT.r], w=[ART.r])
                      S.op("act", lambda e, pT=pT, p=p: e.copy(out=KT.t[:, p, :], in_=pT.t[:, 256:384]), r=[pT.r], w=[KT.r])
                      S.op("dve", lambda e, pT=pT, p=p: e.tensor_copy(out=BT.t[:, p, :], in_=pT.t[:, 384:512]), r=[pT.r], w=[BT.r])
                  if stop == 3:
                      raise _Stop()
                  pS1a, pS1b, pS2a, pS2b, pS3 = PB[0], PB[1], PB[2], PB[3], PB[4]
                  for h in range(8):
                      p, e_ = h // 2, h % 2
                      fe = slice(e_ * 64, e_ * 64 + 64)
                      pa, pb_ = (pS1a, pS2a) if h < 4 else (pS1b, pS2b)
                      hc = (h % 4) * 128
                      for ch in range(2):
                          tc_ = slice(ch * 64, ch * 64 + 64)
                          S.op("pe", lambda e, pa=pa, fe=fe, p=p, ch=ch, tc_=tc_, hc=hc: e.matmul(pa.t[tc_, hc:hc + 128], BT.t[fe, p, tc_], ART.t[fe, p, ch, :], start=True, stop=True), r=[BT.r, ART.r], w=[pa.r])
                          S.op("pe", lambda e, pb_=pb_, fe=fe, p=p, ch=ch, tc_=tc_, hc=hc: e.matmul(pb_.t[tc_, hc:hc + 128], KT.t[fe, p, tc_], ART.t[fe, p, ch, :], start=True, stop=True), r=[KT.r, ART.r], w=[pb_.r])
                          S.op("pe", lambda e, fe=fe, p=p, ch=ch, tc_=tc_, h=h: e.matmul(pS3.t[tc_, h * 64:(h + 1) * 64], ART.t[fe, p, ch, 0:64], BT.t[fe, p, tc_], start=True, stop=True), r=[BT.r, ART.r], w=[pS3.r])
                  m1b = mask1.t[:].unsqueeze(1).to_broadcast([128, 4, 128])
                  for half, (pa, pb_) in enumerate(((pS1a, pS2a), (pS1b, pS2b))):
                      hs = slice(half * 4, half * 4 + 4)
                      S.op("dve", lambda e, pa=pa, hs=hs: e.tensor_tensor(out=Arb.t[:, hs, :], in0=pa.t[:].rearrange("p (h t) -> p h t", t=128), in1=m1b, op=ALU.mult), r=[pa.r, mask1.r], w=[Arb.r])
                      S.op("dve", lambda e, pb_=pb_, hs=hs: e.tensor_tensor(out=Aak.t[:, hs, :], in0=pb_.t[:].rearrange("p (h t) -> p h t", t=128), in1=m1b, op=ALU.mult), r=[pb_.r, mask1.r], w=[Aak.r])
                  for ch in range(2):
                      tc_ = slice(ch * 64, ch * 64 + 64)
                      S.op("dve", lambda e, tc_=tc_: e.tensor_tensor(out=NTm[0].t[tc_, :, tc_], in0=hv(pS3.t[tc_, :]), in1=maskL.t[tc_, :].unsqueeze(1).to_broadcast([64, 8, 64]), op=ALU.mult), r=[pS3.r, maskL.r], w=[NTm[0].r])
                      S.op("act", lambda e, tc_=tc_: e.copy(out=Nm[0].t[tc_, :, tc_], in_=Arb.t[tc_, :, 0:64]), r=[Arb.r], w=[Nm[0].r])
                  if stop == 4:
                      raise _Stop()
                  if debug and tl_ == 0:
                      tails.append(S.dma("sp", dbg["Arb"], Arb.t[:], r=[Arb.r], key="dbg_Arb"))
                  if debug and tl_ == 0:
                      tails.append(S.dma("sp", dbg["Aak"], Aak.t[:], r=[Aak.r], key="dbg_Aak"))
                  if debug and tl_ == 0:
                      tails.append(S.dma("sp", dbg["NT0"], NTm[0].t[:], r=[NTm[0].r], key="dbg_NT0"))
                  if debug and tl_ == 0:
                      tails.append(S.dma("sp", dbg["ART"], ART.t[:], r=[ART.r], key="dbg_ART"))
                  if debug and tl_ == 0:
                      tails.append(S.dma("sp", dbg["KT"], KT.t[:], r=[KT.r], key="dbg_KT"))
                  if debug and tl_ == 0:
                      tails.append(S.dma("sp", dbg["BT"], BT.t[:], r=[BT.r], key="dbg_BT"))
                  if debug and tl_ == 0:
                      tails.append(S.dma("sp", dbg["Bh"], Bh.t[:], r=[Bh.r], key="dbg_Bh"))
                  if debug and tl_ == 0:
                      tails.append(S.dma("sp", dbg["Kh"], Kh.t[:], r=[Kh.r], key="dbg_Kh"))
                  pX = PB[5]
                  for h in range(8):
                      for ch in range(2):
                          tc_ = slice(ch * 64, ch * 64 + 64)
                          S.op("pe", lambda e, h=h, tc_=tc_, v_=v_: e.matmul(pX.t[tc_, h * 64:(h + 1) * 64], Aak.t[tc_, h, 0:64], v_[tc_, h * 64:(h + 1) * 64], start=True, stop=True), r=[Aak.r] + rG, w=[pX.r])
                  S.op("act", lambda e: e.copy(out=Z[0].t[:, :, 64:128], in_=hv(pX.t[:])), r=[pX.r], w=[Z[0].r])
                  if stop == 5:
                      raise _Stop()
                  if debug and tl_ == 0:
                      tails.append(S.dma("sp", dbg["Z0"], Z[0].t[:], r=[Z[0].r], key="dbg_Z0"))
                  zi = 0
                  for lev in range(6):
                      Nc, NTc = Nm[lev % 2], NTm[lev % 2]
                      Nn, NTn = Nm[(lev + 1) % 2], NTm[(lev + 1) % 2]
                      pZa, pZb = PB[0], PB[1]
                      Zc, Zn = Z[zi], Z[1 - zi]
                      if lev < 5:
                          pN, pNT = (PB[2], PB[3]), (PB[4], PB[5])
                          for h in range(8):
                              hb, hc = h // 4, (h % 4) * 128
                              S.op("pe", lambda e, NTc=NTc, Nc=Nc, h=h, hb=hb, hc=hc: e.matmul(pN[hb].t[:, hc:hc + 128], NTc.t[:, h, :], Nc.t[:, h, :], start=True, stop=True), r=[Nc.r, NTc.r], w=[pN[hb].r])
                              if lev < 4:
                                  S.op("pe", lambda e, NTc=NTc, Nc=Nc, h=h, hb=hb, hc=hc: e.matmul(pNT[hb].t[:, hc:hc + 128], Nc.t[:, h, :], NTc.t[:, h, :], start=True, stop=True), r=[Nc.r, NTc.r], w=[pNT[hb].r])
                      for h in range(8):
                          pz = pZa if h < 4 else pZb
                          hc = (h % 4) * 128
                          S.op("pe", lambda e, pz=pz, Nc=Nc, Zc=Zc, h=h, hc=hc: e.matmul(pz.t[:, hc:hc + 128], Nc.t[:, h, :], Zc.t[:, h, :], start=True, stop=True), r=[Nc.r, Zc.r], w=[pz.r])
                      if lev == 5:
                          Zn = Zf32
                      S.op("dve", lambda e, Zc=Zc, Zn=Zn: e.tensor_tensor(out=Zn.t[:, 0:4, :], in0=Zc.t[:, 0:4, :], in1=pZa.t[:].rearrange("p (h t) -> p h t", t=128), op=ALU.add), r=[Zc.r, pZa.r], w=[Zn.r])
                      S.op("dve", lambda e, Zc=Zc, Zn=Zn: e.tensor_tensor(out=Zn.t[:, 4:8, :], in0=Zc.t[:, 4:8, :], in1=pZb.t[:].rearrange("p (h t) -> p h t", t=128), op=ALU.add), r=[Zc.r, pZb.r], w=[Zn.r])
                      zi = 1 - zi
                      if lev < 5:
                          for hb in range(2):
                              S.op("act", lambda e, Nn=Nn, hb=hb: e.copy(out=Nn.t[:, hb * 4:hb * 4 + 4, :], in_=pN[hb].t[:].rearrange("p (h t) -> p h t", t=128)), r=[pN[hb].r], w=[Nn.r])
                              if lev < 4:
                                  S.op("act" if hb == 0 else "dve", (lambda e, NTn=NTn, hb=hb: e.copy(out=NTn.t[:, hb * 4:hb * 4 + 4, :], in_=pNT[hb].t[:].rearrange("p (h t) -> p h t", t=128))) if hb == 0 else (lambda e, NTn=NTn, hb=hb: e.tensor_copy(out=NTn.t[:, hb * 4:hb * 4 + 4, :], in_=pNT[hb].t[:].rearrange("p (h t) -> p h t", t=128))), r=[pNT[hb].r], w=[NTn.r])
                  Zf = Zf32
                  pY0 = PB[4]
                  for h in range(8):
                      for ch in range(2):
                          tc_ = slice(ch * 64, ch * 64 + 64)
                          S.op("pe", lambda e, h=h, tc_=tc_: e.matmul(pY0.t[tc_, h * 64:(h + 1) * 64], Arb.t[tc_, h, 64:128], Zf.t[tc_, h, 64:128], start=True, stop=False), r=[Arb.r, Zf.r], w=[pY0.r])
                          S.op("pe", lambda e, h=h, tc_=tc_, v_=v_: e.matmul(pY0.t[tc_, h * 64:(h + 1) * 64], Aak.t[tc_, h, 64:128], v_[tc_, h * 64:(h + 1) * 64], start=False, stop=True), r=[Aak.r] + rG, w=[pY0.r])
                  S.op("act", lambda e: e.copy(out=Y0.t[:], in_=pY0.t[:]), r=[pY0.r], w=[Y0.r])
                  if stop == 7:
                      raise _Stop()
                  if debug and tl_ == 0:
                      tails.append(S.dma("sp", dbg["Zf"], Zf.t[:], r=[Zf.r], key="dbg_Zf"))
                  if debug and tl_ == 0:
                      tails.append(S.dma("sp", dbg["Y0"], Y0.t[:], r=[Y0.r], key="dbg_Y0"))
                  pR, pQ, pH, pg = PB[5], PB[6], PB[0], PB[1]
                  for h in range(8):
                      p, e_ = h // 2, h % 2
                      fe = slice(e_ * 64, e_ * 64 + 64)
                      for ch in range(2):
                          tc_ = slice(ch * 64, ch * 64 + 64)
                          cs = slice(p * 128 + ch * 64, p * 128 + ch * 64 + 64)
                          hs_ = slice(h * 64, h * 64 + 64)
                          S.op("pe", lambda e, fe=fe, cs=cs, tc_=tc_, h=h: e.matmul(pR.t[fe, cs], Zf.t[tc_, h, 0:64], Arb.t[tc_, h, 64:128], start=True, stop=True), r=[Zf.r, Arb.r], w=[pR.r])
                          S.op("pe", lambda e, fe=fe, cs=cs, tc_=tc_, h=h, hs_=hs_: e.matmul(pQ.t[fe, cs], Zf.t[tc_, h, 0:64], Bh.t[tc_, hs_], start=True, stop=True), r=[Zf.r, Bh.r], w=[pQ.r])
                          S.op("pe", lambda e, fe=fe, cs=cs, tc_=tc_, h=h, hs_=hs_: e.matmul(pH.t[fe, cs], Bh.t[tc_, hs_], Zf.t[tc_, h, 64:128], start=True, stop=False), r=[Zf.r, Bh.r], w=[pH.r])
                          S.op("pe", lambda e, fe=fe, cs=cs, tc_=tc_, hs_=hs_, v_=v_: e.matmul(pH.t[fe, cs], Kh.t[tc_, hs_], v_[tc_, hs_], start=False, stop=True), r=[Kh.r] + rG, w=[pH.r])
                          S.op("pe", lambda e, fe=fe, tc_=tc_, hs_=hs_, p=p, ch=ch: e.matmul(pg.t[fe, p * 2 + ch:p * 2 + ch + 1], GC.t[tc_, hs_], avg.t[tc_, :], start=True, stop=True), r=[GC.r, avg.r], w=[pg.r])
                  for ch in range(2):
                      S.op("dve", lambda e, ch=ch: e.tensor_tensor(out=RbT.t[:, :, ch * 64:(ch + 1) * 64], in0=pR.t[:].rearrange("p (a t) -> p a t", t=128)[:, :, ch * 64:(ch + 1) * 64], in1=ART.t[:, :, ch, 64:128], op=ALU.add), r=[pR.r, ART.r], w=[RbT.r])
                  S.op("act", lambda e: e.copy(out=Qm.t[:], in_=pQ.t[:].rearrange("p (a t) -> p a t", t=128)), r=[pQ.r], w=[Qm.r])
                  S.op("act", lambda e: e.copy(out=Hm.t[:], in_=pH.t[:].rearrange("p (a t) -> p a t", t=128)), r=[pH.r], w=[Hm.r])
                  S.op("act", lambda e: e.copy(out=gcol.t[:], in_=pg.t[:, 0:8]), r=[pg.r], w=[gcol.r])
                  if stop == 8:
                      raise _Stop()
                  if debug and tl_ == 0:
                      tails.append(S.dma("sp", dbg["RbT"], RbT.t[:], r=[RbT.r], key="dbg_RbT"))
                  if debug and tl_ == 0:
                      tails.append(S.dma("sp", dbg["Qm"], Qm.t[:], r=[Qm.r], key="dbg_Qm"))
                  if debug and tl_ == 0:
                      tails.append(S.dma("sp", dbg["Hm"], Hm.t[:], r=[Hm.r], key="dbg_Hm"))
                  if debug and tl_ == 0:
                      tails.append(S.dma("sp", dbg["gcol"], gcol.t[:], r=[gcol.r], key="dbg_gcol"))
                  for ch in range(2):
                      tc_ = slice(ch * 64, ch * 64 + 64)
                      Sc, Sn = ST[st_i], ST[1 - st_i]
                      pYe, pS_ = (PB[2], PB[4]), PB[3]
                      for h in range(8):
                          p, e_ = h // 2, h % 2
                          fe = slice(e_ * 64, e_ * 64 + 64)
                          cs = slice(ch * 64, ch * 64 + 64)
                          pY = pYe[e_]
                          S.op("pe", lambda e, fe=fe, p=p, cs=cs, tc_=tc_, h=h, Sc=Sc, pY=pY: e.matmul(pY.t[tc_, h * 64:(h + 1) * 64], RbT.t[fe, p, cs], Sc.t[fe, p, :], start=True, stop=True), r=[RbT.r, Sc.r], w=[pY.r])
                          S.op("pe", lambda e, fe=fe, p=p, cs=cs, Sc=Sc: e.matmul(pS_.t[fe, p * 64:(p + 1) * 64], Qm.t[fe, p, cs], Sc.t[fe, p, :], start=True, stop=True), r=[Qm.r, Sc.r], w=[pS_.r])
                      v4 = lambda ap: ap.rearrange("p (a e d) -> p a e d", e=2, d=64)
                      for e_ in range(2):
                          S.op("dve", lambda e, tc_=tc_, e_=e_: e.tensor_tensor(out=v4(Yt.t[tc_, :])[:, :, e_, :], in0=v4(Y0.t[tc_, :])[:, :, e_, :], in1=v4(pYe[e_].t[tc_, :])[:, :, e_, :], op=ALU.add), r=[Y0.r, pYe[e_].r], w=[Yt.r])
                      for p in range(4):
                          S.op("dve", lambda e, p=p, ch=ch, Sc=Sc, Sn=Sn: e.scalar_tensor_tensor(out=Sn.t[:, p, :], in0=Sc.t[:, p, :], scalar=gcol.t[:, p * 2 + ch:p * 2 + ch + 1], in1=pS_.t[:, p * 64:(p + 1) * 64], op0=ALU.mult, op1=ALU.add), r=[Sc.r, gcol.r, pS_.r], w=[Sn.r])
                      S.op("dve", lambda e, ch=ch, Sn=Sn: e.tensor_tensor(out=Sn.t[:], in0=Sn.t[:], in1=Hm.t[:, :, ch * 64:(ch + 1) * 64], op=ALU.add), r=[Sn.r, Hm.r], w=[Sn.r])
                      st_i = 1 - st_i
                  if stop == 9:
                      raise _Stop()
                  S.op("dve", lambda e: e.tensor_reduce(out=sm.t[:, 16:24], in_=hv(Yt.t[:]), axis=AX.X, op=ALU.add), r=[Yt.r], w=[sm.r])
                  S.op("act", lambda e: e.activation(out=t2.t[:], in_=Yt.t[:], func=AF.Square), r=[Yt.r, t2.r], w=[t2.r])
                  S.op("dve", lambda e: e.tensor_reduce(out=sm.t[:, 24:32], in_=hv(t2.t[:]), axis=AX.X, op=ALU.add), r=[t2.r], w=[sm.r])
                  S.op("dve", lambda e: e.tensor_scalar(out=sm.t[:, 16:24], in0=sm.t[:, 16:24], scalar1=1.0 / 64, scalar2=None, op0=ALU.mult), r=[sm.r], w=[sm.r])
                  S.op("dve", lambda e: e.tensor_tensor(out=sm.t[:, 32:40], in0=sm.t[:, 16:24], in1=sm.t[:, 16:24], op=ALU.mult), r=[sm.r], w=[sm.r])
                  S.op("dve", lambda e: e.scalar_tensor_tensor(out=sm.t[:, 24:32], in0=sm.t[:, 24:32], scalar=1.0 / 64, in1=sm.t[:, 32:40], op0=ALU.mult, op1=ALU.subtract), r=[sm.r], w=[sm.r])
                  S.op("dve", lambda e: e.tensor_scalar(out=sm.t[:, 24:32], in0=sm.t[:, 24:32], scalar1=GN_EPS, scalar2=None, op0=ALU.add), r=[sm.r], w=[sm.r])
                  S.op("act", lambda e: e.activation(out=sm.t[:, 24:32], in_=sm.t[:, 24:32], func=AF.Sqrt), r=[sm.r], w=[sm.r])
                  S.op("dve", lambda e: e.reciprocal(out=sm.t[:, 24:32], in_=sm.t[:, 24:32]), r=[sm.r], w=[sm.r])
                  S.op("dve", lambda e: e.tensor_tensor(out=hv(yn.t[:]), in0=hv(Yt.t[:]), in1=bc8(sm.t[:, 16:24]), op=ALU.subtract), r=[Yt.r, sm.r], w=[yn.r])
                  S.op("dve", lambda e: e.tensor_tensor(out=hv(yn.t[:]), in0=hv(yn.t[:]), in1=bc8(sm.t[:, 24:32]), op=ALU.mult), r=[yn.r, sm.r], w=[yn.r])
                  S.op("pool", lambda e: e.tensor_tensor(out=yn.t[:], in0=yn.t[:], in1=BC["lngb"][:], op=ALU.mult), r=[yn.r, r_bc], w=[yn.r])
                  S.op("pool", lambda e: e.tensor_tensor(out=yn.t[:], in0=yn.t[:], in1=BC["lnbb"][:], op=ALU.add), r=[yn.r, r_bc], w=[yn.r])
                  S.op("dve", lambda e: e.tensor_tensor(out=yn.t[:], in0=yn.t[:], in1=bonus.t[:], op=ALU.add), r=[yn.r, bonus.r], w=[yn.r])
                  S.op("dve", lambda e: e.tensor_tensor(out=ybf.t[:], in0=yn.t[:], in1=X["gg"].t[:], op=ALU.mult), r=[yn.r, X["gg"].r], w=[ybf.r])
                  for p in range(4):
                      S.op("pe", lambda e, p=p: e.transpose(pbb.t[:, p * 128:(p + 1) * 128], ybf.t[:, p * 128:(p + 1) * 128], identb.t[:]), r=[ybf.r, identb.r], w=[pbb.r])
                  yo = yTo[tl_ % 2]
                  S.op("act", lambda e, yo=yo: e.copy(out=yo.t[:], in_=pbb.t[:, 0:512].rearrange("p (a t) -> p a t", t=128)), r=[pbb.r], w=[yo.r])
                  tails.append(S.dma("sp", ysrc[:, :, tl_ * 128:(tl_ + 1) * 128], yo.t[:], r=[yo.r], key="yTo%d" % (tl_ % 2)))
                  if debug and tl_ == 0:
                      S.op("act", lambda e, r_=r_: e.copy(out=t1.t[:], in_=r_), r=rG + [t1.r], w=[t1.r])
                      S.op("act", lambda e, v_=v_: e.copy(out=t2.t[:], in_=v_), r=rG + [t2.r], w=[t2.r])
                      tails.append(S.dma("sp", dbg["v"], t2.t[:], r=[t2.r], key="dbg_v"))
                      for n, src in (("r", t1), ("kp", kp), ("kkn", kkn), ("a", a_), ("sw", sw), ("Y", Yt), ("bonus", bonus), ("Gm", Gm), ("yn", yn)):
                          tails.append(S.dma("sp", dbg[n], src.t[:], r=[src.r], key="dbg_" + n))
                      tails.append(S.dma("sp", dbg["S"], ST[st_i].t[:], r=[ST[st_i].r], key="dbg_S"))
        except _Stop:
            pass
        if after is not None:
            after(C, tails)
        S.emit(st, tail_waits=tails)


PAIRS = [[0, 1], [2, 3], [4, 5], [6, 7]]
TL_KEYS = ("cT", "adaw", "adab", "lng", "lnb", "wo", "win", "wout")


def tl_decl(dt, sfx):
    return dict(cT=dt("cT" + sfx, [128, 8], F32), adaw=dt("adaw" + sfx, [D, 4 * D], F32), adab=dt("adab" + sfx, [128, 32], F32),
                lng=dt("lng" + sfx, [128, 16], F32), lnb=dt("lnb" + sfx, [128, 16], F32),
                wo=dt("wo" + sfx, [D, D], F32), win=dt("win" + sfx, [D, 2 * DFF], F32), wout=dt("wout" + sfx, [DFF, D], F32))


def build_fused(stop=99):
    nc = bass.Bass("TRN2", target_bir_lowering=False)
    dt = lambda n, s, d, k="ExternalInput": nc.dram_tensor(n, list(s), d, kind=k).ap()
    it = lambda n, s, d: nc.dram_tensor(n, list(s), d).ap()
    fox_in = fox_decl(dt, 4096, "_f")
    oT_d = dt("oT", [D, 2048], F32, "ExternalOutput")
    cin1, cout1 = it("cin1", [4, 128, 4096], BF16), it("cout1", [4, 2, 128, 4096], BF16)
    cin2, cout2 = it("cin2", [8, 128, 2048], F32), it("cout2", [8, 2, 128, 2048], F32)
    cin3, cout3 = it("cin3", [4, 128, 4096], BF16), it("cout3", [4, 2, 128, 4096], BF16)

    def gather(cin, cout, key, nblk):
        def after(C, tails):
            prev = list(tails)
            for k in range(nblk):
                o = C.S.cc("AllGather", PAIRS, cin[k], cout[k].rearrange("r p t -> (r p) t"), key=key, extra=prev)
            tails.append(o)
        return after

    cin1_v = cin1.rearrange("k p t -> (k p) t")
    cin3_v = cin3.rearrange("k p t -> (k p) t")
    cin2_v = cin2.rearrange("k p t -> (k p) t")
    y1_v = cout1
    y3_v = cout3
    fox_phase(nc, "f_", fox_in, cin1_v, 4096, after=gather(cin1, cout1, "g1", 4))
    if stop <= 1:
        return nc
    phase_end(nc)
    sel_d = dt("sel", [128, 2], F32)
    xTh_d = dt("xTh", [D, 2048], F32)
    t = tl_decl(dt, "_t0")
    tl_phase(nc, "t0_", y1_v, xTh_d, t["cT"], t["adaw"], t["adab"], t["lng"], t["lnb"], t["wo"], t["win"], t["wout"], cin2_v, 2048, 1024,
             sel_d=sel_d, after=gather(cin2, cout2, "g2", 8))
    if stop <= 2:
        return nc
    phase_end(nc)
    rw_in = rwkv_decl(dt, "_r")
    c2v = cout2.rearrange("kc r p t -> r p kc t")
    rwkv_phase(nc, "r_", lambda t0, n: c2v[t0 // 2048, :, :, (t0 % 2048):(t0 % 2048) + n], rw_in, cin3_v, 4096, after=gather(cin3, cout3, "g3", 4))
    if stop <= 3:
        return nc
    phase_end(nc)
    t = tl_decl(dt, "_t1")
    tl_phase(nc, "t1_", y3_v, cin2_v, t["cT"], t["adaw"], t["adab"], t["lng"], t["lnb"], t["wo"], t["win"], t["wout"], oT_d, 2048, 1024, sel_d=sel_d)
    return nc


NCORES = 8
_PROGS = {}


def _prog(name, fn):
    if name not in _PROGS:
        _PROGS[name] = fn()
    return _PROGS[name]


def _col_layout(v):
    return np.ascontiguousarray(np.asarray(v).reshape(-1, 128).T)


def _bc(v, n=512):
    v = np.asarray(v, dtype=np.float32).reshape(-1)
    return np.ascontiguousarray(np.broadcast_to(v[None, :], (128, v.shape[0])))


def _run(nc, in_maps):
    res = run_bass_kernel_spmd(nc, in_maps, core_ids=list(range(NCORES)))
    return res.results


def _tl_maps(yT_list, xT_list, c, ada_w_i, ada_b_i, ln_g_i, ln_b_i, wo, win, wout):
    maps = []
    adaw = np.ascontiguousarray(ada_w_i[:, 2 * D:])
    adab = _col_layout(ada_b_i[2 * D:])
    lng = _col_layout(ln_g_i.reshape(-1))
    lnb = _col_layout(ln_b_i.reshape(-1))
    for core in range(NCORES):
        b, th = core // 2, core % 2
        ts = slice(th * 2048, (th + 1) * 2048)
        maps.append({
            "yT": np.ascontiguousarray(yT_list[b][:, ts]),
            "xT": np.ascontiguousarray(xT_list[b][:, ts]),
            "cT": _col_layout(c[b]),
            "adaw": adaw, "adab": adab, "lng": lng, "lnb": lnb,
            "wo": wo, "win": win, "wout": wout,
        })
    return maps


def kernel_unfused(x, c, ada_w, ada_b, ln_g, ln_b, ffn_w_in, ffn_w_out,
           fox_w_in, fox_b_f, fox_q_g, fox_k_g, fox_w_o,
           rwkv_mu, rwkv_w_rkv, rwkv_w0, rwkv_w1, rwkv_w2, rwkv_a0, rwkv_a1, rwkv_a2,
           rwkv_g1, rwkv_g2, rwkv_k_k, rwkv_k_a, rwkv_r_k, rwkv_lnx_g, rwkv_lnx_b, rwkv_w_o):
    f = lambda a: np.ascontiguousarray(np.asarray(a, dtype=np.float32))
    x, c, ada_w, ada_b, ln_g, ln_b = f(x), f(c), f(ada_w), f(ada_b), f(ln_g), f(ln_b)
    ffn_w_in, ffn_w_out = f(ffn_w_in), f(ffn_w_out)
    B = x.shape[0]
    w_in = f(fox_w_in)[0]
    b_f, q_g, k_g = f(fox_b_f)[0], f(fox_q_g)[0], f(fox_k_g)[0]
    maps = []
    for core in range(NCORES):
        b, hh = core // 2, core % 2
        sl = slice(hh * 512, (hh + 1) * 512)
        maps.append({
            "x": x[b],
            "cT": _col_layout(c[b]),
            "adaw": np.ascontiguousarray(ada_w[0][:, 0:2 * D]),
            "adab": _col_layout(ada_b[0][0:2 * D]),
            "wq": np.ascontiguousarray(w_in[:, 0:D][:, sl]),
            "wk": np.ascontiguousarray(w_in[:, D:2 * D][:, sl]),
            "wv": np.ascontiguousarray(w_in[:, 2 * D:3 * D][:, sl]),
            "wf": np.ascontiguousarray(w_in[:, 3 * D + hh * 8:3 * D + hh * 8 + 8]),
            "wg": np.ascontiguousarray(w_in[:, 3 * D + 16:][:, sl]),
            "bfb": _bc(b_f[hh * 8:(hh + 1) * 8]),
            "qgb": _bc(np.tile(q_g, 8)),
            "kgb": _bc(np.tile(k_g, 8)),
        })
    r1 = _run(_prog("fox", build_fox), maps)
    yT = [np.concatenate([r1[2 * b]["ygT"], r1[2 * b + 1]["ygT"]], axis=0) for b in range(B)]
    xT = [np.ascontiguousarray(x[b].T) for b in range(B)]
    tlp = _prog("tl", build_tl)
    r2 = _run(tlp, _tl_maps(yT, xT, c, ada_w[0], ada_b[0], ln_g[0], ln_b[0], f(fox_w_o)[0], ffn_w_in[0], ffn_w_out[0]))
    x1T = [np.concatenate([r2[2 * b]["oT"], r2[2 * b + 1]["oT"]], axis=1) for b in range(B)]
    mu = f(rwkv_mu)[0]
    w_rkv = f(rwkv_w_rkv)[0]
    P = dict(w0=f(rwkv_w0)[0], w1=f(rwkv_w1)[0], w2=f(rwkv_w2)[0], a0=f(rwkv_a0)[0], a1=f(rwkv_a1)[0], a2=f(rwkv_a2)[0],
             g1=f(rwkv_g1)[0], g2=f(rwkv_g2)[0], k_k=f(rwkv_k_k)[0], k_a=f(rwkv_k_a)[0], r_k=f(rwkv_r_k)[0].reshape(-1),
             lnx_g=f(rwkv_lnx_g)[0], lnx_b=f(rwkv_lnx_b)[0])
    mu_l = np.concatenate([_col_layout(mu[n]) for n in range(6)], axis=1)
    maps = []
    for core in range(NCORES):
        b, hh = core // 2, core % 2
        sl = slice(hh * 512, (hh + 1) * 512)
        maps.append({
            "xT": x1T[b],
            "cT": _col_layout(c[b]),
            "adaw": np.ascontiguousarray(ada_w[1][:, 0:2 * D]),
            "adab": _col_layout(ada_b[1][0:2 * D]),
            "mu": mu_l,
            "wr": np.ascontiguousarray(w_rkv[0][:, sl]),
            "wk": np.ascontiguousarray(w_rkv[1][:, sl]),
            "wv": np.ascontiguousarray(w_rkv[2][:, sl]),
            "w1": P["w1"], "a1": P["a1"], "g1": P["g1"],
            "w2": np.ascontiguousarray(P["w2"][:, sl]),
            "a2": np.ascontiguousarray(P["a2"][:, sl]),
            "g2": np.ascontiguousarray(P["g2"][:, sl]),
            "w0b": _bc(P["w0"][sl]), "a0b": _bc(P["a0"][sl]), "kkb": _bc(P["k_k"][sl]), "kab": _bc(P["k_a"][sl]),
            "rkb": _bc(P["r_k"][sl]), "lngb": _bc(P["lnx_g"][sl]), "lnbb": _bc(P["lnx_b"][sl]),
        })
    r3 = _run(_prog("rwkv", build_rwkv), maps)
    y2T = [np.concatenate([r3[2 * b]["yT"], r3[2 * b + 1]["yT"]], axis=0) for b in range(B)]
    r4 = _run(tlp, _tl_maps(y2T, x1T, c, ada_w[1], ada_b[1], ln_g[1], ln_b[1], f(rwkv_w_o)[0], ffn_w_in[1], ffn_w_out[1]))
    out = np.empty(x.shape, np.float32)
    for core in range(NCORES):
        b, th = core // 2, core % 2
        out[b, th * 2048:(th + 1) * 2048, :] = r4[core]["oT"].T
    return out


def kernel(x, c, ada_w, ada_b, ln_g, ln_b, ffn_w_in, ffn_w_out,
           fox_w_in, fox_b_f, fox_q_g, fox_k_g, fox_w_o,
           rwkv_mu, rwkv_w_rkv, rwkv_w0, rwkv_w1, rwkv_w2, rwkv_a0, rwkv_a1, rwkv_a2,
           rwkv_g1, rwkv_g2, rwkv_k_k, rwkv_k_a, rwkv_r_k, rwkv_lnx_g, rwkv_lnx_b, rwkv_w_o):
    f = lambda a: np.ascontiguousarray(np.asarray(a, dtype=np.float32))
    x, c, ada_w, ada_b, ln_g, ln_b = f(x), f(c), f(ada_w), f(ada_b), f(ln_g), f(ln_b)
    ffn_w_in, ffn_w_out = f(ffn_w_in), f(ffn_w_out)
    w_in = f(fox_w_in)[0]
    b_f, q_g, k_g = f(fox_b_f)[0], f(fox_q_g)[0], f(fox_k_g)[0]
    mu = f(rwkv_mu)[0]
    w_rkv = f(rwkv_w_rkv)[0]
    P = dict(w0=f(rwkv_w0)[0], w1=f(rwkv_w1)[0], w2=f(rwkv_w2)[0], a0=f(rwkv_a0)[0], a1=f(rwkv_a1)[0], a2=f(rwkv_a2)[0],
             g1=f(rwkv_g1)[0], g2=f(rwkv_g2)[0], k_k=f(rwkv_k_k)[0], k_a=f(rwkv_k_a)[0], r_k=f(rwkv_r_k)[0].reshape(-1),
             lnx_g=f(rwkv_lnx_g)[0], lnx_b=f(rwkv_lnx_b)[0])
    mu_l = np.concatenate([_col_layout(mu[n]) for n in range(6)], axis=1)
    wos = [f(fox_w_o)[0], f(rwkv_w_o)[0]]
    tl_common = []
    for L in range(2):
        tl_common.append({
            "adaw_t%d" % L: np.ascontiguousarray(ada_w[L][:, 2 * D:]), "adab_t%d" % L: _col_layout(ada_b[L][2 * D:]),
            "lng_t%d" % L: _col_layout(ln_g[L].reshape(-1)), "lnb_t%d" % L: _col_layout(ln_b[L].reshape(-1)),
            "wo_t%d" % L: wos[L], "win_t%d" % L: ffn_w_in[L], "wout_t%d" % L: ffn_w_out[L]})
    adaw_f = np.ascontiguousarray(ada_w[0][:, 0:2 * D])
    adaw_r = np.ascontiguousarray(ada_w[1][:, 0:2 * D])
    maps = []
    for core in range(NCORES):
        b, j = core // 2, core % 2
        sl = slice(j * 512, (j + 1) * 512)
        cT = _col_layout(c[b])
        m = {
            "x_f": x[b], "cT_f": cT, "adaw_f": adaw_f, "adab_f": _col_layout(ada_b[0][0:2 * D]),
            "wq_f": np.ascontiguousarray(w_in[:, 0:D][:, sl]), "wk_f": np.ascontiguousarray(w_in[:, D:2 * D][:, sl]),
            "wv_f": np.ascontiguousarray(w_in[:, 2 * D:3 * D][:, sl]), "wf_f": np.ascontiguousarray(w_in[:, 3 * D + j * 8:3 * D + j * 8 + 8]),
            "wg_f": np.ascontiguousarray(w_in[:, 3 * D + 16:][:, sl]),
            "bfb_f": _bc(b_f[j * 8:(j + 1) * 8]), "qgb_f": _bc(np.tile(q_g, 8)), "kgb_f": _bc(np.tile(k_g, 8)),
            "cT_r": cT, "adaw_r": adaw_r, "adab_r": _col_layout(ada_b[1][0:2 * D]), "mu_r": mu_l,
            "wr_r": np.ascontiguousarray(w_rkv[0][:, sl]), "wk_r": np.ascontiguousarray(w_rkv[1][:, sl]), "wv_r": np.ascontiguousarray(w_rkv[2][:, sl]),
            "w1_r": P["w1"], "a1_r": P["a1"], "g1_r": P["g1"],
            "w2_r": np.ascontiguousarray(P["w2"][:, sl]), "a2_r": np.ascontiguousarray(P["a2"][:, sl]), "g2_r": np.ascontiguousarray(P["g2"][:, sl]),
            "w0b_r": _bc(P["w0"][sl]), "a0b_r": _bc(P["a0"][sl]), "kkb_r": _bc(P["k_k"][sl]), "kab_r": _bc(P["k_a"][sl]),
            "rkb_r": _bc(P["r_k"][sl]), "lngb_r": _bc(P["lnx_g"][sl]), "lnbb_r": _bc(P["lnx_b"][sl]),
            "cT_t0": cT, "cT_t1": cT,
            "sel": np.ascontiguousarray(np.broadcast_to(np.array([1.0 - j, float(j)], np.float32)[None, :], (128, 2))),
            "xTh": np.ascontiguousarray(x[b, j * 2048:(j + 1) * 2048, :].T),
        }
        m.update(tl_common[0])
        m.update(tl_common[1])
        maps.append(m)
    res = _run(_prog("fused", build_fused), maps)
    out = np.empty(x.shape, np.float32)
    for core in range(NCORES):
        b, j = core // 2, core % 2
        out[b, j * 2048:(j + 1) * 2048, :] = res[core]["oT"].T
    return out
```

```python
import numpy as np
from contextlib import ExitStack
import concourse.bass as bass
import concourse.mybir as mybir
from concourse.bass_utils import run_bass_kernel_spmd

F32 = mybir.dt.float32
BF16 = mybir.dt.bfloat16
AF = mybir.ActivationFunctionType
ALU = mybir.AluOpType
AX = mybir.AxisListType

ENGS = ("pe", "act", "dve", "pool", "sp")
LAST_SEMS = []
SEM_CTR = [0]


def phase_end(nc):
    nc.all_engine_barrier()
    nc.clear_and_free_semaphores(list(LAST_SEMS))
    LAST_SEMS[:] = []
    nc.all_engine_barrier()


class Res:
    __slots__ = ("name", "last_w", "readers", "excl")

    def __init__(self, name, excl=False):
        self.name = name
        self.excl = excl
        self.last_w = None
        self.readers = []


class Op:
    __slots__ = ("eng", "fn", "deps", "sig", "count", "is_dma", "key", "semval", "vc", "waits", "gid")


class Sched:
    def __init__(self, nc):
        self.nc = nc
        self.ops = []
        self.dma_counts = {}

    def res(self, name, excl=False):
        return Res(name, excl)

    def _add(self, op, r, w, extra=()):
        deps = [(d, "raw") for d in extra]
        for x in r:
            if x.last_w is not None:
                deps.append((x.last_w, "raw"))
            if x.excl:
                for rd in x.readers:
                    deps.append((rd, "war"))
        for x in w:
            if x.last_w is not None:
                deps.append((x.last_w, "waw"))
            for rd in x.readers:
                deps.append((rd, "war"))
        dd = []
        for d, kind in deps:
            if d is op:
                continue
            if (not d.is_dma) and (not op.is_dma) and d.eng == op.eng:
                if op.eng == "pe" or kind == "war":
                    continue
            dd.append(d)
        op.deps = dd
        for d in dd:
            d.sig = True
        for x in w:
            x.last_w = op
            x.readers = []
        for x in r:
            if x.last_w is not op:
                if not op.is_dma:
                    x.readers = [q for q in x.readers if q.is_dma or q.eng != op.eng]
                x.readers.append(op)
        op.gid = len(self.ops)
        self.ops.append(op)
        return op

    def op(self, eng, fn, r=(), w=()):
        o = Op()
        o.eng = eng
        o.fn = fn
        o.is_dma = False
        o.sig = False
        o.key = eng
        return self._add(o, r, w)

    def dma(self, eng, out, in_, r=(), w=(), key=None, **kw):
        o = Op()
        o.eng = eng
        o.is_dma = True
        o.sig = True
        assert key is not None
        o.key = "dma:" + key
        o.fn = lambda e: e.dma_start(out=out, in_=in_, **kw)
        return self._add(o, r, w)

    def cc(self, kind, groups, in_ap, out_ap, r=(), w=(), key=None, extra=()):
        o = Op()
        o.eng = "pool"
        o.is_dma = True
        o.sig = True
        o.key = "cc:" + key
        o.fn = lambda e: e.collective_compute(kind, ALU.bypass, replica_groups=groups, ins=[in_ap], outs=[out_ap])
        return self._add(o, r, w, extra)

    def finalize(self, final_wait_eng="sp"):
        nc = self.nc
        counts = {e: 0 for e in ENGS}
        dmac = {}
        for o in self.ops:
            if o.is_dma:
                dmac[o.key] = dmac.get(o.key, 0) + (1 if o.key.startswith("cc:") else 16)
                o.semval = dmac[o.key]
            elif o.sig:
                counts[o.eng] += 1
                o.semval = counts[o.eng]
        seen = {e: {} for e in ENGS}
        for o in self.ops:
            s = seen[o.eng]
            need = {}
            for d in o.deps:
                if s.get(d.key, 0) < d.semval:
                    if d.key not in need or need[d.key].semval < d.semval:
                        need[d.key] = d
            o.waits = [(d.key, d.semval) for d in need.values()]
            for d in need.values():
                for k, v in d.vc.items():
                    if s.get(k, 0) < v:
                        s[k] = v
            if o.sig:
                vc = dict(s)
                vc[o.key] = o.semval
                o.vc = vc
            else:
                o.vc = None
        return counts, dmac

    def emit(self, stack, tail_waits=()):
        nc = self.nc
        counts, dmac = self.finalize()
        sems = {}
        for k in list(ENGS) + list(dmac):
            SEM_CTR[0] += 1
            sems[k] = nc.alloc_semaphore(name="sem%d" % SEM_CTR[0])
        self.nsems = len(sems)
        LAST_SEMS[:] = list(sems.values())
        per = {e: [o for o in self.ops if o.eng == e] for e in ENGS}
        block = stack.enter_context(nc.Block())

        def run(eng_name, eng):
            for o in per[eng_name]:
                for k, v in o.waits:
                    eng.wait_ge(sems[k], v)
                ins = o.fn(eng)
                if o.sig:
                    if o.is_dma:
                        ins.then_inc(sems[o.key], 1 if o.key.startswith("cc:") else 16)
                    else:
                        ins.then_inc(sems[o.key], 1)
            if eng_name == "sp":
                done = {}
                for o in tail_waits:
                    done[o.key] = max(done.get(o.key, 0), o.semval)
                for k, v in done.items():
                    eng.wait_ge(sems[k], v)

        @block.tensor
        def _(e):
            run("pe", e)

        @block.scalar
        def _(e):
            run("act", e)

        @block.vector
        def _(e):
            run("dve", e)

        @block.gpsimd
        def _(e):
            run("pool", e)

        @block.sync
        def _(e):
            run("sp", e)


D = 1024
DFF = 2816
NFC = 22
ALPHA = 4.0 ** 0.25
LN_EPS = 1e-5
EPS_P = LN_EPS / (ALPHA * ALPHA)


class Ctx:
    def __init__(self, nc, st, pfx=""):
        self.nc = nc
        self.st = st
        self.S = Sched(nc)
        self.n = 0
        self.pfx = pfx

    def sb(self, name, shape, dt):
        return self.st.enter_context(self.nc.sbuf_tensor(self.pfx + "sb_" + name, list(shape), dt))

    def ps(self, name, shape, dt=F32):
        return self.st.enter_context(self.nc.psum_tensor(self.pfx + "ps_" + name, list(shape), dt))

    def R(self, name, excl=False):
        return self.S.res(name, excl)


def make_consts(C):
    S = C.S
    C.ones = C.sb("ones", [128, 128], F32)
    C.r_ones = C.R("ones")
    S.op("pool", lambda e: e.memset(C.ones[:], 1.0), w=[C.r_ones])
    C.ident = C.sb("ident", [128, 128], F32)
    C.r_ident = C.R("ident")
    S.op("pool", lambda e: e.memset(C.ident[:], 1.0), w=[C.r_ident])
    S.op("pool", lambda e: e.affine_select(out=C.ident[:], in_=C.ident[:], pattern=[[-1, 128]], compare_op=ALU.is_equal, fill=0.0, base=0, channel_multiplier=1), r=[C.r_ident], w=[C.r_ident])


def compute_mods(C, cT_d, adaw_d, adab_d, nm, wbufs, psm, r_psm):
    S = C.S
    cT = C.sb("cT", [128, 8], F32)
    r_cT = C.R("cT")
    S.dma("sp", cT[:], cT_d, w=[r_cT], key="cT")
    S.op("act", lambda e: e.activation(out=cT[:], in_=cT[:], func=AF.Silu), r=[r_cT], w=[r_cT])
    adab = C.sb("adab", [128, nm * 8], F32)
    r_adab = C.R("adab")
    S.dma("sp", adab[:], adab_d, w=[r_adab], key="adab")
    mods = C.sb("mods", [128, nm * 8], F32)
    r_mods = C.R("mods")
    src = adaw_d.rearrange("(kc p) n -> p kc n", p=128)
    for m in range(nm):
        for q in range(4):
            bi = (m * 4 + q) % 2
            wt, r_wt = wbufs[bi]
            c0 = m * 1024 + q * 256
            S.dma("sp", wt[:], src[:, :, c0:c0 + 256], w=[r_wt], key="adaw%d" % bi)
            for j in range(2):
                cc = q * 2 + j
                for kc in range(8):
                    S.op("pe", lambda e, m=m, cc=cc, kc=kc, j=j, wt=wt: e.matmul(psm[:, m * 8 + cc:m * 8 + cc + 1], wt[:, kc, j * 128:(j + 1) * 128], cT[:, kc:kc + 1], start=(kc == 0), stop=(kc == 7)), r=[r_wt, r_cT], w=[r_psm])
    S.op("dve", lambda e: e.tensor_tensor(out=mods[:], in0=psm[:, 0:nm * 8], in1=adab[:], op=ALU.add), r=[r_psm, r_adab], w=[r_mods])
    return mods, r_mods


def ln_group(C, zT, r_z, g, n0, colA, colB, r_cols, outs, P):
    S = C.S
    ps_s, r_ps_s = P["st0"]
    ps_q, r_ps_q = P["st1"]
    sq, r_sq = P["sq"]
    for cc in range(8):
        S.op("act", lambda e, cc=cc: e.activation(out=sq[cc % 2][:], in_=zT[:, cc, n0:n0 + 512], func=AF.Square), r=[r_z[cc]], w=[r_sq[cc % 2]])
        S.op("pe", lambda e, cc=cc: e.matmul(ps_s[:], C.ones[:], zT[:, cc, n0:n0 + 512], start=(cc == 0), stop=(cc == 7)), r=[C.r_ones, r_z[cc]], w=[r_ps_s])
        S.op("pe", lambda e, cc=cc: e.matmul(ps_q[:], C.ones[:], sq[cc % 2][:], start=(cc == 0), stop=(cc == 7)), r=[C.r_ones, r_sq[cc % 2]], w=[r_ps_q])
    mean, r_mean = P["mean"]
    rstd, r_rstd = P["rstd"]
    S.op("act", lambda e: e.mul(out=mean[:], in_=ps_s[:], mul=1.0 / D), r=[r_ps_s], w=[r_mean])
    S.op("dve", lambda e: e.tensor_tensor(out=rstd[:], in0=mean[:], in1=mean[:], op=ALU.mult), r=[r_mean], w=[r_rstd])
    S.op("dve", lambda e: e.scalar_tensor_tensor(out=rstd[:], in0=ps_q[:], scalar=1.0 / D, in1=rstd[:], op0=ALU.mult, op1=ALU.subtract), r=[r_ps_q, r_rstd], w=[r_rstd])
    S.op("dve", lambda e: e.tensor_scalar(out=rstd[:], in0=rstd[:], scalar1=P["eps"], scalar2=None, op0=ALU.add), r=[r_rstd], w=[r_rstd])
    S.op("act", lambda e: e.activation(out=rstd[:], in_=rstd[:], func=AF.Sqrt), r=[r_rstd], w=[r_rstd])
    S.op("dve", lambda e: e.reciprocal(out=rstd[:], in_=rstd[:]), r=[r_rstd], w=[r_rstd])
    S.op("dve", lambda e: e.scalar_tensor_tensor(out=mean[:], in0=mean[:], scalar=-1.0, in1=rstd[:], op0=ALU.mult, op1=ALU.mult), r=[r_mean, r_rstd], w=[r_mean])
    tt, r_tt = P["tt"]
    for cc in range(8):
        k = cc % 2
        S.op("dve", lambda e, cc=cc, k=k: e.tensor_tensor(out=tt[k][:], in0=zT[:, cc, n0:n0 + 512], in1=rstd[:], op=ALU.mult), r=[r_z[cc], r_rstd], w=[r_tt[k]])
        S.op("dve", lambda e, cc=cc, k=k: e.tensor_tensor(out=tt[k][:], in0=tt[k][:], in1=mean[:], op=ALU.add), r=[r_tt[k], r_mean], w=[r_tt[k]])
        for (dst, r_dst, gt, go, bt, bo, eng) in outs:
            if eng == "act":
                S.op("act", lambda e, cc=cc, k=k, dst=dst, gt=gt, go=go, bt=bt, bo=bo: e.activation(out=dst[:, cc, n0:n0 + 512], in_=tt[k][:], func=AF.Identity, scale=gt[:, go + cc:go + cc + 1], bias=bt[:, bo + cc:bo + cc + 1]), r=[r_tt[k]] + r_cols, w=[r_dst[cc]])
            else:
                S.op(eng, lambda e, cc=cc, k=k, dst=dst, gt=gt, go=go, bt=bt, bo=bo: e.tensor_scalar(out=dst[:, cc, n0:n0 + 512], in0=tt[k][:], scalar1=gt[:, go + cc:go + cc + 1], scalar2=bt[:, bo + cc:bo + cc + 1], op0=ALU.mult, op1=ALU.add), r=[r_tt[k]] + r_cols, w=[r_dst[cc]])


def build_tl(NT=2048, HALF=1024, debug=False):
    nc = bass.Bass("TRN2", target_bir_lowering=False)
    dt = lambda n, s, d, k="ExternalInput": nc.dram_tensor(n, list(s), d, kind=k).ap()
    yT_d = dt("yT", [D, NT], BF16)
    xT_d = dt("xT", [D, NT], F32)
    cT_d = dt("cT", [128, 8], F32)
    adaw_d = dt("adaw", [D, 4 * D], F32)
    adab_d = dt("adab", [128, 32], F32)
    lng_d = dt("lng", [128, 16], F32)
    lnb_d = dt("lnb", [128, 16], F32)
    wo_d = dt("wo", [D, D], F32)
    win_d = dt("win", [D, 2 * DFF], F32)
    wout_d = dt("wout", [DFF, D], F32)
    oT_d = dt("oT", [D, NT], F32, "ExternalOutput")
    tl_phase(nc, "", yT_d, xT_d, cT_d, adaw_d, adab_d, lng_d, lnb_d, wo_d, win_d, wout_d, oT_d, NT, HALF, debug=debug)
    return nc


def tl_phase(nc, pfx, yT_d, xT_d, cT_d, adaw_d, adab_d, lng_d, lnb_d, wo_d, win_d, wout_d, oT_d, NT=2048, HALF=1024, debug=False, sel_d=None, after=None):
    dt = lambda n, s, d, k="ExternalInput": nc.dram_tensor(n, list(s), d, kind=k).ap()
    NG = HALF // 512
    if debug:
        dbg_cols = dt("dbg_cols", [128, 48], F32, "ExternalOutput")
        dbg_mods = dt("dbg_mods", [128, 32], F32, "ExternalOutput")
        dbg_x1 = dt("dbg_x1", [D, HALF], F32, "ExternalOutput")
        dbg_h = dt("dbg_h", [D, HALF], BF16, "ExternalOutput")
        dbg_a = dt("dbg_a", [DFF, HALF], BF16, "ExternalOutput")
    with ExitStack() as st:
        C = Ctx(nc, st, pfx)
        S = C.S
        make_consts(C)
        PB = [(C.ps("pb%d" % i, [128, 512]), C.R("pb%d" % i, True)) for i in range(8)]
        if sel_d is not None:
            selt = C.sb("selt", [128, 2], F32)
            r_selt = C.R("selt")
            S.dma("sp", selt[:], sel_d, w=[r_selt], key="selt")
            ystg = [[(C.sb("ystg%d_%d" % (i, j), [128, HALF], BF16), C.R("ystg%d_%d" % (i, j))) for j in range(2)] for i in range(2)]
        xT = C.sb("xT", [128, 8, HALF], F32)
        r_x = [C.R("xT%d" % cc) for cc in range(8)]
        yT = C.sb("yT", [128, 8, HALF], BF16)
        r_y = [C.R("yT%d" % cc) for cc in range(8)]
        hT = yT
        r_h = r_y
        aT = C.sb("aT", [128, NFC, HALF], BF16)
        r_a = [C.R("aT%d" % f) for f in range(NFC)]
        wo = C.sb("wo", [128, 8, D], BF16)
        r_wo = C.R("wo")
        wA = [(C.sb("wA%d" % i, [128, 8, 256], F32), C.R("wA%d" % i)) for i in range(2)]
        wi = [(C.sb("wi%d" % i, [128, 8, 512], BF16), C.R("wi%d" % i)) for i in range(2)]
        wo2 = [(C.sb("wo2%d" % i, [128, NFC, 256], BF16), C.R("wo2%d" % i)) for i in range(2)]
        P = {
            "st0": PB[6], "st1": PB[7],
            "sq": ([C.sb("sq%d" % i, [128, 512], F32) for i in range(2)], [C.R("sq%d" % i) for i in range(2)]),
            "mean": (C.sb("mean", [128, 512], F32), C.R("mean")),
            "rstd": (C.sb("rstd", [128, 512], F32), C.R("rstd")),
            "tt": ([C.sb("tt%d" % i, [128, 512], F32) for i in range(2)], [C.R("tt%d" % i) for i in range(2)]),
            "eps": EPS_P,
        }
        sl = [C.sb("sl%d" % i, [128, 512], F32) for i in range(2)]
        r_sl = [C.R("sl%d" % i) for i in range(2)]
        mods, r_mods = compute_mods(C, cT_d, adaw_d, adab_d, 4, wA, PB[5][0], PB[5][1])
        lng = C.sb("lng", [128, 16], F32)
        lnb = C.sb("lnb", [128, 16], F32)
        r_ln = C.R("ln")
        S.dma("sp", lng[:], lng_d, w=[r_ln], key="lng")
        S.dma("sp", lnb[:], lnb_d, w=[r_ln], key="lnb")
        cols = C.sb("cols", [128, 48], F32)
        r_cols = C.R("cols")
        S.op("dve", lambda e: e.tensor_scalar(out=cols[:, 0:8], in0=mods[:, 0:8], scalar1=1.0 / ALPHA, scalar2=None, op0=ALU.mult), r=[r_mods], w=[r_cols])
        S.op("dve", lambda e: e.tensor_scalar(out=cols[:, 8:16], in0=mods[:, 24:32], scalar1=1.0 / ALPHA, scalar2=None, op0=ALU.mult), r=[r_mods], w=[r_cols])
        S.op("dve", lambda e: e.tensor_scalar(out=cols[:, 32:40], in0=mods[:, 16:24], scalar1=1.0, scalar2=None, op0=ALU.add), r=[r_mods], w=[r_cols])
        S.op("dve", lambda e: e.tensor_tensor(out=cols[:, 16:24], in0=lng[:, 0:8], in1=cols[:, 32:40], op=ALU.mult), r=[r_ln, r_cols], w=[r_cols])
        S.op("dve", lambda e: e.tensor_tensor(out=cols[:, 24:32], in0=lnb[:, 0:8], in1=cols[:, 32:40], op=ALU.mult), r=[r_ln, r_cols], w=[r_cols])
        S.op("dve", lambda e: e.tensor_tensor(out=cols[:, 24:32], in0=cols[:, 24:32], in1=mods[:, 8:16], op=ALU.add), r=[r_mods, r_cols], w=[r_cols])
        rc = [r_cols, r_ln]
        S.dma("pool", wo[:], wo_d.rearrange("(kc p) n -> p kc n", p=128), w=[r_wo], key="wo")
        xsrc = xT_d.rearrange("(kc p) t -> p kc t", p=128)
        if sel_d is None:
            ysrc_ = yT_d.rearrange("(kc p) t -> p kc t", p=128)
            ysrc_fn = lambda cc, a, n: ysrc_[:, cc, a:a + n]
        else:
            ysrc_fn = lambda cc, a, n: yT_d[cc % 4, cc // 4, :, a:a + n]
        osrc = oT_d.rearrange("(kc p) t -> p kc t", p=128)
        winr = win_d.rearrange("(kc p) n -> p kc n", p=128)
        woutr = wout_d.rearrange("(fc p) n -> p fc n", p=128)
        tails = []
        pbi = 0
        for half in range(NT // HALF):
            t0 = half * HALF
            for cc in range(8):
                if sel_d is None:
                    S.dma("sp", yT[:, cc, :], ysrc_fn(cc, t0, HALF), w=[r_y[cc]], key="yT%d" % cc)
                else:
                    (ya, r_ya), (yb, r_yb) = ystg[cc % 2]
                    S.dma("sp", ya[:], ysrc_fn(cc, t0, HALF), w=[r_ya], key="ysa%d" % (cc % 2))
                    S.dma("sp", yb[:], ysrc_fn(cc, NT + t0, HALF), w=[r_yb], key="ysb%d" % (cc % 2))
                    S.op("dve", lambda e, ya=ya: e.tensor_scalar(out=ya[:], in0=ya[:], scalar1=selt[:, 0:1], scalar2=None, op0=ALU.mult), r=[r_ya, r_selt], w=[r_ya])
                    S.op("dve", lambda e, ya=ya, yb=yb, cc=cc: e.scalar_tensor_tensor(out=yT[:, cc, :], in0=yb[:], scalar=selt[:, 1:2], in1=ya[:], op0=ALU.mult, op1=ALU.add), r=[r_ya, r_yb, r_selt], w=[r_y[cc]])
                S.dma("sp", xT[:, cc, :], xsrc[:, cc, t0:t0 + HALF], w=[r_x[cc]], key="xT%d" % cc)
            for g in range(NG):
                n0 = g * 512
                for cc in range(8):
                    pb, r_pb = PB[pbi % 4]
                    pbi += 1
                    for kc in range(8):
                        S.op("pe", lambda e, pb=pb, cc=cc, kc=kc, n0=n0: e.matmul(pb[:], wo[:, kc, cc * 128:(cc + 1) * 128], yT[:, kc, n0:n0 + 512], start=(kc == 0), stop=(kc == 7)), r=[r_wo] + r_y, w=[r_pb])
                    S.op("dve", lambda e, pb=pb, cc=cc, n0=n0: e.scalar_tensor_tensor(out=xT[:, cc, n0:n0 + 512], in0=pb[:], scalar=cols[:, cc:cc + 1], in1=xT[:, cc, n0:n0 + 512], op0=ALU.mult, op1=ALU.add), r=[r_pb, r_x[cc], r_cols], w=[r_x[cc]])
            for g in range(NG):
                n0 = g * 512
                ln_group(C, xT, r_x, g, n0, None, None, rc,
                         [(xT, r_x, lng, 0, lnb, 0, "act"), (hT, r_h, cols, 16, cols, 24, "act")], P)
            if debug and half == 0:
                tails.append(S.dma("sp", dbg_cols, cols[:], r=[r_cols], key="dbg0"))
                tails.append(S.dma("sp", dbg_mods, mods[:], r=[r_mods], key="dbg1"))
                tails.append(S.dma("sp", dbg_x1.rearrange("(kc p) t -> p kc t", p=128), xT[:], r=r_x, key="dbg2"))
                tails.append(S.dma("sp", dbg_h.rearrange("(kc p) t -> p kc t", p=128), hT[:], r=r_h, key="dbg3"))
            for blk in range(NFC // 2):
                wt, r_wt = wi[blk % 2]
                c0 = blk * 256
                S.dma("pool", wt[:, :, 0:256], winr[:, :, c0:c0 + 256], w=[r_wt], key="wi%d" % (blk % 2))
                S.dma("pool", wt[:, :, 256:512], winr[:, :, DFF + c0:DFF + c0 + 256], w=[r_wt], key="wi%d" % (blk % 2))
                for j in range(2):
                    fc = blk * 2 + j
                    for g in range(NG):
                        n0 = g * 512
                        pg, r_pg = PB[pbi % 4]
                        pu, r_pu = PB[(pbi + 1) % 4]
                        pbi += 2
                        for kc in range(8):
                            S.op("pe", lambda e, pg=pg, wt=wt, j=j, kc=kc, n0=n0: e.matmul(pg[:], wt[:, kc, j * 128:(j + 1) * 128], hT[:, kc, n0:n0 + 512], start=(kc == 0), stop=(kc == 7)), r=[r_wt] + r_h, w=[r_pg])
                        for kc in range(8):
                            S.op("pe", lambda e, pu=pu, wt=wt, j=j, kc=kc, n0=n0: e.matmul(pu[:], wt[:, kc, 256 + j * 128:256 + (j + 1) * 128], hT[:, kc, n0:n0 + 512], start=(kc == 0), stop=(kc == 7)), r=[r_wt] + r_h, w=[r_pu])
                        k = (fc * NG + g) % 2
                        S.op("act", lambda e, pg=pg, k=k: e.activation(out=sl[k][:], in_=pg[:], func=AF.Silu), r=[r_pg], w=[r_sl[k]])
                        S.op("dve", lambda e, pu=pu, k=k, fc=fc, n0=n0: e.tensor_tensor(out=aT[:, fc, n0:n0 + 512], in0=pu[:], in1=sl[k][:], op=ALU.mult), r=[r_pu, r_sl[k]], w=[r_a[fc]])
            if debug and half == 0:
                tails.append(S.dma("sp", dbg_a.rearrange("(kc p) t -> p kc t", p=128), aT[:], r=r_a, key="dbg4"))
            for blk in range(4):
                wt, r_wt = wo2[blk % 2]
                S.dma("pool", wt[:], woutr[:, :, blk * 256:(blk + 1) * 256], w=[r_wt], key="wo2%d" % (blk % 2))
                for j in range(2):
                    cc = blk * 2 + j
                    for g in range(NG):
                        n0 = g * 512
                        pb, r_pb = PB[pbi % 4]
                        pbi += 1
                        for fc in range(NFC):
                            S.op("pe", lambda e, pb=pb, wt=wt, j=j, fc=fc, n0=n0: e.matmul(pb[:], wt[:, fc, j * 128:(j + 1) * 128], aT[:, fc, n0:n0 + 512], start=(fc == 0), stop=(fc == NFC - 1)), r=[r_wt, r_a[fc]], w=[r_pb])
                        S.op("dve", lambda e, pb=pb, cc=cc, n0=n0: e.scalar_tensor_tensor(out=xT[:, cc, n0:n0 + 512], in0=pb[:], scalar=cols[:, 8 + cc:9 + cc], in1=xT[:, cc, n0:n0 + 512], op0=ALU.mult, op1=ALU.add), r=[r_pb, r_x[cc], r_cols], w=[r_x[cc]])
            for g in range(NG):
                n0 = g * 512
                ln_group(C, xT, r_x, g, n0, None, None, rc, [(xT, r_x, lng, 8, lnb, 8, "act")], P)
            for cc in range(8):
                o = S.dma("sp", osrc[:, cc, t0:t0 + HALF], xT[:, cc, :], r=[r_x[cc]], w=[], key="oT%d" % cc)
                tails.append(o)
        if after is not None:
            after(C, tails)
        S.emit(st, tail_waits=tails)


T = 4096
NH = 8
HD = 64
QK_EPS = 1e-6
NEG = -30000.0
import os
LAG = int(os.environ.get("FOX_LAG", "2"))
FOX_DVE = int(os.environ.get("FOX_DVE", "1"))


def build_fox(TT=T, debug=False):
    nc = bass.Bass("TRN2", target_bir_lowering=False)
    dt = lambda n, s, d, k="ExternalInput": nc.dram_tensor(n, list(s), d, kind=k).ap()
    ins = fox_decl(dt, TT)
    yg_d = dt("ygT", [512, TT], BF16, "ExternalOutput")
    fox_phase(nc, "", ins, yg_d, TT, debug=debug)
    return nc


def fox_decl(dt, TT=T, sfx=""):
    return dict(
        x=dt("x" + sfx, [TT, D], F32), cT=dt("cT" + sfx, [128, 8], F32), adaw=dt("adaw" + sfx, [D, 2 * D], F32), adab=dt("adab" + sfx, [128, 16], F32),
        wq=dt("wq" + sfx, [D, 512], F32), wk=dt("wk" + sfx, [D, 512], F32), wv=dt("wv" + sfx, [D, 512], F32), wf=dt("wf" + sfx, [D, 8], F32), wg=dt("wg" + sfx, [D, 512], F32),
        bfb=dt("bfb" + sfx, [128, 8], F32), qgb=dt("qgb" + sfx, [128, 512], F32), kgb=dt("kgb" + sfx, [128, 512], F32))


def fox_phase(nc, pfx, ins, yg_d, TT=T, debug=False, after=None):
    dt = lambda n, s, d, k="ExternalInput": nc.dram_tensor(n, list(s), d, kind=k).ap()
    x_d, cT_d, adaw_d, adab_d = ins["x"], ins["cT"], ins["adaw"], ins["adab"]
    wq_d, wk_d, wv_d, wf_d, wg_d = ins["wq"], ins["wk"], ins["wv"], ins["wf"], ins["wg"]
    bf_d, qg_d, kg_d = ins["bfb"], ins["qgb"], ins["kgb"]
    NGR = TT // 512
    NTL = TT // 128
    if debug:
        dbg_h = dt("dbg_h", [D, 512], BF16, "ExternalOutput")
        dbg_q = dt("dbg_q", [128, 4, 512], BF16, "ExternalOutput")
        dbg_k = dt("dbg_k", [128, 4, TT], BF16, "ExternalOutput")
        dbg_F = dt("dbg_F", [128, NTL, 8], F32, "ExternalOutput")
        dbg_v = dt("dbg_v", [128, NTL, 8, 65], BF16, "ExternalOutput")
    with ExitStack() as st:
        C = Ctx(nc, st, pfx)
        S = C.S
        make_consts(C)
        identb = C.sb("identb", [128, 128], BF16)
        r_identb = C.R("identb")
        S.op("dve", lambda e: e.tensor_copy(out=identb[:], in_=C.ident[:]), r=[C.r_ident], w=[r_identb])
        tri = C.sb("tri", [128, 128], F32)
        r_tri = C.R("tri")
        S.op("pool", lambda e: e.memset(tri[:], 1.0), w=[r_tri])
        S.op("pool", lambda e: e.affine_select(out=tri[:], in_=tri[:], pattern=[[1, 128]], compare_op=ALU.is_ge, fill=0.0, base=0, channel_multiplier=-1), r=[r_tri], w=[r_tri])
        selA = C.sb("selA", [16, 8, 128], F32)
        selB = C.sb("selB", [16, 8, 128], F32)
        sel = C.sb("sel", [16, 8, 128], BF16)
        r_sel = C.R("sel")
        S.op("pool", lambda e: e.memset(selA[:], 1.0), w=[r_sel])
        S.op("pool", lambda e: e.memset(selB[:], 1.0), w=[r_sel])
        S.op("pool", lambda e: e.affine_select(out=selA[:], in_=selA[:], pattern=[[-1, 8], [0, 128]], compare_op=ALU.is_equal, fill=0.0, base=0, channel_multiplier=1), r=[r_sel], w=[r_sel])
        S.op("pool", lambda e: e.affine_select(out=selB[:], in_=selB[:], pattern=[[-1, 8], [0, 128]], compare_op=ALU.is_equal, fill=0.0, base=-8, channel_multiplier=1), r=[r_sel], w=[r_sel])
        S.op("dve", lambda e: e.tensor_tensor(out=sel[:], in0=selA[:], in1=selB[:], op=ALU.add), r=[r_sel], w=[r_sel])
        maskf = C.sb("maskf", [128, 4, 512], F32)
        maskb = C.sb("maskb", [128, 4, 512], BF16)
        r_mask = C.R("mask")
        S.op("pool", lambda e: e.memset(maskf[:], 0.0), w=[r_mask])
        for j in range(4):
            S.op("pool", lambda e, j=j: e.affine_select(out=maskf[:, j, :], in_=maskf[:, j, :], pattern=[[1, 512]], compare_op=ALU.is_ge, fill=NEG, base=-128 * j, channel_multiplier=-1), r=[r_mask], w=[r_mask])
        S.op("dve", lambda e: e.tensor_copy(out=maskb[:], in_=maskf[:]), r=[r_mask], w=[r_mask])
        PB = [(C.ps("pb%d" % i, [128, 512]), C.R("pb%d" % i, True)) for i in range(7)]
        pbb = C.ps("pbb", [128, 1024], BF16)
        r_pbb = C.R("pbb", True)
        wA = [(C.sb("wA%d" % i, [128, 8, 256], F32), C.R("wA%d" % i)) for i in range(2)]
        mods, r_mods = compute_mods(C, cT_d, adaw_d, adab_d, 2, wA, PB[6][0], PB[6][1])
        cols = C.sb("cols", [128, 8], F32)
        r_cols = C.R("cols")
        S.op("dve", lambda e: e.tensor_scalar(out=cols[:], in0=mods[:, 8:16], scalar1=1.0, scalar2=None, op0=ALU.add), r=[r_mods], w=[r_cols])
        wts = {}
        r_w = C.R("wts")
        for nm, d_ in (("wq", wq_d), ("wk", wk_d), ("wv", wv_d), ("wg", wg_d)):
            wts[nm] = C.sb(nm, [128, 8, 512], BF16)
            S.dma("pool", wts[nm][:], d_.rearrange("(kc p) n -> p kc n", p=128), w=[r_w], key=nm)
        wf = C.sb("wf", [128, 8, 8], BF16)
        S.dma("pool", wf[:], wf_d.rearrange("(kc p) n -> p kc n", p=128), w=[r_w], key="wf")
        bfb = C.sb("bfb", [128, 8], F32)
        qgb = C.sb("qgb", [128, 512], F32)
        kgb = C.sb("kgb", [128, 512], F32)
        r_par = C.R("par")
        S.dma("sp", bfb[:], bf_d, w=[r_par], key="bfb")
        S.dma("sp", qgb[:], qg_d, w=[r_par], key="qgb")
        S.dma("sp", kgb[:], kg_d, w=[r_par], key="kgb")
        xt = C.sb("xt", [128, 4, D], F32)
        r_xt = C.R("xt")
        hT = C.sb("hT", [128, 8, 512], BF16)
        r_hT = [C.R("hT%d" % k) for k in range(8)]
        kT = C.sb("kT", [128, 4, TT], BF16)
        r_kT = [C.R("kT%d" % g) for g in range(NGR)]
        qT = C.sb("qT", [128, 4, 512], BF16)
        r_qT = C.R("qT")
        Va = C.sb("Va", [128, NTL, NH, 65], BF16)
        r_V = [C.R("V%d" % g) for g in range(NGR)]
        r_Vones = C.R("Vones")
        S.op("pool", lambda e: e.memset(Va[:, :, :, 64:65], 1.0), w=[r_Vones])
        eg = C.sb("eg", [64, NH, 512], F32)
        r_eg = [C.R("eg%d" % h) for h in range(NH)]
        Fcol = C.sb("Fcol", [128, NTL, NH], F32)
        r_F = C.R("Fcol")
        carry = C.sb("carry", [128, NTL + 1, NH], F32)
        r_carry = C.R("carry")
        S.op("dve", lambda e: e.memset(carry[:, 0, :], 0.0), w=[r_carry])
        biasG = C.sb("biasG", [128, NTL, NH], F32)
        r_bias = C.R("biasG")
        FqT = C.sb("FqT", [16, 512], BF16)
        r_FqT = C.R("FqT")
        sq = C.sb("sq", [128, 512], F32); r_sq = C.R("sq")
        tmp = C.sb("tmp", [128, 512], F32); r_tmp = C.R("tmp")
        qtok = C.sb("qtok", [128, 512], BF16); r_qtok = C.R("qtok")
        ktok = C.sb("ktok", [128, 512], BF16); r_ktok = C.R("ktok")
        ssq = C.sb("ssq", [128, 16], F32); r_ssq = C.R("ssq")
        zf = C.sb("zf", [128, 8], F32); r_zf = C.R("zf")
        fq = C.sb("fq", [128, 8], F32); r_fq = C.R("fq")
        fq16 = C.sb("fq16", [128, 16], BF16); r_fq16 = C.R("fq16")
        PT = [(C.sb("PT%d" % i, [128, 512], BF16), C.R("PT%d" % i)) for i in range(4)]
        oT = C.sb("oT", [64, 512], F32); r_oT = C.R("oT")
        dn = C.sb("dn", [128, 512], F32); r_dn = C.R("dn")
        wv_ = C.sb("wv_", [64, 512], F32); r_wv_ = C.R("wv_")
        yg = [(C.sb("yg%d" % i, [64, 512], BF16), C.R("yg%d" % i)) for i in range(2)]
        tails = []
        xsrc = x_d.rearrange("(i p) d -> p i d", p=128)
        ps_i = 0
        for G in range(NGR):
            S.dma("sp", xt[:], xsrc[:, 4 * G:4 * G + 4, :], w=[r_xt], key="xt")
            for kc in range(8):
                pb, r_pb = PB[kc % 2]
                for i in range(4):
                    S.op("pe", lambda e, pb=pb, i=i, kc=kc: e.transpose(pb[:, i * 128:(i + 1) * 128], xt[:, i, kc * 128:(kc + 1) * 128], C.ident[:]), r=[r_xt, C.r_ident], w=[r_pb])
                if kc % 2 == 0:
                    S.op("act", lambda e, pb=pb, kc=kc: e.activation(out=hT[:, kc, :], in_=pb[:], func=AF.Identity, scale=cols[:, kc:kc + 1], bias=mods[:, kc:kc + 1]), r=[r_pb, r_cols, r_mods], w=[r_hT[kc]])
                else:
                    S.op("dve", lambda e, pb=pb, kc=kc: e.tensor_scalar(out=hT[:, kc, :], in0=pb[:], scalar1=cols[:, kc:kc + 1], scalar2=mods[:, kc:kc + 1], op0=ALU.mult, op1=ALU.add), r=[r_pb, r_cols, r_mods], w=[r_hT[kc]])
            if debug and G == 0:
                tails.append(S.dma("sp", dbg_h.rearrange("(kc p) t -> p kc t", p=128), hT[:], r=r_hT, key="dbg0"))
            for h in range(NH):
                pb, r_pb = PB[2 + h % 2]
                for kc in range(8):
                    S.op("pe", lambda e, pb=pb, h=h, kc=kc: e.matmul(pb[0:64, :], wts["wg"][:, kc, h * 64:(h + 1) * 64], hT[:, kc, :], start=(kc == 0), stop=(kc == 7)), r=[r_w, r_hT[kc]], w=[r_pb])
                S.op("act", lambda e, pb=pb, h=h: e.activation(out=eg[:, h, :], in_=pb[0:64, :], func=AF.Exp, scale=-1.0), r=[r_pb], w=[r_eg[h]])
            for i in range(4):
                tl_ = 4 * G + i
                (pq, r_pq), (pk, r_pk), (pv, r_pv), (pf, r_pf) = PB[2], PB[3], PB[4], PB[5]
                for kc in range(8):
                    lw = hT[:, kc, i * 128:(i + 1) * 128]
                    S.op("pe", lambda e, lw=lw, kc=kc: e.matmul(pq[:], lw, wts["wq"][:, kc, :], start=(kc == 0), stop=(kc == 7)), r=[r_w, r_hT[kc]], w=[r_pq])
                    S.op("pe", lambda e, lw=lw, kc=kc: e.matmul(pk[:], lw, wts["wk"][:, kc, :], start=(kc == 0), stop=(kc == 7)), r=[r_w, r_hT[kc]], w=[r_pk])
                    S.op("pe", lambda e, lw=lw, kc=kc: e.matmul(pv[:], lw, wts["wv"][:, kc, :], start=(kc == 0), stop=(kc == 7)), r=[r_w, r_hT[kc]], w=[r_pv])
                    S.op("pe", lambda e, lw=lw, kc=kc: e.matmul(pf[:, 0:8], lw, wf[:, kc, :], start=(kc == 0), stop=(kc == 7)), r=[r_w, r_hT[kc]], w=[r_pf])
                for which, (pp, r_pp), gb, tok, r_tok, eps_, mul_ in (("q", (pq, r_pq), qgb, qtok, r_qtok, 64.0 * QK_EPS, 1.0), ("k", (pk, r_pk), kgb, ktok, r_ktok, QK_EPS, 1.0 / 64.0)):
                    o8 = 0 if which == "q" else 8
                    S.op("act", lambda e, pp=pp: e.activation(out=sq[:], in_=pp[:], func=AF.Square), r=[r_pp], w=[r_sq])
                    S.op("dve", lambda e, o8=o8: e.tensor_reduce(out=ssq[:, o8:o8 + 8], in_=sq[:].rearrange("p (h d) -> p h d", d=64), axis=AX.X, op=ALU.add), r=[r_sq], w=[r_ssq])
                    S.op("dve", lambda e, o8=o8, eps_=eps_, mul_=mul_: e.tensor_scalar(out=ssq[:, o8:o8 + 8], in0=ssq[:, o8:o8 + 8], scalar1=mul_, scalar2=eps_, op0=ALU.mult, op1=ALU.add), r=[r_ssq], w=[r_ssq])
                    S.op("act", lambda e, o8=o8: e.activation(out=ssq[:, o8:o8 + 8], in_=ssq[:, o8:o8 + 8], func=AF.Sqrt), r=[r_ssq], w=[r_ssq])
                    S.op("dve", lambda e, o8=o8: e.reciprocal(out=ssq[:, o8:o8 + 8], in_=ssq[:, o8:o8 + 8]), r=[r_ssq], w=[r_ssq])
                    S.op("dve", lambda e, pp=pp, o8=o8: e.tensor_tensor(out=tmp[:].rearrange("p (h d) -> p h d", d=64), in0=pp[:].rearrange("p (h d) -> p h d", d=64), in1=ssq[:, o8:o8 + 8].unsqueeze(2).to_broadcast([128, 8, 64]), op=ALU.mult), r=[r_pp, r_ssq], w=[r_tmp])
                    S.op("dve", lambda e, gb=gb, tok=tok: e.tensor_tensor(out=tok[:], in0=tmp[:], in1=gb[:], op=ALU.mult), r=[r_tmp, r_par], w=[r_tok])
                S.op("act", lambda e, tl_=tl_: e.copy(out=Va[:, tl_, :, 0:64], in_=pv[:].rearrange("p (h d) -> p h d", d=64)), r=[r_pv, r_Vones], w=[r_V[G]])
                S.op("dve", lambda e: e.tensor_tensor(out=zf[:], in0=pf[:, 0:8], in1=bfb[:], op=ALU.add), r=[r_pf, r_par], w=[r_zf])
                S.op("act", lambda e: e.activation(out=zf[:], in_=zf[:], func=AF.Exp, scale=-1.0), r=[r_zf], w=[r_zf])
                S.op("act", lambda e: e.activation(out=zf[:], in_=zf[:], func=AF.Ln, bias=1.0), r=[r_zf], w=[r_zf])
                pc, r_pc = PB[6]
                S.op("pe", lambda e: e.matmul(pc[:, 0:8], tri[:], zf[:], start=True, stop=True), r=[r_tri, r_zf], w=[r_pc])
                S.op("pe", lambda e: e.matmul(pc[:, 8:16], C.ones[:], zf[:], start=True, stop=True), r=[C.r_ones, r_zf], w=[r_pc])
                S.op("dve", lambda e, tl_=tl_: e.tensor_tensor(out=Fcol[:, tl_, :], in0=carry[:, tl_, :], in1=pc[:, 0:8], op=ALU.subtract), r=[r_carry, r_pc], w=[r_F])
                S.op("dve", lambda e, tl_=tl_: e.tensor_tensor(out=carry[:, tl_ + 1, :], in0=carry[:, tl_, :], in1=pc[:, 8:16], op=ALU.subtract), r=[r_carry, r_pc], w=[r_carry])
                S.op("dve", lambda e, tl_=tl_, G=G: e.tensor_tensor(out=fq[:], in0=Fcol[:, tl_, :], in1=carry[:, 4 * G, :], op=ALU.subtract), r=[r_F, r_carry], w=[r_fq])
                S.op("dve", lambda e: e.tensor_copy(out=fq16[:, 0:8], in_=fq[:]), r=[r_fq], w=[r_fq16])
                S.op("dve", lambda e: e.tensor_tensor(out=fq[:], in0=fq[:], in1=fq16[:, 0:8], op=ALU.subtract), r=[r_fq, r_fq16], w=[r_fq])
                S.op("dve", lambda e: e.tensor_copy(out=fq16[:, 8:16], in_=fq[:]), r=[r_fq], w=[r_fq16])
                for p in range(4):
                    S.op("pe", lambda e, p=p: e.transpose(pbb[:, p * 128:(p + 1) * 128], qtok[:, p * 128:(p + 1) * 128], identb[:]), r=[r_qtok, r_identb], w=[r_pbb])
                for p in range(4):
                    S.op("pe", lambda e, p=p: e.transpose(pbb[:, 512 + p * 128:512 + (p + 1) * 128], ktok[:, p * 128:(p + 1) * 128], identb[:]), r=[r_ktok, r_identb], w=[r_pbb])
                S.op("act", lambda e, i=i: e.copy(out=qT[:, :, i * 128:(i + 1) * 128], in_=pbb[:, 0:512].rearrange("p (a t) -> p a t", t=128)), r=[r_pbb], w=[r_qT])
                S.op("dve", lambda e, tl_=tl_: e.tensor_copy(out=kT[:, :, tl_ * 128:(tl_ + 1) * 128], in_=pbb[:, 512:1024].rearrange("p (a t) -> p a t", t=128)), r=[r_pbb], w=[r_kT[G]])
                S.op("pe", lambda e: e.transpose(pbb[0:16, 0:128], fq16[:, 0:16], identb[:]), r=[r_fq16, r_identb], w=[r_pbb])
                S.op("dve", lambda e, i=i: e.tensor_copy(out=FqT[:, i * 128:(i + 1) * 128], in_=pbb[0:16, 0:128]), r=[r_pbb], w=[r_FqT])
            nkb = 4 * G + 4
            for h in range(NH):
                S.op("dve", lambda e, h=h, nkb=nkb, G=G: e.tensor_scalar(out=biasG[:, 0:nkb, h], in0=Fcol[:, 0:nkb, h], scalar1=-1.0, scalar2=carry[:, 4 * G, h:h + 1], op0=ALU.mult, op1=ALU.add), r=[r_F, r_carry], w=[r_bias])
            tiles = [(2 * p + e, kb) for p in range(NH // 2) for kb in range(nkb) for e in range(2)]

            def emit_pv(h, kb, PTt, r_PT, nkb=nkb, G=G):
                pO, r_pO = PB[4 + h % 2]
                gk = kb // 4
                S.op("pe", lambda e, pO=pO, PTt=PTt, kb=kb, h=h, nkb=nkb: e.matmul(pO[0:65, :], Va[:, kb, h, :], PTt[:], start=(kb == 0), stop=(kb == nkb - 1)), r=[r_V[gk], r_Vones, r_PT], w=[r_pO])
                if kb == nkb - 1:
                    pB_, r_pB = PB[6]
                    ygt, r_yg = yg[h % 2]
                    S.op("act", lambda e, pO=pO: e.copy(out=oT[:], in_=pO[0:64, :]), r=[r_pO], w=[r_oT])
                    S.op("dve", lambda e, pO=pO: e.tensor_copy(out=dn[64:65, :], in_=pO[64:65, :]), r=[r_pO], w=[r_dn])
                    S.op("pe", lambda e: e.matmul(pB_[0:64, :], C.ones[64:65, 0:64], dn[64:65, :], start=True, stop=True), r=[C.r_ones, r_dn], w=[r_pB])
                    S.op("dve", lambda e, h=h: e.scalar_tensor_tensor(out=wv_[:], in0=eg[:, h, :], scalar=1.0, in1=pB_[0:64, :], op0=ALU.add, op1=ALU.mult), r=[r_eg[h], r_pB], w=[r_wv_])
                    S.op("dve", lambda e: e.reciprocal(out=wv_[:], in_=wv_[:]), r=[r_wv_], w=[r_wv_])
                    S.op("dve", lambda e, ygt=ygt: e.tensor_tensor(out=ygt[:], in0=oT[:], in1=wv_[:], op=ALU.mult), r=[r_oT, r_wv_], w=[r_yg])
                    tails.append(S.dma("sp", yg_d[h * 64:(h + 1) * 64, G * 512:(G + 1) * 512], ygt[:], r=[r_yg], key="yg%d" % (h % 2)))

            pendq = []
            for (h, kb) in tiles:
                p_, e_ = h // 2, h % 2
                pS, r_pS = PB[ps_i % 4]
                PTt, r_PT = PT[ps_i % 4]
                ps_i += 1
                diag = kb >= 4 * G
                gk = kb // 4
                if FOX_DVE:
                    fqb, r_fqb = ((tmp, r_tmp), (sq, r_sq))[h % 2]
                    if kb == 0:
                        pB_, r_pB = PB[6]
                        S.op("pe", lambda e, h=h: e.matmul(pB_[:], sel[:, h, :], FqT[:], start=True, stop=True), r=[r_sel, r_FqT], w=[r_pB])
                        S.op("dve", lambda e, fqb=fqb: e.tensor_copy(out=fqb[:], in_=pB_[:]), r=[r_pB], w=[r_fqb])
                    S.op("pe", lambda e, pS=pS, p_=p_, e_=e_, kb=kb, diag=diag: e.matmul(pS[:], kT[e_ * 64:(e_ + 1) * 64, p_, kb * 128:(kb + 1) * 128], qT[e_ * 64:(e_ + 1) * 64, p_, :], start=True, stop=(not diag)), r=[r_kT[gk], r_qT], w=[r_pS])
                    if diag:
                        S.op("pe", lambda e, pS=pS, kb=kb, G=G: e.matmul(pS[:], identb[:], maskb[:, kb - 4 * G, :], start=False, stop=True), r=[r_identb, r_mask], w=[r_pS])
                    S.op("dve", lambda e, pS=pS, fqb=fqb: e.tensor_tensor(out=pS[:], in0=pS[:], in1=fqb[:], op=ALU.add), r=[r_pS, r_fqb], w=[r_pS])
                else:
                    S.op("pe", lambda e, pS=pS, p_=p_, e_=e_, kb=kb: e.matmul(pS[:], kT[e_ * 64:(e_ + 1) * 64, p_, kb * 128:(kb + 1) * 128], qT[e_ * 64:(e_ + 1) * 64, p_, :], start=True, stop=False), r=[r_kT[gk], r_qT], w=[r_pS])
                    S.op("pe", lambda e, pS=pS, h=h, diag=diag: e.matmul(pS[:], sel[:, h, :], FqT[:], start=False, stop=(not diag)), r=[r_sel, r_FqT], w=[r_pS])
                    if diag:
                        S.op("pe", lambda e, pS=pS, kb=kb, G=G: e.matmul(pS[:], identb[:], maskb[:, kb - 4 * G, :], start=False, stop=True), r=[r_identb, r_mask], w=[r_pS])
                S.op("act", lambda e, pS=pS, PTt=PTt, kb=kb, h=h: e.activation(out=PTt[:], in_=pS[:], func=AF.Exp, bias=biasG[:, kb, h:h + 1], scale=1.0), r=[r_pS, r_bias], w=[r_PT])
                pendq.append((h, kb, PTt, r_PT))
                if h % 2 == 1:
                    while len(pendq) > 2:
                        emit_pv(*pendq.pop(0))
            while pendq:
                emit_pv(*pendq.pop(0))
            if debug and G == 0:
                tails.append(S.dma("sp", dbg_q, qT[:], r=[r_qT], key="dbg1"))
        if debug:
            tails.append(S.dma("sp", dbg_k, kT[:], r=r_kT, key="dbg2"))
            tails.append(S.dma("sp", dbg_F, Fcol[:], r=[r_F], key="dbg3"))
            tails.append(S.dma("sp", dbg_v, Va[:], r=r_V + [r_Vones], key="dbg4"))
        if after is not None:
            after(C, tails)
        S.emit(st, tail_waits=tails)

import math, os
SEQ_VAR = 0

NH = 8
HD = 64
C0 = math.exp(-0.5)
GN_EPS = 64 * 1e-5


class TR:
    def __init__(self, C, name, shape, dt, psum=False):
        self.t = C.ps(name, shape, dt) if psum else C.sb(name, shape, dt)
        self.r = C.R(name, excl=psum)


class _Stop(Exception):
    pass


def build_rwkv(TT=4096, debug=False, stop=None):
    nc = bass.Bass("TRN2", target_bir_lowering=False)
    dt = lambda n, s, d, k="ExternalInput": nc.dram_tensor(n, list(s), d, kind=k).ap()
    xT_d = dt("xT", [D, TT], F32)
    ins = rwkv_decl(dt)
    yT_d = dt("yT", [512, TT], BF16, "ExternalOutput")
    xv = xT_d.rearrange("(kc p) t -> p kc t", p=128)
    rwkv_phase(nc, "", lambda t0, n: xv[:, :, t0:t0 + n], ins, yT_d, TT, debug=debug, stop=stop)
    return nc


RW_BC = ["w0b", "a0b", "kkb", "kab", "rkb", "lngb", "lnbb"]


def rwkv_decl(dt, sfx=""):
    d_ = dict(cT=dt("cT" + sfx, [128, 8], F32), adaw=dt("adaw" + sfx, [D, 2 * D], F32), adab=dt("adab" + sfx, [128, 16], F32), mu=dt("mu" + sfx, [128, 48], F32),
              wr=dt("wr" + sfx, [D, 512], F32), wk=dt("wk" + sfx, [D, 512], F32), wv=dt("wv" + sfx, [D, 512], F32),
              w1=dt("w1" + sfx, [D, 64], F32), a1=dt("a1" + sfx, [D, 64], F32), g1=dt("g1" + sfx, [D, 128], F32),
              w2=dt("w2" + sfx, [64, 512], F32), a2=dt("a2" + sfx, [64, 512], F32), g2=dt("g2" + sfx, [128, 512], F32))
    for n in RW_BC:
        d_[n] = dt(n + sfx, [128, 512], F32)
    return d_


def rwkv_phase(nc, pfx, xsrc_fn, ins, yT_d, TT=4096, debug=False, stop=None, after=None):
    dt = lambda n, s, d, k="ExternalInput": nc.dram_tensor(n, list(s), d, kind=k).ap()
    cT_d, adaw_d, adab_d, mu_d = ins["cT"], ins["adaw"], ins["adab"], ins["mu"]
    wr_d, wk_d, wv_d, w1_d, a1_d, g1_d, w2_d, a2_d, g2_d = [ins[k] for k in ("wr", "wk", "wv", "w1", "a1", "g1", "w2", "a2", "g2")]
    bc_names = RW_BC
    bc_d = {n: ins[n] for n in bc_names}
    GS = 256
    NTG = GS // 128
    NGR = TT // GS
    dbg = {}
    if debug:
        for n in ["r", "kp", "kkn", "a", "sw", "Y", "bonus", "Gm", "yn"]:
            dbg[n] = dt("dbg_" + n, [128, 512], F32, "ExternalOutput")
        dbg["S"] = dt("dbg_S", [128, 4, 64], F32, "ExternalOutput")
        for n, shp in (("Arb", [128, 8, 128]), ("Aak", [128, 8, 128]), ("NT0", [128, 8, 128]), ("Z0", [128, 8, 128]), ("Zf", [128, 8, 128]), ("Y0", [128, 512]), ("RbT", [128, 4, 128]), ("Qm", [128, 4, 128]), ("Hm", [128, 4, 128]), ("gcol", [128, 8]), ("v", [128, 512]), ("ART", [128, 4, 2, 128]), ("KT", [128, 4, 128]), ("BT", [128, 4, 128]), ("Bh", [128, 512]), ("Kh", [128, 512])):
            dbg[n] = dt("dbg_" + n, shp, F32, "ExternalOutput")
    with ExitStack() as st:
        C = Ctx(nc, st, pfx)
        S = C.S
        make_consts(C)
        ident, ones = C.ident, C.ones
        r_ident, r_ones = C.r_ident, C.r_ones
        tails = []
        identb = TR(C, "identb", [128, 128], BF16)
        S.op("dve", lambda e: e.tensor_copy(out=identb.t[:], in_=ident[:]), r=[r_ident], w=[identb.r])
        U = TR(C, "U", [128, 128], F32)
        IU = TR(C, "IUf", [128, 128], F32)
        S.op("pool", lambda e: e.memset(U.t[:], 1.0), w=[U.r])
        S.op("pool", lambda e: e.affine_select(out=U.t[:], in_=U.t[:], pattern=[[1, 128]], compare_op=ALU.is_gt, fill=0.0, base=0, channel_multiplier=-1), r=[U.r], w=[U.r])
        S.op("pool", lambda e: e.memset(IU.t[:], 1.0), w=[IU.r])
        S.op("pool", lambda e: e.affine_select(out=IU.t[:], in_=IU.t[:], pattern=[[1, 128]], compare_op=ALU.is_ge, fill=0.0, base=0, channel_multiplier=-1), r=[IU.r], w=[IU.r])
        SLf = TR(C, "SLf", [128, 128], F32)
        S.op("pool", lambda e: e.memset(SLf.t[:], 1.0), w=[SLf.r])
        S.op("pool", lambda e: e.affine_select(out=SLf.t[:], in_=SLf.t[:], pattern=[[-1, 128]], compare_op=ALU.is_gt, fill=0.0, base=0, channel_multiplier=1), r=[SLf.r], w=[SLf.r])
        mask1 = TR(C, "mask1", [128, 128], F32)
        maskL = TR(C, "maskL", [128, 64], F32)
        idl = TR(C, "idl", [128, 64], F32)
        for ch in range(2):
            ps_ = slice(ch * 64, ch * 64 + 64)
            S.op("dve", lambda e, ps_=ps_: e.tensor_copy(out=mask1.t[ps_, 0:64], in_=U.t[ps_, ps_]), r=[U.r], w=[mask1.r])
            S.op("dve", lambda e, ps_=ps_: e.tensor_copy(out=mask1.t[ps_, 64:128], in_=IU.t[ps_, ps_]), r=[IU.r], w=[mask1.r])
            S.op("dve", lambda e, ps_=ps_: e.tensor_copy(out=maskL.t[ps_, :], in_=SLf.t[ps_, ps_]), r=[SLf.r], w=[maskL.r])
            S.op("dve", lambda e, ps_=ps_: e.tensor_copy(out=idl.t[ps_, :], in_=ident[ps_, ps_]), r=[r_ident], w=[idl.r])
        triC = TR(C, "triC", [128, 128], F32)
        blkC = TR(C, "blkC", [128, 128], F32)
        S.op("dve", lambda e: e.tensor_scalar(out=triC.t[:], in0=IU.t[:], scalar1=-C0, scalar2=None, op0=ALU.mult), r=[IU.r], w=[triC.r])
        S.op("dve", lambda e: e.memset(triC.t[0:64, 64:128], 0.0), w=[triC.r])
        S.op("dve", lambda e: e.memset(blkC.t[:], -C0), w=[blkC.r])
        S.op("dve", lambda e: e.memset(blkC.t[0:64, 64:128], 0.0), w=[blkC.r])
        S.op("dve", lambda e: e.memset(blkC.t[64:128, 0:64], 0.0), w=[blkC.r])
        avg = TR(C, "avg", [128, 1], F32)
        S.op("dve", lambda e: e.memset(avg.t[:], 1.0 / 64.0), w=[avg.r])
        PB = [TR(C, "pb%d" % i, [128, 512], F32, psum=True) for i in range(7)]
        pbb = TR(C, "pbb", [128, 1024], BF16, psum=True)
        wA = [(C.sb("wA%d" % i, [128, 8, 256], F32), C.R("wA%d" % i)) for i in range(2)]
        mods, r_mods = compute_mods(C, cT_d, adaw_d, adab_d, 2, wA, PB[6].t, PB[6].r)
        cols = TR(C, "cols", [128, 8], F32)
        S.op("dve", lambda e: e.tensor_scalar(out=cols.t[:], in0=mods[:, 8:16], scalar1=1.0, scalar2=None, op0=ALU.add), r=[r_mods], w=[cols.r])
        mu = TR(C, "mu", [128, 48], F32)
        S.dma("sp", mu.t[:], mu_d, w=[mu.r], key="mu")
        W = {}
        r_w = C.R("wts")
        for nm, d_, shp in (("wr", wr_d, [128, 8, 512]), ("wk", wk_d, [128, 8, 512]), ("wv", wv_d, [128, 8, 512]), ("w1", w1_d, [128, 8, 64]), ("a1", a1_d, [128, 8, 64]), ("g1", g1_d, [128, 8, 128])):
            W[nm] = C.sb(nm, shp, BF16)
            S.dma("pool", W[nm][:], d_.rearrange("(kc p) n -> p kc n", p=128), w=[r_w], key=nm)
        for nm, d_, shp in (("w2", w2_d, [64, 512]), ("a2", a2_d, [64, 512]), ("g2", g2_d, [128, 512])):
            W[nm] = C.sb(nm, shp, BF16)
            S.dma("pool", W[nm][:], d_, w=[r_w], key=nm)
        BC = {}
        r_bc = C.R("bc")
        for n in bc_names:
            BC[n] = C.sb(n, [128, 512], F32)
            S.dma("sp", BC[n][:], bc_d[n], w=[r_bc], key=n)
        xg = TR(C, "xg", [128, 8, GS + 1], F32)
        S.op("dve", lambda e: e.memset(xg.t[:, :, 0:1], 0.0), w=[xg.r])
        dx = TR(C, "dx", [128, 8, GS], F32)
        xs = [TR(C, "xs%d" % i, [128, 8, GS], BF16) for i in range(2)]
        l1 = [TR(C, "l1_%d" % i, [128, GS], BF16) for i in range(3)]
        l1f = TR(C, "l1f", [128, GS], F32)
        G_ = {n: TR(C, "G_" + n, [128, NTG, 512], F32) for n in ("r", "k", "v")}
        names = ["sw", "Gm", "Gi", "Gp", "GC", "a", "kkn", "kp", "t1", "t2", "Bt", "Kt", "Bh", "Kh", "bonus", "Y", "yn", "gg", "At"]
        X = {n: TR(C, "X_" + n, [128, 512], F32) for n in names}
        sm = TR(C, "sm", [128, 64], F32)
        Z = [TR(C, "Z%d" % i, [128, 8, 128], BF16) for i in range(2)]
        Zf32 = TR(C, "Zf32", [128, 8, 128], F32)
        Rt = TR(C, "Rt", [128, 512], F32)
        ART = TR(C, "ART", [128, 4, 2, 128], F32)
        KT = TR(C, "KT", [128, 4, 128], F32)
        BT = TR(C, "BT", [128, 4, 128], F32)
        Nm = [TR(C, "Nm%d" % i, [128, 8, 128], BF16) for i in range(2)]
        NTm = [TR(C, "NTm%d" % i, [128, 8, 128], BF16) for i in range(2)]
        for t_ in Nm + NTm:
            S.op("pool", lambda e, t_=t_: e.memset(t_.t[:], 0.0), w=[t_.r])
        Arb = TR(C, "Arb", [128, 8, 128], F32)
        Aak = TR(C, "Aak", [128, 8, 128], F32)
        RbT = TR(C, "RbT", [128, 4, 128], F32)
        Qm = TR(C, "Qm", [128, 4, 128], F32)
        Hm = TR(C, "Hm", [128, 4, 128], F32)
        gcol = TR(C, "gcol", [128, 8], F32)
        Y0 = TR(C, "Y0", [128, 512], F32)
        ST = [TR(C, "ST%d" % i, [128, 4, 64], F32) for i in range(2)]
        S.op("dve", lambda e: e.memset(ST[0].t[:], 0.0), w=[ST[0].r])
        ybf = TR(C, "ybf", [128, 512], BF16)
        yTo = [TR(C, "yTo%d" % i, [128, 4, 128], BF16) for i in range(2)]
        ysrc = yT_d.rearrange("(a p) t -> p a t", p=128)

        def hv(t):
            return t.rearrange("p (h d) -> p h d", d=64)

        def bc8(ap):
            return ap.unsqueeze(2).to_broadcast([128, 8, 64])

        st_i = 0
        try:
          for G in range(NGR):
              if G > 0:
                  S.op("dve", lambda e: e.tensor_copy(out=xg.t[:, :, 0:1], in_=xg.t[:, :, GS:GS + 1]), r=[xg.r], w=[xg.r])
              S.dma("sp", xg.t[:, :, 1:GS + 1], xsrc_fn(G * GS, GS), r=[xg.r], w=[xg.r], key="xg")
              for kc in range(8):
                  S.op("dve", lambda e, kc=kc: e.tensor_scalar(out=xg.t[:, kc, 1:GS + 1], in0=xg.t[:, kc, 1:GS + 1], scalar1=cols.t[:, kc:kc + 1], scalar2=mods[:, kc:kc + 1], op0=ALU.mult, op1=ALU.add), r=[xg.r, cols.r, r_mods], w=[xg.r])
              S.op("dve", lambda e: e.tensor_tensor(out=dx.t[:], in0=xg.t[:, :, 0:GS], in1=xg.t[:, :, 1:GS + 1], op=ALU.subtract), r=[xg.r], w=[dx.r])
              for n, nm in enumerate(("r", "k", "v", "w", "a", "g")):
                  xb = xs[n % 2]
                  for kc in range(8):
                      eng = "dve"
                      S.op(eng, lambda e, kc=kc, n=n, xb=xb: e.scalar_tensor_tensor(out=xb.t[:, kc, :], in0=dx.t[:, kc, :], scalar=mu.t[:, n * 8 + kc:n * 8 + kc + 1], in1=xg.t[:, kc, 1:GS + 1], op0=ALU.mult, op1=ALU.add), r=[dx.r, mu.r, xg.r], w=[xb.r])
                  if n < 3:
                      wt = W[("wr", "wk", "wv")[n]]
                      for i in range(NTG):
                          pb = PB[(n * NTG + i) % 2]
                          for kc in range(8):
                              S.op("pe", lambda e, pb=pb, xb=xb, wt=wt, i=i, kc=kc: e.matmul(pb.t[:], xb.t[:, kc, i * 128:(i + 1) * 128], wt[:, kc, :], start=(kc == 0), stop=(kc == 7)), r=[xb.r, r_w], w=[pb.r])
                          S.op("act", lambda e, pb=pb, nm=nm, i=i: e.copy(out=G_[nm].t[:, i, :], in_=pb.t[:]), r=[pb.r], w=[G_[nm].r])
                  else:
                      w1n, w2n, rows = (("w1", "w2", 64), ("a1", "a2", 64), ("g1", "g2", 128))[n - 3]
                      pb = PB[2]
                      for kc in range(8):
                          S.op("pe", lambda e, pb=pb, xb=xb, w1n=w1n, rows=rows, kc=kc: e.matmul(pb.t[0:rows, 0:GS], W[w1n][:, kc, :], xb.t[:, kc, :], start=(kc == 0), stop=(kc == 7)), r=[xb.r, r_w], w=[pb.r])
                      lt = l1[n - 3]
                      if nm == "w":
                          S.op("act", lambda e, pb=pb: e.activation(out=l1f.t[0:64, :], in_=pb.t[0:64, 0:GS], func=AF.Exp, scale=-2.0), r=[pb.r, l1f.r], w=[l1f.r])
                          S.op("dve", lambda e: e.tensor_scalar(out=l1f.t[0:64, :], in0=l1f.t[0:64, :], scalar1=1.0, scalar2=None, op0=ALU.add), r=[l1f.r], w=[l1f.r])
                          S.op("dve", lambda e: e.reciprocal(out=l1f.t[0:64, :], in_=l1f.t[0:64, :]), r=[l1f.r], w=[l1f.r])
                          S.op("dve", lambda e, lt=lt: e.tensor_scalar(out=lt.t[0:64, :], in0=l1f.t[0:64, :], scalar1=2.0, scalar2=-1.0, op0=ALU.mult, op1=ALU.add), r=[l1f.r], w=[lt.r])
                      elif nm == "a":
                          S.op("act", lambda e, pb=pb, lt=lt: e.copy(out=lt.t[0:64, :], in_=pb.t[0:64, 0:GS]), r=[pb.r], w=[lt.r])
                      else:
                          S.op("act", lambda e, pb=pb: e.activation(out=l1f.t[:], in_=pb.t[:, 0:GS], func=AF.Exp, scale=-1.0), r=[pb.r, l1f.r], w=[l1f.r])
                          S.op("dve", lambda e: e.tensor_scalar(out=l1f.t[:], in0=l1f.t[:], scalar1=1.0, scalar2=None, op0=ALU.add), r=[l1f.r], w=[l1f.r])
                          S.op("dve", lambda e: e.reciprocal(out=l1f.t[:], in_=l1f.t[:]), r=[l1f.r], w=[l1f.r])
                          S.op("act", lambda e, lt=lt: e.copy(out=lt.t[:], in_=l1f.t[:]), r=[l1f.r], w=[lt.r])
              if stop == 1:
                  raise _Stop()
              for i in range(NTG):
                  tl_ = G * NTG + i
                  r_ = G_["r"].t[:, i, :]; k_ = G_["k"].t[:, i, :]; v_ = G_["v"].t[:, i, :]
                  rG = [G_[n].r for n in G_]
                  sw, Gm, Gi, Gp, GC, a_, kkn, kp, t1, t2 = [X[n] for n in ("sw", "Gm", "Gi", "Gp", "GC", "a", "kkn", "kp", "t1", "t2")]
                  Bt, Kt, Bh, Kh, bonus, Yt, yn = [X[n] for n in ("Bt", "Kt", "Bh", "Kh", "bonus", "Y", "yn")]
                  pzw, pza, pgg = PB[4], PB[5], PB[6]
                  for (pz_, lt, w2n, rows) in ((pzw, l1[0], "w2", 64), (pza, l1[1], "a2", 64), (pgg, l1[2], "g2", 128)):
                      S.op("pe", lambda e, pz_=pz_, lt=lt, w2n=w2n, rows=rows, i=i: e.matmul(pz_.t[:], lt.t[0:rows, i * 128:(i + 1) * 128], W[w2n][0:rows, :], start=True, stop=True), r=[lt.r, r_w], w=[pz_.r])
                  S.op("act", lambda e: e.copy(out=X["gg"].t[:], in_=pgg.t[:]), r=[pgg.r], w=[X["gg"].r])
                  S.op("dve", lambda e: e.tensor_tensor(out=t1.t[:], in0=pzw.t[:], in1=BC["w0b"][:], op=ALU.add), r=[pzw.r, r_bc], w=[t1.r])
                  S.op("dve", lambda e: e.tensor_tensor(out=t2.t[:], in0=pza.t[:], in1=BC["a0b"][:], op=ALU.add), r=[pza.r, r_bc, t2.r], w=[t2.r])
                  S.op("act", lambda e: e.activation(out=sw.t[:], in_=t1.t[:], func=AF.Sigmoid), r=[t1.r], w=[sw.r])
                  S.op("act", lambda e: e.activation(out=a_.t[:], in_=t2.t[:], func=AF.Sigmoid), r=[t2.r], w=[a_.r])
                  pL, pLC = PB[2], PB[3]
                  S.op("pe", lambda e: e.matmul(pL.t[:], triC.t[:], sw.t[:], start=True, stop=True), r=[triC.r, sw.r], w=[pL.r])
                  S.op("pe", lambda e: e.matmul(pLC.t[:], blkC.t[:], sw.t[:], start=True, stop=True), r=[blkC.r, sw.r], w=[pLC.r])
                  S.op("act", lambda e: e.activation(out=Gm.t[:], in_=pL.t[:], func=AF.Exp), r=[pL.r], w=[Gm.r])
                  S.op("act", lambda e: e.activation(out=Gi.t[:], in_=pL.t[:], func=AF.Exp, scale=-1.0), r=[pL.r], w=[Gi.r])
                  S.op("dve", lambda e: e.scalar_tensor_tensor(out=Gp.t[:], in0=sw.t[:], scalar=C0, in1=pL.t[:], op0=ALU.mult, op1=ALU.add), r=[sw.r, pL.r], w=[Gp.r])
                  S.op("act", lambda e: e.activation(out=Gp.t[:], in_=Gp.t[:], func=AF.Exp), r=[Gp.r], w=[Gp.r])
                  S.op("act", lambda e: e.activation(out=GC.t[:], in_=pLC.t[:], func=AF.Exp), r=[pLC.r], w=[GC.r])
                  S.op("dve", lambda e, k_=k_: e.tensor_tensor(out=kkn.t[:], in0=k_, in1=BC["kkb"][:], op=ALU.mult), r=rG + [r_bc], w=[kkn.r])
                  S.op("act", lambda e: e.activation(out=t2.t[:], in_=kkn.t[:], func=AF.Square), r=[kkn.r], w=[t2.r])
                  S.op("dve", lambda e: e.tensor_reduce(out=sm.t[:, 0:8], in_=hv(t2.t[:]), axis=AX.X, op=ALU.add), r=[t2.r], w=[sm.r])
                  S.op("dve", lambda e: e.tensor_scalar(out=sm.t[:, 0:8], in0=sm.t[:, 0:8], scalar1=1e-24, scalar2=None, op0=ALU.max), r=[sm.r], w=[sm.r])
                  S.op("act", lambda e: e.activation(out=sm.t[:, 0:8], in_=sm.t[:, 0:8], func=AF.Sqrt), r=[sm.r], w=[sm.r])
                  S.op("dve", lambda e: e.reciprocal(out=sm.t[:, 0:8], in_=sm.t[:, 0:8]), r=[sm.r], w=[sm.r])
                  S.op("dve", lambda e: e.tensor_tensor(out=hv(kkn.t[:]), in0=hv(kkn.t[:]), in1=bc8(sm.t[:, 0:8]), op=ALU.mult), r=[kkn.r, sm.r], w=[kkn.r])
                  S.op("dve", lambda e: e.scalar_tensor_tensor(out=t2.t[:], in0=a_.t[:], scalar=-1.0, in1=BC["kab"][:], op0=ALU.add, op1=ALU.mult), r=[a_.r, r_bc, t2.r], w=[t2.r])
                  S.op("dve", lambda e, k_=k_: e.scalar_tensor_tensor(out=kp.t[:], in0=t2.t[:], scalar=1.0, in1=k_, op0=ALU.add, op1=ALU.mult), r=[t2.r] + rG, w=[kp.r])
                  S.op("dve", lambda e: e.scalar_tensor_tensor(out=X["At"].t[:], in0=kkn.t[:], scalar=-1.0, in1=Gp.t[:], op0=ALU.mult, op1=ALU.mult), r=[kkn.r, Gp.r], w=[X["At"].r])
                  S.op("pool", lambda e: e.tensor_copy(out=Z[0].t[:, :, 0:64], in_=hv(X["At"].t[:])), r=[X["At"].r], w=[Z[0].r])
                  S.op("pool", lambda e: e.tensor_tensor(out=Bt.t[:], in0=kkn.t[:], in1=a_.t[:], op=ALU.mult), r=[kkn.r, a_.r], w=[Bt.r])
                  S.op("pool", lambda e: e.tensor_tensor(out=Bt.t[:], in0=Bt.t[:], in1=Gi.t[:], op=ALU.mult), r=[Bt.r, Gi.r], w=[Bt.r])
                  S.op("dve", lambda e: e.tensor_tensor(out=Kt.t[:], in0=kp.t[:], in1=Gi.t[:], op=ALU.mult), r=[kp.r, Gi.r], w=[Kt.r])
                  S.op("dve", lambda e, r_=r_: e.tensor_tensor(out=Rt.t[:], in0=r_, in1=Gm.t[:], op=ALU.mult), r=rG + [Gm.r], w=[Rt.r])
                  S.op("pool", lambda e: e.tensor_tensor(out=Bh.t[:], in0=Bt.t[:], in1=GC.t[:], op=ALU.mult), r=[Bt.r, GC.r], w=[Bh.r])
                  S.op("pool", lambda e: e.tensor_tensor(out=Kh.t[:], in0=Kt.t[:], in1=GC.t[:], op=ALU.mult), r=[Kt.r, GC.r], w=[Kh.r])
                  S.op("dve", lambda e, r_=r_: e.tensor_tensor(out=t2.t[:], in0=r_, in1=kp.t[:], op=ALU.mult), r=rG + [kp.r, t2.r], w=[t2.r])
                  S.op("dve", lambda e: e.tensor_tensor(out=t2.t[:], in0=t2.t[:], in1=BC["rkb"][:], op=ALU.mult), r=[t2.r, r_bc], w=[t2.r])
                  S.op("dve", lambda e: e.tensor_reduce(out=sm.t[:, 8:16], in_=hv(t2.t[:]), axis=AX.X, op=ALU.add), r=[t2.r], w=[sm.r])
                  S.op("dve", lambda e, v_=v_: e.tensor_tensor(out=hv(bonus.t[:]), in0=hv(v_), in1=bc8(sm.t[:, 8:16]), op=ALU.mult), r=rG + [sm.r], w=[bonus.r])
                  if stop == 2:
                      raise _Stop()
                  for p in range(4):
                      pT = PB[p % 2]
                      for q_, src in enumerate((X["At"], Rt, Kt, Bt)):
                          S.op("pe", lambda e, pT=pT, p=p, q_=q_, src=src: e.transpose(pT.t[:, q_ * 128:(q_ + 1) * 128], src.t[:, p * 128:(p + 1) * 128], ident[:]), r=[src.r, r_ident], w=[pT.r])
                      S.op("dve", lambda e, pT=pT, p=p: e.tensor_copy(out=ART.t[:, p, :, 0:64], in_=pT.t[:, 0:128].rearrange("p (c t) -> p c t", t=64)), r=[pT.r], w=[ART.r])
                      S.op("act", lambda e, pT=pT, p=p: e.copy(out=ART.t[:, p, :, 64:128], in_=pT.t[:, 128:256].rearrange("p (c t) -> p c t", t=64)), r=[pT.r], w=[ART.r])
                      S.op("act", lambda e, pT=pT, p=p: e.copy(out=KT.t[:, p, :], in_=pT.t[:, 256:384]), r=[pT.r], w=[KT.r])
                      S.op("dve", lambda e, pT=pT, p=p: e.tensor_copy(out=BT.t[:, p, :], in_=pT.t[:, 384:512]), r=[pT.r], w=[BT.r])
                  if stop == 3:
                      raise _Stop()
                  pS1a, pS1b, pS2a, pS2b, pS3 = PB[0], PB[1], PB[2], PB[3], PB[4]
                  for h in range(8):
                      p, e_ = h // 2, h % 2
                      fe = slice(e_ * 64, e_ * 64 + 64)
                      pa, pb_ = (pS1a, pS2a) if h < 4 else (pS1b, pS2b)
                      hc = (h % 4) * 128
                      for ch in range(2):
                          tc_ = slice(ch * 64, ch * 64 + 64)
                          S.op("pe", lambda e, pa=pa, fe=fe, p=p, ch=ch, tc_=tc_, hc=hc: e.matmul(pa.t[tc_, hc:hc + 128], BT.t[fe, p, tc_], ART.t[fe, p, ch, :], start=True, stop=True), r=[BT.r, ART.r], w=[pa.r])
                          S.op("pe", lambda e, pb_=pb_, fe=fe, p=p, ch=ch, tc_=tc_, hc=hc: e.matmul(pb_.t[tc_, hc:hc + 128], KT.t[fe, p, tc_], ART.t[fe, p, ch, :], start=True, stop=True), r=[KT.r, ART.r], w=[pb_.r])
                          S.op("pe", lambda e, fe=fe, p=p, ch=ch, tc_=tc_, h=h: e.matmul(pS3.t[tc_, h * 64:(h + 1) * 64], ART.t[fe, p, ch, 0:64], BT.t[fe, p, tc_], start=True, stop=True), r=[BT.r, ART.r], w=[pS3.r])
                  m1b = mask1.t[:].unsqueeze(1).to_broadcast([128, 4, 128])
                  for half, (pa, pb_) in enumerate(((pS1a, pS2a), (pS1b, pS2b))):
                      hs = slice(half * 4, half * 4 + 4)
                      S.op("dve", lambda e, pa=pa, hs=hs: e.tensor_tensor(out=Arb.t[:, hs, :], in0=pa.t[:].rearrange("p (h t) -> p h t", t=128), in1=m1b, op=ALU.mult), r=[pa.r, mask1.r], w=[Arb.r])
                      S.op("dve", lambda e, pb_=pb_, hs=hs: e.tensor_tensor(out=Aak.t[:, hs, :], in0=pb_.t[:].rearrange("p (h t) -> p h t", t=128), in1=m1b, op=ALU.mult), r=[pb_.r, mask1.r], w=[Aak.r])
                  for ch in range(2):
                      tc_ = slice(ch * 64, ch * 64 + 64)
                      S.op("dve", lambda e, tc_=tc_: e.tensor_tensor(out=NTm[0].t[tc_, :, tc_], in0=hv(pS3.t[tc_, :]), in1=maskL.t[tc_, :].unsqueeze(1).to_broadcast([64, 8, 64]), op=ALU.mult), r=[pS3.r, maskL.r], w=[NTm[0].r])
                      S.op("act", lambda e, tc_=tc_: e.copy(out=Nm[0].t[tc_, :, tc_], in_=Arb.t[tc_, :, 0:64]), r=[Arb.r], w=[Nm[0].r])
                  if stop == 4:
                      raise _Stop()
                  if debug and tl_ == 0:
                      tails.append(S.dma("sp", dbg["Arb"], Arb.t[:], r=[Arb.r], key="dbg_Arb"))
                  if debug and tl_ == 0:
                      tails.append(S.dma("sp", dbg["Aak"], Aak.t[:], r=[Aak.r], key="dbg_Aak"))
                  if debug and tl_ == 0:
                      tails.append(S.dma("sp", dbg["NT0"], NTm[0].t[:], r=[NTm[0].r], key="dbg_NT0"))
                  if debug and tl_ == 0:
                      tails.append(S.dma("sp", dbg["ART"], ART.t[:], r=[ART.r], key="dbg_ART"))
                  if debug and tl_ == 0:
                      tails.append(S.dma("sp", dbg["KT"], KT.t[:], r=[KT.r], key="dbg_KT"))
                  if debug and tl_ == 0:
                      tails.append(S.dma("sp", dbg["BT"], BT.t[:], r=[BT.r], key="dbg_BT"))
                  if debug and tl_ == 0:
                      tails.append(S.dma("sp", dbg["Bh"], Bh.t[:], r=[Bh.r], key="dbg_Bh"))
                  if debug and tl_ == 0:
                      tails.append(S.dma("sp", dbg["Kh"], Kh.t[:], r=[Kh.r], key="dbg_Kh"))
                  pX = PB[5]
                  for h in range(8):
                      for ch in range(2):
                          tc_ = slice(ch * 64, ch * 64 + 64)
                          S.op("pe", lambda e, h=h, tc_=tc_, v_=v_: e.matmul(pX.t[tc_, h * 64:(h + 1) * 64], Aak.t[tc_, h, 0:64], v_[tc_, h * 64:(h + 1) * 64], start=True, stop=True), r=[Aak.r] + rG, w=[pX.r])
                  S.op("act", lambda e: e.copy(out=Z[0].t[:, :, 64:128], in_=hv(pX.t[:])), r=[pX.r], w=[Z[0].r])
                  if stop == 5:
                      raise _Stop()
                  if debug and tl_ == 0:
                      tails.append(S.dma("sp", dbg["Z0"], Z[0].t[:], r=[Z[0].r], key="dbg_Z0"))
                  zi = 0
                  for lev in range(6):
                      Nc, NTc = Nm[lev % 2], NTm[lev % 2]
                      Nn, NTn = Nm[(lev + 1) % 2], NTm[(lev + 1) % 2]
                      pZa, pZb = PB[0], PB[1]
                      Zc, Zn = Z[zi], Z[1 - zi]
                      if lev < 5:
                          pN, pNT = (PB[2], PB[3]), (PB[4], PB[5])
                          for h in range(8):
                              hb, hc = h // 4, (h % 4) * 128
                              S.op("pe", lambda e, NTc=NTc, Nc=Nc, h=h, hb=hb, hc=hc: e.matmul(pN[hb].t[:, hc:hc + 128], NTc.t[:, h, :], Nc.t[:, h, :], start=True, stop=True), r=[Nc.r, NTc.r], w=[pN[hb].r])
                              if lev < 4:
                                  S.op("pe", lambda e, NTc=NTc, Nc=Nc, h=h, hb=hb, hc=hc: e.matmul(pNT[hb].t[:, hc:hc + 128], Nc.t[:, h, :], NTc.t[:, h, :], start=True, stop=True), r=[Nc.r, NTc.r], w=[pNT[hb].r])
                      for h in range(8):
                          pz = pZa if h < 4 else pZb
                          hc = (h % 4) * 128
                          S.op("pe", lambda e, pz=pz, Nc=Nc, Zc=Zc, h=h, hc=hc: e.matmul(pz.t[:, hc:hc + 128], Nc.t[:, h, :], Zc.t[:, h, :], start=True, stop=True), r=[Nc.r, Zc.r], w=[pz.r])
                      if lev == 5:
                          Zn = Zf32
                      S.op("dve", lambda e, Zc=Zc, Zn=Zn: e.tensor_tensor(out=Zn.t[:, 0:4, :], in0=Zc.t[:, 0:4, :], in1=pZa.t[:].rearrange("p (h t) -> p h t", t=128), op=ALU.add), r=[Zc.r, pZa.r], w=[Zn.r])
                      S.op("dve", lambda e, Zc=Zc, Zn=Zn: e.tensor_tensor(out=Zn.t[:, 4:8, :], in0=Zc.t[:, 4:8, :], in1=pZb.t[:].rearrange("p (h t) -> p h t", t=128), op=ALU.add), r=[Zc.r, pZb.r], w=[Zn.r])
                      zi = 1 - zi
                      if lev < 5:
                          for hb in range(2):
                              S.op("act", lambda e, Nn=Nn, hb=hb: e.copy(out=Nn.t[:, hb * 4:hb * 4 + 4, :], in_=pN[hb].t[:].rearrange("p (h t) -> p h t", t=128)), r=[pN[hb].r], w=[Nn.r])
                              if lev < 4:
                                  S.op("act" if hb == 0 else "dve", (lambda e, NTn=NTn, hb=hb: e.copy(out=NTn.t[:, hb * 4:hb * 4 + 4, :], in_=pNT[hb].t[:].rearrange("p (h t) -> p h t", t=128))) if hb == 0 else (lambda e, NTn=NTn, hb=hb: e.tensor_copy(out=NTn.t[:, hb * 4:hb * 4 + 4, :], in_=pNT[hb].t[:].rearrange("p (h t) -> p h t", t=128))), r=[pNT[hb].r], w=[NTn.r])
                  Zf = Zf32
                  pY0 = PB[4]
                  for h in range(8):
                      for ch in range(2):
                          tc_ = slice(ch * 64, ch * 64 + 64)
                          S.op("pe", lambda e, h=h, tc_=tc_: e.matmul(pY0.t[tc_, h * 64:(h + 1) * 64], Arb.t[tc_, h, 64:128], Zf.t[tc_, h, 64:128], start=True, stop=False), r=[Arb.r, Zf.r], w=[pY0.r])
                          S.op("pe", lambda e, h=h, tc_=tc_, v_=v_: e.matmul(pY0.t[tc_, h * 64:(h + 1) * 64], Aak.t[tc_, h, 64:128], v_[tc_, h * 64:(h + 1) * 64], start=False, stop=True), r=[Aak.r] + rG, w=[pY0.r])
                  S.op("act", lambda e: e.copy(out=Y0.t[:], in_=pY0.t[:]), r=[pY0.r], w=[Y0.r])
                  if stop == 7:
                      raise _Stop()
                  if debug and tl_ == 0:
                      tails.append(S.dma("sp", dbg["Zf"], Zf.t[:], r=[Zf.r], key="dbg_Zf"))
                  if debug and tl_ == 0:
                      tails.append(S.dma("sp", dbg["Y0"], Y0.t[:], r=[Y0.r], key="dbg_Y0"))
                  pR, pQ, pH, pg = PB[5], PB[6], PB[0], PB[1]
                  for h in range(8):
                      p, e_ = h // 2, h % 2
                      fe = slice(e_ * 64, e_ * 64 + 64)
                      for ch in range(2):
                          tc_ = slice(ch * 64, ch * 64 + 64)
                          cs = slice(p * 128 + ch * 64, p * 128 + ch * 64 + 64)
                          hs_ = slice(h * 64, h * 64 + 64)
                          S.op("pe", lambda e, fe=fe, cs=cs, tc_=tc_, h=h: e.matmul(pR.t[fe, cs], Zf.t[tc_, h, 0:64], Arb.t[tc_, h, 64:128], start=True, stop=True), r=[Zf.r, Arb.r], w=[pR.r])
                          S.op("pe", lambda e, fe=fe, cs=cs, tc_=tc_, h=h, hs_=hs_: e.matmul(pQ.t[fe, cs], Zf.t[tc_, h, 0:64], Bh.t[tc_, hs_], start=True, stop=True), r=[Zf.r, Bh.r], w=[pQ.r])
                          S.op("pe", lambda e, fe=fe, cs=cs, tc_=tc_, h=h, hs_=hs_: e.matmul(pH.t[fe, cs], Bh.t[tc_, hs_], Zf.t[tc_, h, 64:128], start=True, stop=False), r=[Zf.r, Bh.r], w=[pH.r])
                          S.op("pe", lambda e, fe=fe, cs=cs, tc_=tc_, hs_=hs_, v_=v_: e.matmul(pH.t[fe, cs], Kh.t[tc_, hs_], v_[tc_, hs_], start=False, stop=True), r=[Kh.r] + rG, w=[pH.r])
                          S.op("pe", lambda e, fe=fe, tc_=tc_, hs_=hs_, p=p, ch=ch: e.matmul(pg.t[fe, p * 2 + ch:p * 2 + ch + 1], GC.t[tc_, hs_], avg.t[tc_, :], start=True, stop=True), r=[GC.r, avg.r], w=[pg.r])
                  for ch in range(2):
                      S.op("dve", lambda e, ch=ch: e.tensor_tensor(out=RbT.t[:, :, ch * 64:(ch + 1) * 64], in0=pR.t[:].rearrange("p (a t) -> p a t", t=128)[:, :, ch * 64:(ch + 1) * 64], in1=ART.t[:, :, ch, 64:128], op=ALU.add), r=[pR.r, ART.r], w=[RbT.r])
                  S.op("act", lambda e: e.copy(out=Qm.t[:], in_=pQ.t[:].rearrange("p (a t) -> p a t", t=128)), r=[pQ.r], w=[Qm.r])
                  S.op("act", lambda e: e.copy(out=Hm.t[:], in_=pH.t[:].rearrange("p (a t) -> p a t", t=128)), r=[pH.r], w=[Hm.r])
                  S.op("act", lambda e: e.copy(out=gcol.t[:], in_=pg.t[:, 0:8]), r=[pg.r], w=[gcol.r])
                  if stop == 8:
                      raise _Stop()
                  if debug and tl_ == 0:
                      tails.append(S.dma("sp", dbg["RbT"], RbT.t[:], r=[RbT.r], key="dbg_RbT"))
                  if debug and tl_ == 0:
                      tails.append(S.dma("sp", dbg["Qm"], Qm.t[:], r=[Qm.r], key="dbg_Qm"))
                  if debug and tl_ == 0:
                      tails.append(S.dma("sp", dbg["Hm"], Hm.t[:], r=[Hm.r], key="dbg_Hm"))
                  if debug and tl_ == 0:
                      tails.append(S.dma("sp", dbg["gcol"], gcol.t[:], r=[gcol.r], key="dbg_gcol"))
                  for ch in range(2):
                      tc_ = slice(ch * 64, ch * 64 + 64)
                      Sc, Sn = ST[st_i], ST[1 - st_i]
                      pYe, pS_ = (PB[2], PB[4]), PB[3]
                      for h in range(8):
                          p, e_ = h // 2, h % 2
                          fe = slice(e_ * 64, e_ * 64 + 64)
                          cs = slice(ch * 64, ch * 64 + 64)
                          pY = pYe[e_]
                          S.op("pe", lambda e, fe=fe, p=p, cs=cs, tc_=tc_, h=h, Sc=Sc, pY=pY: e.matmul(pY.t[tc_, h * 64:(h + 1) * 64], RbT.t[fe, p, cs], Sc.t[fe, p, :], start=True, stop=True), r=[RbT.r, Sc.r], w=[pY.r])
                          S.op("pe", lambda e, fe=fe, p=p, cs=cs, Sc=Sc: e.matmul(pS_.t[fe, p * 64:(p + 1) * 64], Qm.t[fe, p, cs], Sc.t[fe, p, :], start=True, stop=True), r=[Qm.r, Sc.r], w=[pS_.r])
                      v4 = lambda ap: ap.rearrange("p (a e d) -> p a e d", e=2, d=64)
                      for e_ in range(2):
                          S.op("dve", lambda e, tc_=tc_, e_=e_: e.tensor_tensor(out=v4(Yt.t[tc_, :])[:, :, e_, :], in0=v4(Y0.t[tc_, :])[:, :, e_, :], in1=v4(pYe[e_].t[tc_, :])[:, :, e_, :], op=ALU.add), r=[Y0.r, pYe[e_].r], w=[Yt.r])
                      for p in range(4):
                          S.op("dve", lambda e, p=p, ch=ch, Sc=Sc, Sn=Sn: e.scalar_tensor_tensor(out=Sn.t[:, p, :], in0=Sc.t[:, p, :], scalar=gcol.t[:, p * 2 + ch:p * 2 + ch + 1], in1=pS_.t[:, p * 64:(p + 1) * 64], op0=ALU.mult, op1=ALU.add), r=[Sc.r, gcol.r, pS_.r], w=[Sn.r])
                      S.op("dve", lambda e, ch=ch, Sn=Sn: e.tensor_tensor(out=Sn.t[:], in0=Sn.t[:], in1=Hm.t[:, :, ch * 64:(ch + 1) * 64], op=ALU.add), r=[Sn.r, Hm.r], w=[Sn.r])
                      st_i = 1 - st_i
                  if stop == 9:
                      raise _Stop()
                  S.op("dve", lambda e: e.tensor_reduce(out=sm.t[:, 16:24], in_=hv(Yt.t[:]), axis=AX.X, op=ALU.add), r=[Yt.r], w=[sm.r])
                  S.op("act", lambda e: e.activation(out=t2.t[:], in_=Yt.t[:], func=AF.Square), r=[Yt.r, t2.r], w=[t2.r])
                  S.op("dve", lambda e: e.tensor_reduce(out=sm.t[:, 24:32], in_=hv(t2.t[:]), axis=AX.X, op=ALU.add), r=[t2.r], w=[sm.r])
                  S.op("dve", lambda e: e.tensor_scalar(out=sm.t[:, 16:24], in0=sm.t[:, 16:24], scalar1=1.0 / 64, scalar2=None, op0=ALU.mult), r=[sm.r], w=[sm.r])
                  S.op("dve", lambda e: e.tensor_tensor(out=sm.t[:, 32:40], in0=sm.t[:, 16:24], in1=sm.t[:, 16:24], op=ALU.mult), r=[sm.r], w=[sm.r])
                  S.op("dve", lambda e: e.scalar_tensor_tensor(out=sm.t[:, 24:32], in0=sm.t[:, 24:32], scalar=1.0 / 64, in1=sm.t[:, 32:40], op0=ALU.mult, op1=ALU.subtract), r=[sm.r], w=[sm.r])
                  S.op("dve", lambda e: e.tensor_scalar(out=sm.t[:, 24:32], in0=sm.t[:, 24:32], scalar1=GN_EPS, scalar2=None, op0=ALU.add), r=[sm.r], w=[sm.r])
                  S.op("act", lambda e: e.activation(out=sm.t[:, 24:32], in_=sm.t[:, 24:32], func=AF.Sqrt), r=[sm.r], w=[sm.r])
                  S.op("dve", lambda e: e.reciprocal(out=sm.t[:, 24:32], in_=sm.t[:, 24:32]), r=[sm.r], w=[sm.r])
                  S.op("dve", lambda e: e.tensor_tensor(out=hv(yn.t[:]), in0=hv(Yt.t[:]), in1=bc8(sm.t[:, 16:24]), op=ALU.subtract), r=[Yt.r, sm.r], w=[yn.r])
                  S.op("dve", lambda e: e.tensor_tensor(out=hv(yn.t[:]), in0=hv(yn.t[:]), in1=bc8(sm.t[:, 24:32]), op=ALU.mult), r=[yn.r, sm.r], w=[yn.r])
                  S.op("pool", lambda e: e.tensor_tensor(out=yn.t[:], in0=yn.t[:], in1=BC["lngb"][:], op=ALU.mult), r=[yn.r, r_bc], w=[yn.r])
                  S.op("pool", lambda e: e.tensor_tensor(out=yn.t[:], in0=yn.t[:], in1=BC["lnbb"][:], op=ALU.add), r=[yn.r, r_bc], w=[yn.r])
                  S.op("dve", lambda e: e.tensor_tensor(out=yn.t[:], in0=yn.t[:], in1=bonus.t[:], op=ALU.add), r=[yn.r, bonus.r], w=[yn.r])
                  S.op("dve", lambda e: e.tensor_tensor(out=ybf.t[:], in0=yn.t[:], in1=X["gg"].t[:], op=ALU.mult), r=[yn.r, X["gg"].r], w=[ybf.r])
                  for p in range(4):
                      S.op("pe", lambda e, p=p: e.transpose(pbb.t[:, p * 128:(p + 1) * 128], ybf.t[:, p * 128:(p + 1) * 128], identb.t[:]), r=[ybf.r, identb.r], w=[pbb.r])
                  yo = yTo[tl_ % 2]
                  S.op("act", lambda e, yo=yo: e.copy(out=yo.t[:], in_=pbb.t[:, 0:512].rearrange("p (a t) -> p a t", t=128)), r=[pbb.r], w=[yo.r])
                  tails.append(S.dma("sp", ysrc[:, :, tl_ * 128:(tl_ + 1) * 128], yo.t[:], r=[yo.r], key="yTo%d" % (tl_ % 2)))
                  if debug and tl_ == 0:
                      S.op("act", lambda e, r_=r_: e.copy(out=t1.t[:], in_=r_), r=rG + [t1.r], w=[t1.r])
                      S.op("act", lambda e, v_=v_: e.copy(out=t2.t[:], in_=v_), r=rG + [t2.r], w=[t2.r])
                      tails.append(S.dma("sp", dbg["v"], t2.t[:], r=[t2.r], key="dbg_v"))
                      for n, src in (("r", t1), ("kp", kp), ("kkn", kkn), ("a", a_), ("sw", sw), ("Y", Yt), ("bonus", bonus), ("Gm", Gm), ("yn", yn)):
                          tails.append(S.dma("sp", dbg[n], src.t[:], r=[src.r], key="dbg_" + n))
                      tails.append(S.dma("sp", dbg["S"], ST[st_i].t[:], r=[ST[st_i].r], key="dbg_S"))
        except _Stop:
            pass
        if after is not None:
            after(C, tails)
        S.emit(st, tail_waits=tails)


PAIRS = [[0, 1], [2, 3], [4, 5], [6, 7]]
TL_KEYS = ("cT", "adaw", "adab", "lng", "lnb", "wo", "win", "wout")


def tl_decl(dt, sfx):
    return dict(cT=dt("cT" + sfx, [128, 8], F32), adaw=dt("adaw" + sfx, [D, 4 * D], F32), adab=dt("adab" + sfx, [128, 32], F32),
                lng=dt("lng" + sfx, [128, 16], F32), lnb=dt("lnb" + sfx, [128, 16], F32),
                wo=dt("wo" + sfx, [D, D], F32), win=dt("win" + sfx, [D, 2 * DFF], F32), wout=dt("wout" + sfx, [DFF, D], F32))


def build_fused(stop=99):
    nc = bass.Bass("TRN2", target_bir_lowering=False)
    dt = lambda n, s, d, k="ExternalInput": nc.dram_tensor(n, list(s), d, kind=k).ap()
    it = lambda n, s, d: nc.dram_tensor(n, list(s), d).ap()
    fox_in = fox_decl(dt, 4096, "_f")
    oT_d = dt("oT", [D, 2048], F32, "ExternalOutput")
    cin1, cout1 = it("cin1", [4, 128, 4096], BF16), it("cout1", [4, 2, 128, 4096], BF16)
    cin2, cout2 = it("cin2", [8, 128, 2048], F32), it("cout2", [8, 2, 128, 2048], F32)
    cin3, cout3 = it("cin3", [4, 128, 4096], BF16), it("cout3", [4, 2, 128, 4096], BF16)

    def gather(cin, cout, key, nblk):
        def after(C, tails):
            prev = list(tails)
            for k in range(nblk):
                o = C.S.cc("AllGather", PAIRS, cin[k], cout[k].rearrange("r p t -> (r p) t"), key=key, extra=prev)
            tails.append(o)
        return after

    cin1_v = cin1.rearrange("k p t -> (k p) t")
    cin3_v = cin3.rearrange("k p t -> (k p) t")
    cin2_v = cin2.rearrange("k p t -> (k p) t")
    y1_v = cout1
    y3_v = cout3
    fox_phase(nc, "f_", fox_in, cin1_v, 4096, after=gather(cin1, cout1, "g1", 4))
    if stop <= 1:
        return nc
    phase_end(nc)
    sel_d = dt("sel", [128, 2], F32)
    xTh_d = dt("xTh", [D, 2048], F32)
    t = tl_decl(dt, "_t0")
    tl_phase(nc, "t0_", y1_v, xTh_d, t["cT"], t["adaw"], t["adab"], t["lng"], t["lnb"], t["wo"], t["win"], t["wout"], cin2_v, 2048, 1024,
             sel_d=sel_d, after=gather(cin2, cout2, "g2", 8))
    if stop <= 2:
        return nc
    phase_end(nc)
    rw_in = rwkv_decl(dt, "_r")
    c2v = cout2.rearrange("kc r p t -> r p kc t")
    rwkv_phase(nc, "r_", lambda t0, n: c2v[t0 // 2048, :, :, (t0 % 2048):(t0 % 2048) + n], rw_in, cin3_v, 4096, after=gather(cin3, cout3, "g3", 4))
    if stop <= 3:
        return nc
    phase_end(nc)
    t = tl_decl(dt, "_t1")
    tl_phase(nc, "t1_", y3_v, cin2_v, t["cT"], t["adaw"], t["adab"], t["lng"], t["lnb"], t["wo"], t["win"], t["wout"], oT_d, 2048, 1024, sel_d=sel_d)
    return nc


NCORES = 8
_PROGS = {}


def _prog(name, fn):
    if name not in _PROGS:
        _PROGS[name] = fn()
    return _PROGS[name]


def _col_layout(v):
    return np.ascontiguousarray(np.asarray(v).reshape(-1, 128).T)


def _bc(v, n=512):
    v = np.asarray(v, dtype=np.float32).reshape(-1)
    return np.ascontiguousarray(np.broadcast_to(v[None, :], (128, v.shape[0])))


def _run(nc, in_maps):
    res = run_bass_kernel_spmd(nc, in_maps, core_ids=list(range(NCORES)))
    return res.results


def _tl_maps(yT_list, xT_list, c, ada_w_i, ada_b_i, ln_g_i, ln_b_i, wo, win, wout):
    maps = []
    adaw = np.ascontiguousarray(ada_w_i[:, 2 * D:])
    adab = _col_layout(ada_b_i[2 * D:])
    lng = _col_layout(ln_g_i.reshape(-1))
    lnb = _col_layout(ln_b_i.reshape(-1))
    for core in range(NCORES):
        b, th = core // 2, core % 2
        ts = slice(th * 2048, (th + 1) * 2048)
        maps.append({
            "yT": np.ascontiguousarray(yT_list[b][:, ts]),
            "xT": np.ascontiguousarray(xT_list[b][:, ts]),
            "cT": _col_layout(c[b]),
            "adaw": adaw, "adab": adab, "lng": lng, "lnb": lnb,
            "wo": wo, "win": win, "wout": wout,
        })
    return maps


def kernel_unfused(x, c, ada_w, ada_b, ln_g, ln_b, ffn_w_in, ffn_w_out,
           fox_w_in, fox_b_f, fox_q_g, fox_k_g, fox_w_o,
           rwkv_mu, rwkv_w_rkv, rwkv_w0, rwkv_w1, rwkv_w2, rwkv_a0, rwkv_a1, rwkv_a2,
           rwkv_g1, rwkv_g2, rwkv_k_k, rwkv_k_a, rwkv_r_k, rwkv_lnx_g, rwkv_lnx_b, rwkv_w_o):
    f = lambda a: np.ascontiguousarray(np.asarray(a, dtype=np.float32))
    x, c, ada_w, ada_b, ln_g, ln_b = f(x), f(c), f(ada_w), f(ada_b), f(ln_g), f(ln_b)
    ffn_w_in, ffn_w_out = f(ffn_w_in), f(ffn_w_out)
    B = x.shape[0]
    w_in = f(fox_w_in)[0]
    b_f, q_g, k_g = f(fox_b_f)[0], f(fox_q_g)[0], f(fox_k_g)[0]
    maps = []
    for core in range(NCORES):
        b, hh = core // 2, core % 2
        sl = slice(hh * 512, (hh + 1) * 512)
        maps.append({
            "x": x[b],
            "cT": _col_layout(c[b]),
            "adaw": np.ascontiguousarray(ada_w[0][:, 0:2 * D]),
            "adab": _col_layout(ada_b[0][0:2 * D]),
            "wq": np.ascontiguousarray(w_in[:, 0:D][:, sl]),
            "wk": np.ascontiguousarray(w_in[:, D:2 * D][:, sl]),
            "wv": np.ascontiguousarray(w_in[:, 2 * D:3 * D][:, sl]),
            "wf": np.ascontiguousarray(w_in[:, 3 * D + hh * 8:3 * D + hh * 8 + 8]),
            "wg": np.ascontiguousarray(w_in[:, 3 * D + 16:][:, sl]),
            "bfb": _bc(b_f[hh * 8:(hh + 1) * 8]),
            "qgb": _bc(np.tile(q_g, 8)),
            "kgb": _bc(np.tile(k_g, 8)),
        })
    r1 = _run(_prog("fox", build_fox), maps)
    yT = [np.concatenate([r1[2 * b]["ygT"], r1[2 * b + 1]["ygT"]], axis=0) for b in range(B)]
    xT = [np.ascontiguousarray(x[b].T) for b in range(B)]
    tlp = _prog("tl", build_tl)
    r2 = _run(tlp, _tl_maps(yT, xT, c, ada_w[0], ada_b[0], ln_g[0], ln_b[0], f(fox_w_o)[0], ffn_w_in[0], ffn_w_out[0]))
    x1T = [np.concatenate([r2[2 * b]["oT"], r2[2 * b + 1]["oT"]], axis=1) for b in range(B)]
    mu = f(rwkv_mu)[0]
    w_rkv = f(rwkv_w_rkv)[0]
    P = dict(w0=f(rwkv_w0)[0], w1=f(rwkv_w1)[0], w2=f(rwkv_w2)[0], a0=f(rwkv_a0)[0], a1=f(rwkv_a1)[0], a2=f(rwkv_a2)[0],
             g1=f(rwkv_g1)[0], g2=f(rwkv_g2)[0], k_k=f(rwkv_k_k)[0], k_a=f(rwkv_k_a)[0], r_k=f(rwkv_r_k)[0].reshape(-1),
             lnx_g=f(rwkv_lnx_g)[0], lnx_b=f(rwkv_lnx_b)[0])
    mu_l = np.concatenate([_col_layout(mu[n]) for n in range(6)], axis=1)
    maps = []
    for core in range(NCORES):
        b, hh = core // 2, core % 2
        sl = slice(hh * 512, (hh + 1) * 512)
        maps.append({
            "xT": x1T[b],
            "cT": _col_layout(c[b]),
            "adaw": np.ascontiguousarray(ada_w[1][:, 0:2 * D]),
            "adab": _col_layout(ada_b[1][0:2 * D]),
            "mu": mu_l,
            "wr": np.ascontiguousarray(w_rkv[0][:, sl]),
            "wk": np.ascontiguousarray(w_rkv[1][:, sl]),
            "wv": np.ascontiguousarray(w_rkv[2][:, sl]),
            "w1": P["w1"], "a1": P["a1"], "g1": P["g1"],
            "w2": np.ascontiguousarray(P["w2"][:, sl]),
            "a2": np.ascontiguousarray(P["a2"][:, sl]),
            "g2": np.ascontiguousarray(P["g2"][:, sl]),
            "w0b": _bc(P["w0"][sl]), "a0b": _bc(P["a0"][sl]), "kkb": _bc(P["k_k"][sl]), "kab": _bc(P["k_a"][sl]),
            "rkb": _bc(P["r_k"][sl]), "lngb": _bc(P["lnx_g"][sl]), "lnbb": _bc(P["lnx_b"][sl]),
        })
    r3 = _run(_prog("rwkv", build_rwkv), maps)
    y2T = [np.concatenate([r3[2 * b]["yT"], r3[2 * b + 1]["yT"]], axis=0) for b in range(B)]
    r4 = _run(tlp, _tl_maps(y2T, x1T, c, ada_w[1], ada_b[1], ln_g[1], ln_b[1], f(rwkv_w_o)[0], ffn_w_in[1], ffn_w_out[1]))
    out = np.empty(x.shape, np.float32)
    for core in range(NCORES):
        b, th = core // 2, core % 2
        out[b, th * 2048:(th + 1) * 2048, :] = r4[core]["oT"].T
    return out


def kernel(x, c, ada_w, ada_b, ln_g, ln_b, ffn_w_in, ffn_w_out,
           fox_w_in, fox_b_f, fox_q_g, fox_k_g, fox_w_o,
           rwkv_mu, rwkv_w_rkv, rwkv_w0, rwkv_w1, rwkv_w2, rwkv_a0, rwkv_a1, rwkv_a2,
           rwkv_g1, rwkv_g2, rwkv_k_k, rwkv_k_a, rwkv_r_k, rwkv_lnx_g, rwkv_lnx_b, rwkv_w_o):
    f = lambda a: np.ascontiguousarray(np.asarray(a, dtype=np.float32))
    x, c, ada_w, ada_b, ln_g, ln_b = f(x), f(c), f(ada_w), f(ada_b), f(ln_g), f(ln_b)
    ffn_w_in, ffn_w_out = f(ffn_w_in), f(ffn_w_out)
    w_in = f(fox_w_in)[0]
    b_f, q_g, k_g = f(fox_b_f)[0], f(fox_q_g)[0], f(fox_k_g)[0]
    mu = f(rwkv_mu)[0]
    w_rkv = f(rwkv_w_rkv)[0]
    P = dict(w0=f(rwkv_w0)[0], w1=f(rwkv_w1)[0], w2=f(rwkv_w2)[0], a0=f(rwkv_a0)[0], a1=f(rwkv_a1)[0], a2=f(rwkv_a2)[0],
             g1=f(rwkv_g1)[0], g2=f(rwkv_g2)[0], k_k=f(rwkv_k_k)[0], k_a=f(rwkv_k_a)[0], r_k=f(rwkv_r_k)[0].reshape(-1),
             lnx_g=f(rwkv_lnx_g)[0], lnx_b=f(rwkv_lnx_b)[0])
    mu_l = np.concatenate([_col_layout(mu[n]) for n in range(6)], axis=1)
    wos = [f(fox_w_o)[0], f(rwkv_w_o)[0]]
    tl_common = []
    for L in range(2):
        tl_common.append({
            "adaw_t%d" % L: np.ascontiguousarray(ada_w[L][:, 2 * D:]), "adab_t%d" % L: _col_layout(ada_b[L][2 * D:]),
            "lng_t%d" % L: _col_layout(ln_g[L].reshape(-1)), "lnb_t%d" % L: _col_layout(ln_b[L].reshape(-1)),
            "wo_t%d" % L: wos[L], "win_t%d" % L: ffn_w_in[L], "wout_t%d" % L: ffn_w_out[L]})
    adaw_f = np.ascontiguousarray(ada_w[0][:, 0:2 * D])
    adaw_r = np.ascontiguousarray(ada_w[1][:, 0:2 * D])
    maps = []
    for core in range(NCORES):
        b, j = core // 2, core % 2
        sl = slice(j * 512, (j + 1) * 512)
        cT = _col_layout(c[b])
        m = {
            "x_f": x[b], "cT_f": cT, "adaw_f": adaw_f, "adab_f": _col_layout(ada_b[0][0:2 * D]),
            "wq_f": np.ascontiguousarray(w_in[:, 0:D][:, sl]), "wk_f": np.ascontiguousarray(w_in[:, D:2 * D][:, sl]),
            "wv_f": np.ascontiguousarray(w_in[:, 2 * D:3 * D][:, sl]), "wf_f": np.ascontiguousarray(w_in[:, 3 * D + j * 8:3 * D + j * 8 + 8]),
            "wg_f": np.ascontiguousarray(w_in[:, 3 * D + 16:][:, sl]),
            "bfb_f": _bc(b_f[j * 8:(j + 1) * 8]), "qgb_f": _bc(np.tile(q_g, 8)), "kgb_f": _bc(np.tile(k_g, 8)),
            "cT_r": cT, "adaw_r": adaw_r, "adab_r": _col_layout(ada_b[1][0:2 * D]), "mu_r": mu_l,
            "wr_r": np.ascontiguousarray(w_rkv[0][:, sl]), "wk_r": np.ascontiguousarray(w_rkv[1][:, sl]), "wv_r": np.ascontiguousarray(w_rkv[2][:, sl]),
            "w1_r": P["w1"], "a1_r": P["a1"], "g1_r": P["g1"],
            "w2_r": np.ascontiguousarray(P["w2"][:, sl]), "a2_r": np.ascontiguousarray(P["a2"][:, sl]), "g2_r": np.ascontiguousarray(P["g2"][:, sl]),
            "w0b_r": _bc(P["w0"][sl]), "a0b_r": _bc(P["a0"][sl]), "kkb_r": _bc(P["k_k"][sl]), "kab_r": _bc(P["k_a"][sl]),
            "rkb_r": _bc(P["r_k"][sl]), "lngb_r": _bc(P["lnx_g"][sl]), "lnbb_r": _bc(P["lnx_b"][sl]),
            "cT_t0": cT, "cT_t1": cT,
            "sel": np.ascontiguousarray(np.broadcast_to(np.array([1.0 - j, float(j)], np.float32)[None, :], (128, 2))),
            "xTh": np.ascontiguousarray(x[b, j * 2048:(j + 1) * 2048, :].T),
        }
        m.update(tl_common[0])
        m.update(tl_common[1])
        maps.append(m)
    res = _run(_prog("fused", build_fused), maps)
    out = np.empty(x.shape, np.float32)
    for core in range(NCORES):
        b, j = core // 2, core % 2
        out[b, j * 2048:(j + 1) * 2048, :] = res[core]["oT"].T
    return out
```

```python
import numpy as np
from contextlib import ExitStack
import concourse.bass as bass
import concourse.mybir as mybir
from concourse.bass_utils import run_bass_kernel_spmd

F32 = mybir.dt.float32
BF16 = mybir.dt.bfloat16
AF = mybir.ActivationFunctionType
ALU = mybir.AluOpType
AX = mybir.AxisListType

ENGS = ("pe", "act", "dve", "pool", "sp")
LAST_SEMS = []
SEM_CTR = [0]


def phase_end(nc):
    nc.all_engine_barrier()
    nc.clear_and_free_semaphores(list(LAST_SEMS))
    LAST_SEMS[:] = []
    nc.all_engine_barrier()


class Res:
    __slots__ = ("name", "last_w", "readers", "excl")

    def __init__(self, name, excl=False):
        self.name = name
        self.excl = excl
        self.last_w = None
        self.readers = []


class Op:
    __slots__ = ("eng", "fn", "deps", "sig", "count", "is_dma", "key", "semval", "vc", "waits", "gid")


class Sched:
    def __init__(self, nc):
        self.nc = nc
        self.ops = []
        self.dma_counts = {}

    def res(self, name, excl=False):
        return Res(name, excl)

    def _add(self, op, r, w, extra=()):
        deps = [(d, "raw") for d in extra]
        for x in r:
            if x.last_w is not None:
                deps.append((x.last_w, "raw"))
            if x.excl:
                for rd in x.readers:
                    deps.append((rd, "war"))
        for x in w:
            if x.last_w is not None:
                deps.append((x.last_w, "waw"))
            for rd in x.readers:
                deps.append((rd, "war"))
        dd = []
        for d, kind in deps:
            if d is op:
                continue
            if (not d.is_dma) and (not op.is_dma) and d.eng == op.eng:
                if op.eng == "pe" or kind == "war":
                    continue
            dd.append(d)
        op.deps = dd
        for d in dd:
            d.sig = True
        for x in w:
            x.last_w = op
            x.readers = []
        for x in r:
            if x.last_w is not op:
                if not op.is_dma:
                    x.readers = [q for q in x.readers if q.is_dma or q.eng != op.eng]
                x.readers.append(op)
        op.gid = len(self.ops)
        self.ops.append(op)
        return op

    def op(self, eng, fn, r=(), w=()):
        o = Op()
        o.eng = eng
        o.fn = fn
        o.is_dma = False
        o.sig = False
        o.key = eng
        return self._add(o, r, w)

    def dma(self, eng, out, in_, r=(), w=(), key=None, **kw):
        o = Op()
        o.eng = eng
        o.is_dma = True
        o.sig = True
        assert key is not None
        o.key = "dma:" + key
        o.fn = lambda e: e.dma_start(out=out, in_=in_, **kw)
        return self._add(o, r, w)

    def cc(self, kind, groups, in_ap, out_ap, r=(), w=(), key=None, extra=()):
        o = Op()
        o.eng = "pool"
        o.is_dma = True
        o.sig = True
        o.key = "cc:" + key
        o.fn = lambda e: e.collective_compute(kind, ALU.bypass, replica_groups=groups, ins=[in_ap], outs=[out_ap])
        return self._add(o, r, w, extra)

    def finalize(self, final_wait_eng="sp"):
        nc = self.nc
        counts = {e: 0 for e in ENGS}
        dmac = {}
        for o in self.ops:
            if o.is_dma:
                dmac[o.key] = dmac.get(o.key, 0) + (1 if o.key.startswith("cc:") else 16)
                o.semval = dmac[o.key]
            elif o.sig:
                counts[o.eng] += 1
                o.semval = counts[o.eng]
        seen = {e: {} for e in ENGS}
        for o in self.ops:
            s = seen[o.eng]
            need = {}
            for d in o.deps:
                if s.get(d.key, 0) < d.semval:
                    if d.key not in need or need[d.key].semval < d.semval:
                        need[d.key] = d
            o.waits = [(d.key, d.semval) for d in need.values()]
            for d in need.values():
                for k, v in d.vc.items():
                    if s.get(k, 0) < v:
                        s[k] = v
            if o.sig:
                vc = dict(s)
                vc[o.key] = o.semval
                o.vc = vc
            else:
                o.vc = None
        return counts, dmac

    def emit(self, stack, tail_waits=()):
        nc = self.nc
        counts, dmac = self.finalize()
        sems = {}
        for k in list(ENGS) + list(dmac):
            SEM_CTR[0] += 1
            sems[k] = nc.alloc_semaphore(name="sem%d" % SEM_CTR[0])
        self.nsems = len(sems)
        LAST_SEMS[:] = list(sems.values())
        per = {e: [o for o in self.ops if o.eng == e] for e in ENGS}
        block = stack.enter_context(nc.Block())

        def run(eng_name, eng):
            for o in per[eng_name]:
                for k, v in o.waits:
                    eng.wait_ge(sems[k], v)
                ins = o.fn(eng)
                if o.sig:
                    if o.is_dma:
                        ins.then_inc(sems[o.key], 1 if o.key.startswith("cc:") else 16)
                    else:
                        ins.then_inc(sems[o.key], 1)
            if eng_name == "sp":
                done = {}
                for o in tail_waits:
                    done[o.key] = max(done.get(o.key, 0), o.semval)
                for k, v in done.items():
                    eng.wait_ge(sems[k], v)

        @block.tensor
        def _(e):
            run("pe", e)

        @block.scalar
        def _(e):
            run("act", e)

        @block.vector
        def _(e):
            run("dve", e)

        @block.gpsimd
        def _(e):
            run("pool", e)

        @block.sync
        def _(e):
            run("sp", e)


D = 1024
DFF = 2816
NFC = 22
ALPHA = 4.0 ** 0.25
LN_EPS = 1e-5
EPS_P = LN_EPS / (ALPHA * ALPHA)


class Ctx:
    def __init__(self, nc, st, pfx=""):
        self.nc = nc
        self.st = st
        self.S = Sched(nc)
        self.n = 0
        self.pfx = pfx

    def sb(self, name, shape, dt):
        return self.st.enter_context(self.nc.sbuf_tensor(self.pfx + "sb_" + name, list(shape), dt))

    def ps(self, name, shape, dt=F32):
        return self.st.enter_context(self.nc.psum_tensor(self.pfx + "ps_" + name, list(shape), dt))

    def R(self, name, excl=False):
        return self.S.res(name, excl)


def make_consts(C):
    S = C.S
    C.ones = C.sb("ones", [128, 128], F32)
    C.r_ones = C.R("ones")
    S.op("pool", lambda e: e.memset(C.ones[:], 1.0), w=[C.r_ones])
    C.ident = C.sb("ident", [128, 128], F32)
    C.r_ident = C.R("ident")
    S.op("pool", lambda e: e.memset(C.ident[:], 1.0), w=[C.r_ident])
    S.op("pool", lambda e: e.affine_select(out=C.ident[:], in_=C.ident[:], pattern=[[-1, 128]], compare_op=ALU.is_equal, fill=0.0, base=0, channel_multiplier=1), r=[C.r_ident], w=[C.r_ident])


def compute_mods(C, cT_d, adaw_d, adab_d, nm, wbufs, psm, r_psm):
    S = C.S
    cT = C.sb("cT", [128, 8], F32)
    r_cT = C.R("cT")
    S.dma("sp", cT[:], cT_d, w=[r_cT], key="cT")
    S.op("act", lambda e: e.activation(out=cT[:], in_=cT[:], func=AF.Silu), r=[r_cT], w=[r_cT])
    adab = C.sb("adab", [128, nm * 8], F32)
    r_adab = C.R("adab")
    S.dma("sp", adab[:], adab_d, w=[r_adab], key="adab")
    mods = C.sb("mods", [128, nm * 8], F32)
    r_mods = C.R("mods")
    src = adaw_d.rearrange("(kc p) n -> p kc n", p=128)
    for m in range(nm):
        for q in range(4):
            bi = (m * 4 + q) % 2
            wt, r_wt = wbufs[bi]
            c0 = m * 1024 + q * 256
            S.dma("sp", wt[:], src[:, :, c0:c0 + 256], w=[r_wt], key="adaw%d" % bi)
            for j in range(2):
                cc = q * 2 + j
                for kc in range(8):
                    S.op("pe", lambda e, m=m, cc=cc, kc=kc, j=j, wt=wt: e.matmul(psm[:, m * 8 + cc:m * 8 + cc + 1], wt[:, kc, j * 128:(j + 1) * 128], cT[:, kc:kc + 1], start=(kc == 0), stop=(kc == 7)), r=[r_wt, r_cT], w=[r_psm])
    S.op("dve", lambda e: e.tensor_tensor(out=mods[:], in0=psm[:, 0:nm * 8], in1=adab[:], op=ALU.add), r=[r_psm, r_adab], w=[r_mods])
    return mods, r_mods


def ln_group(C, zT, r_z, g, n0, colA, colB, r_cols, outs, P):
    S = C.S
    ps_s, r_ps_s = P["st0"]
    ps_q, r_ps_q = P["st1"]
    sq, r_sq = P["sq"]
    for cc in range(8):
        S.op("act", lambda e, cc=cc: e.activation(out=sq[cc % 2][:], in_=zT[:, cc, n0:n0 + 512], func=AF.Square), r=[r_z[cc]], w=[r_sq[cc % 2]])
        S.op("pe", lambda e, cc=cc: e.matmul(ps_s[:], C.ones[:], zT[:, cc, n0:n0 + 512], start=(cc == 0), stop=(cc == 7)), r=[C.r_ones, r_z[cc]], w=[r_ps_s])
        S.op("pe", lambda e, cc=cc: e.matmul(ps_q[:], C.ones[:], sq[cc % 2][:], start=(cc == 0), stop=(cc == 7)), r=[C.r_ones, r_sq[cc % 2]], w=[r_ps_q])
    mean, r_mean = P["mean"]
    rstd, r_rstd = P["rstd"]
    S.op("act", lambda e: e.mul(out=mean[:], in_=ps_s[:], mul=1.0 / D), r=[r_ps_s], w=[r_mean])
    S.op("dve", lambda e: e.tensor_tensor(out=rstd[:], in0=mean[:], in1=mean[:], op=ALU.mult), r=[r_mean], w=[r_rstd])
    S.op("dve", lambda e: e.scalar_tensor_tensor(out=rstd[:], in0=ps_q[:], scalar=1.0 / D, in1=rstd[:], op0=ALU.mult, op1=ALU.subtract), r=[r_ps_q, r_rstd], w=[r_rstd])
    S.op("dve", lambda e: e.tensor_scalar(out=rstd[:], in0=rstd[:], scalar1=P["eps"], scalar2=None, op0=ALU.add), r=[r_rstd], w=[r_rstd])
    S.op("act", lambda e: e.activation(out=rstd[:], in_=rstd[:], func=AF.Sqrt), r=[r_rstd], w=[r_rstd])
    S.op("dve", lambda e: e.reciprocal(out=rstd[:], in_=rstd[:]), r=[r_rstd], w=[r_rstd])
    S.op("dve", lambda e: e.scalar_tensor_tensor(out=mean[:], in0=mean[:], scalar=-1.0, in1=rstd[:], op0=ALU.mult, op1=ALU.mult), r=[r_mean, r_rstd], w=[r_mean])
    tt, r_tt = P["tt"]
    for cc in range(8):
        k = cc % 2
        S.op("dve", lambda e, cc=cc, k=k: e.tensor_tensor(out=tt[k][:], in0=zT[:, cc, n0:n0 + 512], in1=rstd[:], op=ALU.mult), r=[r_z[cc], r_rstd], w=[r_tt[k]])
        S.op("dve", lambda e, cc=cc, k=k: e.tensor_tensor(out=tt[k][:], in0=tt[k][:], in1=mean[:], op=ALU.add), r=[r_tt[k], r_mean], w=[r_tt[k]])
        for (dst, r_dst, gt, go, bt, bo, eng) in outs:
            if eng == "act":
                S.op("act", lambda e, cc=cc, k=k, dst=dst, gt=gt, go=go, bt=bt, bo=bo: e.activation(out=dst[:, cc, n0:n0 + 512], in_=tt[k][:], func=AF.Identity, scale=gt[:, go + cc:go + cc + 1], bias=bt[:, bo + cc:bo + cc + 1]), r=[r_tt[k]] + r_cols, w=[r_dst[cc]])
            else:
                S.op(eng, lambda e, cc=cc, k=k, dst=dst, gt=gt, go=go, bt=bt, bo=bo: e.tensor_scalar(out=dst[:, cc, n0:n0 + 512], in0=tt[k][:], scalar1=gt[:, go + cc:go + cc + 1], scalar2=bt[:, bo + cc:bo + cc + 1], op0=ALU.mult, op1=ALU.add), r=[r_tt[k]] + r_cols, w=[r_dst[cc]])


def build_tl(NT=2048, HALF=1024, debug=False):
    nc = bass.Bass("TRN2", target_bir_lowering=False)
    dt = lambda n, s, d, k="ExternalInput": nc.dram_tensor(n, list(s), d, kind=k).ap()
    yT_d = dt("yT", [D, NT], BF16)
    xT_d = dt("xT", [D, NT], F32)
    cT_d = dt("cT", [128, 8], F32)
    adaw_d = dt("adaw", [D, 4 * D], F32)
    adab_d = dt("adab", [128, 32], F32)
    lng_d = dt("lng", [128, 16], F32)
    lnb_d = dt("lnb", [128, 16], F32)
    wo_d = dt("wo", [D, D], F32)
    win_d = dt("win", [D, 2 * DFF], F32)
    wout_d = dt("wout", [DFF, D], F32)
    oT_d = dt("oT", [D, NT], F32, "ExternalOutput")
    tl_phase(nc, "", yT_d, xT_d, cT_d, adaw_d, adab_d, lng_d, lnb_d, wo_d, win_d, wout_d, oT_d, NT, HALF, debug=debug)
    return nc


def tl_phase(nc, pfx, yT_d, xT_d, cT_d, adaw_d, adab_d, lng_d, lnb_d, wo_d, win_d, wout_d, oT_d, NT=2048, HALF=1024, debug=False, sel_d=None, after=None):
    dt = lambda n, s, d, k="ExternalInput": nc.dram_tensor(n, list(s), d, kind=k).ap()
    NG = HALF // 512
    if debug:
        dbg_cols = dt("dbg_cols", [128, 48], F32, "ExternalOutput")
        dbg_mods = dt("dbg_mods", [128, 32], F32, "ExternalOutput")
        dbg_x1 = dt("dbg_x1", [D, HALF], F32, "ExternalOutput")
        dbg_h = dt("dbg_h", [D, HALF], BF16, "ExternalOutput")
        dbg_a = dt("dbg_a", [DFF, HALF], BF16, "ExternalOutput")
    with ExitStack() as st:
        C = Ctx(nc, st, pfx)
        S = C.S
        make_consts(C)
        PB = [(C.ps("pb%d" % i, [128, 512]), C.R("pb%d" % i, True)) for i in range(8)]
        if sel_d is not None:
            selt = C.sb("selt", [128, 2], F32)
            r_selt = C.R("selt")
            S.dma("sp", selt[:], sel_d, w=[r_selt], key="selt")
            ystg = [[(C.sb("ystg%d_%d" % (i, j), [128, HALF], BF16), C.R("ystg%d_%d" % (i, j))) for j in range(2)] for i in range(2)]
        xT = C.sb("xT", [128, 8, HALF], F32)
        r_x = [C.R("xT%d" % cc) for cc in range(8)]
        yT = C.sb("yT", [128, 8, HALF], BF16)
        r_y = [C.R("yT%d" % cc) for cc in range(8)]
        hT = yT
        r_h = r_y
        aT = C.sb("aT", [128, NFC, HALF], BF16)
        r_a = [C.R("aT%d" % f) for f in range(NFC)]
        wo = C.sb("wo", [128, 8, D], BF16)
        r_wo = C.R("wo")
        wA = [(C.sb("wA%d" % i, [128, 8, 256], F32), C.R("wA%d" % i)) for i in range(2)]
        wi = [(C.sb("wi%d" % i, [128, 8, 512], BF16), C.R("wi%d" % i)) for i in range(2)]
        wo2 = [(C.sb("wo2%d" % i, [128, NFC, 256], BF16), C.R("wo2%d" % i)) for i in range(2)]
        P = {
            "st0": PB[6], "st1": PB[7],
            "sq": ([C.sb("sq%d" % i, [128, 512], F32) for i in range(2)], [C.R("sq%d" % i) for i in range(2)]),
            "mean": (C.sb("mean", [128, 512], F32), C.R("mean")),
            "rstd": (C.sb("rstd", [128, 512], F32), C.R("rstd")),
            "tt": ([C.sb("tt%d" % i, [128, 512], F32) for i in range(2)], [C.R("tt%d" % i) for i in range(2)]),
            "eps": EPS_P,
        }
        sl = [C.sb("sl%d" % i, [128, 512], F32) for i in range(2)]
        r_sl = [C.R("sl%d" % i) for i in range(2)]
        mods, r_mods = compute_mods(C, cT_d, adaw_d, adab_d, 4, wA, PB[5][0], PB[5][1])
        lng = C.sb("lng", [128, 16], F32)
        lnb = C.sb("lnb", [128, 16], F32)
        r_ln = C.R("ln")
        S.dma("sp", lng[:], lng_d, w=[r_ln], key="lng")
        S.dma("sp", lnb[:], lnb_d, w=[r_ln], key="lnb")
        cols = C.sb("cols", [128, 48], F32)
        r_cols = C.R("cols")
        S.op("dve", lambda e: e.tensor_scalar(out=cols[:, 0:8], in0=mods[:, 0:8], scalar1=1.0 / ALPHA, scalar2=None, op0=ALU.mult), r=[r_mods], w=[r_cols])
        S.op("dve", lambda e: e.tensor_scalar(out=cols[:, 8:16], in0=mods[:, 24:32], scalar1=1.0 / ALPHA, scalar2=None, op0=ALU.mult), r=[r_mods], w=[r_cols])
        S.op("dve", lambda e: e.tensor_scalar(out=cols[:, 32:40], in0=mods[:, 16:24], scalar1=1.0, scalar2=None, op0=ALU.add), r=[r_mods], w=[r_cols])
        S.op("dve", lambda e: e.tensor_tensor(out=cols[:, 16:24], in0=lng[:, 0:8], in1=cols[:, 32:40], op=ALU.mult), r=[r_ln, r_cols], w=[r_cols])
        S.op("dve", lambda e: e.tensor_tensor(out=cols[:, 24:32], in0=lnb[:, 0:8], in1=cols[:, 32:40], op=ALU.mult), r=[r_ln, r_cols], w=[r_cols])
        S.op("dve", lambda e: e.tensor_tensor(out=cols[:, 24:32], in0=cols[:, 24:32], in1=mods[:, 8:16], op=ALU.add), r=[r_mods, r_cols], w=[r_cols])
        rc = [r_cols, r_ln]
        S.dma("pool", wo[:], wo_d.rearrange("(kc p) n -> p kc n", p=128), w=[r_wo], key="wo")
        xsrc = xT_d.rearrange("(kc p) t -> p kc t", p=128)
        if sel_d is None:
            ysrc_ = yT_d.rearrange("(kc p) t -> p kc t", p=128)
            ysrc_fn = lambda cc, a, n: ysrc_[:, cc, a:a + n]
        else:
            ysrc_fn = lambda cc, a, n: yT_d[cc % 4, cc // 4, :, a:a + n]
        osrc = oT_d.rearrange("(kc p) t -> p kc t", p=128)
        winr = win_d.rearrange("(kc p) n -> p kc n", p=128)
        woutr = wout_d.rearrange("(fc p) n -> p fc n", p=128)
        tails = []
        pbi = 0
        for half in range(NT // HALF):
            t0 = half * HALF
            for cc in range(8):
                if sel_d is None:
                    S.dma("sp", yT[:, cc, :], ysrc_fn(cc, t0, HALF), w=[r_y[cc]], key="yT%d" % cc)
                else:
                    (ya, r_ya), (yb, r_yb) = ystg[cc % 2]
                    S.dma("sp", ya[:], ysrc_fn(cc, t0, HALF), w=[r_ya], key="ysa%d" % (cc % 2))
                    S.dma("sp", yb[:], ysrc_fn(cc, NT + t0, HALF), w=[r_yb], key="ysb%d" % (cc % 2))
                    S.op("dve", lambda e, ya=ya: e.tensor_scalar(out=ya[:], in0=ya[:], scalar1=selt[:, 0:1], scalar2=None, op0=ALU.mult), r=[r_ya, r_selt], w=[r_ya])
                    S.op("dve", lambda e, ya=ya, yb=yb, cc=cc: e.scalar_tensor_tensor(out=yT[:, cc, :], in0=yb[:], scalar=selt[:, 1:2], in1=ya[:], op0=ALU.mult, op1=ALU.add), r=[r_ya, r_yb, r_selt], w=[r_y[cc]])
                S.dma("sp", xT[:, cc, :], xsrc[:, cc, t0:t0 + HALF], w=[r_x[cc]], key="xT%d" % cc)
            for g in range(NG):
                n0 = g * 512
                for cc in range(8):
                    pb, r_pb = PB[pbi % 4]
                    pbi += 1
                    for kc in range(8):
                        S.op("pe", lambda e, pb=pb, cc=cc, kc=kc, n0=n0: e.matmul(pb[:], wo[:, kc, cc * 128:(cc + 1) * 128], yT[:, kc, n0:n0 + 512], start=(kc == 0), stop=(kc == 7)), r=[r_wo] + r_y, w=[r_pb])
                    S.op("dve", lambda e, pb=pb, cc=cc, n0=n0: e.scalar_tensor_tensor(out=xT[:, cc, n0:n0 + 512], in0=pb[:], scalar=cols[:, cc:cc + 1], in1=xT[:, cc, n0:n0 + 512], op0=ALU.mult, op1=ALU.add), r=[r_pb, r_x[cc], r_cols], w=[r_x[cc]])
            for g in range(NG):
                n0 = g * 512
                ln_group(C, xT, r_x, g, n0, None, None, rc,
                         [(xT, r_x, lng, 0, lnb, 0, "act"), (hT, r_h, cols, 16, cols, 24, "act")], P)
            if debug and half == 0:
                tails.append(S.dma("sp", dbg_cols, cols[:], r=[r_cols], key="dbg0"))
                tails.append(S.dma("sp", dbg_mods, mods[:], r=[r_mods], key="dbg1"))
                tails.append(S.dma("sp", dbg_x1.rearrange("(kc p) t -> p kc t", p=128), xT[:], r=r_x, key="dbg2"))
                tails.append(S.dma("sp", dbg_h.rearrange("(kc p) t -> p kc t", p=128), hT[:], r=r_h, key="dbg3"))
            for blk in range(NFC // 2):
                wt, r_wt = wi[blk % 2]
                c0 = blk * 256
                S.dma("pool", wt[:, :, 0:256], winr[:, :, c0:c0 + 256], w=[r_wt], key="wi%d" % (blk % 2))
                S.dma("pool", wt[:, :, 256:512], winr[:, :, DFF + c0:DFF + c0 + 256], w=[r_wt], key="wi%d" % (blk % 2))
                for j in range(2):
                    fc = blk * 2 + j
                    for g in range(NG):
                        n0 = g * 512
                        pg, r_pg = PB[pbi % 4]
                        pu, r_pu = PB[(pbi + 1) % 4]
                        pbi += 2
                        for kc in range(8):
                            S.op("pe", lambda e, pg=pg, wt=wt, j=j, kc=kc, n0=n0: e.matmul(pg[:], wt[:, kc, j * 128:(j + 1) * 128], hT[:, kc, n0:n0 + 512], start=(kc == 0), stop=(kc == 7)), r=[r_wt] + r_h, w=[r_pg])
                        for kc in range(8):
                            S.op("pe", lambda e, pu=pu, wt=wt, j=j, kc=kc, n0=n0: e.matmul(pu[:], wt[:, kc, 256 + j * 128:256 + (j + 1) * 128], hT[:, kc, n0:n0 + 512], start=(kc == 0), stop=(kc == 7)), r=[r_wt] + r_h, w=[r_pu])
                        k = (fc * NG + g) % 2
                        S.op("act", lambda e, pg=pg, k=k: e.activation(out=sl[k][:], in_=pg[:], func=AF.Silu), r=[r_pg], w=[r_sl[k]])
                        S.op("dve", lambda e, pu=pu, k=k, fc=fc, n0=n0: e.tensor_tensor(out=aT[:, fc, n0:n0 + 512], in0=pu[:], in1=sl[k][:], op=ALU.mult), r=[r_pu, r_sl[k]], w=[r_a[fc]])
            if debug and half == 0:
                tails.append(S.dma("sp", dbg_a.rearrange("(kc p) t -> p kc t", p=128), aT[:], r=r_a, key="dbg4"))
            for blk in range(4):
                wt, r_wt = wo2[blk % 2]
                S.dma("pool", wt[:], woutr[:, :, blk * 256:(blk + 1) * 256], w=[r_wt], key="wo2%d" % (blk % 2))
                for j in range(2):
                    cc = blk * 2 + j
                    for g in range(NG):
                        n0 = g * 512
                        pb, r_pb = PB[pbi % 4]
                        pbi += 1
                        for fc in range(NFC):
                            S.op("pe", lambda e, pb=pb, wt=wt, j=j, fc=fc, n0=n0: e.matmul(pb[:], wt[:, fc, j * 128:(j + 1) * 128], aT[:, fc, n0:n0 + 512], start=(fc == 0), stop=(fc == NFC - 1)), r=[r_wt, r_a[fc]], w=[r_pb])
                        S.op("dve", lambda e, pb=pb, cc=cc, n0=n0: e.scalar_tensor_tensor(out=xT[:, cc, n0:n0 + 512], in0=pb[:], scalar=cols[:, 8 + cc:9 + cc], in1=xT[:, cc, n0:n0 + 512], op0=ALU.mult, op1=ALU.add), r=[r_pb, r_x[cc], r_cols], w=[r_x[cc]])
            for g in range(NG):
                n0 = g * 512
                ln_group(C, xT, r_x, g, n0, None, None, rc, [(xT, r_x, lng, 8, lnb, 8, "act")], P)
            for cc in range(8):
                o = S.dma("sp", osrc[:, cc, t0:t0 + HALF], xT[:, cc, :], r=[r_x[cc]], w=[], key="oT%d" % cc)
                tails.append(o)
        if after is not None:
            after(C, tails)
        S.emit(st, tail_waits=tails)


T = 4096
NH = 8
HD = 64
QK_EPS = 1e-6
NEG = -30000.0
import os
LAG = int(os.environ.get("FOX_LAG", "2"))
FOX_DVE = int(os.environ.get("FOX_DVE", "1"))


def build_fox(TT=T, debug=False):
    nc = bass.Bass("TRN2", target_bir_lowering=False)
    dt = lambda n, s, d, k="ExternalInput": nc.dram_tensor(n, list(s), d, kind=k).ap()
    ins = fox_decl(dt, TT)
    yg_d = dt("ygT", [512, TT], BF16, "ExternalOutput")
    fox_phase(nc, "", ins, yg_d, TT, debug=debug)
    return nc


def fox_decl(dt, TT=T, sfx=""):
    return dict(
        x=dt("x" + sfx, [TT, D], F32), cT=dt("cT" + sfx, [128, 8], F32), adaw=dt("adaw" + sfx, [D, 2 * D], F32), adab=dt("adab" + sfx, [128, 16], F32),
        wq=dt("wq" + sfx, [D, 512], F32), wk=dt("wk" + sfx, [D, 512], F32), wv=dt("wv" + sfx, [D, 512], F32), wf=dt("wf" + sfx, [D, 8], F32), wg=dt("wg" + sfx, [D, 512], F32),
        bfb=dt("bfb" + sfx, [128, 8], F32), qgb=dt("qgb" + sfx, [128, 512], F32), kgb=dt("kgb" + sfx, [128, 512], F32))


def fox_phase(nc, pfx, ins, yg_d, TT=T, debug=False, after=None):
    dt = lambda n, s, d, k="ExternalInput": nc.dram_tensor(n, list(s), d, kind=k).ap()
    x_d, cT_d, adaw_d, adab_d = ins["x"], ins["cT"], ins["adaw"], ins["adab"]
    wq_d, wk_d, wv_d, wf_d, wg_d = ins["wq"], ins["wk"], ins["wv"], ins["wf"], ins["wg"]
    bf_d, qg_d, kg_d = ins["bfb"], ins["qgb"], ins["kgb"]
    NGR = TT // 512
    NTL = TT // 128
    if debug:
        dbg_h = dt("dbg_h", [D, 512], BF16, "ExternalOutput")
        dbg_q = dt("dbg_q", [128, 4, 512], BF16, "ExternalOutput")
        dbg_k = dt("dbg_k", [128, 4, TT], BF16, "ExternalOutput")
        dbg_F = dt("dbg_F", [128, NTL, 8], F32, "ExternalOutput")
        dbg_v = dt("dbg_v", [128, NTL, 8, 65], BF16, "ExternalOutput")
    with ExitStack() as st:
        C = Ctx(nc, st, pfx)
        S = C.S
        make_consts(C)
        identb = C.sb("identb", [128, 128], BF16)
        r_identb = C.R("identb")
        S.op("dve", lambda e: e.tensor_copy(out=identb[:], in_=C.ident[:]), r=[C.r_ident], w=[r_identb])
        tri = C.sb("tri", [128, 128], F32)
        r_tri = C.R("tri")
        S.op("pool", lambda e: e.memset(tri[:], 1.0), w=[r_tri])
        S.op("pool", lambda e: e.affine_select(out=tri[:], in_=tri[:], pattern=[[1, 128]], compare_op=ALU.is_ge, fill=0.0, base=0, channel_multiplier=-1), r=[r_tri], w=[r_tri])
        selA = C.sb("selA", [16, 8, 128], F32)
        selB = C.sb("selB", [16, 8, 128], F32)
        sel = C.sb("sel", [16, 8, 128], BF16)
        r_sel = C.R("sel")
        S.op("pool", lambda e: e.memset(selA[:], 1.0), w=[r_sel])
        S.op("pool", lambda e: e.memset(selB[:], 1.0), w=[r_sel])
        S.op("pool", lambda e: e.affine_select(out=selA[:], in_=selA[:], pattern=[[-1, 8], [0, 128]], compare_op=ALU.is_equal, fill=0.0, base=0, channel_multiplier=1), r=[r_sel], w=[r_sel])
        S.op("pool", lambda e: e.affine_select(out=selB[:], in_=selB[:], pattern=[[-1, 8], [0, 128]], compare_op=ALU.is_equal, fill=0.0, base=-8, channel_multiplier=1), r=[r_sel], w=[r_sel])
        S.op("dve", lambda e: e.tensor_tensor(out=sel[:], in0=selA[:], in1=selB[:], op=ALU.add), r=[r_sel], w=[r_sel])
        maskf = C.sb("maskf", [128, 4, 512], F32)
        maskb = C.sb("maskb", [128, 4, 512], BF16)
        r_mask = C.R("mask")
        S.op("pool", lambda e: e.memset(maskf[:], 0.0), w=[r_mask])
        for j in range(4):
            S.op("pool", lambda e, j=j: e.affine_select(out=maskf[:, j, :], in_=maskf[:, j, :], pattern=[[1, 512]], compare_op=ALU.is_ge, fill=NEG, base=-128 * j, channel_multiplier=-1), r=[r_mask], w=[r_mask])
        S.op("dve", lambda e: e.tensor_copy(out=maskb[:], in_=maskf[:]), r=[r_mask], w=[r_mask])
        PB = [(C.ps("pb%d" % i, [128, 512]), C.R("pb%d" % i, True)) for i in range(7)]
        pbb = C.ps("pbb", [128, 1024], BF16)
        r_pbb = C.R("pbb", True)
        wA = [(C.sb("wA%d" % i, [128, 8, 256], F32), C.R("wA%d" % i)) for i in range(2)]
        mods, r_mods = compute_mods(C, cT_d, adaw_d, adab_d, 2, wA, PB[6][0], PB[6][1])
        cols = C.sb("cols", [128, 8], F32)
        r_cols = C.R("cols")
        S.op("dve", lambda e: e.tensor_scalar(out=cols[:], in0=mods[:, 8:16], scalar1=1.0, scalar2=None, op0=ALU.add), r=[r_mods], w=[r_cols])
        wts = {}
        r_w = C.R("wts")
        for nm, d_ in (("wq", wq_d), ("wk", wk_d), ("wv", wv_d), ("wg", wg_d)):
            wts[nm] = C.sb(nm, [128, 8, 512], BF16)
            S.dma("pool", wts[nm][:], d_.rearrange("(kc p) n -> p kc n", p=128), w=[r_w], key=nm)
        wf = C.sb("wf", [128, 8, 8], BF16)
        S.dma("pool", wf[:], wf_d.rearrange("(kc p) n -> p kc n", p=128), w=[r_w], key="wf")
        bfb = C.sb("bfb", [128, 8], F32)
        qgb = C.sb("qgb", [128, 512], F32)
        kgb = C.sb("kgb", [128, 512], F32)
        r_par = C.R("par")
        S.dma("sp", bfb[:], bf_d, w=[r_par], key="bfb")
        S.dma("sp", qgb[:], qg_d, w=[r_par], key="qgb")
        S.dma("sp", kgb[:], kg_d, w=[r_par], key="kgb")
        xt = C.sb("xt", [128, 4, D], F32)
        r_xt = C.R("xt")
        hT = C.sb("hT", [128, 8, 512], BF16)
        r_hT = [C.R("hT%d" % k) for k in range(8)]
        kT = C.sb("kT", [128, 4, TT], BF16)
        r_kT = [C.R("kT%d" % g) for g in range(NGR)]
        qT = C.sb("qT", [128, 4, 512], BF16)
        r_qT = C.R("qT")
        Va = C.sb("Va", [128, NTL, NH, 65], BF16)
        r_V = [C.R("V%d" % g) for g in range(NGR)]
        r_Vones = C.R("Vones")
        S.op("pool", lambda e: e.memset(Va[:, :, :, 64:65], 1.0), w=[r_Vones])
        eg = C.sb("eg", [64, NH, 512], F32)
        r_eg = [C.R("eg%d" % h) for h in range(NH)]
        Fcol = C.sb("Fcol", [128, NTL, NH], F32)
        r_F = C.R("Fcol")
        carry = C.sb("carry", [128, NTL + 1, NH], F32)
        r_carry = C.R("carry")
        S.op("dve", lambda e: e.memset(carry[:, 0, :], 0.0), w=[r_carry])
        biasG = C.sb("biasG", [128, NTL, NH], F32)
        r_bias = C.R("biasG")
        FqT = C.sb("FqT", [16, 512], BF16)
        r_FqT = C.R("FqT")
        sq = C.sb("sq", [128, 512], F32); r_sq = C.R("sq")
        tmp = C.sb("tmp", [128, 512], F32); r_tmp = C.R("tmp")
        qtok = C.sb("qtok", [128, 512], BF16); r_qtok = C.R("qtok")
        ktok = C.sb("ktok", [128, 512], BF16); r_ktok = C.R("ktok")
        ssq = C.sb("ssq", [128, 16], F32); r_ssq = C.R("ssq")
        zf = C.sb("zf", [128, 8], F32); r_zf = C.R("zf")
        fq = C.sb("fq", [128, 8], F32); r_fq = C.R("fq")
        fq16 = C.sb("fq16", [128, 16], BF16); r_fq16 = C.R("fq16")
        PT = [(C.sb("PT%d" % i, [128, 512], BF16), C.R("PT%d" % i)) for i in range(4)]
        oT = C.sb("oT", [64, 512], F32); r_oT = C.R("oT")
        dn = C.sb("dn", [128, 512], F32); r_dn = C.R("dn")
        wv_ = C.sb("wv_", [64, 512], F32); r_wv_ = C.R("wv_")
        yg = [(C.sb("yg%d" % i, [64, 512], BF16), C.R("yg%d" % i)) for i in range(2)]
        tails = []
        xsrc = x_d.rearrange("(i p) d -> p i d", p=128)
        ps_i = 0
        for G in range(NGR):
            if G == 0:
                S.dma("sp", xt[:], xsrc[:, 0:4, :], w=[r_xt], key="xt")
            for kc in range(8):
                pb, r_pb = PB[kc % 2]
                for i in range(4):
                    S.op("pe", lambda e, pb=pb, i=i, kc=kc: e.transpose(pb[:, i * 128:(i + 1) * 128], xt[:, i, kc * 128:(kc + 1) * 128], C.ident[:]), r=[r_xt, C.r_ident], w=[r_pb])
                if kc % 2 == 0:
                    S.op("act", lambda e, pb=pb, kc=kc: e.activation(out=hT[:, kc, :], in_=pb[:], func=AF.Identity, scale=cols[:, kc:kc + 1], bias=mods[:, kc:kc + 1]), r=[r_pb, r_cols, r_mods], w=[r_hT[kc]])
                else:
                    S.op("dve", lambda e, pb=pb, kc=kc: e.tensor_scalar(out=hT[:, kc, :], in0=pb[:], scalar1=cols[:, kc:kc + 1], scalar2=mods[:, kc:kc + 1], op0=ALU.mult, op1=ALU.add), r=[r_pb, r_cols, r_mods], w=[r_hT[kc]])
            if debug and G == 0:
                tails.append(S.dma("sp", dbg_h.rearrange("(kc p) t -> p kc t", p=128), hT[:], r=r_hT, key="dbg0"))
            if G + 1 < NGR:
                S.dma("sp", xt[:], xsrc[:, 4 * (G + 1):4 * (G + 1) + 4, :], w=[r_xt], key="xt")
            for h in range(NH):
                pb, r_pb = PB[2 + h % 2]
                for kc in range(8):
                    S.op("pe", lambda e, pb=pb, h=h, kc=kc: e.matmul(pb[0:64, :], wts["wg"][:, kc, h * 64:(h + 1) * 64], hT[:, kc, :], start=(kc == 0), stop=(kc == 7)), r=[r_w, r_hT[kc]], w=[r_pb])
                S.op("act", lambda e, pb=pb, h=h: e.activation(out=eg[:, h, :], in_=pb[0:64, :], func=AF.Exp, scale=-1.0), r=[r_pb], w=[r_eg[h]])
            for i in range(4):
                tl_ = 4 * G + i
                (pq, r_pq), (pk, r_pk), (pv, r_pv), (pf, r_pf) = PB[2], PB[3], PB[4], PB[5]
                for kc in range(8):
                    lw = hT[:, kc, i * 128:(i + 1) * 128]
                    S.op("pe", lambda e, lw=lw, kc=kc: e.matmul(pq[:], lw, wts["wq"][:, kc, :], start=(kc == 0), stop=(kc == 7)), r=[r_w, r_hT[kc]], w=[r_pq])
                    S.op("pe", lambda e, lw=lw, kc=kc: e.matmul(pk[:], lw, wts["wk"][:, kc, :], start=(kc == 0), stop=(kc == 7)), r=[r_w, r_hT[kc]], w=[r_pk])
                    S.op("pe", lambda e, lw=lw, kc=kc: e.matmul(pv[:], lw, wts["wv"][:, kc, :], start=(kc == 0), stop=(kc == 7)), r=[r_w, r_hT[kc]], w=[r_pv])
                    S.op("pe", lambda e, lw=lw, kc=kc: e.matmul(pf[:, 0:8], lw, wf[:, kc, :], start=(kc == 0), stop=(kc == 7)), r=[r_w, r_hT[kc]], w=[r_pf])
                for which, (pp, r_pp), gb, tok, r_tok, eps_, mul_ in (("q", (pq, r_pq), qgb, qtok, r_qtok, 64.0 * QK_EPS, 1.0), ("k", (pk, r_pk), kgb, ktok, r_ktok, QK_EPS, 1.0 / 64.0)):
                    o8 = 0 if which == "q" else 8
                    S.op("act", lambda e, pp=pp: e.activation(out=sq[:], in_=pp[:], func=AF.Square), r=[r_pp], w=[r_sq])
                    S.op("dve", lambda e, o8=o8: e.tensor_reduce(out=ssq[:, o8:o8 + 8], in_=sq[:].rearrange("p (h d) -> p h d", d=64), axis=AX.X, op=ALU.add), r=[r_sq], w=[r_ssq])
                    S.op("dve", lambda e, o8=o8, eps_=eps_, mul_=mul_: e.tensor_scalar(out=ssq[:, o8:o8 + 8], in0=ssq[:, o8:o8 + 8], scalar1=mul_, scalar2=eps_, op0=ALU.mult, op1=ALU.add), r=[r_ssq], w=[r_ssq])
                    S.op("act", lambda e, o8=o8: e.activation(out=ssq[:, o8:o8 + 8], in_=ssq[:, o8:o8 + 8], func=AF.Ln), r=[r_ssq], w=[r_ssq])
                    S.op("act", lambda e, o8=o8: e.activation(out=ssq[:, o8:o8 + 8], in_=ssq[:, o8:o8 + 8], func=AF.Exp, scale=-0.5), r=[r_ssq], w=[r_ssq])
                    S.op("dve", lambda e, pp=pp, o8=o8: e.tensor_tensor(out=tmp[:].rearrange("p (h d) -> p h d", d=64), in0=pp[:].rearrange("p (h d) -> p h d", d=64), in1=ssq[:, o8:o8 + 8].unsqueeze(2).to_broadcast([128, 8, 64]), op=ALU.mult), r=[r_pp, r_ssq], w=[r_tmp])
                    S.op("dve", lambda e, gb=gb, tok=tok: e.tensor_tensor(out=tok[:], in0=tmp[:], in1=gb[:], op=ALU.mult), r=[r_tmp, r_par], w=[r_tok])
                S.op("act", lambda e, tl_=tl_: e.copy(out=Va[:, tl_, :, 0:64], in_=pv[:].rearrange("p (h d) -> p h d", d=64)), r=[r_pv, r_Vones], w=[r_V[G]])
                S.op("dve", lambda e: e.tensor_tensor(out=zf[:], in0=pf[:, 0:8], in1=bfb[:], op=ALU.add), r=[r_pf, r_par], w=[r_zf])
                S.op("act", lambda e: e.activation(out=zf[:], in_=zf[:], func=AF.Exp, scale=-1.0), r=[r_zf], w=[r_zf])
                S.op("act", lambda e: e.activation(out=zf[:], in_=zf[:], func=AF.Ln, bias=1.0), r=[r_zf], w=[r_zf])
                pc, r_pc = PB[6]
                S.op("pe", lambda e: e.matmul(pc[:, 0:8], tri[:], zf[:], start=True, stop=True), r=[r_tri, r_zf], w=[r_pc])
                S.op("pe", lambda e: e.matmul(pc[:, 8:16], C.ones[:], zf[:], start=True, stop=True), r=[C.r_ones, r_zf], w=[r_pc])
                S.op("dve", lambda e, tl_=tl_: e.tensor_tensor(out=Fcol[:, tl_, :], in0=carry[:, tl_, :], in1=pc[:, 0:8], op=ALU.subtract), r=[r_carry, r_pc], w=[r_F])
                S.op("dve", lambda e, tl_=tl_: e.tensor_tensor(out=carry[:, tl_ + 1, :], in0=carry[:, tl_, :], in1=pc[:, 8:16], op=ALU.subtract), r=[r_carry, r_pc], w=[r_carry])
                S.op("dve", lambda e, tl_=tl_, G=G: e.tensor_tensor(out=fq[:], in0=Fcol[:, tl_, :], in1=carry[:, 4 * G, :], op=ALU.subtract), r=[r_F, r_carry], w=[r_fq])
                S.op("dve", lambda e: e.tensor_copy(out=fq16[:, 0:8], in_=fq[:]), r=[r_fq], w=[r_fq16])
                S.op("dve", lambda e: e.tensor_tensor(out=fq[:], in0=fq[:], in1=fq16[:, 0:8], op=ALU.subtract), r=[r_fq, r_fq16], w=[r_fq])
                S.op("dve", lambda e: e.tensor_copy(out=fq16[:, 8:16], in_=fq[:]), r=[r_fq], w=[r_fq16])
                for p in range(4):
                    S.op("pe", lambda e, p=p: e.transpose(pbb[:, p * 128:(p + 1) * 128], qtok[:, p * 128:(p + 1) * 128], identb[:]), r=[r_qtok, r_identb], w=[r_pbb])
                for p in range(4):
                    S.op("pe", lambda e, p=p: e.transpose(pbb[:, 512 + p * 128:512 + (p + 1) * 128], ktok[:, p * 128:(p + 1) * 128], identb[:]), r=[r_ktok, r_identb], w=[r_pbb])
                S.op("act", lambda e, i=i: e.copy(out=qT[:, :, i * 128:(i + 1) * 128], in_=pbb[:, 0:512].rearrange("p (a t) -> p a t", t=128)), r=[r_pbb], w=[r_qT])
                S.op("dve", lambda e, tl_=tl_: e.tensor_copy(out=kT[:, :, tl_ * 128:(tl_ + 1) * 128], in_=pbb[:, 512:1024].rearrange("p (a t) -> p a t", t=128)), r=[r_pbb], w=[r_kT[G]])
                S.op("pe", lambda e: e.transpose(pbb[0:16, 0:128], fq16[:, 0:16], identb[:]), r=[r_fq16, r_identb], w=[r_pbb])
                S.op("dve", lambda e, i=i: e.tensor_copy(out=FqT[:, i * 128:(i + 1) * 128], in_=pbb[0:16, 0:128]), r=[r_pbb], w=[r_FqT])
            nkb = 4 * G + 4
            for h in range(NH):
                S.op("dve", lambda e, h=h, nkb=nkb, G=G: e.tensor_scalar(out=biasG[:, 0:nkb, h], in0=Fcol[:, 0:nkb, h], scalar1=-1.0, scalar2=carry[:, 4 * G, h:h + 1], op0=ALU.mult, op1=ALU.add), r=[r_F, r_carry], w=[r_bias])
            tiles = [(2 * p + e, kb) for p in range(NH // 2) for kb in range(nkb) for e in range(2)]

            def emit_pv(h, kb, PTt, r_PT, nkb=nkb, G=G):
                pO, r_pO = PB[4 + h % 2]
                gk = kb // 4
                S.op("pe", lambda e, pO=pO, PTt=PTt, kb=kb, h=h, nkb=nkb: e.matmul(pO[0:65, :], Va[:, kb, h, :], PTt[:], start=(kb == 0), stop=(kb == nkb - 1)), r=[r_V[gk], r_Vones, r_PT], w=[r_pO])
                if kb == nkb - 1:
                    pB_, r_pB = PB[6]
                    ygt, r_yg = yg[h % 2]
                    S.op("act", lambda e, pO=pO: e.copy(out=oT[:], in_=pO[0:64, :]), r=[r_pO], w=[r_oT])
                    S.op("dve", lambda e, pO=pO: e.tensor_copy(out=dn[64:65, :], in_=pO[64:65, :]), r=[r_pO], w=[r_dn])
                    S.op("pe", lambda e: e.matmul(pB_[0:64, :], C.ones[64:65, 0:64], dn[64:65, :], start=True, stop=True), r=[C.r_ones, r_dn], w=[r_pB])
                    S.op("dve", lambda e, h=h: e.scalar_tensor_tensor(out=wv_[:], in0=eg[:, h, :], scalar=1.0, in1=pB_[0:64, :], op0=ALU.add, op1=ALU.mult), r=[r_eg[h], r_pB], w=[r_wv_])
                    S.op("dve", lambda e: e.reciprocal(out=wv_[:], in_=wv_[:]), r=[r_wv_], w=[r_wv_])
                    S.op("dve", lambda e, ygt=ygt: e.tensor_tensor(out=ygt[:], in0=oT[:], in1=wv_[:], op=ALU.mult), r=[r_oT, r_wv_], w=[r_yg])
                    tails.append(S.dma("sp", yg_d[h * 64:(h + 1) * 64, G * 512:(G + 1) * 512], ygt[:], r=[r_yg], key="yg%d" % (h % 2)))

            pendq = []
            for (h, kb) in tiles:
                p_, e_ = h // 2, h % 2
                pS, r_pS = PB[ps_i % 4]
                PTt, r_PT = PT[ps_i % 4]
                ps_i += 1
                diag = kb >= 4 * G
                gk = kb // 4
                if FOX_DVE:
                    fqb, r_fqb = ((tmp, r_tmp), (sq, r_sq))[h % 2]
                    if kb == 0:
                        pB_, r_pB = PB[6]
                        S.op("pe", lambda e, h=h: e.matmul(pB_[:], sel[:, h, :], FqT[:], start=True, stop=True), r=[r_sel, r_FqT], w=[r_pB])
                        S.op("dve", lambda e, fqb=fqb: e.tensor_copy(out=fqb[:], in_=pB_[:]), r=[r_pB], w=[r_fqb])
                    S.op("pe", lambda e, pS=pS, p_=p_, e_=e_, kb=kb, diag=diag: e.matmul(pS[:], kT[e_ * 64:(e_ + 1) * 64, p_, kb * 128:(kb + 1) * 128], qT[e_ * 64:(e_ + 1) * 64, p_, :], start=True, stop=(not diag)), r=[r_kT[gk], r_qT], w=[r_pS])
                    if diag:
                        S.op("pe", lambda e, pS=pS, kb=kb, G=G: e.matmul(pS[:], identb[:], maskb[:, kb - 4 * G, :], start=False, stop=True), r=[r_identb, r_mask], w=[r_pS])
                    S.op("dve", lambda e, pS=pS, fqb=fqb: e.tensor_tensor(out=pS[:], in0=pS[:], in1=fqb[:], op=ALU.add), r=[r_pS, r_fqb], w=[r_pS])
                else:
                    S.op("pe", lambda e, pS=pS, p_=p_, e_=e_, kb=kb: e.matmul(pS[:], kT[e_ * 64:(e_ + 1) * 64, p_, kb * 128:(kb + 1) * 128], qT[e_ * 64:(e_ + 1) * 64, p_, :], start=True, stop=False), r=[r_kT[gk], r_qT], w=[r_pS])
                    S.op("pe", lambda e, pS=pS, h=h, diag=diag: e.matmul(pS[:], sel[:, h, :], FqT[:], start=False, stop=(not diag)), r=[r_sel, r_FqT], w=[r_pS])
                    if diag:
                        S.op("pe", lambda e, pS=pS, kb=kb, G=G: e.matmul(pS[:], identb[:], maskb[:, kb - 4 * G, :], start=False, stop=True), r=[r_identb, r_mask], w=[r_pS])
                S.op("act", lambda e, pS=pS, PTt=PTt, kb=kb, h=h: e.activation(out=PTt[:], in_=pS[:], func=AF.Exp, bias=biasG[:, kb, h:h + 1], scale=1.0), r=[r_pS, r_bias], w=[r_PT])
                pendq.append((h, kb, PTt, r_PT))
                if h % 2 == 1:
                    while len(pendq) > 2:
                        emit_pv(*pendq.pop(0))
            while pendq:
                emit_pv(*pendq.pop(0))
            if debug and G == 0:
                tails.append(S.dma("sp", dbg_q, qT[:], r=[r_qT], key="dbg1"))
        if debug:
            tails.append(S.dma("sp", dbg_k, kT[:], r=r_kT, key="dbg2"))
            tails.append(S.dma("sp", dbg_F, Fcol[:], r=[r_F], key="dbg3"))
            tails.append(S.dma("sp", dbg_v, Va[:], r=r_V + [r_Vones], key="dbg4"))
        if after is not None:
            after(C, tails)
        S.emit(st, tail_waits=tails)

import math, os
SEQ_VAR = 0

NH = 8
HD = 64
C0 = math.exp(-0.5)
GN_EPS = 64 * 1e-5


class TR:
    def __init__(self, C, name, shape, dt, psum=False):
        self.t = C.ps(name, shape, dt) if psum else C.sb(name, shape, dt)
        self.r = C.R(name, excl=psum)


class _Stop(Exception):
    pass


def build_rwkv(TT=4096, debug=False, stop=None):
    nc = bass.Bass("TRN2", target_bir_lowering=False)
    dt = lambda n, s, d, k="ExternalInput": nc.dram_tensor(n, list(s), d, kind=k).ap()
    xT_d = dt("xT", [D, TT], F32)
    ins = rwkv_decl(dt)
    yT_d = dt("yT", [512, TT], BF16, "ExternalOutput")
    xv = xT_d.rearrange("(kc p) t -> p kc t", p=128)
    rwkv_phase(nc, "", lambda t0, n: xv[:, :, t0:t0 + n], ins, yT_d, TT, debug=debug, stop=stop)
    return nc


RW_BC = ["w0b", "a0b", "kkb", "kab", "rkb", "lngb", "lnbb"]


def rwkv_decl(dt, sfx=""):
    d_ = dict(cT=dt("cT" + sfx, [128, 8], F32), adaw=dt("adaw" + sfx, [D, 2 * D], F32), adab=dt("adab" + sfx, [128, 16], F32), mu=dt("mu" + sfx, [128, 48], F32),
              wr=dt("wr" + sfx, [D, 512], F32), wk=dt("wk" + sfx, [D, 512], F32), wv=dt("wv" + sfx, [D, 512], F32),
              w1=dt("w1" + sfx, [D, 64], F32), a1=dt("a1" + sfx, [D, 64], F32), g1=dt("g1" + sfx, [D, 128], F32),
              w2=dt("w2" + sfx, [64, 512], F32), a2=dt("a2" + sfx, [64, 512], F32), g2=dt("g2" + sfx, [128, 512], F32))
    for n in RW_BC:
        d_[n] = dt(n + sfx, [128, 512], F32)
    return d_


def rwkv_phase(nc, pfx, xsrc_fn, ins, yT_d, TT=4096, debug=False, stop=None, after=None):
    dt = lambda n, s, d, k="ExternalInput": nc.dram_tensor(n, list(s), d, kind=k).ap()
    cT_d, adaw_d, adab_d, mu_d = ins["cT"], ins["adaw"], ins["adab"], ins["mu"]
    wr_d, wk_d, wv_d, w1_d, a1_d, g1_d, w2_d, a2_d, g2_d = [ins[k] for k in ("wr", "wk", "wv", "w1", "a1", "g1", "w2", "a2", "g2")]
    bc_names = RW_BC
    bc_d = {n: ins[n] for n in bc_names}
    GS = 256
    NTG = GS // 128
    NGR = TT // GS
    dbg = {}
    if debug:
        for n in ["r", "kp", "kkn", "a", "sw", "Y", "bonus", "Gm", "yn"]:
            dbg[n] = dt("dbg_" + n, [128, 512], F32, "ExternalOutput")
        dbg["S"] = dt("dbg_S", [128, 4, 64], F32, "ExternalOutput")
        for n, shp in (("Arb", [128, 8, 128]), ("Aak", [128, 8, 128]), ("NT0", [128, 8, 128]), ("Z0", [128, 8, 128]), ("Zf", [128, 8, 128]), ("Y0", [128, 512]), ("RbT", [128, 4, 128]), ("Qm", [128, 4, 128]), ("Hm", [128, 4, 128]), ("gcol", [128, 8]), ("v", [128, 512]), ("ART", [128, 4, 2, 128]), ("KT", [128, 4, 128]), ("BT", [128, 4, 128]), ("Bh", [128, 512]), ("Kh", [128, 512])):
            dbg[n] = dt("dbg_" + n, shp, F32, "ExternalOutput")
    with ExitStack() as st:
        C = Ctx(nc, st, pfx)
        S = C.S
        make_consts(C)
        ident, ones = C.ident, C.ones
        r_ident, r_ones = C.r_ident, C.r_ones
        tails = []
        identb = TR(C, "identb", [128, 128], BF16)
        S.op("dve", lambda e: e.tensor_copy(out=identb.t[:], in_=ident[:]), r=[r_ident], w=[identb.r])
        U = TR(C, "U", [128, 128], F32)
        IU = TR(C, "IUf", [128, 128], F32)
        S.op("pool", lambda e: e.memset(U.t[:], 1.0), w=[U.r])
        S.op("pool", lambda e: e.affine_select(out=U.t[:], in_=U.t[:], pattern=[[1, 128]], compare_op=ALU.is_gt, fill=0.0, base=0, channel_multiplier=-1), r=[U.r], w=[U.r])
        S.op("pool", lambda e: e.memset(IU.t[:], 1.0), w=[IU.r])
        S.op("pool", lambda e: e.affine_select(out=IU.t[:], in_=IU.t[:], pattern=[[1, 128]], compare_op=ALU.is_ge, fill=0.0, base=0, channel_multiplier=-1), r=[IU.r], w=[IU.r])
        SLf = TR(C, "SLf", [128, 128], F32)
        S.op("pool", lambda e: e.memset(SLf.t[:], 1.0), w=[SLf.r])
        S.op("pool", lambda e: e.affine_select(out=SLf.t[:], in_=SLf.t[:], pattern=[[-1, 128]], compare_op=ALU.is_gt, fill=0.0, base=0, channel_multiplier=1), r=[SLf.r], w=[SLf.r])
        mask1 = TR(C, "mask1", [128, 128], F32)
        maskL = TR(C, "maskL", [128, 64], F32)
        idl = TR(C, "idl", [128, 64], F32)
        for ch in range(2):
            ps_ = slice(ch * 64, ch * 64 + 64)
            S.op("dve", lambda e, ps_=ps_: e.tensor_copy(out=mask1.t[ps_, 0:64], in_=U.t[ps_, ps_]), r=[U.r], w=[mask1.r])
            S.op("dve", lambda e, ps_=ps_: e.tensor_copy(out=mask1.t[ps_, 64:128], in_=IU.t[ps_, ps_]), r=[IU.r], w=[mask1.r])
            S.op("dve", lambda e, ps_=ps_: e.tensor_copy(out=maskL.t[ps_, :], in_=SLf.t[ps_, ps_]), r=[SLf.r], w=[maskL.r])
            S.op("dve", lambda e, ps_=ps_: e.tensor_copy(out=idl.t[ps_, :], in_=ident[ps_, ps_]), r=[r_ident], w=[idl.r])
        triC = TR(C, "triC", [128, 128], F32)
        blkC = TR(C, "blkC", [128, 128], F32)
        S.op("dve", lambda e: e.tensor_scalar(out=triC.t[:], in0=IU.t[:], scalar1=-C0, scalar2=None, op0=ALU.mult), r=[IU.r], w=[triC.r])
        S.op("dve", lambda e: e.memset(triC.t[0:64, 64:128], 0.0), w=[triC.r])
        S.op("dve", lambda e: e.memset(blkC.t[:], -C0), w=[blkC.r])
        S.op("dve", lambda e: e.memset(blkC.t[0:64, 64:128], 0.0), w=[blkC.r])
        S.op("dve", lambda e: e.memset(blkC.t[64:128, 0:64], 0.0), w=[blkC.r])
        avg = TR(C, "avg", [128, 1], F32)
        S.op("dve", lambda e: e.memset(avg.t[:], 1.0 / 64.0), w=[avg.r])
        PB = [TR(C, "pb%d" % i, [128, 512], F32, psum=True) for i in range(7)]
        pbb = TR(C, "pbb", [128, 1024], BF16, psum=True)
        wA = [(C.sb("wA%d" % i, [128, 8, 256], F32), C.R("wA%d" % i)) for i in range(2)]
        mods, r_mods = compute_mods(C, cT_d, adaw_d, adab_d, 2, wA, PB[6].t, PB[6].r)
        cols = TR(C, "cols", [128, 8], F32)
        S.op("dve", lambda e: e.tensor_scalar(out=cols.t[:], in0=mods[:, 8:16], scalar1=1.0, scalar2=None, op0=ALU.add), r=[r_mods], w=[cols.r])
        mu = TR(C, "mu", [128, 48], F32)
        S.dma("sp", mu.t[:], mu_d, w=[mu.r], key="mu")
        W = {}
        r_w = C.R("wts")
        for nm, d_, shp in (("wr", wr_d, [128, 8, 512]), ("wk", wk_d, [128, 8, 512]), ("wv", wv_d, [128, 8, 512]), ("w1", w1_d, [128, 8, 64]), ("a1", a1_d, [128, 8, 64]), ("g1", g1_d, [128, 8, 128])):
            W[nm] = C.sb(nm, shp, BF16)
            S.dma("pool", W[nm][:], d_.rearrange("(kc p) n -> p kc n", p=128), w=[r_w], key=nm)
        for nm, d_, shp in (("w2", w2_d, [64, 512]), ("a2", a2_d, [64, 512]), ("g2", g2_d, [128, 512])):
            W[nm] = C.sb(nm, shp, BF16)
            S.dma("pool", W[nm][:], d_, w=[r_w], key=nm)
        BC = {}
        r_bc = C.R("bc")
        for n in bc_names:
            BC[n] = C.sb(n, [128, 512], F32)
            S.dma("sp", BC[n][:], bc_d[n], w=[r_bc], key=n)
        xg = TR(C, "xg", [128, 8, GS + 1], F32)
        S.op("dve", lambda e: e.memset(xg.t[:, :, 0:1], 0.0), w=[xg.r])
        dx = TR(C, "dx", [128, 8, GS], F32)
        xs = [TR(C, "xs%d" % i, [128, 8, GS], BF16) for i in range(2)]
        l1 = [TR(C, "l1_%d" % i, [128, GS], BF16) for i in range(3)]
        l1f = TR(C, "l1f", [128, GS], F32)
        G_ = {n: TR(C, "G_" + n, [128, NTG, 512], F32) for n in ("r", "k", "v")}
        names = ["sw", "Gm", "Gi", "Gp", "GC", "a", "kkn", "kp", "t1", "t2", "Bt", "Kt", "Bh", "Kh", "bonus", "Y", "yn", "gg", "At", "glg"]
        X = {n: TR(C, "X_" + n, [128, 512], F32) for n in names}
        sm = TR(C, "sm", [128, 64], F32)
        Z = [TR(C, "Z%d" % i, [128, 8, 128], BF16) for i in range(2)]
        Zf32 = TR(C, "Zf32", [128, 8, 128], F32)
        Rt = TR(C, "Rt", [128, 512], F32)
        ART = TR(C, "ART", [128, 4, 2, 128], F32)
        KT = TR(C, "KT", [128, 4, 128], F32)
        BT = TR(C, "BT", [128, 4, 128], F32)
        Nm = [TR(C, "Nm%d" % i, [128, 8, 128], BF16) for i in range(2)]
        NTm = [TR(C, "NTm%d" % i, [128, 8, 128], BF16) for i in range(2)]
        for t_ in Nm + NTm:
            S.op("pool", lambda e, t_=t_: e.memset(t_.t[:], 0.0), w=[t_.r])
        Arb = TR(C, "Arb", [128, 8, 128], F32)
        Aak = TR(C, "Aak", [128, 8, 128], F32)
        RbT = TR(C, "RbT", [128, 4, 128], F32)
        Qm = TR(C, "Qm", [128, 4, 128], F32)
        Hm = TR(C, "Hm", [128, 4, 128], F32)
        gcol = TR(C, "gcol", [128, 8], F32)
        Y0 = TR(C, "Y0", [128, 512], F32)
        ST = [TR(C, "ST%d" % i, [128, 4, 64], F32) for i in range(2)]
        S.op("dve", lambda e: e.memset(ST[0].t[:], 0.0), w=[ST[0].r])
        ybf = TR(C, "ybf", [128, 512], BF16)
        yTo = [TR(C, "yTo%d" % i, [128, 4, 128], BF16) for i in range(2)]
        ysrc = yT_d.rearrange("(a p) t -> p a t", p=128)

        def hv(t):
            return t.rearrange("p (h d) -> p h d", d=64)

        def bc8(ap):
            return ap.unsqueeze(2).to_broadcast([128, 8, 64])

        st_i = 0
        try:
          for G in range(NGR):
              if G > 0:
                  S.op("dve", lambda e: e.tensor_copy(out=xg.t[:, :, 0:1], in_=xg.t[:, :, GS:GS + 1]), r=[xg.r], w=[xg.r])
              S.dma("sp", xg.t[:, :, 1:GS + 1], xsrc_fn(G * GS, GS), r=[xg.r], w=[xg.r], key="xg")
              for kc in range(8):
                  S.op("dve", lambda e, kc=kc: e.tensor_scalar(out=xg.t[:, kc, 1:GS + 1], in0=xg.t[:, kc, 1:GS + 1], scalar1=cols.t[:, kc:kc + 1], scalar2=mods[:, kc:kc + 1], op0=ALU.mult, op1=ALU.add), r=[xg.r, cols.r, r_mods], w=[xg.r])
              S.op("dve", lambda e: e.tensor_tensor(out=dx.t[:], in0=xg.t[:, :, 0:GS], in1=xg.t[:, :, 1:GS + 1], op=ALU.subtract), r=[xg.r], w=[dx.r])
              for n, nm in enumerate(("r", "k", "v", "w", "a", "g")):
                  xb = xs[n % 2]
                  for kc in range(8):
                      eng = "dve"
                      S.op(eng, lambda e, kc=kc, n=n, xb=xb: e.scalar_tensor_tensor(out=xb.t[:, kc, :], in0=dx.t[:, kc, :], scalar=mu.t[:, n * 8 + kc:n * 8 + kc + 1], in1=xg.t[:, kc, 1:GS + 1], op0=ALU.mult, op1=ALU.add), r=[dx.r, mu.r, xg.r], w=[xb.r])
                  if n < 3:
                      wt = W[("wr", "wk", "wv")[n]]
                      for i in range(NTG):
                          pb = PB[(n * NTG + i) % 2]
                          for kc in range(8):
                              S.op("pe", lambda e, pb=pb, xb=xb, wt=wt, i=i, kc=kc: e.matmul(pb.t[:], xb.t[:, kc, i * 128:(i + 1) * 128], wt[:, kc, :], start=(kc == 0), stop=(kc == 7)), r=[xb.r, r_w], w=[pb.r])
                          S.op("act", lambda e, pb=pb, nm=nm, i=i: e.copy(out=G_[nm].t[:, i, :], in_=pb.t[:]), r=[pb.r], w=[G_[nm].r])
                  else:
                      w1n, w2n, rows = (("w1", "w2", 64), ("a1", "a2", 64), ("g1", "g2", 128))[n - 3]
                      pb = PB[2]
                      for kc in range(8):
                          S.op("pe", lambda e, pb=pb, xb=xb, w1n=w1n, rows=rows, kc=kc: e.matmul(pb.t[0:rows, 0:GS], W[w1n][:, kc, :], xb.t[:, kc, :], start=(kc == 0), stop=(kc == 7)), r=[xb.r, r_w], w=[pb.r])
                      lt = l1[n - 3]
                      if nm == "w":
                          S.op("act", lambda e, pb=pb: e.activation(out=l1f.t[0:64, :], in_=pb.t[0:64, 0:GS], func=AF.Exp, scale=-2.0), r=[pb.r, l1f.r], w=[l1f.r])
                          S.op("dve", lambda e: e.tensor_scalar(out=l1f.t[0:64, :], in0=l1f.t[0:64, :], scalar1=1.0, scalar2=None, op0=ALU.add), r=[l1f.r], w=[l1f.r])
                          S.op("dve", lambda e: e.reciprocal(out=l1f.t[0:64, :], in_=l1f.t[0:64, :]), r=[l1f.r], w=[l1f.r])
                          S.op("dve", lambda e, lt=lt: e.tensor_scalar(out=lt.t[0:64, :], in0=l1f.t[0:64, :], scalar1=2.0, scalar2=-1.0, op0=ALU.mult, op1=ALU.add), r=[l1f.r], w=[lt.r])
                      elif nm == "a":
                          S.op("act", lambda e, pb=pb, lt=lt: e.copy(out=lt.t[0:64, :], in_=pb.t[0:64, 0:GS]), r=[pb.r], w=[lt.r])
                      else:
                          S.op("act", lambda e, pb=pb: e.activation(out=l1f.t[:], in_=pb.t[:, 0:GS], func=AF.Exp, scale=-1.0), r=[pb.r, l1f.r], w=[l1f.r])
                          S.op("dve", lambda e: e.tensor_scalar(out=l1f.t[:], in0=l1f.t[:], scalar1=1.0, scalar2=None, op0=ALU.add), r=[l1f.r], w=[l1f.r])
                          S.op("dve", lambda e: e.reciprocal(out=l1f.t[:], in_=l1f.t[:]), r=[l1f.r], w=[l1f.r])
                          S.op("act", lambda e, lt=lt: e.copy(out=lt.t[:], in_=l1f.t[:]), r=[l1f.r], w=[lt.r])
              if stop == 1:
                  raise _Stop()
              for i in range(NTG):
                  tl_ = G * NTG + i
                  r_ = G_["r"].t[:, i, :]; k_ = G_["k"].t[:, i, :]; v_ = G_["v"].t[:, i, :]
                  rG = [G_[n].r for n in G_]
                  sw, Gm, Gi, Gp, GC, a_, kkn, kp, t1, t2 = [X[n] for n in ("sw", "Gm", "Gi", "Gp", "GC", "a", "kkn", "kp", "t1", "t2")]
                  Bt, Kt, Bh, Kh, bonus, Yt, yn = [X[n] for n in ("Bt", "Kt", "Bh", "Kh", "bonus", "Y", "yn")]
                  pzw, pza, pgg = PB[4], PB[5], PB[6]
                  for (pz_, lt, w2n, rows) in ((pzw, l1[0], "w2", 64), (pza, l1[1], "a2", 64), (pgg, l1[2], "g2", 128)):
                      S.op("pe", lambda e, pz_=pz_, lt=lt, w2n=w2n, rows=rows, i=i: e.matmul(pz_.t[:], lt.t[0:rows, i * 128:(i + 1) * 128], W[w2n][0:rows, :], start=True, stop=True), r=[lt.r, r_w], w=[pz_.r])
                  S.op("act", lambda e: e.copy(out=X["gg"].t[:], in_=pgg.t[:]), r=[pgg.r], w=[X["gg"].r])
                  S.op("dve", lambda e: e.tensor_tensor(out=t1.t[:], in0=pzw.t[:], in1=BC["w0b"][:], op=ALU.add), r=[pzw.r, r_bc], w=[t1.r])
                  S.op("dve", lambda e: e.tensor_tensor(out=t2.t[:], in0=pza.t[:], in1=BC["a0b"][:], op=ALU.add), r=[pza.r, r_bc, t2.r], w=[t2.r])
                  S.op("act", lambda e: e.activation(out=sw.t[:], in_=t1.t[:], func=AF.Sigmoid), r=[t1.r], w=[sw.r])
                  S.op("act", lambda e: e.activation(out=a_.t[:], in_=t2.t[:], func=AF.Sigmoid), r=[t2.r], w=[a_.r])
                  pL, pLC = PB[2], PB[3]
                  S.op("pe", lambda e: e.matmul(pL.t[:], triC.t[:], sw.t[:], start=True, stop=True), r=[triC.r, sw.r], w=[pL.r])
                  S.op("pe", lambda e: e.matmul(pLC.t[:], blkC.t[:], sw.t[:], start=True, stop=True), r=[blkC.r, sw.r], w=[pLC.r])
                  S.op("act", lambda e: e.activation(out=Gm.t[:], in_=pL.t[:], func=AF.Exp), r=[pL.r], w=[Gm.r])
                  S.op("act", lambda e: e.activation(out=Gi.t[:], in_=pL.t[:], func=AF.Exp, scale=-1.0), r=[pL.r], w=[Gi.r])
                  S.op("dve", lambda e: e.scalar_tensor_tensor(out=Gp.t[:], in0=sw.t[:], scalar=C0, in1=pL.t[:], op0=ALU.mult, op1=ALU.add), r=[sw.r, pL.r], w=[Gp.r])
                  S.op("act", lambda e: e.activation(out=Gp.t[:], in_=Gp.t[:], func=AF.Exp), r=[Gp.r], w=[Gp.r])
                  S.op("act", lambda e: e.activation(out=GC.t[:], in_=pLC.t[:], func=AF.Exp), r=[pLC.r], w=[GC.r])
                  S.op("dve", lambda e, k_=k_: e.tensor_tensor(out=kkn.t[:], in0=k_, in1=BC["kkb"][:], op=ALU.mult), r=rG + [r_bc], w=[kkn.r])
                  S.op("act", lambda e: e.activation(out=t2.t[:], in_=kkn.t[:], func=AF.Square), r=[kkn.r], w=[t2.r])
                  S.op("dve", lambda e: e.tensor_reduce(out=sm.t[:, 0:8], in_=hv(t2.t[:]), axis=AX.X, op=ALU.add), r=[t2.r], w=[sm.r])
                  S.op("dve", lambda e: e.tensor_scalar(out=sm.t[:, 0:8], in0=sm.t[:, 0:8], scalar1=1e-24, scalar2=None, op0=ALU.max), r=[sm.r], w=[sm.r])
                  S.op("act", lambda e: e.activation(out=sm.t[:, 0:8], in_=sm.t[:, 0:8], func=AF.Sqrt), r=[sm.r], w=[sm.r])
                  S.op("dve", lambda e: e.reciprocal(out=sm.t[:, 0:8], in_=sm.t[:, 0:8]), r=[sm.r], w=[sm.r])
                  S.op("dve", lambda e: e.tensor_tensor(out=hv(kkn.t[:]), in0=hv(kkn.t[:]), in1=bc8(sm.t[:, 0:8]), op=ALU.mult), r=[kkn.r, sm.r], w=[kkn.r])
                  S.op("dve", lambda e: e.scalar_tensor_tensor(out=t2.t[:], in0=a_.t[:], scalar=-1.0, in1=BC["kab"][:], op0=ALU.add, op1=ALU.mult), r=[a_.r, r_bc, t2.r], w=[t2.r])
                  S.op("dve", lambda e, k_=k_: e.scalar_tensor_tensor(out=kp.t[:], in0=t2.t[:], scalar=1.0, in1=k_, op0=ALU.add, op1=ALU.mult), r=[t2.r] + rG, w=[kp.r])
                  S.op("dve", lambda e: e.scalar_tensor_tensor(out=X["At"].t[:], in0=kkn.t[:], scalar=-1.0, in1=Gp.t[:], op0=ALU.mult, op1=ALU.mult), r=[kkn.r, Gp.r], w=[X["At"].r])
                  S.op("pool", lambda e: e.tensor_copy(out=Z[0].t[:, :, 0:64], in_=hv(X["At"].t[:])), r=[X["At"].r], w=[Z[0].r])
                  S.op("pool", lambda e: e.tensor_tensor(out=Bt.t[:], in0=kkn.t[:], in1=a_.t[:], op=ALU.mult), r=[kkn.r, a_.r], w=[Bt.r])
                  S.op("pool", lambda e: e.tensor_tensor(out=Bt.t[:], in0=Bt.t[:], in1=Gi.t[:], op=ALU.mult), r=[Bt.r, Gi.r], w=[Bt.r])
                  S.op("dve", lambda e: e.tensor_tensor(out=Kt.t[:], in0=kp.t[:], in1=Gi.t[:], op=ALU.mult), r=[kp.r, Gi.r], w=[Kt.r])
                  S.op("dve", lambda e, r_=r_: e.tensor_tensor(out=Rt.t[:], in0=r_, in1=Gm.t[:], op=ALU.mult), r=rG + [Gm.r], w=[Rt.r])
                  S.op("pool", lambda e: e.tensor_tensor(out=Bh.t[:], in0=Bt.t[:], in1=GC.t[:], op=ALU.mult), r=[Bt.r, GC.r], w=[Bh.r])
                  S.op("pool", lambda e: e.tensor_tensor(out=Kh.t[:], in0=Kt.t[:], in1=GC.t[:], op=ALU.mult), r=[Kt.r, GC.r], w=[Kh.r])
                  S.op("dve", lambda e, r_=r_: e.tensor_tensor(out=t2.t[:], in0=r_, in1=kp.t[:], op=ALU.mult), r=rG + [kp.r, t2.r], w=[t2.r])
                  S.op("dve", lambda e: e.tensor_tensor(out=t2.t[:], in0=t2.t[:], in1=BC["rkb"][:], op=ALU.mult), r=[t2.r, r_bc], w=[t2.r])
                  S.op("dve", lambda e: e.tensor_reduce(out=sm.t[:, 8:16], in_=hv(t2.t[:]), axis=AX.X, op=ALU.add), r=[t2.r], w=[sm.r])
                  S.op("dve", lambda e, v_=v_: e.tensor_tensor(out=hv(bonus.t[:]), in0=hv(v_), in1=bc8(sm.t[:, 8:16]), op=ALU.mult), r=rG + [sm.r], w=[bonus.r])
                  S.op("pool", lambda e: e.tensor_tensor(out=bonus.t[:], in0=bonus.t[:], in1=BC["lnbb"][:], op=ALU.add), r=[bonus.r, r_bc], w=[bonus.r])
                  S.op("pool", lambda e: e.tensor_tensor(out=bonus.t[:], in0=bonus.t[:], in1=X["gg"].t[:], op=ALU.mult), r=[bonus.r, X["gg"].r], w=[bonus.r])
                  S.op("pool", lambda e: e.tensor_tensor(out=X["glg"].t[:], in0=X["gg"].t[:], in1=BC["lngb"][:], op=ALU.mult), r=[X["gg"].r, r_bc], w=[X["glg"].r])
                  if stop == 2:
                      raise _Stop()
                  for p in range(4):
                      pT = PB[p % 2]
                      for q_, src in enumerate((X["At"], Rt, Kt, Bt)):
                          S.op("pe", lambda e, pT=pT, p=p, q_=q_, src=src: e.transpose(pT.t[:, q_ * 128:(q_ + 1) * 128], src.t[:, p * 128:(p + 1) * 128], ident[:]), r=[src.r, r_ident], w=[pT.r])
                      S.op("dve", lambda e, pT=pT, p=p: e.tensor_copy(out=ART.t[:, p, :, 0:64], in_=pT.t[:, 0:128].rearrange("p (c t) -> p c t", t=64)), r=[pT.r], w=[ART.r])
                      S.op("act", lambda e, pT=pT, p=p: e.copy(out=ART.t[:, p, :, 64:128], in_=pT.t[:, 128:256].rearrange("p (c t) -> p c t", t=64)), r=[pT.r], w=[ART.r])
                      S.op("act", lambda e, pT=pT, p=p: e.copy(out=KT.t[:, p, :], in_=pT.t[:, 256:384]), r=[pT.r], w=[KT.r])
                      S.op("dve", lambda e, pT=pT, p=p: e.tensor_copy(out=BT.t[:, p, :], in_=pT.t[:, 384:512]), r=[pT.r], w=[BT.r])
                  if stop == 3:
                      raise _Stop()
                  pS1a, pS1b, pS2a, pS2b, pS3 = PB[0], PB[1], PB[2], PB[3], PB[4]
                  for h in range(8):
                      p, e_ = h // 2, h % 2
                      fe = slice(e_ * 64, e_ * 64 + 64)
                      pa, pb_ = (pS1a, pS2a) if h < 4 else (pS1b, pS2b)
                      hc = (h % 4) * 128
                      for ch in range(2):
                          tc_ = slice(ch * 64, ch * 64 + 64)
                          S.op("pe", lambda e, pa=pa, fe=fe, p=p, ch=ch, tc_=tc_, hc=hc: e.matmul(pa.t[tc_, hc:hc + 128], BT.t[fe, p, tc_], ART.t[fe, p, ch, :], start=True, stop=True), r=[BT.r, ART.r], w=[pa.r])
                          S.op("pe", lambda e, pb_=pb_, fe=fe, p=p, ch=ch, tc_=tc_, hc=hc: e.matmul(pb_.t[tc_, hc:hc + 128], KT.t[fe, p, tc_], ART.t[fe, p, ch, :], start=True, stop=True), r=[KT.r, ART.r], w=[pb_.r])
                          S.op("pe", lambda e, fe=fe, p=p, ch=ch, tc_=tc_, h=h: e.matmul(pS3.t[tc_, h * 64:(h + 1) * 64], ART.t[fe, p, ch, 0:64], BT.t[fe, p, tc_], start=True, stop=True), r=[BT.r, ART.r], w=[pS3.r])
                  m1b = mask1.t[:].unsqueeze(1).to_broadcast([128, 4, 128])
                  for half, (pa, pb_) in enumerate(((pS1a, pS2a), (pS1b, pS2b))):
                      hs = slice(half * 4, half * 4 + 4)
                      S.op("dve", lambda e, pa=pa, hs=hs: e.tensor_tensor(out=Arb.t[:, hs, :], in0=pa.t[:].rearrange("p (h t) -> p h t", t=128), in1=m1b, op=ALU.mult), r=[pa.r, mask1.r], w=[Arb.r])
                      S.op("dve", lambda e, pb_=pb_, hs=hs: e.tensor_tensor(out=Aak.t[:, hs, :], in0=pb_.t[:].rearrange("p (h t) -> p h t", t=128), in1=m1b, op=ALU.mult), r=[pb_.r, mask1.r], w=[Aak.r])
                  for ch in range(2):
                      tc_ = slice(ch * 64, ch * 64 + 64)
                      S.op("dve", lambda e, tc_=tc_: e.tensor_tensor(out=NTm[0].t[tc_, :, tc_], in0=hv(pS3.t[tc_, :]), in1=maskL.t[tc_, :].unsqueeze(1).to_broadcast([64, 8, 64]), op=ALU.mult), r=[pS3.r, maskL.r], w=[NTm[0].r])
                      S.op("act", lambda e, tc_=tc_: e.copy(out=Nm[0].t[tc_, :, tc_], in_=Arb.t[tc_, :, 0:64]), r=[Arb.r], w=[Nm[0].r])
                  if stop == 4:
                      raise _Stop()
                  if debug and tl_ == 0:
                      tails.append(S.dma("sp", dbg["Arb"], Arb.t[:], r=[Arb.r], key="dbg_Arb"))
                  if debug and tl_ == 0:
                      tails.append(S.dma("sp", dbg["Aak"], Aak.t[:], r=[Aak.r], key="dbg_Aak"))
                  if debug and tl_ == 0:
                      tails.append(S.dma("sp", dbg["NT0"], NTm[0].t[:], r=[NTm[0].r], key="dbg_NT0"))
                  if debug and tl_ == 0:
                      tails.append(S.dma("sp", dbg["ART"], ART.t[:], r=[ART.r], key="dbg_ART"))
                  if debug and tl_ == 0:
                      tails.append(S.dma("sp", dbg["KT"], KT.t[:], r=[KT.r], key="dbg_KT"))
                  if debug and tl_ == 0:
                      tails.append(S.dma("sp", dbg["BT"], BT.t[:], r=[BT.r], key="dbg_BT"))
                  if debug and tl_ == 0:
                      tails.append(S.dma("sp", dbg["Bh"], Bh.t[:], r=[Bh.r], key="dbg_Bh"))
                  if debug and tl_ == 0:
                      tails.append(S.dma("sp", dbg["Kh"], Kh.t[:], r=[Kh.r], key="dbg_Kh"))
                  pX = PB[5]
                  for h in range(8):
                      for ch in range(2):
                          tc_ = slice(ch * 64, ch * 64 + 64)
                          S.op("pe", lambda e, h=h, tc_=tc_, v_=v_: e.matmul(pX.t[tc_, h * 64:(h + 1) * 64], Aak.t[tc_, h, 0:64], v_[tc_, h * 64:(h + 1) * 64], start=True, stop=True), r=[Aak.r] + rG, w=[pX.r])
                  S.op("act", lambda e: e.copy(out=Z[0].t[:, :, 64:128], in_=hv(pX.t[:])), r=[pX.r], w=[Z[0].r])
                  if stop == 5:
                      raise _Stop()
                  if debug and tl_ == 0:
                      tails.append(S.dma("sp", dbg["Z0"], Z[0].t[:], r=[Z[0].r], key="dbg_Z0"))
                  zi = 0
                  for lev in range(6):
                      Nc, NTc = Nm[lev % 2], NTm[lev % 2]
                      Nn, NTn = Nm[(lev + 1) % 2], NTm[(lev + 1) % 2]
                      pZa, pZb = PB[0], PB[1]
                      Zc, Zn = Z[zi], Z[1 - zi]
                      if lev < 5:
                          pN, pNT = (PB[2], PB[3]), (PB[4], PB[5])
                          for h in range(8):
                              hb, hc = h // 4, (h % 4) * 128
                              S.op("pe", lambda e, NTc=NTc, Nc=Nc, h=h, hb=hb, hc=hc: e.matmul(pN[hb].t[:, hc:hc + 128], NTc.t[:, h, :], Nc.t[:, h, :], start=True, stop=True), r=[Nc.r, NTc.r], w=[pN[hb].r])
                              if lev < 4:
                                  S.op("pe", lambda e, NTc=NTc, Nc=Nc, h=h, hb=hb, hc=hc: e.matmul(pNT[hb].t[:, hc:hc + 128], Nc.t[:, h, :], NTc.t[:, h, :], start=True, stop=True), r=[Nc.r, NTc.r], w=[pNT[hb].r])
                      for h in range(8):
                          pz = pZa if h < 4 else pZb
                          hc = (h % 4) * 128
                          S.op("pe", lambda e, pz=pz, Nc=Nc, Zc=Zc, h=h, hc=hc: e.matmul(pz.t[:, hc:hc + 128], Nc.t[:, h, :], Zc.t[:, h, :], start=True, stop=True), r=[Nc.r, Zc.r], w=[pz.r])
                      if lev == 5:
                          Zn = Zf32
                      S.op("dve", lambda e, Zc=Zc, Zn=Zn: e.tensor_tensor(out=Zn.t[:, 0:4, :], in0=Zc.t[:, 0:4, :], in1=pZa.t[:].rearrange("p (h t) -> p h t", t=128), op=ALU.add), r=[Zc.r, pZa.r], w=[Zn.r])
                      S.op("dve", lambda e, Zc=Zc, Zn=Zn: e.tensor_tensor(out=Zn.t[:, 4:8, :], in0=Zc.t[:, 4:8, :], in1=pZb.t[:].rearrange("p (h t) -> p h t", t=128), op=ALU.add), r=[Zc.r, pZb.r], w=[Zn.r])
                      zi = 1 - zi
                      if lev < 5:
                          for hb in range(2):
                              S.op("act", lambda e, Nn=Nn, hb=hb: e.copy(out=Nn.t[:, hb * 4:hb * 4 + 4, :], in_=pN[hb].t[:].rearrange("p (h t) -> p h t", t=128)), r=[pN[hb].r], w=[Nn.r])
                              if lev < 4:
                                  S.op("act" if hb == 0 else "dve", (lambda e, NTn=NTn, hb=hb: e.copy(out=NTn.t[:, hb * 4:hb * 4 + 4, :], in_=pNT[hb].t[:].rearrange("p (h t) -> p h t", t=128))) if hb == 0 else (lambda e, NTn=NTn, hb=hb: e.tensor_copy(out=NTn.t[:, hb * 4:hb * 4 + 4, :], in_=pNT[hb].t[:].rearrange("p (h t) -> p h t", t=128))), r=[pNT[hb].r], w=[NTn.r])
                  Zf = Zf32
                  pY0 = PB[4]
                  for h in range(8):
                      for ch in range(2):
                          tc_ = slice(ch * 64, ch * 64 + 64)
                          S.op("pe", lambda e, h=h, tc_=tc_: e.matmul(pY0.t[tc_, h * 64:(h + 1) * 64], Arb.t[tc_, h, 64:128], Zf.t[tc_, h, 64:128], start=True, stop=False), r=[Arb.r, Zf.r], w=[pY0.r])
                          S.op("pe", lambda e, h=h, tc_=tc_, v_=v_: e.matmul(pY0.t[tc_, h * 64:(h + 1) * 64], Aak.t[tc_, h, 64:128], v_[tc_, h * 64:(h + 1) * 64], start=False, stop=True), r=[Aak.r] + rG, w=[pY0.r])
                  S.op("act", lambda e: e.copy(out=Y0.t[:], in_=pY0.t[:]), r=[pY0.r], w=[Y0.r])
                  if stop == 7:
                      raise _Stop()
                  if debug and tl_ == 0:
                      tails.append(S.dma("sp", dbg["Zf"], Zf.t[:], r=[Zf.r], key="dbg_Zf"))
                  if debug and tl_ == 0:
                      tails.append(S.dma("sp", dbg["Y0"], Y0.t[:], r=[Y0.r], key="dbg_Y0"))
                  pR, pQ, pH, pg = PB[5], PB[6], PB[0], PB[1]
                  for h in range(8):
                      p, e_ = h // 2, h % 2
                      fe = slice(e_ * 64, e_ * 64 + 64)
                      for ch in range(2):
                          tc_ = slice(ch * 64, ch * 64 + 64)
                          cs = slice(p * 128 + ch * 64, p * 128 + ch * 64 + 64)
                          hs_ = slice(h * 64, h * 64 + 64)
                          S.op("pe", lambda e, fe=fe, cs=cs, tc_=tc_, h=h: e.matmul(pR.t[fe, cs], Zf.t[tc_, h, 0:64], Arb.t[tc_, h, 64:128], start=True, stop=True), r=[Zf.r, Arb.r], w=[pR.r])
                          S.op("pe", lambda e, fe=fe, cs=cs, tc_=tc_, h=h, hs_=hs_: e.matmul(pQ.t[fe, cs], Zf.t[tc_, h, 0:64], Bh.t[tc_, hs_], start=True, stop=True), r=[Zf.r, Bh.r], w=[pQ.r])
                          S.op("pe", lambda e, fe=fe, cs=cs, tc_=tc_, h=h, hs_=hs_: e.matmul(pH.t[fe, cs], Bh.t[tc_, hs_], Zf.t[tc_, h, 64:128], start=True, stop=False), r=[Zf.r, Bh.r], w=[pH.r])
                          S.op("pe", lambda e, fe=fe, cs=cs, tc_=tc_, hs_=hs_, v_=v_: e.matmul(pH.t[fe, cs], Kh.t[tc_, hs_], v_[tc_, hs_], start=False, stop=True), r=[Kh.r] + rG, w=[pH.r])
                          S.op("pe", lambda e, fe=fe, tc_=tc_, hs_=hs_, p=p, ch=ch: e.matmul(pg.t[fe, p * 2 + ch:p * 2 + ch + 1], GC.t[tc_, hs_], avg.t[tc_, :], start=True, stop=True), r=[GC.r, avg.r], w=[pg.r])
                  for ch in range(2):
                      S.op("dve", lambda e, ch=ch: e.tensor_tensor(out=RbT.t[:, :, ch * 64:(ch + 1) * 64], in0=pR.t[:].rearrange("p (a t) -> p a t", t=128)[:, :, ch * 64:(ch + 1) * 64], in1=ART.t[:, :, ch, 64:128], op=ALU.add), r=[pR.r, ART.r], w=[RbT.r])
                  S.op("act", lambda e: e.copy(out=Qm.t[:], in_=pQ.t[:].rearrange("p (a t) -> p a t", t=128)), r=[pQ.r], w=[Qm.r])
                  S.op("act", lambda e: e.copy(out=Hm.t[:], in_=pH.t[:].rearrange("p (a t) -> p a t", t=128)), r=[pH.r], w=[Hm.r])
                  S.op("act", lambda e: e.copy(out=gcol.t[:], in_=pg.t[:, 0:8]), r=[pg.r], w=[gcol.r])
                  if stop == 8:
                      raise _Stop()
                  if debug and tl_ == 0:
                      tails.append(S.dma("sp", dbg["RbT"], RbT.t[:], r=[RbT.r], key="dbg_RbT"))
                  if debug and tl_ == 0:
                      tails.append(S.dma("sp", dbg["Qm"], Qm.t[:], r=[Qm.r], key="dbg_Qm"))
                  if debug and tl_ == 0:
                      tails.append(S.dma("sp", dbg["Hm"], Hm.t[:], r=[Hm.r], key="dbg_Hm"))
                  if debug and tl_ == 0:
                      tails.append(S.dma("sp", dbg["gcol"], gcol.t[:], r=[gcol.r], key="dbg_gcol"))
                  for ch in range(2):
                      tc_ = slice(ch * 64, ch * 64 + 64)
                      Sc, Sn = ST[st_i], ST[1 - st_i]
                      pYe, pS_ = (PB[2], PB[4]), PB[3]
                      for h in range(8):
                          p, e_ = h // 2, h % 2
                          fe = slice(e_ * 64, e_ * 64 + 64)
                          cs = slice(ch * 64, ch * 64 + 64)
                          pY = pYe[e_]
                          S.op("pe", lambda e, fe=fe, p=p, cs=cs, tc_=tc_, h=h, Sc=Sc, pY=pY: e.matmul(pY.t[tc_, h * 64:(h + 1) * 64], RbT.t[fe, p, cs], Sc.t[fe, p, :], start=True, stop=True), r=[RbT.r, Sc.r], w=[pY.r])
                          S.op("pe", lambda e, fe=fe, p=p, cs=cs, Sc=Sc: e.matmul(pS_.t[fe, p * 64:(p + 1) * 64], Qm.t[fe, p, cs], Sc.t[fe, p, :], start=True, stop=True), r=[Qm.r, Sc.r], w=[pS_.r])
                      v4 = lambda ap: ap.rearrange("p (a e d) -> p a e d", e=2, d=64)
                      for e_ in range(2):
                          S.op("dve", lambda e, tc_=tc_, e_=e_: e.tensor_tensor(out=v4(Yt.t[tc_, :])[:, :, e_, :], in0=v4(Y0.t[tc_, :])[:, :, e_, :], in1=v4(pYe[e_].t[tc_, :])[:, :, e_, :], op=ALU.add), r=[Y0.r, pYe[e_].r], w=[Yt.r])
                      for p in range(4):
                          S.op("dve", lambda e, p=p, ch=ch, Sc=Sc, Sn=Sn: e.scalar_tensor_tensor(out=Sn.t[:, p, :], in0=Sc.t[:, p, :], scalar=gcol.t[:, p * 2 + ch:p * 2 + ch + 1], in1=pS_.t[:, p * 64:(p + 1) * 64], op0=ALU.mult, op1=ALU.add), r=[Sc.r, gcol.r, pS_.r], w=[Sn.r])
                      S.op("dve", lambda e, ch=ch, Sn=Sn: e.tensor_tensor(out=Sn.t[:], in0=Sn.t[:], in1=Hm.t[:, :, ch * 64:(ch + 1) * 64], op=ALU.add), r=[Sn.r, Hm.r], w=[Sn.r])
                      st_i = 1 - st_i
                  if stop == 9:
                      raise _Stop()
                  S.op("dve", lambda e: e.tensor_reduce(out=sm.t[:, 16:24], in_=hv(Yt.t[:]), axis=AX.X, op=ALU.add), r=[Yt.r], w=[sm.r])
                  S.op("act", lambda e: e.activation(out=t2.t[:], in_=Yt.t[:], func=AF.Square), r=[Yt.r, t2.r], w=[t2.r])
                  S.op("dve", lambda e: e.tensor_reduce(out=sm.t[:, 24:32], in_=hv(t2.t[:]), axis=AX.X, op=ALU.add), r=[t2.r], w=[sm.r])
                  S.op("dve", lambda e: e.tensor_scalar(out=sm.t[:, 16:24], in0=sm.t[:, 16:24], scalar1=1.0 / 64, scalar2=None, op0=ALU.mult), r=[sm.r], w=[sm.r])
                  S.op("dve", lambda e: e.tensor_tensor(out=sm.t[:, 32:40], in0=sm.t[:, 16:24], in1=sm.t[:, 16:24], op=ALU.mult), r=[sm.r], w=[sm.r])
                  S.op("dve", lambda e: e.scalar_tensor_tensor(out=sm.t[:, 24:32], in0=sm.t[:, 24:32], scalar=1.0 / 64, in1=sm.t[:, 32:40], op0=ALU.mult, op1=ALU.subtract), r=[sm.r], w=[sm.r])
                  S.op("dve", lambda e: e.tensor_scalar(out=sm.t[:, 24:32], in0=sm.t[:, 24:32], scalar1=GN_EPS, scalar2=None, op0=ALU.add), r=[sm.r], w=[sm.r])
                  S.op("act", lambda e: e.activation(out=sm.t[:, 24:32], in_=sm.t[:, 24:32], func=AF.Sqrt), r=[sm.r], w=[sm.r])
                  S.op("dve", lambda e: e.reciprocal(out=sm.t[:, 24:32], in_=sm.t[:, 24:32]), r=[sm.r], w=[sm.r])
                  S.op("dve", lambda e: e.tensor_tensor(out=hv(yn.t[:]), in0=hv(Yt.t[:]), in1=bc8(sm.t[:, 16:24]), op=ALU.subtract), r=[Yt.r, sm.r], w=[yn.r])
                  S.op("dve", lambda e: e.tensor_tensor(out=hv(yn.t[:]), in0=hv(yn.t[:]), in1=bc8(sm.t[:, 24:32]), op=ALU.mult), r=[yn.r, sm.r], w=[yn.r])
                  S.op("dve", lambda e: e.tensor_tensor(out=yn.t[:], in0=yn.t[:], in1=X["glg"].t[:], op=ALU.mult), r=[yn.r, X["glg"].r], w=[yn.r])
                  S.op("dve", lambda e: e.tensor_tensor(out=ybf.t[:], in0=yn.t[:], in1=bonus.t[:], op=ALU.add), r=[yn.r, bonus.r], w=[ybf.r])
                  for p in range(4):
                      S.op("pe", lambda e, p=p: e.transpose(pbb.t[:, p * 128:(p + 1) * 128], ybf.t[:, p * 128:(p + 1) * 128], identb.t[:]), r=[ybf.r, identb.r], w=[pbb.r])
                  yo = yTo[tl_ % 2]
                  S.op("act", lambda e, yo=yo: e.copy(out=yo.t[:], in_=pbb.t[:, 0:512].rearrange("p (a t) -> p a t", t=128)), r=[pbb.r], w=[yo.r])
                  tails.append(S.dma("sp", ysrc[:, :, tl_ * 128:(tl_ + 1) * 128], yo.t[:], r=[yo.r], key="yTo%d" % (tl_ % 2)))
                  if debug and tl_ == 0:
                      S.op("act", lambda e, r_=r_: e.copy(out=t1.t[:], in_=r_), r=rG + [t1.r], w=[t1.r])
                      S.op("act", lambda e, v_=v_: e.copy(out=t2.t[:], in_=v_), r=rG + [t2.r], w=[t2.r])
                      tails.append(S.dma("sp", dbg["v"], t2.t[:], r=[t2.r], key="dbg_v"))
                      for n, src in (("r", t1), ("kp", kp), ("kkn", kkn), ("a", a_), ("sw", sw), ("Y", Yt), ("bonus", bonus), ("Gm", Gm), ("yn", yn)):
                          tails.append(S.dma("sp", dbg[n], src.t[:], r=[src.r], key="dbg_" + n))
                      tails.append(S.dma("sp", dbg["S"], ST[st_i].t[:], r=[ST[st_i].r], key="dbg_S"))
        except _Stop:
            pass
        if after is not None:
            after(C, tails)
        S.emit(st, tail_waits=tails)


PAIRS = [[0, 1], [2, 3], [4, 5], [6, 7]]
TL_KEYS = ("cT", "adaw", "adab", "lng", "lnb", "wo", "win", "wout")


def tl_decl(dt, sfx):
    return dict(cT=dt("cT" + sfx, [128, 8], F32), adaw=dt("adaw" + sfx, [D, 4 * D], F32), adab=dt("adab" + sfx, [128, 32], F32),
                lng=dt("lng" + sfx, [128, 16], F32), lnb=dt("lnb" + sfx, [128, 16], F32),
                wo=dt("wo" + sfx, [D, D], F32), win=dt("win" + sfx, [D, 2 * DFF], F32), wout=dt("wout" + sfx, [DFF, D], F32))


def build_fused(stop=99):
    nc = bass.Bass("TRN2", target_bir_lowering=False)
    dt = lambda n, s, d, k="ExternalInput": nc.dram_tensor(n, list(s), d, kind=k).ap()
    it = lambda n, s, d: nc.dram_tensor(n, list(s), d).ap()
    fox_in = fox_decl(dt, 4096, "_f")
    oT_d = dt("oT", [D, 2048], F32, "ExternalOutput")
    cin1, cout1 = it("cin1", [4, 128, 4096], BF16), it("cout1", [4, 2, 128, 4096], BF16)
    cin2, cout2 = it("cin2", [8, 128, 2048], F32), it("cout2", [8, 2, 128, 2048], F32)
    cin3, cout3 = it("cin3", [4, 128, 4096], BF16), it("cout3", [4, 2, 128, 4096], BF16)

    def gather(cin, cout, key, nblk):
        def after(C, tails):
            prev = list(tails)
            for k in range(nblk):
                o = C.S.cc("AllGather", PAIRS, cin[k], cout[k].rearrange("r p t -> (r p) t"), key=key, extra=prev)
            tails.append(o)
        return after

    cin1_v = cin1.rearrange("k p t -> (k p) t")
    cin3_v = cin3.rearrange("k p t -> (k p) t")
    cin2_v = cin2.rearrange("k p t -> (k p) t")
    y1_v = cout1
    y3_v = cout3
    fox_phase(nc, "f_", fox_in, cin1_v, 4096, after=gather(cin1, cout1, "g1", 4))
    if stop <= 1:
        return nc
    phase_end(nc)
    sel_d = dt("sel", [128, 2], F32)
    xTh_d = dt("xTh", [D, 2048], F32)
    t = tl_decl(dt, "_t0")
    tl_phase(nc, "t0_", y1_v, xTh_d, t["cT"], t["adaw"], t["adab"], t["lng"], t["lnb"], t["wo"], t["win"], t["wout"], cin2_v, 2048, 1024,
             sel_d=sel_d, after=gather(cin2, cout2, "g2", 8))
    if stop <= 2:
        return nc
    phase_end(nc)
    rw_in = rwkv_decl(dt, "_r")
    c2v = cout2.rearrange("kc r p t -> r p kc t")
    rwkv_phase(nc, "r_", lambda t0, n: c2v[t0 // 2048, :, :, (t0 % 2048):(t0 % 2048) + n], rw_in, cin3_v, 4096, after=gather(cin3, cout3, "g3", 4))
    if stop <= 3:
        return nc
    phase_end(nc)
    t = tl_decl(dt, "_t1")
    tl_phase(nc, "t1_", y3_v, cin2_v, t["cT"], t["adaw"], t["adab"], t["lng"], t["lnb"], t["wo"], t["win"], t["wout"], oT_d, 2048, 1024, sel_d=sel_d)
    return nc


NCORES = 8
_PROGS = {}


def _prog(name, fn):
    if name not in _PROGS:
        _PROGS[name] = fn()
    return _PROGS[name]


def _col_layout(v):
    return np.ascontiguousarray(np.asarray(v).reshape(-1, 128).T)


def _bc(v, n=512):
    v = np.asarray(v, dtype=np.float32).reshape(-1)
    return np.ascontiguousarray(np.broadcast_to(v[None, :], (128, v.shape[0])))


def _run(nc, in_maps):
    res = run_bass_kernel_spmd(nc, in_maps, core_ids=list(range(NCORES)))
    return res.results


def _tl_maps(yT_list, xT_list, c, ada_w_i, ada_b_i, ln_g_i, ln_b_i, wo, win, wout):
    maps = []
    adaw = np.ascontiguousarray(ada_w_i[:, 2 * D:])
    adab = _col_layout(ada_b_i[2 * D:])
    lng = _col_layout(ln_g_i.reshape(-1))
    lnb = _col_layout(ln_b_i.reshape(-1))
    for core in range(NCORES):
        b, th = core // 2, core % 2
        ts = slice(th * 2048, (th + 1) * 2048)
        maps.append({
            "yT": np.ascontiguousarray(yT_list[b][:, ts]),
            "xT": np.ascontiguousarray(xT_list[b][:, ts]),
            "cT": _col_layout(c[b]),
            "adaw": adaw, "adab": adab, "lng": lng, "lnb": lnb,
            "wo": wo, "win": win, "wout": wout,
        })
    return maps


def kernel_unfused(x, c, ada_w, ada_b, ln_g, ln_b, ffn_w_in, ffn_w_out,
           fox_w_in, fox_b_f, fox_q_g, fox_k_g, fox_w_o,
           rwkv_mu, rwkv_w_rkv, rwkv_w0, rwkv_w1, rwkv_w2, rwkv_a0, rwkv_a1, rwkv_a2,
           rwkv_g1, rwkv_g2, rwkv_k_k, rwkv_k_a, rwkv_r_k, rwkv_lnx_g, rwkv_lnx_b, rwkv_w_o):
    f = lambda a: np.ascontiguousarray(np.asarray(a, dtype=np.float32))
    x, c, ada_w, ada_b, ln_g, ln_b = f(x), f(c), f(ada_w), f(ada_b), f(ln_g), f(ln_b)
    ffn_w_in, ffn_w_out = f(ffn_w_in), f(ffn_w_out)
    B = x.shape[0]
    w_in = f(fox_w_in)[0]
    b_f, q_g, k_g = f(fox_b_f)[0], f(fox_q_g)[0], f(fox_k_g)[0]
    maps = []
    for core in range(NCORES):
        b, hh = core // 2, core % 2
        sl = slice(hh * 512, (hh + 1) * 512)
        maps.append({
            "x": x[b],
            "cT": _col_layout(c[b]),
            "adaw": np.ascontiguousarray(ada_w[0][:, 0:2 * D]),
            "adab": _col_layout(ada_b[0][0:2 * D]),
            "wq": np.ascontiguousarray(w_in[:, 0:D][:, sl]),
            "wk": np.ascontiguousarray(w_in[:, D:2 * D][:, sl]),
            "wv": np.ascontiguousarray(w_in[:, 2 * D:3 * D][:, sl]),
            "wf": np.ascontiguousarray(w_in[:, 3 * D + hh * 8:3 * D + hh * 8 + 8]),
            "wg": np.ascontiguousarray(w_in[:, 3 * D + 16:][:, sl]),
            "bfb": _bc(b_f[hh * 8:(hh + 1) * 8]),
            "qgb": _bc(np.tile(q_g, 8)),
            "kgb": _bc(np.tile(k_g, 8)),
        })
    r1 = _run(_prog("fox", build_fox), maps)
    yT = [np.concatenate([r1[2 * b]["ygT"], r1[2 * b + 1]["ygT"]], axis=0) for b in range(B)]
    xT = [np.ascontiguousarray(x[b].T) for b in range(B)]
    tlp = _prog("tl", build_tl)
    r2 = _run(tlp, _tl_maps(yT, xT, c, ada_w[0], ada_b[0], ln_g[0], ln_b[0], f(fox_w_o)[0], ffn_w_in[0], ffn_w_out[0]))
    x1T = [np.concatenate([r2[2 * b]["oT"], r2[2 * b + 1]["oT"]], axis=1) for b in range(B)]
    mu = f(rwkv_mu)[0]
    w_rkv = f(rwkv_w_rkv)[0]
    P = dict(w0=f(rwkv_w0)[0], w1=f(rwkv_w1)[0], w2=f(rwkv_w2)[0], a0=f(rwkv_a0)[0], a1=f(rwkv_a1)[0], a2=f(rwkv_a2)[0],
             g1=f(rwkv_g1)[0], g2=f(rwkv_g2)[0], k_k=f(rwkv_k_k)[0], k_a=f(rwkv_k_a)[0], r_k=f(rwkv_r_k)[0].reshape(-1),
             lnx_g=f(rwkv_lnx_g)[0], lnx_b=f(rwkv_lnx_b)[0])
    mu_l = np.concatenate([_col_layout(mu[n]) for n in range(6)], axis=1)
    maps = []
    for core in range(NCORES):
        b, hh = core // 2, core % 2
        sl = slice(hh * 512, (hh + 1) * 512)
        maps.append({
            "xT": x1T[b],
            "cT": _col_layout(c[b]),
            "adaw": np.ascontiguousarray(ada_w[1][:, 0:2 * D]),
            "adab": _col_layout(ada_b[1][0:2 * D]),
            "mu": mu_l,
            "wr": np.ascontiguousarray(w_rkv[0][:, sl]),
            "wk": np.ascontiguousarray(w_rkv[1][:, sl]),
            "wv": np.ascontiguousarray(w_rkv[2][:, sl]),
            "w1": P["w1"], "a1": P["a1"], "g1": P["g1"],
            "w2": np.ascontiguousarray(P["w2"][:, sl]),
            "a2": np.ascontiguousarray(P["a2"][:, sl]),
            "g2": np.ascontiguousarray(P["g2"][:, sl]),
            "w0b": _bc(P["w0"][sl]), "a0b": _bc(P["a0"][sl]), "kkb": _bc(P["k_k"][sl]), "kab": _bc(P["k_a"][sl]),
            "rkb": _bc(P["r_k"][sl]), "lngb": _bc(P["lnx_g"][sl]), "lnbb": _bc(P["lnx_b"][sl]),
        })
    r3 = _run(_prog("rwkv", build_rwkv), maps)
    y2T = [np.concatenate([r3[2 * b]["yT"], r3[2 * b + 1]["yT"]], axis=0) for b in range(B)]
    r4 = _run(tlp, _tl_maps(y2T, x1T, c, ada_w[1], ada_b[1], ln_g[1], ln_b[1], f(rwkv_w_o)[0], ffn_w_in[1], ffn_w_out[1]))
    out = np.empty(x.shape, np.float32)
    for core in range(NCORES):
        b, th = core // 2, core % 2
        out[b, th * 2048:(th + 1) * 2048, :] = r4[core]["oT"].T
    return out


def kernel(x, c, ada_w, ada_b, ln_g, ln_b, ffn_w_in, ffn_w_out,
           fox_w_in, fox_b_f, fox_q_g, fox_k_g, fox_w_o,
           rwkv_mu, rwkv_w_rkv, rwkv_w0, rwkv_w1, rwkv_w2, rwkv_a0, rwkv_a1, rwkv_a2,
           rwkv_g1, rwkv_g2, rwkv_k_k, rwkv_k_a, rwkv_r_k, rwkv_lnx_g, rwkv_lnx_b, rwkv_w_o):
    f = lambda a: np.ascontiguousarray(np.asarray(a, dtype=np.float32))
    x, c, ada_w, ada_b, ln_g, ln_b = f(x), f(c), f(ada_w), f(ada_b), f(ln_g), f(ln_b)
    ffn_w_in, ffn_w_out = f(ffn_w_in), f(ffn_w_out)
    w_in = f(fox_w_in)[0]
    b_f, q_g, k_g = f(fox_b_f)[0], f(fox_q_g)[0], f(fox_k_g)[0]
    mu = f(rwkv_mu)[0]
    w_rkv = f(rwkv_w_rkv)[0]
    P = dict(w0=f(rwkv_w0)[0], w1=f(rwkv_w1)[0], w2=f(rwkv_w2)[0], a0=f(rwkv_a0)[0], a1=f(rwkv_a1)[0], a2=f(rwkv_a2)[0],
             g1=f(rwkv_g1)[0], g2=f(rwkv_g2)[0], k_k=f(rwkv_k_k)[0], k_a=f(rwkv_k_a)[0], r_k=f(rwkv_r_k)[0].reshape(-1),
             lnx_g=f(rwkv_lnx_g)[0], lnx_b=f(rwkv_lnx_b)[0])
    mu_l = np.concatenate([_col_layout(mu[n]) for n in range(6)], axis=1)
    wos = [f(fox_w_o)[0], f(rwkv_w_o)[0]]
    tl_common = []
    for L in range(2):
        tl_common.append({
            "adaw_t%d" % L: np.ascontiguousarray(ada_w[L][:, 2 * D:]), "adab_t%d" % L: _col_layout(ada_b[L][2 * D:]),
            "lng_t%d" % L: _col_layout(ln_g[L].reshape(-1)), "lnb_t%d" % L: _col_layout(ln_b[L].reshape(-1)),
            "wo_t%d" % L: wos[L], "win_t%d" % L: ffn_w_in[L], "wout_t%d" % L: ffn_w_out[L]})
    adaw_f = np.ascontiguousarray(ada_w[0][:, 0:2 * D])
    adaw_r = np.ascontiguousarray(ada_w[1][:, 0:2 * D])
    maps = []
    for core in range(NCORES):
        b, j = core // 2, core % 2
        sl = slice(j * 512, (j + 1) * 512)
        cT = _col_layout(c[b])
        m = {
            "x_f": x[b], "cT_f": cT, "adaw_f": adaw_f, "adab_f": _col_layout(ada_b[0][0:2 * D]),
            "wq_f": np.ascontiguousarray(w_in[:, 0:D][:, sl]), "wk_f": np.ascontiguousarray(w_in[:, D:2 * D][:, sl]),
            "wv_f": np.ascontiguousarray(w_in[:, 2 * D:3 * D][:, sl]), "wf_f": np.ascontiguousarray(w_in[:, 3 * D + j * 8:3 * D + j * 8 + 8]),
            "wg_f": np.ascontiguousarray(w_in[:, 3 * D + 16:][:, sl]),
            "bfb_f": _bc(b_f[j * 8:(j + 1) * 8]), "qgb_f": _bc(np.tile(q_g, 8)), "kgb_f": _bc(np.tile(k_g, 8)),
            "cT_r": cT, "adaw_r": adaw_r, "adab_r": _col_layout(ada_b[1][0:2 * D]), "mu_r": mu_l,
            "wr_r": np.ascontiguousarray(w_rkv[0][:, sl]), "wk_r": np.ascontiguousarray(w_rkv[1][:, sl]), "wv_r": np.ascontiguousarray(w_rkv[2][:, sl]),
            "w1_r": P["w1"], "a1_r": P["a1"], "g1_r": P["g1"],
            "w2_r": np.ascontiguousarray(P["w2"][:, sl]), "a2_r": np.ascontiguousarray(P["a2"][:, sl]), "g2_r": np.ascontiguousarray(P["g2"][:, sl]),
            "w0b_r": _bc(P["w0"][sl]), "a0b_r": _bc(P["a0"][sl]), "kkb_r": _bc(P["k_k"][sl]), "kab_r": _bc(P["k_a"][sl]),
            "rkb_r": _bc(P["r_k"][sl]), "lngb_r": _bc(P["lnx_g"][sl]), "lnbb_r": _bc(P["lnx_b"][sl]),
            "cT_t0": cT, "cT_t1": cT,
            "sel": np.ascontiguousarray(np.broadcast_to(np.array([1.0 - j, float(j)], np.float32)[None, :], (128, 2))),
            "xTh": np.ascontiguousarray(x[b, j * 2048:(j + 1) * 2048, :].T),
        }
        m.update(tl_common[0])
        m.update(tl_common[1])
        maps.append(m)
    res = _run(_prog("fused", build_fused), maps)
    out = np.empty(x.shape, np.float32)
    for core in range(NCORES):
        b, j = core // 2, core % 2
        out[b, j * 2048:(j + 1) * 2048, :] = res[core]["oT"].T
    return out
```

```python
import numpy as np
from contextlib import ExitStack
import concourse.bass as bass
import concourse.mybir as mybir
from concourse.bass_utils import run_bass_kernel_spmd

F32 = mybir.dt.float32
BF16 = mybir.dt.bfloat16
AF = mybir.ActivationFunctionType
ALU = mybir.AluOpType
AX = mybir.AxisListType

ENGS = ("pe", "act", "dve", "pool", "sp")
LAST_SEMS = []
SEM_CTR = [0]


def phase_end(nc):
    nc.all_engine_barrier()
    nc.clear_and_free_semaphores(list(LAST_SEMS))
    LAST_SEMS[:] = []
    nc.all_engine_barrier()


class Res:
    __slots__ = ("name", "last_w", "readers", "excl")

    def __init__(self, name, excl=False):
        self.name = name
        self.excl = excl
        self.last_w = None
        self.readers = []


class Op:
    __slots__ = ("eng", "fn", "deps", "sig", "count", "is_dma", "key", "semval", "vc", "waits", "gid")


class Deferred:
    op = None


class Sched:
    def __init__(self, nc):
        self.nc = nc
        self.ops = []
        self.dma_counts = {}

    def res(self, name, excl=False):
        return Res(name, excl)

    def _add(self, op, r, w, extra=()):
        deps = [(d, "raw") for d in extra]
        for x in r:
            if x.last_w is not None:
                deps.append((x.last_w, "raw"))
            if x.excl:
                for rd in x.readers:
                    deps.append((rd, "war"))
        for x in w:
            if x.last_w is not None:
                deps.append((x.last_w, "waw"))
            for rd in x.readers:
                deps.append((rd, "war"))
        dd = []
        for d, kind in deps:
            if d is op:
                continue
            if (not d.is_dma) and (not op.is_dma) and d.eng == op.eng:
                if op.eng == "pe" or kind == "war":
                    continue
            dd.append(d)
        op.deps = dd
        for d in dd:
            d.sig = True
        for x in w:
            x.last_w = op
            x.readers = []
        for x in r:
            if x.last_w is not op:
                if not op.is_dma:
                    x.readers = [q for q in x.readers if q.is_dma or q.eng != op.eng]
                x.readers.append(op)
        op.gid = len(self.ops)
        self.ops.append(op)
        return op

    def begin_defer(self):
        self.flush_deferred()
        self._defer = []
        self._capturing = True

    def end_defer(self):
        self._capturing = False

    def flush_deferred(self):
        q = getattr(self, "_defer", None)
        self._defer = None
        self._capturing = False
        if q:
            for kind, ph, a, kw in q:
                real = getattr(self, kind)(*a, **kw)
                if ph is not None:
                    ph.op = real
        self._deferred_done = True

    def op(self, eng, fn, r=(), w=()):
        if getattr(self, "_capturing", False):
            self._defer.append(("op", None, (eng, fn), dict(r=list(r), w=list(w))))
            return None
        o = Op()
        o.eng = eng
        o.fn = fn
        o.is_dma = False
        o.sig = False
        o.key = eng
        return self._add(o, r, w)

    def dma(self, eng, out, in_, r=(), w=(), key=None, **kw):
        if getattr(self, "_capturing", False):
            ph = Deferred()
            self._defer.append(("dma", ph, (eng, out, in_), dict(r=list(r), w=list(w), key=key, **kw)))
            return ph
        o = Op()
        o.eng = eng
        o.is_dma = True
        o.sig = True
        assert key is not None
        o.key = "dma:" + key
        o.fn = lambda e: e.dma_start(out=out, in_=in_, **kw)
        return self._add(o, r, w)

    def cc(self, kind, groups, in_ap, out_ap, r=(), w=(), key=None, extra=()):
        o = Op()
        o.eng = "pool"
        o.is_dma = True
        o.sig = True
        o.key = "cc:" + key
        o.fn = lambda e: e.collective_compute(kind, ALU.bypass, replica_groups=groups, ins=[in_ap], outs=[out_ap])
        return self._add(o, r, w, extra)

    def finalize(self, final_wait_eng="sp"):
        nc = self.nc
        counts = {e: 0 for e in ENGS}
        dmac = {}
        for o in self.ops:
            if o.is_dma:
                dmac[o.key] = dmac.get(o.key, 0) + (1 if o.key.startswith("cc:") else 16)
                o.semval = dmac[o.key]
            elif o.sig:
                counts[o.eng] += 1
                o.semval = counts[o.eng]
        seen = {e: {} for e in ENGS}
        for o in self.ops:
            s = seen[o.eng]
            need = {}
            for d in o.deps:
                if s.get(d.key, 0) < d.semval:
                    if d.key not in need or need[d.key].semval < d.semval:
                        need[d.key] = d
            o.waits = [(d.key, d.semval) for d in need.values()]
            for d in need.values():
                for k, v in d.vc.items():
                    if s.get(k, 0) < v:
                        s[k] = v
            if o.sig:
                vc = dict(s)
                vc[o.key] = o.semval
                o.vc = vc
            else:
                o.vc = None
        return counts, dmac

    def emit(self, stack, tail_waits=()):
        nc = self.nc
        self.flush_deferred()
        tail_waits = [getattr(o, "op", None) or o for o in tail_waits]
        counts, dmac = self.finalize()
        sems = {}
        for k in list(ENGS) + list(dmac):
            SEM_CTR[0] += 1
            sems[k] = nc.alloc_semaphore(name="sem%d" % SEM_CTR[0])
        self.nsems = len(sems)
        LAST_SEMS[:] = list(sems.values())
        per = {e: [o for o in self.ops if o.eng == e] for e in ENGS}
        block = stack.enter_context(nc.Block())

        def run(eng_name, eng):
            for o in per[eng_name]:
                for k, v in o.waits:
                    eng.wait_ge(sems[k], v)
                ins = o.fn(eng)
                if o.sig:
                    if o.is_dma:
                        ins.then_inc(sems[o.key], 1 if o.key.startswith("cc:") else 16)
                    else:
                        ins.then_inc(sems[o.key], 1)
            if eng_name == "sp":
                done = {}
                for o in tail_waits:
                    done[o.key] = max(done.get(o.key, 0), o.semval)
                for k, v in done.items():
                    eng.wait_ge(sems[k], v)

        @block.tensor
        def _(e):
            run("pe", e)

        @block.scalar
        def _(e):
            run("act", e)

        @block.vector
        def _(e):
            run("dve", e)

        @block.gpsimd
        def _(e):
            run("pool", e)

        @block.sync
        def _(e):
            run("sp", e)


D = 1024
DFF = 2816
NFC = 22
ALPHA = 4.0 ** 0.25
LN_EPS = 1e-5
EPS_P = LN_EPS / (ALPHA * ALPHA)


class Ctx:
    def __init__(self, nc, st, pfx=""):
        self.nc = nc
        self.st = st
        self.S = Sched(nc)
        self.n = 0
        self.pfx = pfx

    def sb(self, name, shape, dt):
        return self.st.enter_context(self.nc.sbuf_tensor(self.pfx + "sb_" + name, list(shape), dt))

    def ps(self, name, shape, dt=F32):
        return self.st.enter_context(self.nc.psum_tensor(self.pfx + "ps_" + name, list(shape), dt))

    def R(self, name, excl=False):
        return self.S.res(name, excl)


def make_consts(C):
    S = C.S
    C.ones = C.sb("ones", [128, 128], F32)
    C.r_ones = C.R("ones")
    S.op("pool", lambda e: e.memset(C.ones[:], 1.0), w=[C.r_ones])
    C.ident = C.sb("ident", [128, 128], F32)
    C.r_ident = C.R("ident")
    S.op("pool", lambda e: e.memset(C.ident[:], 1.0), w=[C.r_ident])
    S.op("pool", lambda e: e.affine_select(out=C.ident[:], in_=C.ident[:], pattern=[[-1, 128]], compare_op=ALU.is_equal, fill=0.0, base=0, channel_multiplier=1), r=[C.r_ident], w=[C.r_ident])


def compute_mods(C, cT_d, adaw_d, adab_d, nm, wbufs, psm, r_psm):
    S = C.S
    cT = C.sb("cT", [128, 8], F32)
    r_cT = C.R("cT")
    S.dma("sp", cT[:], cT_d, w=[r_cT], key="cT")
    S.op("act", lambda e: e.activation(out=cT[:], in_=cT[:], func=AF.Silu), r=[r_cT], w=[r_cT])
    adab = C.sb("adab", [128, nm * 8], F32)
    r_adab = C.R("adab")
    S.dma("sp", adab[:], adab_d, w=[r_adab], key="adab")
    mods = C.sb("mods", [128, nm * 8], F32)
    r_mods = C.R("mods")
    src = adaw_d.rearrange("(kc p) n -> p kc n", p=128)
    st_tmp = None
    if wbufs is None:
        st_keep, st_tmp = C.st, ExitStack()
        C.st = st_tmp
        wbufs = [(C.sb("wA%d" % i, [128, 8, 256], F32), C.R("wA%d" % i)) for i in range(2)]
        C.st = st_keep
    for m in range(nm):
        for q in range(4):
            bi = (m * 4 + q) % 2
            wt, r_wt = wbufs[bi]
            c0 = m * 1024 + q * 256
            S.dma("sp", wt[:], src[:, :, c0:c0 + 256], w=[r_wt], key="adaw%d" % bi)
            for j in range(2):
                cc = q * 2 + j
                for kc in range(8):
                    S.op("pe", lambda e, m=m, cc=cc, kc=kc, j=j, wt=wt: e.matmul(psm[:, m * 8 + cc:m * 8 + cc + 1], wt[:, kc, j * 128:(j + 1) * 128], cT[:, kc:kc + 1], start=(kc == 0), stop=(kc == 7)), r=[r_wt, r_cT], w=[r_psm])
    S.op("dve", lambda e: e.tensor_tensor(out=mods[:], in0=psm[:, 0:nm * 8], in1=adab[:], op=ALU.add), r=[r_psm, r_adab], w=[r_mods])
    if st_tmp is not None:
        st_tmp.close()
    return mods, r_mods


def ln_group(C, zT, r_z, g, n0, colA, colB, r_cols, outs, P):
    S = C.S
    ps_s, r_ps_s = P["st0"]
    ps_q, r_ps_q = P["st1"]
    sq, r_sq = P["sq"]
    for cc in range(8):
        S.op("act", lambda e, cc=cc: e.activation(out=sq[cc % 2][:], in_=zT[:, cc, n0:n0 + 512], func=AF.Square), r=[r_z[cc]], w=[r_sq[cc % 2]])
        S.op("pe", lambda e, cc=cc: e.matmul(ps_s[:], C.ones[:], zT[:, cc, n0:n0 + 512], start=(cc == 0), stop=(cc == 7)), r=[C.r_ones, r_z[cc]], w=[r_ps_s])
        S.op("pe", lambda e, cc=cc: e.matmul(ps_q[:], C.ones[:], sq[cc % 2][:], start=(cc == 0), stop=(cc == 7)), r=[C.r_ones, r_sq[cc % 2]], w=[r_ps_q])
    mean, r_mean = P["mean"]
    rstd, r_rstd = P["rstd"]
    S.op("act", lambda e: e.mul(out=mean[:], in_=ps_s[:], mul=1.0 / D), r=[r_ps_s], w=[r_mean])
    S.op("dve", lambda e: e.tensor_tensor(out=rstd[:], in0=mean[:], in1=mean[:], op=ALU.mult), r=[r_mean], w=[r_rstd])
    S.op("dve", lambda e: e.scalar_tensor_tensor(out=rstd[:], in0=ps_q[:], scalar=1.0 / D, in1=rstd[:], op0=ALU.mult, op1=ALU.subtract), r=[r_ps_q, r_rstd], w=[r_rstd])
    S.op("dve", lambda e: e.tensor_scalar(out=rstd[:], in0=rstd[:], scalar1=P["eps"], scalar2=None, op0=ALU.add), r=[r_rstd], w=[r_rstd])
    S.op("act", lambda e: e.activation(out=rstd[:], in_=rstd[:], func=AF.Sqrt), r=[r_rstd], w=[r_rstd])
    S.op("dve", lambda e: e.reciprocal(out=rstd[:], in_=rstd[:]), r=[r_rstd], w=[r_rstd])
    S.op("dve", lambda e: e.scalar_tensor_tensor(out=mean[:], in0=mean[:], scalar=-1.0, in1=rstd[:], op0=ALU.mult, op1=ALU.mult), r=[r_mean, r_rstd], w=[r_mean])
    tt, r_tt = P["tt"]
    for cc in range(8):
        k = cc % 2
        S.op("dve", lambda e, cc=cc, k=k: e.tensor_tensor(out=tt[k][:], in0=zT[:, cc, n0:n0 + 512], in1=rstd[:], op=ALU.mult), r=[r_z[cc], r_rstd], w=[r_tt[k]])
        S.op("dve", lambda e, cc=cc, k=k: e.tensor_tensor(out=tt[k][:], in0=tt[k][:], in1=mean[:], op=ALU.add), r=[r_tt[k], r_mean], w=[r_tt[k]])
        for (dst, r_dst, gt, go, bt, bo, eng) in outs:
            if eng == "act":
                S.op("act", lambda e, cc=cc, k=k, dst=dst, gt=gt, go=go, bt=bt, bo=bo: e.activation(out=dst[:, cc, n0:n0 + 512], in_=tt[k][:], func=AF.Identity, scale=gt[:, go + cc:go + cc + 1], bias=bt[:, bo + cc:bo + cc + 1]), r=[r_tt[k]] + r_cols, w=[r_dst[cc]])
            else:
                S.op(eng, lambda e, cc=cc, k=k, dst=dst, gt=gt, go=go, bt=bt, bo=bo: e.tensor_scalar(out=dst[:, cc, n0:n0 + 512], in0=tt[k][:], scalar1=gt[:, go + cc:go + cc + 1], scalar2=bt[:, bo + cc:bo + cc + 1], op0=ALU.mult, op1=ALU.add), r=[r_tt[k]] + r_cols, w=[r_dst[cc]])


def build_tl(NT=2048, HALF=1024, debug=False):
    nc = bass.Bass("TRN2", target_bir_lowering=False)
    dt = lambda n, s, d, k="ExternalInput": nc.dram_tensor(n, list(s), d, kind=k).ap()
    yT_d = dt("yT", [D, NT], BF16)
    xT_d = dt("xT", [D, NT], F32)
    cT_d = dt("cT", [128, 8], F32)
    adaw_d = dt("adaw", [D, 4 * D], F32)
    adab_d = dt("adab", [128, 32], F32)
    lng_d = dt("lng", [128, 16], F32)
    lnb_d = dt("lnb", [128, 16], F32)
    wo_d = dt("wo", [D, D], F32)
    win_d = dt("win", [D, 2 * DFF], F32)
    wout_d = dt("wout", [DFF, D], F32)
    oT_d = dt("oT", [D, NT], F32, "ExternalOutput")
    tl_phase(nc, "", yT_d, xT_d, cT_d, adaw_d, adab_d, lng_d, lnb_d, wo_d, win_d, wout_d, oT_d, NT, HALF, debug=debug)
    return nc


def tl_phase(nc, pfx, yT_d, xT_d, cT_d, adaw_d, adab_d, lng_d, lnb_d, wo_d, win_d, wout_d, oT_d, NT=2048, HALF=1024, debug=False, sel_d=None, after=None):
    dt = lambda n, s, d, k="ExternalInput": nc.dram_tensor(n, list(s), d, kind=k).ap()
    NG = HALF // 512
    if debug:
        dbg_cols = dt("dbg_cols", [128, 48], F32, "ExternalOutput")
        dbg_mods = dt("dbg_mods", [128, 32], F32, "ExternalOutput")
        dbg_x1 = dt("dbg_x1", [D, HALF], F32, "ExternalOutput")
        dbg_h = dt("dbg_h", [D, HALF], BF16, "ExternalOutput")
        dbg_a = dt("dbg_a", [DFF, HALF], BF16, "ExternalOutput")
    with ExitStack() as st:
        C = Ctx(nc, st, pfx)
        S = C.S
        make_consts(C)
        PB = [(C.ps("pb%d" % i, [128, 512]), C.R("pb%d" % i, True)) for i in range(8)]
        if sel_d is not None:
            selt = C.sb("selt", [128, 2], F32)
            r_selt = C.R("selt")
            S.dma("sp", selt[:], sel_d, w=[r_selt], key="selt")
            ystg = [[(C.sb("ystg%d_%d" % (i, j), [128, HALF], BF16), C.R("ystg%d_%d" % (i, j))) for j in range(2)] for i in range(2)]
        xT = C.sb("xT", [128, 8, HALF], F32)
        r_x = [C.R("xT%d" % cc) for cc in range(8)]
        yT = C.sb("yT", [128, 8, HALF], BF16)
        r_y = [C.R("yT%d" % cc) for cc in range(8)]
        hT = yT
        r_h = r_y
        aT = C.sb("aT", [128, NFC, HALF], BF16)
        r_a = [C.R("aT%d" % f) for f in range(NFC)]
        wo = C.sb("wo", [128, 8, D], BF16)
        r_wo = C.R("wo")
        wA = [(C.sb("wA%d" % i, [128, 8, 256], F32), C.R("wA%d" % i)) for i in range(2)]
        wi = [(C.sb("wi%d" % i, [128, 8, 512], BF16), C.R("wi%d" % i)) for i in range(2)]
        wo2 = [(C.sb("wo2%d" % i, [128, NFC, 256], BF16), C.R("wo2%d" % i)) for i in range(2)]
        P = {
            "st0": PB[6], "st1": PB[7],
            "sq": ([C.sb("sq%d" % i, [128, 512], F32) for i in range(2)], [C.R("sq%d" % i) for i in range(2)]),
            "mean": (C.sb("mean", [128, 512], F32), C.R("mean")),
            "rstd": (C.sb("rstd", [128, 512], F32), C.R("rstd")),
            "tt": ([C.sb("tt%d" % i, [128, 512], F32) for i in range(2)], [C.R("tt%d" % i) for i in range(2)]),
            "eps": EPS_P,
        }
        sl = [C.sb("sl%d" % i, [128, 512], F32) for i in range(2)]
        r_sl = [C.R("sl%d" % i) for i in range(2)]
        mods, r_mods = compute_mods(C, cT_d, adaw_d, adab_d, 4, wA, PB[5][0], PB[5][1])
        lng = C.sb("lng", [128, 16], F32)
        lnb = C.sb("lnb", [128, 16], F32)
        r_ln = C.R("ln")
        S.dma("sp", lng[:], lng_d, w=[r_ln], key="lng")
        S.dma("sp", lnb[:], lnb_d, w=[r_ln], key="lnb")
        cols = C.sb("cols", [128, 48], F32)
        r_cols = C.R("cols")
        S.op("dve", lambda e: e.tensor_scalar(out=cols[:, 0:8], in0=mods[:, 0:8], scalar1=1.0 / ALPHA, scalar2=None, op0=ALU.mult), r=[r_mods], w=[r_cols])
        S.op("dve", lambda e: e.tensor_scalar(out=cols[:, 8:16], in0=mods[:, 24:32], scalar1=1.0 / ALPHA, scalar2=None, op0=ALU.mult), r=[r_mods], w=[r_cols])
        S.op("dve", lambda e: e.tensor_scalar(out=cols[:, 32:40], in0=mods[:, 16:24], scalar1=1.0, scalar2=None, op0=ALU.add), r=[r_mods], w=[r_cols])
        S.op("dve", lambda e: e.tensor_tensor(out=cols[:, 16:24], in0=lng[:, 0:8], in1=cols[:, 32:40], op=ALU.mult), r=[r_ln, r_cols], w=[r_cols])
        S.op("dve", lambda e: e.tensor_tensor(out=cols[:, 24:32], in0=lnb[:, 0:8], in1=cols[:, 32:40], op=ALU.mult), r=[r_ln, r_cols], w=[r_cols])
        S.op("dve", lambda e: e.tensor_tensor(out=cols[:, 24:32], in0=cols[:, 24:32], in1=mods[:, 8:16], op=ALU.add), r=[r_mods, r_cols], w=[r_cols])
        rc = [r_cols, r_ln]
        S.dma("pool", wo[:], wo_d.rearrange("(kc p) n -> p kc n", p=128), w=[r_wo], key="wo")
        xsrc = xT_d.rearrange("(kc p) t -> p kc t", p=128)
        if sel_d is None:
            ysrc_ = yT_d.rearrange("(kc p) t -> p kc t", p=128)
            ysrc_fn = lambda cc, a, n: ysrc_[:, cc, a:a + n]
        else:
            ysrc_fn = lambda cc, a, n: yT_d[cc % 4, cc // 4, :, a:a + n]
        osrc = oT_d.rearrange("(kc p) t -> p kc t", p=128)
        winr = win_d.rearrange("(kc p) n -> p kc n", p=128)
        woutr = wout_d.rearrange("(fc p) n -> p fc n", p=128)
        tails = []
        pbi = 0
        for half in range(NT // HALF):
            t0 = half * HALF
            for cc in range(8):
                if sel_d is None:
                    S.dma("sp", yT[:, cc, :], ysrc_fn(cc, t0, HALF), w=[r_y[cc]], key="yT%d" % cc)
                else:
                    (ya, r_ya), (yb, r_yb) = ystg[cc % 2]
                    S.dma("sp", ya[:], ysrc_fn(cc, t0, HALF), w=[r_ya], key="ysa%d" % (cc % 2))
                    S.dma("sp", yb[:], ysrc_fn(cc, NT + t0, HALF), w=[r_yb], key="ysb%d" % (cc % 2))
                    S.op("dve", lambda e, ya=ya: e.tensor_scalar(out=ya[:], in0=ya[:], scalar1=selt[:, 0:1], scalar2=None, op0=ALU.mult), r=[r_ya, r_selt], w=[r_ya])
                    S.op("dve", lambda e, ya=ya, yb=yb, cc=cc: e.scalar_tensor_tensor(out=yT[:, cc, :], in0=yb[:], scalar=selt[:, 1:2], in1=ya[:], op0=ALU.mult, op1=ALU.add), r=[r_ya, r_yb, r_selt], w=[r_y[cc]])
                S.dma("sp", xT[:, cc, :], xsrc[:, cc, t0:t0 + HALF], w=[r_x[cc]], key="xT%d" % cc)
            for g in range(NG):
                n0 = g * 512
                for cc in range(8):
                    pb, r_pb = PB[pbi % 4]
                    pbi += 1
                    for kc in range(8):
                        S.op("pe", lambda e, pb=pb, cc=cc, kc=kc, n0=n0: e.matmul(pb[:], wo[:, kc, cc * 128:(cc + 1) * 128], yT[:, kc, n0:n0 + 512], start=(kc == 0), stop=(kc == 7)), r=[r_wo] + r_y, w=[r_pb])
                    S.op("dve", lambda e, pb=pb, cc=cc, n0=n0: e.scalar_tensor_tensor(out=xT[:, cc, n0:n0 + 512], in0=pb[:], scalar=cols[:, cc:cc + 1], in1=xT[:, cc, n0:n0 + 512], op0=ALU.mult, op1=ALU.add), r=[r_pb, r_x[cc], r_cols], w=[r_x[cc]])
            for g in range(NG):
                n0 = g * 512
                ln_group(C, xT, r_x, g, n0, None, None, rc,
                         [(xT, r_x, lng, 0, lnb, 0, "act"), (hT, r_h, cols, 16, cols, 24, "act")], P)
            if debug and half == 0:
                tails.append(S.dma("sp", dbg_cols, cols[:], r=[r_cols], key="dbg0"))
                tails.append(S.dma("sp", dbg_mods, mods[:], r=[r_mods], key="dbg1"))
                tails.append(S.dma("sp", dbg_x1.rearrange("(kc p) t -> p kc t", p=128), xT[:], r=r_x, key="dbg2"))
                tails.append(S.dma("sp", dbg_h.rearrange("(kc p) t -> p kc t", p=128), hT[:], r=r_h, key="dbg3"))
            for blk in range(NFC // 2):
                wt, r_wt = wi[blk % 2]
                c0 = blk * 256
                S.dma("pool", wt[:, :, 0:256], winr[:, :, c0:c0 + 256], w=[r_wt], key="wi%d" % (blk % 2))
                S.dma("pool", wt[:, :, 256:512], winr[:, :, DFF + c0:DFF + c0 + 256], w=[r_wt], key="wi%d" % (blk % 2))
                for j in range(2):
                    fc = blk * 2 + j
                    for g in range(NG):
                        n0 = g * 512
                        pg, r_pg = PB[pbi % 4]
                        pu, r_pu = PB[(pbi + 1) % 4]
                        pbi += 2
                        for kc in range(8):
                            S.op("pe", lambda e, pg=pg, wt=wt, j=j, kc=kc, n0=n0: e.matmul(pg[:], wt[:, kc, j * 128:(j + 1) * 128], hT[:, kc, n0:n0 + 512], start=(kc == 0), stop=(kc == 7)), r=[r_wt] + r_h, w=[r_pg])
                        for kc in range(8):
                            S.op("pe", lambda e, pu=pu, wt=wt, j=j, kc=kc, n0=n0: e.matmul(pu[:], wt[:, kc, 256 + j * 128:256 + (j + 1) * 128], hT[:, kc, n0:n0 + 512], start=(kc == 0), stop=(kc == 7)), r=[r_wt] + r_h, w=[r_pu])
                        k = (fc * NG + g) % 2
                        S.op("act", lambda e, pg=pg, k=k: e.activation(out=sl[k][:], in_=pg[:], func=AF.Silu), r=[r_pg], w=[r_sl[k]])
                        S.op("dve", lambda e, pu=pu, k=k, fc=fc, n0=n0: e.tensor_tensor(out=aT[:, fc, n0:n0 + 512], in0=pu[:], in1=sl[k][:], op=ALU.mult), r=[r_pu, r_sl[k]], w=[r_a[fc]])
            if debug and half == 0:
                tails.append(S.dma("sp", dbg_a.rearrange("(kc p) t -> p kc t", p=128), aT[:], r=r_a, key="dbg4"))
            for blk in range(4):
                wt, r_wt = wo2[blk % 2]
                S.dma("pool", wt[:], woutr[:, :, blk * 256:(blk + 1) * 256], w=[r_wt], key="wo2%d" % (blk % 2))
                for j in range(2):
                    cc = blk * 2 + j
                    for g in range(NG):
                        n0 = g * 512
                        pb, r_pb = PB[pbi % 4]
                        pbi += 1
                        for fc in range(NFC):
                            S.op("pe", lambda e, pb=pb, wt=wt, j=j, fc=fc, n0=n0: e.matmul(pb[:], wt[:, fc, j * 128:(j + 1) * 128], aT[:, fc, n0:n0 + 512], start=(fc == 0), stop=(fc == NFC - 1)), r=[r_wt, r_a[fc]], w=[r_pb])
                        S.op("dve", lambda e, pb=pb, cc=cc, n0=n0: e.scalar_tensor_tensor(out=xT[:, cc, n0:n0 + 512], in0=pb[:], scalar=cols[:, 8 + cc:9 + cc], in1=xT[:, cc, n0:n0 + 512], op0=ALU.mult, op1=ALU.add), r=[r_pb, r_x[cc], r_cols], w=[r_x[cc]])
            for g in range(NG):
                n0 = g * 512
                ln_group(C, xT, r_x, g, n0, None, None, rc, [(xT, r_x, lng, 8, lnb, 8, "act")], P)
            for cc in range(8):
                o = S.dma("sp", osrc[:, cc, t0:t0 + HALF], xT[:, cc, :], r=[r_x[cc]], w=[], key="oT%d" % cc)
                tails.append(o)
        if after is not None:
            after(C, tails)
        S.emit(st, tail_waits=tails)


T = 4096
NH = 8
HD = 64
QK_EPS = 1e-6
NEG = -30000.0
import os
LAG = int(os.environ.get("FOX_LAG", "2"))
FOX_DVE = int(os.environ.get("FOX_DVE", "1"))


def build_fox(TT=T, debug=False):
    nc = bass.Bass("TRN2", target_bir_lowering=False)
    dt = lambda n, s, d, k="ExternalInput": nc.dram_tensor(n, list(s), d, kind=k).ap()
    ins = fox_decl(dt, TT)
    yg_d = dt("ygT", [512, TT], BF16, "ExternalOutput")
    fox_phase(nc, "", ins, yg_d, TT, debug=debug)
    return nc


def fox_decl(dt, TT=T, sfx=""):
    return dict(
        x=dt("x" + sfx, [TT, D], F32), cT=dt("cT" + sfx, [128, 8], F32), adaw=dt("adaw" + sfx, [D, 2 * D], F32), adab=dt("adab" + sfx, [128, 16], F32),
        wq=dt("wq" + sfx, [D, 512], F32), wk=dt("wk" + sfx, [D, 512], F32), wv=dt("wv" + sfx, [D, 512], F32), wf=dt("wf" + sfx, [D, 8], F32), wg=dt("wg" + sfx, [D, 512], F32),
        bfb=dt("bfb" + sfx, [128, 8], F32), qgb=dt("qgb" + sfx, [128, 512], F32), kgb=dt("kgb" + sfx, [128, 512], F32))


def fox_phase(nc, pfx, ins, yg_d, TT=T, debug=False, after=None):
    dt = lambda n, s, d, k="ExternalInput": nc.dram_tensor(n, list(s), d, kind=k).ap()
    x_d, cT_d, adaw_d, adab_d = ins["x"], ins["cT"], ins["adaw"], ins["adab"]
    wq_d, wk_d, wv_d, wf_d, wg_d = ins["wq"], ins["wk"], ins["wv"], ins["wf"], ins["wg"]
    bf_d, qg_d, kg_d = ins["bfb"], ins["qgb"], ins["kgb"]
    NGR = TT // 512
    NTL = TT // 128
    if debug:
        dbg_h = dt("dbg_h", [D, 512], BF16, "ExternalOutput")
        dbg_q = dt("dbg_q", [128, 4, 512], BF16, "ExternalOutput")
        dbg_k = dt("dbg_k", [128, 4, TT], BF16, "ExternalOutput")
        dbg_F = dt("dbg_F", [128, NTL, 8], F32, "ExternalOutput")
        dbg_v = dt("dbg_v", [128, NTL, 8, 65], BF16, "ExternalOutput")
    with ExitStack() as st:
        C = Ctx(nc, st, pfx)
        S = C.S
        make_consts(C)
        identb = C.sb("identb", [128, 128], BF16)
        r_identb = C.R("identb")
        S.op("dve", lambda e: e.tensor_copy(out=identb[:], in_=C.ident[:]), r=[C.r_ident], w=[r_identb])
        tri = C.sb("tri", [128, 128], F32)
        r_tri = C.R("tri")
        S.op("pool", lambda e: e.memset(tri[:], 1.0), w=[r_tri])
        S.op("pool", lambda e: e.affine_select(out=tri[:], in_=tri[:], pattern=[[1, 128]], compare_op=ALU.is_ge, fill=0.0, base=0, channel_multiplier=-1), r=[r_tri], w=[r_tri])
        selA = C.sb("selA", [16, 8, 128], F32)
        selB = C.sb("selB", [16, 8, 128], F32)
        sel = C.sb("sel", [16, 8, 128], BF16)
        r_sel = C.R("sel")
        S.op("pool", lambda e: e.memset(selA[:], 1.0), w=[r_sel])
        S.op("pool", lambda e: e.memset(selB[:], 1.0), w=[r_sel])
        S.op("pool", lambda e: e.affine_select(out=selA[:], in_=selA[:], pattern=[[-1, 8], [0, 128]], compare_op=ALU.is_equal, fill=0.0, base=0, channel_multiplier=1), r=[r_sel], w=[r_sel])
        S.op("pool", lambda e: e.affine_select(out=selB[:], in_=selB[:], pattern=[[-1, 8], [0, 128]], compare_op=ALU.is_equal, fill=0.0, base=-8, channel_multiplier=1), r=[r_sel], w=[r_sel])
        S.op("dve", lambda e: e.tensor_tensor(out=sel[:], in0=selA[:], in1=selB[:], op=ALU.add), r=[r_sel], w=[r_sel])
        maskf = C.sb("maskf", [128, 4, 512], F32)
        maskb = C.sb("maskb", [128, 4, 512], BF16)
        r_mask = C.R("mask")
        S.op("pool", lambda e: e.memset(maskf[:], 0.0), w=[r_mask])
        for j in range(4):
            S.op("pool", lambda e, j=j: e.affine_select(out=maskf[:, j, :], in_=maskf[:, j, :], pattern=[[1, 512]], compare_op=ALU.is_ge, fill=NEG, base=-128 * j, channel_multiplier=-1), r=[r_mask], w=[r_mask])
        S.op("dve", lambda e: e.tensor_copy(out=maskb[:], in_=maskf[:]), r=[r_mask], w=[r_mask])
        PB = [(C.ps("pb%d" % i, [128, 512]), C.R("pb%d" % i, True)) for i in range(7)]
        pbb = C.ps("pbb", [128, 1024], BF16)
        r_pbb = C.R("pbb", True)
        wA = [(C.sb("wA%d" % i, [128, 8, 256], F32), C.R("wA%d" % i)) for i in range(2)]
        mods, r_mods = compute_mods(C, cT_d, adaw_d, adab_d, 2, wA, PB[6][0], PB[6][1])
        cols = C.sb("cols", [128, 8], F32)
        r_cols = C.R("cols")
        S.op("dve", lambda e: e.tensor_scalar(out=cols[:], in0=mods[:, 8:16], scalar1=1.0, scalar2=None, op0=ALU.add), r=[r_mods], w=[r_cols])
        wts = {}
        r_w = C.R("wts")
        for nm, d_ in (("wq", wq_d), ("wk", wk_d), ("wv", wv_d), ("wg", wg_d)):
            wts[nm] = C.sb(nm, [128, 8, 512], BF16)
            S.dma("pool", wts[nm][:], d_.rearrange("(kc p) n -> p kc n", p=128), w=[r_w], key=nm)
        wf = C.sb("wf", [128, 8, 8], BF16)
        S.dma("pool", wf[:], wf_d.rearrange("(kc p) n -> p kc n", p=128), w=[r_w], key="wf")
        bfb = C.sb("bfb", [128, 8], F32)
        qgb = C.sb("qgb", [128, 512], F32)
        kgb = C.sb("kgb", [128, 512], F32)
        r_par = C.R("par")
        S.dma("sp", bfb[:], bf_d, w=[r_par], key="bfb")
        S.dma("sp", qgb[:], qg_d, w=[r_par], key="qgb")
        S.dma("sp", kgb[:], kg_d, w=[r_par], key="kgb")
        xt = C.sb("xt", [128, 4, D], F32)
        r_xt = C.R("xt")
        hT = C.sb("hT", [128, 8, 512], BF16)
        r_hT = [C.R("hT%d" % k) for k in range(8)]
        kT = C.sb("kT", [128, 4, TT], BF16)
        r_kT = [C.R("kT%d" % g) for g in range(NGR)]
        qT = C.sb("qT", [128, 4, 512], BF16)
        r_qT = C.R("qT")
        Va = C.sb("Va", [128, NTL, NH, 65], BF16)
        r_V = [C.R("V%d" % g) for g in range(NGR)]
        r_Vones = C.R("Vones")
        S.op("pool", lambda e: e.memset(Va[:, :, :, 64:65], 1.0), w=[r_Vones])
        eg = C.sb("eg", [64, NH, 512], F32)
        r_eg = [C.R("eg%d" % h) for h in range(NH)]
        Fcol = C.sb("Fcol", [128, NTL, NH], F32)
        r_F = C.R("Fcol")
        carry = C.sb("carry", [128, NTL + 1, NH], F32)
        r_carry = C.R("carry")
        S.op("dve", lambda e: e.memset(carry[:, 0, :], 0.0), w=[r_carry])
        biasG = C.sb("biasG", [128, NTL, NH], F32)
        r_bias = C.R("biasG")
        FqT = C.sb("FqT", [16, 512], BF16)
        r_FqT = C.R("FqT")
        sq = C.sb("sq", [128, 512], F32); r_sq = C.R("sq")
        tmp = C.sb("tmp", [128, 512], F32); r_tmp = C.R("tmp")
        qtok = C.sb("qtok", [128, 512], BF16); r_qtok = C.R("qtok")
        ktok = C.sb("ktok", [128, 512], BF16); r_ktok = C.R("ktok")
        ssq = C.sb("ssq", [128, 16], F32); r_ssq = C.R("ssq")
        zf = C.sb("zf", [128, 8], F32); r_zf = C.R("zf")
        fq = C.sb("fq", [128, 8], F32); r_fq = C.R("fq")
        fq16 = C.sb("fq16", [128, 16], BF16); r_fq16 = C.R("fq16")
        PT = [(C.sb("PT%d" % i, [128, 512], BF16), C.R("PT%d" % i)) for i in range(4)]
        oT = C.sb("oT", [64, 512], F32); r_oT = C.R("oT")
        dn = C.sb("dn", [128, 512], F32); r_dn = C.R("dn")
        wv_ = C.sb("wv_", [64, 512], F32); r_wv_ = C.R("wv_")
        yg = [(C.sb("yg%d" % i, [64, 512], BF16), C.R("yg%d" % i)) for i in range(2)]
        tails = []
        xsrc = x_d.rearrange("(i p) d -> p i d", p=128)
        ps_i = 0
        for G in range(NGR):
            if G == 0:
                S.dma("sp", xt[:], xsrc[:, 0:4, :], w=[r_xt], key="xt")
            for kc in range(8):
                pb, r_pb = PB[kc % 2]
                for i in range(4):
                    S.op("pe", lambda e, pb=pb, i=i, kc=kc: e.transpose(pb[:, i * 128:(i + 1) * 128], xt[:, i, kc * 128:(kc + 1) * 128], C.ident[:]), r=[r_xt, C.r_ident], w=[r_pb])
                if kc % 2 == 0:
                    S.op("act", lambda e, pb=pb, kc=kc: e.activation(out=hT[:, kc, :], in_=pb[:], func=AF.Identity, scale=cols[:, kc:kc + 1], bias=mods[:, kc:kc + 1]), r=[r_pb, r_cols, r_mods], w=[r_hT[kc]])
                else:
                    S.op("dve", lambda e, pb=pb, kc=kc: e.tensor_scalar(out=hT[:, kc, :], in0=pb[:], scalar1=cols[:, kc:kc + 1], scalar2=mods[:, kc:kc + 1], op0=ALU.mult, op1=ALU.add), r=[r_pb, r_cols, r_mods], w=[r_hT[kc]])
            if debug and G == 0:
                tails.append(S.dma("sp", dbg_h.rearrange("(kc p) t -> p kc t", p=128), hT[:], r=r_hT, key="dbg0"))
            if G + 1 < NGR:
                S.dma("sp", xt[:], xsrc[:, 4 * (G + 1):4 * (G + 1) + 4, :], w=[r_xt], key="xt")
            for h in range(NH):
                pb, r_pb = PB[2 + h % 2]
                for kc in range(8):
                    S.op("pe", lambda e, pb=pb, h=h, kc=kc: e.matmul(pb[0:64, :], wts["wg"][:, kc, h * 64:(h + 1) * 64], hT[:, kc, :], start=(kc == 0), stop=(kc == 7)), r=[r_w, r_hT[kc]], w=[r_pb])
                S.op("act", lambda e, pb=pb, h=h: e.activation(out=eg[:, h, :], in_=pb[0:64, :], func=AF.Exp, scale=-1.0), r=[r_pb], w=[r_eg[h]])
            for i in range(4):
                tl_ = 4 * G + i
                (pq, r_pq), (pk, r_pk), (pv, r_pv), (pf, r_pf) = PB[2], PB[3], PB[4], PB[5]
                for kc in range(8):
                    lw = hT[:, kc, i * 128:(i + 1) * 128]
                    S.op("pe", lambda e, lw=lw, kc=kc: e.matmul(pq[:], lw, wts["wq"][:, kc, :], start=(kc == 0), stop=(kc == 7)), r=[r_w, r_hT[kc]], w=[r_pq])
                    S.op("pe", lambda e, lw=lw, kc=kc: e.matmul(pk[:], lw, wts["wk"][:, kc, :], start=(kc == 0), stop=(kc == 7)), r=[r_w, r_hT[kc]], w=[r_pk])
                    S.op("pe", lambda e, lw=lw, kc=kc: e.matmul(pv[:], lw, wts["wv"][:, kc, :], start=(kc == 0), stop=(kc == 7)), r=[r_w, r_hT[kc]], w=[r_pv])
                    S.op("pe", lambda e, lw=lw, kc=kc: e.matmul(pf[:, 0:8], lw, wf[:, kc, :], start=(kc == 0), stop=(kc == 7)), r=[r_w, r_hT[kc]], w=[r_pf])
                for which, (pp, r_pp), gb, tok, r_tok, eps_, mul_ in (("q", (pq, r_pq), qgb, qtok, r_qtok, 64.0 * QK_EPS, 1.0), ("k", (pk, r_pk), kgb, ktok, r_ktok, QK_EPS, 1.0 / 64.0)):
                    o8 = 0 if which == "q" else 8
                    S.op("act", lambda e, pp=pp: e.activation(out=sq[:], in_=pp[:], func=AF.Square), r=[r_pp], w=[r_sq])
                    S.op("dve", lambda e, o8=o8: e.tensor_reduce(out=ssq[:, o8:o8 + 8], in_=sq[:].rearrange("p (h d) -> p h d", d=64), axis=AX.X, op=ALU.add), r=[r_sq], w=[r_ssq])
                    S.op("dve", lambda e, o8=o8, eps_=eps_, mul_=mul_: e.tensor_scalar(out=ssq[:, o8:o8 + 8], in0=ssq[:, o8:o8 + 8], scalar1=mul_, scalar2=eps_, op0=ALU.mult, op1=ALU.add), r=[r_ssq], w=[r_ssq])
                    S.op("act", lambda e, o8=o8: e.activation(out=ssq[:, o8:o8 + 8], in_=ssq[:, o8:o8 + 8], func=AF.Ln), r=[r_ssq], w=[r_ssq])
                    S.op("act", lambda e, o8=o8: e.activation(out=ssq[:, o8:o8 + 8], in_=ssq[:, o8:o8 + 8], func=AF.Exp, scale=-0.5), r=[r_ssq], w=[r_ssq])
                    S.op("dve", lambda e, pp=pp, o8=o8: e.tensor_tensor(out=tmp[:].rearrange("p (h d) -> p h d", d=64), in0=pp[:].rearrange("p (h d) -> p h d", d=64), in1=ssq[:, o8:o8 + 8].unsqueeze(2).to_broadcast([128, 8, 64]), op=ALU.mult), r=[r_pp, r_ssq], w=[r_tmp])
                    S.op("dve", lambda e, gb=gb, tok=tok: e.tensor_tensor(out=tok[:], in0=tmp[:], in1=gb[:], op=ALU.mult), r=[r_tmp, r_par], w=[r_tok])
                S.op("act", lambda e, tl_=tl_: e.copy(out=Va[:, tl_, :, 0:64], in_=pv[:].rearrange("p (h d) -> p h d", d=64)), r=[r_pv, r_Vones], w=[r_V[G]])
                S.op("dve", lambda e: e.tensor_tensor(out=zf[:], in0=pf[:, 0:8], in1=bfb[:], op=ALU.add), r=[r_pf, r_par], w=[r_zf])
                S.op("act", lambda e: e.activation(out=zf[:], in_=zf[:], func=AF.Exp, scale=-1.0), r=[r_zf], w=[r_zf])
                S.op("act", lambda e: e.activation(out=zf[:], in_=zf[:], func=AF.Ln, bias=1.0), r=[r_zf], w=[r_zf])
                pc, r_pc = PB[6]
                S.op("pe", lambda e: e.matmul(pc[:, 0:8], tri[:], zf[:], start=True, stop=True), r=[r_tri, r_zf], w=[r_pc])
                S.op("pe", lambda e: e.matmul(pc[:, 8:16], C.ones[:], zf[:], start=True, stop=True), r=[C.r_ones, r_zf], w=[r_pc])
                S.op("dve", lambda e, tl_=tl_: e.tensor_tensor(out=Fcol[:, tl_, :], in0=carry[:, tl_, :], in1=pc[:, 0:8], op=ALU.subtract), r=[r_carry, r_pc], w=[r_F])
                S.op("dve", lambda e, tl_=tl_: e.tensor_tensor(out=carry[:, tl_ + 1, :], in0=carry[:, tl_, :], in1=pc[:, 8:16], op=ALU.subtract), r=[r_carry, r_pc], w=[r_carry])
                S.op("dve", lambda e, tl_=tl_, G=G: e.tensor_tensor(out=fq[:], in0=Fcol[:, tl_, :], in1=carry[:, 4 * G, :], op=ALU.subtract), r=[r_F, r_carry], w=[r_fq])
                S.op("dve", lambda e: e.tensor_copy(out=fq16[:, 0:8], in_=fq[:]), r=[r_fq], w=[r_fq16])
                S.op("dve", lambda e: e.tensor_tensor(out=fq[:], in0=fq[:], in1=fq16[:, 0:8], op=ALU.subtract), r=[r_fq, r_fq16], w=[r_fq])
                S.op("dve", lambda e: e.tensor_copy(out=fq16[:, 8:16], in_=fq[:]), r=[r_fq], w=[r_fq16])
                for p in range(4):
                    S.op("pe", lambda e, p=p: e.transpose(pbb[:, p * 128:(p + 1) * 128], qtok[:, p * 128:(p + 1) * 128], identb[:]), r=[r_qtok, r_identb], w=[r_pbb])
                for p in range(4):
                    S.op("pe", lambda e, p=p: e.transpose(pbb[:, 512 + p * 128:512 + (p + 1) * 128], ktok[:, p * 128:(p + 1) * 128], identb[:]), r=[r_ktok, r_identb], w=[r_pbb])
                S.op("act", lambda e, i=i: e.copy(out=qT[:, :, i * 128:(i + 1) * 128], in_=pbb[:, 0:512].rearrange("p (a t) -> p a t", t=128)), r=[r_pbb], w=[r_qT])
                S.op("dve", lambda e, tl_=tl_: e.tensor_copy(out=kT[:, :, tl_ * 128:(tl_ + 1) * 128], in_=pbb[:, 512:1024].rearrange("p (a t) -> p a t", t=128)), r=[r_pbb], w=[r_kT[G]])
                S.op("pe", lambda e: e.transpose(pbb[0:16, 0:128], fq16[:, 0:16], identb[:]), r=[r_fq16, r_identb], w=[r_pbb])
                S.op("dve", lambda e, i=i: e.tensor_copy(out=FqT[:, i * 128:(i + 1) * 128], in_=pbb[0:16, 0:128]), r=[r_pbb], w=[r_FqT])
            nkb = 4 * G + 4
            for h in range(NH):
                S.op("dve", lambda e, h=h, nkb=nkb, G=G: e.tensor_scalar(out=biasG[:, 0:nkb, h], in0=Fcol[:, 0:nkb, h], scalar1=-1.0, scalar2=carry[:, 4 * G, h:h + 1], op0=ALU.mult, op1=ALU.add), r=[r_F, r_carry], w=[r_bias])
            tiles = [(2 * p + e, kb) for p in range(NH // 2) for kb in range(nkb) for e in range(2)]

            def emit_pv(h, kb, PTt, r_PT, nkb=nkb, G=G):
                pO, r_pO = PB[4 + h % 2]
                gk = kb // 4
                S.op("pe", lambda e, pO=pO, PTt=PTt, kb=kb, h=h, nkb=nkb: e.matmul(pO[0:65, :], Va[:, kb, h, :], PTt[:], start=(kb == 0), stop=(kb == nkb - 1)), r=[r_V[gk], r_Vones, r_PT], w=[r_pO])
                if kb == nkb - 1:
                    pB_, r_pB = PB[6]
                    ygt, r_yg = yg[h % 2]
                    S.op("act", lambda e, pO=pO: e.copy(out=oT[:], in_=pO[0:64, :]), r=[r_pO], w=[r_oT])
                    S.op("dve", lambda e, pO=pO: e.tensor_copy(out=dn[64:65, :], in_=pO[64:65, :]), r=[r_pO], w=[r_dn])
                    S.op("pe", lambda e: e.matmul(pB_[0:64, :], C.ones[64:65, 0:64], dn[64:65, :], start=True, stop=True), r=[C.r_ones, r_dn], w=[r_pB])
                    S.op("dve", lambda e, h=h: e.scalar_tensor_tensor(out=wv_[:], in0=eg[:, h, :], scalar=1.0, in1=pB_[0:64, :], op0=ALU.add, op1=ALU.mult), r=[r_eg[h], r_pB], w=[r_wv_])
                    S.op("dve", lambda e: e.reciprocal(out=wv_[:], in_=wv_[:]), r=[r_wv_], w=[r_wv_])
                    S.op("dve", lambda e, ygt=ygt: e.tensor_tensor(out=ygt[:], in0=oT[:], in1=wv_[:], op=ALU.mult), r=[r_oT, r_wv_], w=[r_yg])
                    tails.append(S.dma("sp", yg_d[h * 64:(h + 1) * 64, G * 512:(G + 1) * 512], ygt[:], r=[r_yg], key="yg%d" % (h % 2)))

            pendq = []
            for (h, kb) in tiles:
                p_, e_ = h // 2, h % 2
                pS, r_pS = PB[ps_i % 4]
                PTt, r_PT = PT[ps_i % 4]
                ps_i += 1
                diag = kb >= 4 * G
                gk = kb // 4
                if FOX_DVE:
                    fqb, r_fqb = ((tmp, r_tmp), (sq, r_sq))[h % 2]
                    if kb == 0:
                        pB_, r_pB = PB[6]
                        S.op("pe", lambda e, h=h: e.matmul(pB_[:], sel[:, h, :], FqT[:], start=True, stop=True), r=[r_sel, r_FqT], w=[r_pB])
                        S.op("dve", lambda e, fqb=fqb: e.tensor_copy(out=fqb[:], in_=pB_[:]), r=[r_pB], w=[r_fqb])
                    S.op("pe", lambda e, pS=pS, p_=p_, e_=e_, kb=kb, diag=diag: e.matmul(pS[:], kT[e_ * 64:(e_ + 1) * 64, p_, kb * 128:(kb + 1) * 128], qT[e_ * 64:(e_ + 1) * 64, p_, :], start=True, stop=(not diag)), r=[r_kT[gk], r_qT], w=[r_pS])
                    if diag:
                        S.op("pe", lambda e, pS=pS, kb=kb, G=G: e.matmul(pS[:], identb[:], maskb[:, kb - 4 * G, :], start=False, stop=True), r=[r_identb, r_mask], w=[r_pS])
                    S.op("dve", lambda e, pS=pS, fqb=fqb: e.tensor_tensor(out=pS[:], in0=pS[:], in1=fqb[:], op=ALU.add), r=[r_pS, r_fqb], w=[r_pS])
                else:
                    S.op("pe", lambda e, pS=pS, p_=p_, e_=e_, kb=kb: e.matmul(pS[:], kT[e_ * 64:(e_ + 1) * 64, p_, kb * 128:(kb + 1) * 128], qT[e_ * 64:(e_ + 1) * 64, p_, :], start=True, stop=False), r=[r_kT[gk], r_qT], w=[r_pS])
                    S.op("pe", lambda e, pS=pS, h=h, diag=diag: e.matmul(pS[:], sel[:, h, :], FqT[:], start=False, stop=(not diag)), r=[r_sel, r_FqT], w=[r_pS])
                    if diag:
                        S.op("pe", lambda e, pS=pS, kb=kb, G=G: e.matmul(pS[:], identb[:], maskb[:, kb - 4 * G, :], start=False, stop=True), r=[r_identb, r_mask], w=[r_pS])
                S.op("act", lambda e, pS=pS, PTt=PTt, kb=kb, h=h: e.activation(out=PTt[:], in_=pS[:], func=AF.Exp, bias=biasG[:, kb, h:h + 1], scale=1.0), r=[r_pS, r_bias], w=[r_PT])
                pendq.append((h, kb, PTt, r_PT))
                if h % 2 == 1:
                    while len(pendq) > 2:
                        emit_pv(*pendq.pop(0))
            while pendq:
                emit_pv(*pendq.pop(0))
            if debug and G == 0:
                tails.append(S.dma("sp", dbg_q, qT[:], r=[r_qT], key="dbg1"))
        if debug:
            tails.append(S.dma("sp", dbg_k, kT[:], r=r_kT, key="dbg2"))
            tails.append(S.dma("sp", dbg_F, Fcol[:], r=[r_F], key="dbg3"))
            tails.append(S.dma("sp", dbg_v, Va[:], r=r_V + [r_Vones], key="dbg4"))
        if after is not None:
            after(C, tails)
        S.emit(st, tail_waits=tails)

import math, os
SEQ_VAR = 0

NH = 8
HD = 64
C0 = math.exp(-0.5)
GN_EPS = 64 * 1e-5


class TR:
    def __init__(self, C, name, shape, dt, psum=False):
        self.t = C.ps(name, shape, dt) if psum else C.sb(name, shape, dt)
        self.r = C.R(name, excl=psum)


class _Stop(Exception):
    pass


def build_rwkv(TT=4096, debug=False, stop=None):
    nc = bass.Bass("TRN2", target_bir_lowering=False)
    dt = lambda n, s, d, k="ExternalInput": nc.dram_tensor(n, list(s), d, kind=k).ap()
    xT_d = dt("xT", [D, TT], F32)
    ins = rwkv_decl(dt)
    yT_d = dt("yT", [512, TT], BF16, "ExternalOutput")
    xv = xT_d.rearrange("(kc p) t -> p kc t", p=128)
    rwkv_phase(nc, "", lambda t0, n: xv[:, :, t0:t0 + n], ins, yT_d, TT, debug=debug, stop=stop)
    return nc


RW_BC = ["w0b", "a0b", "kkb", "kab", "rkb", "lngb", "lnbb"]


def rwkv_decl(dt, sfx=""):
    d_ = dict(cT=dt("cT" + sfx, [128, 8], F32), adaw=dt("adaw" + sfx, [D, 2 * D], F32), adab=dt("adab" + sfx, [128, 16], F32), mu=dt("mu" + sfx, [128, 48], F32),
              wr=dt("wr" + sfx, [D, 512], F32), wk=dt("wk" + sfx, [D, 512], F32), wv=dt("wv" + sfx, [D, 512], F32),
              w1=dt("w1" + sfx, [D, 64], F32), a1=dt("a1" + sfx, [D, 64], F32), g1=dt("g1" + sfx, [D, 128], F32),
              w2=dt("w2" + sfx, [64, 512], F32), a2=dt("a2" + sfx, [64, 512], F32), g2=dt("g2" + sfx, [128, 512], F32))
    for n in RW_BC:
        d_[n] = dt(n + sfx, [128, 512], F32)
    return d_


def rwkv_phase(nc, pfx, xsrc_fn, ins, yT_d, TT=4096, debug=False, stop=None, after=None):
    dt = lambda n, s, d, k="ExternalInput": nc.dram_tensor(n, list(s), d, kind=k).ap()
    cT_d, adaw_d, adab_d, mu_d = ins["cT"], ins["adaw"], ins["adab"], ins["mu"]
    wr_d, wk_d, wv_d, w1_d, a1_d, g1_d, w2_d, a2_d, g2_d = [ins[k] for k in ("wr", "wk", "wv", "w1", "a1", "g1", "w2", "a2", "g2")]
    bc_names = RW_BC
    bc_d = {n: ins[n] for n in bc_names}
    GS = 256
    NTG = GS // 128
    NGR = TT // GS
    dbg = {}
    if debug:
        for n in ["r", "kp", "kkn", "a", "sw", "Y", "bonus", "Gm", "yn"]:
            dbg[n] = dt("dbg_" + n, [128, 512], F32, "ExternalOutput")
        dbg["S"] = dt("dbg_S", [128, 4, 64], F32, "ExternalOutput")
        for n, shp in (("Arb", [128, 8, 128]), ("Aak", [128, 8, 128]), ("NT0", [128, 8, 128]), ("Z0", [128, 8, 128]), ("Zf", [128, 8, 128]), ("Y0", [128, 512]), ("RbT", [128, 4, 128]), ("Qm", [128, 4, 128]), ("Hm", [128, 4, 128]), ("gcol", [128, 8]), ("v", [128, 512]), ("ART", [128, 4, 2, 128]), ("KT", [128, 4, 128]), ("BT", [128, 4, 128]), ("Bh", [128, 512]), ("Kh", [128, 512])):
            dbg[n] = dt("dbg_" + n, shp, F32, "ExternalOutput")
    with ExitStack() as st:
        C = Ctx(nc, st, pfx)
        S = C.S
        make_consts(C)
        ident, ones = C.ident, C.ones
        r_ident, r_ones = C.r_ident, C.r_ones
        tails = []
        identb = TR(C, "identb", [128, 128], BF16)
        S.op("dve", lambda e: e.tensor_copy(out=identb.t[:], in_=ident[:]), r=[r_ident], w=[identb.r])
        U = TR(C, "U", [128, 128], F32)
        IU = TR(C, "IUf", [128, 128], F32)
        S.op("pool", lambda e: e.memset(U.t[:], 1.0), w=[U.r])
        S.op("pool", lambda e: e.affine_select(out=U.t[:], in_=U.t[:], pattern=[[1, 128]], compare_op=ALU.is_gt, fill=0.0, base=0, channel_multiplier=-1), r=[U.r], w=[U.r])
        S.op("pool", lambda e: e.memset(IU.t[:], 1.0), w=[IU.r])
        S.op("pool", lambda e: e.affine_select(out=IU.t[:], in_=IU.t[:], pattern=[[1, 128]], compare_op=ALU.is_ge, fill=0.0, base=0, channel_multiplier=-1), r=[IU.r], w=[IU.r])
        SLf = TR(C, "SLf", [128, 128], F32)
        S.op("pool", lambda e: e.memset(SLf.t[:], 1.0), w=[SLf.r])
        S.op("pool", lambda e: e.affine_select(out=SLf.t[:], in_=SLf.t[:], pattern=[[-1, 128]], compare_op=ALU.is_gt, fill=0.0, base=0, channel_multiplier=1), r=[SLf.r], w=[SLf.r])
        mask1 = TR(C, "mask1", [128, 128], F32)
        maskL = TR(C, "maskL", [128, 64], F32)
        idl = TR(C, "idl", [128, 64], F32)
        for ch in range(2):
            ps_ = slice(ch * 64, ch * 64 + 64)
            S.op("dve", lambda e, ps_=ps_: e.tensor_copy(out=mask1.t[ps_, 0:64], in_=U.t[ps_, ps_]), r=[U.r], w=[mask1.r])
            S.op("dve", lambda e, ps_=ps_: e.tensor_copy(out=mask1.t[ps_, 64:128], in_=IU.t[ps_, ps_]), r=[IU.r], w=[mask1.r])
            S.op("dve", lambda e, ps_=ps_: e.tensor_copy(out=maskL.t[ps_, :], in_=SLf.t[ps_, ps_]), r=[SLf.r], w=[maskL.r])
            S.op("dve", lambda e, ps_=ps_: e.tensor_copy(out=idl.t[ps_, :], in_=ident[ps_, ps_]), r=[r_ident], w=[idl.r])
        triC = TR(C, "triC", [128, 128], F32)
        blkC = TR(C, "blkC", [128, 128], F32)
        S.op("dve", lambda e: e.tensor_scalar(out=triC.t[:], in0=IU.t[:], scalar1=-C0, scalar2=None, op0=ALU.mult), r=[IU.r], w=[triC.r])
        S.op("dve", lambda e: e.memset(triC.t[0:64, 64:128], 0.0), w=[triC.r])
        S.op("dve", lambda e: e.memset(blkC.t[:], -C0), w=[blkC.r])
        S.op("dve", lambda e: e.memset(blkC.t[0:64, 64:128], 0.0), w=[blkC.r])
        S.op("dve", lambda e: e.memset(blkC.t[64:128, 0:64], 0.0), w=[blkC.r])
        avg = TR(C, "avg", [128, 1], F32)
        S.op("dve", lambda e: e.memset(avg.t[:], 1.0 / 64.0), w=[avg.r])
        PB = [TR(C, "pb%d" % i, [128, 512], F32, psum=True) for i in range(7)]
        pbb = TR(C, "pbb", [128, 1024], BF16, psum=True)
        mu = TR(C, "mu", [128, 48], F32)
        S.dma("sp", mu.t[:], mu_d, w=[mu.r], key="mu")
        W = {}
        r_w = C.R("wts")
        for nm, d_, shp in (("wr", wr_d, [128, 8, 512]), ("wk", wk_d, [128, 8, 512]), ("wv", wv_d, [128, 8, 512]), ("w1", w1_d, [128, 8, 64]), ("a1", a1_d, [128, 8, 64]), ("g1", g1_d, [128, 8, 128])):
            W[nm] = C.sb(nm, shp, BF16)
            S.dma("pool", W[nm][:], d_.rearrange("(kc p) n -> p kc n", p=128), w=[r_w], key=nm)
        for nm, d_, shp in (("w2", w2_d, [64, 512]), ("a2", a2_d, [64, 512]), ("g2", g2_d, [128, 512])):
            W[nm] = C.sb(nm, shp, BF16)
            S.dma("pool", W[nm][:], d_, w=[r_w], key=nm)
        BC = {}
        r_bc = C.R("bc")
        for n in bc_names:
            BC[n] = C.sb(n, [128, 512], F32)
            S.dma("sp", BC[n][:], bc_d[n], w=[r_bc], key=n)
        xg = TR(C, "xg", [128, 8, GS + 1], F32)
        S.op("dve", lambda e: e.memset(xg.t[:, :, 0:1], 0.0), w=[xg.r])
        Nm = [TR(C, "Nm%d" % i, [128, 8, 128], BF16) for i in range(2)]
        NTm = [TR(C, "NTm%d" % i, [128, 8, 128], BF16) for i in range(2)]
        for t_ in Nm + NTm:
            S.op("pool", lambda e, t_=t_: e.memset(t_.t[:], 0.0), w=[t_.r])
        ST = [TR(C, "ST%d" % i, [128, 4, 64], F32) for i in range(2)]
        S.op("dve", lambda e: e.memset(ST[0].t[:], 0.0), w=[ST[0].r])
        mods, r_mods = compute_mods(C, cT_d, adaw_d, adab_d, 2, None, PB[6].t, PB[6].r)
        cols = TR(C, "cols", [128, 8], F32)
        S.op("dve", lambda e: e.tensor_scalar(out=cols.t[:], in0=mods[:, 8:16], scalar1=1.0, scalar2=None, op0=ALU.add), r=[r_mods], w=[cols.r])
        dx = TR(C, "dx", [128, 8, GS], F32)
        xs = [TR(C, "xs%d" % i, [128, 8, GS], BF16) for i in range(2)]
        l1 = [TR(C, "l1_%d" % i, [128, GS], BF16) for i in range(3)]
        l1f = TR(C, "l1f", [128, GS], F32)
        G_ = {n: TR(C, "G_" + n, [128, NTG, 512], F32) for n in ("r", "k", "v")}
        names = ["sw", "Gm", "Gi", "Gp", "GC", "a", "kkn", "kp", "t1", "t2", "Bt", "Kt", "Bh", "Kh", "bonus", "Y", "yn", "gg", "At", "Rt", "glg", "glg2", "bonus2", "t3", "At2", "Rt2", "Kt2", "Bt2", "Bh2", "Kh2", "GC2", "V0", "V2"]
        X = {n: TR(C, "X_" + n, [128, 512], F32) for n in names if n not in ("Bh", "Kh", "Bh2", "Kh2")}
        for n in ("Bh", "Kh", "Bh2", "Kh2", "Vb0", "Vb2"):
            X[n] = TR(C, "X_" + n, [128, 512], BF16)
        Zfb = TR(C, "Zfb", [128, 8, 128], BF16)
        Arbb = TR(C, "Arbb", [128, 8, 64], BF16)
        sm = TR(C, "sm", [128, 64], F32)
        smT = TR(C, "smT", [128, 64], F32)
        Z = [TR(C, "Z%d" % i, [128, 8, 128], BF16) for i in range(2)]
        Zf32 = TR(C, "Zf32", [128, 8, 128], F32)
        ART = TR(C, "ART", [128, 4, 2, 128], F32)
        KT = TR(C, "KT", [128, 4, 128], F32)
        BT = TR(C, "BT", [128, 4, 128], F32)
        Arb = TR(C, "Arb", [128, 8, 128], BF16)
        Aak = TR(C, "Aak", [128, 8, 128], BF16)
        RbT = TR(C, "RbT", [128, 4, 128], F32)
        Qm = TR(C, "Qm", [128, 4, 128], F32)
        Hm = TR(C, "Hm", [128, 4, 128], F32)
        gcol = TR(C, "gcol", [128, 8], F32)
        Y0 = TR(C, "Y0", [128, 512], F32)
        ybf = TR(C, "ybf", [128, 512], BF16)
        yTo = [TR(C, "yTo%d" % i, [128, 4, 128], BF16) for i in range(2)]
        ysrc = yT_d.rearrange("(a p) t -> p a t", p=128)

        def hv(t):
            return t.rearrange("p (h d) -> p h d", d=64)

        def bc8(ap):
            return ap.unsqueeze(2).to_broadcast([128, 8, 64])

        sti = [0]

        def tile_front_a(i, G):
            tl_ = G * NTG + i
            r_ = G_["r"].t[:, i, :]; k_ = G_["k"].t[:, i, :]; v_ = G_["v"].t[:, i, :]
            rG = [G_[n].r for n in G_]
            sw, Gm, Gi, Gp, a_, kkn, kp, t1, t2 = [X[n] for n in ("sw", "Gm", "Gi", "Gp", "a", "kkn", "kp", "t1", "t2")]
            Yt, yn = X["Y"], X["yn"]
            sfx = "" if tl_ % 2 == 0 else "2"
            Bt, Kt, Bh, Kh, GC, At_, Rt = [X[n + sfx] if (n + sfx) in X else X[n] for n in ("Bt", "Kt", "Bh", "Kh", "GC", "At", "Rt")]
            Vt = X["V0"] if tl_ % 2 == 0 else X["V2"]
            Vb = X["Vb0"] if tl_ % 2 == 0 else X["Vb2"]
            bonus = X["bonus"] if tl_ % 2 == 0 else X["bonus2"]
            glg = X["glg"] if tl_ % 2 == 0 else X["glg2"]
            t3 = X["t3"]
            pzw, pza, pgg = PB[4], PB[5], PB[6]
            for (pz_, lt, w2n, rows) in ((pzw, l1[0], "w2", 64), (pza, l1[1], "a2", 64), (pgg, l1[2], "g2", 128)):
                S.op("pe", lambda e, pz_=pz_, lt=lt, w2n=w2n, rows=rows, i=i: e.matmul(pz_.t[:], lt.t[0:rows, i * 128:(i + 1) * 128], W[w2n][0:rows, :], start=True, stop=True), r=[lt.r, r_w], w=[pz_.r])
            S.op("act", lambda e: e.copy(out=X["gg"].t[:], in_=pgg.t[:]), r=[pgg.r], w=[X["gg"].r])
            S.op("dve", lambda e: e.tensor_tensor(out=t1.t[:], in0=pzw.t[:], in1=BC["w0b"][:], op=ALU.add), r=[pzw.r, r_bc], w=[t1.r])
            S.op("dve", lambda e: e.tensor_tensor(out=t2.t[:], in0=pza.t[:], in1=BC["a0b"][:], op=ALU.add), r=[pza.r, r_bc, t2.r], w=[t2.r])
            S.op("act", lambda e: e.activation(out=sw.t[:], in_=t1.t[:], func=AF.Sigmoid), r=[t1.r], w=[sw.r])
            S.op("act", lambda e: e.activation(out=a_.t[:], in_=t2.t[:], func=AF.Sigmoid), r=[t2.r], w=[a_.r])

        def tile_front_b(i, G):
            tl_ = G * NTG + i
            r_ = G_["r"].t[:, i, :]; k_ = G_["k"].t[:, i, :]; v_ = G_["v"].t[:, i, :]
            rG = [G_[n].r for n in G_]
            sw, Gm, Gi, Gp, a_, kkn, kp, t1, t2 = [X[n] for n in ("sw", "Gm", "Gi", "Gp", "a", "kkn", "kp", "t1", "t2")]
            Yt, yn = X["Y"], X["yn"]
            sfx = "" if tl_ % 2 == 0 else "2"
            Bt, Kt, Bh, Kh, GC, At_, Rt = [X[n + sfx] if (n + sfx) in X else X[n] for n in ("Bt", "Kt", "Bh", "Kh", "GC", "At", "Rt")]
            Vt = X["V0"] if tl_ % 2 == 0 else X["V2"]
            Vb = X["Vb0"] if tl_ % 2 == 0 else X["Vb2"]
            bonus = X["bonus"] if tl_ % 2 == 0 else X["bonus2"]
            glg = X["glg"] if tl_ % 2 == 0 else X["glg2"]
            t3 = X["t3"]
            pL, pLC = PB[2], PB[3]
            S.op("pe", lambda e: e.matmul(pL.t[:], triC.t[:], sw.t[:], start=True, stop=True), r=[triC.r, sw.r], w=[pL.r])
            S.op("pe", lambda e: e.matmul(pLC.t[:], blkC.t[:], sw.t[:], start=True, stop=True), r=[blkC.r, sw.r], w=[pLC.r])
            S.op("act", lambda e: e.activation(out=Gm.t[:], in_=pL.t[:], func=AF.Exp), r=[pL.r], w=[Gm.r])
            S.op("act", lambda e: e.activation(out=Gi.t[:], in_=pL.t[:], func=AF.Exp, scale=-1.0), r=[pL.r], w=[Gi.r])
            S.op("dve", lambda e: e.scalar_tensor_tensor(out=Gp.t[:], in0=sw.t[:], scalar=C0, in1=pL.t[:], op0=ALU.mult, op1=ALU.add), r=[sw.r, pL.r], w=[Gp.r])
            S.op("act", lambda e: e.activation(out=Gp.t[:], in_=Gp.t[:], func=AF.Exp), r=[Gp.r], w=[Gp.r])
            S.op("act", lambda e: e.activation(out=GC.t[:], in_=pLC.t[:], func=AF.Exp), r=[pLC.r], w=[GC.r])
            S.op("dve", lambda e, k_=k_: e.tensor_tensor(out=kkn.t[:], in0=k_, in1=BC["kkb"][:], op=ALU.mult), r=rG + [r_bc], w=[kkn.r])
            S.op("act", lambda e: e.activation(out=t2.t[:], in_=kkn.t[:], func=AF.Square), r=[kkn.r], w=[t2.r])
            S.op("dve", lambda e: e.tensor_reduce(out=sm.t[:, 0:8], in_=hv(t2.t[:]), axis=AX.X, op=ALU.add), r=[t2.r], w=[sm.r])
            S.op("dve", lambda e: e.tensor_scalar(out=sm.t[:, 0:8], in0=sm.t[:, 0:8], scalar1=1e-24, scalar2=None, op0=ALU.max), r=[sm.r], w=[sm.r])
            S.op("act", lambda e: e.activation(out=sm.t[:, 0:8], in_=sm.t[:, 0:8], func=AF.Sqrt), r=[sm.r], w=[sm.r])
            S.op("dve", lambda e: e.reciprocal(out=sm.t[:, 0:8], in_=sm.t[:, 0:8]), r=[sm.r], w=[sm.r])
            S.op("dve", lambda e: e.tensor_tensor(out=hv(kkn.t[:]), in0=hv(kkn.t[:]), in1=bc8(sm.t[:, 0:8]), op=ALU.mult), r=[kkn.r, sm.r], w=[kkn.r])
            S.op("dve", lambda e: e.scalar_tensor_tensor(out=t2.t[:], in0=a_.t[:], scalar=-1.0, in1=BC["kab"][:], op0=ALU.add, op1=ALU.mult), r=[a_.r, r_bc, t2.r], w=[t2.r])
            S.op("dve", lambda e, k_=k_: e.scalar_tensor_tensor(out=kp.t[:], in0=t2.t[:], scalar=1.0, in1=k_, op0=ALU.add, op1=ALU.mult), r=[t2.r] + rG, w=[kp.r])
            S.op("dve", lambda e: e.scalar_tensor_tensor(out=At_.t[:], in0=kkn.t[:], scalar=-1.0, in1=Gp.t[:], op0=ALU.mult, op1=ALU.mult), r=[kkn.r, Gp.r], w=[At_.r])
            S.op("pool", lambda e: e.tensor_tensor(out=Bt.t[:], in0=kkn.t[:], in1=a_.t[:], op=ALU.mult), r=[kkn.r, a_.r], w=[Bt.r])
            S.op("pool", lambda e: e.tensor_tensor(out=Bt.t[:], in0=Bt.t[:], in1=Gi.t[:], op=ALU.mult), r=[Bt.r, Gi.r], w=[Bt.r])
            S.op("dve", lambda e: e.tensor_tensor(out=Kt.t[:], in0=kp.t[:], in1=Gi.t[:], op=ALU.mult), r=[kp.r, Gi.r], w=[Kt.r])
            S.op("dve", lambda e, r_=r_: e.tensor_tensor(out=Rt.t[:], in0=r_, in1=Gm.t[:], op=ALU.mult), r=rG + [Gm.r], w=[Rt.r])
            S.op("pool", lambda e: e.tensor_tensor(out=Bh.t[:], in0=Bt.t[:], in1=GC.t[:], op=ALU.mult), r=[Bt.r, GC.r], w=[Bh.r])
            S.op("pool", lambda e: e.tensor_tensor(out=Kh.t[:], in0=Kt.t[:], in1=GC.t[:], op=ALU.mult), r=[Kt.r, GC.r], w=[Kh.r])
            S.op("dve", lambda e, r_=r_: e.tensor_tensor(out=t2.t[:], in0=r_, in1=kp.t[:], op=ALU.mult), r=rG + [kp.r, t2.r], w=[t2.r])
            S.op("dve", lambda e: e.tensor_tensor(out=t2.t[:], in0=t2.t[:], in1=BC["rkb"][:], op=ALU.mult), r=[t2.r, r_bc], w=[t2.r])
            S.op("dve", lambda e: e.tensor_reduce(out=sm.t[:, 8:16], in_=hv(t2.t[:]), axis=AX.X, op=ALU.add), r=[t2.r], w=[sm.r])
            S.op("dve", lambda e, v_=v_, bonus=bonus: e.tensor_tensor(out=hv(bonus.t[:]), in0=hv(v_), in1=bc8(sm.t[:, 8:16]), op=ALU.mult), r=rG + [sm.r], w=[bonus.r])
            S.op("pool", lambda e, bonus=bonus: e.tensor_tensor(out=bonus.t[:], in0=bonus.t[:], in1=BC["lnbb"][:], op=ALU.add), r=[bonus.r, r_bc], w=[bonus.r])
            S.op("pool", lambda e, bonus=bonus: e.tensor_tensor(out=bonus.t[:], in0=bonus.t[:], in1=X["gg"].t[:], op=ALU.mult), r=[bonus.r, X["gg"].r], w=[bonus.r])
            S.op("pool", lambda e, glg=glg: e.tensor_tensor(out=glg.t[:], in0=X["gg"].t[:], in1=BC["lngb"][:], op=ALU.mult), r=[X["gg"].r, r_bc], w=[glg.r])
            if stop == 2:
                raise _Stop()
            S.op("pool", lambda e: e.tensor_copy(out=Vb.t[:], in_=v_), r=rG, w=[Vb.r])

        def tile_back(i, G, hook=None):
            tl_ = G * NTG + i
            r_ = G_["r"].t[:, i, :]; k_ = G_["k"].t[:, i, :]; v_ = G_["v"].t[:, i, :]
            rG = [G_[n].r for n in G_]
            sw, Gm, Gi, Gp, a_, kkn, kp, t1, t2 = [X[n] for n in ("sw", "Gm", "Gi", "Gp", "a", "kkn", "kp", "t1", "t2")]
            Yt, yn = X["Y"], X["yn"]
            sfx = "" if tl_ % 2 == 0 else "2"
            Bt, Kt, Bh, Kh, GC, At_, Rt = [X[n + sfx] if (n + sfx) in X else X[n] for n in ("Bt", "Kt", "Bh", "Kh", "GC", "At", "Rt")]
            Vt = X["V0"] if tl_ % 2 == 0 else X["V2"]
            Vb = X["Vb0"] if tl_ % 2 == 0 else X["Vb2"]
            bonus = X["bonus"] if tl_ % 2 == 0 else X["bonus2"]
            glg = X["glg"] if tl_ % 2 == 0 else X["glg2"]
            t3 = X["t3"]
            v_ = Vb.t[:]
            rG = [Vb.r]
            S.op("pool", lambda e: e.tensor_copy(out=Z[0].t[:, :, 0:64], in_=hv(At_.t[:])), r=[At_.r], w=[Z[0].r])
            for p in range(4):
                pT = PB[p % 2]
                for q_, src in enumerate((At_, Rt, Kt, Bt)):
                    S.op("pe", lambda e, pT=pT, p=p, q_=q_, src=src: e.transpose(pT.t[:, q_ * 128:(q_ + 1) * 128], src.t[:, p * 128:(p + 1) * 128], ident[:]), r=[src.r, r_ident], w=[pT.r])
                S.op("dve", lambda e, pT=pT, p=p: e.tensor_copy(out=ART.t[:, p, :, 0:64], in_=pT.t[:, 0:128].rearrange("p (c t) -> p c t", t=64)), r=[pT.r], w=[ART.r])
                S.op("act", lambda e, pT=pT, p=p: e.copy(out=ART.t[:, p, :, 64:128], in_=pT.t[:, 128:256].rearrange("p (c t) -> p c t", t=64)), r=[pT.r], w=[ART.r])
                S.op("act", lambda e, pT=pT, p=p: e.copy(out=KT.t[:, p, :], in_=pT.t[:, 256:384]), r=[pT.r], w=[KT.r])
                S.op("dve", lambda e, pT=pT, p=p: e.tensor_copy(out=BT.t[:, p, :], in_=pT.t[:, 384:512]), r=[pT.r], w=[BT.r])
            if stop == 3:
                raise _Stop()
            pS1a, pS1b, pS2a, pS2b, pS3 = PB[0], PB[1], PB[2], PB[3], PB[4]
            for h in range(8):
                p, e_ = h // 2, h % 2
                fe = slice(e_ * 64, e_ * 64 + 64)
                pa, pb_ = (pS1a, pS2a) if h < 4 else (pS1b, pS2b)
                hc = (h % 4) * 128
                for ch in range(2):
                    tc_ = slice(ch * 64, ch * 64 + 64)
                    S.op("pe", lambda e, pa=pa, fe=fe, p=p, ch=ch, tc_=tc_, hc=hc: e.matmul(pa.t[tc_, hc:hc + 128], BT.t[fe, p, tc_], ART.t[fe, p, ch, :], start=True, stop=True), r=[BT.r, ART.r], w=[pa.r])
                    S.op("pe", lambda e, pb_=pb_, fe=fe, p=p, ch=ch, tc_=tc_, hc=hc: e.matmul(pb_.t[tc_, hc:hc + 128], KT.t[fe, p, tc_], ART.t[fe, p, ch, :], start=True, stop=True), r=[KT.r, ART.r], w=[pb_.r])
                    S.op("pe", lambda e, fe=fe, p=p, ch=ch, tc_=tc_, h=h: e.matmul(pS3.t[tc_, h * 64:(h + 1) * 64], ART.t[fe, p, ch, 0:64], BT.t[fe, p, tc_], start=True, stop=True), r=[BT.r, ART.r], w=[pS3.r])
            m1b = mask1.t[:].unsqueeze(1).to_broadcast([128, 4, 128])
            for half, (pa, pb_) in enumerate(((pS1a, pS2a), (pS1b, pS2b))):
                hs = slice(half * 4, half * 4 + 4)
                S.op("dve", lambda e, pa=pa, hs=hs: e.tensor_tensor(out=Arb.t[:, hs, :], in0=pa.t[:].rearrange("p (h t) -> p h t", t=128), in1=m1b, op=ALU.mult), r=[pa.r, mask1.r], w=[Arb.r])
                S.op("dve", lambda e, pb_=pb_, hs=hs: e.tensor_tensor(out=Aak.t[:, hs, :], in0=pb_.t[:].rearrange("p (h t) -> p h t", t=128), in1=m1b, op=ALU.mult), r=[pb_.r, mask1.r], w=[Aak.r])
            for ch in range(2):
                tc_ = slice(ch * 64, ch * 64 + 64)
                S.op("dve", lambda e, tc_=tc_: e.tensor_tensor(out=NTm[0].t[tc_, :, tc_], in0=hv(pS3.t[tc_, :]), in1=maskL.t[tc_, :].unsqueeze(1).to_broadcast([64, 8, 64]), op=ALU.mult), r=[pS3.r, maskL.r], w=[NTm[0].r])
                S.op("act", lambda e, tc_=tc_: e.copy(out=Nm[0].t[tc_, :, tc_], in_=Arb.t[tc_, :, 0:64]), r=[Arb.r], w=[Nm[0].r])
            if stop == 4:
                raise _Stop()
            if debug and tl_ == 0:
                tails.append(S.dma("sp", dbg["Arb"], Arb.t[:], r=[Arb.r], key="dbg_Arb"))
            if debug and tl_ == 0:
                tails.append(S.dma("sp", dbg["Aak"], Aak.t[:], r=[Aak.r], key="dbg_Aak"))
            if debug and tl_ == 0:
                tails.append(S.dma("sp", dbg["NT0"], NTm[0].t[:], r=[NTm[0].r], key="dbg_NT0"))
            if debug and tl_ == 0:
                tails.append(S.dma("sp", dbg["ART"], ART.t[:], r=[ART.r], key="dbg_ART"))
            if debug and tl_ == 0:
                tails.append(S.dma("sp", dbg["KT"], KT.t[:], r=[KT.r], key="dbg_KT"))
            if debug and tl_ == 0:
                tails.append(S.dma("sp", dbg["BT"], BT.t[:], r=[BT.r], key="dbg_BT"))
            if debug and tl_ == 0:
                tails.append(S.dma("sp", dbg["Bh"], Bh.t[:], r=[Bh.r], key="dbg_Bh"))
            if debug and tl_ == 0:
                tails.append(S.dma("sp", dbg["Kh"], Kh.t[:], r=[Kh.r], key="dbg_Kh"))
            if hook is not None:
                hook()
            pX = PB[5]
            for h in range(8):
                for ch in range(2):
                    tc_ = slice(ch * 64, ch * 64 + 64)
                    S.op("pe", lambda e, h=h, tc_=tc_, v_=v_: e.matmul(pX.t[tc_, h * 64:(h + 1) * 64], Aak.t[tc_, h, 0:64], v_[tc_, h * 64:(h + 1) * 64], start=True, stop=True), r=[Aak.r] + rG, w=[pX.r])
            S.op("act", lambda e: e.copy(out=Z[0].t[:, :, 64:128], in_=hv(pX.t[:])), r=[pX.r], w=[Z[0].r])
            if stop == 5:
                raise _Stop()
            if debug and tl_ == 0:
                tails.append(S.dma("sp", dbg["Z0"], Z[0].t[:], r=[Z[0].r], key="dbg_Z0"))
            zi = 0
            for lev in range(6):
                Nc, NTc = Nm[lev % 2], NTm[lev % 2]
                Nn, NTn = Nm[(lev + 1) % 2], NTm[(lev + 1) % 2]
                pZa, pZb = PB[0], PB[1]
                Zc, Zn = Z[zi], Z[1 - zi]
                if lev < 5:
                    pN, pNT = (PB[2], PB[3]), (PB[4], PB[5])
                    for h in range(8):
                        hb, hc = h // 4, (h % 4) * 128
                        S.op("pe", lambda e, NTc=NTc, Nc=Nc, h=h, hb=hb, hc=hc: e.matmul(pN[hb].t[:, hc:hc + 128], NTc.t[:, h, :], Nc.t[:, h, :], start=True, stop=True), r=[Nc.r, NTc.r], w=[pN[hb].r])
                        if lev < 4:
                            S.op("pe", lambda e, NTc=NTc, Nc=Nc, h=h, hb=hb, hc=hc: e.matmul(pNT[hb].t[:, hc:hc + 128], Nc.t[:, h, :], NTc.t[:, h, :], start=True, stop=True), r=[Nc.r, NTc.r], w=[pNT[hb].r])
                for h in range(8):
                    pz = pZa if h < 4 else pZb
                    hc = (h % 4) * 128
                    S.op("pe", lambda e, pz=pz, Nc=Nc, Zc=Zc, h=h, hc=hc: e.matmul(pz.t[:, hc:hc + 128], Nc.t[:, h, :], Zc.t[:, h, :], start=True, stop=True), r=[Nc.r, Zc.r], w=[pz.r])
                if lev == 5:
                    Zn = Zf32
                S.op("dve", lambda e, Zc=Zc, Zn=Zn: e.tensor_tensor(out=Zn.t[:, 0:4, :], in0=Zc.t[:, 0:4, :], in1=pZa.t[:].rearrange("p (h t) -> p h t", t=128), op=ALU.add), r=[Zc.r, pZa.r], w=[Zn.r])
                S.op("dve", lambda e, Zc=Zc, Zn=Zn: e.tensor_tensor(out=Zn.t[:, 4:8, :], in0=Zc.t[:, 4:8, :], in1=pZb.t[:].rearrange("p (h t) -> p h t", t=128), op=ALU.add), r=[Zc.r, pZb.r], w=[Zn.r])
                zi = 1 - zi
                if lev < 5:
                    for hb in range(2):
                        S.op("act", lambda e, Nn=Nn, hb=hb: e.copy(out=Nn.t[:, hb * 4:hb * 4 + 4, :], in_=pN[hb].t[:].rearrange("p (h t) -> p h t", t=128)), r=[pN[hb].r], w=[Nn.r])
                        if lev < 4:
                            S.op("act" if hb == 0 else "dve", (lambda e, NTn=NTn, hb=hb: e.copy(out=NTn.t[:, hb * 4:hb * 4 + 4, :], in_=pNT[hb].t[:].rearrange("p (h t) -> p h t", t=128))) if hb == 0 else (lambda e, NTn=NTn, hb=hb: e.tensor_copy(out=NTn.t[:, hb * 4:hb * 4 + 4, :], in_=pNT[hb].t[:].rearrange("p (h t) -> p h t", t=128))), r=[pNT[hb].r], w=[NTn.r])
            Zf = Zf32
            S.op("act", lambda e: e.copy(out=Zfb.t[:], in_=Zf32.t[:]), r=[Zf32.r], w=[Zfb.r])
            pY0 = PB[4]
            for h in range(8):
                for ch in range(2):
                    tc_ = slice(ch * 64, ch * 64 + 64)
                    S.op("pe", lambda e, h=h, tc_=tc_: e.matmul(pY0.t[tc_, h * 64:(h + 1) * 64], Arb.t[tc_, h, 64:128], Zfb.t[tc_, h, 64:128], start=True, stop=False), r=[Arb.r, Zfb.r], w=[pY0.r])
                    S.op("pe", lambda e, h=h, tc_=tc_, v_=v_: e.matmul(pY0.t[tc_, h * 64:(h + 1) * 64], Aak.t[tc_, h, 64:128], v_[tc_, h * 64:(h + 1) * 64], start=False, stop=True), r=[Aak.r] + rG, w=[pY0.r])
            S.op("act", lambda e: e.copy(out=Y0.t[:], in_=pY0.t[:]), r=[pY0.r], w=[Y0.r])
            if stop == 7:
                raise _Stop()
            if debug and tl_ == 0:
                tails.append(S.dma("sp", dbg["Zf"], Zf.t[:], r=[Zf.r], key="dbg_Zf"))
            if debug and tl_ == 0:
                tails.append(S.dma("sp", dbg["Y0"], Y0.t[:], r=[Y0.r], key="dbg_Y0"))
            pR, pQ, pH, pg = PB[5], PB[6], PB[0], PB[1]
            for h in range(8):
                p, e_ = h // 2, h % 2
                fe = slice(e_ * 64, e_ * 64 + 64)
                for ch in range(2):
                    tc_ = slice(ch * 64, ch * 64 + 64)
                    cs = slice(p * 128 + ch * 64, p * 128 + ch * 64 + 64)
                    hs_ = slice(h * 64, h * 64 + 64)
                    S.op("pe", lambda e, fe=fe, cs=cs, tc_=tc_, h=h: e.matmul(pR.t[fe, cs], Zfb.t[tc_, h, 0:64], Arb.t[tc_, h, 64:128], start=True, stop=True), r=[Zfb.r, Arb.r], w=[pR.r])
                    S.op("pe", lambda e, fe=fe, cs=cs, tc_=tc_, h=h, hs_=hs_: e.matmul(pQ.t[fe, cs], Zfb.t[tc_, h, 0:64], Bh.t[tc_, hs_], start=True, stop=True), r=[Zfb.r, Bh.r], w=[pQ.r])
                    S.op("pe", lambda e, fe=fe, cs=cs, tc_=tc_, h=h, hs_=hs_: e.matmul(pH.t[fe, cs], Bh.t[tc_, hs_], Zfb.t[tc_, h, 64:128], start=True, stop=False), r=[Zfb.r, Bh.r], w=[pH.r])
                    S.op("pe", lambda e, fe=fe, cs=cs, tc_=tc_, hs_=hs_: e.matmul(pH.t[fe, cs], Kh.t[tc_, hs_], Vb.t[tc_, hs_], start=False, stop=True), r=[Kh.r, Vb.r], w=[pH.r])
                    S.op("pe", lambda e, fe=fe, tc_=tc_, hs_=hs_, p=p, ch=ch: e.matmul(pg.t[fe, p * 2 + ch:p * 2 + ch + 1], GC.t[tc_, hs_], avg.t[tc_, :], start=True, stop=True), r=[GC.r, avg.r], w=[pg.r])
            for ch in range(2):
                S.op("dve", lambda e, ch=ch: e.tensor_tensor(out=RbT.t[:, :, ch * 64:(ch + 1) * 64], in0=pR.t[:].rearrange("p (a t) -> p a t", t=128)[:, :, ch * 64:(ch + 1) * 64], in1=ART.t[:, :, ch, 64:128], op=ALU.add), r=[pR.r, ART.r], w=[RbT.r])
            S.op("act", lambda e: e.copy(out=Qm.t[:], in_=pQ.t[:].rearrange("p (a t) -> p a t", t=128)), r=[pQ.r], w=[Qm.r])
            S.op("act", lambda e: e.copy(out=Hm.t[:], in_=pH.t[:].rearrange("p (a t) -> p a t", t=128)), r=[pH.r], w=[Hm.r])
            S.op("act", lambda e: e.copy(out=gcol.t[:], in_=pg.t[:, 0:8]), r=[pg.r], w=[gcol.r])
            if stop == 8:
                raise _Stop()
            if debug and tl_ == 0:
                tails.append(S.dma("sp", dbg["RbT"], RbT.t[:], r=[RbT.r], key="dbg_RbT"))
            if debug and tl_ == 0:
                tails.append(S.dma("sp", dbg["Qm"], Qm.t[:], r=[Qm.r], key="dbg_Qm"))
            if debug and tl_ == 0:
                tails.append(S.dma("sp", dbg["Hm"], Hm.t[:], r=[Hm.r], key="dbg_Hm"))
            if debug and tl_ == 0:
                tails.append(S.dma("sp", dbg["gcol"], gcol.t[:], r=[gcol.r], key="dbg_gcol"))
            for ch in range(2):
                tc_ = slice(ch * 64, ch * 64 + 64)
                Sc, Sn = ST[sti[0]], ST[1 - sti[0]]
                pYe, pS_ = (PB[2], PB[4]), PB[3]
                for h in range(8):
                    p, e_ = h // 2, h % 2
                    fe = slice(e_ * 64, e_ * 64 + 64)
                    cs = slice(ch * 64, ch * 64 + 64)
                    pY = pYe[e_]
                    S.op("pe", lambda e, fe=fe, p=p, cs=cs, tc_=tc_, h=h, Sc=Sc, pY=pY: e.matmul(pY.t[tc_, h * 64:(h + 1) * 64], RbT.t[fe, p, cs], Sc.t[fe, p, :], start=True, stop=True), r=[RbT.r, Sc.r], w=[pY.r])
                    S.op("pe", lambda e, fe=fe, p=p, cs=cs, Sc=Sc: e.matmul(pS_.t[fe, p * 64:(p + 1) * 64], Qm.t[fe, p, cs], Sc.t[fe, p, :], start=True, stop=True), r=[Qm.r, Sc.r], w=[pS_.r])
                v4 = lambda ap: ap.rearrange("p (a e d) -> p a e d", e=2, d=64)
                for e_ in range(2):
                    S.op("dve", lambda e, tc_=tc_, e_=e_: e.tensor_tensor(out=v4(Yt.t[tc_, :])[:, :, e_, :], in0=v4(Y0.t[tc_, :])[:, :, e_, :], in1=v4(pYe[e_].t[tc_, :])[:, :, e_, :], op=ALU.add), r=[Y0.r, pYe[e_].r], w=[Yt.r])
                for p in range(4):
                    S.op("dve", lambda e, p=p, ch=ch, Sc=Sc, Sn=Sn: e.scalar_tensor_tensor(out=Sn.t[:, p, :], in0=Sc.t[:, p, :], scalar=gcol.t[:, p * 2 + ch:p * 2 + ch + 1], in1=pS_.t[:, p * 64:(p + 1) * 64], op0=ALU.mult, op1=ALU.add), r=[Sc.r, gcol.r, pS_.r], w=[Sn.r])
                S.op("dve", lambda e, ch=ch, Sn=Sn: e.tensor_tensor(out=Sn.t[:], in0=Sn.t[:], in1=Hm.t[:, :, ch * 64:(ch + 1) * 64], op=ALU.add), r=[Sn.r, Hm.r], w=[Sn.r])
                sti[0] = 1 - sti[0]
            if stop == 9:
                raise _Stop()
            if debug and tl_ == 0:
                S.op("act", lambda e, r_=r_: e.copy(out=t1.t[:], in_=r_), r=rG + [t1.r], w=[t1.r])
                S.op("act", lambda e, v_=v_: e.copy(out=t2.t[:], in_=v_), r=rG + [t2.r], w=[t2.r])
                tails.append(S.dma("sp", dbg["v"], t2.t[:], r=[t2.r], key="dbg_v"))
                for n, src in (("r", t1), ("kp", kp), ("kkn", kkn), ("a", a_), ("sw", sw), ("Y", Yt), ("bonus", bonus), ("Gm", Gm), ("yn", yn)):
                    tails.append(S.dma("sp", dbg[n], src.t[:], r=[src.r], key="dbg_" + n))
                tails.append(S.dma("sp", dbg["S"], ST[sti[0]].t[:], r=[ST[sti[0]].r], key="dbg_S"))


        def tile_tail(i, G):
            tl_ = G * NTG + i
            r_ = G_["r"].t[:, i, :]; k_ = G_["k"].t[:, i, :]; v_ = G_["v"].t[:, i, :]
            rG = [G_[n].r for n in G_]
            sw, Gm, Gi, Gp, a_, kkn, kp, t1, t2 = [X[n] for n in ("sw", "Gm", "Gi", "Gp", "a", "kkn", "kp", "t1", "t2")]
            Yt, yn = X["Y"], X["yn"]
            sfx = "" if tl_ % 2 == 0 else "2"
            Bt, Kt, Bh, Kh, GC, At_, Rt = [X[n + sfx] if (n + sfx) in X else X[n] for n in ("Bt", "Kt", "Bh", "Kh", "GC", "At", "Rt")]
            Vt = X["V0"] if tl_ % 2 == 0 else X["V2"]
            Vb = X["Vb0"] if tl_ % 2 == 0 else X["Vb2"]
            bonus = X["bonus"] if tl_ % 2 == 0 else X["bonus2"]
            glg = X["glg"] if tl_ % 2 == 0 else X["glg2"]
            t3 = X["t3"]
            v_ = Vb.t[:]
            rG = [Vb.r]
            S.op("dve", lambda e: e.tensor_reduce(out=smT.t[:, 16:24], in_=hv(Yt.t[:]), axis=AX.X, op=ALU.add), r=[Yt.r], w=[smT.r])
            S.op("act", lambda e: e.activation(out=t3.t[:], in_=Yt.t[:], func=AF.Square), r=[Yt.r, t3.r], w=[t3.r])
            S.op("dve", lambda e: e.tensor_reduce(out=smT.t[:, 24:32], in_=hv(t3.t[:]), axis=AX.X, op=ALU.add), r=[t3.r], w=[smT.r])
            S.op("dve", lambda e: e.tensor_scalar(out=smT.t[:, 16:24], in0=smT.t[:, 16:24], scalar1=1.0 / 64, scalar2=None, op0=ALU.mult), r=[smT.r], w=[smT.r])
            S.op("dve", lambda e: e.tensor_tensor(out=smT.t[:, 32:40], in0=smT.t[:, 16:24], in1=smT.t[:, 16:24], op=ALU.mult), r=[smT.r], w=[smT.r])
            S.op("dve", lambda e: e.scalar_tensor_tensor(out=smT.t[:, 24:32], in0=smT.t[:, 24:32], scalar=1.0 / 64, in1=smT.t[:, 32:40], op0=ALU.mult, op1=ALU.subtract), r=[smT.r], w=[smT.r])
            S.op("dve", lambda e: e.tensor_scalar(out=smT.t[:, 24:32], in0=smT.t[:, 24:32], scalar1=GN_EPS, scalar2=None, op0=ALU.add), r=[smT.r], w=[smT.r])
            S.op("act", lambda e: e.activation(out=smT.t[:, 24:32], in_=smT.t[:, 24:32], func=AF.Sqrt), r=[smT.r], w=[smT.r])
            S.op("dve", lambda e: e.reciprocal(out=smT.t[:, 24:32], in_=smT.t[:, 24:32]), r=[smT.r], w=[smT.r])
            S.op("dve", lambda e: e.tensor_tensor(out=hv(yn.t[:]), in0=hv(Yt.t[:]), in1=bc8(smT.t[:, 16:24]), op=ALU.subtract), r=[Yt.r, smT.r], w=[yn.r])
            S.op("dve", lambda e: e.tensor_tensor(out=hv(yn.t[:]), in0=hv(yn.t[:]), in1=bc8(smT.t[:, 24:32]), op=ALU.mult), r=[yn.r, smT.r], w=[yn.r])
            S.op("dve", lambda e, glg=glg: e.tensor_tensor(out=yn.t[:], in0=yn.t[:], in1=glg.t[:], op=ALU.mult), r=[yn.r, glg.r], w=[yn.r])
            S.op("dve", lambda e, bonus=bonus: e.tensor_tensor(out=ybf.t[:], in0=yn.t[:], in1=bonus.t[:], op=ALU.add), r=[yn.r, bonus.r], w=[ybf.r])
            for p in range(4):
                S.op("pe", lambda e, p=p: e.transpose(pbb.t[:, p * 128:(p + 1) * 128], ybf.t[:, p * 128:(p + 1) * 128], identb.t[:]), r=[ybf.r, identb.r], w=[pbb.r])
            yo = yTo[tl_ % 2]
            S.op("act", lambda e, yo=yo: e.copy(out=yo.t[:], in_=pbb.t[:, 0:512].rearrange("p (a t) -> p a t", t=128)), r=[pbb.r], w=[yo.r])
            tails.append(S.dma("sp", ysrc[:, :, tl_ * 128:(tl_ + 1) * 128], yo.t[:], r=[yo.r], key="yTo%d" % (tl_ % 2)))

        pend = [None, None]
        try:
          for G in range(NGR):
              if G > 0:
                  S.op("dve", lambda e: e.tensor_copy(out=xg.t[:, :, 0:1], in_=xg.t[:, :, GS:GS + 1]), r=[xg.r], w=[xg.r])
              S.dma("sp", xg.t[:, :, 1:GS + 1], xsrc_fn(G * GS, GS), r=[xg.r], w=[xg.r], key="xg")
              for kc in range(8):
                  S.op("dve", lambda e, kc=kc: e.tensor_scalar(out=xg.t[:, kc, 1:GS + 1], in0=xg.t[:, kc, 1:GS + 1], scalar1=cols.t[:, kc:kc + 1], scalar2=mods[:, kc:kc + 1], op0=ALU.mult, op1=ALU.add), r=[xg.r, cols.r, r_mods], w=[xg.r])
              S.op("dve", lambda e: e.tensor_tensor(out=dx.t[:], in0=xg.t[:, :, 0:GS], in1=xg.t[:, :, 1:GS + 1], op=ALU.subtract), r=[xg.r], w=[dx.r])
              for n, nm in enumerate(("r", "k", "v", "w", "a", "g")):
                  xb = xs[n % 2]
                  for kc in range(8):
                      eng = "dve"
                      S.op(eng, lambda e, kc=kc, n=n, xb=xb: e.scalar_tensor_tensor(out=xb.t[:, kc, :], in0=dx.t[:, kc, :], scalar=mu.t[:, n * 8 + kc:n * 8 + kc + 1], in1=xg.t[:, kc, 1:GS + 1], op0=ALU.mult, op1=ALU.add), r=[dx.r, mu.r, xg.r], w=[xb.r])
                  if n < 3:
                      wt = W[("wr", "wk", "wv")[n]]
                      for i in range(NTG):
                          pb = PB[(n * NTG + i) % 2]
                          for kc in range(8):
                              S.op("pe", lambda e, pb=pb, xb=xb, wt=wt, i=i, kc=kc: e.matmul(pb.t[:], xb.t[:, kc, i * 128:(i + 1) * 128], wt[:, kc, :], start=(kc == 0), stop=(kc == 7)), r=[xb.r, r_w], w=[pb.r])
                          S.op("act", lambda e, pb=pb, nm=nm, i=i: e.copy(out=G_[nm].t[:, i, :], in_=pb.t[:]), r=[pb.r], w=[G_[nm].r])
                  else:
                      w1n, w2n, rows = (("w1", "w2", 64), ("a1", "a2", 64), ("g1", "g2", 128))[n - 3]
                      pb = PB[2]
                      for kc in range(8):
                          S.op("pe", lambda e, pb=pb, xb=xb, w1n=w1n, rows=rows, kc=kc: e.matmul(pb.t[0:rows, 0:GS], W[w1n][:, kc, :], xb.t[:, kc, :], start=(kc == 0), stop=(kc == 7)), r=[xb.r, r_w], w=[pb.r])
                      lt = l1[n - 3]
                      if nm == "w":
                          S.op("act", lambda e, pb=pb: e.activation(out=l1f.t[0:64, :], in_=pb.t[0:64, 0:GS], func=AF.Exp, scale=-2.0), r=[pb.r, l1f.r], w=[l1f.r])
                          S.op("dve", lambda e: e.tensor_scalar(out=l1f.t[0:64, :], in0=l1f.t[0:64, :], scalar1=1.0, scalar2=None, op0=ALU.add), r=[l1f.r], w=[l1f.r])
                          S.op("dve", lambda e: e.reciprocal(out=l1f.t[0:64, :], in_=l1f.t[0:64, :]), r=[l1f.r], w=[l1f.r])
                          S.op("dve", lambda e, lt=lt: e.tensor_scalar(out=lt.t[0:64, :], in0=l1f.t[0:64, :], scalar1=2.0, scalar2=-1.0, op0=ALU.mult, op1=ALU.add), r=[l1f.r], w=[lt.r])
                      elif nm == "a":
                          S.op("act", lambda e, pb=pb, lt=lt: e.copy(out=lt.t[0:64, :], in_=pb.t[0:64, 0:GS]), r=[pb.r], w=[lt.r])
                      else:
                          S.op("act", lambda e, pb=pb: e.activation(out=l1f.t[:], in_=pb.t[:, 0:GS], func=AF.Exp, scale=-1.0), r=[pb.r, l1f.r], w=[l1f.r])
                          S.op("dve", lambda e: e.tensor_scalar(out=l1f.t[:], in0=l1f.t[:], scalar1=1.0, scalar2=None, op0=ALU.add), r=[l1f.r], w=[l1f.r])
                          S.op("dve", lambda e: e.reciprocal(out=l1f.t[:], in_=l1f.t[:]), r=[l1f.r], w=[l1f.r])
                          S.op("act", lambda e, lt=lt: e.copy(out=lt.t[:], in_=l1f.t[:]), r=[l1f.r], w=[lt.r])
              if stop == 1:
                  raise _Stop()
              for i in range(NTG):
                  tile_front_a(i, G)
                  if pend[0] is None:
                      tile_front_b(i, G)
                  else:
                      def hook(i=i, G=G, prev=pend[1]):
                          if prev is not None and not debug:
                              tile_tail(*prev)
                          tile_front_b(i, G)
                      tile_back(*pend[0], hook=hook)
                      if debug:
                          tile_tail(*pend[0])
                      pend[1] = pend[0]
                  pend[0] = (i, G)
          if pend[0] is not None:
              def hook_last(prev=pend[1]):
                  if prev is not None and not debug:
                      tile_tail(*prev)
              tile_back(*pend[0], hook=hook_last)
              tile_tail(*pend[0])
        except _Stop:
            pass
        if after is not None:
            after(C, tails)
        S.emit(st, tail_waits=tails)


PAIRS = [[0, 1], [2, 3], [4, 5], [6, 7]]
TL_KEYS = ("cT", "adaw", "adab", "lng", "lnb", "wo", "win", "wout")


def tl_decl(dt, sfx):
    return dict(cT=dt("cT" + sfx, [128, 8], F32), adaw=dt("adaw" + sfx, [D, 4 * D], F32), adab=dt("adab" + sfx, [128, 32], F32),
                lng=dt("lng" + sfx, [128, 16], F32), lnb=dt("lnb" + sfx, [128, 16], F32),
                wo=dt("wo" + sfx, [D, D], F32), win=dt("win" + sfx, [D, 2 * DFF], F32), wout=dt("wout" + sfx, [DFF, D], F32))


def build_fused(stop=99):
    nc = bass.Bass("TRN2", target_bir_lowering=False)
    dt = lambda n, s, d, k="ExternalInput": nc.dram_tensor(n, list(s), d, kind=k).ap()
    it = lambda n, s, d: nc.dram_tensor(n, list(s), d).ap()
    fox_in = fox_decl(dt, 4096, "_f")
    oT_d = dt("oT", [D, 2048], F32, "ExternalOutput")
    cin1, cout1 = it("cin1", [4, 128, 4096], BF16), it("cout1", [4, 2, 128, 4096], BF16)
    cin2, cout2 = it("cin2", [8, 128, 2048], F32), it("cout2", [8, 2, 128, 2048], F32)
    cin3, cout3 = it("cin3", [4, 128, 4096], BF16), it("cout3", [4, 2, 128, 4096], BF16)

    def gather(cin, cout, key, nblk):
        def after(C, tails):
            prev = list(tails)
            for k in range(nblk):
                o = C.S.cc("AllGather", PAIRS, cin[k], cout[k].rearrange("r p t -> (r p) t"), key=key, extra=prev)
            tails.append(o)
        return after

    cin1_v = cin1.rearrange("k p t -> (k p) t")
    cin3_v = cin3.rearrange("k p t -> (k p) t")
    cin2_v = cin2.rearrange("k p t -> (k p) t")
    y1_v = cout1
    y3_v = cout3
    fox_phase(nc, "f_", fox_in, cin1_v, 4096, after=gather(cin1, cout1, "g1", 4))
    if stop <= 1:
        return nc
    phase_end(nc)
    sel_d = dt("sel", [128, 2], F32)
    xTh_d = dt("xTh", [D, 2048], F32)
    t = tl_decl(dt, "_t0")
    tl_phase(nc, "t0_", y1_v, xTh_d, t["cT"], t["adaw"], t["adab"], t["lng"], t["lnb"], t["wo"], t["win"], t["wout"], cin2_v, 2048, 1024,
             sel_d=sel_d, after=gather(cin2, cout2, "g2", 8))
    if stop <= 2:
        return nc
    phase_end(nc)
    rw_in = rwkv_decl(dt, "_r")
    c2v = cout2.rearrange("kc r p t -> r p kc t")
    rwkv_phase(nc, "r_", lambda t0, n: c2v[t0 // 2048, :, :, (t0 % 2048):(t0 % 2048) + n], rw_in, cin3_v, 4096, after=gather(cin3, cout3, "g3", 4))
    if stop <= 3:
        return nc
    phase_end(nc)
    t = tl_decl(dt, "_t1")
    tl_phase(nc, "t1_", y3_v, cin2_v, t["cT"], t["adaw"], t["adab"], t["lng"], t["lnb"], t["wo"], t["win"], t["wout"], oT_d, 2048, 1024, sel_d=sel_d)
    return nc


NCORES = 8
_PROGS = {}


def _prog(name, fn):
    if name not in _PROGS:
        _PROGS[name] = fn()
    return _PROGS[name]


def _col_layout(v):
    return np.ascontiguousarray(np.asarray(v).reshape(-1, 128).T)


def _bc(v, n=512):
    v = np.asarray(v, dtype=np.float32).reshape(-1)
    return np.ascontiguousarray(np.broadcast_to(v[None, :], (128, v.shape[0])))


def _run(nc, in_maps):
    res = run_bass_kernel_spmd(nc, in_maps, core_ids=list(range(NCORES)))
    return res.results


def _tl_maps(yT_list, xT_list, c, ada_w_i, ada_b_i, ln_g_i, ln_b_i, wo, win, wout):
    maps = []
    adaw = np.ascontiguousarray(ada_w_i[:, 2 * D:])
    adab = _col_layout(ada_b_i[2 * D:])
    lng = _col_layout(ln_g_i.reshape(-1))
    lnb = _col_layout(ln_b_i.reshape(-1))
    for core in range(NCORES):
        b, th = core // 2, core % 2
        ts = slice(th * 2048, (th + 1) * 2048)
        maps.append({
            "yT": np.ascontiguousarray(yT_list[b][:, ts]),
            "xT": np.ascontiguousarray(xT_list[b][:, ts]),
            "cT": _col_layout(c[b]),
            "adaw": adaw, "adab": adab, "lng": lng, "lnb": lnb,
            "wo": wo, "win": win, "wout": wout,
        })
    return maps


def kernel_unfused(x, c, ada_w, ada_b, ln_g, ln_b, ffn_w_in, ffn_w_out,
           fox_w_in, fox_b_f, fox_q_g, fox_k_g, fox_w_o,
           rwkv_mu, rwkv_w_rkv, rwkv_w0, rwkv_w1, rwkv_w2, rwkv_a0, rwkv_a1, rwkv_a2,
           rwkv_g1, rwkv_g2, rwkv_k_k, rwkv_k_a, rwkv_r_k, rwkv_lnx_g, rwkv_lnx_b, rwkv_w_o):
    f = lambda a: np.ascontiguousarray(np.asarray(a, dtype=np.float32))
    x, c, ada_w, ada_b, ln_g, ln_b = f(x), f(c), f(ada_w), f(ada_b), f(ln_g), f(ln_b)
    ffn_w_in, ffn_w_out = f(ffn_w_in), f(ffn_w_out)
    B = x.shape[0]
    w_in = f(fox_w_in)[0]
    b_f, q_g, k_g = f(fox_b_f)[0], f(fox_q_g)[0], f(fox_k_g)[0]
    maps = []
    for core in range(NCORES):
        b, hh = core // 2, core % 2
        sl = slice(hh * 512, (hh + 1) * 512)
        maps.append({
            "x": x[b],
            "cT": _col_layout(c[b]),
            "adaw": np.ascontiguousarray(ada_w[0][:, 0:2 * D]),
            "adab": _col_layout(ada_b[0][0:2 * D]),
            "wq": np.ascontiguousarray(w_in[:, 0:D][:, sl]),
            "wk": np.ascontiguousarray(w_in[:, D:2 * D][:, sl]),
            "wv": np.ascontiguousarray(w_in[:, 2 * D:3 * D][:, sl]),
            "wf": np.ascontiguousarray(w_in[:, 3 * D + hh * 8:3 * D + hh * 8 + 8]),
            "wg": np.ascontiguousarray(w_in[:, 3 * D + 16:][:, sl]),
            "bfb": _bc(b_f[hh * 8:(hh + 1) * 8]),
            "qgb": _bc(np.tile(q_g, 8)),
            "kgb": _bc(np.tile(k_g, 8)),
        })
    r1 = _run(_prog("fox", build_fox), maps)
    yT = [np.concatenate([r1[2 * b]["ygT"], r1[2 * b + 1]["ygT"]], axis=0) for b in range(B)]
    xT = [np.ascontiguousarray(x[b].T) for b in range(B)]
    tlp = _prog("tl", build_tl)
    r2 = _run(tlp, _tl_maps(yT, xT, c, ada_w[0], ada_b[0], ln_g[0], ln_b[0], f(fox_w_o)[0], ffn_w_in[0], ffn_w_out[0]))
    x1T = [np.concatenate([r2[2 * b]["oT"], r2[2 * b + 1]["oT"]], axis=1) for b in range(B)]
    mu = f(rwkv_mu)[0]
    w_rkv = f(rwkv_w_rkv)[0]
    P = dict(w0=f(rwkv_w0)[0], w1=f(rwkv_w1)[0], w2=f(rwkv_w2)[0], a0=f(rwkv_a0)[0], a1=f(rwkv_a1)[0], a2=f(rwkv_a2)[0],
             g1=f(rwkv_g1)[0], g2=f(rwkv_g2)[0], k_k=f(rwkv_k_k)[0], k_a=f(rwkv_k_a)[0], r_k=f(rwkv_r_k)[0].reshape(-1),
             lnx_g=f(rwkv_lnx_g)[0], lnx_b=f(rwkv_lnx_b)[0])
    mu_l = np.concatenate([_col_layout(mu[n]) for n in range(6)], axis=1)
    maps = []
    for core in range(NCORES):
        b, hh = core // 2, core % 2
        sl = slice(hh * 512, (hh + 1) * 512)
        maps.append({
            "xT": x1T[b],
            "cT": _col_layout(c[b]),
            "adaw": np.ascontiguousarray(ada_w[1][:, 0:2 * D]),
            "adab": _col_layout(ada_b[1][0:2 * D]),
            "mu": mu_l,
            "wr": np.ascontiguousarray(w_rkv[0][:, sl]),
            "wk": np.ascontiguousarray(w_rkv[1][:, sl]),
            "wv": np.ascontiguousarray(w_rkv[2][:, sl]),
            "w1": P["w1"], "a1": P["a1"], "g1": P["g1"],
            "w2": np.ascontiguousarray(P["w2"][:, sl]),
            "a2": np.ascontiguousarray(P["a2"][:, sl]),
            "g2": np.ascontiguousarray(P["g2"][:, sl]),
            "w0b": _bc(P["w0"][sl]), "a0b": _bc(P["a0"][sl]), "kkb": _bc(P["k_k"][sl]), "kab": _bc(P["k_a"][sl]),
            "rkb": _bc(P["r_k"][sl]), "lngb": _bc(P["lnx_g"][sl]), "lnbb": _bc(P["lnx_b"][sl]),
        })
    r3 = _run(_prog("rwkv", build_rwkv), maps)
    y2T = [np.concatenate([r3[2 * b]["yT"], r3[2 * b + 1]["yT"]], axis=0) for b in range(B)]
    r4 = _run(tlp, _tl_maps(y2T, x1T, c, ada_w[1], ada_b[1], ln_g[1], ln_b[1], f(rwkv_w_o)[0], ffn_w_in[1], ffn_w_out[1]))
    out = np.empty(x.shape, np.float32)
    for core in range(NCORES):
        b, th = core // 2, core % 2
        out[b, th * 2048:(th + 1) * 2048, :] = r4[core]["oT"].T
    return out


def kernel(x, c, ada_w, ada_b, ln_g, ln_b, ffn_w_in, ffn_w_out,
           fox_w_in, fox_b_f, fox_q_g, fox_k_g, fox_w_o,
           rwkv_mu, rwkv_w_rkv, rwkv_w0, rwkv_w1, rwkv_w2, rwkv_a0, rwkv_a1, rwkv_a2,
           rwkv_g1, rwkv_g2, rwkv_k_k, rwkv_k_a, rwkv_r_k, rwkv_lnx_g, rwkv_lnx_b, rwkv_w_o):
    f = lambda a: np.ascontiguousarray(np.asarray(a, dtype=np.float32))
    x, c, ada_w, ada_b, ln_g, ln_b = f(x), f(c), f(ada_w), f(ada_b), f(ln_g), f(ln_b)
    ffn_w_in, ffn_w_out = f(ffn_w_in), f(ffn_w_out)
    w_in = f(fox_w_in)[0]
    b_f, q_g, k_g = f(fox_b_f)[0], f(fox_q_g)[0], f(fox_k_g)[0]
    mu = f(rwkv_mu)[0]
    w_rkv = f(rwkv_w_rkv)[0]
    P = dict(w0=f(rwkv_w0)[0], w1=f(rwkv_w1)[0], w2=f(rwkv_w2)[0], a0=f(rwkv_a0)[0], a1=f(rwkv_a1)[0], a2=f(rwkv_a2)[0],
             g1=f(rwkv_g1)[0], g2=f(rwkv_g2)[0], k_k=f(rwkv_k_k)[0], k_a=f(rwkv_k_a)[0], r_k=f(rwkv_r_k)[0].reshape(-1),
             lnx_g=f(rwkv_lnx_g)[0], lnx_b=f(rwkv_lnx_b)[0])
    mu_l = np.concatenate([_col_layout(mu[n]) for n in range(6)], axis=1)
    wos = [f(fox_w_o)[0], f(rwkv_w_o)[0]]
    tl_common = []
    for L in range(2):
        tl_common.append({
            "adaw_t%d" % L: np.ascontiguousarray(ada_w[L][:, 2 * D:]), "adab_t%d" % L: _col_layout(ada_b[L][2 * D:]),
            "lng_t%d" % L: _col_layout(ln_g[L].reshape(-1)), "lnb_t%d" % L: _col_layout(ln_b[L].reshape(-1)),
            "wo_t%d" % L: wos[L], "win_t%d" % L: ffn_w_in[L], "wout_t%d" % L: ffn_w_out[L]})
    adaw_f = np.ascontiguousarray(ada_w[0][:, 0:2 * D])
    adaw_r = np.ascontiguousarray(ada_w[1][:, 0:2 * D])
    maps = []
    for core in range(NCORES):
        b, j = core // 2, core % 2
        sl = slice(j * 512, (j + 1) * 512)
        cT = _col_layout(c[b])
        m = {
            "x_f": x[b], "cT_f": cT, "adaw_f": adaw_f, "adab_f": _col_layout(ada_b[0][0:2 * D]),
            "wq_f": np.ascontiguousarray(w_in[:, 0:D][:, sl]), "wk_f": np.ascontiguousarray(w_in[:, D:2 * D][:, sl]),
            "wv_f": np.ascontiguousarray(w_in[:, 2 * D:3 * D][:, sl]), "wf_f": np.ascontiguousarray(w_in[:, 3 * D + j * 8:3 * D + j * 8 + 8]),
            "wg_f": np.ascontiguousarray(w_in[:, 3 * D + 16:][:, sl]),
            "bfb_f": _bc(b_f[j * 8:(j + 1) * 8]), "qgb_f": _bc(np.tile(q_g, 8)), "kgb_f": _bc(np.tile(k_g, 8)),
            "cT_r": cT, "adaw_r": adaw_r, "adab_r": _col_layout(ada_b[1][0:2 * D]), "mu_r": mu_l,
            "wr_r": np.ascontiguousarray(w_rkv[0][:, sl]), "wk_r": np.ascontiguousarray(w_rkv[1][:, sl]), "wv_r": np.ascontiguousarray(w_rkv[2][:, sl]),
            "w1_r": P["w1"], "a1_r": P["a1"], "g1_r": P["g1"],
            "w2_r": np.ascontiguousarray(P["w2"][:, sl]), "a2_r": np.ascontiguousarray(P["a2"][:, sl]), "g2_r": np.ascontiguousarray(P["g2"][:, sl]),
            "w0b_r": _bc(P["w0"][sl]), "a0b_r": _bc(P["a0"][sl]), "kkb_r": _bc(P["k_k"][sl]), "kab_r": _bc(P["k_a"][sl]),
            "rkb_r": _bc(P["r_k"][sl]), "lngb_r": _bc(P["lnx_g"][sl]), "lnbb_r": _bc(P["lnx_b"][sl]),
            "cT_t0": cT, "cT_t1": cT,
            "sel": np.ascontiguousarray(np.broadcast_to(np.array([1.0 - j, float(j)], np.float32)[None, :], (128, 2))),
            "xTh": np.ascontiguousarray(x[b, j * 2048:(j + 1) * 2048, :].T),
        }
        m.update(tl_common[0])
        m.update(tl_common[1])
        maps.append(m)
    res = _run(_prog("fused", build_fused), maps)
    out = np.empty(x.shape, np.float32)
    for core in range(NCORES):
        b, j = core // 2, core % 2
        out[b, j * 2048:(j + 1) * 2048, :] = res[core]["oT"].T
    return out
```

```python
import numpy as np
from contextlib import ExitStack
import concourse.bass as bass
import concourse.mybir as mybir
from concourse.bass_utils import run_bass_kernel_spmd

F32 = mybir.dt.float32
BF16 = mybir.dt.bfloat16
AF = mybir.ActivationFunctionType
ALU = mybir.AluOpType
AX = mybir.AxisListType

ENGS = ("pe", "act", "dve", "pool", "sp")
LAST_SEMS = []
SEM_CTR = [0]


def phase_end(nc):
    nc.all_engine_barrier()
    nc.clear_and_free_semaphores(list(LAST_SEMS))
    LAST_SEMS[:] = []
    nc.all_engine_barrier()


class Res:
    __slots__ = ("name", "last_w", "readers", "excl")

    def __init__(self, name, excl=False):
        self.name = name
        self.excl = excl
        self.last_w = None
        self.readers = []


class Op:
    __slots__ = ("eng", "fn", "deps", "sig", "count", "is_dma", "key", "semval", "vc", "waits", "gid")


class Deferred:
    op = None


class Sched:
    def __init__(self, nc):
        self.nc = nc
        self.ops = []
        self.dma_counts = {}

    def res(self, name, excl=False):
        return Res(name, excl)

    def _add(self, op, r, w, extra=()):
        deps = [(d, "raw") for d in extra]
        for x in r:
            if x.last_w is not None:
                deps.append((x.last_w, "raw"))
            if x.excl:
                for rd in x.readers:
                    deps.append((rd, "war"))
        for x in w:
            if x.last_w is not None:
                deps.append((x.last_w, "waw"))
            for rd in x.readers:
                deps.append((rd, "war"))
        dd = []
        for d, kind in deps:
            if d is op:
                continue
            if (not d.is_dma) and (not op.is_dma) and d.eng == op.eng:
                if op.eng == "pe" or kind == "war":
                    continue
            dd.append(d)
        op.deps = dd
        for d in dd:
            d.sig = True
        for x in w:
            x.last_w = op
            x.readers = []
        for x in r:
            if x.last_w is not op:
                if not op.is_dma:
                    x.readers = [q for q in x.readers if q.is_dma or q.eng != op.eng]
                x.readers.append(op)
        op.gid = len(self.ops)
        self.ops.append(op)
        return op

    def begin_defer(self):
        self.flush_deferred()
        self._defer = []
        self._capturing = True

    def end_defer(self):
        self._capturing = False

    def flush_deferred(self):
        q = getattr(self, "_defer", None)
        self._defer = None
        self._capturing = False
        if q:
            for kind, ph, a, kw in q:
                real = getattr(self, kind)(*a, **kw)
                if ph is not None:
                    ph.op = real
        self._deferred_done = True

    def op(self, eng, fn, r=(), w=()):
        if getattr(self, "_capturing", False):
            self._defer.append(("op", None, (eng, fn), dict(r=list(r), w=list(w))))
            return None
        o = Op()
        o.eng = eng
        o.fn = fn
        o.is_dma = False
        o.sig = False
        o.key = eng
        return self._add(o, r, w)

    def dma(self, eng, out, in_, r=(), w=(), key=None, **kw):
        if getattr(self, "_capturing", False):
            ph = Deferred()
            self._defer.append(("dma", ph, (eng, out, in_), dict(r=list(r), w=list(w), key=key, **kw)))
            return ph
        o = Op()
        o.eng = eng
        o.is_dma = True
        o.sig = True
        assert key is not None
        o.key = "dma:" + key
        o.fn = lambda e: e.dma_start(out=out, in_=in_, **kw)
        return self._add(o, r, w)

    def ext_wait(self, eng, sem, value):
        return self.op(eng, lambda e: e.wait_ge(sem, value))

    def cc(self, kind, groups, in_ap, out_ap, r=(), w=(), key=None, extra=(), ext_sem=None):
        o = Op()
        o.eng = "pool"
        o.is_dma = True
        o.sig = True
        o.key = "cc:" + key
        if ext_sem is not None:
            self.ext_sems = getattr(self, "ext_sems", {})
            self.ext_sems[o.key] = ext_sem
        o.fn = lambda e: e.collective_compute(kind, ALU.bypass, replica_groups=groups, ins=[in_ap], outs=[out_ap])
        return self._add(o, r, w, extra)

    def finalize(self, final_wait_eng="sp"):
        nc = self.nc
        counts = {e: 0 for e in ENGS}
        dmac = {}
        for o in self.ops:
            if o.is_dma:
                dmac[o.key] = dmac.get(o.key, 0) + (1 if o.key.startswith("cc:") else 16)
                o.semval = dmac[o.key]
            elif o.sig:
                counts[o.eng] += 1
                o.semval = counts[o.eng]
        seen = {e: {} for e in ENGS}
        for o in self.ops:
            s = seen[o.eng]
            need = {}
            for d in o.deps:
                if s.get(d.key, 0) < d.semval:
                    if d.key not in need or need[d.key].semval < d.semval:
                        need[d.key] = d
            o.waits = [(d.key, d.semval) for d in need.values()]
            for d in need.values():
                for k, v in d.vc.items():
                    if s.get(k, 0) < v:
                        s[k] = v
            if o.sig:
                vc = dict(s)
                vc[o.key] = o.semval
                o.vc = vc
            else:
                o.vc = None
        return counts, dmac

    def emit(self, stack, tail_waits=()):
        nc = self.nc
        self.flush_deferred()
        tail_waits = [getattr(o, "op", None) or o for o in tail_waits]
        counts, dmac = self.finalize()
        sems = {}
        ext = getattr(self, "ext_sems", {})
        for k in list(ENGS) + list(dmac):
            if k in ext:
                continue
            SEM_CTR[0] += 1
            sems[k] = nc.alloc_semaphore(name="sem%d" % SEM_CTR[0])
        self.nsems = len(sems)
        LAST_SEMS[:] = list(sems.values())
        sems.update(ext)
        per = {e: [o for o in self.ops if o.eng == e] for e in ENGS}
        block = stack.enter_context(nc.Block())

        def run(eng_name, eng):
            for o in per[eng_name]:
                for k, v in o.waits:
                    eng.wait_ge(sems[k], v)
                ins = o.fn(eng)
                if o.sig:
                    if o.is_dma:
                        ins.then_inc(sems[o.key], 1 if o.key.startswith("cc:") else 16)
                    else:
                        ins.then_inc(sems[o.key], 1)
            if eng_name == "sp":
                done = {}
                for o in tail_waits:
                    done[o.key] = max(done.get(o.key, 0), o.semval)
                for k, v in done.items():
                    eng.wait_ge(sems[k], v)

        @block.tensor
        def _(e):
            run("pe", e)

        @block.scalar
        def _(e):
            run("act", e)

        @block.vector
        def _(e):
            run("dve", e)

        @block.gpsimd
        def _(e):
            run("pool", e)

        @block.sync
        def _(e):
            run("sp", e)


D = 1024
DFF = 2816
NFC = 22
ALPHA = 4.0 ** 0.25
LN_EPS = 1e-5
EPS_P = LN_EPS / (ALPHA * ALPHA)


class Ctx:
    def __init__(self, nc, st, pfx=""):
        self.nc = nc
        self.st = st
        self.S = Sched(nc)
        self.n = 0
        self.pfx = pfx

    def sb(self, name, shape, dt):
        return self.st.enter_context(self.nc.sbuf_tensor(self.pfx + "sb_" + name, list(shape), dt))

    def ps(self, name, shape, dt=F32):
        return self.st.enter_context(self.nc.psum_tensor(self.pfx + "ps_" + name, list(shape), dt))

    def R(self, name, excl=False):
        return self.S.res(name, excl)


def make_consts(C):
    S = C.S
    C.ones = C.sb("ones", [128, 128], F32)
    C.r_ones = C.R("ones")
    S.op("pool", lambda e: e.memset(C.ones[:], 1.0), w=[C.r_ones])
    C.ident = C.sb("ident", [128, 128], F32)
    C.r_ident = C.R("ident")
    S.op("pool", lambda e: e.memset(C.ident[:], 1.0), w=[C.r_ident])
    S.op("pool", lambda e: e.affine_select(out=C.ident[:], in_=C.ident[:], pattern=[[-1, 128]], compare_op=ALU.is_equal, fill=0.0, base=0, channel_multiplier=1), r=[C.r_ident], w=[C.r_ident])


def compute_mods(C, cT_d, adaw_d, adab_d, nm, wbufs, psm, r_psm):
    S = C.S
    cT = C.sb("cT", [128, 8], F32)
    r_cT = C.R("cT")
    S.dma("sp", cT[:], cT_d, w=[r_cT], key="cT")
    S.op("act", lambda e: e.activation(out=cT[:], in_=cT[:], func=AF.Silu), r=[r_cT], w=[r_cT])
    adab = C.sb("adab", [128, nm * 8], F32)
    r_adab = C.R("adab")
    S.dma("sp", adab[:], adab_d, w=[r_adab], key="adab")
    mods = C.sb("mods", [128, nm * 8], F32)
    r_mods = C.R("mods")
    src = adaw_d.rearrange("(kc p) n -> p kc n", p=128)
    st_tmp = None
    if wbufs is None:
        st_keep, st_tmp = C.st, ExitStack()
        C.st = st_tmp
        wbufs = [(C.sb("wA%d" % i, [128, 8, 256], F32), C.R("wA%d" % i)) for i in range(2)]
        C.st = st_keep
    for m in range(nm):
        for q in range(4):
            bi = (m * 4 + q) % 2
            wt, r_wt = wbufs[bi]
            c0 = m * 1024 + q * 256
            S.dma("sp", wt[:], src[:, :, c0:c0 + 256], w=[r_wt], key="adaw%d" % bi)
            for j in range(2):
                cc = q * 2 + j
                for kc in range(8):
                    S.op("pe", lambda e, m=m, cc=cc, kc=kc, j=j, wt=wt: e.matmul(psm[:, m * 8 + cc:m * 8 + cc + 1], wt[:, kc, j * 128:(j + 1) * 128], cT[:, kc:kc + 1], start=(kc == 0), stop=(kc == 7)), r=[r_wt, r_cT], w=[r_psm])
    S.op("dve", lambda e: e.tensor_tensor(out=mods[:], in0=psm[:, 0:nm * 8], in1=adab[:], op=ALU.add), r=[r_psm, r_adab], w=[r_mods])
    if st_tmp is not None:
        st_tmp.close()
    return mods, r_mods


def ln_group(C, zT, r_z, g, n0, colA, colB, r_cols, outs, P):
    S = C.S
    ps_s, r_ps_s = P["st0"]
    ps_q, r_ps_q = P["st1"]
    sq, r_sq = P["sq"]
    for cc in range(8):
        S.op("act", lambda e, cc=cc: e.activation(out=sq[cc % 2][:], in_=zT[:, cc, n0:n0 + 512], func=AF.Square), r=[r_z[cc]], w=[r_sq[cc % 2]])
        S.op("pe", lambda e, cc=cc: e.matmul(ps_s[:], C.ones[:], zT[:, cc, n0:n0 + 512], start=(cc == 0), stop=(cc == 7)), r=[C.r_ones, r_z[cc]], w=[r_ps_s])
        S.op("pe", lambda e, cc=cc: e.matmul(ps_q[:], C.ones[:], sq[cc % 2][:], start=(cc == 0), stop=(cc == 7)), r=[C.r_ones, r_sq[cc % 2]], w=[r_ps_q])
    mean, r_mean = P["mean"]
    rstd, r_rstd = P["rstd"]
    S.op("act", lambda e: e.mul(out=mean[:], in_=ps_s[:], mul=1.0 / D), r=[r_ps_s], w=[r_mean])
    S.op("dve", lambda e: e.tensor_tensor(out=rstd[:], in0=mean[:], in1=mean[:], op=ALU.mult), r=[r_mean], w=[r_rstd])
    S.op("dve", lambda e: e.scalar_tensor_tensor(out=rstd[:], in0=ps_q[:], scalar=1.0 / D, in1=rstd[:], op0=ALU.mult, op1=ALU.subtract), r=[r_ps_q, r_rstd], w=[r_rstd])
    S.op("dve", lambda e: e.tensor_scalar(out=rstd[:], in0=rstd[:], scalar1=P["eps"], scalar2=None, op0=ALU.add), r=[r_rstd], w=[r_rstd])
    S.op("act", lambda e: e.activation(out=rstd[:], in_=rstd[:], func=AF.Sqrt), r=[r_rstd], w=[r_rstd])
    S.op("dve", lambda e: e.reciprocal(out=rstd[:], in_=rstd[:]), r=[r_rstd], w=[r_rstd])
    S.op("dve", lambda e: e.scalar_tensor_tensor(out=mean[:], in0=mean[:], scalar=-1.0, in1=rstd[:], op0=ALU.mult, op1=ALU.mult), r=[r_mean, r_rstd], w=[r_mean])
    tt, r_tt = P["tt"]
    for cc in range(8):
        k = cc % 2
        S.op("dve", lambda e, cc=cc, k=k: e.tensor_tensor(out=tt[k][:], in0=zT[:, cc, n0:n0 + 512], in1=rstd[:], op=ALU.mult), r=[r_z[cc], r_rstd], w=[r_tt[k]])
        S.op("dve", lambda e, cc=cc, k=k: e.tensor_tensor(out=tt[k][:], in0=tt[k][:], in1=mean[:], op=ALU.add), r=[r_tt[k], r_mean], w=[r_tt[k]])
        for (dst, r_dst, gt, go, bt, bo, eng) in outs:
            if eng == "act":
                S.op("act", lambda e, cc=cc, k=k, dst=dst, gt=gt, go=go, bt=bt, bo=bo: e.activation(out=dst[:, cc, n0:n0 + 512], in_=tt[k][:], func=AF.Identity, scale=gt[:, go + cc:go + cc + 1], bias=bt[:, bo + cc:bo + cc + 1]), r=[r_tt[k]] + r_cols, w=[r_dst[cc]])
            else:
                S.op(eng, lambda e, cc=cc, k=k, dst=dst, gt=gt, go=go, bt=bt, bo=bo: e.tensor_scalar(out=dst[:, cc, n0:n0 + 512], in0=tt[k][:], scalar1=gt[:, go + cc:go + cc + 1], scalar2=bt[:, bo + cc:bo + cc + 1], op0=ALU.mult, op1=ALU.add), r=[r_tt[k]] + r_cols, w=[r_dst[cc]])


def build_tl(NT=2048, HALF=1024, debug=False):
    nc = bass.Bass("TRN2", target_bir_lowering=False)
    dt = lambda n, s, d, k="ExternalInput": nc.dram_tensor(n, list(s), d, kind=k).ap()
    yT_d = dt("yT", [D, NT], BF16)
    xT_d = dt("xT", [D, NT], F32)
    cT_d = dt("cT", [128, 8], F32)
    adaw_d = dt("adaw", [D, 4 * D], F32)
    adab_d = dt("adab", [128, 32], F32)
    lng_d = dt("lng", [128, 16], F32)
    lnb_d = dt("lnb", [128, 16], F32)
    wo_d = dt("wo", [D, D], F32)
    win_d = dt("win", [D, 2 * DFF], F32)
    wout_d = dt("wout", [DFF, D], F32)
    oT_d = dt("oT", [D, NT], F32, "ExternalOutput")
    tl_phase(nc, "", yT_d, xT_d, cT_d, adaw_d, adab_d, lng_d, lnb_d, wo_d, win_d, wout_d, oT_d, NT, HALF, debug=debug)
    return nc


def tl_phase(nc, pfx, yT_d, xT_d, cT_d, adaw_d, adab_d, lng_d, lnb_d, wo_d, win_d, wout_d, oT_d, NT=2048, HALF=1024, debug=False, sel_d=None, after=None, before_y=None):
    dt = lambda n, s, d, k="ExternalInput": nc.dram_tensor(n, list(s), d, kind=k).ap()
    NG = HALF // 512
    if debug:
        dbg_cols = dt("dbg_cols", [128, 48], F32, "ExternalOutput")
        dbg_mods = dt("dbg_mods", [128, 32], F32, "ExternalOutput")
        dbg_x1 = dt("dbg_x1", [D, HALF], F32, "ExternalOutput")
        dbg_h = dt("dbg_h", [D, HALF], BF16, "ExternalOutput")
        dbg_a = dt("dbg_a", [DFF, HALF], BF16, "ExternalOutput")
    with ExitStack() as st:
        C = Ctx(nc, st, pfx)
        S = C.S
        make_consts(C)
        PB = [(C.ps("pb%d" % i, [128, 512]), C.R("pb%d" % i, True)) for i in range(8)]
        if sel_d is not None:
            selt = C.sb("selt", [128, 2], F32)
            r_selt = C.R("selt")
            S.dma("sp", selt[:], sel_d, w=[r_selt], key="selt")
            ystg = [[(C.sb("ystg%d_%d" % (i, j), [128, HALF], BF16), C.R("ystg%d_%d" % (i, j))) for j in range(2)] for i in range(2)]
        xT = C.sb("xT", [128, 8, HALF], F32)
        r_x = [C.R("xT%d" % cc) for cc in range(8)]
        yT = C.sb("yT", [128, 8, HALF], BF16)
        r_y = [C.R("yT%d" % cc) for cc in range(8)]
        hT = yT
        r_h = r_y
        aT = C.sb("aT", [128, NFC, HALF], BF16)
        r_a = [C.R("aT%d" % f) for f in range(NFC)]
        wo = C.sb("wo", [128, 8, D], BF16)
        r_wo = C.R("wo")
        wA = [(C.sb("wA%d" % i, [128, 8, 256], F32), C.R("wA%d" % i)) for i in range(2)]
        wi = [(C.sb("wi%d" % i, [128, 8, 512], BF16), C.R("wi%d" % i)) for i in range(2)]
        wo2 = [(C.sb("wo2%d" % i, [128, NFC, 256], BF16), C.R("wo2%d" % i)) for i in range(2)]
        P = {
            "st0": PB[6], "st1": PB[7],
            "sq": ([C.sb("sq%d" % i, [128, 512], F32) for i in range(2)], [C.R("sq%d" % i) for i in range(2)]),
            "mean": (C.sb("mean", [128, 512], F32), C.R("mean")),
            "rstd": (C.sb("rstd", [128, 512], F32), C.R("rstd")),
            "tt": ([C.sb("tt%d" % i, [128, 512], F32) for i in range(2)], [C.R("tt%d" % i) for i in range(2)]),
            "eps": EPS_P,
        }
        sl = [C.sb("sl%d" % i, [128, 512], F32) for i in range(2)]
        r_sl = [C.R("sl%d" % i) for i in range(2)]
        mods, r_mods = compute_mods(C, cT_d, adaw_d, adab_d, 4, wA, PB[5][0], PB[5][1])
        lng = C.sb("lng", [128, 16], F32)
        lnb = C.sb("lnb", [128, 16], F32)
        r_ln = C.R("ln")
        S.dma("sp", lng[:], lng_d, w=[r_ln], key="lng")
        S.dma("sp", lnb[:], lnb_d, w=[r_ln], key="lnb")
        cols = C.sb("cols", [128, 48], F32)
        r_cols = C.R("cols")
        S.op("dve", lambda e: e.tensor_scalar(out=cols[:, 0:8], in0=mods[:, 0:8], scalar1=1.0 / ALPHA, scalar2=None, op0=ALU.mult), r=[r_mods], w=[r_cols])
        S.op("dve", lambda e: e.tensor_scalar(out=cols[:, 8:16], in0=mods[:, 24:32], scalar1=1.0 / ALPHA, scalar2=None, op0=ALU.mult), r=[r_mods], w=[r_cols])
        S.op("dve", lambda e: e.tensor_scalar(out=cols[:, 32:40], in0=mods[:, 16:24], scalar1=1.0, scalar2=None, op0=ALU.add), r=[r_mods], w=[r_cols])
        S.op("dve", lambda e: e.tensor_tensor(out=cols[:, 16:24], in0=lng[:, 0:8], in1=cols[:, 32:40], op=ALU.mult), r=[r_ln, r_cols], w=[r_cols])
        S.op("dve", lambda e: e.tensor_tensor(out=cols[:, 24:32], in0=lnb[:, 0:8], in1=cols[:, 32:40], op=ALU.mult), r=[r_ln, r_cols], w=[r_cols])
        S.op("dve", lambda e: e.tensor_tensor(out=cols[:, 24:32], in0=cols[:, 24:32], in1=mods[:, 8:16], op=ALU.add), r=[r_mods, r_cols], w=[r_cols])
        rc = [r_cols, r_ln]
        S.dma("pool", wo[:], wo_d.rearrange("(kc p) n -> p kc n", p=128), w=[r_wo], key="wo")
        xsrc = xT_d.rearrange("(kc p) t -> p kc t", p=128)
        if sel_d is None:
            ysrc_ = yT_d.rearrange("(kc p) t -> p kc t", p=128)
            ysrc_fn = lambda cc, a, n: ysrc_[:, cc, a:a + n]
        else:
            ysrc_fn = lambda cc, a, n: yT_d[cc % 4, cc // 4, :, a:a + n]
        osrc = oT_d.rearrange("(kc p) t -> p kc t", p=128)
        winr = win_d.rearrange("(kc p) n -> p kc n", p=128)
        woutr = wout_d.rearrange("(fc p) n -> p fc n", p=128)
        tails = []
        pbi = 0
        for half in range(NT // HALF):
            t0 = half * HALF
            if before_y is not None and half == 0:
                before_y(S)
            for cc in range(8):
                if sel_d is None:
                    S.dma("sp", yT[:, cc, :], ysrc_fn(cc, t0, HALF), w=[r_y[cc]], key="yT%d" % cc)
                else:
                    (ya, r_ya), (yb, r_yb) = ystg[cc % 2]
                    S.dma("sp", ya[:], ysrc_fn(cc, t0, HALF), w=[r_ya], key="ysa%d" % (cc % 2))
                    S.dma("sp", yb[:], ysrc_fn(cc, NT + t0, HALF), w=[r_yb], key="ysb%d" % (cc % 2))
                    S.op("dve", lambda e, ya=ya: e.tensor_scalar(out=ya[:], in0=ya[:], scalar1=selt[:, 0:1], scalar2=None, op0=ALU.mult), r=[r_ya, r_selt], w=[r_ya])
                    S.op("dve", lambda e, ya=ya, yb=yb, cc=cc: e.scalar_tensor_tensor(out=yT[:, cc, :], in0=yb[:], scalar=selt[:, 1:2], in1=ya[:], op0=ALU.mult, op1=ALU.add), r=[r_ya, r_yb, r_selt], w=[r_y[cc]])
                S.dma("sp", xT[:, cc, :], xsrc[:, cc, t0:t0 + HALF], w=[r_x[cc]], key="xT%d" % cc)
            for g in range(NG):
                n0 = g * 512
                for cc in range(8):
                    pb, r_pb = PB[pbi % 4]
                    pbi += 1
                    for kc in range(8):
                        S.op("pe", lambda e, pb=pb, cc=cc, kc=kc, n0=n0: e.matmul(pb[:], wo[:, kc, cc * 128:(cc + 1) * 128], yT[:, kc, n0:n0 + 512], start=(kc == 0), stop=(kc == 7)), r=[r_wo] + r_y, w=[r_pb])
                    S.op("dve", lambda e, pb=pb, cc=cc, n0=n0: e.scalar_tensor_tensor(out=xT[:, cc, n0:n0 + 512], in0=pb[:], scalar=cols[:, cc:cc + 1], in1=xT[:, cc, n0:n0 + 512], op0=ALU.mult, op1=ALU.add), r=[r_pb, r_x[cc], r_cols], w=[r_x[cc]])
                ln_group(C, xT, r_x, g, n0, None, None, rc,
                         [(xT, r_x, lng, 0, lnb, 0, "act"), (hT, r_h, cols, 16, cols, 24, "act")], P)
            if debug and half == 0:
                tails.append(S.dma("sp", dbg_cols, cols[:], r=[r_cols], key="dbg0"))
                tails.append(S.dma("sp", dbg_mods, mods[:], r=[r_mods], key="dbg1"))
                tails.append(S.dma("sp", dbg_x1.rearrange("(kc p) t -> p kc t", p=128), xT[:], r=r_x, key="dbg2"))
                tails.append(S.dma("sp", dbg_h.rearrange("(kc p) t -> p kc t", p=128), hT[:], r=r_h, key="dbg3"))
            for blk in range(NFC // 2):
                wt, r_wt = wi[blk % 2]
                c0 = blk * 256
                S.dma("pool", wt[:, :, 0:256], winr[:, :, c0:c0 + 256], w=[r_wt], key="wi%d" % (blk % 2))
                S.dma("pool", wt[:, :, 256:512], winr[:, :, DFF + c0:DFF + c0 + 256], w=[r_wt], key="wi%d" % (blk % 2))
                for g in range(NG):
                    n0 = g * 512
                    for j in range(2):
                        fc = blk * 2 + j
                        pg, r_pg = PB[pbi % 4]
                        pu, r_pu = PB[(pbi + 1) % 4]
                        pbi += 2
                        for kc in range(8):
                            S.op("pe", lambda e, pg=pg, wt=wt, j=j, kc=kc, n0=n0: e.matmul(pg[:], wt[:, kc, j * 128:(j + 1) * 128], hT[:, kc, n0:n0 + 512], start=(kc == 0), stop=(kc == 7)), r=[r_wt] + r_h, w=[r_pg])
                        for kc in range(8):
                            S.op("pe", lambda e, pu=pu, wt=wt, j=j, kc=kc, n0=n0: e.matmul(pu[:], wt[:, kc, 256 + j * 128:256 + (j + 1) * 128], hT[:, kc, n0:n0 + 512], start=(kc == 0), stop=(kc == 7)), r=[r_wt] + r_h, w=[r_pu])
                        k = (fc * NG + g) % 2
                        S.op("act", lambda e, pg=pg, k=k: e.activation(out=sl[k][:], in_=pg[:], func=AF.Silu), r=[r_pg], w=[r_sl[k]])
                        S.op("dve", lambda e, pu=pu, k=k, fc=fc, n0=n0: e.tensor_tensor(out=aT[:, fc, n0:n0 + 512], in0=pu[:], in1=sl[k][:], op=ALU.mult), r=[r_pu, r_sl[k]], w=[r_a[fc]])
            if debug and half == 0:
                tails.append(S.dma("sp", dbg_a.rearrange("(kc p) t -> p kc t", p=128), aT[:], r=r_a, key="dbg4"))
            for blk in range(4):
                wt, r_wt = wo2[blk % 2]
                S.dma("pool", wt[:], woutr[:, :, blk * 256:(blk + 1) * 256], w=[r_wt], key="wo2%d" % (blk % 2))
                for j in range(2):
                    cc = blk * 2 + j
                    for g in range(NG):
                        n0 = g * 512
                        pb, r_pb = PB[pbi % 4]
                        pbi += 1
                        for fc in range(NFC):
                            S.op("pe", lambda e, pb=pb, wt=wt, j=j, fc=fc, n0=n0: e.matmul(pb[:], wt[:, fc, j * 128:(j + 1) * 128], aT[:, fc, n0:n0 + 512], start=(fc == 0), stop=(fc == NFC - 1)), r=[r_wt, r_a[fc]], w=[r_pb])
                        S.op("dve", lambda e, pb=pb, cc=cc, n0=n0: e.scalar_tensor_tensor(out=xT[:, cc, n0:n0 + 512], in0=pb[:], scalar=cols[:, 8 + cc:9 + cc], in1=xT[:, cc, n0:n0 + 512], op0=ALU.mult, op1=ALU.add), r=[r_pb, r_x[cc], r_cols], w=[r_x[cc]])
            for g in range(NG):
                n0 = g * 512
                ln_group(C, xT, r_x, g, n0, None, None, rc, [(xT, r_x, lng, 8, lnb, 8, "act")], P)
            for cc in range(8):
                o = S.dma("sp", osrc[:, cc, t0:t0 + HALF], xT[:, cc, :], r=[r_x[cc]], w=[], key="oT%d" % cc)
                tails.append(o)
        if after is not None:
            after(C, tails)
        S.emit(st, tail_waits=tails)


T = 4096
NH = 8
HD = 64
QK_EPS = 1e-6
NEG = -30000.0
import os
LAG = int(os.environ.get("FOX_LAG", "2"))
FOX_DVE = int(os.environ.get("FOX_DVE", "1"))


def build_fox(TT=T, debug=False):
    nc = bass.Bass("TRN2", target_bir_lowering=False)
    dt = lambda n, s, d, k="ExternalInput": nc.dram_tensor(n, list(s), d, kind=k).ap()
    ins = fox_decl(dt, TT)
    yg_d = dt("ygT", [512, TT], BF16, "ExternalOutput")
    fox_phase(nc, "", ins, yg_d, TT, debug=debug)
    return nc


def fox_decl(dt, TT=T, sfx=""):
    return dict(
        x=dt("x" + sfx, [TT, D], F32), cT=dt("cT" + sfx, [128, 8], F32), adaw=dt("adaw" + sfx, [D, 2 * D], F32), adab=dt("adab" + sfx, [128, 16], F32),
        wq=dt("wq" + sfx, [D, 512], F32), wk=dt("wk" + sfx, [D, 512], F32), wv=dt("wv" + sfx, [D, 512], F32), wf=dt("wf" + sfx, [D, 8], F32), wg=dt("wg" + sfx, [D, 512], F32),
        bfb=dt("bfb" + sfx, [128, 8], F32), qgb=dt("qgb" + sfx, [128, 512], F32), kgb=dt("kgb" + sfx, [128, 512], F32))


def fox_phase(nc, pfx, ins, yg_d, TT=T, debug=False, after=None):
    dt = lambda n, s, d, k="ExternalInput": nc.dram_tensor(n, list(s), d, kind=k).ap()
    x_d, cT_d, adaw_d, adab_d = ins["x"], ins["cT"], ins["adaw"], ins["adab"]
    wq_d, wk_d, wv_d, wf_d, wg_d = ins["wq"], ins["wk"], ins["wv"], ins["wf"], ins["wg"]
    bf_d, qg_d, kg_d = ins["bfb"], ins["qgb"], ins["kgb"]
    NGR = TT // 512
    NTL = TT // 128
    if debug:
        dbg_h = dt("dbg_h", [D, 512], BF16, "ExternalOutput")
        dbg_q = dt("dbg_q", [128, 4, 512], BF16, "ExternalOutput")
        dbg_k = dt("dbg_k", [128, 4, TT], BF16, "ExternalOutput")
        dbg_F = dt("dbg_F", [128, NTL, 8], F32, "ExternalOutput")
        dbg_v = dt("dbg_v", [128, NTL, 8, 65], BF16, "ExternalOutput")
    with ExitStack() as st:
        C = Ctx(nc, st, pfx)
        S = C.S
        make_consts(C)
        identb = C.sb("identb", [128, 128], BF16)
        r_identb = C.R("identb")
        S.op("dve", lambda e: e.tensor_copy(out=identb[:], in_=C.ident[:]), r=[C.r_ident], w=[r_identb])
        tri = C.sb("tri", [128, 128], F32)
        r_tri = C.R("tri")
        S.op("pool", lambda e: e.memset(tri[:], 1.0), w=[r_tri])
        S.op("pool", lambda e: e.affine_select(out=tri[:], in_=tri[:], pattern=[[1, 128]], compare_op=ALU.is_ge, fill=0.0, base=0, channel_multiplier=-1), r=[r_tri], w=[r_tri])
        selA = C.sb("selA", [16, 8, 128], F32)
        selB = C.sb("selB", [16, 8, 128], F32)
        sel = C.sb("sel", [16, 8, 128], BF16)
        r_sel = C.R("sel")
        S.op("pool", lambda e: e.memset(selA[:], 1.0), w=[r_sel])
        S.op("pool", lambda e: e.memset(selB[:], 1.0), w=[r_sel])
        S.op("pool", lambda e: e.affine_select(out=selA[:], in_=selA[:], pattern=[[-1, 8], [0, 128]], compare_op=ALU.is_equal, fill=0.0, base=0, channel_multiplier=1), r=[r_sel], w=[r_sel])
        S.op("pool", lambda e: e.affine_select(out=selB[:], in_=selB[:], pattern=[[-1, 8], [0, 128]], compare_op=ALU.is_equal, fill=0.0, base=-8, channel_multiplier=1), r=[r_sel], w=[r_sel])
        S.op("dve", lambda e: e.tensor_tensor(out=sel[:], in0=selA[:], in1=selB[:], op=ALU.add), r=[r_sel], w=[r_sel])
        maskf = C.sb("maskf", [128, 4, 512], F32)
        maskb = C.sb("maskb", [128, 4, 512], BF16)
        r_mask = C.R("mask")
        S.op("pool", lambda e: e.memset(maskf[:], 0.0), w=[r_mask])
        for j in range(4):
            S.op("pool", lambda e, j=j: e.affine_select(out=maskf[:, j, :], in_=maskf[:, j, :], pattern=[[1, 512]], compare_op=ALU.is_ge, fill=NEG, base=-128 * j, channel_multiplier=-1), r=[r_mask], w=[r_mask])
        S.op("dve", lambda e: e.tensor_copy(out=maskb[:], in_=maskf[:]), r=[r_mask], w=[r_mask])
        PB = [(C.ps("pb%d" % i, [128, 512]), C.R("pb%d" % i, True)) for i in range(7)]
        pbb = C.ps("pbb", [128, 1024], BF16)
        r_pbb = C.R("pbb", True)
        wA = [(C.sb("wA%d" % i, [128, 8, 256], F32), C.R("wA%d" % i)) for i in range(2)]
        mods, r_mods = compute_mods(C, cT_d, adaw_d, adab_d, 2, wA, PB[6][0], PB[6][1])
        cols = C.sb("cols", [128, 8], F32)
        r_cols = C.R("cols")
        S.op("dve", lambda e: e.tensor_scalar(out=cols[:], in0=mods[:, 8:16], scalar1=1.0, scalar2=None, op0=ALU.add), r=[r_mods], w=[r_cols])
        wts = {}
        r_w = C.R("wts")
        for nm, d_ in (("wq", wq_d), ("wk", wk_d), ("wv", wv_d), ("wg", wg_d)):
            wts[nm] = C.sb(nm, [128, 8, 512], BF16)
            S.dma("pool", wts[nm][:], d_.rearrange("(kc p) n -> p kc n", p=128), w=[r_w], key=nm)
        wf = C.sb("wf", [128, 8, 8], BF16)
        S.dma("pool", wf[:], wf_d.rearrange("(kc p) n -> p kc n", p=128), w=[r_w], key="wf")
        bfb = C.sb("bfb", [128, 8], F32)
        qgb = C.sb("qgb", [128, 512], F32)
        kgb = C.sb("kgb", [128, 512], F32)
        r_par = C.R("par")
        S.dma("sp", bfb[:], bf_d, w=[r_par], key="bfb")
        S.dma("sp", qgb[:], qg_d, w=[r_par], key="qgb")
        S.dma("sp", kgb[:], kg_d, w=[r_par], key="kgb")
        xt = C.sb("xt", [128, 4, D], F32)
        r_xt = C.R("xt")
        hT = C.sb("hT", [128, 8, 512], BF16)
        r_hT = [C.R("hT%d" % k) for k in range(8)]
        kT = C.sb("kT", [128, 4, TT], BF16)
        r_kT = [C.R("kT%d" % g) for g in range(NGR)]
        qT = C.sb("qT", [128, 4, 512], BF16)
        r_qT = C.R("qT")
        Va = C.sb("Va", [128, NTL, NH, 65], BF16)
        r_V = [C.R("V%d" % g) for g in range(NGR)]
        r_Vones = C.R("Vones")
        S.op("pool", lambda e: e.memset(Va[:, :, :, 64:65], 1.0), w=[r_Vones])
        eg = C.sb("eg", [64, NH, 512], F32)
        r_eg = [C.R("eg%d" % h) for h in range(NH)]
        Fcol = C.sb("Fcol", [128, NTL, NH], F32)
        r_F = C.R("Fcol")
        carry = C.sb("carry", [128, NTL + 1, NH], F32)
        r_carry = C.R("carry")
        S.op("dve", lambda e: e.memset(carry[:, 0, :], 0.0), w=[r_carry])
        biasG = C.sb("biasG", [128, NTL, NH], F32)
        r_bias = C.R("biasG")
        FqT = C.sb("FqT", [16, 512], BF16)
        r_FqT = C.R("FqT")
        sq = C.sb("sq", [128, 512], F32); r_sq = C.R("sq")
        tmp = C.sb("tmp", [128, 512], F32); r_tmp = C.R("tmp")
        qtok = C.sb("qtok", [128, 512], BF16); r_qtok = C.R("qtok")
        ktok = C.sb("ktok", [128, 512], BF16); r_ktok = C.R("ktok")
        ssq = C.sb("ssq", [128, 16], F32); r_ssq = C.R("ssq")
        zf = C.sb("zf", [128, 8], F32); r_zf = C.R("zf")
        fq = C.sb("fq", [128, 8], F32); r_fq = C.R("fq")
        fq16 = C.sb("fq16", [128, 16], BF16); r_fq16 = C.R("fq16")
        PT = [(C.sb("PT%d" % i, [128, 512], BF16), C.R("PT%d" % i)) for i in range(4)]
        oT = C.sb("oT", [64, 512], F32); r_oT = C.R("oT")
        dn = C.sb("dn", [128, 512], F32); r_dn = C.R("dn")
        wv_ = C.sb("wv_", [64, 512], F32); r_wv_ = C.R("wv_")
        yg = [(C.sb("yg%d" % i, [64, 512], BF16), C.R("yg%d" % i)) for i in range(2)]
        tails = []
        xsrc = x_d.rearrange("(i p) d -> p i d", p=128)
        ps_i = 0
        for G in range(NGR):
            if G == 0:
                S.dma("sp", xt[:], xsrc[:, 0:4, :], w=[r_xt], key="xt")
            for kc in range(8):
                pb, r_pb = PB[kc % 2]
                for i in range(4):
                    S.op("pe", lambda e, pb=pb, i=i, kc=kc: e.transpose(pb[:, i * 128:(i + 1) * 128], xt[:, i, kc * 128:(kc + 1) * 128], C.ident[:]), r=[r_xt, C.r_ident], w=[r_pb])
                if kc % 2 == 0:
                    S.op("act", lambda e, pb=pb, kc=kc: e.activation(out=hT[:, kc, :], in_=pb[:], func=AF.Identity, scale=cols[:, kc:kc + 1], bias=mods[:, kc:kc + 1]), r=[r_pb, r_cols, r_mods], w=[r_hT[kc]])
                else:
                    S.op("dve", lambda e, pb=pb, kc=kc: e.tensor_scalar(out=hT[:, kc, :], in0=pb[:], scalar1=cols[:, kc:kc + 1], scalar2=mods[:, kc:kc + 1], op0=ALU.mult, op1=ALU.add), r=[r_pb, r_cols, r_mods], w=[r_hT[kc]])
            if debug and G == 0:
                tails.append(S.dma("sp", dbg_h.rearrange("(kc p) t -> p kc t", p=128), hT[:], r=r_hT, key="dbg0"))
            if G + 1 < NGR:
                S.dma("sp", xt[:], xsrc[:, 4 * (G + 1):4 * (G + 1) + 4, :], w=[r_xt], key="xt")
            for i in range(4):
                tl_ = 4 * G + i
                (pq, r_pq), (pk, r_pk), (pv, r_pv), (pf, r_pf) = PB[2], PB[3], PB[4], PB[5]
                for kc in range(8):
                    lw = hT[:, kc, i * 128:(i + 1) * 128]
                    S.op("pe", lambda e, lw=lw, kc=kc: e.matmul(pq[:], lw, wts["wq"][:, kc, :], start=(kc == 0), stop=(kc == 7)), r=[r_w, r_hT[kc]], w=[r_pq])
                    S.op("pe", lambda e, lw=lw, kc=kc: e.matmul(pk[:], lw, wts["wk"][:, kc, :], start=(kc == 0), stop=(kc == 7)), r=[r_w, r_hT[kc]], w=[r_pk])
                    S.op("pe", lambda e, lw=lw, kc=kc: e.matmul(pv[:], lw, wts["wv"][:, kc, :], start=(kc == 0), stop=(kc == 7)), r=[r_w, r_hT[kc]], w=[r_pv])
                    S.op("pe", lambda e, lw=lw, kc=kc: e.matmul(pf[:, 0:8], lw, wf[:, kc, :], start=(kc == 0), stop=(kc == 7)), r=[r_w, r_hT[kc]], w=[r_pf])
                for h in (2 * i, 2 * i + 1):
                    pb, r_pb = PB[h % 2]
                    for kc in range(8):
                        S.op("pe", lambda e, pb=pb, h=h, kc=kc: e.matmul(pb[0:64, :], wts["wg"][:, kc, h * 64:(h + 1) * 64], hT[:, kc, :], start=(kc == 0), stop=(kc == 7)), r=[r_w, r_hT[kc]], w=[r_pb])
                    S.op("act", lambda e, pb=pb, h=h: e.activation(out=eg[:, h, :], in_=pb[0:64, :], func=AF.Exp, scale=-1.0), r=[r_pb], w=[r_eg[h]])
                for which, (pp, r_pp), gb, tok, r_tok, eps_, mul_ in (("q", (pq, r_pq), qgb, qtok, r_qtok, 64.0 * QK_EPS, 1.0), ("k", (pk, r_pk), kgb, ktok, r_ktok, QK_EPS, 1.0 / 64.0)):
                    o8 = 0 if which == "q" else 8
                    S.op("act", lambda e, pp=pp: e.activation(out=sq[:], in_=pp[:], func=AF.Square), r=[r_pp], w=[r_sq])
                    S.op("dve", lambda e, o8=o8: e.tensor_reduce(out=ssq[:, o8:o8 + 8], in_=sq[:].rearrange("p (h d) -> p h d", d=64), axis=AX.X, op=ALU.add), r=[r_sq], w=[r_ssq])
                    S.op("dve", lambda e, o8=o8, eps_=eps_, mul_=mul_: e.tensor_scalar(out=ssq[:, o8:o8 + 8], in0=ssq[:, o8:o8 + 8], scalar1=mul_, scalar2=eps_, op0=ALU.mult, op1=ALU.add), r=[r_ssq], w=[r_ssq])
                    S.op("act", lambda e, o8=o8: e.activation(out=ssq[:, o8:o8 + 8], in_=ssq[:, o8:o8 + 8], func=AF.Ln), r=[r_ssq], w=[r_ssq])
                    S.op("act", lambda e, o8=o8: e.activation(out=ssq[:, o8:o8 + 8], in_=ssq[:, o8:o8 + 8], func=AF.Exp, scale=-0.5), r=[r_ssq], w=[r_ssq])
                    S.op("dve", lambda e, pp=pp, o8=o8: e.tensor_tensor(out=tmp[:].rearrange("p (h d) -> p h d", d=64), in0=pp[:].rearrange("p (h d) -> p h d", d=64), in1=ssq[:, o8:o8 + 8].unsqueeze(2).to_broadcast([128, 8, 64]), op=ALU.mult), r=[r_pp, r_ssq], w=[r_tmp])
                    S.op("dve", lambda e, gb=gb, tok=tok: e.tensor_tensor(out=tok[:], in0=tmp[:], in1=gb[:], op=ALU.mult), r=[r_tmp, r_par], w=[r_tok])
                S.op("act", lambda e, tl_=tl_: e.copy(out=Va[:, tl_, :, 0:64], in_=pv[:].rearrange("p (h d) -> p h d", d=64)), r=[r_pv, r_Vones], w=[r_V[G]])
                S.op("dve", lambda e: e.tensor_tensor(out=zf[:], in0=pf[:, 0:8], in1=bfb[:], op=ALU.add), r=[r_pf, r_par], w=[r_zf])
                S.op("act", lambda e: e.activation(out=zf[:], in_=zf[:], func=AF.Exp, scale=-1.0), r=[r_zf], w=[r_zf])
                S.op("act", lambda e: e.activation(out=zf[:], in_=zf[:], func=AF.Ln, bias=1.0), r=[r_zf], w=[r_zf])
                pc, r_pc = PB[6]
                S.op("pe", lambda e: e.matmul(pc[:, 0:8], tri[:], zf[:], start=True, stop=True), r=[r_tri, r_zf], w=[r_pc])
                S.op("pe", lambda e: e.matmul(pc[:, 8:16], C.ones[:], zf[:], start=True, stop=True), r=[C.r_ones, r_zf], w=[r_pc])
                S.op("dve", lambda e, tl_=tl_: e.tensor_tensor(out=Fcol[:, tl_, :], in0=carry[:, tl_, :], in1=pc[:, 0:8], op=ALU.subtract), r=[r_carry, r_pc], w=[r_F])
                S.op("dve", lambda e, tl_=tl_: e.tensor_tensor(out=carry[:, tl_ + 1, :], in0=carry[:, tl_, :], in1=pc[:, 8:16], op=ALU.subtract), r=[r_carry, r_pc], w=[r_carry])
                S.op("dve", lambda e, tl_=tl_, G=G: e.tensor_tensor(out=fq[:], in0=Fcol[:, tl_, :], in1=carry[:, 4 * G, :], op=ALU.subtract), r=[r_F, r_carry], w=[r_fq])
                S.op("dve", lambda e: e.tensor_copy(out=fq16[:, 0:8], in_=fq[:]), r=[r_fq], w=[r_fq16])
                S.op("dve", lambda e: e.tensor_tensor(out=fq[:], in0=fq[:], in1=fq16[:, 0:8], op=ALU.subtract), r=[r_fq, r_fq16], w=[r_fq])
                S.op("dve", lambda e: e.tensor_copy(out=fq16[:, 8:16], in_=fq[:]), r=[r_fq], w=[r_fq16])
                for p in range(4):
                    S.op("pe", lambda e, p=p: e.transpose(pbb[:, p * 128:(p + 1) * 128], qtok[:, p * 128:(p + 1) * 128], identb[:]), r=[r_qtok, r_identb], w=[r_pbb])
                for p in range(4):
                    S.op("pe", lambda e, p=p: e.transpose(pbb[:, 512 + p * 128:512 + (p + 1) * 128], ktok[:, p * 128:(p + 1) * 128], identb[:]), r=[r_ktok, r_identb], w=[r_pbb])
                S.op("act", lambda e, i=i: e.copy(out=qT[:, :, i * 128:(i + 1) * 128], in_=pbb[:, 0:512].rearrange("p (a t) -> p a t", t=128)), r=[r_pbb], w=[r_qT])
                S.op("dve", lambda e, tl_=tl_: e.tensor_copy(out=kT[:, :, tl_ * 128:(tl_ + 1) * 128], in_=pbb[:, 512:1024].rearrange("p (a t) -> p a t", t=128)), r=[r_pbb], w=[r_kT[G]])
                S.op("pe", lambda e: e.transpose(pbb[0:16, 0:128], fq16[:, 0:16], identb[:]), r=[r_fq16, r_identb], w=[r_pbb])
                S.op("dve", lambda e, i=i: e.tensor_copy(out=FqT[:, i * 128:(i + 1) * 128], in_=pbb[0:16, 0:128]), r=[r_pbb], w=[r_FqT])
            nkb = 4 * G + 4
            for h in range(NH):
                S.op("dve", lambda e, h=h, nkb=nkb, G=G: e.tensor_scalar(out=biasG[:, 0:nkb, h], in0=Fcol[:, 0:nkb, h], scalar1=-1.0, scalar2=carry[:, 4 * G, h:h + 1], op0=ALU.mult, op1=ALU.add), r=[r_F, r_carry], w=[r_bias])
            tiles = [(2 * p + e, kb) for p in range(NH // 2) for kb in range(nkb) for e in range(2)]

            def emit_pv(h, kb, PTt, r_PT, nkb=nkb, G=G):
                pO, r_pO = PB[4 + h % 2]
                gk = kb // 4
                S.op("pe", lambda e, pO=pO, PTt=PTt, kb=kb, h=h, nkb=nkb: e.matmul(pO[0:65, :], Va[:, kb, h, :], PTt[:], start=(kb == 0), stop=(kb == nkb - 1)), r=[r_V[gk], r_Vones, r_PT], w=[r_pO])
                if kb == nkb - 1:
                    pB_, r_pB = PB[6]
                    ygt, r_yg = yg[h % 2]
                    S.op("act", lambda e, pO=pO: e.copy(out=oT[:], in_=pO[0:64, :]), r=[r_pO], w=[r_oT])
                    S.op("dve", lambda e, pO=pO: e.tensor_copy(out=dn[64:65, :], in_=pO[64:65, :]), r=[r_pO], w=[r_dn])
                    S.op("pe", lambda e: e.matmul(pB_[0:64, :], C.ones[64:65, 0:64], dn[64:65, :], start=True, stop=True), r=[C.r_ones, r_dn], w=[r_pB])
                    S.op("dve", lambda e, h=h: e.scalar_tensor_tensor(out=wv_[:], in0=eg[:, h, :], scalar=1.0, in1=pB_[0:64, :], op0=ALU.add, op1=ALU.mult), r=[r_eg[h], r_pB], w=[r_wv_])
                    S.op("dve", lambda e: e.reciprocal(out=wv_[:], in_=wv_[:]), r=[r_wv_], w=[r_wv_])
                    S.op("dve", lambda e, ygt=ygt: e.tensor_tensor(out=ygt[:], in0=oT[:], in1=wv_[:], op=ALU.mult), r=[r_oT, r_wv_], w=[r_yg])
                    tails.append(S.dma("sp", yg_d[h * 64:(h + 1) * 64, G * 512:(G + 1) * 512], ygt[:], r=[r_yg], key="yg%d" % (h % 2)))

            pendq = []
            for (h, kb) in tiles:
                p_, e_ = h // 2, h % 2
                pS, r_pS = PB[ps_i % 4]
                PTt, r_PT = PT[ps_i % 4]
                ps_i += 1
                diag = kb >= 4 * G
                gk = kb // 4
                if FOX_DVE:
                    fqb, r_fqb = ((tmp, r_tmp), (sq, r_sq))[h % 2]
                    if kb == 0:
                        pB_, r_pB = PB[6]
                        S.op("pe", lambda e, h=h: e.matmul(pB_[:], sel[:, h, :], FqT[:], start=True, stop=True), r=[r_sel, r_FqT], w=[r_pB])
                        S.op("dve", lambda e, fqb=fqb: e.tensor_copy(out=fqb[:], in_=pB_[:]), r=[r_pB], w=[r_fqb])
                    S.op("pe", lambda e, pS=pS, p_=p_, e_=e_, kb=kb, diag=diag: e.matmul(pS[:], kT[e_ * 64:(e_ + 1) * 64, p_, kb * 128:(kb + 1) * 128], qT[e_ * 64:(e_ + 1) * 64, p_, :], start=True, stop=(not diag)), r=[r_kT[gk], r_qT], w=[r_pS])
                    if diag:
                        S.op("pe", lambda e, pS=pS, kb=kb, G=G: e.matmul(pS[:], identb[:], maskb[:, kb - 4 * G, :], start=False, stop=True), r=[r_identb, r_mask], w=[r_pS])
                    S.op("dve", lambda e, pS=pS, fqb=fqb: e.tensor_tensor(out=pS[:], in0=pS[:], in1=fqb[:], op=ALU.add), r=[r_pS, r_fqb], w=[r_pS])
                else:
                    S.op("pe", lambda e, pS=pS, p_=p_, e_=e_, kb=kb: e.matmul(pS[:], kT[e_ * 64:(e_ + 1) * 64, p_, kb * 128:(kb + 1) * 128], qT[e_ * 64:(e_ + 1) * 64, p_, :], start=True, stop=False), r=[r_kT[gk], r_qT], w=[r_pS])
                    S.op("pe", lambda e, pS=pS, h=h, diag=diag: e.matmul(pS[:], sel[:, h, :], FqT[:], start=False, stop=(not diag)), r=[r_sel, r_FqT], w=[r_pS])
                    if diag:
                        S.op("pe", lambda e, pS=pS, kb=kb, G=G: e.matmul(pS[:], identb[:], maskb[:, kb - 4 * G, :], start=False, stop=True), r=[r_identb, r_mask], w=[r_pS])
                S.op("act", lambda e, pS=pS, PTt=PTt, kb=kb, h=h: e.activation(out=PTt[:], in_=pS[:], func=AF.Exp, bias=biasG[:, kb, h:h + 1], scale=1.0), r=[r_pS, r_bias], w=[r_PT])
                pendq.append((h, kb, PTt, r_PT))
                if h % 2 == 1:
                    while len(pendq) > 2:
                        emit_pv(*pendq.pop(0))
            while pendq:
                emit_pv(*pendq.pop(0))
            if debug and G == 0:
                tails.append(S.dma("sp", dbg_q, qT[:], r=[r_qT], key="dbg1"))
        if debug:
            tails.append(S.dma("sp", dbg_k, kT[:], r=r_kT, key="dbg2"))
            tails.append(S.dma("sp", dbg_F, Fcol[:], r=[r_F], key="dbg3"))
            tails.append(S.dma("sp", dbg_v, Va[:], r=r_V + [r_Vones], key="dbg4"))
        if after is not None:
            after(C, tails)
        S.emit(st, tail_waits=tails)

import math, os
SEQ_VAR = 0

NH = 8
HD = 64
C0 = math.exp(-0.5)
GN_EPS = 64 * 1e-5


class TR:
    def __init__(self, C, name, shape, dt, psum=False):
        self.t = C.ps(name, shape, dt) if psum else C.sb(name, shape, dt)
        self.r = C.R(name, excl=psum)


class _Stop(Exception):
    pass


def build_rwkv(TT=4096, debug=False, stop=None):
    nc = bass.Bass("TRN2", target_bir_lowering=False)
    dt = lambda n, s, d, k="ExternalInput": nc.dram_tensor(n, list(s), d, kind=k).ap()
    xT_d = dt("xT", [D, TT], F32)
    ins = rwkv_decl(dt)
    yT_d = dt("yT", [512, TT], BF16, "ExternalOutput")
    xv = xT_d.rearrange("(kc p) t -> p kc t", p=128)
    rwkv_phase(nc, "", lambda t0, n: xv[:, :, t0:t0 + n], ins, yT_d, TT, debug=debug, stop=stop)
    return nc


RW_BC = ["w0b", "a0b", "kkb", "kab", "rkb", "lngb", "lnbb"]


def rwkv_decl(dt, sfx=""):
    d_ = dict(cT=dt("cT" + sfx, [128, 8], F32), adaw=dt("adaw" + sfx, [D, 2 * D], F32), adab=dt("adab" + sfx, [128, 16], F32), mu=dt("mu" + sfx, [128, 48], F32),
              wr=dt("wr" + sfx, [D, 512], F32), wk=dt("wk" + sfx, [D, 512], F32), wv=dt("wv" + sfx, [D, 512], F32),
              w1=dt("w1" + sfx, [D, 64], F32), a1=dt("a1" + sfx, [D, 64], F32), g1=dt("g1" + sfx, [D, 128], F32),
              w2=dt("w2" + sfx, [64, 512], F32), a2=dt("a2" + sfx, [64, 512], F32), g2=dt("g2" + sfx, [128, 512], F32))
    for n in RW_BC:
        d_[n] = dt(n + sfx, [128, 512], F32)
    return d_


def rwkv_phase(nc, pfx, xsrc_fn, ins, yT_d, TT=4096, debug=False, stop=None, after=None, before_x=None):
    dt = lambda n, s, d, k="ExternalInput": nc.dram_tensor(n, list(s), d, kind=k).ap()
    cT_d, adaw_d, adab_d, mu_d = ins["cT"], ins["adaw"], ins["adab"], ins["mu"]
    wr_d, wk_d, wv_d, w1_d, a1_d, g1_d, w2_d, a2_d, g2_d = [ins[k] for k in ("wr", "wk", "wv", "w1", "a1", "g1", "w2", "a2", "g2")]
    bc_names = RW_BC
    bc_d = {n: ins[n] for n in bc_names}
    GS = 256
    NTG = GS // 128
    NGR = TT // GS
    dbg = {}
    if debug:
        for n in ["r", "kp", "kkn", "a", "sw", "Y", "bonus", "Gm", "yn"]:
            dbg[n] = dt("dbg_" + n, [128, 512], F32, "ExternalOutput")
        dbg["S"] = dt("dbg_S", [128, 4, 64], F32, "ExternalOutput")
        for n, shp in (("Arb", [128, 8, 128]), ("Aak", [128, 8, 128]), ("NT0", [128, 8, 128]), ("Z0", [128, 8, 128]), ("Zf", [128, 8, 128]), ("Y0", [128, 512]), ("RbT", [128, 4, 128]), ("Qm", [128, 4, 128]), ("Hm", [128, 4, 128]), ("gcol", [128, 8]), ("v", [128, 512]), ("ART", [128, 4, 2, 128]), ("KT", [128, 4, 128]), ("BT", [128, 4, 128]), ("Bh", [128, 512]), ("Kh", [128, 512])):
            dbg[n] = dt("dbg_" + n, shp, F32, "ExternalOutput")
    with ExitStack() as st:
        C = Ctx(nc, st, pfx)
        S = C.S
        make_consts(C)
        ident, ones = C.ident, C.ones
        r_ident, r_ones = C.r_ident, C.r_ones
        tails = []
        identb = TR(C, "identb", [128, 128], BF16)
        S.op("dve", lambda e: e.tensor_copy(out=identb.t[:], in_=ident[:]), r=[r_ident], w=[identb.r])
        U = TR(C, "U", [128, 128], F32)
        IU = TR(C, "IUf", [128, 128], F32)
        S.op("pool", lambda e: e.memset(U.t[:], 1.0), w=[U.r])
        S.op("pool", lambda e: e.affine_select(out=U.t[:], in_=U.t[:], pattern=[[1, 128]], compare_op=ALU.is_gt, fill=0.0, base=0, channel_multiplier=-1), r=[U.r], w=[U.r])
        S.op("pool", lambda e: e.memset(IU.t[:], 1.0), w=[IU.r])
        S.op("pool", lambda e: e.affine_select(out=IU.t[:], in_=IU.t[:], pattern=[[1, 128]], compare_op=ALU.is_ge, fill=0.0, base=0, channel_multiplier=-1), r=[IU.r], w=[IU.r])
        SLf = TR(C, "SLf", [128, 128], F32)
        S.op("pool", lambda e: e.memset(SLf.t[:], 1.0), w=[SLf.r])
        S.op("pool", lambda e: e.affine_select(out=SLf.t[:], in_=SLf.t[:], pattern=[[-1, 128]], compare_op=ALU.is_gt, fill=0.0, base=0, channel_multiplier=1), r=[SLf.r], w=[SLf.r])
        mask1 = TR(C, "mask1", [128, 128], F32)
        maskL = TR(C, "maskL", [128, 64], F32)
        idl = TR(C, "idl", [128, 64], F32)
        for ch in range(2):
            ps_ = slice(ch * 64, ch * 64 + 64)
            S.op("dve", lambda e, ps_=ps_: e.tensor_copy(out=mask1.t[ps_, 0:64], in_=U.t[ps_, ps_]), r=[U.r], w=[mask1.r])
            S.op("dve", lambda e, ps_=ps_: e.tensor_copy(out=mask1.t[ps_, 64:128], in_=IU.t[ps_, ps_]), r=[IU.r], w=[mask1.r])
            S.op("dve", lambda e, ps_=ps_: e.tensor_copy(out=maskL.t[ps_, :], in_=SLf.t[ps_, ps_]), r=[SLf.r], w=[maskL.r])
            S.op("dve", lambda e, ps_=ps_: e.tensor_copy(out=idl.t[ps_, :], in_=ident[ps_, ps_]), r=[r_ident], w=[idl.r])
        triC = TR(C, "triC", [128, 128], F32)
        blkC = TR(C, "blkC", [128, 128], F32)
        S.op("dve", lambda e: e.tensor_scalar(out=triC.t[:], in0=IU.t[:], scalar1=-C0, scalar2=None, op0=ALU.mult), r=[IU.r], w=[triC.r])
        S.op("dve", lambda e: e.memset(triC.t[0:64, 64:128], 0.0), w=[triC.r])
        S.op("dve", lambda e: e.memset(blkC.t[:], -C0), w=[blkC.r])
        S.op("dve", lambda e: e.memset(blkC.t[0:64, 64:128], 0.0), w=[blkC.r])
        S.op("dve", lambda e: e.memset(blkC.t[64:128, 0:64], 0.0), w=[blkC.r])
        avg = TR(C, "avg", [128, 1], F32)
        S.op("dve", lambda e: e.memset(avg.t[:], 1.0 / 64.0), w=[avg.r])
        PB = [TR(C, "pb%d" % i, [128, 512], F32, psum=True) for i in range(7)]
        pbb = TR(C, "pbb", [128, 1024], BF16, psum=True)
        mu = TR(C, "mu", [128, 48], F32)
        S.dma("sp", mu.t[:], mu_d, w=[mu.r], key="mu")
        W = {}
        r_w = C.R("wts")
        for nm, d_, shp in (("wr", wr_d, [128, 8, 512]), ("wk", wk_d, [128, 8, 512]), ("wv", wv_d, [128, 8, 512]), ("w1", w1_d, [128, 8, 64]), ("a1", a1_d, [128, 8, 64]), ("g1", g1_d, [128, 8, 128])):
            W[nm] = C.sb(nm, shp, BF16)
            S.dma("pool", W[nm][:], d_.rearrange("(kc p) n -> p kc n", p=128), w=[r_w], key=nm)
        for nm, d_, shp in (("w2", w2_d, [64, 512]), ("a2", a2_d, [64, 512]), ("g2", g2_d, [128, 512])):
            W[nm] = C.sb(nm, shp, BF16)
            S.dma("pool", W[nm][:], d_, w=[r_w], key=nm)
        BC = {}
        r_bc = C.R("bc")
        for n in bc_names:
            BC[n] = C.sb(n, [128, 512], F32)
            S.dma("sp", BC[n][:], bc_d[n], w=[r_bc], key=n)
        xg = TR(C, "xg", [128, 8, GS + 1], F32)
        S.op("dve", lambda e: e.memset(xg.t[:, :, 0:1], 0.0), w=[xg.r])
        Nm = [TR(C, "Nm%d" % i, [128, 8, 128], BF16) for i in range(2)]
        NTm = [TR(C, "NTm%d" % i, [128, 8, 128], BF16) for i in range(2)]
        for t_ in Nm + NTm:
            S.op("pool", lambda e, t_=t_: e.memset(t_.t[:], 0.0), w=[t_.r])
        ST = [TR(C, "ST%d" % i, [128, 4, 64], F32) for i in range(2)]
        S.op("dve", lambda e: e.memset(ST[0].t[:], 0.0), w=[ST[0].r])
        mods, r_mods = compute_mods(C, cT_d, adaw_d, adab_d, 2, None, PB[6].t, PB[6].r)
        cols = TR(C, "cols", [128, 8], F32)
        S.op("dve", lambda e: e.tensor_scalar(out=cols.t[:], in0=mods[:, 8:16], scalar1=1.0, scalar2=None, op0=ALU.add), r=[r_mods], w=[cols.r])
        dx = TR(C, "dx", [128, 8, GS], F32)
        xs = [TR(C, "xs%d" % i, [128, 8, GS], BF16) for i in range(2)]
        l1 = [TR(C, "l1_%d" % i, [128, GS], BF16) for i in range(3)]
        l1f = TR(C, "l1f", [128, GS], F32)
        G_ = {n: TR(C, "G_" + n, [128, NTG, 512], F32) for n in ("r", "k", "v")}
        names = ["sw", "Gm", "Gi", "Gp", "GC", "a", "kkn", "kp", "t1", "t2", "Bt", "Kt", "Bh", "Kh", "bonus", "Y", "yn", "gg", "At", "Rt", "glg", "glg2", "bonus2", "t3", "At2", "Rt2", "Kt2", "Bt2", "Bh2", "Kh2", "GC2", "V0", "V2"]
        X = {n: TR(C, "X_" + n, [128, 512], F32) for n in names if n not in ("Bh", "Kh", "Bh2", "Kh2")}
        for n in ("Bh", "Kh", "Bh2", "Kh2", "Vb0", "Vb2"):
            X[n] = TR(C, "X_" + n, [128, 512], BF16)
        Zfb = TR(C, "Zfb", [128, 8, 128], BF16)
        Arbb = TR(C, "Arbb", [128, 8, 64], BF16)
        sm = TR(C, "sm", [128, 64], F32)
        smT = TR(C, "smT", [128, 64], F32)
        Z = [TR(C, "Z%d" % i, [128, 8, 128], BF16) for i in range(2)]
        Zf32 = TR(C, "Zf32", [128, 8, 128], F32)
        ART = TR(C, "ART", [128, 4, 2, 128], F32)
        KT = TR(C, "KT", [128, 4, 128], F32)
        BT = TR(C, "BT", [128, 4, 128], F32)
        Arb = TR(C, "Arb", [128, 8, 128], BF16)
        Aak = TR(C, "Aak", [128, 8, 128], BF16)
        RbT = TR(C, "RbT", [128, 4, 128], F32)
        Qm = TR(C, "Qm", [128, 4, 128], F32)
        Hm = TR(C, "Hm", [128, 4, 128], F32)
        gcol = TR(C, "gcol", [128, 8], F32)
        Y0 = TR(C, "Y0", [128, 512], F32)
        ybf = TR(C, "ybf", [128, 512], BF16)
        yTo = [TR(C, "yTo%d" % i, [128, 4, 128], BF16) for i in range(2)]
        ysrc = yT_d.rearrange("(a p) t -> p a t", p=128)

        def hv(t):
            return t.rearrange("p (h d) -> p h d", d=64)

        def bc8(ap):
            return ap.unsqueeze(2).to_broadcast([128, 8, 64])

        sti = [0]

        def tile_front_a(i, G):
            tl_ = G * NTG + i
            r_ = G_["r"].t[:, i, :]; k_ = G_["k"].t[:, i, :]; v_ = G_["v"].t[:, i, :]
            rG = [G_[n].r for n in G_]
            sw, Gm, Gi, Gp, a_, kkn, kp, t1, t2 = [X[n] for n in ("sw", "Gm", "Gi", "Gp", "a", "kkn", "kp", "t1", "t2")]
            Yt, yn = X["Y"], X["yn"]
            sfx = "" if tl_ % 2 == 0 else "2"
            Bt, Kt, Bh, Kh, GC, At_, Rt = [X[n + sfx] if (n + sfx) in X else X[n] for n in ("Bt", "Kt", "Bh", "Kh", "GC", "At", "Rt")]
            Vt = X["V0"] if tl_ % 2 == 0 else X["V2"]
            Vb = X["Vb0"] if tl_ % 2 == 0 else X["Vb2"]
            bonus = X["bonus"] if tl_ % 2 == 0 else X["bonus2"]
            glg = X["glg"] if tl_ % 2 == 0 else X["glg2"]
            t3 = X["t3"]
            pzw, pza, pgg = PB[4], PB[5], PB[6]
            for (pz_, lt, w2n, rows) in ((pzw, l1[0], "w2", 64), (pza, l1[1], "a2", 64), (pgg, l1[2], "g2", 128)):
                S.op("pe", lambda e, pz_=pz_, lt=lt, w2n=w2n, rows=rows, i=i: e.matmul(pz_.t[:], lt.t[0:rows, i * 128:(i + 1) * 128], W[w2n][0:rows, :], start=True, stop=True), r=[lt.r, r_w], w=[pz_.r])
            S.op("act", lambda e: e.copy(out=X["gg"].t[:], in_=pgg.t[:]), r=[pgg.r], w=[X["gg"].r])
            S.op("dve", lambda e: e.tensor_tensor(out=t1.t[:], in0=pzw.t[:], in1=BC["w0b"][:], op=ALU.add), r=[pzw.r, r_bc], w=[t1.r])
            S.op("dve", lambda e: e.tensor_tensor(out=t2.t[:], in0=pza.t[:], in1=BC["a0b"][:], op=ALU.add), r=[pza.r, r_bc, t2.r], w=[t2.r])
            S.op("act", lambda e: e.activation(out=sw.t[:], in_=t1.t[:], func=AF.Sigmoid), r=[t1.r], w=[sw.r])
            S.op("act", lambda e: e.activation(out=a_.t[:], in_=t2.t[:], func=AF.Sigmoid), r=[t2.r], w=[a_.r])

        def tile_front_b(i, G):
            tl_ = G * NTG + i
            r_ = G_["r"].t[:, i, :]; k_ = G_["k"].t[:, i, :]; v_ = G_["v"].t[:, i, :]
            rG = [G_[n].r for n in G_]
            sw, Gm, Gi, Gp, a_, kkn, kp, t1, t2 = [X[n] for n in ("sw", "Gm", "Gi", "Gp", "a", "kkn", "kp", "t1", "t2")]
            Yt, yn = X["Y"], X["yn"]
            sfx = "" if tl_ % 2 == 0 else "2"
            Bt, Kt, Bh, Kh, GC, At_, Rt = [X[n + sfx] if (n + sfx) in X else X[n] for n in ("Bt", "Kt", "Bh", "Kh", "GC", "At", "Rt")]
            Vt = X["V0"] if tl_ % 2 == 0 else X["V2"]
            Vb = X["Vb0"] if tl_ % 2 == 0 else X["Vb2"]
            bonus = X["bonus"] if tl_ % 2 == 0 else X["bonus2"]
            glg = X["glg"] if tl_ % 2 == 0 else X["glg2"]
            t3 = X["t3"]
            pL, pLC = PB[2], PB[3]
            S.op("pe", lambda e: e.matmul(pL.t[:], triC.t[:], sw.t[:], start=True, stop=True), r=[triC.r, sw.r], w=[pL.r])
            S.op("pe", lambda e: e.matmul(pLC.t[:], blkC.t[:], sw.t[:], start=True, stop=True), r=[blkC.r, sw.r], w=[pLC.r])
            S.op("act", lambda e: e.activation(out=Gm.t[:], in_=pL.t[:], func=AF.Exp), r=[pL.r], w=[Gm.r])
            S.op("act", lambda e: e.activation(out=Gi.t[:], in_=pL.t[:], func=AF.Exp, scale=-1.0), r=[pL.r], w=[Gi.r])
            S.op("dve", lambda e: e.scalar_tensor_tensor(out=Gp.t[:], in0=sw.t[:], scalar=C0, in1=pL.t[:], op0=ALU.mult, op1=ALU.add), r=[sw.r, pL.r], w=[Gp.r])
            S.op("act", lambda e: e.activation(out=Gp.t[:], in_=Gp.t[:], func=AF.Exp), r=[Gp.r], w=[Gp.r])
            S.op("act", lambda e: e.activation(out=GC.t[:], in_=pLC.t[:], func=AF.Exp), r=[pLC.r], w=[GC.r])
            S.op("dve", lambda e, k_=k_: e.tensor_tensor(out=kkn.t[:], in0=k_, in1=BC["kkb"][:], op=ALU.mult), r=rG + [r_bc], w=[kkn.r])
            S.op("act", lambda e: e.activation(out=t2.t[:], in_=kkn.t[:], func=AF.Square), r=[kkn.r], w=[t2.r])
            S.op("dve", lambda e: e.tensor_reduce(out=sm.t[:, 0:8], in_=hv(t2.t[:]), axis=AX.X, op=ALU.add), r=[t2.r], w=[sm.r])
            S.op("dve", lambda e: e.tensor_scalar(out=sm.t[:, 0:8], in0=sm.t[:, 0:8], scalar1=1e-24, scalar2=None, op0=ALU.max), r=[sm.r], w=[sm.r])
            S.op("act", lambda e: e.activation(out=sm.t[:, 0:8], in_=sm.t[:, 0:8], func=AF.Sqrt), r=[sm.r], w=[sm.r])
            S.op("dve", lambda e: e.reciprocal(out=sm.t[:, 0:8], in_=sm.t[:, 0:8]), r=[sm.r], w=[sm.r])
            S.op("dve", lambda e: e.tensor_tensor(out=hv(kkn.t[:]), in0=hv(kkn.t[:]), in1=bc8(sm.t[:, 0:8]), op=ALU.mult), r=[kkn.r, sm.r], w=[kkn.r])
            S.op("dve", lambda e: e.scalar_tensor_tensor(out=t2.t[:], in0=a_.t[:], scalar=-1.0, in1=BC["kab"][:], op0=ALU.add, op1=ALU.mult), r=[a_.r, r_bc, t2.r], w=[t2.r])
            S.op("dve", lambda e, k_=k_: e.scalar_tensor_tensor(out=kp.t[:], in0=t2.t[:], scalar=1.0, in1=k_, op0=ALU.add, op1=ALU.mult), r=[t2.r] + rG, w=[kp.r])
            S.op("dve", lambda e: e.scalar_tensor_tensor(out=At_.t[:], in0=kkn.t[:], scalar=-1.0, in1=Gp.t[:], op0=ALU.mult, op1=ALU.mult), r=[kkn.r, Gp.r], w=[At_.r])
            S.op("pool", lambda e: e.tensor_tensor(out=Bt.t[:], in0=kkn.t[:], in1=a_.t[:], op=ALU.mult), r=[kkn.r, a_.r], w=[Bt.r])
            S.op("pool", lambda e: e.tensor_tensor(out=Bt.t[:], in0=Bt.t[:], in1=Gi.t[:], op=ALU.mult), r=[Bt.r, Gi.r], w=[Bt.r])
            S.op("dve", lambda e: e.tensor_tensor(out=Kt.t[:], in0=kp.t[:], in1=Gi.t[:], op=ALU.mult), r=[kp.r, Gi.r], w=[Kt.r])
            S.op("dve", lambda e, r_=r_: e.tensor_tensor(out=Rt.t[:], in0=r_, in1=Gm.t[:], op=ALU.mult), r=rG + [Gm.r], w=[Rt.r])
            S.op("pool", lambda e: e.tensor_tensor(out=Bh.t[:], in0=Bt.t[:], in1=GC.t[:], op=ALU.mult), r=[Bt.r, GC.r], w=[Bh.r])
            S.op("pool", lambda e: e.tensor_tensor(out=Kh.t[:], in0=Kt.t[:], in1=GC.t[:], op=ALU.mult), r=[Kt.r, GC.r], w=[Kh.r])
            S.op("dve", lambda e, r_=r_: e.tensor_tensor(out=t2.t[:], in0=r_, in1=kp.t[:], op=ALU.mult), r=rG + [kp.r, t2.r], w=[t2.r])
            S.op("dve", lambda e: e.tensor_tensor(out=t2.t[:], in0=t2.t[:], in1=BC["rkb"][:], op=ALU.mult), r=[t2.r, r_bc], w=[t2.r])
            S.op("dve", lambda e: e.tensor_reduce(out=sm.t[:, 8:16], in_=hv(t2.t[:]), axis=AX.X, op=ALU.add), r=[t2.r], w=[sm.r])
            S.op("dve", lambda e, v_=v_, bonus=bonus: e.tensor_tensor(out=hv(bonus.t[:]), in0=hv(v_), in1=bc8(sm.t[:, 8:16]), op=ALU.mult), r=rG + [sm.r], w=[bonus.r])
            S.op("pool", lambda e, bonus=bonus: e.tensor_tensor(out=bonus.t[:], in0=bonus.t[:], in1=BC["lnbb"][:], op=ALU.add), r=[bonus.r, r_bc], w=[bonus.r])
            S.op("pool", lambda e, bonus=bonus: e.tensor_tensor(out=bonus.t[:], in0=bonus.t[:], in1=X["gg"].t[:], op=ALU.mult), r=[bonus.r, X["gg"].r], w=[bonus.r])
            S.op("pool", lambda e, glg=glg: e.tensor_tensor(out=glg.t[:], in0=X["gg"].t[:], in1=BC["lngb"][:], op=ALU.mult), r=[X["gg"].r, r_bc], w=[glg.r])
            if stop == 2:
                raise _Stop()
            S.op("pool", lambda e: e.tensor_copy(out=Vb.t[:], in_=v_), r=rG, w=[Vb.r])

        def tile_back(i, G, hook=None):
            tl_ = G * NTG + i
            r_ = G_["r"].t[:, i, :]; k_ = G_["k"].t[:, i, :]; v_ = G_["v"].t[:, i, :]
            rG = [G_[n].r for n in G_]
            sw, Gm, Gi, Gp, a_, kkn, kp, t1, t2 = [X[n] for n in ("sw", "Gm", "Gi", "Gp", "a", "kkn", "kp", "t1", "t2")]
            Yt, yn = X["Y"], X["yn"]
            sfx = "" if tl_ % 2 == 0 else "2"
            Bt, Kt, Bh, Kh, GC, At_, Rt = [X[n + sfx] if (n + sfx) in X else X[n] for n in ("Bt", "Kt", "Bh", "Kh", "GC", "At", "Rt")]
            Vt = X["V0"] if tl_ % 2 == 0 else X["V2"]
            Vb = X["Vb0"] if tl_ % 2 == 0 else X["Vb2"]
            bonus = X["bonus"] if tl_ % 2 == 0 else X["bonus2"]
            glg = X["glg"] if tl_ % 2 == 0 else X["glg2"]
            t3 = X["t3"]
            v_ = Vb.t[:]
            rG = [Vb.r]
            S.op("pool", lambda e: e.tensor_copy(out=Z[0].t[:, :, 0:64], in_=hv(At_.t[:])), r=[At_.r], w=[Z[0].r])
            for p in range(4):
                pT = PB[p % 2]
                for q_, src in enumerate((At_, Rt, Kt, Bt)):
                    S.op("pe", lambda e, pT=pT, p=p, q_=q_, src=src: e.transpose(pT.t[:, q_ * 128:(q_ + 1) * 128], src.t[:, p * 128:(p + 1) * 128], ident[:]), r=[src.r, r_ident], w=[pT.r])
                S.op("dve", lambda e, pT=pT, p=p: e.tensor_copy(out=ART.t[:, p, :, 0:64], in_=pT.t[:, 0:128].rearrange("p (c t) -> p c t", t=64)), r=[pT.r], w=[ART.r])
                S.op("act", lambda e, pT=pT, p=p: e.copy(out=ART.t[:, p, :, 64:128], in_=pT.t[:, 128:256].rearrange("p (c t) -> p c t", t=64)), r=[pT.r], w=[ART.r])
                S.op("act", lambda e, pT=pT, p=p: e.copy(out=KT.t[:, p, :], in_=pT.t[:, 256:384]), r=[pT.r], w=[KT.r])
                S.op("dve", lambda e, pT=pT, p=p: e.tensor_copy(out=BT.t[:, p, :], in_=pT.t[:, 384:512]), r=[pT.r], w=[BT.r])
            if stop == 3:
                raise _Stop()
            pS1a, pS1b, pS2a, pS2b, pS3 = PB[0], PB[1], PB[2], PB[3], PB[4]
            for h in range(8):
                p, e_ = h // 2, h % 2
                fe = slice(e_ * 64, e_ * 64 + 64)
                pa, pb_ = (pS1a, pS2a) if h < 4 else (pS1b, pS2b)
                hc = (h % 4) * 128
                for ch in range(2):
                    tc_ = slice(ch * 64, ch * 64 + 64)
                    S.op("pe", lambda e, pa=pa, fe=fe, p=p, ch=ch, tc_=tc_, hc=hc: e.matmul(pa.t[tc_, hc:hc + 128], BT.t[fe, p, tc_], ART.t[fe, p, ch, :], start=True, stop=True), r=[BT.r, ART.r], w=[pa.r])
                    S.op("pe", lambda e, pb_=pb_, fe=fe, p=p, ch=ch, tc_=tc_, hc=hc: e.matmul(pb_.t[tc_, hc:hc + 128], KT.t[fe, p, tc_], ART.t[fe, p, ch, :], start=True, stop=True), r=[KT.r, ART.r], w=[pb_.r])
                    S.op("pe", lambda e, fe=fe, p=p, ch=ch, tc_=tc_, h=h: e.matmul(pS3.t[tc_, h * 64:(h + 1) * 64], ART.t[fe, p, ch, 0:64], BT.t[fe, p, tc_], start=True, stop=True), r=[BT.r, ART.r], w=[pS3.r])
            m1b = mask1.t[:].unsqueeze(1).to_broadcast([128, 4, 128])
            for half, (pa, pb_) in enumerate(((pS1a, pS2a), (pS1b, pS2b))):
                hs = slice(half * 4, half * 4 + 4)
                S.op("dve", lambda e, pa=pa, hs=hs: e.tensor_tensor(out=Arb.t[:, hs, :], in0=pa.t[:].rearrange("p (h t) -> p h t", t=128), in1=m1b, op=ALU.mult), r=[pa.r, mask1.r], w=[Arb.r])
                S.op("dve", lambda e, pb_=pb_, hs=hs: e.tensor_tensor(out=Aak.t[:, hs, :], in0=pb_.t[:].rearrange("p (h t) -> p h t", t=128), in1=m1b, op=ALU.mult), r=[pb_.r, mask1.r], w=[Aak.r])
            for ch in range(2):
                tc_ = slice(ch * 64, ch * 64 + 64)
                S.op("dve", lambda e, tc_=tc_: e.tensor_tensor(out=NTm[0].t[tc_, :, tc_], in0=hv(pS3.t[tc_, :]), in1=maskL.t[tc_, :].unsqueeze(1).to_broadcast([64, 8, 64]), op=ALU.mult), r=[pS3.r, maskL.r], w=[NTm[0].r])
                S.op("act", lambda e, tc_=tc_: e.copy(out=Nm[0].t[tc_, :, tc_], in_=Arb.t[tc_, :, 0:64]), r=[Arb.r], w=[Nm[0].r])
            if stop == 4:
                raise _Stop()
            if debug and tl_ == 0:
                tails.append(S.dma("sp", dbg["Arb"], Arb.t[:], r=[Arb.r], key="dbg_Arb"))
            if debug and tl_ == 0:
                tails.append(S.dma("sp", dbg["Aak"], Aak.t[:], r=[Aak.r], key="dbg_Aak"))
            if debug and tl_ == 0:
                tails.append(S.dma("sp", dbg["NT0"], NTm[0].t[:], r=[NTm[0].r], key="dbg_NT0"))
            if debug and tl_ == 0:
                tails.append(S.dma("sp", dbg["ART"], ART.t[:], r=[ART.r], key="dbg_ART"))
            if debug and tl_ == 0:
                tails.append(S.dma("sp", dbg["KT"], KT.t[:], r=[KT.r], key="dbg_KT"))
            if debug and tl_ == 0:
                tails.append(S.dma("sp", dbg["BT"], BT.t[:], r=[BT.r], key="dbg_BT"))
            if debug and tl_ == 0:
                tails.append(S.dma("sp", dbg["Bh"], Bh.t[:], r=[Bh.r], key="dbg_Bh"))
            if debug and tl_ == 0:
                tails.append(S.dma("sp", dbg["Kh"], Kh.t[:], r=[Kh.r], key="dbg_Kh"))
            if hook is not None:
                hook()
            pX = PB[5]
            for h in range(8):
                for ch in range(2):
                    tc_ = slice(ch * 64, ch * 64 + 64)
                    S.op("pe", lambda e, h=h, tc_=tc_, v_=v_: e.matmul(pX.t[tc_, h * 64:(h + 1) * 64], Aak.t[tc_, h, 0:64], v_[tc_, h * 64:(h + 1) * 64], start=True, stop=True), r=[Aak.r] + rG, w=[pX.r])
            S.op("act", lambda e: e.copy(out=Z[0].t[:, :, 64:128], in_=hv(pX.t[:])), r=[pX.r], w=[Z[0].r])
            if stop == 5:
                raise _Stop()
            if debug and tl_ == 0:
                tails.append(S.dma("sp", dbg["Z0"], Z[0].t[:], r=[Z[0].r], key="dbg_Z0"))
            zi = 0
            for lev in range(6):
                Nc, NTc = Nm[lev % 2], NTm[lev % 2]
                Nn, NTn = Nm[(lev + 1) % 2], NTm[(lev + 1) % 2]
                pZa, pZb = PB[0], PB[1]
                Zc, Zn = Z[zi], Z[1 - zi]
                if lev < 5:
                    pN, pNT = (PB[2], PB[3]), (PB[4], PB[5])
                    for h in range(8):
                        hb, hc = h // 4, (h % 4) * 128
                        S.op("pe", lambda e, NTc=NTc, Nc=Nc, h=h, hb=hb, hc=hc: e.matmul(pN[hb].t[:, hc:hc + 128], NTc.t[:, h, :], Nc.t[:, h, :], start=True, stop=True), r=[Nc.r, NTc.r], w=[pN[hb].r])
                        if lev < 4:
                            S.op("pe", lambda e, NTc=NTc, Nc=Nc, h=h, hb=hb, hc=hc: e.matmul(pNT[hb].t[:, hc:hc + 128], Nc.t[:, h, :], NTc.t[:, h, :], start=True, stop=True), r=[Nc.r, NTc.r], w=[pNT[hb].r])
                for h in range(8):
                    pz = pZa if h < 4 else pZb
                    hc = (h % 4) * 128
                    S.op("pe", lambda e, pz=pz, Nc=Nc, Zc=Zc, h=h, hc=hc: e.matmul(pz.t[:, hc:hc + 128], Nc.t[:, h, :], Zc.t[:, h, :], start=True, stop=True), r=[Nc.r, Zc.r], w=[pz.r])
                if lev == 5:
                    Zn = Zf32
                S.op("dve", lambda e, Zc=Zc, Zn=Zn: e.tensor_tensor(out=Zn.t[:, 0:4, :], in0=Zc.t[:, 0:4, :], in1=pZa.t[:].rearrange("p (h t) -> p h t", t=128), op=ALU.add), r=[Zc.r, pZa.r], w=[Zn.r])
                S.op("dve", lambda e, Zc=Zc, Zn=Zn: e.tensor_tensor(out=Zn.t[:, 4:8, :], in0=Zc.t[:, 4:8, :], in1=pZb.t[:].rearrange("p (h t) -> p h t", t=128), op=ALU.add), r=[Zc.r, pZb.r], w=[Zn.r])
                zi = 1 - zi
                if lev < 5:
                    for hb in range(2):
                        S.op("act", lambda e, Nn=Nn, hb=hb: e.copy(out=Nn.t[:, hb * 4:hb * 4 + 4, :], in_=pN[hb].t[:].rearrange("p (h t) -> p h t", t=128)), r=[pN[hb].r], w=[Nn.r])
                        if lev < 4:
                            S.op("act" if hb == 0 else "dve", (lambda e, NTn=NTn, hb=hb: e.copy(out=NTn.t[:, hb * 4:hb * 4 + 4, :], in_=pNT[hb].t[:].rearrange("p (h t) -> p h t", t=128))) if hb == 0 else (lambda e, NTn=NTn, hb=hb: e.tensor_copy(out=NTn.t[:, hb * 4:hb * 4 + 4, :], in_=pNT[hb].t[:].rearrange("p (h t) -> p h t", t=128))), r=[pNT[hb].r], w=[NTn.r])
            Zf = Zf32
            S.op("act", lambda e: e.copy(out=Zfb.t[:], in_=Zf32.t[:]), r=[Zf32.r], w=[Zfb.r])
            pY0 = PB[4]
            for h in range(8):
                for ch in range(2):
                    tc_ = slice(ch * 64, ch * 64 + 64)
                    S.op("pe", lambda e, h=h, tc_=tc_: e.matmul(pY0.t[tc_, h * 64:(h + 1) * 64], Arb.t[tc_, h, 64:128], Zfb.t[tc_, h, 64:128], start=True, stop=False), r=[Arb.r, Zfb.r], w=[pY0.r])
                    S.op("pe", lambda e, h=h, tc_=tc_, v_=v_: e.matmul(pY0.t[tc_, h * 64:(h + 1) * 64], Aak.t[tc_, h, 64:128], v_[tc_, h * 64:(h + 1) * 64], start=False, stop=True), r=[Aak.r] + rG, w=[pY0.r])
            S.op("act", lambda e: e.copy(out=Y0.t[:], in_=pY0.t[:]), r=[pY0.r], w=[Y0.r])
            if stop == 7:
                raise _Stop()
            if debug and tl_ == 0:
                tails.append(S.dma("sp", dbg["Zf"], Zf.t[:], r=[Zf.r], key="dbg_Zf"))
            if debug and tl_ == 0:
                tails.append(S.dma("sp", dbg["Y0"], Y0.t[:], r=[Y0.r], key="dbg_Y0"))
            pR, pQ, pH, pg = PB[5], PB[6], PB[0], PB[1]
            for h in range(8):
                p, e_ = h // 2, h % 2
                fe = slice(e_ * 64, e_ * 64 + 64)
                for ch in range(2):
                    tc_ = slice(ch * 64, ch * 64 + 64)
                    cs = slice(p * 128 + ch * 64, p * 128 + ch * 64 + 64)
                    hs_ = slice(h * 64, h * 64 + 64)
                    S.op("pe", lambda e, fe=fe, cs=cs, tc_=tc_, h=h: e.matmul(pR.t[fe, cs], Zfb.t[tc_, h, 0:64], Arb.t[tc_, h, 64:128], start=True, stop=True), r=[Zfb.r, Arb.r], w=[pR.r])
                    S.op("pe", lambda e, fe=fe, cs=cs, tc_=tc_, h=h, hs_=hs_: e.matmul(pQ.t[fe, cs], Zfb.t[tc_, h, 0:64], Bh.t[tc_, hs_], start=True, stop=True), r=[Zfb.r, Bh.r], w=[pQ.r])
                    S.op("pe", lambda e, fe=fe, cs=cs, tc_=tc_, h=h, hs_=hs_: e.matmul(pH.t[fe, cs], Bh.t[tc_, hs_], Zfb.t[tc_, h, 64:128], start=True, stop=False), r=[Zfb.r, Bh.r], w=[pH.r])
                    S.op("pe", lambda e, fe=fe, cs=cs, tc_=tc_, hs_=hs_: e.matmul(pH.t[fe, cs], Kh.t[tc_, hs_], Vb.t[tc_, hs_], start=False, stop=True), r=[Kh.r, Vb.r], w=[pH.r])
                    S.op("pe", lambda e, fe=fe, tc_=tc_, hs_=hs_, p=p, ch=ch: e.matmul(pg.t[fe, p * 2 + ch:p * 2 + ch + 1], GC.t[tc_, hs_], avg.t[tc_, :], start=True, stop=True), r=[GC.r, avg.r], w=[pg.r])
            for ch in range(2):
                S.op("dve", lambda e, ch=ch: e.tensor_tensor(out=RbT.t[:, :, ch * 64:(ch + 1) * 64], in0=pR.t[:].rearrange("p (a t) -> p a t", t=128)[:, :, ch * 64:(ch + 1) * 64], in1=ART.t[:, :, ch, 64:128], op=ALU.add), r=[pR.r, ART.r], w=[RbT.r])
            S.op("act", lambda e: e.copy(out=Qm.t[:], in_=pQ.t[:].rearrange("p (a t) -> p a t", t=128)), r=[pQ.r], w=[Qm.r])
            S.op("act", lambda e: e.copy(out=Hm.t[:], in_=pH.t[:].rearrange("p (a t) -> p a t", t=128)), r=[pH.r], w=[Hm.r])
            S.op("act", lambda e: e.copy(out=gcol.t[:], in_=pg.t[:, 0:8]), r=[pg.r], w=[gcol.r])
            if stop == 8:
                raise _Stop()
            if debug and tl_ == 0:
                tails.append(S.dma("sp", dbg["RbT"], RbT.t[:], r=[RbT.r], key="dbg_RbT"))
            if debug and tl_ == 0:
                tails.append(S.dma("sp", dbg["Qm"], Qm.t[:], r=[Qm.r], key="dbg_Qm"))
            if debug and tl_ == 0:
                tails.append(S.dma("sp", dbg["Hm"], Hm.t[:], r=[Hm.r], key="dbg_Hm"))
            if debug and tl_ == 0:
                tails.append(S.dma("sp", dbg["gcol"], gcol.t[:], r=[gcol.r], key="dbg_gcol"))
            for ch in range(2):
                tc_ = slice(ch * 64, ch * 64 + 64)
                Sc, Sn = ST[sti[0]], ST[1 - sti[0]]
                pYe, pS_ = (PB[2], PB[4]), PB[3]
                for h in range(8):
                    p, e_ = h // 2, h % 2
                    fe = slice(e_ * 64, e_ * 64 + 64)
                    cs = slice(ch * 64, ch * 64 + 64)
                    pY = pYe[e_]
                    S.op("pe", lambda e, fe=fe, p=p, cs=cs, tc_=tc_, h=h, Sc=Sc, pY=pY: e.matmul(pY.t[tc_, h * 64:(h + 1) * 64], RbT.t[fe, p, cs], Sc.t[fe, p, :], start=True, stop=True), r=[RbT.r, Sc.r], w=[pY.r])
                    S.op("pe", lambda e, fe=fe, p=p, cs=cs, Sc=Sc: e.matmul(pS_.t[fe, p * 64:(p + 1) * 64], Qm.t[fe, p, cs], Sc.t[fe, p, :], start=True, stop=True), r=[Qm.r, Sc.r], w=[pS_.r])
                v4 = lambda ap: ap.rearrange("p (a e d) -> p a e d", e=2, d=64)
                for e_ in range(2):
                    S.op("dve", lambda e, tc_=tc_, e_=e_: e.tensor_tensor(out=v4(Yt.t[tc_, :])[:, :, e_, :], in0=v4(Y0.t[tc_, :])[:, :, e_, :], in1=v4(pYe[e_].t[tc_, :])[:, :, e_, :], op=ALU.add), r=[Y0.r, pYe[e_].r], w=[Yt.r])
                for p in range(4):
                    S.op("dve", lambda e, p=p, ch=ch, Sc=Sc, Sn=Sn: e.scalar_tensor_tensor(out=Sn.t[:, p, :], in0=Sc.t[:, p, :], scalar=gcol.t[:, p * 2 + ch:p * 2 + ch + 1], in1=pS_.t[:, p * 64:(p + 1) * 64], op0=ALU.mult, op1=ALU.add), r=[Sc.r, gcol.r, pS_.r], w=[Sn.r])
                S.op("dve", lambda e, ch=ch, Sn=Sn: e.tensor_tensor(out=Sn.t[:], in0=Sn.t[:], in1=Hm.t[:, :, ch * 64:(ch + 1) * 64], op=ALU.add), r=[Sn.r, Hm.r], w=[Sn.r])
                sti[0] = 1 - sti[0]
            if stop == 9:
                raise _Stop()
            if debug and tl_ == 0:
                S.op("act", lambda e, r_=r_: e.copy(out=t1.t[:], in_=r_), r=rG + [t1.r], w=[t1.r])
                S.op("act", lambda e, v_=v_: e.copy(out=t2.t[:], in_=v_), r=rG + [t2.r], w=[t2.r])
                tails.append(S.dma("sp", dbg["v"], t2.t[:], r=[t2.r], key="dbg_v"))
                for n, src in (("r", t1), ("kp", kp), ("kkn", kkn), ("a", a_), ("sw", sw), ("Y", Yt), ("bonus", bonus), ("Gm", Gm), ("yn", yn)):
                    tails.append(S.dma("sp", dbg[n], src.t[:], r=[src.r], key="dbg_" + n))
                tails.append(S.dma("sp", dbg["S"], ST[sti[0]].t[:], r=[ST[sti[0]].r], key="dbg_S"))


        def tile_tail(i, G):
            tl_ = G * NTG + i
            r_ = G_["r"].t[:, i, :]; k_ = G_["k"].t[:, i, :]; v_ = G_["v"].t[:, i, :]
            rG = [G_[n].r for n in G_]
            sw, Gm, Gi, Gp, a_, kkn, kp, t1, t2 = [X[n] for n in ("sw", "Gm", "Gi", "Gp", "a", "kkn", "kp", "t1", "t2")]
            Yt, yn = X["Y"], X["yn"]
            sfx = "" if tl_ % 2 == 0 else "2"
            Bt, Kt, Bh, Kh, GC, At_, Rt = [X[n + sfx] if (n + sfx) in X else X[n] for n in ("Bt", "Kt", "Bh", "Kh", "GC", "At", "Rt")]
            Vt = X["V0"] if tl_ % 2 == 0 else X["V2"]
            Vb = X["Vb0"] if tl_ % 2 == 0 else X["Vb2"]
            bonus = X["bonus"] if tl_ % 2 == 0 else X["bonus2"]
            glg = X["glg"] if tl_ % 2 == 0 else X["glg2"]
            t3 = X["t3"]
            v_ = Vb.t[:]
            rG = [Vb.r]
            S.op("dve", lambda e: e.tensor_reduce(out=smT.t[:, 16:24], in_=hv(Yt.t[:]), axis=AX.X, op=ALU.add), r=[Yt.r], w=[smT.r])
            S.op("act", lambda e: e.activation(out=t3.t[:], in_=Yt.t[:], func=AF.Square), r=[Yt.r, t3.r], w=[t3.r])
            S.op("dve", lambda e: e.tensor_reduce(out=smT.t[:, 24:32], in_=hv(t3.t[:]), axis=AX.X, op=ALU.add), r=[t3.r], w=[smT.r])
            S.op("dve", lambda e: e.tensor_scalar(out=smT.t[:, 16:24], in0=smT.t[:, 16:24], scalar1=1.0 / 64, scalar2=None, op0=ALU.mult), r=[smT.r], w=[smT.r])
            S.op("dve", lambda e: e.tensor_tensor(out=smT.t[:, 32:40], in0=smT.t[:, 16:24], in1=smT.t[:, 16:24], op=ALU.mult), r=[smT.r], w=[smT.r])
            S.op("dve", lambda e: e.scalar_tensor_tensor(out=smT.t[:, 24:32], in0=smT.t[:, 24:32], scalar=1.0 / 64, in1=smT.t[:, 32:40], op0=ALU.mult, op1=ALU.subtract), r=[smT.r], w=[smT.r])
            S.op("dve", lambda e: e.tensor_scalar(out=smT.t[:, 24:32], in0=smT.t[:, 24:32], scalar1=GN_EPS, scalar2=None, op0=ALU.add), r=[smT.r], w=[smT.r])
            S.op("act", lambda e: e.activation(out=smT.t[:, 24:32], in_=smT.t[:, 24:32], func=AF.Sqrt), r=[smT.r], w=[smT.r])
            S.op("dve", lambda e: e.reciprocal(out=smT.t[:, 24:32], in_=smT.t[:, 24:32]), r=[smT.r], w=[smT.r])
            S.op("dve", lambda e: e.tensor_tensor(out=hv(yn.t[:]), in0=hv(Yt.t[:]), in1=bc8(smT.t[:, 16:24]), op=ALU.subtract), r=[Yt.r, smT.r], w=[yn.r])
            S.op("dve", lambda e: e.tensor_tensor(out=hv(yn.t[:]), in0=hv(yn.t[:]), in1=bc8(smT.t[:, 24:32]), op=ALU.mult), r=[yn.r, smT.r], w=[yn.r])
            S.op("dve", lambda e, glg=glg: e.tensor_tensor(out=yn.t[:], in0=yn.t[:], in1=glg.t[:], op=ALU.mult), r=[yn.r, glg.r], w=[yn.r])
            S.op("dve", lambda e, bonus=bonus: e.tensor_tensor(out=ybf.t[:], in0=yn.t[:], in1=bonus.t[:], op=ALU.add), r=[yn.r, bonus.r], w=[ybf.r])
            for p in range(4):
                S.op("pe", lambda e, p=p: e.transpose(pbb.t[:, p * 128:(p + 1) * 128], ybf.t[:, p * 128:(p + 1) * 128], identb.t[:]), r=[ybf.r, identb.r], w=[pbb.r])
            yo = yTo[tl_ % 2]
            S.op("act", lambda e, yo=yo: e.copy(out=yo.t[:], in_=pbb.t[:, 0:512].rearrange("p (a t) -> p a t", t=128)), r=[pbb.r], w=[yo.r])
            tails.append(S.dma("sp", ysrc[:, :, tl_ * 128:(tl_ + 1) * 128], yo.t[:], r=[yo.r], key="yTo%d" % (tl_ % 2)))

        pend = [None, None]
        try:
          for G in range(NGR):
              if G > 0:
                  S.op("dve", lambda e: e.tensor_copy(out=xg.t[:, :, 0:1], in_=xg.t[:, :, GS:GS + 1]), r=[xg.r], w=[xg.r])
              if before_x is not None and G == 0:
                  before_x(S)
              S.dma("sp", xg.t[:, :, 1:GS + 1], xsrc_fn(G * GS, GS), r=[xg.r], w=[xg.r], key="xg")
              for kc in range(8):
                  S.op("dve", lambda e, kc=kc: e.tensor_scalar(out=xg.t[:, kc, 1:GS + 1], in0=xg.t[:, kc, 1:GS + 1], scalar1=cols.t[:, kc:kc + 1], scalar2=mods[:, kc:kc + 1], op0=ALU.mult, op1=ALU.add), r=[xg.r, cols.r, r_mods], w=[xg.r])
              S.op("dve", lambda e: e.tensor_tensor(out=dx.t[:], in0=xg.t[:, :, 0:GS], in1=xg.t[:, :, 1:GS + 1], op=ALU.subtract), r=[xg.r], w=[dx.r])
              for n, nm in enumerate(("r", "k", "v", "w", "a", "g")):
                  xb = xs[n % 2]
                  for kc in range(8):
                      eng = "dve"
                      S.op(eng, lambda e, kc=kc, n=n, xb=xb: e.scalar_tensor_tensor(out=xb.t[:, kc, :], in0=dx.t[:, kc, :], scalar=mu.t[:, n * 8 + kc:n * 8 + kc + 1], in1=xg.t[:, kc, 1:GS + 1], op0=ALU.mult, op1=ALU.add), r=[dx.r, mu.r, xg.r], w=[xb.r])
                  if n < 3:
                      wt = W[("wr", "wk", "wv")[n]]
                      for i in range(NTG):
                          pb = PB[(n * NTG + i) % 2]
                          for kc in range(8):
                              S.op("pe", lambda e, pb=pb, xb=xb, wt=wt, i=i, kc=kc: e.matmul(pb.t[:], xb.t[:, kc, i * 128:(i + 1) * 128], wt[:, kc, :], start=(kc == 0), stop=(kc == 7)), r=[xb.r, r_w], w=[pb.r])
                          S.op("act", lambda e, pb=pb, nm=nm, i=i: e.copy(out=G_[nm].t[:, i, :], in_=pb.t[:]), r=[pb.r], w=[G_[nm].r])
                  else:
                      w1n, w2n, rows = (("w1", "w2", 64), ("a1", "a2", 64), ("g1", "g2", 128))[n - 3]
                      pb = PB[2]
                      for kc in range(8):
                          S.op("pe", lambda e, pb=pb, xb=xb, w1n=w1n, rows=rows, kc=kc: e.matmul(pb.t[0:rows, 0:GS], W[w1n][:, kc, :], xb.t[:, kc, :], start=(kc == 0), stop=(kc == 7)), r=[xb.r, r_w], w=[pb.r])
                      lt = l1[n - 3]
                      if nm == "w":
                          S.op("act", lambda e, pb=pb: e.activation(out=l1f.t[0:64, :], in_=pb.t[0:64, 0:GS], func=AF.Exp, scale=-2.0), r=[pb.r, l1f.r], w=[l1f.r])
                          S.op("dve", lambda e: e.tensor_scalar(out=l1f.t[0:64, :], in0=l1f.t[0:64, :], scalar1=1.0, scalar2=None, op0=ALU.add), r=[l1f.r], w=[l1f.r])
                          S.op("dve", lambda e: e.reciprocal(out=l1f.t[0:64, :], in_=l1f.t[0:64, :]), r=[l1f.r], w=[l1f.r])
                          S.op("dve", lambda e, lt=lt: e.tensor_scalar(out=lt.t[0:64, :], in0=l1f.t[0:64, :], scalar1=2.0, scalar2=-1.0, op0=ALU.mult, op1=ALU.add), r=[l1f.r], w=[lt.r])
                      elif nm == "a":
                          S.op("act", lambda e, pb=pb, lt=lt: e.copy(out=lt.t[0:64, :], in_=pb.t[0:64, 0:GS]), r=[pb.r], w=[lt.r])
                      else:
                          S.op("act", lambda e, pb=pb: e.activation(out=l1f.t[:], in_=pb.t[:, 0:GS], func=AF.Exp, scale=-1.0), r=[pb.r, l1f.r], w=[l1f.r])
                          S.op("dve", lambda e: e.tensor_scalar(out=l1f.t[:], in0=l1f.t[:], scalar1=1.0, scalar2=None, op0=ALU.add), r=[l1f.r], w=[l1f.r])
                          S.op("dve", lambda e: e.reciprocal(out=l1f.t[:], in_=l1f.t[:]), r=[l1f.r], w=[l1f.r])
                          S.op("act", lambda e, lt=lt: e.copy(out=lt.t[:], in_=l1f.t[:]), r=[l1f.r], w=[lt.r])
              if stop == 1:
                  raise _Stop()
              for i in range(NTG):
                  tile_front_a(i, G)
                  if pend[0] is None:
                      tile_front_b(i, G)
                  else:
                      def hook(i=i, G=G, prev=pend[1]):
                          if prev is not None and not debug:
                              tile_tail(*prev)
                          tile_front_b(i, G)
                      tile_back(*pend[0], hook=hook)
                      if debug:
                          tile_tail(*pend[0])
                      pend[1] = pend[0]
                  pend[0] = (i, G)
          if pend[0] is not None:
              def hook_last(prev=pend[1]):
                  if prev is not None and not debug:
                      tile_tail(*prev)
              tile_back(*pend[0], hook=hook_last)
              tile_tail(*pend[0])
        except _Stop:
            pass
        if after is not None:
            after(C, tails)
        S.emit(st, tail_waits=tails)


PAIRS = [[0, 1], [2, 3], [4, 5], [6, 7]]
TL_KEYS = ("cT", "adaw", "adab", "lng", "lnb", "wo", "win", "wout")


def tl_decl(dt, sfx):
    return dict(cT=dt("cT" + sfx, [128, 8], F32), adaw=dt("adaw" + sfx, [D, 4 * D], F32), adab=dt("adab" + sfx, [128, 32], F32),
                lng=dt("lng" + sfx, [128, 16], F32), lnb=dt("lnb" + sfx, [128, 16], F32),
                wo=dt("wo" + sfx, [D, D], F32), win=dt("win" + sfx, [D, 2 * DFF], F32), wout=dt("wout" + sfx, [DFF, D], F32))


def build_fused(stop=99):
    nc = bass.Bass("TRN2", target_bir_lowering=False)
    dt = lambda n, s, d, k="ExternalInput": nc.dram_tensor(n, list(s), d, kind=k).ap()
    it = lambda n, s, d: nc.dram_tensor(n, list(s), d).ap()
    fox_in = fox_decl(dt, 4096, "_f")
    oT_d = dt("oT", [D, 2048], F32, "ExternalOutput")
    cin1, cout1 = it("cin1", [4, 128, 4096], BF16), it("cout1", [4, 2, 128, 4096], BF16)
    cin2, cout2 = it("cin2", [8, 128, 2048], F32), it("cout2", [8, 2, 128, 2048], F32)
    cin3, cout3 = it("cin3", [4, 128, 4096], BF16), it("cout3", [4, 2, 128, 4096], BF16)

    GS_ = {k: nc.alloc_semaphore(name="xchg_" + k) for k in ("g1", "g2", "g3")}

    def gather(cin, cout, key, nblk):
        def after(C, tails):
            prev = list(tails)
            for k in range(nblk):
                C.S.cc("AllGather", PAIRS, cin[k], cout[k].rearrange("r p t -> (r p) t"), key=key, extra=prev, ext_sem=GS_[key])
        return after

    def waiter(key, nblk):
        return lambda S: S.ext_wait("sp", GS_[key], nblk)

    cin1_v = cin1.rearrange("k p t -> (k p) t")
    cin3_v = cin3.rearrange("k p t -> (k p) t")
    cin2_v = cin2.rearrange("k p t -> (k p) t")
    y1_v = cout1
    y3_v = cout3
    fox_phase(nc, "f_", fox_in, cin1_v, 4096, after=gather(cin1, cout1, "g1", 4))
    if stop <= 1:
        return nc
    phase_end(nc)
    sel_d = dt("sel", [128, 2], F32)
    xTh_d = dt("xTh", [D, 2048], F32)
    t = tl_decl(dt, "_t0")
    tl_phase(nc, "t0_", y1_v, xTh_d, t["cT"], t["adaw"], t["adab"], t["lng"], t["lnb"], t["wo"], t["win"], t["wout"], cin2_v, 2048, 1024,
             sel_d=sel_d, after=gather(cin2, cout2, "g2", 8), before_y=waiter("g1", 4))
    if stop <= 2:
        return nc
    phase_end(nc)
    rw_in = rwkv_decl(dt, "_r")
    c2v = cout2.rearrange("kc r p t -> r p kc t")
    rwkv_phase(nc, "r_", lambda t0, n: c2v[t0 // 2048, :, :, (t0 % 2048):(t0 % 2048) + n], rw_in, cin3_v, 4096, after=gather(cin3, cout3, "g3", 4), before_x=waiter("g2", 8))
    if stop <= 3:
        return nc
    phase_end(nc)
    t = tl_decl(dt, "_t1")
    tl_phase(nc, "t1_", y3_v, cin2_v, t["cT"], t["adaw"], t["adab"], t["lng"], t["lnb"], t["wo"], t["win"], t["wout"], oT_d, 2048, 1024, sel_d=sel_d, before_y=waiter("g3", 4))
    return nc


NCORES = 8
_PROGS = {}


def _prog(name, fn):
    if name not in _PROGS:
        _PROGS[name] = fn()
    return _PROGS[name]


def _col_layout(v):
    return np.ascontiguousarray(np.asarray(v).reshape(-1, 128).T)


def _bc(v, n=512):
    v = np.asarray(v, dtype=np.float32).reshape(-1)
    return np.ascontiguousarray(np.broadcast_to(v[None, :], (128, v.shape[0])))


def _run(nc, in_maps):
    res = run_bass_kernel_spmd(nc, in_maps, core_ids=list(range(NCORES)))
    return res.results


def _tl_maps(yT_list, xT_list, c, ada_w_i, ada_b_i, ln_g_i, ln_b_i, wo, win, wout):
    maps = []
    adaw = np.ascontiguousarray(ada_w_i[:, 2 * D:])
    adab = _col_layout(ada_b_i[2 * D:])
    lng = _col_layout(ln_g_i.reshape(-1))
    lnb = _col_layout(ln_b_i.reshape(-1))
    for core in range(NCORES):
        b, th = core // 2, core % 2
        ts = slice(th * 2048, (th + 1) * 2048)
        maps.append({
            "yT": np.ascontiguousarray(yT_list[b][:, ts]),
            "xT": np.ascontiguousarray(xT_list[b][:, ts]),
            "cT": _col_layout(c[b]),
            "adaw": adaw, "adab": adab, "lng": lng, "lnb": lnb,
            "wo": wo, "win": win, "wout": wout,
        })
    return maps


def kernel_unfused(x, c, ada_w, ada_b, ln_g, ln_b, ffn_w_in, ffn_w_out,
           fox_w_in, fox_b_f, fox_q_g, fox_k_g, fox_w_o,
           rwkv_mu, rwkv_w_rkv, rwkv_w0, rwkv_w1, rwkv_w2, rwkv_a0, rwkv_a1, rwkv_a2,
           rwkv_g1, rwkv_g2, rwkv_k_k, rwkv_k_a, rwkv_r_k, rwkv_lnx_g, rwkv_lnx_b, rwkv_w_o):
    f = lambda a: np.ascontiguousarray(np.asarray(a, dtype=np.float32))
    x, c, ada_w, ada_b, ln_g, ln_b = f(x), f(c), f(ada_w), f(ada_b), f(ln_g), f(ln_b)
    ffn_w_in, ffn_w_out = f(ffn_w_in), f(ffn_w_out)
    B = x.shape[0]
    w_in = f(fox_w_in)[0]
    b_f, q_g, k_g = f(fox_b_f)[0], f(fox_q_g)[0], f(fox_k_g)[0]
    maps = []
    for core in range(NCORES):
        b, hh = core // 2, core % 2
        sl = slice(hh * 512, (hh + 1) * 512)
        maps.append({
            "x": x[b],
            "cT": _col_layout(c[b]),
            "adaw": np.ascontiguousarray(ada_w[0][:, 0:2 * D]),
            "adab": _col_layout(ada_b[0][0:2 * D]),
            "wq": np.ascontiguousarray(w_in[:, 0:D][:, sl]),
            "wk": np.ascontiguousarray(w_in[:, D:2 * D][:, sl]),
            "wv": np.ascontiguousarray(w_in[:, 2 * D:3 * D][:, sl]),
            "wf": np.ascontiguousarray(w_in[:, 3 * D + hh * 8:3 * D + hh * 8 + 8]),
            "wg": np.ascontiguousarray(w_in[:, 3 * D + 16:][:, sl]),
            "bfb": _bc(b_f[hh * 8:(hh + 1) * 8]),
            "qgb": _bc(np.tile(q_g, 8)),
            "kgb": _bc(np.tile(k_g, 8)),
        })
    r1 = _run(_prog("fox", build_fox), maps)
    yT = [np.concatenate([r1[2 * b]["ygT"], r1[2 * b + 1]["ygT"]], axis=0) for b in range(B)]
    xT = [np.ascontiguousarray(x[b].T) for b in range(B)]
    tlp = _prog("tl", build_tl)
    r2 = _run(tlp, _tl_maps(yT, xT, c, ada_w[0], ada_b[0], ln_g[0], ln_b[0], f(fox_w_o)[0], ffn_w_in[0], ffn_w_out[0]))
    x1T = [np.concatenate([r2[2 * b]["oT"], r2[2 * b + 1]["oT"]], axis=1) for b in range(B)]
    mu = f(rwkv_mu)[0]
    w_rkv = f(rwkv_w_rkv)[0]
    P = dict(w0=f(rwkv_w0)[0], w1=f(rwkv_w1)[0], w2=f(rwkv_w2)[0], a0=f(rwkv_a0)[0], a1=f(rwkv_a1)[0], a2=f(rwkv_a2)[0],
             g1=f(rwkv_g1)[0], g2=f(rwkv_g2)[0], k_k=f(rwkv_k_k)[0], k_a=f(rwkv_k_a)[0], r_k=f(rwkv_r_k)[0].reshape(-1),
             lnx_g=f(rwkv_lnx_g)[0], lnx_b=f(rwkv_lnx_b)[0])
    mu_l = np.concatenate([_col_layout(mu[n]) for n in range(6)], axis=1)
    maps = []
    for core in range(NCORES):
        b, hh = core // 2, core % 2
        sl = slice(hh * 512, (hh + 1) * 512)
        maps.append({
            "xT": x1T[b],
            "cT": _col_layout(c[b]),
            "adaw": np.ascontiguousarray(ada_w[1][:, 0:2 * D]),
            "adab": _col_layout(ada_b[1][0:2 * D]),
            "mu": mu_l,
            "wr": np.ascontiguousarray(w_rkv[0][:, sl]),
            "wk": np.ascontiguousarray(w_rkv[1][:, sl]),
            "wv": np.ascontiguousarray(w_rkv[2][:, sl]),
            "w1": P["w1"], "a1": P["a1"], "g1": P["g1"],
            "w2": np.ascontiguousarray(P["w2"][:, sl]),
            "a2": np.ascontiguousarray(P["a2"][:, sl]),
            "g2": np.ascontiguousarray(P["g2"][:, sl]),
            "w0b": _bc(P["w0"][sl]), "a0b": _bc(P["a0"][sl]), "kkb": _bc(P["k_k"][sl]), "kab": _bc(P["k_a"][sl]),
            "rkb": _bc(P["r_k"][sl]), "lngb": _bc(P["lnx_g"][sl]), "lnbb": _bc(P["lnx_b"][sl]),
        })
    r3 = _run(_prog("rwkv", build_rwkv), maps)
    y2T = [np.concatenate([r3[2 * b]["yT"], r3[2 * b + 1]["yT"]], axis=0) for b in range(B)]
    r4 = _run(tlp, _tl_maps(y2T, x1T, c, ada_w[1], ada_b[1], ln_g[1], ln_b[1], f(rwkv_w_o)[0], ffn_w_in[1], ffn_w_out[1]))
    out = np.empty(x.shape, np.float32)
    for core in range(NCORES):
        b, th = core // 2, core % 2
        out[b, th * 2048:(th + 1) * 2048, :] = r4[core]["oT"].T
    return out


def kernel(x, c, ada_w, ada_b, ln_g, ln_b, ffn_w_in, ffn_w_out,
           fox_w_in, fox_b_f, fox_q_g, fox_k_g, fox_w_o,
           rwkv_mu, rwkv_w_rkv, rwkv_w0, rwkv_w1, rwkv_w2, rwkv_a0, rwkv_a1, rwkv_a2,
           rwkv_g1, rwkv_g2, rwkv_k_k, rwkv_k_a, rwkv_r_k, rwkv_lnx_g, rwkv_lnx_b, rwkv_w_o):
    f = lambda a: np.ascontiguousarray(np.asarray(a, dtype=np.float32))
    x, c, ada_w, ada_b, ln_g, ln_b = f(x), f(c), f(ada_w), f(ada_b), f(ln_g), f(ln_b)
    ffn_w_in, ffn_w_out = f(ffn_w_in), f(ffn_w_out)
    w_in = f(fox_w_in)[0]
    b_f, q_g, k_g = f(fox_b_f)[0], f(fox_q_g)[0], f(fox_k_g)[0]
    mu = f(rwkv_mu)[0]
    w_rkv = f(rwkv_w_rkv)[0]
    P = dict(w0=f(rwkv_w0)[0], w1=f(rwkv_w1)[0], w2=f(rwkv_w2)[0], a0=f(rwkv_a0)[0], a1=f(rwkv_a1)[0], a2=f(rwkv_a2)[0],
             g1=f(rwkv_g1)[0], g2=f(rwkv_g2)[0], k_k=f(rwkv_k_k)[0], k_a=f(rwkv_k_a)[0], r_k=f(rwkv_r_k)[0].reshape(-1),
             lnx_g=f(rwkv_lnx_g)[0], lnx_b=f(rwkv_lnx_b)[0])
    mu_l = np.concatenate([_col_layout(mu[n]) for n in range(6)], axis=1)
    wos = [f(fox_w_o)[0], f(rwkv_w_o)[0]]
    tl_common = []
    for L in range(2):
        tl_common.append({
            "adaw_t%d" % L: np.ascontiguousarray(ada_w[L][:, 2 * D:]), "adab_t%d" % L: _col_layout(ada_b[L][2 * D:]),
            "lng_t%d" % L: _col_layout(ln_g[L].reshape(-1)), "lnb_t%d" % L: _col_layout(ln_b[L].reshape(-1)),
            "wo_t%d" % L: wos[L], "win_t%d" % L: ffn_w_in[L], "wout_t%d" % L: ffn_w_out[L]})
    adaw_f = np.ascontiguousarray(ada_w[0][:, 0:2 * D])
    adaw_r = np.ascontiguousarray(ada_w[1][:, 0:2 * D])
    maps = []
    for core in range(NCORES):
        b, j = core // 2, core % 2
        sl = slice(j * 512, (j + 1) * 512)
        cT = _col_layout(c[b])
        m = {
            "x_f": x[b], "cT_f": cT, "adaw_f": adaw_f, "adab_f": _col_layout(ada_b[0][0:2 * D]),
            "wq_f": np.ascontiguousarray(w_in[:, 0:D][:, sl]), "wk_f": np.ascontiguousarray(w_in[:, D:2 * D][:, sl]),
            "wv_f": np.ascontiguousarray(w_in[:, 2 * D:3 * D][:, sl]), "wf_f": np.ascontiguousarray(w_in[:, 3 * D + j * 8:3 * D + j * 8 + 8]),
            "wg_f": np.ascontiguousarray(w_in[:, 3 * D + 16:][:, sl]),
            "bfb_f": _bc(b_f[j * 8:(j + 1) * 8]), "qgb_f": _bc(np.tile(q_g, 8)), "kgb_f": _bc(np.tile(k_g, 8)),
            "cT_r": cT, "adaw_r": adaw_r, "adab_r": _col_layout(ada_b[1][0:2 * D]), "mu_r": mu_l,
            "wr_r": np.ascontiguousarray(w_rkv[0][:, sl]), "wk_r": np.ascontiguousarray(w_rkv[1][:, sl]), "wv_r": np.ascontiguousarray(w_rkv[2][:, sl]),
            "w1_r": P["w1"], "a1_r": P["a1"], "g1_r": P["g1"],
            "w2_r": np.ascontiguousarray(P["w2"][:, sl]), "a2_r": np.ascontiguousarray(P["a2"][:, sl]), "g2_r": np.ascontiguousarray(P["g2"][:, sl]),
            "w0b_r": _bc(P["w0"][sl]), "a0b_r": _bc(P["a0"][sl]), "kkb_r": _bc(P["k_k"][sl]), "kab_r": _bc(P["k_a"][sl]),
            "rkb_r": _bc(P["r_k"][sl]), "lngb_r": _bc(P["lnx_g"][sl]), "lnbb_r": _bc(P["lnx_b"][sl]),
            "cT_t0": cT, "cT_t1": cT,
            "sel": np.ascontiguousarray(np.broadcast_to(np.array([1.0 - j, float(j)], np.float32)[None, :], (128, 2))),
            "xTh": np.ascontiguousarray(x[b, j * 2048:(j + 1) * 2048, :].T),
        }
        m.update(tl_common[0])
        m.update(tl_common[1])
        maps.append(m)
    res = _run(_prog("fused", build_fused), maps)
    out = np.empty(x.shape, np.float32)
    for core in range(NCORES):
        b, j = core // 2, core % 2
        out[b, j * 2048:(j + 1) * 2048, :] = res[core]["oT"].T
    return out
```

```python
import numpy as np
from contextlib import ExitStack
import concourse.bass as bass
import concourse.mybir as mybir
from concourse.bass_utils import run_bass_kernel_spmd

F32 = mybir.dt.float32
BF16 = mybir.dt.bfloat16
AF = mybir.ActivationFunctionType
ALU = mybir.AluOpType
AX = mybir.AxisListType

ENGS = ("pe", "act", "dve", "pool", "sp")
LAST_SEMS = []
SEM_CTR = [0]


def phase_end(nc):
    nc.all_engine_barrier()
    nc.clear_and_free_semaphores(list(LAST_SEMS))
    LAST_SEMS[:] = []
    nc.all_engine_barrier()


class Res:
    __slots__ = ("name", "last_w", "readers", "excl")

    def __init__(self, name, excl=False):
        self.name = name
        self.excl = excl
        self.last_w = None
        self.readers = []


class Op:
    __slots__ = ("eng", "fn", "deps", "sig", "count", "is_dma", "key", "semval", "vc", "waits", "gid")


class Deferred:
    op = None


class Sched:
    def __init__(self, nc):
        self.nc = nc
        self.ops = []
        self.dma_counts = {}

    def res(self, name, excl=False):
        return Res(name, excl)

    def _add(self, op, r, w, extra=()):
        deps = [(d, "raw") for d in extra]
        for x in r:
            if x.last_w is not None:
                deps.append((x.last_w, "raw"))
            if x.excl:
                for rd in x.readers:
                    deps.append((rd, "war"))
        for x in w:
            if x.last_w is not None:
                deps.append((x.last_w, "waw"))
            for rd in x.readers:
                deps.append((rd, "war"))
        dd = []
        for d, kind in deps:
            if d is op:
                continue
            if (not d.is_dma) and (not op.is_dma) and d.eng == op.eng:
                if op.eng == "pe" or kind == "war":
                    continue
            dd.append(d)
        op.deps = dd
        for d in dd:
            d.sig = True
        for x in w:
            x.last_w = op
            x.readers = []
        for x in r:
            if x.last_w is not op:
                if not op.is_dma:
                    x.readers = [q for q in x.readers if q.is_dma or q.eng != op.eng]
                x.readers.append(op)
        op.gid = len(self.ops)
        self.ops.append(op)
        return op

    def begin_defer(self):
        self.flush_deferred()
        self._defer = []
        self._capturing = True

    def end_defer(self):
        self._capturing = False

    def flush_deferred(self):
        q = getattr(self, "_defer", None)
        self._defer = None
        self._capturing = False
        if q:
            for kind, ph, a, kw in q:
                real = getattr(self, kind)(*a, **kw)
                if ph is not None:
                    ph.op = real
        self._deferred_done = True

    def op(self, eng, fn, r=(), w=()):
        if getattr(self, "_capturing", False):
            self._defer.append(("op", None, (eng, fn), dict(r=list(r), w=list(w))))
            return None
        o = Op()
        o.eng = eng
        o.fn = fn
        o.is_dma = False
        o.sig = False
        o.key = eng
        return self._add(o, r, w)

    def dma(self, eng, out, in_, r=(), w=(), key=None, **kw):
        if getattr(self, "_capturing", False):
            ph = Deferred()
            self._defer.append(("dma", ph, (eng, out, in_), dict(r=list(r), w=list(w), key=key, **kw)))
            return ph
        o = Op()
        o.eng = eng
        o.is_dma = True
        o.sig = True
        assert key is not None
        o.key = "dma:" + key
        o.fn = lambda e: e.dma_start(out=out, in_=in_, **kw)
        return self._add(o, r, w)

    def ext_wait(self, eng, sem, value):
        return self.op(eng, lambda e: e.wait_ge(sem, value))

    def cc(self, kind, groups, in_ap, out_ap, r=(), w=(), key=None, extra=(), ext_sem=None):
        o = Op()
        o.eng = "pool"
        o.is_dma = True
        o.sig = True
        o.key = "cc:" + key
        if ext_sem is not None:
            self.ext_sems = getattr(self, "ext_sems", {})
            self.ext_sems[o.key] = ext_sem
        o.fn = lambda e: e.collective_compute(kind, ALU.bypass, replica_groups=groups, ins=[in_ap], outs=[out_ap])
        return self._add(o, r, w, extra)

    def finalize(self, final_wait_eng="sp"):
        nc = self.nc
        counts = {e: 0 for e in ENGS}
        dmac = {}
        for o in self.ops:
            if o.is_dma:
                dmac[o.key] = dmac.get(o.key, 0) + (1 if o.key.startswith("cc:") else 16)
                o.semval = dmac[o.key]
            elif o.sig:
                counts[o.eng] += 1
                o.semval = counts[o.eng]
        seen = {e: {} for e in ENGS}
        for o in self.ops:
            s = seen[o.eng]
            need = {}
            for d in o.deps:
                if s.get(d.key, 0) < d.semval:
                    if d.key not in need or need[d.key].semval < d.semval:
                        need[d.key] = d
            o.waits = [(d.key, d.semval) for d in need.values()]
            for d in need.values():
                for k, v in d.vc.items():
                    if s.get(k, 0) < v:
                        s[k] = v
            if o.sig:
                vc = dict(s)
                vc[o.key] = o.semval
                o.vc = vc
            else:
                o.vc = None
        return counts, dmac

    def emit(self, stack, tail_waits=()):
        nc = self.nc
        self.flush_deferred()
        tail_waits = [getattr(o, "op", None) or o for o in tail_waits]
        counts, dmac = self.finalize()
        sems = {}
        ext = getattr(self, "ext_sems", {})
        for k in list(ENGS) + list(dmac):
            if k in ext:
                continue
            SEM_CTR[0] += 1
            sems[k] = nc.alloc_semaphore(name="sem%d" % SEM_CTR[0])
        self.nsems = len(sems)
        LAST_SEMS[:] = list(sems.values())
        sems.update(ext)
        per = {e: [o for o in self.ops if o.eng == e] for e in ENGS}
        block = stack.enter_context(nc.Block())

        def run(eng_name, eng):
            for o in per[eng_name]:
                for k, v in o.waits:
                    eng.wait_ge(sems[k], v)
                ins = o.fn(eng)
                if o.sig:
                    if o.is_dma:
                        ins.then_inc(sems[o.key], 1 if o.key.startswith("cc:") else 16)
                    else:
                        ins.then_inc(sems[o.key], 1)
            if eng_name == "sp":
                done = {}
                for o in tail_waits:
                    done[o.key] = max(done.get(o.key, 0), o.semval)
                for k, v in done.items():
                    eng.wait_ge(sems[k], v)

        @block.tensor
        def _(e):
            run("pe", e)

        @block.scalar
        def _(e):
            run("act", e)

        @block.vector
        def _(e):
            run("dve", e)

        @block.gpsimd
        def _(e):
            run("pool", e)

        @block.sync
        def _(e):
            run("sp", e)


D = 1024
DFF = 2816
NFC = 22
ALPHA = 4.0 ** 0.25
LN_EPS = 1e-5
EPS_P = LN_EPS / (ALPHA * ALPHA)


class Ctx:
    def __init__(self, nc, st, pfx=""):
        self.nc = nc
        self.st = st
        self.S = Sched(nc)
        self.n = 0
        self.pfx = pfx

    def sb(self, name, shape, dt):
        return self.st.enter_context(self.nc.sbuf_tensor(self.pfx + "sb_" + name, list(shape), dt))

    def ps(self, name, shape, dt=F32):
        return self.st.enter_context(self.nc.psum_tensor(self.pfx + "ps_" + name, list(shape), dt))

    def R(self, name, excl=False):
        return self.S.res(name, excl)


def make_consts(C):
    S = C.S
    C.ones = C.sb("ones", [128, 128], F32)
    C.r_ones = C.R("ones")
    S.op("pool", lambda e: e.memset(C.ones[:], 1.0), w=[C.r_ones])
    C.ident = C.sb("ident", [128, 128], F32)
    C.r_ident = C.R("ident")
    S.op("pool", lambda e: e.memset(C.ident[:], 1.0), w=[C.r_ident])
    S.op("pool", lambda e: e.affine_select(out=C.ident[:], in_=C.ident[:], pattern=[[-1, 128]], compare_op=ALU.is_equal, fill=0.0, base=0, channel_multiplier=1), r=[C.r_ident], w=[C.r_ident])


def compute_mods(C, cT_d, adaw_d, adab_d, nm, wbufs, psm, r_psm):
    S = C.S
    cT = C.sb("cT", [128, 8], F32)
    r_cT = C.R("cT")
    S.dma("sp", cT[:], cT_d, w=[r_cT], key="cT")
    S.op("act", lambda e: e.activation(out=cT[:], in_=cT[:], func=AF.Silu), r=[r_cT], w=[r_cT])
    adab = C.sb("adab", [128, nm * 8], F32)
    r_adab = C.R("adab")
    S.dma("sp", adab[:], adab_d, w=[r_adab], key="adab")
    mods = C.sb("mods", [128, nm * 8], F32)
    r_mods = C.R("mods")
    src = adaw_d.rearrange("(kc p) n -> p kc n", p=128)
    st_tmp = None
    if wbufs is None:
        st_keep, st_tmp = C.st, ExitStack()
        C.st = st_tmp
        wbufs = [(C.sb("wA%d" % i, [128, 8, 256], F32), C.R("wA%d" % i)) for i in range(2)]
        C.st = st_keep
    for m in range(nm):
        for q in range(4):
            bi = (m * 4 + q) % 2
            wt, r_wt = wbufs[bi]
            c0 = m * 1024 + q * 256
            S.dma("sp", wt[:], src[:, :, c0:c0 + 256], w=[r_wt], key="adaw%d" % bi)
            for j in range(2):
                cc = q * 2 + j
                for kc in range(8):
                    S.op("pe", lambda e, m=m, cc=cc, kc=kc, j=j, wt=wt: e.matmul(psm[:, m * 8 + cc:m * 8 + cc + 1], wt[:, kc, j * 128:(j + 1) * 128], cT[:, kc:kc + 1], start=(kc == 0), stop=(kc == 7)), r=[r_wt, r_cT], w=[r_psm])
    S.op("dve", lambda e: e.tensor_tensor(out=mods[:], in0=psm[:, 0:nm * 8], in1=adab[:], op=ALU.add), r=[r_psm, r_adab], w=[r_mods])
    if st_tmp is not None:
        st_tmp.close()
    return mods, r_mods


def ln_group(C, zT, r_z, g, n0, colA, colB, r_cols, outs, P):
    S = C.S
    ps_s, r_ps_s = P["st0"]
    ps_q, r_ps_q = P["st1"]
    sq, r_sq = P["sq"]
    for cc in range(8):
        S.op("act", lambda e, cc=cc: e.activation(out=sq[cc % 2][:], in_=zT[:, cc, n0:n0 + 512], func=AF.Square), r=[r_z[cc]], w=[r_sq[cc % 2]])
        S.op("pe", lambda e, cc=cc: e.matmul(ps_s[:], C.ones[:], zT[:, cc, n0:n0 + 512], start=(cc == 0), stop=(cc == 7)), r=[C.r_ones, r_z[cc]], w=[r_ps_s])
        S.op("pe", lambda e, cc=cc: e.matmul(ps_q[:], C.ones[:], sq[cc % 2][:], start=(cc == 0), stop=(cc == 7)), r=[C.r_ones, r_sq[cc % 2]], w=[r_ps_q])
    mean, r_mean = P["mean"]
    rstd, r_rstd = P["rstd"]
    S.op("act", lambda e: e.mul(out=mean[:], in_=ps_s[:], mul=1.0 / D), r=[r_ps_s], w=[r_mean])
    S.op("dve", lambda e: e.tensor_tensor(out=rstd[:], in0=mean[:], in1=mean[:], op=ALU.mult), r=[r_mean], w=[r_rstd])
    S.op("dve", lambda e: e.scalar_tensor_tensor(out=rstd[:], in0=ps_q[:], scalar=1.0 / D, in1=rstd[:], op0=ALU.mult, op1=ALU.subtract), r=[r_ps_q, r_rstd], w=[r_rstd])
    S.op("dve", lambda e: e.tensor_scalar(out=rstd[:], in0=rstd[:], scalar1=P["eps"], scalar2=None, op0=ALU.add), r=[r_rstd], w=[r_rstd])
    S.op("act", lambda e: e.activation(out=rstd[:], in_=rstd[:], func=AF.Ln), r=[r_rstd], w=[r_rstd])
    S.op("act", lambda e: e.activation(out=rstd[:], in_=rstd[:], func=AF.Exp, scale=-0.5), r=[r_rstd], w=[r_rstd])
    S.op("dve", lambda e: e.scalar_tensor_tensor(out=mean[:], in0=mean[:], scalar=-1.0, in1=rstd[:], op0=ALU.mult, op1=ALU.mult), r=[r_mean, r_rstd], w=[r_mean])
    tt, r_tt = P["tt"]
    for cc in range(8):
        k = cc % 2
        S.op("dve", lambda e, cc=cc, k=k: e.tensor_tensor(out=tt[k][:], in0=zT[:, cc, n0:n0 + 512], in1=rstd[:], op=ALU.mult), r=[r_z[cc], r_rstd], w=[r_tt[k]])
        S.op("dve", lambda e, cc=cc, k=k: e.tensor_tensor(out=tt[k][:], in0=tt[k][:], in1=mean[:], op=ALU.add), r=[r_tt[k], r_mean], w=[r_tt[k]])
        for (dst, r_dst, gt, go, bt, bo, eng) in outs:
            if eng == "act":
                S.op("act", lambda e, cc=cc, k=k, dst=dst, gt=gt, go=go, bt=bt, bo=bo: e.activation(out=dst[:, cc, n0:n0 + 512], in_=tt[k][:], func=AF.Identity, scale=gt[:, go + cc:go + cc + 1], bias=bt[:, bo + cc:bo + cc + 1]), r=[r_tt[k]] + r_cols, w=[r_dst[cc]])
            else:
                S.op(eng, lambda e, cc=cc, k=k, dst=dst, gt=gt, go=go, bt=bt, bo=bo: e.tensor_scalar(out=dst[:, cc, n0:n0 + 512], in0=tt[k][:], scalar1=gt[:, go + cc:go + cc + 1], scalar2=bt[:, bo + cc:bo + cc + 1], op0=ALU.mult, op1=ALU.add), r=[r_tt[k]] + r_cols, w=[r_dst[cc]])


def build_tl(NT=2048, HALF=1024, debug=False):
    nc = bass.Bass("TRN2", target_bir_lowering=False)
    dt = lambda n, s, d, k="ExternalInput": nc.dram_tensor(n, list(s), d, kind=k).ap()
    yT_d = dt("yT", [D, NT], BF16)
    xT_d = dt("xT", [D, NT], F32)
    cT_d = dt("cT", [128, 8], F32)
    adaw_d = dt("adaw", [D, 4 * D], F32)
    adab_d = dt("adab", [128, 32], F32)
    lng_d = dt("lng", [128, 16], F32)
    lnb_d = dt("lnb", [128, 16], F32)
    wo_d = dt("wo", [D, D], F32)
    win_d = dt("win", [D, 2 * DFF], F32)
    wout_d = dt("wout", [DFF, D], F32)
    oT_d = dt("oT", [D, NT], F32, "ExternalOutput")
    tl_phase(nc, "", yT_d, xT_d, cT_d, adaw_d, adab_d, lng_d, lnb_d, wo_d, win_d, wout_d, oT_d, NT, HALF, debug=debug)
    return nc


def tl_phase(nc, pfx, yT_d, xT_d, cT_d, adaw_d, adab_d, lng_d, lnb_d, wo_d, win_d, wout_d, oT_d, NT=2048, HALF=1024, debug=False, sel_d=None, after=None, before_y=None):
    dt = lambda n, s, d, k="ExternalInput": nc.dram_tensor(n, list(s), d, kind=k).ap()
    NG = HALF // 512
    if debug:
        dbg_cols = dt("dbg_cols", [128, 48], F32, "ExternalOutput")
        dbg_mods = dt("dbg_mods", [128, 32], F32, "ExternalOutput")
        dbg_x1 = dt("dbg_x1", [D, HALF], F32, "ExternalOutput")
        dbg_h = dt("dbg_h", [D, HALF], BF16, "ExternalOutput")
        dbg_a = dt("dbg_a", [DFF, HALF], BF16, "ExternalOutput")
    with ExitStack() as st:
        C = Ctx(nc, st, pfx)
        S = C.S
        make_consts(C)
        PB = [(C.ps("pb%d" % i, [128, 512]), C.R("pb%d" % i, True)) for i in range(8)]
        if sel_d is not None:
            selt = C.sb("selt", [128, 2], F32)
            r_selt = C.R("selt")
            S.dma("sp", selt[:], sel_d, w=[r_selt], key="selt")
            ystg = [[(C.sb("ystg%d_%d" % (i, j), [128, HALF], BF16), C.R("ystg%d_%d" % (i, j))) for j in range(2)] for i in range(2)]
        xT = C.sb("xT", [128, 8, HALF], F32)
        r_x = [C.R("xT%d" % cc) for cc in range(8)]
        yT = C.sb("yT", [128, 8, HALF], BF16)
        r_y = [C.R("yT%d" % cc) for cc in range(8)]
        hT = yT
        r_h = r_y
        aT = C.sb("aT", [128, NFC, HALF], BF16)
        r_a = [C.R("aT%d" % f) for f in range(NFC)]
        wo = C.sb("wo", [128, 8, D], BF16)
        r_wo = C.R("wo")
        wA = [(C.sb("wA%d" % i, [128, 8, 256], F32), C.R("wA%d" % i)) for i in range(2)]
        wi = [(C.sb("wi%d" % i, [128, 8, 512], BF16), C.R("wi%d" % i)) for i in range(2)]
        wo2 = [(C.sb("wo2%d" % i, [128, NFC, 256], BF16), C.R("wo2%d" % i)) for i in range(2)]
        P = {
            "st0": PB[6], "st1": PB[7],
            "sq": ([C.sb("sq%d" % i, [128, 512], F32) for i in range(2)], [C.R("sq%d" % i) for i in range(2)]),
            "mean": (C.sb("mean", [128, 512], F32), C.R("mean")),
            "rstd": (C.sb("rstd", [128, 512], F32), C.R("rstd")),
            "tt": ([C.sb("tt%d" % i, [128, 512], F32) for i in range(2)], [C.R("tt%d" % i) for i in range(2)]),
            "eps": EPS_P,
        }
        sl = [C.sb("sl%d" % i, [128, 512], F32) for i in range(2)]
        r_sl = [C.R("sl%d" % i) for i in range(2)]
        mods, r_mods = compute_mods(C, cT_d, adaw_d, adab_d, 4, wA, PB[5][0], PB[5][1])
        lng = C.sb("lng", [128, 16], F32)
        lnb = C.sb("lnb", [128, 16], F32)
        r_ln = C.R("ln")
        S.dma("sp", lng[:], lng_d, w=[r_ln], key="lng")
        S.dma("sp", lnb[:], lnb_d, w=[r_ln], key="lnb")
        cols = C.sb("cols", [128, 48], F32)
        r_cols = C.R("cols")
        S.op("dve", lambda e: e.tensor_scalar(out=cols[:, 0:8], in0=mods[:, 0:8], scalar1=1.0 / ALPHA, scalar2=None, op0=ALU.mult), r=[r_mods], w=[r_cols])
        S.op("dve", lambda e: e.tensor_scalar(out=cols[:, 8:16], in0=mods[:, 24:32], scalar1=1.0 / ALPHA, scalar2=None, op0=ALU.mult), r=[r_mods], w=[r_cols])
        S.op("dve", lambda e: e.tensor_scalar(out=cols[:, 32:40], in0=mods[:, 16:24], scalar1=1.0, scalar2=None, op0=ALU.add), r=[r_mods], w=[r_cols])
        S.op("dve", lambda e: e.tensor_tensor(out=cols[:, 16:24], in0=lng[:, 0:8], in1=cols[:, 32:40], op=ALU.mult), r=[r_ln, r_cols], w=[r_cols])
        S.op("dve", lambda e: e.tensor_tensor(out=cols[:, 24:32], in0=lnb[:, 0:8], in1=cols[:, 32:40], op=ALU.mult), r=[r_ln, r_cols], w=[r_cols])
        S.op("dve", lambda e: e.tensor_tensor(out=cols[:, 24:32], in0=cols[:, 24:32], in1=mods[:, 8:16], op=ALU.add), r=[r_mods, r_cols], w=[r_cols])
        rc = [r_cols, r_ln]
        S.dma("pool", wo[:], wo_d.rearrange("(kc p) n -> p kc n", p=128), w=[r_wo], key="wo")
        xsrc = xT_d.rearrange("(kc p) t -> p kc t", p=128)
        if sel_d is None:
            ysrc_ = yT_d.rearrange("(kc p) t -> p kc t", p=128)
            ysrc_fn = lambda cc, a, n: ysrc_[:, cc, a:a + n]
        else:
            ysrc_fn = lambda cc, a, n: yT_d[cc % 4, cc // 4, :, a:a + n]
        osrc = oT_d.rearrange("(kc p) t -> p kc t", p=128)
        winr = win_d.rearrange("(kc p) n -> p kc n", p=128)
        woutr = wout_d.rearrange("(fc p) n -> p fc n", p=128)
        tails = []
        pbi = 0
        for half in range(NT // HALF):
            t0 = half * HALF
            def load_y(hf):
                ty = hf * HALF
                if before_y is not None and hf == 0:
                    before_y(S)
                for cc in range(8):
                    if sel_d is None:
                        S.dma("sp", yT[:, cc, :], ysrc_fn(cc, ty, HALF), w=[r_y[cc]], key="yT%d" % cc)
                    else:
                        (ya, r_ya), (yb, r_yb) = ystg[cc % 2]
                        S.dma("sp", ya[:], ysrc_fn(cc, ty, HALF), w=[r_ya], key="ysa%d" % (cc % 2))
                        S.dma("sp", yb[:], ysrc_fn(cc, NT + ty, HALF), w=[r_yb], key="ysb%d" % (cc % 2))
                        S.op("dve", lambda e, ya=ya: e.tensor_scalar(out=ya[:], in0=ya[:], scalar1=selt[:, 0:1], scalar2=None, op0=ALU.mult), r=[r_ya, r_selt], w=[r_ya])
                        S.op("dve", lambda e, ya=ya, yb=yb, cc=cc: e.scalar_tensor_tensor(out=yT[:, cc, :], in0=yb[:], scalar=selt[:, 1:2], in1=ya[:], op0=ALU.mult, op1=ALU.add), r=[r_ya, r_yb, r_selt], w=[r_y[cc]])

            if half == 0:
                load_y(0)
            for cc in range(8):
                S.dma("sp", xT[:, cc, :], xsrc[:, cc, t0:t0 + HALF], w=[r_x[cc]], key="xT%d" % cc)
            for g in range(NG):
                n0 = g * 512
                for cc in range(8):
                    pb, r_pb = PB[pbi % 4]
                    pbi += 1
                    for kc in range(8):
                        S.op("pe", lambda e, pb=pb, cc=cc, kc=kc, n0=n0: e.matmul(pb[:], wo[:, kc, cc * 128:(cc + 1) * 128], yT[:, kc, n0:n0 + 512], start=(kc == 0), stop=(kc == 7)), r=[r_wo] + r_y, w=[r_pb])
                    S.op("dve", lambda e, pb=pb, cc=cc, n0=n0: e.scalar_tensor_tensor(out=xT[:, cc, n0:n0 + 512], in0=pb[:], scalar=cols[:, cc:cc + 1], in1=xT[:, cc, n0:n0 + 512], op0=ALU.mult, op1=ALU.add), r=[r_pb, r_x[cc], r_cols], w=[r_x[cc]])
                ln_group(C, xT, r_x, g, n0, None, None, rc,
                         [(xT, r_x, lng, 0, lnb, 0, "act"), (hT, r_h, cols, 16, cols, 24, "act")], P)
            if debug and half == 0:
                tails.append(S.dma("sp", dbg_cols, cols[:], r=[r_cols], key="dbg0"))
                tails.append(S.dma("sp", dbg_mods, mods[:], r=[r_mods], key="dbg1"))
                tails.append(S.dma("sp", dbg_x1.rearrange("(kc p) t -> p kc t", p=128), xT[:], r=r_x, key="dbg2"))
                tails.append(S.dma("sp", dbg_h.rearrange("(kc p) t -> p kc t", p=128), hT[:], r=r_h, key="dbg3"))
            for blk in range(NFC // 2):
                wt, r_wt = wi[blk % 2]
                c0 = blk * 256
                S.dma("pool", wt[:, :, 0:256], winr[:, :, c0:c0 + 256], w=[r_wt], key="wi%d" % (blk % 2))
                S.dma("pool", wt[:, :, 256:512], winr[:, :, DFF + c0:DFF + c0 + 256], w=[r_wt], key="wi%d" % (blk % 2))
                for g in range(NG):
                    n0 = g * 512
                    for j in range(2):
                        fc = blk * 2 + j
                        pg, r_pg = PB[pbi % 4]
                        pu, r_pu = PB[(pbi + 1) % 4]
                        pbi += 2
                        for kc in range(8):
                            S.op("pe", lambda e, pg=pg, wt=wt, j=j, kc=kc, n0=n0: e.matmul(pg[:], wt[:, kc, j * 128:(j + 1) * 128], hT[:, kc, n0:n0 + 512], start=(kc == 0), stop=(kc == 7)), r=[r_wt] + r_h, w=[r_pg])
                        for kc in range(8):
                            S.op("pe", lambda e, pu=pu, wt=wt, j=j, kc=kc, n0=n0: e.matmul(pu[:], wt[:, kc, 256 + j * 128:256 + (j + 1) * 128], hT[:, kc, n0:n0 + 512], start=(kc == 0), stop=(kc == 7)), r=[r_wt] + r_h, w=[r_pu])
                        k = (fc * NG + g) % 2
                        S.op("act", lambda e, pg=pg, k=k: e.activation(out=sl[k][:], in_=pg[:], func=AF.Silu), r=[r_pg], w=[r_sl[k]])
                        S.op("dve", lambda e, pu=pu, k=k, fc=fc, n0=n0: e.tensor_tensor(out=aT[:, fc, n0:n0 + 512], in0=pu[:], in1=sl[k][:], op=ALU.mult), r=[r_pu, r_sl[k]], w=[r_a[fc]])
            if half + 1 < NT // HALF:
                load_y(half + 1)
            if debug and half == 0:
                tails.append(S.dma("sp", dbg_a.rearrange("(kc p) t -> p kc t", p=128), aT[:], r=r_a, key="dbg4"))
            for blk in range(4):
                wt, r_wt = wo2[blk % 2]
                S.dma("pool", wt[:], woutr[:, :, blk * 256:(blk + 1) * 256], w=[r_wt], key="wo2%d" % (blk % 2))
                for j in range(2):
                    cc = blk * 2 + j
                    for g in range(NG):
                        n0 = g * 512
                        pb, r_pb = PB[pbi % 4]
                        pbi += 1
                        for fc in range(NFC):
                            S.op("pe", lambda e, pb=pb, wt=wt, j=j, fc=fc, n0=n0: e.matmul(pb[:], wt[:, fc, j * 128:(j + 1) * 128], aT[:, fc, n0:n0 + 512], start=(fc == 0), stop=(fc == NFC - 1)), r=[r_wt, r_a[fc]], w=[r_pb])
                        S.op("dve", lambda e, pb=pb, cc=cc, n0=n0: e.scalar_tensor_tensor(out=xT[:, cc, n0:n0 + 512], in0=pb[:], scalar=cols[:, 8 + cc:9 + cc], in1=xT[:, cc, n0:n0 + 512], op0=ALU.mult, op1=ALU.add), r=[r_pb, r_x[cc], r_cols], w=[r_x[cc]])
            for g in range(NG):
                n0 = g * 512
                ln_group(C, xT, r_x, g, n0, None, None, rc, [(xT, r_x, lng, 8, lnb, 8, "act")], P)
            for cc in range(8):
                o = S.dma("sp", osrc[:, cc, t0:t0 + HALF], xT[:, cc, :], r=[r_x[cc]], w=[], key="oT%d" % cc)
                tails.append(o)
        if after is not None:
            after(C, tails)
        S.emit(st, tail_waits=tails)


T = 4096
NH = 8
HD = 64
QK_EPS = 1e-6
NEG = -30000.0
import os
LAG = int(os.environ.get("FOX_LAG", "2"))
FOX_DVE = int(os.environ.get("FOX_DVE", "1"))


def build_fox(TT=T, debug=False):
    nc = bass.Bass("TRN2", target_bir_lowering=False)
    dt = lambda n, s, d, k="ExternalInput": nc.dram_tensor(n, list(s), d, kind=k).ap()
    ins = fox_decl(dt, TT)
    yg_d = dt("ygT", [512, TT], BF16, "ExternalOutput")
    fox_phase(nc, "", ins, yg_d, TT, debug=debug)
    return nc


def fox_decl(dt, TT=T, sfx=""):
    return dict(
        x=dt("x" + sfx, [TT, D], F32), cT=dt("cT" + sfx, [128, 8], F32), adaw=dt("adaw" + sfx, [D, 2 * D], F32), adab=dt("adab" + sfx, [128, 16], F32),
        wq=dt("wq" + sfx, [D, 512], F32), wk=dt("wk" + sfx, [D, 512], F32), wv=dt("wv" + sfx, [D, 512], F32), wf=dt("wf" + sfx, [D, 8], F32), wg=dt("wg" + sfx, [D, 512], F32),
        bfb=dt("bfb" + sfx, [128, 8], F32), qgb=dt("qgb" + sfx, [128, 512], F32), kgb=dt("kgb" + sfx, [128, 512], F32))


def fox_phase(nc, pfx, ins, yg_d, TT=T, debug=False, after=None):
    dt = lambda n, s, d, k="ExternalInput": nc.dram_tensor(n, list(s), d, kind=k).ap()
    x_d, cT_d, adaw_d, adab_d = ins["x"], ins["cT"], ins["adaw"], ins["adab"]
    wq_d, wk_d, wv_d, wf_d, wg_d = ins["wq"], ins["wk"], ins["wv"], ins["wf"], ins["wg"]
    bf_d, qg_d, kg_d = ins["bfb"], ins["qgb"], ins["kgb"]
    NGR = TT // 512
    NTL = TT // 128
    if debug:
        dbg_h = dt("dbg_h", [D, 512], BF16, "ExternalOutput")
        dbg_q = dt("dbg_q", [128, 4, 512], BF16, "ExternalOutput")
        dbg_k = dt("dbg_k", [128, 4, TT], BF16, "ExternalOutput")
        dbg_F = dt("dbg_F", [128, NTL, 8], F32, "ExternalOutput")
        dbg_v = dt("dbg_v", [128, NTL, 8, 65], BF16, "ExternalOutput")
    with ExitStack() as st:
        C = Ctx(nc, st, pfx)
        S = C.S
        make_consts(C)
        identb = C.sb("identb", [128, 128], BF16)
        r_identb = C.R("identb")
        S.op("dve", lambda e: e.tensor_copy(out=identb[:], in_=C.ident[:]), r=[C.r_ident], w=[r_identb])
        tri = C.sb("tri", [128, 128], F32)
        r_tri = C.R("tri")
        S.op("pool", lambda e: e.memset(tri[:], 1.0), w=[r_tri])
        S.op("pool", lambda e: e.affine_select(out=tri[:], in_=tri[:], pattern=[[1, 128]], compare_op=ALU.is_ge, fill=0.0, base=0, channel_multiplier=-1), r=[r_tri], w=[r_tri])
        selA = C.sb("selA", [16, 8, 128], F32)
        selB = C.sb("selB", [16, 8, 128], F32)
        sel = C.sb("sel", [16, 8, 128], BF16)
        r_sel = C.R("sel")
        S.op("pool", lambda e: e.memset(selA[:], 1.0), w=[r_sel])
        S.op("pool", lambda e: e.memset(selB[:], 1.0), w=[r_sel])
        S.op("pool", lambda e: e.affine_select(out=selA[:], in_=selA[:], pattern=[[-1, 8], [0, 128]], compare_op=ALU.is_equal, fill=0.0, base=0, channel_multiplier=1), r=[r_sel], w=[r_sel])
        S.op("pool", lambda e: e.affine_select(out=selB[:], in_=selB[:], pattern=[[-1, 8], [0, 128]], compare_op=ALU.is_equal, fill=0.0, base=-8, channel_multiplier=1), r=[r_sel], w=[r_sel])
        S.op("dve", lambda e: e.tensor_tensor(out=sel[:], in0=selA[:], in1=selB[:], op=ALU.add), r=[r_sel], w=[r_sel])
        maskf = C.sb("maskf", [128, 4, 512], F32)
        maskb = C.sb("maskb", [128, 4, 512], BF16)
        r_mask = C.R("mask")
        S.op("pool", lambda e: e.memset(maskf[:], 0.0), w=[r_mask])
        for j in range(4):
            S.op("pool", lambda e, j=j: e.affine_select(out=maskf[:, j, :], in_=maskf[:, j, :], pattern=[[1, 512]], compare_op=ALU.is_ge, fill=NEG, base=-128 * j, channel_multiplier=-1), r=[r_mask], w=[r_mask])
        S.op("dve", lambda e: e.tensor_copy(out=maskb[:], in_=maskf[:]), r=[r_mask], w=[r_mask])
        PB = [(C.ps("pb%d" % i, [128, 512]), C.R("pb%d" % i, True)) for i in range(7)]
        pbb = C.ps("pbb", [128, 1024], BF16)
        r_pbb = C.R("pbb", True)
        wA = [(C.sb("wA%d" % i, [128, 8, 256], F32), C.R("wA%d" % i)) for i in range(2)]
        mods, r_mods = compute_mods(C, cT_d, adaw_d, adab_d, 2, wA, PB[6][0], PB[6][1])
        cols = C.sb("cols", [128, 8], F32)
        r_cols = C.R("cols")
        S.op("dve", lambda e: e.tensor_scalar(out=cols[:], in0=mods[:, 8:16], scalar1=1.0, scalar2=None, op0=ALU.add), r=[r_mods], w=[r_cols])
        wts = {}
        r_w = C.R("wts")
        for nm, d_ in (("wq", wq_d), ("wk", wk_d), ("wv", wv_d), ("wg", wg_d)):
            wts[nm] = C.sb(nm, [128, 8, 512], BF16)
            S.dma("pool", wts[nm][:], d_.rearrange("(kc p) n -> p kc n", p=128), w=[r_w], key=nm)
        wf = C.sb("wf", [128, 8, 8], BF16)
        S.dma("pool", wf[:], wf_d.rearrange("(kc p) n -> p kc n", p=128), w=[r_w], key="wf")
        bfb = C.sb("bfb", [128, 8], F32)
        qgb = C.sb("qgb", [128, 512], F32)
        kgb = C.sb("kgb", [128, 512], F32)
        r_par = C.R("par")
        S.dma("sp", bfb[:], bf_d, w=[r_par], key="bfb")
        S.dma("sp", qgb[:], qg_d, w=[r_par], key="qgb")
        S.dma("sp", kgb[:], kg_d, w=[r_par], key="kgb")
        xt = C.sb("xt", [128, 4, D], F32)
        r_xt = C.R("xt")
        hT = C.sb("hT", [128, 8, 512], BF16)
        r_hT = [C.R("hT%d" % k) for k in range(8)]
        kT = C.sb("kT", [128, 4, TT], BF16)
        r_kT = [C.R("kT%d" % g) for g in range(NGR)]
        qT = C.sb("qT", [128, 4, 512], BF16)
        r_qT = C.R("qT")
        Va = C.sb("Va", [128, NTL, NH, 65], BF16)
        r_V = [C.R("V%d" % g) for g in range(NGR)]
        r_Vones = C.R("Vones")
        S.op("pool", lambda e: e.memset(Va[:, :, :, 64:65], 1.0), w=[r_Vones])
        eg = C.sb("eg", [64, NH, 512], F32)
        r_eg = [C.R("eg%d" % h) for h in range(NH)]
        Fcol = C.sb("Fcol", [128, NTL, NH], F32)
        r_F = C.R("Fcol")
        carry = C.sb("carry", [128, NTL + 1, NH], F32)
        r_carry = C.R("carry")
        S.op("dve", lambda e: e.memset(carry[:, 0, :], 0.0), w=[r_carry])
        biasG = C.sb("biasG", [128, NTL, NH], F32)
        r_bias = C.R("biasG")
        FqT = C.sb("FqT", [16, 512], BF16)
        r_FqT = C.R("FqT")
        sq = C.sb("sq", [128, 512], F32); r_sq = C.R("sq")
        tmp = C.sb("tmp", [128, 512], F32); r_tmp = C.R("tmp")
        qtok = C.sb("qtok", [128, 512], BF16); r_qtok = C.R("qtok")
        ktok = C.sb("ktok", [128, 512], BF16); r_ktok = C.R("ktok")
        ssq = C.sb("ssq", [128, 16], F32); r_ssq = C.R("ssq")
        zf = C.sb("zf", [128, 8], F32); r_zf = C.R("zf")
        fq = C.sb("fq", [128, 8], F32); r_fq = C.R("fq")
        fq16 = C.sb("fq16", [128, 16], BF16); r_fq16 = C.R("fq16")
        PT = [(C.sb("PT%d" % i, [128, 512], BF16), C.R("PT%d" % i)) for i in range(4)]
        oT = C.sb("oT", [64, 512], F32); r_oT = C.R("oT")
        dn = C.sb("dn", [128, 512], F32); r_dn = C.R("dn")
        wv_ = C.sb("wv_", [64, 512], F32); r_wv_ = C.R("wv_")
        yg = [(C.sb("yg%d" % i, [64, 512], BF16), C.R("yg%d" % i)) for i in range(2)]
        tails = []
        xsrc = x_d.rearrange("(i p) d -> p i d", p=128)
        ps_i = 0
        for G in range(NGR):
            if G == 0:
                S.dma("sp", xt[:], xsrc[:, 0:4, :], w=[r_xt], key="xt")
            for kc in range(8):
                pb, r_pb = PB[kc % 2]
                for i in range(4):
                    S.op("pe", lambda e, pb=pb, i=i, kc=kc: e.transpose(pb[:, i * 128:(i + 1) * 128], xt[:, i, kc * 128:(kc + 1) * 128], C.ident[:]), r=[r_xt, C.r_ident], w=[r_pb])
                if kc % 2 == 0:
                    S.op("act", lambda e, pb=pb, kc=kc: e.activation(out=hT[:, kc, :], in_=pb[:], func=AF.Identity, scale=cols[:, kc:kc + 1], bias=mods[:, kc:kc + 1]), r=[r_pb, r_cols, r_mods], w=[r_hT[kc]])
                else:
                    S.op("dve", lambda e, pb=pb, kc=kc: e.tensor_scalar(out=hT[:, kc, :], in0=pb[:], scalar1=cols[:, kc:kc + 1], scalar2=mods[:, kc:kc + 1], op0=ALU.mult, op1=ALU.add), r=[r_pb, r_cols, r_mods], w=[r_hT[kc]])
            if debug and G == 0:
                tails.append(S.dma("sp", dbg_h.rearrange("(kc p) t -> p kc t", p=128), hT[:], r=r_hT, key="dbg0"))
            if G + 1 < NGR:
                S.dma("sp", xt[:], xsrc[:, 4 * (G + 1):4 * (G + 1) + 4, :], w=[r_xt], key="xt")
            for i in range(4):
                tl_ = 4 * G + i
                (pq, r_pq), (pk, r_pk), (pv, r_pv), (pf, r_pf) = PB[2], PB[3], PB[4], PB[5]
                for kc in range(8):
                    lw = hT[:, kc, i * 128:(i + 1) * 128]
                    S.op("pe", lambda e, lw=lw, kc=kc: e.matmul(pq[:], lw, wts["wq"][:, kc, :], start=(kc == 0), stop=(kc == 7)), r=[r_w, r_hT[kc]], w=[r_pq])
                    S.op("pe", lambda e, lw=lw, kc=kc: e.matmul(pk[:], lw, wts["wk"][:, kc, :], start=(kc == 0), stop=(kc == 7)), r=[r_w, r_hT[kc]], w=[r_pk])
                    S.op("pe", lambda e, lw=lw, kc=kc: e.matmul(pv[:], lw, wts["wv"][:, kc, :], start=(kc == 0), stop=(kc == 7)), r=[r_w, r_hT[kc]], w=[r_pv])
                    S.op("pe", lambda e, lw=lw, kc=kc: e.matmul(pf[:, 0:8], lw, wf[:, kc, :], start=(kc == 0), stop=(kc == 7)), r=[r_w, r_hT[kc]], w=[r_pf])
                for h in (2 * i, 2 * i + 1):
                    pb, r_pb = PB[h % 2]
                    for kc in range(8):
                        S.op("pe", lambda e, pb=pb, h=h, kc=kc: e.matmul(pb[0:64, :], wts["wg"][:, kc, h * 64:(h + 1) * 64], hT[:, kc, :], start=(kc == 0), stop=(kc == 7)), r=[r_w, r_hT[kc]], w=[r_pb])
                    S.op("act", lambda e, pb=pb, h=h: e.activation(out=eg[:, h, :], in_=pb[0:64, :], func=AF.Exp, scale=-1.0), r=[r_pb], w=[r_eg[h]])
                for which, (pp, r_pp), gb, tok, r_tok, eps_, mul_ in (("q", (pq, r_pq), qgb, qtok, r_qtok, 64.0 * QK_EPS, 1.0), ("k", (pk, r_pk), kgb, ktok, r_ktok, QK_EPS, 1.0 / 64.0)):
                    o8 = 0 if which == "q" else 8
                    S.op("act", lambda e, pp=pp: e.activation(out=sq[:], in_=pp[:], func=AF.Square), r=[r_pp], w=[r_sq])
                    S.op("dve", lambda e, o8=o8: e.tensor_reduce(out=ssq[:, o8:o8 + 8], in_=sq[:].rearrange("p (h d) -> p h d", d=64), axis=AX.X, op=ALU.add), r=[r_sq], w=[r_ssq])
                    S.op("dve", lambda e, o8=o8, eps_=eps_, mul_=mul_: e.tensor_scalar(out=ssq[:, o8:o8 + 8], in0=ssq[:, o8:o8 + 8], scalar1=mul_, scalar2=eps_, op0=ALU.mult, op1=ALU.add), r=[r_ssq], w=[r_ssq])
                    S.op("act", lambda e, o8=o8: e.activation(out=ssq[:, o8:o8 + 8], in_=ssq[:, o8:o8 + 8], func=AF.Ln), r=[r_ssq], w=[r_ssq])
                    S.op("act", lambda e, o8=o8: e.activation(out=ssq[:, o8:o8 + 8], in_=ssq[:, o8:o8 + 8], func=AF.Exp, scale=-0.5), r=[r_ssq], w=[r_ssq])
                    S.op("dve", lambda e, pp=pp, o8=o8: e.tensor_tensor(out=tmp[:].rearrange("p (h d) -> p h d", d=64), in0=pp[:].rearrange("p (h d) -> p h d", d=64), in1=ssq[:, o8:o8 + 8].unsqueeze(2).to_broadcast([128, 8, 64]), op=ALU.mult), r=[r_pp, r_ssq], w=[r_tmp])
                    S.op("dve", lambda e, gb=gb, tok=tok: e.tensor_tensor(out=tok[:], in0=tmp[:], in1=gb[:], op=ALU.mult), r=[r_tmp, r_par], w=[r_tok])
                S.op("act", lambda e, tl_=tl_: e.copy(out=Va[:, tl_, :, 0:64], in_=pv[:].rearrange("p (h d) -> p h d", d=64)), r=[r_pv, r_Vones], w=[r_V[G]])
                S.op("dve", lambda e: e.tensor_tensor(out=zf[:], in0=pf[:, 0:8], in1=bfb[:], op=ALU.add), r=[r_pf, r_par], w=[r_zf])
                S.op("act", lambda e: e.activation(out=zf[:], in_=zf[:], func=AF.Exp, scale=-1.0), r=[r_zf], w=[r_zf])
                S.op("act", lambda e: e.activation(out=zf[:], in_=zf[:], func=AF.Ln, bias=1.0), r=[r_zf], w=[r_zf])
                pc, r_pc = PB[6]
                S.op("pe", lambda e: e.matmul(pc[:, 0:8], tri[:], zf[:], start=True, stop=True), r=[r_tri, r_zf], w=[r_pc])
                S.op("pe", lambda e: e.matmul(pc[:, 8:16], C.ones[:], zf[:], start=True, stop=True), r=[C.r_ones, r_zf], w=[r_pc])
                S.op("dve", lambda e, tl_=tl_: e.tensor_tensor(out=Fcol[:, tl_, :], in0=carry[:, tl_, :], in1=pc[:, 0:8], op=ALU.subtract), r=[r_carry, r_pc], w=[r_F])
                S.op("dve", lambda e, tl_=tl_: e.tensor_tensor(out=carry[:, tl_ + 1, :], in0=carry[:, tl_, :], in1=pc[:, 8:16], op=ALU.subtract), r=[r_carry, r_pc], w=[r_carry])
                S.op("dve", lambda e, tl_=tl_, G=G: e.tensor_tensor(out=fq[:], in0=Fcol[:, tl_, :], in1=carry[:, 4 * G, :], op=ALU.subtract), r=[r_F, r_carry], w=[r_fq])
                S.op("dve", lambda e: e.tensor_copy(out=fq16[:, 0:8], in_=fq[:]), r=[r_fq], w=[r_fq16])
                S.op("dve", lambda e: e.tensor_tensor(out=fq[:], in0=fq[:], in1=fq16[:, 0:8], op=ALU.subtract), r=[r_fq, r_fq16], w=[r_fq])
                S.op("dve", lambda e: e.tensor_copy(out=fq16[:, 8:16], in_=fq[:]), r=[r_fq], w=[r_fq16])
                for p in range(4):
                    S.op("pe", lambda e, p=p: e.transpose(pbb[:, p * 128:(p + 1) * 128], qtok[:, p * 128:(p + 1) * 128], identb[:]), r=[r_qtok, r_identb], w=[r_pbb])
                for p in range(4):
                    S.op("pe", lambda e, p=p: e.transpose(pbb[:, 512 + p * 128:512 + (p + 1) * 128], ktok[:, p * 128:(p + 1) * 128], identb[:]), r=[r_ktok, r_identb], w=[r_pbb])
                S.op("act", lambda e, i=i: e.copy(out=qT[:, :, i * 128:(i + 1) * 128], in_=pbb[:, 0:512].rearrange("p (a t) -> p a t", t=128)), r=[r_pbb], w=[r_qT])
                S.op("dve", lambda e, tl_=tl_: e.tensor_copy(out=kT[:, :, tl_ * 128:(tl_ + 1) * 128], in_=pbb[:, 512:1024].rearrange("p (a t) -> p a t", t=128)), r=[r_pbb], w=[r_kT[G]])
                S.op("pe", lambda e: e.transpose(pbb[0:16, 0:128], fq16[:, 0:16], identb[:]), r=[r_fq16, r_identb], w=[r_pbb])
                S.op("dve", lambda e, i=i: e.tensor_copy(out=FqT[:, i * 128:(i + 1) * 128], in_=pbb[0:16, 0:128]), r=[r_pbb], w=[r_FqT])
            nkb = 4 * G + 4
            for h in range(NH):
                S.op("dve", lambda e, h=h, nkb=nkb, G=G: e.tensor_scalar(out=biasG[:, 0:nkb, h], in0=Fcol[:, 0:nkb, h], scalar1=-1.0, scalar2=carry[:, 4 * G, h:h + 1], op0=ALU.mult, op1=ALU.add), r=[r_F, r_carry], w=[r_bias])
            tiles = [(2 * p + e, kb) for p in range(NH // 2) for kb in range(nkb) for e in range(2)]

            def emit_pv(h, kb, PTt, r_PT, nkb=nkb, G=G):
                pO, r_pO = PB[4 + h % 2]
                gk = kb // 4
                S.op("pe", lambda e, pO=pO, PTt=PTt, kb=kb, h=h, nkb=nkb: e.matmul(pO[0:65, :], Va[:, kb, h, :], PTt[:], start=(kb == 0), stop=(kb == nkb - 1)), r=[r_V[gk], r_Vones, r_PT], w=[r_pO])
                if kb == nkb - 1:
                    pB_, r_pB = PB[6]
                    ygt, r_yg = yg[h % 2]
                    S.op("act", lambda e, pO=pO: e.copy(out=oT[:], in_=pO[0:64, :]), r=[r_pO], w=[r_oT])
                    S.op("dve", lambda e, pO=pO: e.tensor_copy(out=dn[64:65, :], in_=pO[64:65, :]), r=[r_pO], w=[r_dn])
                    S.op("pe", lambda e: e.matmul(pB_[0:64, :], C.ones[64:65, 0:64], dn[64:65, :], start=True, stop=True), r=[C.r_ones, r_dn], w=[r_pB])
                    S.op("dve", lambda e, h=h: e.scalar_tensor_tensor(out=wv_[:], in0=eg[:, h, :], scalar=1.0, in1=pB_[0:64, :], op0=ALU.add, op1=ALU.mult), r=[r_eg[h], r_pB], w=[r_wv_])
                    S.op("dve", lambda e: e.reciprocal(out=wv_[:], in_=wv_[:]), r=[r_wv_], w=[r_wv_])
                    S.op("dve", lambda e, ygt=ygt: e.tensor_tensor(out=ygt[:], in0=oT[:], in1=wv_[:], op=ALU.mult), r=[r_oT, r_wv_], w=[r_yg])
                    tails.append(S.dma("sp", yg_d[h * 64:(h + 1) * 64, G * 512:(G + 1) * 512], ygt[:], r=[r_yg], key="yg%d" % (h % 2)))

            pendq = []
            for (h, kb) in tiles:
                p_, e_ = h // 2, h % 2
                pS, r_pS = PB[ps_i % 4]
                PTt, r_PT = PT[ps_i % 4]
                ps_i += 1
                diag = kb >= 4 * G
                gk = kb // 4
                if FOX_DVE:
                    fqb, r_fqb = ((tmp, r_tmp), (sq, r_sq))[h % 2]
                    if kb == 0:
                        pB_, r_pB = PB[6]
                        S.op("pe", lambda e, h=h: e.matmul(pB_[:], sel[:, h, :], FqT[:], start=True, stop=True), r=[r_sel, r_FqT], w=[r_pB])
                        S.op("dve", lambda e, fqb=fqb: e.tensor_copy(out=fqb[:], in_=pB_[:]), r=[r_pB], w=[r_fqb])
                    S.op("pe", lambda e, pS=pS, p_=p_, e_=e_, kb=kb, diag=diag: e.matmul(pS[:], kT[e_ * 64:(e_ + 1) * 64, p_, kb * 128:(kb + 1) * 128], qT[e_ * 64:(e_ + 1) * 64, p_, :], start=True, stop=(not diag)), r=[r_kT[gk], r_qT], w=[r_pS])
                    if diag:
                        S.op("pe", lambda e, pS=pS, kb=kb, G=G: e.matmul(pS[:], identb[:], maskb[:, kb - 4 * G, :], start=False, stop=True), r=[r_identb, r_mask], w=[r_pS])
                    S.op("dve", lambda e, pS=pS, fqb=fqb: e.tensor_tensor(out=pS[:], in0=pS[:], in1=fqb[:], op=ALU.add), r=[r_pS, r_fqb], w=[r_pS])
                else:
                    S.op("pe", lambda e, pS=pS, p_=p_, e_=e_, kb=kb: e.matmul(pS[:], kT[e_ * 64:(e_ + 1) * 64, p_, kb * 128:(kb + 1) * 128], qT[e_ * 64:(e_ + 1) * 64, p_, :], start=True, stop=False), r=[r_kT[gk], r_qT], w=[r_pS])
                    S.op("pe", lambda e, pS=pS, h=h, diag=diag: e.matmul(pS[:], sel[:, h, :], FqT[:], start=False, stop=(not diag)), r=[r_sel, r_FqT], w=[r_pS])
                    if diag:
                        S.op("pe", lambda e, pS=pS, kb=kb, G=G: e.matmul(pS[:], identb[:], maskb[:, kb - 4 * G, :], start=False, stop=True), r=[r_identb, r_mask], w=[r_pS])
                S.op("act", lambda e, pS=pS, PTt=PTt, kb=kb, h=h: e.activation(out=PTt[:], in_=pS[:], func=AF.Exp, bias=biasG[:, kb, h:h + 1], scale=1.0), r=[r_pS, r_bias], w=[r_PT])
                pendq.append((h, kb, PTt, r_PT))
                if h % 2 == 1:
                    while len(pendq) > 2:
                        emit_pv(*pendq.pop(0))
            while pendq:
                emit_pv(*pendq.pop(0))
            if debug and G == 0:
                tails.append(S.dma("sp", dbg_q, qT[:], r=[r_qT], key="dbg1"))
        if debug:
            tails.append(S.dma("sp", dbg_k, kT[:], r=r_kT, key="dbg2"))
            tails.append(S.dma("sp", dbg_F, Fcol[:], r=[r_F], key="dbg3"))
            tails.append(S.dma("sp", dbg_v, Va[:], r=r_V + [r_Vones], key="dbg4"))
        if after is not None:
            after(C, tails)
        S.emit(st, tail_waits=tails)

import math, os
SEQ_VAR = 0

NH = 8
HD = 64
C0 = math.exp(-0.5)
GN_EPS = 64 * 1e-5


class TR:
    def __init__(self, C, name, shape, dt, psum=False):
        self.t = C.ps(name, shape, dt) if psum else C.sb(name, shape, dt)
        self.r = C.R(name, excl=psum)


class _Stop(Exception):
    pass


def build_rwkv(TT=4096, debug=False, stop=None):
    nc = bass.Bass("TRN2", target_bir_lowering=False)
    dt = lambda n, s, d, k="ExternalInput": nc.dram_tensor(n, list(s), d, kind=k).ap()
    xT_d = dt("xT", [D, TT], F32)
    ins = rwkv_decl(dt)
    yT_d = dt("yT", [512, TT], BF16, "ExternalOutput")
    xv = xT_d.rearrange("(kc p) t -> p kc t", p=128)
    rwkv_phase(nc, "", lambda t0, n: xv[:, :, t0:t0 + n], ins, yT_d, TT, debug=debug, stop=stop)
    return nc


RW_BC = ["w0b", "a0b", "kkb", "kab", "rkb", "lngb", "lnbb"]


def rwkv_decl(dt, sfx=""):
    d_ = dict(cT=dt("cT" + sfx, [128, 8], F32), adaw=dt("adaw" + sfx, [D, 2 * D], F32), adab=dt("adab" + sfx, [128, 16], F32), mu=dt("mu" + sfx, [128, 48], F32),
              wr=dt("wr" + sfx, [D, 512], F32), wk=dt("wk" + sfx, [D, 512], F32), wv=dt("wv" + sfx, [D, 512], F32),
              w1=dt("w1" + sfx, [D, 64], F32), a1=dt("a1" + sfx, [D, 64], F32), g1=dt("g1" + sfx, [D, 128], F32),
              w2=dt("w2" + sfx, [64, 512], F32), a2=dt("a2" + sfx, [64, 512], F32), g2=dt("g2" + sfx, [128, 512], F32))
    for n in RW_BC:
        d_[n] = dt(n + sfx, [128, 512], F32)
    return d_


def rwkv_phase(nc, pfx, xsrc_fn, ins, yT_d, TT=4096, debug=False, stop=None, after=None, before_x=None):
    dt = lambda n, s, d, k="ExternalInput": nc.dram_tensor(n, list(s), d, kind=k).ap()
    cT_d, adaw_d, adab_d, mu_d = ins["cT"], ins["adaw"], ins["adab"], ins["mu"]
    wr_d, wk_d, wv_d, w1_d, a1_d, g1_d, w2_d, a2_d, g2_d = [ins[k] for k in ("wr", "wk", "wv", "w1", "a1", "g1", "w2", "a2", "g2")]
    bc_names = RW_BC
    bc_d = {n: ins[n] for n in bc_names}
    GS = 256
    NTG = GS // 128
    NGR = TT // GS
    dbg = {}
    if debug:
        for n in ["r", "kp", "kkn", "a", "sw", "Y", "bonus", "Gm", "yn"]:
            dbg[n] = dt("dbg_" + n, [128, 512], F32, "ExternalOutput")
        dbg["S"] = dt("dbg_S", [128, 4, 64], F32, "ExternalOutput")
        for n, shp in (("Arb", [128, 8, 128]), ("Aak", [128, 8, 128]), ("NT0", [128, 8, 128]), ("Z0", [128, 8, 128]), ("Zf", [128, 8, 128]), ("Y0", [128, 512]), ("RbT", [128, 4, 128]), ("Qm", [128, 4, 128]), ("Hm", [128, 4, 128]), ("gcol", [128, 8]), ("v", [128, 512]), ("ART", [128, 4, 2, 128]), ("KT", [128, 4, 128]), ("BT", [128, 4, 128]), ("Bh", [128, 512]), ("Kh", [128, 512])):
            dbg[n] = dt("dbg_" + n, shp, F32, "ExternalOutput")
    with ExitStack() as st:
        C = Ctx(nc, st, pfx)
        S = C.S
        make_consts(C)
        ident, ones = C.ident, C.ones
        r_ident, r_ones = C.r_ident, C.r_ones
        tails = []
        identb = TR(C, "identb", [128, 128], BF16)
        S.op("dve", lambda e: e.tensor_copy(out=identb.t[:], in_=ident[:]), r=[r_ident], w=[identb.r])
        U = TR(C, "U", [128, 128], F32)
        IU = TR(C, "IUf", [128, 128], F32)
        S.op("pool", lambda e: e.memset(U.t[:], 1.0), w=[U.r])
        S.op("pool", lambda e: e.affine_select(out=U.t[:], in_=U.t[:], pattern=[[1, 128]], compare_op=ALU.is_gt, fill=0.0, base=0, channel_multiplier=-1), r=[U.r], w=[U.r])
        S.op("pool", lambda e: e.memset(IU.t[:], 1.0), w=[IU.r])
        S.op("pool", lambda e: e.affine_select(out=IU.t[:], in_=IU.t[:], pattern=[[1, 128]], compare_op=ALU.is_ge, fill=0.0, base=0, channel_multiplier=-1), r=[IU.r], w=[IU.r])
        SLf = TR(C, "SLf", [128, 128], F32)
        S.op("pool", lambda e: e.memset(SLf.t[:], 1.0), w=[SLf.r])
        S.op("pool", lambda e: e.affine_select(out=SLf.t[:], in_=SLf.t[:], pattern=[[-1, 128]], compare_op=ALU.is_gt, fill=0.0, base=0, channel_multiplier=1), r=[SLf.r], w=[SLf.r])
        mask1 = TR(C, "mask1", [128, 128], F32)
        maskL = TR(C, "maskL", [128, 64], F32)
        idl = TR(C, "idl", [128, 64], F32)
        for ch in range(2):
            ps_ = slice(ch * 64, ch * 64 + 64)
            S.op("dve", lambda e, ps_=ps_: e.tensor_copy(out=mask1.t[ps_, 0:64], in_=U.t[ps_, ps_]), r=[U.r], w=[mask1.r])
            S.op("dve", lambda e, ps_=ps_: e.tensor_copy(out=mask1.t[ps_, 64:128], in_=IU.t[ps_, ps_]), r=[IU.r], w=[mask1.r])
            S.op("dve", lambda e, ps_=ps_: e.tensor_copy(out=maskL.t[ps_, :], in_=SLf.t[ps_, ps_]), r=[SLf.r], w=[maskL.r])
            S.op("dve", lambda e, ps_=ps_: e.tensor_copy(out=idl.t[ps_, :], in_=ident[ps_, ps_]), r=[r_ident], w=[idl.r])
        triC = TR(C, "triC", [128, 128], F32)
        blkC = TR(C, "blkC", [128, 128], F32)
        S.op("dve", lambda e: e.tensor_scalar(out=triC.t[:], in0=IU.t[:], scalar1=-C0, scalar2=None, op0=ALU.mult), r=[IU.r], w=[triC.r])
        S.op("dve", lambda e: e.memset(triC.t[0:64, 64:128], 0.0), w=[triC.r])
        S.op("dve", lambda e: e.memset(blkC.t[:], -C0), w=[blkC.r])
        S.op("dve", lambda e: e.memset(blkC.t[0:64, 64:128], 0.0), w=[blkC.r])
        S.op("dve", lambda e: e.memset(blkC.t[64:128, 0:64], 0.0), w=[blkC.r])
        avg = TR(C, "avg", [128, 1], F32)
        S.op("dve", lambda e: e.memset(avg.t[:], 1.0 / 64.0), w=[avg.r])
        PB = [TR(C, "pb%d" % i, [128, 512], F32, psum=True) for i in range(7)]
        pbb = TR(C, "pbb", [128, 1024], BF16, psum=True)
        mu = TR(C, "mu", [128, 48], F32)
        S.dma("sp", mu.t[:], mu_d, w=[mu.r], key="mu")
        W = {}
        r_w = C.R("wts")
        for nm, d_, shp in (("wr", wr_d, [128, 8, 512]), ("wk", wk_d, [128, 8, 512]), ("wv", wv_d, [128, 8, 512]), ("w1", w1_d, [128, 8, 64]), ("a1", a1_d, [128, 8, 64]), ("g1", g1_d, [128, 8, 128])):
            W[nm] = C.sb(nm, shp, BF16)
            S.dma("pool", W[nm][:], d_.rearrange("(kc p) n -> p kc n", p=128), w=[r_w], key=nm)
        for nm, d_, shp in (("w2", w2_d, [64, 512]), ("a2", a2_d, [64, 512]), ("g2", g2_d, [128, 512])):
            W[nm] = C.sb(nm, shp, BF16)
            S.dma("pool", W[nm][:], d_, w=[r_w], key=nm)
        BC = {}
        r_bc = C.R("bc")
        for n in bc_names:
            BC[n] = C.sb(n, [128, 512], F32)
            S.dma("sp", BC[n][:], bc_d[n], w=[r_bc], key=n)
        xg = TR(C, "xg", [128, 8, GS + 1], F32)
        S.op("dve", lambda e: e.memset(xg.t[:, :, 0:1], 0.0), w=[xg.r])
        Nm = [TR(C, "Nm%d" % i, [128, 8, 128], BF16) for i in range(2)]
        NTm = [TR(C, "NTm%d" % i, [128, 8, 128], BF16) for i in range(2)]
        for t_ in Nm + NTm:
            S.op("pool", lambda e, t_=t_: e.memset(t_.t[:], 0.0), w=[t_.r])
        ST = [TR(C, "ST%d" % i, [128, 4, 64], F32) for i in range(2)]
        S.op("dve", lambda e: e.memset(ST[0].t[:], 0.0), w=[ST[0].r])
        mods, r_mods = compute_mods(C, cT_d, adaw_d, adab_d, 2, None, PB[6].t, PB[6].r)
        cols = TR(C, "cols", [128, 8], F32)
        S.op("dve", lambda e: e.tensor_scalar(out=cols.t[:], in0=mods[:, 8:16], scalar1=1.0, scalar2=None, op0=ALU.add), r=[r_mods], w=[cols.r])
        dx = TR(C, "dx", [128, 8, GS], F32)
        xs = [TR(C, "xs%d" % i, [128, 8, GS], BF16) for i in range(2)]
        l1 = [TR(C, "l1_%d" % i, [128, GS], BF16) for i in range(3)]
        l1f = TR(C, "l1f", [128, GS], F32)
        G_ = {n: TR(C, "G_" + n, [128, NTG, 512], F32) for n in ("r", "k", "v")}
        names = ["sw", "Gm", "Gi", "Gp", "GC", "a", "kkn", "kp", "t1", "t2", "Bt", "Kt", "Bh", "Kh", "bonus", "Y", "yn", "gg", "At", "Rt", "glg", "glg2", "bonus2", "t3", "At2", "Rt2", "Kt2", "Bt2", "Bh2", "Kh2", "GC2", "V0", "V2"]
        X = {n: TR(C, "X_" + n, [128, 512], F32) for n in names if n not in ("Bh", "Kh", "Bh2", "Kh2")}
        for n in ("Bh", "Kh", "Bh2", "Kh2", "Vb0", "Vb2"):
            X[n] = TR(C, "X_" + n, [128, 512], BF16)
        Zfb = TR(C, "Zfb", [128, 8, 128], BF16)
        Arbb = TR(C, "Arbb", [128, 8, 64], BF16)
        sm = TR(C, "sm", [128, 64], F32)
        smT = TR(C, "smT", [128, 64], F32)
        Z = [TR(C, "Z%d" % i, [128, 8, 128], BF16) for i in range(2)]
        Zf32 = TR(C, "Zf32", [128, 8, 128], F32)
        ART = TR(C, "ART", [128, 4, 2, 128], F32)
        KT = TR(C, "KT", [128, 4, 128], F32)
        BT = TR(C, "BT", [128, 4, 128], F32)
        Arb = TR(C, "Arb", [128, 8, 128], BF16)
        Aak = TR(C, "Aak", [128, 8, 128], BF16)
        RbT = TR(C, "RbT", [128, 4, 128], F32)
        Qm = TR(C, "Qm", [128, 4, 128], F32)
        Hm = TR(C, "Hm", [128, 4, 128], F32)
        gcol = TR(C, "gcol", [128, 8], F32)
        Y0 = TR(C, "Y0", [128, 512], F32)
        ybf = TR(C, "ybf", [128, 512], BF16)
        yTo = [TR(C, "yTo%d" % i, [128, 4, 128], BF16) for i in range(2)]
        ysrc = yT_d.rearrange("(a p) t -> p a t", p=128)

        def hv(t):
            return t.rearrange("p (h d) -> p h d", d=64)

        def bc8(ap):
            return ap.unsqueeze(2).to_broadcast([128, 8, 64])

        sti = [0]

        def tile_front_a(i, G):
            tl_ = G * NTG + i
            r_ = G_["r"].t[:, i, :]; k_ = G_["k"].t[:, i, :]; v_ = G_["v"].t[:, i, :]
            rG = [G_[n].r for n in G_]
            sw, Gm, Gi, Gp, a_, kkn, kp, t1, t2 = [X[n] for n in ("sw", "Gm", "Gi", "Gp", "a", "kkn", "kp", "t1", "t2")]
            Yt, yn = X["Y"], X["yn"]
            sfx = "" if tl_ % 2 == 0 else "2"
            Bt, Kt, Bh, Kh, GC, At_, Rt = [X[n + sfx] if (n + sfx) in X else X[n] for n in ("Bt", "Kt", "Bh", "Kh", "GC", "At", "Rt")]
            Vt = X["V0"] if tl_ % 2 == 0 else X["V2"]
            Vb = X["Vb0"] if tl_ % 2 == 0 else X["Vb2"]
            bonus = X["bonus"] if tl_ % 2 == 0 else X["bonus2"]
            glg = X["glg"] if tl_ % 2 == 0 else X["glg2"]
            t3 = X["t3"]
            pzw, pza, pgg = PB[4], PB[5], PB[6]
            for (pz_, lt, w2n, rows) in ((pzw, l1[0], "w2", 64), (pza, l1[1], "a2", 64), (pgg, l1[2], "g2", 128)):
                S.op("pe", lambda e, pz_=pz_, lt=lt, w2n=w2n, rows=rows, i=i: e.matmul(pz_.t[:], lt.t[0:rows, i * 128:(i + 1) * 128], W[w2n][0:rows, :], start=True, stop=True), r=[lt.r, r_w], w=[pz_.r])
            S.op("act", lambda e: e.copy(out=X["gg"].t[:], in_=pgg.t[:]), r=[pgg.r], w=[X["gg"].r])
            S.op("dve", lambda e: e.tensor_tensor(out=t1.t[:], in0=pzw.t[:], in1=BC["w0b"][:], op=ALU.add), r=[pzw.r, r_bc], w=[t1.r])
            S.op("dve", lambda e: e.tensor_tensor(out=t2.t[:], in0=pza.t[:], in1=BC["a0b"][:], op=ALU.add), r=[pza.r, r_bc, t2.r], w=[t2.r])
            S.op("act", lambda e: e.activation(out=sw.t[:], in_=t1.t[:], func=AF.Sigmoid), r=[t1.r], w=[sw.r])
            S.op("act", lambda e: e.activation(out=a_.t[:], in_=t2.t[:], func=AF.Sigmoid), r=[t2.r], w=[a_.r])

        def tile_front_b(i, G):
            tl_ = G * NTG + i
            r_ = G_["r"].t[:, i, :]; k_ = G_["k"].t[:, i, :]; v_ = G_["v"].t[:, i, :]
            rG = [G_[n].r for n in G_]
            sw, Gm, Gi, Gp, a_, kkn, kp, t1, t2 = [X[n] for n in ("sw", "Gm", "Gi", "Gp", "a", "kkn", "kp", "t1", "t2")]
            Yt, yn = X["Y"], X["yn"]
            sfx = "" if tl_ % 2 == 0 else "2"
            Bt, Kt, Bh, Kh, GC, At_, Rt = [X[n + sfx] if (n + sfx) in X else X[n] for n in ("Bt", "Kt", "Bh", "Kh", "GC", "At", "Rt")]
            Vt = X["V0"] if tl_ % 2 == 0 else X["V2"]
            Vb = X["Vb0"] if tl_ % 2 == 0 else X["Vb2"]
            bonus = X["bonus"] if tl_ % 2 == 0 else X["bonus2"]
            glg = X["glg"] if tl_ % 2 == 0 else X["glg2"]
            t3 = X["t3"]
            pL, pLC = PB[2], PB[3]
            S.op("pe", lambda e: e.matmul(pL.t[:], triC.t[:], sw.t[:], start=True, stop=True), r=[triC.r, sw.r], w=[pL.r])
            S.op("pe", lambda e: e.matmul(pLC.t[:], blkC.t[:], sw.t[:], start=True, stop=True), r=[blkC.r, sw.r], w=[pLC.r])
            S.op("act", lambda e: e.activation(out=Gm.t[:], in_=pL.t[:], func=AF.Exp), r=[pL.r], w=[Gm.r])
            S.op("act", lambda e: e.activation(out=Gi.t[:], in_=pL.t[:], func=AF.Exp, scale=-1.0), r=[pL.r], w=[Gi.r])
            S.op("dve", lambda e: e.scalar_tensor_tensor(out=Gp.t[:], in0=sw.t[:], scalar=C0, in1=pL.t[:], op0=ALU.mult, op1=ALU.add), r=[sw.r, pL.r], w=[Gp.r])
            S.op("act", lambda e: e.activation(out=Gp.t[:], in_=Gp.t[:], func=AF.Exp), r=[Gp.r], w=[Gp.r])
            S.op("act", lambda e: e.activation(out=GC.t[:], in_=pLC.t[:], func=AF.Exp), r=[pLC.r], w=[GC.r])
            S.op("dve", lambda e, k_=k_: e.tensor_tensor(out=kkn.t[:], in0=k_, in1=BC["kkb"][:], op=ALU.mult), r=rG + [r_bc], w=[kkn.r])
            S.op("act", lambda e: e.activation(out=t2.t[:], in_=kkn.t[:], func=AF.Square), r=[kkn.r], w=[t2.r])
            S.op("dve", lambda e: e.tensor_reduce(out=sm.t[:, 0:8], in_=hv(t2.t[:]), axis=AX.X, op=ALU.add), r=[t2.r], w=[sm.r])
            S.op("dve", lambda e: e.tensor_scalar(out=sm.t[:, 0:8], in0=sm.t[:, 0:8], scalar1=1e-24, scalar2=None, op0=ALU.max), r=[sm.r], w=[sm.r])
            S.op("act", lambda e: e.activation(out=sm.t[:, 0:8], in_=sm.t[:, 0:8], func=AF.Sqrt), r=[sm.r], w=[sm.r])
            S.op("dve", lambda e: e.reciprocal(out=sm.t[:, 0:8], in_=sm.t[:, 0:8]), r=[sm.r], w=[sm.r])
            S.op("dve", lambda e: e.tensor_tensor(out=hv(kkn.t[:]), in0=hv(kkn.t[:]), in1=bc8(sm.t[:, 0:8]), op=ALU.mult), r=[kkn.r, sm.r], w=[kkn.r])
            S.op("dve", lambda e: e.scalar_tensor_tensor(out=t2.t[:], in0=a_.t[:], scalar=-1.0, in1=BC["kab"][:], op0=ALU.add, op1=ALU.mult), r=[a_.r, r_bc, t2.r], w=[t2.r])
            S.op("dve", lambda e, k_=k_: e.scalar_tensor_tensor(out=kp.t[:], in0=t2.t[:], scalar=1.0, in1=k_, op0=ALU.add, op1=ALU.mult), r=[t2.r] + rG, w=[kp.r])
            S.op("dve", lambda e: e.scalar_tensor_tensor(out=At_.t[:], in0=kkn.t[:], scalar=-1.0, in1=Gp.t[:], op0=ALU.mult, op1=ALU.mult), r=[kkn.r, Gp.r], w=[At_.r])
            S.op("pool", lambda e: e.tensor_tensor(out=Bt.t[:], in0=kkn.t[:], in1=a_.t[:], op=ALU.mult), r=[kkn.r, a_.r], w=[Bt.r])
            S.op("pool", lambda e: e.tensor_tensor(out=Bt.t[:], in0=Bt.t[:], in1=Gi.t[:], op=ALU.mult), r=[Bt.r, Gi.r], w=[Bt.r])
            S.op("dve", lambda e: e.tensor_tensor(out=Kt.t[:], in0=kp.t[:], in1=Gi.t[:], op=ALU.mult), r=[kp.r, Gi.r], w=[Kt.r])
            S.op("dve", lambda e, r_=r_: e.tensor_tensor(out=Rt.t[:], in0=r_, in1=Gm.t[:], op=ALU.mult), r=rG + [Gm.r], w=[Rt.r])
            S.op("pool", lambda e: e.tensor_tensor(out=Bh.t[:], in0=Bt.t[:], in1=GC.t[:], op=ALU.mult), r=[Bt.r, GC.r], w=[Bh.r])
            S.op("pool", lambda e: e.tensor_tensor(out=Kh.t[:], in0=Kt.t[:], in1=GC.t[:], op=ALU.mult), r=[Kt.r, GC.r], w=[Kh.r])
            S.op("dve", lambda e, r_=r_: e.tensor_tensor(out=t2.t[:], in0=r_, in1=kp.t[:], op=ALU.mult), r=rG + [kp.r, t2.r], w=[t2.r])
            S.op("dve", lambda e: e.tensor_tensor(out=t2.t[:], in0=t2.t[:], in1=BC["rkb"][:], op=ALU.mult), r=[t2.r, r_bc], w=[t2.r])
            S.op("dve", lambda e: e.tensor_reduce(out=sm.t[:, 8:16], in_=hv(t2.t[:]), axis=AX.X, op=ALU.add), r=[t2.r], w=[sm.r])
            S.op("dve", lambda e, v_=v_, bonus=bonus: e.tensor_tensor(out=hv(bonus.t[:]), in0=hv(v_), in1=bc8(sm.t[:, 8:16]), op=ALU.mult), r=rG + [sm.r], w=[bonus.r])
            S.op("pool", lambda e, bonus=bonus: e.tensor_tensor(out=bonus.t[:], in0=bonus.t[:], in1=BC["lnbb"][:], op=ALU.add), r=[bonus.r, r_bc], w=[bonus.r])
            S.op("pool", lambda e, bonus=bonus: e.tensor_tensor(out=bonus.t[:], in0=bonus.t[:], in1=X["gg"].t[:], op=ALU.mult), r=[bonus.r, X["gg"].r], w=[bonus.r])
            S.op("pool", lambda e, glg=glg: e.tensor_tensor(out=glg.t[:], in0=X["gg"].t[:], in1=BC["lngb"][:], op=ALU.mult), r=[X["gg"].r, r_bc], w=[glg.r])
            if stop == 2:
                raise _Stop()
            S.op("pool", lambda e: e.tensor_copy(out=Vb.t[:], in_=v_), r=rG, w=[Vb.r])

        def tile_back(i, G, hook=None):
            tl_ = G * NTG + i
            r_ = G_["r"].t[:, i, :]; k_ = G_["k"].t[:, i, :]; v_ = G_["v"].t[:, i, :]
            rG = [G_[n].r for n in G_]
            sw, Gm, Gi, Gp, a_, kkn, kp, t1, t2 = [X[n] for n in ("sw", "Gm", "Gi", "Gp", "a", "kkn", "kp", "t1", "t2")]
            Yt, yn = X["Y"], X["yn"]
            sfx = "" if tl_ % 2 == 0 else "2"
            Bt, Kt, Bh, Kh, GC, At_, Rt = [X[n + sfx] if (n + sfx) in X else X[n] for n in ("Bt", "Kt", "Bh", "Kh", "GC", "At", "Rt")]
            Vt = X["V0"] if tl_ % 2 == 0 else X["V2"]
            Vb = X["Vb0"] if tl_ % 2 == 0 else X["Vb2"]
            bonus = X["bonus"] if tl_ % 2 == 0 else X["bonus2"]
            glg = X["glg"] if tl_ % 2 == 0 else X["glg2"]
            t3 = X["t3"]
            v_ = Vb.t[:]
            rG = [Vb.r]
            S.op("pool", lambda e: e.tensor_copy(out=Z[0].t[:, :, 0:64], in_=hv(At_.t[:])), r=[At_.r], w=[Z[0].r])
            for p in range(4):
                pT = PB[p % 2]
                for q_, src in enumerate((At_, Rt, Kt, Bt)):
                    S.op("pe", lambda e, pT=pT, p=p, q_=q_, src=src: e.transpose(pT.t[:, q_ * 128:(q_ + 1) * 128], src.t[:, p * 128:(p + 1) * 128], ident[:]), r=[src.r, r_ident], w=[pT.r])
                S.op("dve", lambda e, pT=pT, p=p: e.tensor_copy(out=ART.t[:, p, :, 0:64], in_=pT.t[:, 0:128].rearrange("p (c t) -> p c t", t=64)), r=[pT.r], w=[ART.r])
                S.op("act", lambda e, pT=pT, p=p: e.copy(out=ART.t[:, p, :, 64:128], in_=pT.t[:, 128:256].rearrange("p (c t) -> p c t", t=64)), r=[pT.r], w=[ART.r])
                S.op("act", lambda e, pT=pT, p=p: e.copy(out=KT.t[:, p, :], in_=pT.t[:, 256:384]), r=[pT.r], w=[KT.r])
                S.op("dve", lambda e, pT=pT, p=p: e.tensor_copy(out=BT.t[:, p, :], in_=pT.t[:, 384:512]), r=[pT.r], w=[BT.r])
            if stop == 3:
                raise _Stop()
            pS1a, pS1b, pS2a, pS2b, pS3 = PB[0], PB[1], PB[2], PB[3], PB[4]
            for h in range(8):
                p, e_ = h // 2, h % 2
                fe = slice(e_ * 64, e_ * 64 + 64)
                pa, pb_ = (pS1a, pS2a) if h < 4 else (pS1b, pS2b)
                hc = (h % 4) * 128
                for ch in range(2):
                    tc_ = slice(ch * 64, ch * 64 + 64)
                    S.op("pe", lambda e, pa=pa, fe=fe, p=p, ch=ch, tc_=tc_, hc=hc: e.matmul(pa.t[tc_, hc:hc + 128], BT.t[fe, p, tc_], ART.t[fe, p, ch, :], start=True, stop=True), r=[BT.r, ART.r], w=[pa.r])
                    S.op("pe", lambda e, pb_=pb_, fe=fe, p=p, ch=ch, tc_=tc_, hc=hc: e.matmul(pb_.t[tc_, hc:hc + 128], KT.t[fe, p, tc_], ART.t[fe, p, ch, :], start=True, stop=True), r=[KT.r, ART.r], w=[pb_.r])
                    S.op("pe", lambda e, fe=fe, p=p, ch=ch, tc_=tc_, h=h: e.matmul(pS3.t[tc_, h * 64:(h + 1) * 64], ART.t[fe, p, ch, 0:64], BT.t[fe, p, tc_], start=True, stop=True), r=[BT.r, ART.r], w=[pS3.r])
            m1b = mask1.t[:].unsqueeze(1).to_broadcast([128, 4, 128])
            for half, (pa, pb_) in enumerate(((pS1a, pS2a), (pS1b, pS2b))):
                hs = slice(half * 4, half * 4 + 4)
                S.op("dve", lambda e, pa=pa, hs=hs: e.tensor_tensor(out=Arb.t[:, hs, :], in0=pa.t[:].rearrange("p (h t) -> p h t", t=128), in1=m1b, op=ALU.mult), r=[pa.r, mask1.r], w=[Arb.r])
                S.op("dve", lambda e, pb_=pb_, hs=hs: e.tensor_tensor(out=Aak.t[:, hs, :], in0=pb_.t[:].rearrange("p (h t) -> p h t", t=128), in1=m1b, op=ALU.mult), r=[pb_.r, mask1.r], w=[Aak.r])
            for ch in range(2):
                tc_ = slice(ch * 64, ch * 64 + 64)
                S.op("dve", lambda e, tc_=tc_: e.tensor_tensor(out=NTm[0].t[tc_, :, tc_], in0=hv(pS3.t[tc_, :]), in1=maskL.t[tc_, :].unsqueeze(1).to_broadcast([64, 8, 64]), op=ALU.mult), r=[pS3.r, maskL.r], w=[NTm[0].r])
                S.op("act", lambda e, tc_=tc_: e.copy(out=Nm[0].t[tc_, :, tc_], in_=Arb.t[tc_, :, 0:64]), r=[Arb.r], w=[Nm[0].r])
            if stop == 4:
                raise _Stop()
            if debug and tl_ == 0:
                tails.append(S.dma("sp", dbg["Arb"], Arb.t[:], r=[Arb.r], key="dbg_Arb"))
            if debug and tl_ == 0:
                tails.append(S.dma("sp", dbg["Aak"], Aak.t[:], r=[Aak.r], key="dbg_Aak"))
            if debug and tl_ == 0:
                tails.append(S.dma("sp", dbg["NT0"], NTm[0].t[:], r=[NTm[0].r], key="dbg_NT0"))
            if debug and tl_ == 0:
                tails.append(S.dma("sp", dbg["ART"], ART.t[:], r=[ART.r], key="dbg_ART"))
            if debug and tl_ == 0:
                tails.append(S.dma("sp", dbg["KT"], KT.t[:], r=[KT.r], key="dbg_KT"))
            if debug and tl_ == 0:
                tails.append(S.dma("sp", dbg["BT"], BT.t[:], r=[BT.r], key="dbg_BT"))
            if debug and tl_ == 0:
                tails.append(S.dma("sp", dbg["Bh"], Bh.t[:], r=[Bh.r], key="dbg_Bh"))
            if debug and tl_ == 0:
                tails.append(S.dma("sp", dbg["Kh"], Kh.t[:], r=[Kh.r], key="dbg_Kh"))
            if hook is not None:
                hook()
            pX = PB[5]
            for h in range(8):
                for ch in range(2):
                    tc_ = slice(ch * 64, ch * 64 + 64)
                    S.op("pe", lambda e, h=h, tc_=tc_, v_=v_: e.matmul(pX.t[tc_, h * 64:(h + 1) * 64], Aak.t[tc_, h, 0:64], v_[tc_, h * 64:(h + 1) * 64], start=True, stop=True), r=[Aak.r] + rG, w=[pX.r])
            S.op("act", lambda e: e.copy(out=Z[0].t[:, :, 64:128], in_=hv(pX.t[:])), r=[pX.r], w=[Z[0].r])
            if stop == 5:
                raise _Stop()
            if debug and tl_ == 0:
                tails.append(S.dma("sp", dbg["Z0"], Z[0].t[:], r=[Z[0].r], key="dbg_Z0"))
            zi = 0
            for lev in range(6):
                Nc, NTc = Nm[lev % 2], NTm[lev % 2]
                Nn, NTn = Nm[(lev + 1) % 2], NTm[(lev + 1) % 2]
                pZa, pZb = PB[0], PB[1]
                Zc, Zn = Z[zi], Z[1 - zi]
                if lev < 5:
                    pN, pNT = (PB[2], PB[3]), (PB[4], PB[5])
                    for h in range(8):
                        hb, hc = h // 4, (h % 4) * 128
                        S.op("pe", lambda e, NTc=NTc, Nc=Nc, h=h, hb=hb, hc=hc: e.matmul(pN[hb].t[:, hc:hc + 128], NTc.t[:, h, :], Nc.t[:, h, :], start=True, stop=True), r=[Nc.r, NTc.r], w=[pN[hb].r])
                        if lev < 4:
                            S.op("pe", lambda e, NTc=NTc, Nc=Nc, h=h, hb=hb, hc=hc: e.matmul(pNT[hb].t[:, hc:hc + 128], Nc.t[:, h, :], NTc.t[:, h, :], start=True, stop=True), r=[Nc.r, NTc.r], w=[pNT[hb].r])
                for h in range(8):
                    pz = pZa if h < 4 else pZb
                    hc = (h % 4) * 128
                    S.op("pe", lambda e, pz=pz, Nc=Nc, Zc=Zc, h=h, hc=hc: e.matmul(pz.t[:, hc:hc + 128], Nc.t[:, h, :], Zc.t[:, h, :], start=True, stop=True), r=[Nc.r, Zc.r], w=[pz.r])
                if lev == 5:
                    Zn = Zf32
                S.op("dve", lambda e, Zc=Zc, Zn=Zn: e.tensor_tensor(out=Zn.t[:, 0:4, :], in0=Zc.t[:, 0:4, :], in1=pZa.t[:].rearrange("p (h t) -> p h t", t=128), op=ALU.add), r=[Zc.r, pZa.r], w=[Zn.r])
                S.op("dve", lambda e, Zc=Zc, Zn=Zn: e.tensor_tensor(out=Zn.t[:, 4:8, :], in0=Zc.t[:, 4:8, :], in1=pZb.t[:].rearrange("p (h t) -> p h t", t=128), op=ALU.add), r=[Zc.r, pZb.r], w=[Zn.r])
                zi = 1 - zi
                if lev < 5:
                    for hb in range(2):
                        S.op("act", lambda e, Nn=Nn, hb=hb: e.copy(out=Nn.t[:, hb * 4:hb * 4 + 4, :], in_=pN[hb].t[:].rearrange("p (h t) -> p h t", t=128)), r=[pN[hb].r], w=[Nn.r])
                        if lev < 4:
                            S.op("act" if hb == 0 else "dve", (lambda e, NTn=NTn, hb=hb: e.copy(out=NTn.t[:, hb * 4:hb * 4 + 4, :], in_=pNT[hb].t[:].rearrange("p (h t) -> p h t", t=128))) if hb == 0 else (lambda e, NTn=NTn, hb=hb: e.tensor_copy(out=NTn.t[:, hb * 4:hb * 4 + 4, :], in_=pNT[hb].t[:].rearrange("p (h t) -> p h t", t=128))), r=[pNT[hb].r], w=[NTn.r])
            Zf = Zf32
            S.op("act", lambda e: e.copy(out=Zfb.t[:], in_=Zf32.t[:]), r=[Zf32.r], w=[Zfb.r])
            pY0 = PB[4]
            for h in range(8):
                for ch in range(2):
                    tc_ = slice(ch * 64, ch * 64 + 64)
                    S.op("pe", lambda e, h=h, tc_=tc_: e.matmul(pY0.t[tc_, h * 64:(h + 1) * 64], Arb.t[tc_, h, 64:128], Zfb.t[tc_, h, 64:128], start=True, stop=False), r=[Arb.r, Zfb.r], w=[pY0.r])
                    S.op("pe", lambda e, h=h, tc_=tc_, v_=v_: e.matmul(pY0.t[tc_, h * 64:(h + 1) * 64], Aak.t[tc_, h, 64:128], v_[tc_, h * 64:(h + 1) * 64], start=False, stop=True), r=[Aak.r] + rG, w=[pY0.r])
            S.op("act", lambda e: e.copy(out=Y0.t[:], in_=pY0.t[:]), r=[pY0.r], w=[Y0.r])
            if stop == 7:
                raise _Stop()
            if debug and tl_ == 0:
                tails.append(S.dma("sp", dbg["Zf"], Zf.t[:], r=[Zf.r], key="dbg_Zf"))
            if debug and tl_ == 0:
                tails.append(S.dma("sp", dbg["Y0"], Y0.t[:], r=[Y0.r], key="dbg_Y0"))
            pR, pQ, pH, pg = PB[5], PB[6], PB[0], PB[1]
            for h in range(8):
                p, e_ = h // 2, h % 2
                fe = slice(e_ * 64, e_ * 64 + 64)
                for ch in range(2):
                    tc_ = slice(ch * 64, ch * 64 + 64)
                    cs = slice(p * 128 + ch * 64, p * 128 + ch * 64 + 64)
                    hs_ = slice(h * 64, h * 64 + 64)
                    S.op("pe", lambda e, fe=fe, cs=cs, tc_=tc_, h=h: e.matmul(pR.t[fe, cs], Zfb.t[tc_, h, 0:64], Arb.t[tc_, h, 64:128], start=True, stop=True), r=[Zfb.r, Arb.r], w=[pR.r])
                    S.op("pe", lambda e, fe=fe, cs=cs, tc_=tc_, h=h, hs_=hs_: e.matmul(pQ.t[fe, cs], Zfb.t[tc_, h, 0:64], Bh.t[tc_, hs_], start=True, stop=True), r=[Zfb.r, Bh.r], w=[pQ.r])
                    S.op("pe", lambda e, fe=fe, cs=cs, tc_=tc_, h=h, hs_=hs_: e.matmul(pH.t[fe, cs], Bh.t[tc_, hs_], Zfb.t[tc_, h, 64:128], start=True, stop=False), r=[Zfb.r, Bh.r], w=[pH.r])
                    S.op("pe", lambda e, fe=fe, cs=cs, tc_=tc_, hs_=hs_: e.matmul(pH.t[fe, cs], Kh.t[tc_, hs_], Vb.t[tc_, hs_], start=False, stop=True), r=[Kh.r, Vb.r], w=[pH.r])
                    S.op("pe", lambda e, fe=fe, tc_=tc_, hs_=hs_, p=p, ch=ch: e.matmul(pg.t[fe, p * 2 + ch:p * 2 + ch + 1], GC.t[tc_, hs_], avg.t[tc_, :], start=True, stop=True), r=[GC.r, avg.r], w=[pg.r])
            for ch in range(2):
                S.op("dve", lambda e, ch=ch: e.tensor_tensor(out=RbT.t[:, :, ch * 64:(ch + 1) * 64], in0=pR.t[:].rearrange("p (a t) -> p a t", t=128)[:, :, ch * 64:(ch + 1) * 64], in1=ART.t[:, :, ch, 64:128], op=ALU.add), r=[pR.r, ART.r], w=[RbT.r])
            S.op("act", lambda e: e.copy(out=Qm.t[:], in_=pQ.t[:].rearrange("p (a t) -> p a t", t=128)), r=[pQ.r], w=[Qm.r])
            S.op("act", lambda e: e.copy(out=Hm.t[:], in_=pH.t[:].rearrange("p (a t) -> p a t", t=128)), r=[pH.r], w=[Hm.r])
            S.op("act", lambda e: e.copy(out=gcol.t[:], in_=pg.t[:, 0:8]), r=[pg.r], w=[gcol.r])
            if stop == 8:
                raise _Stop()
            if debug and tl_ == 0:
                tails.append(S.dma("sp", dbg["RbT"], RbT.t[:], r=[RbT.r], key="dbg_RbT"))
            if debug and tl_ == 0:
                tails.append(S.dma("sp", dbg["Qm"], Qm.t[:], r=[Qm.r], key="dbg_Qm"))
            if debug and tl_ == 0:
                tails.append(S.dma("sp", dbg["Hm"], Hm.t[:], r=[Hm.r], key="dbg_Hm"))
            if debug and tl_ == 0:
                tails.append(S.dma("sp", dbg["gcol"], gcol.t[:], r=[gcol.r], key="dbg_gcol"))
            for ch in range(2):
                tc_ = slice(ch * 64, ch * 64 + 64)
                Sc, Sn = ST[sti[0]], ST[1 - sti[0]]
                pYe, pS_ = (PB[2], PB[4]), PB[3]
                for h in range(8):
                    p, e_ = h // 2, h % 2
                    fe = slice(e_ * 64, e_ * 64 + 64)
                    cs = slice(ch * 64, ch * 64 + 64)
                    pY = pYe[e_]
                    S.op("pe", lambda e, fe=fe, p=p, cs=cs, tc_=tc_, h=h, Sc=Sc, pY=pY: e.matmul(pY.t[tc_, h * 64:(h + 1) * 64], RbT.t[fe, p, cs], Sc.t[fe, p, :], start=True, stop=True), r=[RbT.r, Sc.r], w=[pY.r])
                    S.op("pe", lambda e, fe=fe, p=p, cs=cs, Sc=Sc: e.matmul(pS_.t[fe, p * 64:(p + 1) * 64], Qm.t[fe, p, cs], Sc.t[fe, p, :], start=True, stop=True), r=[Qm.r, Sc.r], w=[pS_.r])
                v4 = lambda ap: ap.rearrange("p (a e d) -> p a e d", e=2, d=64)
                for e_ in range(2):
                    S.op("dve", lambda e, tc_=tc_, e_=e_: e.tensor_tensor(out=v4(Yt.t[tc_, :])[:, :, e_, :], in0=v4(Y0.t[tc_, :])[:, :, e_, :], in1=v4(pYe[e_].t[tc_, :])[:, :, e_, :], op=ALU.add), r=[Y0.r, pYe[e_].r], w=[Yt.r])
                for p in range(4):
                    S.op("dve", lambda e, p=p, ch=ch, Sc=Sc, Sn=Sn: e.scalar_tensor_tensor(out=Sn.t[:, p, :], in0=Sc.t[:, p, :], scalar=gcol.t[:, p * 2 + ch:p * 2 + ch + 1], in1=pS_.t[:, p * 64:(p + 1) * 64], op0=ALU.mult, op1=ALU.add), r=[Sc.r, gcol.r, pS_.r], w=[Sn.r])
                S.op("dve", lambda e, ch=ch, Sn=Sn: e.tensor_tensor(out=Sn.t[:], in0=Sn.t[:], in1=Hm.t[:, :, ch * 64:(ch + 1) * 64], op=ALU.add), r=[Sn.r, Hm.r], w=[Sn.r])
                sti[0] = 1 - sti[0]
            if stop == 9:
                raise _Stop()
            if debug and tl_ == 0:
                S.op("act", lambda e, r_=r_: e.copy(out=t1.t[:], in_=r_), r=rG + [t1.r], w=[t1.r])
                S.op("act", lambda e, v_=v_: e.copy(out=t2.t[:], in_=v_), r=rG + [t2.r], w=[t2.r])
                tails.append(S.dma("sp", dbg["v"], t2.t[:], r=[t2.r], key="dbg_v"))
                for n, src in (("r", t1), ("kp", kp), ("kkn", kkn), ("a", a_), ("sw", sw), ("Y", Yt), ("bonus", bonus), ("Gm", Gm), ("yn", yn)):
                    tails.append(S.dma("sp", dbg[n], src.t[:], r=[src.r], key="dbg_" + n))
                tails.append(S.dma("sp", dbg["S"], ST[sti[0]].t[:], r=[ST[sti[0]].r], key="dbg_S"))


        def tile_tail(i, G):
            tl_ = G * NTG + i
            r_ = G_["r"].t[:, i, :]; k_ = G_["k"].t[:, i, :]; v_ = G_["v"].t[:, i, :]
            rG = [G_[n].r for n in G_]
            sw, Gm, Gi, Gp, a_, kkn, kp, t1, t2 = [X[n] for n in ("sw", "Gm", "Gi", "Gp", "a", "kkn", "kp", "t1", "t2")]
            Yt, yn = X["Y"], X["yn"]
            sfx = "" if tl_ % 2 == 0 else "2"
            Bt, Kt, Bh, Kh, GC, At_, Rt = [X[n + sfx] if (n + sfx) in X else X[n] for n in ("Bt", "Kt", "Bh", "Kh", "GC", "At", "Rt")]
            Vt = X["V0"] if tl_ % 2 == 0 else X["V2"]
            Vb = X["Vb0"] if tl_ % 2 == 0 else X["Vb2"]
            bonus = X["bonus"] if tl_ % 2 == 0 else X["bonus2"]
            glg = X["glg"] if tl_ % 2 == 0 else X["glg2"]
            t3 = X["t3"]
            v_ = Vb.t[:]
            rG = [Vb.r]
            S.op("dve", lambda e: e.tensor_reduce(out=smT.t[:, 16:24], in_=hv(Yt.t[:]), axis=AX.X, op=ALU.add), r=[Yt.r], w=[smT.r])
            S.op("act", lambda e: e.activation(out=t3.t[:], in_=Yt.t[:], func=AF.Square), r=[Yt.r, t3.r], w=[t3.r])
            S.op("dve", lambda e: e.tensor_reduce(out=smT.t[:, 24:32], in_=hv(t3.t[:]), axis=AX.X, op=ALU.add), r=[t3.r], w=[smT.r])
            S.op("dve", lambda e: e.tensor_scalar(out=smT.t[:, 16:24], in0=smT.t[:, 16:24], scalar1=1.0 / 64, scalar2=None, op0=ALU.mult), r=[smT.r], w=[smT.r])
            S.op("dve", lambda e: e.tensor_tensor(out=smT.t[:, 32:40], in0=smT.t[:, 16:24], in1=smT.t[:, 16:24], op=ALU.mult), r=[smT.r], w=[smT.r])
            S.op("dve", lambda e: e.scalar_tensor_tensor(out=smT.t[:, 24:32], in0=smT.t[:, 24:32], scalar=1.0 / 64, in1=smT.t[:, 32:40], op0=ALU.mult, op1=ALU.subtract), r=[smT.r], w=[smT.r])
            S.op("dve", lambda e: e.tensor_scalar(out=smT.t[:, 24:32], in0=smT.t[:, 24:32], scalar1=GN_EPS, scalar2=None, op0=ALU.add), r=[smT.r], w=[smT.r])
            S.op("act", lambda e: e.activation(out=smT.t[:, 24:32], in_=smT.t[:, 24:32], func=AF.Sqrt), r=[smT.r], w=[smT.r])
            S.op("dve", lambda e: e.reciprocal(out=smT.t[:, 24:32], in_=smT.t[:, 24:32]), r=[smT.r], w=[smT.r])
            S.op("dve", lambda e: e.tensor_tensor(out=hv(yn.t[:]), in0=hv(Yt.t[:]), in1=bc8(smT.t[:, 16:24]), op=ALU.subtract), r=[Yt.r, smT.r], w=[yn.r])
            S.op("dve", lambda e: e.tensor_tensor(out=hv(yn.t[:]), in0=hv(yn.t[:]), in1=bc8(smT.t[:, 24:32]), op=ALU.mult), r=[yn.r, smT.r], w=[yn.r])
            S.op("dve", lambda e, glg=glg: e.tensor_tensor(out=yn.t[:], in0=yn.t[:], in1=glg.t[:], op=ALU.mult), r=[yn.r, glg.r], w=[yn.r])
            S.op("dve", lambda e, bonus=bonus: e.tensor_tensor(out=ybf.t[:], in0=yn.t[:], in1=bonus.t[:], op=ALU.add), r=[yn.r, bonus.r], w=[ybf.r])
            for p in range(4):
                S.op("pe", lambda e, p=p: e.transpose(pbb.t[:, p * 128:(p + 1) * 128], ybf.t[:, p * 128:(p + 1) * 128], identb.t[:]), r=[ybf.r, identb.r], w=[pbb.r])
            yo = yTo[tl_ % 2]
            S.op("act", lambda e, yo=yo: e.copy(out=yo.t[:], in_=pbb.t[:, 0:512].rearrange("p (a t) -> p a t", t=128)), r=[pbb.r], w=[yo.r])
            tails.append(S.dma("sp", ysrc[:, :, tl_ * 128:(tl_ + 1) * 128], yo.t[:], r=[yo.r], key="yTo%d" % (tl_ % 2)))

        pend = [None, None]
        try:
          for G in range(NGR):
              if G > 0:
                  S.op("dve", lambda e: e.tensor_copy(out=xg.t[:, :, 0:1], in_=xg.t[:, :, GS:GS + 1]), r=[xg.r], w=[xg.r])
              if before_x is not None and G == 0:
                  before_x(S)
              S.dma("sp", xg.t[:, :, 1:GS + 1], xsrc_fn(G * GS, GS), r=[xg.r], w=[xg.r], key="xg")
              for kc in range(8):
                  S.op("dve", lambda e, kc=kc: e.tensor_scalar(out=xg.t[:, kc, 1:GS + 1], in0=xg.t[:, kc, 1:GS + 1], scalar1=cols.t[:, kc:kc + 1], scalar2=mods[:, kc:kc + 1], op0=ALU.mult, op1=ALU.add), r=[xg.r, cols.r, r_mods], w=[xg.r])
              S.op("dve", lambda e: e.tensor_tensor(out=dx.t[:], in0=xg.t[:, :, 0:GS], in1=xg.t[:, :, 1:GS + 1], op=ALU.subtract), r=[xg.r], w=[dx.r])
              for n, nm in enumerate(("r", "k", "v", "w", "a", "g")):
                  xb = xs[n % 2]
                  for kc in range(8):
                      eng = "dve"
                      S.op(eng, lambda e, kc=kc, n=n, xb=xb: e.scalar_tensor_tensor(out=xb.t[:, kc, :], in0=dx.t[:, kc, :], scalar=mu.t[:, n * 8 + kc:n * 8 + kc + 1], in1=xg.t[:, kc, 1:GS + 1], op0=ALU.mult, op1=ALU.add), r=[dx.r, mu.r, xg.r], w=[xb.r])
                  if n < 3:
                      wt = W[("wr", "wk", "wv")[n]]
                      for i in range(NTG):
                          pb = PB[(n * NTG + i) % 2]
                          for kc in range(8):
                              S.op("pe", lambda e, pb=pb, xb=xb, wt=wt, i=i, kc=kc: e.matmul(pb.t[:], xb.t[:, kc, i * 128:(i + 1) * 128], wt[:, kc, :], start=(kc == 0), stop=(kc == 7)), r=[xb.r, r_w], w=[pb.r])
                          S.op("act", lambda e, pb=pb, nm=nm, i=i: e.copy(out=G_[nm].t[:, i, :], in_=pb.t[:]), r=[pb.r], w=[G_[nm].r])
                  else:
                      w1n, w2n, rows = (("w1", "w2", 64), ("a1", "a2", 64), ("g1", "g2", 128))[n - 3]
                      pb = PB[2]
                      for kc in range(8):
                          S.op("pe", lambda e, pb=pb, xb=xb, w1n=w1n, rows=rows, kc=kc: e.matmul(pb.t[0:rows, 0:GS], W[w1n][:, kc, :], xb.t[:, kc, :], start=(kc == 0), stop=(kc == 7)), r=[xb.r, r_w], w=[pb.r])
                      lt = l1[n - 3]
                      if nm == "w":
                          S.op("act", lambda e, pb=pb: e.activation(out=l1f.t[0:64, :], in_=pb.t[0:64, 0:GS], func=AF.Exp, scale=-2.0), r=[pb.r, l1f.r], w=[l1f.r])
                          S.op("dve", lambda e: e.tensor_scalar(out=l1f.t[0:64, :], in0=l1f.t[0:64, :], scalar1=1.0, scalar2=None, op0=ALU.add), r=[l1f.r], w=[l1f.r])
                          S.op("dve", lambda e: e.reciprocal(out=l1f.t[0:64, :], in_=l1f.t[0:64, :]), r=[l1f.r], w=[l1f.r])
                          S.op("dve", lambda e, lt=lt: e.tensor_scalar(out=lt.t[0:64, :], in0=l1f.t[0:64, :], scalar1=2.0, scalar2=-1.0, op0=ALU.mult, op1=ALU.add), r=[l1f.r], w=[lt.r])
                      elif nm == "a":
                          S.op("act", lambda e, pb=pb, lt=lt: e.copy(out=lt.t[0:64, :], in_=pb.t[0:64, 0:GS]), r=[pb.r], w=[lt.r])
                      else:
                          S.op("act", lambda e, pb=pb: e.activation(out=l1f.t[:], in_=pb.t[:, 0:GS], func=AF.Exp, scale=-1.0), r=[pb.r, l1f.r], w=[l1f.r])
                          S.op("dve", lambda e: e.tensor_scalar(out=l1f.t[:], in0=l1f.t[:], scalar1=1.0, scalar2=None, op0=ALU.add), r=[l1f.r], w=[l1f.r])
                          S.op("dve", lambda e: e.reciprocal(out=l1f.t[:], in_=l1f.t[:]), r=[l1f.r], w=[l1f.r])
                          S.op("act", lambda e, lt=lt: e.copy(out=lt.t[:], in_=l1f.t[:]), r=[l1f.r], w=[lt.r])
              if stop == 1:
                  raise _Stop()
              for i in range(NTG):
                  tile_front_a(i, G)
                  if pend[0] is None:
                      tile_front_b(i, G)
                  else:
                      def hook(i=i, G=G, prev=pend[1]):
                          if prev is not None and not debug:
                              tile_tail(*prev)
                          tile_front_b(i, G)
                      tile_back(*pend[0], hook=hook)
                      if debug:
                          tile_tail(*pend[0])
                      pend[1] = pend[0]
                  pend[0] = (i, G)
          if pend[0] is not None:
              def hook_last(prev=pend[1]):
                  if prev is not None and not debug:
                      tile_tail(*prev)
              tile_back(*pend[0], hook=hook_last)
              tile_tail(*pend[0])
        except _Stop:
            pass
        if after is not None:
            after(C, tails)
        S.emit(st, tail_waits=tails)


PAIRS = [[0, 1], [2, 3], [4, 5], [6, 7]]
TL_KEYS = ("cT", "adaw", "adab", "lng", "lnb", "wo", "win", "wout")


def tl_decl(dt, sfx):
    return dict(cT=dt("cT" + sfx, [128, 8], F32), adaw=dt("adaw" + sfx, [D, 4 * D], F32), adab=dt("adab" + sfx, [128, 32], F32),
                lng=dt("lng" + sfx, [128, 16], F32), lnb=dt("lnb" + sfx, [128, 16], F32),
                wo=dt("wo" + sfx, [D, D], F32), win=dt("win" + sfx, [D, 2 * DFF], F32), wout=dt("wout" + sfx, [DFF, D], F32))


def build_fused(stop=99):
    nc = bass.Bass("TRN2", target_bir_lowering=False)
    dt = lambda n, s, d, k="ExternalInput": nc.dram_tensor(n, list(s), d, kind=k).ap()
    it = lambda n, s, d: nc.dram_tensor(n, list(s), d).ap()
    fox_in = fox_decl(dt, 4096, "_f")
    oT_d = dt("oT", [D, 2048], F32, "ExternalOutput")
    cin1, cout1 = it("cin1", [4, 128, 4096], BF16), it("cout1", [4, 2, 128, 4096], BF16)
    cin2, cout2 = it("cin2", [8, 128, 2048], F32), it("cout2", [8, 2, 128, 2048], F32)
    cin3, cout3 = it("cin3", [4, 128, 4096], BF16), it("cout3", [4, 2, 128, 4096], BF16)

    GS_ = {k: nc.alloc_semaphore(name="xchg_" + k) for k in ("g1", "g2", "g3")}

    def gather(cin, cout, key, nblk):
        def after(C, tails):
            prev = list(tails)
            for k in range(nblk):
                C.S.cc("AllGather", PAIRS, cin[k], cout[k].rearrange("r p t -> (r p) t"), key=key, extra=prev, ext_sem=GS_[key])
        return after

    def waiter(key, nblk):
        return lambda S: S.ext_wait("sp", GS_[key], nblk)

    cin1_v = cin1.rearrange("k p t -> (k p) t")
    cin3_v = cin3.rearrange("k p t -> (k p) t")
    cin2_v = cin2.rearrange("k p t -> (k p) t")
    y1_v = cout1
    y3_v = cout3
    fox_phase(nc, "f_", fox_in, cin1_v, 4096, after=gather(cin1, cout1, "g1", 4))
    if stop <= 1:
        return nc
    phase_end(nc)
    sel_d = dt("sel", [128, 2], F32)
    xTh_d = dt("xTh", [D, 2048], F32)
    t = tl_decl(dt, "_t0")
    tl_phase(nc, "t0_", y1_v, xTh_d, t["cT"], t["adaw"], t["adab"], t["lng"], t["lnb"], t["wo"], t["win"], t["wout"], cin2_v, 2048, 1024,
             sel_d=sel_d, after=gather(cin2, cout2, "g2", 8), before_y=waiter("g1", 4))
    if stop <= 2:
        return nc
    phase_end(nc)
    rw_in = rwkv_decl(dt, "_r")
    c2v = cout2.rearrange("kc r p t -> r p kc t")
    rwkv_phase(nc, "r_", lambda t0, n: c2v[t0 // 2048, :, :, (t0 % 2048):(t0 % 2048) + n], rw_in, cin3_v, 4096, after=gather(cin3, cout3, "g3", 4), before_x=waiter("g2", 8))
    if stop <= 3:
        return nc
    phase_end(nc)
    t = tl_decl(dt, "_t1")
    tl_phase(nc, "t1_", y3_v, cin2_v, t["cT"], t["adaw"], t["adab"], t["lng"], t["lnb"], t["wo"], t["win"], t["wout"], oT_d, 2048, 1024, sel_d=sel_d, before_y=waiter("g3", 4))
    return nc


NCORES = 8
_PROGS = {}


def _prog(name, fn):
    if name not in _PROGS:
        _PROGS[name] = fn()
    return _PROGS[name]


def _col_layout(v):
    return np.ascontiguousarray(np.asarray(v).reshape(-1, 128).T)


def _bc(v, n=512):
    v = np.asarray(v, dtype=np.float32).reshape(-1)
    return np.ascontiguousarray(np.broadcast_to(v[None, :], (128, v.shape[0])))


def _run(nc, in_maps):
    res = run_bass_kernel_spmd(nc, in_maps, core_ids=list(range(NCORES)))
    return res.results


def _tl_maps(yT_list, xT_list, c, ada_w_i, ada_b_i, ln_g_i, ln_b_i, wo, win, wout):
    maps = []
    adaw = np.ascontiguousarray(ada_w_i[:, 2 * D:])
    adab = _col_layout(ada_b_i[2 * D:])
    lng = _col_layout(ln_g_i.reshape(-1))
    lnb = _col_layout(ln_b_i.reshape(-1))
    for core in range(NCORES):
        b, th = core // 2, core % 2
        ts = slice(th * 2048, (th + 1) * 2048)
        maps.append({
            "yT": np.ascontiguousarray(yT_list[b][:, ts]),
            "xT": np.ascontiguousarray(xT_list[b][:, ts]),
            "cT": _col_layout(c[b]),
            "adaw": adaw, "adab": adab, "lng": lng, "lnb": lnb,
            "wo": wo, "win": win, "wout": wout,
        })
    return maps


def kernel_unfused(x, c, ada_w, ada_b, ln_g, ln_b, ffn_w_in, ffn_w_out,
           fox_w_in, fox_b_f, fox_q_g, fox_k_g, fox_w_o,
           rwkv_mu, rwkv_w_rkv, rwkv_w0, rwkv_w1, rwkv_w2, rwkv_a0, rwkv_a1, rwkv_a2,
           rwkv_g1, rwkv_g2, rwkv_k_k, rwkv_k_a, rwkv_r_k, rwkv_lnx_g, rwkv_lnx_b, rwkv_w_o):
    f = lambda a: np.ascontiguousarray(np.asarray(a, dtype=np.float32))
    x, c, ada_w, ada_b, ln_g, ln_b = f(x), f(c), f(ada_w), f(ada_b), f(ln_g), f(ln_b)
    ffn_w_in, ffn_w_out = f(ffn_w_in), f(ffn_w_out)
    B = x.shape[0]
    w_in = f(fox_w_in)[0]
    b_f, q_g, k_g = f(fox_b_f)[0], f(fox_q_g)[0], f(fox_k_g)[0]
    maps = []
    for core in range(NCORES):
        b, hh = core // 2, core % 2
        sl = slice(hh * 512, (hh + 1) * 512)
        maps.append({
            "x": x[b],
            "cT": _col_layout(c[b]),
            "adaw": np.ascontiguousarray(ada_w[0][:, 0:2 * D]),
            "adab": _col_layout(ada_b[0][0:2 * D]),
            "wq": np.ascontiguousarray(w_in[:, 0:D][:, sl]),
            "wk": np.ascontiguousarray(w_in[:, D:2 * D][:, sl]),
            "wv": np.ascontiguousarray(w_in[:, 2 * D:3 * D][:, sl]),
            "wf": np.ascontiguousarray(w_in[:, 3 * D + hh * 8:3 * D + hh * 8 + 8]),
            "wg": np.ascontiguousarray(w_in[:, 3 * D + 16:][:, sl]),
            "bfb": _bc(b_f[hh * 8:(hh + 1) * 8]),
            "qgb": _bc(np.tile(q_g, 8)),
            "kgb": _bc(np.tile(k_g, 8)),
        })
    r1 = _run(_prog("fox", build_fox), maps)
    yT = [np.concatenate([r1[2 * b]["ygT"], r1[2 * b + 1]["ygT"]], axis=0) for b in range(B)]
    xT = [np.ascontiguousarray(x[b].T) for b in range(B)]
    tlp = _prog("tl", build_tl)
    r2 = _run(tlp, _tl_maps(yT, xT, c, ada_w[0], ada_b[0], ln_g[0], ln_b[0], f(fox_w_o)[0], ffn_w_in[0], ffn_w_out[0]))
    x1T = [np.concatenate([r2[2 * b]["oT"], r2[2 * b + 1]["oT"]], axis=1) for b in range(B)]
    mu = f(rwkv_mu)[0]
    w_rkv = f(rwkv_w_rkv)[0]
    P = dict(w0=f(rwkv_w0)[0], w1=f(rwkv_w1)[0], w2=f(rwkv_w2)[0], a0=f(rwkv_a0)[0], a1=f(rwkv_a1)[0], a2=f(rwkv_a2)[0],
             g1=f(rwkv_g1)[0], g2=f(rwkv_g2)[0], k_k=f(rwkv_k_k)[0], k_a=f(rwkv_k_a)[0], r_k=f(rwkv_r_k)[0].reshape(-1),
             lnx_g=f(rwkv_lnx_g)[0], lnx_b=f(rwkv_lnx_b)[0])
    mu_l = np.concatenate([_col_layout(mu[n]) for n in range(6)], axis=1)
    maps = []
    for core in range(NCORES):
        b, hh = core // 2, core % 2
        sl = slice(hh * 512, (hh + 1) * 512)
        maps.append({
            "xT": x1T[b],
            "cT": _col_layout(c[b]),
            "adaw": np.ascontiguousarray(ada_w[1][:, 0:2 * D]),
            "adab": _col_layout(ada_b[1][0:2 * D]),
            "mu": mu_l,
            "wr": np.ascontiguousarray(w_rkv[0][:, sl]),
            "wk": np.ascontiguousarray(w_rkv[1][:, sl]),
            "wv": np.ascontiguousarray(w_rkv[2][:, sl]),
            "w1": P["w1"], "a1": P["a1"], "g1": P["g1"],
            "w2": np.ascontiguousarray(P["w2"][:, sl]),
            "a2": np.ascontiguousarray(P["a2"][:, sl]),
            "g2": np.ascontiguousarray(P["g2"][:, sl]),
            "w0b": _bc(P["w0"][sl]), "a0b": _bc(P["a0"][sl]), "kkb": _bc(P["k_k"][sl]), "kab": _bc(P["k_a"][sl]),
            "rkb": _bc(P["r_k"][sl]), "lngb": _bc(P["lnx_g"][sl]), "lnbb": _bc(P["lnx_b"][sl]),
        })
    r3 = _run(_prog("rwkv", build_rwkv), maps)
    y2T = [np.concatenate([r3[2 * b]["yT"], r3[2 * b + 1]["yT"]], axis=0) for b in range(B)]
    r4 = _run(tlp, _tl_maps(y2T, x1T, c, ada_w[1], ada_b[1], ln_g[1], ln_b[1], f(rwkv_w_o)[0], ffn_w_in[1], ffn_w_out[1]))
    out = np.empty(x.shape, np.float32)
    for core in range(NCORES):
        b, th = core // 2, core % 2
        out[b, th * 2048:(th + 1) * 2048, :] = r4[core]["oT"].T
    return out


def kernel(x, c, ada_w, ada_b, ln_g, ln_b, ffn_w_in, ffn_w_out,
           fox_w_in, fox_b_f, fox_q_g, fox_k_g, fox_w_o,
           rwkv_mu, rwkv_w_rkv, rwkv_w0, rwkv_w1, rwkv_w2, rwkv_a0, rwkv_a1, rwkv_a2,
           rwkv_g1, rwkv_g2, rwkv_k_k, rwkv_k_a, rwkv_r_k, rwkv_lnx_g, rwkv_lnx_b, rwkv_w_o):
    f = lambda a: np.ascontiguousarray(np.asarray(a, dtype=np.float32))
    x, c, ada_w, ada_b, ln_g, ln_b = f(x), f(c), f(ada_w), f(ada_b), f(ln_g), f(ln_b)
    ffn_w_in, ffn_w_out = f(ffn_w_in), f(ffn_w_out)
    w_in = f(fox_w_in)[0]
    b_f, q_g, k_g = f(fox_b_f)[0], f(fox_q_g)[0], f(fox_k_g)[0]
    mu = f(rwkv_mu)[0]
    w_rkv = f(rwkv_w_rkv)[0]
    P = dict(w0=f(rwkv_w0)[0], w1=f(rwkv_w1)[0], w2=f(rwkv_w2)[0], a0=f(rwkv_a0)[0], a1=f(rwkv_a1)[0], a2=f(rwkv_a2)[0],
             g1=f(rwkv_g1)[0], g2=f(rwkv_g2)[0], k_k=f(rwkv_k_k)[0], k_a=f(rwkv_k_a)[0], r_k=f(rwkv_r_k)[0].reshape(-1),
             lnx_g=f(rwkv_lnx_g)[0], lnx_b=f(rwkv_lnx_b)[0])
    mu_l = np.concatenate([_col_layout(mu[n]) for n in range(6)], axis=1)
    wos = [f(fox_w_o)[0], f(rwkv_w_o)[0]]
    tl_common = []
    for L in range(2):
        tl_common.append({
            "adaw_t%d" % L: np.ascontiguousarray(ada_w[L][:, 2 * D:]), "adab_t%d" % L: _col_layout(ada_b[L][2 * D:]),
            "lng_t%d" % L: _col_layout(ln_g[L].reshape(-1)), "lnb_t%d" % L: _col_layout(ln_b[L].reshape(-1)),
            "wo_t%d" % L: wos[L], "win_t%d" % L: ffn_w_in[L], "wout_t%d" % L: ffn_w_out[L]})
    adaw_f = np.ascontiguousarray(ada_w[0][:, 0:2 * D])
    adaw_r = np.ascontiguousarray(ada_w[1][:, 0:2 * D])
    maps = []
    for core in range(NCORES):
        b, j = core // 2, core % 2
        sl = slice(j * 512, (j + 1) * 512)
        cT = _col_layout(c[b])
        m = {
            "x_f": x[b], "cT_f": cT, "adaw_f": adaw_f, "adab_f": _col_layout(ada_b[0][0:2 * D]),
            "wq_f": np.ascontiguousarray(w_in[:, 0:D][:, sl]), "wk_f": np.ascontiguousarray(w_in[:, D:2 * D][:, sl]),
            "wv_f": np.ascontiguousarray(w_in[:, 2 * D:3 * D][:, sl]), "wf_f": np.ascontiguousarray(w_in[:, 3 * D + j * 8:3 * D + j * 8 + 8]),
            "wg_f": np.ascontiguousarray(w_in[:, 3 * D + 16:][:, sl]),
            "bfb_f": _bc(b_f[j * 8:(j + 1) * 8]), "qgb_f": _bc(np.tile(q_g, 8)), "kgb_f": _bc(np.tile(k_g, 8)),
            "cT_r": cT, "adaw_r": adaw_r, "adab_r": _col_layout(ada_b[1][0:2 * D]), "mu_r": mu_l,
            "wr_r": np.ascontiguousarray(w_rkv[0][:, sl]), "wk_r": np.ascontiguousarray(w_rkv[1][:, sl]), "wv_r": np.ascontiguousarray(w_rkv[2][:, sl]),
            "w1_r": P["w1"], "a1_r": P["a1"], "g1_r": P["g1"],
            "w2_r": np.ascontiguousarray(P["w2"][:, sl]), "a2_r": np.ascontiguousarray(P["a2"][:, sl]), "g2_r": np.ascontiguousarray(P["g2"][:, sl]),
            "w0b_r": _bc(P["w0"][sl]), "a0b_r": _bc(P["a0"][sl]), "kkb_r": _bc(P["k_k"][sl]), "kab_r": _bc(P["k_a"][sl]),
            "rkb_r": _bc(P["r_k"][sl]), "lngb_r": _bc(P["lnx_g"][sl]), "lnbb_r": _bc(P["lnx_b"][sl]),
            "cT_t0": cT, "cT_t1": cT,
            "sel": np.ascontiguousarray(np.broadcast_to(np.array([1.0 - j, float(j)], np.float32)[None, :], (128, 2))),
            "xTh": np.ascontiguousarray(x[b, j * 2048:(j + 1) * 2048, :].T),
        }
        m.update(tl_common[0])
        m.update(tl_common[1])
        maps.append(m)
    res = _run(_prog("fused", build_fused), maps)
    out = np.empty(x.shape, np.float32)
    for core in range(NCORES):
        b, j = core // 2, core % 2
        out[b, j * 2048:(j + 1) * 2048, :] = res[core]["oT"].T
    return out
```

```python
import numpy as np
from contextlib import ExitStack
import concourse.bass as bass
import concourse.mybir as mybir
from concourse.bass_utils import run_bass_kernel_spmd

F32 = mybir.dt.float32
BF16 = mybir.dt.bfloat16
AF = mybir.ActivationFunctionType
ALU = mybir.AluOpType
AX = mybir.AxisListType

ENGS = ("pe", "act", "dve", "pool", "sp")
LAST_SEMS = []
SEM_CTR = [0]


def phase_end(nc):
    nc.all_engine_barrier()
    nc.clear_and_free_semaphores(list(LAST_SEMS))
    LAST_SEMS[:] = []
    nc.all_engine_barrier()


class Res:
    __slots__ = ("name", "last_w", "readers", "excl")

    def __init__(self, name, excl=False):
        self.name = name
        self.excl = excl
        self.last_w = None
        self.readers = []


class Op:
    __slots__ = ("eng", "fn", "deps", "sig", "count", "is_dma", "key", "semval", "vc", "waits", "gid")


class Deferred:
    op = None


class Sched:
    def __init__(self, nc):
        self.nc = nc
        self.ops = []
        self.dma_counts = {}

    def res(self, name, excl=False):
        return Res(name, excl)

    def _add(self, op, r, w, extra=()):
        deps = [(d, "raw") for d in extra]
        for x in r:
            if x.last_w is not None:
                deps.append((x.last_w, "raw"))
            if x.excl:
                for rd in x.readers:
                    deps.append((rd, "war"))
        for x in w:
            if x.last_w is not None:
                deps.append((x.last_w, "waw"))
            for rd in x.readers:
                deps.append((rd, "war"))
        dd = []
        for d, kind in deps:
            if d is op:
                continue
            if (not d.is_dma) and (not op.is_dma) and d.eng == op.eng:
                if op.eng == "pe" or kind == "war":
                    continue
            dd.append(d)
        op.deps = dd
        for d in dd:
            d.sig = True
        for x in w:
            x.last_w = op
            x.readers = []
        for x in r:
            if x.last_w is not op:
                if not op.is_dma:
                    x.readers = [q for q in x.readers if q.is_dma or q.eng != op.eng]
                x.readers.append(op)
        op.gid = len(self.ops)
        self.ops.append(op)
        return op

    def begin_defer(self):
        self.flush_deferred()
        self._defer = []
        self._capturing = True

    def end_defer(self):
        self._capturing = False

    def flush_deferred(self):
        q = getattr(self, "_defer", None)
        self._defer = None
        self._capturing = False
        if q:
            for kind, ph, a, kw in q:
                real = getattr(self, kind)(*a, **kw)
                if ph is not None:
                    ph.op = real
        self._deferred_done = True

    def op(self, eng, fn, r=(), w=()):
        if getattr(self, "_capturing", False):
            self._defer.append(("op", None, (eng, fn), dict(r=list(r), w=list(w))))
            return None
        o = Op()
        o.eng = eng
        o.fn = fn
        o.is_dma = False
        o.sig = False
        o.key = eng
        return self._add(o, r, w)

    def dma(self, eng, out, in_, r=(), w=(), key=None, **kw):
        if getattr(self, "_capturing", False):
            ph = Deferred()
            self._defer.append(("dma", ph, (eng, out, in_), dict(r=list(r), w=list(w), key=key, **kw)))
            return ph
        o = Op()
        o.eng = eng
        o.is_dma = True
        o.sig = True
        assert key is not None
        o.key = "dma:" + key
        o.fn = lambda e: e.dma_start(out=out, in_=in_, **kw)
        return self._add(o, r, w)

    def ext_wait(self, eng, sem, value):
        return self.op(eng, lambda e: e.wait_ge(sem, value))

    def cc(self, kind, groups, in_ap, out_ap, r=(), w=(), key=None, extra=(), ext_sem=None):
        o = Op()
        o.eng = "pool"
        o.is_dma = True
        o.sig = True
        o.key = "cc:" + key
        if ext_sem is not None:
            self.ext_sems = getattr(self, "ext_sems", {})
            self.ext_sems[o.key] = ext_sem
        o.fn = lambda e: e.collective_compute(kind, ALU.bypass, replica_groups=groups, ins=[in_ap], outs=[out_ap])
        return self._add(o, r, w, extra)

    def finalize(self, final_wait_eng="sp"):
        nc = self.nc
        counts = {e: 0 for e in ENGS}
        dmac = {}
        for o in self.ops:
            if o.is_dma:
                dmac[o.key] = dmac.get(o.key, 0) + (1 if o.key.startswith("cc:") else 16)
                o.semval = dmac[o.key]
            elif o.sig:
                counts[o.eng] += 1
                o.semval = counts[o.eng]
        seen = {e: {} for e in ENGS}
        for o in self.ops:
            s = seen[o.eng]
            need = {}
            for d in o.deps:
                if s.get(d.key, 0) < d.semval:
                    if d.key not in need or need[d.key].semval < d.semval:
                        need[d.key] = d
            o.waits = [(d.key, d.semval) for d in need.values()]
            for d in need.values():
                for k, v in d.vc.items():
                    if s.get(k, 0) < v:
                        s[k] = v
            if o.sig:
                vc = dict(s)
                vc[o.key] = o.semval
                o.vc = vc
            else:
                o.vc = None
        return counts, dmac

    def emit(self, stack, tail_waits=()):
        nc = self.nc
        self.flush_deferred()
        tail_waits = [getattr(o, "op", None) or o for o in tail_waits]
        counts, dmac = self.finalize()
        sems = {}
        ext = getattr(self, "ext_sems", {})
        for k in list(ENGS) + list(dmac):
            if k in ext:
                continue
            SEM_CTR[0] += 1
            sems[k] = nc.alloc_semaphore(name="sem%d" % SEM_CTR[0])
        self.nsems = len(sems)
        LAST_SEMS[:] = list(sems.values())
        sems.update(ext)
        per = {e: [o for o in self.ops if o.eng == e] for e in ENGS}
        block = stack.enter_context(nc.Block())

        def run(eng_name, eng):
            for o in per[eng_name]:
                for k, v in o.waits:
                    eng.wait_ge(sems[k], v)
                ins = o.fn(eng)
                if o.sig:
                    if o.is_dma:
                        ins.then_inc(sems[o.key], 1 if o.key.startswith("cc:") else 16)
                    else:
                        ins.then_inc(sems[o.key], 1)
            if eng_name == "sp":
                done = {}
                for o in tail_waits:
                    done[o.key] = max(done.get(o.key, 0), o.semval)
                for k, v in done.items():
                    eng.wait_ge(sems[k], v)

        @block.tensor
        def _(e):
            run("pe", e)

        @block.scalar
        def _(e):
            run("act", e)

        @block.vector
        def _(e):
            run("dve", e)

        @block.gpsimd
        def _(e):
            run("pool", e)

        @block.sync
        def _(e):
            run("sp", e)


D = 1024
DFF = 2816
NFC = 22
ALPHA = 4.0 ** 0.25
LN_EPS = 1e-5
EPS_P = LN_EPS / (ALPHA * ALPHA)


class Ctx:
    def __init__(self, nc, st, pfx=""):
        self.nc = nc
        self.st = st
        self.S = Sched(nc)
        self.n = 0
        self.pfx = pfx

    def sb(self, name, shape, dt):
        return self.st.enter_context(self.nc.sbuf_tensor(self.pfx + "sb_" + name, list(shape), dt))

    def ps(self, name, shape, dt=F32):
        return self.st.enter_context(self.nc.psum_tensor(self.pfx + "ps_" + name, list(shape), dt))

    def R(self, name, excl=False):
        return self.S.res(name, excl)


def make_consts(C):
    S = C.S
    C.ones = C.sb("ones", [128, 128], F32)
    C.r_ones = C.R("ones")
    S.op("pool", lambda e: e.memset(C.ones[:], 1.0), w=[C.r_ones])
    C.ident = C.sb("ident", [128, 128], F32)
    C.r_ident = C.R("ident")
    S.op("pool", lambda e: e.memset(C.ident[:], 1.0), w=[C.r_ident])
    S.op("pool", lambda e: e.affine_select(out=C.ident[:], in_=C.ident[:], pattern=[[-1, 128]], compare_op=ALU.is_equal, fill=0.0, base=0, channel_multiplier=1), r=[C.r_ident], w=[C.r_ident])


def compute_mods(C, cT_d, adaw_d, adab_d, nm, wbufs, psm, r_psm):
    S = C.S
    cT = C.sb("cT", [128, 8], F32)
    r_cT = C.R("cT")
    S.dma("sp", cT[:], cT_d, w=[r_cT], key="cT")
    S.op("act", lambda e: e.activation(out=cT[:], in_=cT[:], func=AF.Silu), r=[r_cT], w=[r_cT])
    adab = C.sb("adab", [128, nm * 8], F32)
    r_adab = C.R("adab")
    S.dma("sp", adab[:], adab_d, w=[r_adab], key="adab")
    mods = C.sb("mods", [128, nm * 8], F32)
    r_mods = C.R("mods")
    src = adaw_d.rearrange("(kc p) n -> p kc n", p=128)
    st_tmp = None
    if wbufs is None:
        st_keep, st_tmp = C.st, ExitStack()
        C.st = st_tmp
        wbufs = [(C.sb("wA%d" % i, [128, 8, 256], F32), C.R("wA%d" % i)) for i in range(2)]
        C.st = st_keep
    for m in range(nm):
        for q in range(4):
            bi = (m * 4 + q) % 2
            wt, r_wt = wbufs[bi]
            c0 = m * 1024 + q * 256
            S.dma("sp", wt[:], src[:, :, c0:c0 + 256], w=[r_wt], key="adaw%d" % bi)
            for j in range(2):
                cc = q * 2 + j
                for kc in range(8):
                    S.op("pe", lambda e, m=m, cc=cc, kc=kc, j=j, wt=wt: e.matmul(psm[:, m * 8 + cc:m * 8 + cc + 1], wt[:, kc, j * 128:(j + 1) * 128], cT[:, kc:kc + 1], start=(kc == 0), stop=(kc == 7)), r=[r_wt, r_cT], w=[r_psm])
    S.op("dve", lambda e: e.tensor_tensor(out=mods[:], in0=psm[:, 0:nm * 8], in1=adab[:], op=ALU.add), r=[r_psm, r_adab], w=[r_mods])
    if st_tmp is not None:
        st_tmp.close()
    return mods, r_mods


def ln_group(C, zT, r_z, g, n0, colA, colB, r_cols, outs, P):
    S = C.S
    ps_s, r_ps_s = P["st0"]
    ps_q, r_ps_q = P["st1"]
    sq, r_sq = P["sq"]
    for cc in range(8):
        S.op("act", lambda e, cc=cc: e.activation(out=sq[cc % 2][:], in_=zT[:, cc, n0:n0 + 512], func=AF.Square), r=[r_z[cc]], w=[r_sq[cc % 2]])
        S.op("pe", lambda e, cc=cc: e.matmul(ps_s[:], C.ones[:], zT[:, cc, n0:n0 + 512], start=(cc == 0), stop=(cc == 7)), r=[C.r_ones, r_z[cc]], w=[r_ps_s])
        S.op("pe", lambda e, cc=cc: e.matmul(ps_q[:], C.ones[:], sq[cc % 2][:], start=(cc == 0), stop=(cc == 7)), r=[C.r_ones, r_sq[cc % 2]], w=[r_ps_q])
    mean, r_mean = P["mean"]
    rstd, r_rstd = P["rstd"]
    S.op("act", lambda e: e.mul(out=mean[:], in_=ps_s[:], mul=1.0 / D), r=[r_ps_s], w=[r_mean])
    S.op("dve", lambda e: e.tensor_tensor(out=rstd[:], in0=mean[:], in1=mean[:], op=ALU.mult), r=[r_mean], w=[r_rstd])
    S.op("dve", lambda e: e.scalar_tensor_tensor(out=rstd[:], in0=ps_q[:], scalar=1.0 / D, in1=rstd[:], op0=ALU.mult, op1=ALU.subtract), r=[r_ps_q, r_rstd], w=[r_rstd])
    S.op("dve", lambda e: e.tensor_scalar(out=rstd[:], in0=rstd[:], scalar1=P["eps"], scalar2=None, op0=ALU.add), r=[r_rstd], w=[r_rstd])
    S.op("act", lambda e: e.activation(out=rstd[:], in_=rstd[:], func=AF.Ln), r=[r_rstd], w=[r_rstd])
    S.op("act", lambda e: e.activation(out=rstd[:], in_=rstd[:], func=AF.Exp, scale=-0.5), r=[r_rstd], w=[r_rstd])
    S.op("dve", lambda e: e.scalar_tensor_tensor(out=mean[:], in0=mean[:], scalar=-1.0, in1=rstd[:], op0=ALU.mult, op1=ALU.mult), r=[r_mean, r_rstd], w=[r_mean])
    tt, r_tt = P["tt"]
    for cc in range(8):
        k = cc % 2
        S.op("dve", lambda e, cc=cc, k=k: e.tensor_tensor(out=tt[k][:], in0=zT[:, cc, n0:n0 + 512], in1=rstd[:], op=ALU.mult), r=[r_z[cc], r_rstd], w=[r_tt[k]])
        S.op("dve", lambda e, cc=cc, k=k: e.tensor_tensor(out=tt[k][:], in0=tt[k][:], in1=mean[:], op=ALU.add), r=[r_tt[k], r_mean], w=[r_tt[k]])
        for (dst, r_dst, gt, go, bt, bo, eng) in outs:
            if eng == "act":
                S.op("act", lambda e, cc=cc, k=k, dst=dst, gt=gt, go=go, bt=bt, bo=bo: e.activation(out=dst[:, cc, n0:n0 + 512], in_=tt[k][:], func=AF.Identity, scale=gt[:, go + cc:go + cc + 1], bias=bt[:, bo + cc:bo + cc + 1]), r=[r_tt[k]] + r_cols, w=[r_dst[cc]])
            else:
                S.op(eng, lambda e, cc=cc, k=k, dst=dst, gt=gt, go=go, bt=bt, bo=bo: e.tensor_scalar(out=dst[:, cc, n0:n0 + 512], in0=tt[k][:], scalar1=gt[:, go + cc:go + cc + 1], scalar2=bt[:, bo + cc:bo + cc + 1], op0=ALU.mult, op1=ALU.add), r=[r_tt[k]] + r_cols, w=[r_dst[cc]])


def build_tl(NT=2048, HALF=1024, debug=False):
    nc = bass.Bass("TRN2", target_bir_lowering=False)
    dt = lambda n, s, d, k="ExternalInput": nc.dram_tensor(n, list(s), d, kind=k).ap()
    yT_d = dt("yT", [D, NT], BF16)
    xT_d = dt("xT", [D, NT], F32)
    cT_d = dt("cT", [128, 8], F32)
    adaw_d = dt("adaw", [D, 4 * D], F32)
    adab_d = dt("adab", [128, 32], F32)
    lng_d = dt("lng", [128, 16], F32)
    lnb_d = dt("lnb", [128, 16], F32)
    wo_d = dt("wo", [D, D], F32)
    win_d = dt("win", [D, 2 * DFF], F32)
    wout_d = dt("wout", [DFF, D], F32)
    oT_d = dt("oT", [D, NT], F32, "ExternalOutput")
    tl_phase(nc, "", yT_d, xT_d, cT_d, adaw_d, adab_d, lng_d, lnb_d, wo_d, win_d, wout_d, oT_d, NT, HALF, debug=debug)
    return nc


def tl_phase(nc, pfx, yT_d, xT_d, cT_d, adaw_d, adab_d, lng_d, lnb_d, wo_d, win_d, wout_d, oT_d, NT=2048, HALF=1024, debug=False, sel_d=None, after=None, before_y=None):
    dt = lambda n, s, d, k="ExternalInput": nc.dram_tensor(n, list(s), d, kind=k).ap()
    NG = HALF // 512
    if debug:
        dbg_cols = dt("dbg_cols", [128, 48], F32, "ExternalOutput")
        dbg_mods = dt("dbg_mods", [128, 32], F32, "ExternalOutput")
        dbg_x1 = dt("dbg_x1", [D, HALF], F32, "ExternalOutput")
        dbg_h = dt("dbg_h", [D, HALF], BF16, "ExternalOutput")
        dbg_a = dt("dbg_a", [DFF, HALF], BF16, "ExternalOutput")
    with ExitStack() as st:
        C = Ctx(nc, st, pfx)
        S = C.S
        make_consts(C)
        PB = [(C.ps("pb%d" % i, [128, 512]), C.R("pb%d" % i, True)) for i in range(8)]
        if sel_d is not None:
            selt = C.sb("selt", [128, 2], F32)
            r_selt = C.R("selt")
            S.dma("sp", selt[:], sel_d, w=[r_selt], key="selt")
            ystg = [[(C.sb("ystg%d_%d" % (i, j), [128, HALF], BF16), C.R("ystg%d_%d" % (i, j))) for j in range(2)] for i in range(2)]
        xT = C.sb("xT", [128, 8, HALF], F32)
        r_x = [C.R("xT%d" % cc) for cc in range(8)]
        yT = C.sb("yT", [128, 8, HALF], BF16)
        r_y = [C.R("yT%d" % cc) for cc in range(8)]
        hT = yT
        r_h = r_y
        aT = C.sb("aT", [128, NFC, HALF], BF16)
        r_a = [C.R("aT%d" % f) for f in range(NFC)]
        wo = C.sb("wo", [128, 8, D], BF16)
        r_wo = C.R("wo")
        wA = [(C.sb("wA%d" % i, [128, 8, 256], F32), C.R("wA%d" % i)) for i in range(2)]
        wi = [(C.sb("wi%d" % i, [128, 8, 512], BF16), C.R("wi%d" % i)) for i in range(2)]
        wo2 = [(C.sb("wo2%d" % i, [128, NFC, 256], BF16), C.R("wo2%d" % i)) for i in range(2)]
        P = {
            "st0": PB[6], "st1": PB[7],
            "sq": ([C.sb("sq%d" % i, [128, 512], F32) for i in range(2)], [C.R("sq%d" % i) for i in range(2)]),
            "mean": (C.sb("mean", [128, 512], F32), C.R("mean")),
            "rstd": (C.sb("rstd", [128, 512], F32), C.R("rstd")),
            "tt": ([C.sb("tt%d" % i, [128, 512], F32) for i in range(2)], [C.R("tt%d" % i) for i in range(2)]),
            "eps": EPS_P,
        }
        sl = [C.sb("sl%d" % i, [128, 512], F32) for i in range(2)]
        r_sl = [C.R("sl%d" % i) for i in range(2)]
        mods, r_mods = compute_mods(C, cT_d, adaw_d, adab_d, 4, wA, PB[5][0], PB[5][1])
        lng = C.sb("lng", [128, 16], F32)
        lnb = C.sb("lnb", [128, 16], F32)
        r_ln = C.R("ln")
        S.dma("sp", lng[:], lng_d, w=[r_ln], key="lng")
        S.dma("sp", lnb[:], lnb_d, w=[r_ln], key="lnb")
        cols = C.sb("cols", [128, 48], F32)
        r_cols = C.R("cols")
        S.op("dve", lambda e: e.tensor_scalar(out=cols[:, 0:8], in0=mods[:, 0:8], scalar1=1.0 / ALPHA, scalar2=None, op0=ALU.mult), r=[r_mods], w=[r_cols])
        S.op("dve", lambda e: e.tensor_scalar(out=cols[:, 8:16], in0=mods[:, 24:32], scalar1=1.0 / ALPHA, scalar2=None, op0=ALU.mult), r=[r_mods], w=[r_cols])
        S.op("dve", lambda e: e.tensor_scalar(out=cols[:, 32:40], in0=mods[:, 16:24], scalar1=1.0, scalar2=None, op0=ALU.add), r=[r_mods], w=[r_cols])
        S.op("dve", lambda e: e.tensor_tensor(out=cols[:, 16:24], in0=lng[:, 0:8], in1=cols[:, 32:40], op=ALU.mult), r=[r_ln, r_cols], w=[r_cols])
        S.op("dve", lambda e: e.tensor_tensor(out=cols[:, 24:32], in0=lnb[:, 0:8], in1=cols[:, 32:40], op=ALU.mult), r=[r_ln, r_cols], w=[r_cols])
        S.op("dve", lambda e: e.tensor_tensor(out=cols[:, 24:32], in0=cols[:, 24:32], in1=mods[:, 8:16], op=ALU.add), r=[r_mods, r_cols], w=[r_cols])
        rc = [r_cols, r_ln]
        S.dma("pool", wo[:], wo_d.rearrange("(kc p) n -> p kc n", p=128), w=[r_wo], key="wo")
        xsrc = xT_d.rearrange("(kc p) t -> p kc t", p=128)
        if sel_d is None:
            ysrc_ = yT_d.rearrange("(kc p) t -> p kc t", p=128)
            ysrc_fn = lambda cc, a, n: ysrc_[:, cc, a:a + n]
        else:
            ysrc_fn = lambda cc, a, n: yT_d[cc % 4, cc // 4, :, a:a + n]
        osrc = oT_d.rearrange("(kc p) t -> p kc t", p=128)
        winr = win_d.rearrange("(kc p) n -> p kc n", p=128)
        woutr = wout_d.rearrange("(fc p) n -> p fc n", p=128)
        tails = []
        pbi = 0
        for half in range(NT // HALF):
            t0 = half * HALF
            def load_y(hf):
                ty = hf * HALF
                if before_y is not None and hf == 0:
                    before_y(S)
                for cc in range(8):
                    if sel_d is None:
                        S.dma("sp", yT[:, cc, :], ysrc_fn(cc, ty, HALF), w=[r_y[cc]], key="yT%d" % cc)
                    else:
                        (ya, r_ya), (yb, r_yb) = ystg[cc % 2]
                        S.dma("sp", ya[:], ysrc_fn(cc, ty, HALF), w=[r_ya], key="ysa%d" % (cc % 2))
                        S.dma("sp", yb[:], ysrc_fn(cc, NT + ty, HALF), w=[r_yb], key="ysb%d" % (cc % 2))
                        S.op("dve", lambda e, ya=ya: e.tensor_scalar(out=ya[:], in0=ya[:], scalar1=selt[:, 0:1], scalar2=None, op0=ALU.mult), r=[r_ya, r_selt], w=[r_ya])
                        S.op("dve", lambda e, ya=ya, yb=yb, cc=cc: e.scalar_tensor_tensor(out=yT[:, cc, :], in0=yb[:], scalar=selt[:, 1:2], in1=ya[:], op0=ALU.mult, op1=ALU.add), r=[r_ya, r_yb, r_selt], w=[r_y[cc]])

            if half == 0:
                load_y(0)
            for cc in range(8):
                S.dma("sp", xT[:, cc, :], xsrc[:, cc, t0:t0 + HALF], w=[r_x[cc]], key="xT%d" % cc)
            for g in range(NG):
                n0 = g * 512
                for cc in range(8):
                    pb, r_pb = PB[pbi % 4]
                    pbi += 1
                    for kc in range(8):
                        S.op("pe", lambda e, pb=pb, cc=cc, kc=kc, n0=n0: e.matmul(pb[:], wo[:, kc, cc * 128:(cc + 1) * 128], yT[:, kc, n0:n0 + 512], start=(kc == 0), stop=(kc == 7)), r=[r_wo] + r_y, w=[r_pb])
                    S.op("dve", lambda e, pb=pb, cc=cc, n0=n0: e.scalar_tensor_tensor(out=xT[:, cc, n0:n0 + 512], in0=pb[:], scalar=cols[:, cc:cc + 1], in1=xT[:, cc, n0:n0 + 512], op0=ALU.mult, op1=ALU.add), r=[r_pb, r_x[cc], r_cols], w=[r_x[cc]])
                ln_group(C, xT, r_x, g, n0, None, None, rc,
                         [(xT, r_x, lng, 0, lnb, 0, "act"), (hT, r_h, cols, 16, cols, 24, "act")], P)
            if debug and half == 0:
                tails.append(S.dma("sp", dbg_cols, cols[:], r=[r_cols], key="dbg0"))
                tails.append(S.dma("sp", dbg_mods, mods[:], r=[r_mods], key="dbg1"))
                tails.append(S.dma("sp", dbg_x1.rearrange("(kc p) t -> p kc t", p=128), xT[:], r=r_x, key="dbg2"))
                tails.append(S.dma("sp", dbg_h.rearrange("(kc p) t -> p kc t", p=128), hT[:], r=r_h, key="dbg3"))
            for blk in range(NFC // 2):
                wt, r_wt = wi[blk % 2]
                c0 = blk * 256
                S.dma("pool", wt[:, :, 0:256], winr[:, :, c0:c0 + 256], w=[r_wt], key="wi%d" % (blk % 2))
                S.dma("pool", wt[:, :, 256:512], winr[:, :, DFF + c0:DFF + c0 + 256], w=[r_wt], key="wi%d" % (blk % 2))
                for g in range(NG):
                    n0 = g * 512
                    for j in range(2):
                        fc = blk * 2 + j
                        pg, r_pg = PB[pbi % 4]
                        pu, r_pu = PB[(pbi + 1) % 4]
                        pbi += 2
                        for kc in range(8):
                            S.op("pe", lambda e, pg=pg, wt=wt, j=j, kc=kc, n0=n0: e.matmul(pg[:], wt[:, kc, j * 128:(j + 1) * 128], hT[:, kc, n0:n0 + 512], start=(kc == 0), stop=(kc == 7)), r=[r_wt] + r_h, w=[r_pg])
                        for kc in range(8):
                            S.op("pe", lambda e, pu=pu, wt=wt, j=j, kc=kc, n0=n0: e.matmul(pu[:], wt[:, kc, 256 + j * 128:256 + (j + 1) * 128], hT[:, kc, n0:n0 + 512], start=(kc == 0), stop=(kc == 7)), r=[r_wt] + r_h, w=[r_pu])
                        k = (fc * NG + g) % 2
                        S.op("act", lambda e, pg=pg, k=k: e.activation(out=sl[k][:], in_=pg[:], func=AF.Silu), r=[r_pg], w=[r_sl[k]])
                        S.op("dve", lambda e, pu=pu, k=k, fc=fc, n0=n0: e.tensor_tensor(out=aT[:, fc, n0:n0 + 512], in0=pu[:], in1=sl[k][:], op=ALU.mult), r=[r_pu, r_sl[k]], w=[r_a[fc]])
            if half + 1 < NT // HALF:
                load_y(half + 1)
            if debug and half == 0:
                tails.append(S.dma("sp", dbg_a.rearrange("(kc p) t -> p kc t", p=128), aT[:], r=r_a, key="dbg4"))
            for blk in range(4):
                wt, r_wt = wo2[blk % 2]
                S.dma("pool", wt[:], woutr[:, :, blk * 256:(blk + 1) * 256], w=[r_wt], key="wo2%d" % (blk % 2))
                for j in range(2):
                    cc = blk * 2 + j
                    for g in range(NG):
                        n0 = g * 512
                        pb, r_pb = PB[pbi % 4]
                        pbi += 1
                        for fc in range(NFC):
                            S.op("pe", lambda e, pb=pb, wt=wt, j=j, fc=fc, n0=n0: e.matmul(pb[:], wt[:, fc, j * 128:(j + 1) * 128], aT[:, fc, n0:n0 + 512], start=(fc == 0), stop=(fc == NFC - 1)), r=[r_wt, r_a[fc]], w=[r_pb])
                        S.op("dve", lambda e, pb=pb, cc=cc, n0=n0: e.scalar_tensor_tensor(out=xT[:, cc, n0:n0 + 512], in0=pb[:], scalar=cols[:, 8 + cc:9 + cc], in1=xT[:, cc, n0:n0 + 512], op0=ALU.mult, op1=ALU.add), r=[r_pb, r_x[cc], r_cols], w=[r_x[cc]])
            for g in range(NG):
                n0 = g * 512
                ln_group(C, xT, r_x, g, n0, None, None, rc, [(xT, r_x, lng, 8, lnb, 8, "act")], P)
            for cc in range(8):
                o = S.dma("sp", osrc[:, cc, t0:t0 + HALF], xT[:, cc, :], r=[r_x[cc]], w=[], key="oT%d" % cc)
                tails.append(o)
        if after is not None:
            after(C, tails)
        S.emit(st, tail_waits=tails)


T = 4096
NH = 8
HD = 64
QK_EPS = 1e-6
NEG = -30000.0
import os
LAG = int(os.environ.get("FOX_LAG", "2"))
FOX_DVE = int(os.environ.get("FOX_DVE", "1"))


def build_fox(TT=T, debug=False):
    nc = bass.Bass("TRN2", target_bir_lowering=False)
    dt = lambda n, s, d, k="ExternalInput": nc.dram_tensor(n, list(s), d, kind=k).ap()
    ins = fox_decl(dt, TT)
    yg_d = dt("ygT", [512, TT], BF16, "ExternalOutput")
    fox_phase(nc, "", ins, yg_d, TT, debug=debug)
    return nc


def fox_decl(dt, TT=T, sfx=""):
    return dict(
        x=dt("x" + sfx, [TT, D], F32), cT=dt("cT" + sfx, [128, 8], F32), adaw=dt("adaw" + sfx, [D, 2 * D], F32), adab=dt("adab" + sfx, [128, 16], F32),
        wq=dt("wq" + sfx, [D, 512], F32), wk=dt("wk" + sfx, [D, 512], F32), wv=dt("wv" + sfx, [D, 512], F32), wf=dt("wf" + sfx, [D, 8], F32), wg=dt("wg" + sfx, [D, 512], F32),
        bfb=dt("bfb" + sfx, [128, 8], F32), qgb=dt("qgb" + sfx, [128, 512], F32), kgb=dt("kgb" + sfx, [128, 512], F32))


def fox_phase(nc, pfx, ins, yg_d, TT=T, debug=False, after=None):
    dt = lambda n, s, d, k="ExternalInput": nc.dram_tensor(n, list(s), d, kind=k).ap()
    x_d, cT_d, adaw_d, adab_d = ins["x"], ins["cT"], ins["adaw"], ins["adab"]
    wq_d, wk_d, wv_d, wf_d, wg_d = ins["wq"], ins["wk"], ins["wv"], ins["wf"], ins["wg"]
    bf_d, qg_d, kg_d = ins["bfb"], ins["qgb"], ins["kgb"]
    NGR = TT // 512
    NTL = TT // 128
    if debug:
        dbg_h = dt("dbg_h", [D, 512], BF16, "ExternalOutput")
        dbg_q = dt("dbg_q", [128, 4, 512], BF16, "ExternalOutput")
        dbg_k = dt("dbg_k", [128, 4, TT], BF16, "ExternalOutput")
        dbg_F = dt("dbg_F", [128, NTL, 8], F32, "ExternalOutput")
        dbg_v = dt("dbg_v", [128, NTL, 8, 65], BF16, "ExternalOutput")
    with ExitStack() as st:
        C = Ctx(nc, st, pfx)
        S = C.S
        make_consts(C)
        identb = C.sb("identb", [128, 128], BF16)
        r_identb = C.R("identb")
        S.op("dve", lambda e: e.tensor_copy(out=identb[:], in_=C.ident[:]), r=[C.r_ident], w=[r_identb])
        tri = C.sb("tri", [128, 128], F32)
        r_tri = C.R("tri")
        S.op("pool", lambda e: e.memset(tri[:], 1.0), w=[r_tri])
        S.op("pool", lambda e: e.affine_select(out=tri[:], in_=tri[:], pattern=[[1, 128]], compare_op=ALU.is_ge, fill=0.0, base=0, channel_multiplier=-1), r=[r_tri], w=[r_tri])
        selA = C.sb("selA", [16, 8, 128], F32)
        selB = C.sb("selB", [16, 8, 128], F32)
        sel = C.sb("sel", [16, 8, 128], BF16)
        r_sel = C.R("sel")
        S.op("pool", lambda e: e.memset(selA[:], 1.0), w=[r_sel])
        S.op("pool", lambda e: e.memset(selB[:], 1.0), w=[r_sel])
        S.op("pool", lambda e: e.affine_select(out=selA[:], in_=selA[:], pattern=[[-1, 8], [0, 128]], compare_op=ALU.is_equal, fill=0.0, base=0, channel_multiplier=1), r=[r_sel], w=[r_sel])
        S.op("pool", lambda e: e.affine_select(out=selB[:], in_=selB[:], pattern=[[-1, 8], [0, 128]], compare_op=ALU.is_equal, fill=0.0, base=-8, channel_multiplier=1), r=[r_sel], w=[r_sel])
        S.op("dve", lambda e: e.tensor_tensor(out=sel[:], in0=selA[:], in1=selB[:], op=ALU.add), r=[r_sel], w=[r_sel])
        maskf = C.sb("maskf", [128, 4, 512], F32)
        maskb = C.sb("maskb", [128, 4, 512], BF16)
        r_mask = C.R("mask")
        S.op("pool", lambda e: e.memset(maskf[:], 0.0), w=[r_mask])
        for j in range(4):
            S.op("pool", lambda e, j=j: e.affine_select(out=maskf[:, j, :], in_=maskf[:, j, :], pattern=[[1, 512]], compare_op=ALU.is_ge, fill=NEG, base=-128 * j, channel_multiplier=-1), r=[r_mask], w=[r_mask])
        S.op("dve", lambda e: e.tensor_copy(out=maskb[:], in_=maskf[:]), r=[r_mask], w=[r_mask])
        PB = [(C.ps("pb%d" % i, [128, 512]), C.R("pb%d" % i, True)) for i in range(7)]
        pbb = C.ps("pbb", [128, 1024], BF16)
        r_pbb = C.R("pbb", True)
        wA = [(C.sb("wA%d" % i, [128, 8, 256], F32), C.R("wA%d" % i)) for i in range(2)]
        mods, r_mods = compute_mods(C, cT_d, adaw_d, adab_d, 2, wA, PB[6][0], PB[6][1])
        cols = C.sb("cols", [128, 8], F32)
        r_cols = C.R("cols")
        S.op("dve", lambda e: e.tensor_scalar(out=cols[:], in0=mods[:, 8:16], scalar1=1.0, scalar2=None, op0=ALU.add), r=[r_mods], w=[r_cols])
        wts = {}
        r_w = C.R("wts")
        for nm, d_ in (("wq", wq_d), ("wk", wk_d), ("wv", wv_d), ("wg", wg_d)):
            wts[nm] = C.sb(nm, [128, 8, 512], BF16)
            S.dma("pool", wts[nm][:], d_.rearrange("(kc p) n -> p kc n", p=128), w=[r_w], key=nm)
        wf = C.sb("wf", [128, 8, 8], BF16)
        S.dma("pool", wf[:], wf_d.rearrange("(kc p) n -> p kc n", p=128), w=[r_w], key="wf")
        bfb = C.sb("bfb", [128, 8], F32)
        qgb = C.sb("qgb", [128, 512], F32)
        kgb = C.sb("kgb", [128, 512], F32)
        r_par = C.R("par")
        S.dma("sp", bfb[:], bf_d, w=[r_par], key="bfb")
        S.dma("sp", qgb[:], qg_d, w=[r_par], key="qgb")
        S.dma("sp", kgb[:], kg_d, w=[r_par], key="kgb")
        xt = C.sb("xt", [128, 4, D], F32)
        r_xt = C.R("xt")
        hT = C.sb("hT", [128, 8, 512], BF16)
        r_hT = [C.R("hT%d" % k) for k in range(8)]
        kT = C.sb("kT", [128, 4, TT], BF16)
        r_kT = [C.R("kT%d" % g) for g in range(NGR)]
        qT = C.sb("qT", [128, 4, 512], BF16)
        r_qT = C.R("qT")
        Va = C.sb("Va", [128, NTL, NH, 65], BF16)
        r_V = [C.R("V%d" % g) for g in range(NGR)]
        r_Vones = C.R("Vones")
        S.op("pool", lambda e: e.memset(Va[:, :, :, 64:65], 1.0), w=[r_Vones])
        eg = C.sb("eg", [64, NH, 512], F32)
        r_eg = [C.R("eg%d" % h) for h in range(NH)]
        Fcol = C.sb("Fcol", [128, NTL, NH], F32)
        r_F = C.R("Fcol")
        carry = C.sb("carry", [128, NTL + 1, NH], F32)
        r_carry = C.R("carry")
        S.op("dve", lambda e: e.memset(carry[:, 0, :], 0.0), w=[r_carry])
        biasG = C.sb("biasG", [128, NTL, NH], F32)
        r_bias = C.R("biasG")
        FqT = C.sb("FqT", [16, 512], BF16)
        r_FqT = C.R("FqT")
        sq = C.sb("sq", [128, 512], F32); r_sq = C.R("sq")
        tmp = C.sb("tmp", [128, 512], F32); r_tmp = C.R("tmp")
        qtok = C.sb("qtok", [128, 512], BF16); r_qtok = C.R("qtok")
        ktok = C.sb("ktok", [128, 512], BF16); r_ktok = C.R("ktok")
        ssq = C.sb("ssq", [128, 16], F32); r_ssq = C.R("ssq")
        zf = C.sb("zf", [128, 8], F32); r_zf = C.R("zf")
        fq = C.sb("fq", [128, 8], F32); r_fq = C.R("fq")
        fq16 = C.sb("fq16", [128, 16], BF16); r_fq16 = C.R("fq16")
        PT = [(C.sb("PT%d" % i, [128, 512], BF16), C.R("PT%d" % i)) for i in range(4)]
        oT = C.sb("oT", [64, 512], F32); r_oT = C.R("oT")
        dn = C.sb("dn", [128, 512], F32); r_dn = C.R("dn")
        wv_ = C.sb("wv_", [64, 512], F32); r_wv_ = C.R("wv_")
        yg = [(C.sb("yg%d" % i, [64, 512], BF16), C.R("yg%d" % i)) for i in range(2)]
        tails = []
        xsrc = x_d.rearrange("(i p) d -> p i d", p=128)
        ps_i = 0
        for G in range(NGR):
            if G == 0:
                S.dma("sp", xt[:], xsrc[:, 0:4, :], w=[r_xt], key="xt")
            for kc in range(8):
                pb, r_pb = PB[kc % 2]
                for i in range(4):
                    S.op("pe", lambda e, pb=pb, i=i, kc=kc: e.transpose(pb[:, i * 128:(i + 1) * 128], xt[:, i, kc * 128:(kc + 1) * 128], C.ident[:]), r=[r_xt, C.r_ident], w=[r_pb])
                if kc % 2 == 0:
                    S.op("act", lambda e, pb=pb, kc=kc: e.activation(out=hT[:, kc, :], in_=pb[:], func=AF.Identity, scale=cols[:, kc:kc + 1], bias=mods[:, kc:kc + 1]), r=[r_pb, r_cols, r_mods], w=[r_hT[kc]])
                else:
                    S.op("dve", lambda e, pb=pb, kc=kc: e.tensor_scalar(out=hT[:, kc, :], in0=pb[:], scalar1=cols[:, kc:kc + 1], scalar2=mods[:, kc:kc + 1], op0=ALU.mult, op1=ALU.add), r=[r_pb, r_cols, r_mods], w=[r_hT[kc]])
            if debug and G == 0:
                tails.append(S.dma("sp", dbg_h.rearrange("(kc p) t -> p kc t", p=128), hT[:], r=r_hT, key="dbg0"))
            if G + 1 < NGR:
                S.dma("sp", xt[:], xsrc[:, 4 * (G + 1):4 * (G + 1) + 4, :], w=[r_xt], key="xt")
            for i in range(4):
                tl_ = 4 * G + i
                (pq, r_pq), (pk, r_pk), (pv, r_pv), (pf, r_pf) = PB[2], PB[3], PB[4], PB[5]
                for kc in range(8):
                    lw = hT[:, kc, i * 128:(i + 1) * 128]
                    S.op("pe", lambda e, lw=lw, kc=kc: e.matmul(pq[:], lw, wts["wq"][:, kc, :], start=(kc == 0), stop=(kc == 7)), r=[r_w, r_hT[kc]], w=[r_pq])
                    S.op("pe", lambda e, lw=lw, kc=kc: e.matmul(pk[:], lw, wts["wk"][:, kc, :], start=(kc == 0), stop=(kc == 7)), r=[r_w, r_hT[kc]], w=[r_pk])
                    S.op("pe", lambda e, lw=lw, kc=kc: e.matmul(pv[:], lw, wts["wv"][:, kc, :], start=(kc == 0), stop=(kc == 7)), r=[r_w, r_hT[kc]], w=[r_pv])
                    S.op("pe", lambda e, lw=lw, kc=kc: e.matmul(pf[:, 0:8], lw, wf[:, kc, :], start=(kc == 0), stop=(kc == 7)), r=[r_w, r_hT[kc]], w=[r_pf])
                for h in (2 * i, 2 * i + 1):
                    pb, r_pb = PB[h % 2]
                    for kc in range(8):
                        S.op("pe", lambda e, pb=pb, h=h, kc=kc: e.matmul(pb[0:64, :], wts["wg"][:, kc, h * 64:(h + 1) * 64], hT[:, kc, :], start=(kc == 0), stop=(kc == 7)), r=[r_w, r_hT[kc]], w=[r_pb])
                    S.op("act", lambda e, pb=pb, h=h: e.activation(out=eg[:, h, :], in_=pb[0:64, :], func=AF.Exp, scale=-1.0), r=[r_pb], w=[r_eg[h]])
                for which, (pp, r_pp), gb, tok, r_tok, eps_, mul_ in (("q", (pq, r_pq), qgb, qtok, r_qtok, 64.0 * QK_EPS, 1.0), ("k", (pk, r_pk), kgb, ktok, r_ktok, QK_EPS, 1.0 / 64.0)):
                    o8 = 0 if which == "q" else 8
                    S.op("act", lambda e, pp=pp: e.activation(out=sq[:], in_=pp[:], func=AF.Square), r=[r_pp], w=[r_sq])
                    S.op("dve", lambda e, o8=o8: e.tensor_reduce(out=ssq[:, o8:o8 + 8], in_=sq[:].rearrange("p (h d) -> p h d", d=64), axis=AX.X, op=ALU.add), r=[r_sq], w=[r_ssq])
                    S.op("dve", lambda e, o8=o8, eps_=eps_, mul_=mul_: e.tensor_scalar(out=ssq[:, o8:o8 + 8], in0=ssq[:, o8:o8 + 8], scalar1=mul_, scalar2=eps_, op0=ALU.mult, op1=ALU.add), r=[r_ssq], w=[r_ssq])
                    S.op("act", lambda e, o8=o8: e.activation(out=ssq[:, o8:o8 + 8], in_=ssq[:, o8:o8 + 8], func=AF.Ln), r=[r_ssq], w=[r_ssq])
                    S.op("act", lambda e, o8=o8: e.activation(out=ssq[:, o8:o8 + 8], in_=ssq[:, o8:o8 + 8], func=AF.Exp, scale=-0.5), r=[r_ssq], w=[r_ssq])
                    S.op("dve", lambda e, pp=pp, o8=o8: e.tensor_tensor(out=tmp[:].rearrange("p (h d) -> p h d", d=64), in0=pp[:].rearrange("p (h d) -> p h d", d=64), in1=ssq[:, o8:o8 + 8].unsqueeze(2).to_broadcast([128, 8, 64]), op=ALU.mult), r=[r_pp, r_ssq], w=[r_tmp])
                    S.op("dve", lambda e, gb=gb, tok=tok: e.tensor_tensor(out=tok[:], in0=tmp[:], in1=gb[:], op=ALU.mult), r=[r_tmp, r_par], w=[r_tok])
                S.op("act", lambda e, tl_=tl_: e.copy(out=Va[:, tl_, :, 0:64], in_=pv[:].rearrange("p (h d) -> p h d", d=64)), r=[r_pv, r_Vones], w=[r_V[G]])
                S.op("dve", lambda e: e.tensor_tensor(out=zf[:], in0=pf[:, 0:8], in1=bfb[:], op=ALU.add), r=[r_pf, r_par], w=[r_zf])
                S.op("act", lambda e: e.activation(out=zf[:], in_=zf[:], func=AF.Exp, scale=-1.0), r=[r_zf], w=[r_zf])
                S.op("act", lambda e: e.activation(out=zf[:], in_=zf[:], func=AF.Ln, bias=1.0), r=[r_zf], w=[r_zf])
                pc, r_pc = PB[6]
                S.op("pe", lambda e: e.matmul(pc[:, 0:8], tri[:], zf[:], start=True, stop=True), r=[r_tri, r_zf], w=[r_pc])
                S.op("pe", lambda e: e.matmul(pc[:, 8:16], C.ones[:], zf[:], start=True, stop=True), r=[C.r_ones, r_zf], w=[r_pc])
                S.op("dve", lambda e, tl_=tl_: e.tensor_tensor(out=Fcol[:, tl_, :], in0=carry[:, tl_, :], in1=pc[:, 0:8], op=ALU.subtract), r=[r_carry, r_pc], w=[r_F])
                S.op("dve", lambda e, tl_=tl_: e.tensor_tensor(out=carry[:, tl_ + 1, :], in0=carry[:, tl_, :], in1=pc[:, 8:16], op=ALU.subtract), r=[r_carry, r_pc], w=[r_carry])
                S.op("dve", lambda e, tl_=tl_, G=G: e.tensor_tensor(out=fq[:], in0=Fcol[:, tl_, :], in1=carry[:, 4 * G, :], op=ALU.subtract), r=[r_F, r_carry], w=[r_fq])
                S.op("dve", lambda e: e.tensor_copy(out=fq16[:, 0:8], in_=fq[:]), r=[r_fq], w=[r_fq16])
                S.op("dve", lambda e: e.tensor_tensor(out=fq[:], in0=fq[:], in1=fq16[:, 0:8], op=ALU.subtract), r=[r_fq, r_fq16], w=[r_fq])
                S.op("dve", lambda e: e.tensor_copy(out=fq16[:, 8:16], in_=fq[:]), r=[r_fq], w=[r_fq16])
                for p in range(4):
                    S.op("pe", lambda e, p=p: e.transpose(pbb[:, p * 128:(p + 1) * 128], qtok[:, p * 128:(p + 1) * 128], identb[:]), r=[r_qtok, r_identb], w=[r_pbb])
                for p in range(4):
                    S.op("pe", lambda e, p=p: e.transpose(pbb[:, 512 + p * 128:512 + (p + 1) * 128], ktok[:, p * 128:(p + 1) * 128], identb[:]), r=[r_ktok, r_identb], w=[r_pbb])
                S.op("act", lambda e, i=i: e.copy(out=qT[:, :, i * 128:(i + 1) * 128], in_=pbb[:, 0:512].rearrange("p (a t) -> p a t", t=128)), r=[r_pbb], w=[r_qT])
                S.op("dve", lambda e, tl_=tl_: e.tensor_copy(out=kT[:, :, tl_ * 128:(tl_ + 1) * 128], in_=pbb[:, 512:1024].rearrange("p (a t) -> p a t", t=128)), r=[r_pbb], w=[r_kT[G]])
                S.op("pe", lambda e: e.transpose(pbb[0:16, 0:128], fq16[:, 0:16], identb[:]), r=[r_fq16, r_identb], w=[r_pbb])
                S.op("dve", lambda e, i=i: e.tensor_copy(out=FqT[:, i * 128:(i + 1) * 128], in_=pbb[0:16, 0:128]), r=[r_pbb], w=[r_FqT])
            nkb = 4 * G + 4
            for h in range(NH):
                S.op("dve", lambda e, h=h, nkb=nkb, G=G: e.tensor_scalar(out=biasG[:, 0:nkb, h], in0=Fcol[:, 0:nkb, h], scalar1=-1.0, scalar2=carry[:, 4 * G, h:h + 1], op0=ALU.mult, op1=ALU.add), r=[r_F, r_carry], w=[r_bias])
            tiles = [(2 * p + e, kb) for p in range(NH // 2) for kb in range(nkb) for e in range(2)]

            def emit_pv(h, kb, PTt, r_PT, nkb=nkb, G=G):
                pO, r_pO = PB[4 + h % 2]
                gk = kb // 4
                S.op("pe", lambda e, pO=pO, PTt=PTt, kb=kb, h=h, nkb=nkb: e.matmul(pO[0:65, :], Va[:, kb, h, :], PTt[:], start=(kb == 0), stop=(kb == nkb - 1)), r=[r_V[gk], r_Vones, r_PT], w=[r_pO])
                if kb == nkb - 1:
                    pB_, r_pB = PB[6]
                    ygt, r_yg = yg[h % 2]
                    S.op("act", lambda e, pO=pO: e.copy(out=oT[:], in_=pO[0:64, :]), r=[r_pO], w=[r_oT])
                    S.op("dve", lambda e, pO=pO: e.tensor_copy(out=dn[64:65, :], in_=pO[64:65, :]), r=[r_pO], w=[r_dn])
                    S.op("pe", lambda e: e.matmul(pB_[0:64, :], C.ones[64:65, 0:64], dn[64:65, :], start=True, stop=True), r=[C.r_ones, r_dn], w=[r_pB])
                    S.op("dve", lambda e, h=h: e.scalar_tensor_tensor(out=wv_[:], in0=eg[:, h, :], scalar=1.0, in1=pB_[0:64, :], op0=ALU.add, op1=ALU.mult), r=[r_eg[h], r_pB], w=[r_wv_])
                    S.op("dve", lambda e: e.reciprocal(out=wv_[:], in_=wv_[:]), r=[r_wv_], w=[r_wv_])
                    S.op("dve", lambda e, ygt=ygt: e.tensor_tensor(out=ygt[:], in0=oT[:], in1=wv_[:], op=ALU.mult), r=[r_oT, r_wv_], w=[r_yg])
                    tails.append(S.dma("sp", yg_d[h * 64:(h + 1) * 64, G * 512:(G + 1) * 512], ygt[:], r=[r_yg], key="yg%d" % (h % 2)))

            pendq = []
            for (h, kb) in tiles:
                p_, e_ = h // 2, h % 2
                pS, r_pS = PB[ps_i % 4]
                PTt, r_PT = PT[ps_i % 4]
                ps_i += 1
                diag = kb >= 4 * G
                gk = kb // 4
                if FOX_DVE:
                    fqb, r_fqb = ((tmp, r_tmp), (sq, r_sq))[h % 2]
                    if kb == 0:
                        pB_, r_pB = PB[6]
                        S.op("pe", lambda e, h=h: e.matmul(pB_[:], sel[:, h, :], FqT[:], start=True, stop=True), r=[r_sel, r_FqT], w=[r_pB])
                        S.op("dve", lambda e, fqb=fqb: e.tensor_copy(out=fqb[:], in_=pB_[:]), r=[r_pB], w=[r_fqb])
                    S.op("pe", lambda e, pS=pS, p_=p_, e_=e_, kb=kb, diag=diag: e.matmul(pS[:], kT[e_ * 64:(e_ + 1) * 64, p_, kb * 128:(kb + 1) * 128], qT[e_ * 64:(e_ + 1) * 64, p_, :], start=True, stop=(not diag)), r=[r_kT[gk], r_qT], w=[r_pS])
                    if diag:
                        S.op("pe", lambda e, pS=pS, kb=kb, G=G: e.matmul(pS[:], identb[:], maskb[:, kb - 4 * G, :], start=False, stop=True), r=[r_identb, r_mask], w=[r_pS])
                    S.op("dve", lambda e, pS=pS, fqb=fqb: e.tensor_tensor(out=pS[:], in0=pS[:], in1=fqb[:], op=ALU.add), r=[r_pS, r_fqb], w=[r_pS])
                else:
                    S.op("pe", lambda e, pS=pS, p_=p_, e_=e_, kb=kb: e.matmul(pS[:], kT[e_ * 64:(e_ + 1) * 64, p_, kb * 128:(kb + 1) * 128], qT[e_ * 64:(e_ + 1) * 64, p_, :], start=True, stop=False), r=[r_kT[gk], r_qT], w=[r_pS])
                    S.op("pe", lambda e, pS=pS, h=h, diag=diag: e.matmul(pS[:], sel[:, h, :], FqT[:], start=False, stop=(not diag)), r=[r_sel, r_FqT], w=[r_pS])
                    if diag:
                        S.op("pe", lambda e, pS=pS, kb=kb, G=G: e.matmul(pS[:], identb[:], maskb[:, kb - 4 * G, :], start=False, stop=True), r=[r_identb, r_mask], w=[r_pS])
                S.op("act", lambda e, pS=pS, PTt=PTt, kb=kb, h=h: e.activation(out=PTt[:], in_=pS[:], func=AF.Exp, bias=biasG[:, kb, h:h + 1], scale=1.0), r=[r_pS, r_bias], w=[r_PT])
                pendq.append((h, kb, PTt, r_PT))
                if h % 2 == 1:
                    while len(pendq) > 2:
                        emit_pv(*pendq.pop(0))
            while pendq:
                emit_pv(*pendq.pop(0))
            if debug and G == 0:
                tails.append(S.dma("sp", dbg_q, qT[:], r=[r_qT], key="dbg1"))
        if debug:
            tails.append(S.dma("sp", dbg_k, kT[:], r=r_kT, key="dbg2"))
            tails.append(S.dma("sp", dbg_F, Fcol[:], r=[r_F], key="dbg3"))
            tails.append(S.dma("sp", dbg_v, Va[:], r=r_V + [r_Vones], key="dbg4"))
        if after is not None:
            after(C, tails)
        S.emit(st, tail_waits=tails)

import math, os
SEQ_VAR = 0

NH = 8
HD = 64
C0 = math.exp(-0.5)
GN_EPS = 64 * 1e-5


class TR:
    def __init__(self, C, name, shape, dt, psum=False):
        self.t = C.ps(name, shape, dt) if psum else C.sb(name, shape, dt)
        self.r = C.R(name, excl=psum)


class _Stop(Exception):
    pass


def build_rwkv(TT=4096, debug=False, stop=None):
    nc = bass.Bass("TRN2", target_bir_lowering=False)
    dt = lambda n, s, d, k="ExternalInput": nc.dram_tensor(n, list(s), d, kind=k).ap()
    xT_d = dt("xT", [D, TT], F32)
    ins = rwkv_decl(dt)
    yT_d = dt("yT", [512, TT], BF16, "ExternalOutput")
    xv = xT_d.rearrange("(kc p) t -> p kc t", p=128)
    rwkv_phase(nc, "", lambda t0, n: xv[:, :, t0:t0 + n], ins, yT_d, TT, debug=debug, stop=stop)
    return nc


RW_BC = ["w0b", "a0b", "kkb", "kab", "rkb", "lngb", "lnbb"]


def rwkv_decl(dt, sfx=""):
    d_ = dict(cT=dt("cT" + sfx, [128, 8], F32), adaw=dt("adaw" + sfx, [D, 2 * D], F32), adab=dt("adab" + sfx, [128, 16], F32), mu=dt("mu" + sfx, [128, 48], F32),
              wr=dt("wr" + sfx, [D, 512], F32), wk=dt("wk" + sfx, [D, 512], F32), wv=dt("wv" + sfx, [D, 512], F32),
              w1=dt("w1" + sfx, [D, 64], F32), a1=dt("a1" + sfx, [D, 64], F32), g1=dt("g1" + sfx, [D, 128], F32),
              w2=dt("w2" + sfx, [64, 512], F32), a2=dt("a2" + sfx, [64, 512], F32), g2=dt("g2" + sfx, [128, 512], F32))
    for n in RW_BC:
        d_[n] = dt(n + sfx, [128, 512], F32)
    return d_


def rwkv_phase(nc, pfx, xsrc_fn, ins, yT_d, TT=4096, debug=False, stop=None, after=None, before_x=None):
    dt = lambda n, s, d, k="ExternalInput": nc.dram_tensor(n, list(s), d, kind=k).ap()
    cT_d, adaw_d, adab_d, mu_d = ins["cT"], ins["adaw"], ins["adab"], ins["mu"]
    wr_d, wk_d, wv_d, w1_d, a1_d, g1_d, w2_d, a2_d, g2_d = [ins[k] for k in ("wr", "wk", "wv", "w1", "a1", "g1", "w2", "a2", "g2")]
    bc_names = RW_BC
    bc_d = {n: ins[n] for n in bc_names}
    GS = 256
    NTG = GS // 128
    NGR = TT // GS
    dbg = {}
    if debug:
        for n in ["r", "kp", "kkn", "a", "sw", "Y", "bonus", "Gm", "yn"]:
            dbg[n] = dt("dbg_" + n, [128, 512], F32, "ExternalOutput")
        dbg["S"] = dt("dbg_S", [128, 4, 64], F32, "ExternalOutput")
        for n, shp in (("Arb", [128, 8, 128]), ("Aak", [128, 8, 128]), ("NT0", [128, 8, 128]), ("Z0", [128, 8, 128]), ("Zf", [128, 8, 128]), ("Y0", [128, 512]), ("RbT", [128, 4, 128]), ("Qm", [128, 4, 128]), ("Hm", [128, 4, 128]), ("gcol", [128, 8]), ("v", [128, 512]), ("ART", [128, 4, 2, 128]), ("KT", [128, 4, 128]), ("BT", [128, 4, 128]), ("Bh", [128, 512]), ("Kh", [128, 512])):
            dbg[n] = dt("dbg_" + n, shp, F32, "ExternalOutput")
    with ExitStack() as st:
        C = Ctx(nc, st, pfx)
        S = C.S
        make_consts(C)
        ident, ones = C.ident, C.ones
        r_ident, r_ones = C.r_ident, C.r_ones
        tails = []
        identb = TR(C, "identb", [128, 128], BF16)
        S.op("dve", lambda e: e.tensor_copy(out=identb.t[:], in_=ident[:]), r=[r_ident], w=[identb.r])
        U = TR(C, "U", [128, 128], F32)
        IU = TR(C, "IUf", [128, 128], F32)
        S.op("pool", lambda e: e.memset(U.t[:], 1.0), w=[U.r])
        S.op("pool", lambda e: e.affine_select(out=U.t[:], in_=U.t[:], pattern=[[1, 128]], compare_op=ALU.is_gt, fill=0.0, base=0, channel_multiplier=-1), r=[U.r], w=[U.r])
        S.op("pool", lambda e: e.memset(IU.t[:], 1.0), w=[IU.r])
        S.op("pool", lambda e: e.affine_select(out=IU.t[:], in_=IU.t[:], pattern=[[1, 128]], compare_op=ALU.is_ge, fill=0.0, base=0, channel_multiplier=-1), r=[IU.r], w=[IU.r])
        SLf = TR(C, "SLf", [128, 128], F32)
        S.op("pool", lambda e: e.memset(SLf.t[:], 1.0), w=[SLf.r])
        S.op("pool", lambda e: e.affine_select(out=SLf.t[:], in_=SLf.t[:], pattern=[[-1, 128]], compare_op=ALU.is_gt, fill=0.0, base=0, channel_multiplier=1), r=[SLf.r], w=[SLf.r])
        mask1 = TR(C, "mask1", [128, 128], F32)
        maskL = TR(C, "maskL", [128, 64], F32)
        idl = TR(C, "idl", [128, 64], F32)
        for ch in range(2):
            ps_ = slice(ch * 64, ch * 64 + 64)
            S.op("dve", lambda e, ps_=ps_: e.tensor_copy(out=mask1.t[ps_, 0:64], in_=U.t[ps_, ps_]), r=[U.r], w=[mask1.r])
            S.op("dve", lambda e, ps_=ps_: e.tensor_copy(out=mask1.t[ps_, 64:128], in_=IU.t[ps_, ps_]), r=[IU.r], w=[mask1.r])
            S.op("dve", lambda e, ps_=ps_: e.tensor_copy(out=maskL.t[ps_, :], in_=SLf.t[ps_, ps_]), r=[SLf.r], w=[maskL.r])
            S.op("dve", lambda e, ps_=ps_: e.tensor_copy(out=idl.t[ps_, :], in_=ident[ps_, ps_]), r=[r_ident], w=[idl.r])
        triC = TR(C, "triC", [128, 128], F32)
        blkC = TR(C, "blkC", [128, 128], F32)
        S.op("dve", lambda e: e.tensor_scalar(out=triC.t[:], in0=IU.t[:], scalar1=-C0, scalar2=None, op0=ALU.mult), r=[IU.r], w=[triC.r])
        S.op("dve", lambda e: e.memset(triC.t[0:64, 64:128], 0.0), w=[triC.r])
        S.op("dve", lambda e: e.memset(blkC.t[:], -C0), w=[blkC.r])
        S.op("dve", lambda e: e.memset(blkC.t[0:64, 64:128], 0.0), w=[blkC.r])
        S.op("dve", lambda e: e.memset(blkC.t[64:128, 0:64], 0.0), w=[blkC.r])
        avg = TR(C, "avg", [128, 1], F32)
        S.op("dve", lambda e: e.memset(avg.t[:], 1.0 / 64.0), w=[avg.r])
        PB = [TR(C, "pb%d" % i, [128, 512], F32, psum=True) for i in range(7)]
        pbb = TR(C, "pbb", [128, 1024], BF16, psum=True)
        mu = TR(C, "mu", [128, 48], F32)
        S.dma("sp", mu.t[:], mu_d, w=[mu.r], key="mu")
        W = {}
        r_w = C.R("wts")
        for nm, d_, shp in (("wr", wr_d, [128, 8, 512]), ("wk", wk_d, [128, 8, 512]), ("wv", wv_d, [128, 8, 512]), ("w1", w1_d, [128, 8, 64]), ("a1", a1_d, [128, 8, 64]), ("g1", g1_d, [128, 8, 128])):
            W[nm] = C.sb(nm, shp, BF16)
            S.dma("pool", W[nm][:], d_.rearrange("(kc p) n -> p kc n", p=128), w=[r_w], key=nm)
        for nm, d_, shp in (("w2", w2_d, [64, 512]), ("a2", a2_d, [64, 512]), ("g2", g2_d, [128, 512])):
            W[nm] = C.sb(nm, shp, BF16)
            S.dma("pool", W[nm][:], d_, w=[r_w], key=nm)
        BC = {}
        r_bc = C.R("bc")
        for n in bc_names:
            BC[n] = C.sb(n, [128, 512], F32)
            S.dma("sp", BC[n][:], bc_d[n], w=[r_bc], key=n)
        xg = TR(C, "xg", [128, 8, GS + 1], F32)
        S.op("dve", lambda e: e.memset(xg.t[:, :, 0:1], 0.0), w=[xg.r])
        Nm = [TR(C, "Nm%d" % i, [128, 8, 128], BF16) for i in range(2)]
        NTm = [TR(C, "NTm%d" % i, [128, 8, 128], BF16) for i in range(2)]
        for t_ in Nm + NTm:
            S.op("pool", lambda e, t_=t_: e.memset(t_.t[:], 0.0), w=[t_.r])
        ST = [TR(C, "ST%d" % i, [128, 4, 64], F32) for i in range(2)]
        S.op("dve", lambda e: e.memset(ST[0].t[:], 0.0), w=[ST[0].r])
        mods, r_mods = compute_mods(C, cT_d, adaw_d, adab_d, 2, None, PB[6].t, PB[6].r)
        cols = TR(C, "cols", [128, 8], F32)
        S.op("dve", lambda e: e.tensor_scalar(out=cols.t[:], in0=mods[:, 8:16], scalar1=1.0, scalar2=None, op0=ALU.add), r=[r_mods], w=[cols.r])
        dx = TR(C, "dx", [128, 8, GS], F32)
        xs = [TR(C, "xs%d" % i, [128, 8, GS], BF16) for i in range(2)]
        l1 = [TR(C, "l1_%d" % i, [128, GS], BF16) for i in range(3)]
        l1f = TR(C, "l1f", [128, GS], F32)
        G_ = {n: TR(C, "G_" + n, [128, NTG, 512], F32) for n in ("r", "k", "v")}
        names = ["sw", "Gm", "Gi", "Gp", "GC", "a", "kkn", "kp", "t1", "t2", "Bt", "Kt", "Bh", "Kh", "bonus", "Y", "yn", "gg", "At", "Rt", "glg", "glg2", "bonus2", "t3", "At2", "Rt2", "Kt2", "Bt2", "Bh2", "Kh2", "GC2", "V0", "V2"]
        X = {n: TR(C, "X_" + n, [128, 512], F32) for n in names if n not in ("Bh", "Kh", "Bh2", "Kh2")}
        for n in ("Bh", "Kh", "Bh2", "Kh2", "Vb0", "Vb2"):
            X[n] = TR(C, "X_" + n, [128, 512], BF16)
        Zfb = TR(C, "Zfb", [128, 8, 128], BF16)
        Arbb = TR(C, "Arbb", [128, 8, 64], BF16)
        sm = TR(C, "sm", [128, 64], F32)
        smT = TR(C, "smT", [128, 64], F32)
        Z = [TR(C, "Z%d" % i, [128, 8, 128], BF16) for i in range(2)]
        Zf32 = TR(C, "Zf32", [128, 8, 128], F32)
        ART = TR(C, "ART", [128, 4, 2, 128], F32)
        KT = TR(C, "KT", [128, 4, 128], F32)
        BT = TR(C, "BT", [128, 4, 128], F32)
        Arb = TR(C, "Arb", [128, 8, 128], BF16)
        Aak = TR(C, "Aak", [128, 8, 128], BF16)
        RbT = TR(C, "RbT", [128, 4, 128], F32)
        Qm = TR(C, "Qm", [128, 4, 128], F32)
        Hm = TR(C, "Hm", [128, 4, 128], F32)
        gcol = TR(C, "gcol", [128, 8], F32)
        Y0 = TR(C, "Y0", [128, 512], F32)
        ybf = TR(C, "ybf", [128, 512], BF16)
        yTo = [TR(C, "yTo%d" % i, [128, 4, 128], BF16) for i in range(2)]
        ysrc = yT_d.rearrange("(a p) t -> p a t", p=128)

        def hv(t):
            return t.rearrange("p (h d) -> p h d", d=64)

        def bc8(ap):
            return ap.unsqueeze(2).to_broadcast([128, 8, 64])

        sti = [0]

        def tile_front_a(i, G):
            tl_ = G * NTG + i
            r_ = G_["r"].t[:, i, :]; k_ = G_["k"].t[:, i, :]; v_ = G_["v"].t[:, i, :]
            rG = [G_[n].r for n in G_]
            sw, Gm, Gi, Gp, a_, kkn, kp, t1, t2 = [X[n] for n in ("sw", "Gm", "Gi", "Gp", "a", "kkn", "kp", "t1", "t2")]
            Yt, yn = X["Y"], X["yn"]
            sfx = "" if tl_ % 2 == 0 else "2"
            Bt, Kt, Bh, Kh, GC, At_, Rt = [X[n + sfx] if (n + sfx) in X else X[n] for n in ("Bt", "Kt", "Bh", "Kh", "GC", "At", "Rt")]
            Vt = X["V0"] if tl_ % 2 == 0 else X["V2"]
            Vb = X["Vb0"] if tl_ % 2 == 0 else X["Vb2"]
            bonus = X["bonus"] if tl_ % 2 == 0 else X["bonus2"]
            glg = X["glg"] if tl_ % 2 == 0 else X["glg2"]
            t3 = X["t3"]
            pzw, pza, pgg = PB[4], PB[5], PB[6]
            for (pz_, lt, w2n, rows) in ((pzw, l1[0], "w2", 64), (pza, l1[1], "a2", 64), (pgg, l1[2], "g2", 128)):
                S.op("pe", lambda e, pz_=pz_, lt=lt, w2n=w2n, rows=rows, i=i: e.matmul(pz_.t[:], lt.t[0:rows, i * 128:(i + 1) * 128], W[w2n][0:rows, :], start=True, stop=True), r=[lt.r, r_w], w=[pz_.r])
            S.op("act", lambda e: e.copy(out=X["gg"].t[:], in_=pgg.t[:]), r=[pgg.r], w=[X["gg"].r])
            S.op("dve", lambda e: e.tensor_tensor(out=t1.t[:], in0=pzw.t[:], in1=BC["w0b"][:], op=ALU.add), r=[pzw.r, r_bc], w=[t1.r])
            S.op("dve", lambda e: e.tensor_tensor(out=t2.t[:], in0=pza.t[:], in1=BC["a0b"][:], op=ALU.add), r=[pza.r, r_bc, t2.r], w=[t2.r])
            S.op("act", lambda e: e.activation(out=sw.t[:], in_=t1.t[:], func=AF.Sigmoid), r=[t1.r], w=[sw.r])
            S.op("act", lambda e: e.activation(out=a_.t[:], in_=t2.t[:], func=AF.Sigmoid), r=[t2.r], w=[a_.r])

        def tile_front_b(i, G):
            tl_ = G * NTG + i
            r_ = G_["r"].t[:, i, :]; k_ = G_["k"].t[:, i, :]; v_ = G_["v"].t[:, i, :]
            rG = [G_[n].r for n in G_]
            sw, Gm, Gi, Gp, a_, kkn, kp, t1, t2 = [X[n] for n in ("sw", "Gm", "Gi", "Gp", "a", "kkn", "kp", "t1", "t2")]
            Yt, yn = X["Y"], X["yn"]
            sfx = "" if tl_ % 2 == 0 else "2"
            Bt, Kt, Bh, Kh, GC, At_, Rt = [X[n + sfx] if (n + sfx) in X else X[n] for n in ("Bt", "Kt", "Bh", "Kh", "GC", "At", "Rt")]
            Vt = X["V0"] if tl_ % 2 == 0 else X["V2"]
            Vb = X["Vb0"] if tl_ % 2 == 0 else X["Vb2"]
            bonus = X["bonus"] if tl_ % 2 == 0 else X["bonus2"]
            glg = X["glg"] if tl_ % 2 == 0 else X["glg2"]
            t3 = X["t3"]
            pL, pLC = PB[2], PB[3]
            S.op("pe", lambda e: e.matmul(pL.t[:], triC.t[:], sw.t[:], start=True, stop=True), r=[triC.r, sw.r], w=[pL.r])
            S.op("pe", lambda e: e.matmul(pLC.t[:], blkC.t[:], sw.t[:], start=True, stop=True), r=[blkC.r, sw.r], w=[pLC.r])
            S.op("act", lambda e: e.activation(out=Gm.t[:], in_=pL.t[:], func=AF.Exp), r=[pL.r], w=[Gm.r])
            S.op("act", lambda e: e.activation(out=Gi.t[:], in_=pL.t[:], func=AF.Exp, scale=-1.0), r=[pL.r], w=[Gi.r])
            S.op("dve", lambda e: e.scalar_tensor_tensor(out=Gp.t[:], in0=sw.t[:], scalar=C0, in1=pL.t[:], op0=ALU.mult, op1=ALU.add), r=[sw.r, pL.r], w=[Gp.r])
            S.op("act", lambda e: e.activation(out=Gp.t[:], in_=Gp.t[:], func=AF.Exp), r=[Gp.r], w=[Gp.r])
            S.op("act", lambda e: e.activation(out=GC.t[:], in_=pLC.t[:], func=AF.Exp), r=[pLC.r], w=[GC.r])
            S.op("dve", lambda e, k_=k_: e.tensor_tensor(out=kkn.t[:], in0=k_, in1=BC["kkb"][:], op=ALU.mult), r=rG + [r_bc], w=[kkn.r])
            S.op("act", lambda e: e.activation(out=t2.t[:], in_=kkn.t[:], func=AF.Square), r=[kkn.r], w=[t2.r])
            S.op("dve", lambda e: e.tensor_reduce(out=sm.t[:, 0:8], in_=hv(t2.t[:]), axis=AX.X, op=ALU.add), r=[t2.r], w=[sm.r])
            S.op("dve", lambda e: e.tensor_scalar(out=sm.t[:, 0:8], in0=sm.t[:, 0:8], scalar1=1e-24, scalar2=None, op0=ALU.max), r=[sm.r], w=[sm.r])
            S.op("act", lambda e: e.activation(out=sm.t[:, 0:8], in_=sm.t[:, 0:8], func=AF.Ln), r=[sm.r], w=[sm.r])
            S.op("act", lambda e: e.activation(out=sm.t[:, 0:8], in_=sm.t[:, 0:8], func=AF.Exp, scale=-0.5), r=[sm.r], w=[sm.r])
            S.op("dve", lambda e: e.tensor_tensor(out=hv(kkn.t[:]), in0=hv(kkn.t[:]), in1=bc8(sm.t[:, 0:8]), op=ALU.mult), r=[kkn.r, sm.r], w=[kkn.r])
            S.op("dve", lambda e: e.scalar_tensor_tensor(out=t2.t[:], in0=a_.t[:], scalar=-1.0, in1=BC["kab"][:], op0=ALU.add, op1=ALU.mult), r=[a_.r, r_bc, t2.r], w=[t2.r])
            S.op("dve", lambda e, k_=k_: e.scalar_tensor_tensor(out=kp.t[:], in0=t2.t[:], scalar=1.0, in1=k_, op0=ALU.add, op1=ALU.mult), r=[t2.r] + rG, w=[kp.r])
            S.op("dve", lambda e: e.scalar_tensor_tensor(out=At_.t[:], in0=kkn.t[:], scalar=-1.0, in1=Gp.t[:], op0=ALU.mult, op1=ALU.mult), r=[kkn.r, Gp.r], w=[At_.r])
            S.op("pool", lambda e: e.tensor_tensor(out=Bt.t[:], in0=kkn.t[:], in1=a_.t[:], op=ALU.mult), r=[kkn.r, a_.r], w=[Bt.r])
            S.op("pool", lambda e: e.tensor_tensor(out=Bt.t[:], in0=Bt.t[:], in1=Gi.t[:], op=ALU.mult), r=[Bt.r, Gi.r], w=[Bt.r])
            S.op("dve", lambda e: e.tensor_tensor(out=Kt.t[:], in0=kp.t[:], in1=Gi.t[:], op=ALU.mult), r=[kp.r, Gi.r], w=[Kt.r])
            S.op("dve", lambda e, r_=r_: e.tensor_tensor(out=Rt.t[:], in0=r_, in1=Gm.t[:], op=ALU.mult), r=rG + [Gm.r], w=[Rt.r])
            S.op("pool", lambda e: e.tensor_tensor(out=Bh.t[:], in0=Bt.t[:], in1=GC.t[:], op=ALU.mult), r=[Bt.r, GC.r], w=[Bh.r])
            S.op("pool", lambda e: e.tensor_tensor(out=Kh.t[:], in0=Kt.t[:], in1=GC.t[:], op=ALU.mult), r=[Kt.r, GC.r], w=[Kh.r])
            S.op("dve", lambda e, r_=r_: e.tensor_tensor(out=t2.t[:], in0=r_, in1=kp.t[:], op=ALU.mult), r=rG + [kp.r, t2.r], w=[t2.r])
            S.op("dve", lambda e: e.tensor_tensor(out=t2.t[:], in0=t2.t[:], in1=BC["rkb"][:], op=ALU.mult), r=[t2.r, r_bc], w=[t2.r])
            S.op("dve", lambda e: e.tensor_reduce(out=sm.t[:, 8:16], in_=hv(t2.t[:]), axis=AX.X, op=ALU.add), r=[t2.r], w=[sm.r])
            S.op("dve", lambda e, v_=v_, bonus=bonus: e.tensor_tensor(out=hv(bonus.t[:]), in0=hv(v_), in1=bc8(sm.t[:, 8:16]), op=ALU.mult), r=rG + [sm.r], w=[bonus.r])
            S.op("pool", lambda e, bonus=bonus: e.tensor_tensor(out=bonus.t[:], in0=bonus.t[:], in1=BC["lnbb"][:], op=ALU.add), r=[bonus.r, r_bc], w=[bonus.r])
            S.op("pool", lambda e, bonus=bonus: e.tensor_tensor(out=bonus.t[:], in0=bonus.t[:], in1=X["gg"].t[:], op=ALU.mult), r=[bonus.r, X["gg"].r], w=[bonus.r])
            S.op("pool", lambda e, glg=glg: e.tensor_tensor(out=glg.t[:], in0=X["gg"].t[:], in1=BC["lngb"][:], op=ALU.mult), r=[X["gg"].r, r_bc], w=[glg.r])
            if stop == 2:
                raise _Stop()
            S.op("pool", lambda e: e.tensor_copy(out=Vb.t[:], in_=v_), r=rG, w=[Vb.r])

        def tile_back(i, G, hook=None):
            tl_ = G * NTG + i
            r_ = G_["r"].t[:, i, :]; k_ = G_["k"].t[:, i, :]; v_ = G_["v"].t[:, i, :]
            rG = [G_[n].r for n in G_]
            sw, Gm, Gi, Gp, a_, kkn, kp, t1, t2 = [X[n] for n in ("sw", "Gm", "Gi", "Gp", "a", "kkn", "kp", "t1", "t2")]
            Yt, yn = X["Y"], X["yn"]
            sfx = "" if tl_ % 2 == 0 else "2"
            Bt, Kt, Bh, Kh, GC, At_, Rt = [X[n + sfx] if (n + sfx) in X else X[n] for n in ("Bt", "Kt", "Bh", "Kh", "GC", "At", "Rt")]
            Vt = X["V0"] if tl_ % 2 == 0 else X["V2"]
            Vb = X["Vb0"] if tl_ % 2 == 0 else X["Vb2"]
            bonus = X["bonus"] if tl_ % 2 == 0 else X["bonus2"]
            glg = X["glg"] if tl_ % 2 == 0 else X["glg2"]
            t3 = X["t3"]
            v_ = Vb.t[:]
            rG = [Vb.r]
            S.op("pool", lambda e: e.tensor_copy(out=Z[0].t[:, :, 0:64], in_=hv(At_.t[:])), r=[At_.r], w=[Z[0].r])
            for p in range(4):
                pT = PB[p % 2]
                for q_, src in enumerate((At_, Rt, Kt, Bt)):
                    S.op("pe", lambda e, pT=pT, p=p, q_=q_, src=src: e.transpose(pT.t[:, q_ * 128:(q_ + 1) * 128], src.t[:, p * 128:(p + 1) * 128], ident[:]), r=[src.r, r_ident], w=[pT.r])
                S.op("dve", lambda e, pT=pT, p=p: e.tensor_copy(out=ART.t[:, p, :, 0:64], in_=pT.t[:, 0:128].rearrange("p (c t) -> p c t", t=64)), r=[pT.r], w=[ART.r])
                S.op("act", lambda e, pT=pT, p=p: e.copy(out=ART.t[:, p, :, 64:128], in_=pT.t[:, 128:256].rearrange("p (c t) -> p c t", t=64)), r=[pT.r], w=[ART.r])
                S.op("act", lambda e, pT=pT, p=p: e.copy(out=KT.t[:, p, :], in_=pT.t[:, 256:384]), r=[pT.r], w=[KT.r])
                S.op("dve", lambda e, pT=pT, p=p: e.tensor_copy(out=BT.t[:, p, :], in_=pT.t[:, 384:512]), r=[pT.r], w=[BT.r])
            if stop == 3:
                raise _Stop()
            pS1a, pS1b, pS2a, pS2b, pS3 = PB[0], PB[1], PB[2], PB[3], PB[4]
            for h in range(8):
                p, e_ = h // 2, h % 2
                fe = slice(e_ * 64, e_ * 64 + 64)
                pa, pb_ = (pS1a, pS2a) if h < 4 else (pS1b, pS2b)
                hc = (h % 4) * 128
                for ch in range(2):
                    tc_ = slice(ch * 64, ch * 64 + 64)
                    S.op("pe", lambda e, pa=pa, fe=fe, p=p, ch=ch, tc_=tc_, hc=hc: e.matmul(pa.t[tc_, hc:hc + 128], BT.t[fe, p, tc_], ART.t[fe, p, ch, :], start=True, stop=True), r=[BT.r, ART.r], w=[pa.r])
                    S.op("pe", lambda e, pb_=pb_, fe=fe, p=p, ch=ch, tc_=tc_, hc=hc: e.matmul(pb_.t[tc_, hc:hc + 128], KT.t[fe, p, tc_], ART.t[fe, p, ch, :], start=True, stop=True), r=[KT.r, ART.r], w=[pb_.r])
                    S.op("pe", lambda e, fe=fe, p=p, ch=ch, tc_=tc_, h=h: e.matmul(pS3.t[tc_, h * 64:(h + 1) * 64], ART.t[fe, p, ch, 0:64], BT.t[fe, p, tc_], start=True, stop=True), r=[BT.r, ART.r], w=[pS3.r])
            m1b = mask1.t[:].unsqueeze(1).to_broadcast([128, 4, 128])
            for half, (pa, pb_) in enumerate(((pS1a, pS2a), (pS1b, pS2b))):
                hs = slice(half * 4, half * 4 + 4)
                S.op("dve", lambda e, pa=pa, hs=hs: e.tensor_tensor(out=Arb.t[:, hs, :], in0=pa.t[:].rearrange("p (h t) -> p h t", t=128), in1=m1b, op=ALU.mult), r=[pa.r, mask1.r], w=[Arb.r])
                S.op("dve", lambda e, pb_=pb_, hs=hs: e.tensor_tensor(out=Aak.t[:, hs, :], in0=pb_.t[:].rearrange("p (h t) -> p h t", t=128), in1=m1b, op=ALU.mult), r=[pb_.r, mask1.r], w=[Aak.r])
            for ch in range(2):
                tc_ = slice(ch * 64, ch * 64 + 64)
                S.op("dve", lambda e, tc_=tc_: e.tensor_tensor(out=NTm[0].t[tc_, :, tc_], in0=hv(pS3.t[tc_, :]), in1=maskL.t[tc_, :].unsqueeze(1).to_broadcast([64, 8, 64]), op=ALU.mult), r=[pS3.r, maskL.r], w=[NTm[0].r])
                S.op("act", lambda e, tc_=tc_: e.copy(out=Nm[0].t[tc_, :, tc_], in_=Arb.t[tc_, :, 0:64]), r=[Arb.r], w=[Nm[0].r])
            if stop == 4:
                raise _Stop()
            if debug and tl_ == 0:
                tails.append(S.dma("sp", dbg["Arb"], Arb.t[:], r=[Arb.r], key="dbg_Arb"))
            if debug and tl_ == 0:
                tails.append(S.dma("sp", dbg["Aak"], Aak.t[:], r=[Aak.r], key="dbg_Aak"))
            if debug and tl_ == 0:
                tails.append(S.dma("sp", dbg["NT0"], NTm[0].t[:], r=[NTm[0].r], key="dbg_NT0"))
            if debug and tl_ == 0:
                tails.append(S.dma("sp", dbg["ART"], ART.t[:], r=[ART.r], key="dbg_ART"))
            if debug and tl_ == 0:
                tails.append(S.dma("sp", dbg["KT"], KT.t[:], r=[KT.r], key="dbg_KT"))
            if debug and tl_ == 0:
                tails.append(S.dma("sp", dbg["BT"], BT.t[:], r=[BT.r], key="dbg_BT"))
            if debug and tl_ == 0:
                tails.append(S.dma("sp", dbg["Bh"], Bh.t[:], r=[Bh.r], key="dbg_Bh"))
            if debug and tl_ == 0:
                tails.append(S.dma("sp", dbg["Kh"], Kh.t[:], r=[Kh.r], key="dbg_Kh"))
            if hook is not None:
                hook()
            pX = PB[5]
            for h in range(8):
                for ch in range(2):
                    tc_ = slice(ch * 64, ch * 64 + 64)
                    S.op("pe", lambda e, h=h, tc_=tc_, v_=v_: e.matmul(pX.t[tc_, h * 64:(h + 1) * 64], Aak.t[tc_, h, 0:64], v_[tc_, h * 64:(h + 1) * 64], start=True, stop=True), r=[Aak.r] + rG, w=[pX.r])
            S.op("act", lambda e: e.copy(out=Z[0].t[:, :, 64:128], in_=hv(pX.t[:])), r=[pX.r], w=[Z[0].r])
            if stop == 5:
                raise _Stop()
            if debug and tl_ == 0:
                tails.append(S.dma("sp", dbg["Z0"], Z[0].t[:], r=[Z[0].r], key="dbg_Z0"))
            zi = 0
            for lev in range(6):
                Nc, NTc = Nm[lev % 2], NTm[lev % 2]
                Nn, NTn = Nm[(lev + 1) % 2], NTm[(lev + 1) % 2]
                pZa, pZb = PB[0], PB[1]
                Zc, Zn = Z[zi], Z[1 - zi]
                if lev < 5:
                    pN, pNT = (PB[2], PB[3]), (PB[4], PB[5])
                    for h in range(8):
                        hb, hc = h // 4, (h % 4) * 128
                        S.op("pe", lambda e, NTc=NTc, Nc=Nc, h=h, hb=hb, hc=hc: e.matmul(pN[hb].t[:, hc:hc + 128], NTc.t[:, h, :], Nc.t[:, h, :], start=True, stop=True), r=[Nc.r, NTc.r], w=[pN[hb].r])
                        if lev < 4:
                            S.op("pe", lambda e, NTc=NTc, Nc=Nc, h=h, hb=hb, hc=hc: e.matmul(pNT[hb].t[:, hc:hc + 128], Nc.t[:, h, :], NTc.t[:, h, :], start=True, stop=True), r=[Nc.r, NTc.r], w=[pNT[hb].r])
                for h in range(8):
                    pz = pZa if h < 4 else pZb
                    hc = (h % 4) * 128
                    S.op("pe", lambda e, pz=pz, Nc=Nc, Zc=Zc, h=h, hc=hc: e.matmul(pz.t[:, hc:hc + 128], Nc.t[:, h, :], Zc.t[:, h, :], start=True, stop=True), r=[Nc.r, Zc.r], w=[pz.r])
                if lev == 5:
                    Zn = Zf32
                S.op("dve", lambda e, Zc=Zc, Zn=Zn: e.tensor_tensor(out=Zn.t[:, 0:4, :], in0=Zc.t[:, 0:4, :], in1=pZa.t[:].rearrange("p (h t) -> p h t", t=128), op=ALU.add), r=[Zc.r, pZa.r], w=[Zn.r])
                S.op("dve", lambda e, Zc=Zc, Zn=Zn: e.tensor_tensor(out=Zn.t[:, 4:8, :], in0=Zc.t[:, 4:8, :], in1=pZb.t[:].rearrange("p (h t) -> p h t", t=128), op=ALU.add), r=[Zc.r, pZb.r], w=[Zn.r])
                zi = 1 - zi
                if lev < 5:
                    for hb in range(2):
                        S.op("act", lambda e, Nn=Nn, hb=hb: e.copy(out=Nn.t[:, hb * 4:hb * 4 + 4, :], in_=pN[hb].t[:].rearrange("p (h t) -> p h t", t=128)), r=[pN[hb].r], w=[Nn.r])
                        if lev < 4:
                            S.op("act" if hb == 0 else "dve", (lambda e, NTn=NTn, hb=hb: e.copy(out=NTn.t[:, hb * 4:hb * 4 + 4, :], in_=pNT[hb].t[:].rearrange("p (h t) -> p h t", t=128))) if hb == 0 else (lambda e, NTn=NTn, hb=hb: e.tensor_copy(out=NTn.t[:, hb * 4:hb * 4 + 4, :], in_=pNT[hb].t[:].rearrange("p (h t) -> p h t", t=128))), r=[pNT[hb].r], w=[NTn.r])
            Zf = Zf32
            S.op("act", lambda e: e.copy(out=Zfb.t[:], in_=Zf32.t[:]), r=[Zf32.r], w=[Zfb.r])
            pY0 = PB[4]
            for h in range(8):
                for ch in range(2):
                    tc_ = slice(ch * 64, ch * 64 + 64)
                    S.op("pe", lambda e, h=h, tc_=tc_: e.matmul(pY0.t[tc_, h * 64:(h + 1) * 64], Arb.t[tc_, h, 64:128], Zfb.t[tc_, h, 64:128], start=True, stop=False), r=[Arb.r, Zfb.r], w=[pY0.r])
                    S.op("pe", lambda e, h=h, tc_=tc_, v_=v_: e.matmul(pY0.t[tc_, h * 64:(h + 1) * 64], Aak.t[tc_, h, 64:128], v_[tc_, h * 64:(h + 1) * 64], start=False, stop=True), r=[Aak.r] + rG, w=[pY0.r])
            S.op("act", lambda e: e.copy(out=Y0.t[:], in_=pY0.t[:]), r=[pY0.r], w=[Y0.r])
            if stop == 7:
                raise _Stop()
            if debug and tl_ == 0:
                tails.append(S.dma("sp", dbg["Zf"], Zf.t[:], r=[Zf.r], key="dbg_Zf"))
            if debug and tl_ == 0:
                tails.append(S.dma("sp", dbg["Y0"], Y0.t[:], r=[Y0.r], key="dbg_Y0"))
            pR, pQ, pH, pg = PB[5], PB[6], PB[0], PB[1]
            for h in range(8):
                p, e_ = h // 2, h % 2
                fe = slice(e_ * 64, e_ * 64 + 64)
                for ch in range(2):
                    tc_ = slice(ch * 64, ch * 64 + 64)
                    cs = slice(p * 128 + ch * 64, p * 128 + ch * 64 + 64)
                    hs_ = slice(h * 64, h * 64 + 64)
                    S.op("pe", lambda e, fe=fe, cs=cs, tc_=tc_, h=h: e.matmul(pR.t[fe, cs], Zfb.t[tc_, h, 0:64], Arb.t[tc_, h, 64:128], start=True, stop=True), r=[Zfb.r, Arb.r], w=[pR.r])
                    S.op("pe", lambda e, fe=fe, cs=cs, tc_=tc_, h=h, hs_=hs_: e.matmul(pQ.t[fe, cs], Zfb.t[tc_, h, 0:64], Bh.t[tc_, hs_], start=True, stop=True), r=[Zfb.r, Bh.r], w=[pQ.r])
                    S.op("pe", lambda e, fe=fe, cs=cs, tc_=tc_, h=h, hs_=hs_: e.matmul(pH.t[fe, cs], Bh.t[tc_, hs_], Zfb.t[tc_, h, 64:128], start=True, stop=False), r=[Zfb.r, Bh.r], w=[pH.r])
                    S.op("pe", lambda e, fe=fe, cs=cs, tc_=tc_, hs_=hs_: e.matmul(pH.t[fe, cs], Kh.t[tc_, hs_], Vb.t[tc_, hs_], start=False, stop=True), r=[Kh.r, Vb.r], w=[pH.r])
                    S.op("pe", lambda e, fe=fe, tc_=tc_, hs_=hs_, p=p, ch=ch: e.matmul(pg.t[fe, p * 2 + ch:p * 2 + ch + 1], GC.t[tc_, hs_], avg.t[tc_, :], start=True, stop=True), r=[GC.r, avg.r], w=[pg.r])
            for ch in range(2):
                S.op("dve", lambda e, ch=ch: e.tensor_tensor(out=RbT.t[:, :, ch * 64:(ch + 1) * 64], in0=pR.t[:].rearrange("p (a t) -> p a t", t=128)[:, :, ch * 64:(ch + 1) * 64], in1=ART.t[:, :, ch, 64:128], op=ALU.add), r=[pR.r, ART.r], w=[RbT.r])
            S.op("act", lambda e: e.copy(out=Qm.t[:], in_=pQ.t[:].rearrange("p (a t) -> p a t", t=128)), r=[pQ.r], w=[Qm.r])
            S.op("act", lambda e: e.copy(out=Hm.t[:], in_=pH.t[:].rearrange("p (a t) -> p a t", t=128)), r=[pH.r], w=[Hm.r])
            S.op("act", lambda e: e.copy(out=gcol.t[:], in_=pg.t[:, 0:8]), r=[pg.r], w=[gcol.r])
            if stop == 8:
                raise _Stop()
            if debug and tl_ == 0:
                tails.append(S.dma("sp", dbg["RbT"], RbT.t[:], r=[RbT.r], key="dbg_RbT"))
            if debug and tl_ == 0:
                tails.append(S.dma("sp", dbg["Qm"], Qm.t[:], r=[Qm.r], key="dbg_Qm"))
            if debug and tl_ == 0:
                tails.append(S.dma("sp", dbg["Hm"], Hm.t[:], r=[Hm.r], key="dbg_Hm"))
            if debug and tl_ == 0:
                tails.append(S.dma("sp", dbg["gcol"], gcol.t[:], r=[gcol.r], key="dbg_gcol"))
            for ch in range(2):
                tc_ = slice(ch * 64, ch * 64 + 64)
                Sc, Sn = ST[sti[0]], ST[1 - sti[0]]
                pYe, pS_ = (PB[2], PB[4]), PB[3]
                for h in range(8):
                    p, e_ = h // 2, h % 2
                    fe = slice(e_ * 64, e_ * 64 + 64)
                    cs = slice(ch * 64, ch * 64 + 64)
                    pY = pYe[e_]
                    S.op("pe", lambda e, fe=fe, p=p, cs=cs, tc_=tc_, h=h, Sc=Sc, pY=pY: e.matmul(pY.t[tc_, h * 64:(h + 1) * 64], RbT.t[fe, p, cs], Sc.t[fe, p, :], start=True, stop=True), r=[RbT.r, Sc.r], w=[pY.r])
                    S.op("pe", lambda e, fe=fe, p=p, cs=cs, Sc=Sc: e.matmul(pS_.t[fe, p * 64:(p + 1) * 64], Qm.t[fe, p, cs], Sc.t[fe, p, :], start=True, stop=True), r=[Qm.r, Sc.r], w=[pS_.r])
                v4 = lambda ap: ap.rearrange("p (a e d) -> p a e d", e=2, d=64)
                for e_ in range(2):
                    S.op("dve", lambda e, tc_=tc_, e_=e_: e.tensor_tensor(out=v4(Yt.t[tc_, :])[:, :, e_, :], in0=v4(Y0.t[tc_, :])[:, :, e_, :], in1=v4(pYe[e_].t[tc_, :])[:, :, e_, :], op=ALU.add), r=[Y0.r, pYe[e_].r], w=[Yt.r])
                for p in range(4):
                    S.op("dve", lambda e, p=p, ch=ch, Sc=Sc, Sn=Sn: e.scalar_tensor_tensor(out=Sn.t[:, p, :], in0=Sc.t[:, p, :], scalar=gcol.t[:, p * 2 + ch:p * 2 + ch + 1], in1=pS_.t[:, p * 64:(p + 1) * 64], op0=ALU.mult, op1=ALU.add), r=[Sc.r, gcol.r, pS_.r], w=[Sn.r])
                S.op("dve", lambda e, ch=ch, Sn=Sn: e.tensor_tensor(out=Sn.t[:], in0=Sn.t[:], in1=Hm.t[:, :, ch * 64:(ch + 1) * 64], op=ALU.add), r=[Sn.r, Hm.r], w=[Sn.r])
                sti[0] = 1 - sti[0]
            if stop == 9:
                raise _Stop()
            if debug and tl_ == 0:
                S.op("act", lambda e, r_=r_: e.copy(out=t1.t[:], in_=r_), r=rG + [t1.r], w=[t1.r])
                S.op("act", lambda e, v_=v_: e.copy(out=t2.t[:], in_=v_), r=rG + [t2.r], w=[t2.r])
                tails.append(S.dma("sp", dbg["v"], t2.t[:], r=[t2.r], key="dbg_v"))
                for n, src in (("r", t1), ("kp", kp), ("kkn", kkn), ("a", a_), ("sw", sw), ("Y", Yt), ("bonus", bonus), ("Gm", Gm), ("yn", yn)):
                    tails.append(S.dma("sp", dbg[n], src.t[:], r=[src.r], key="dbg_" + n))
                tails.append(S.dma("sp", dbg["S"], ST[sti[0]].t[:], r=[ST[sti[0]].r], key="dbg_S"))


        def tile_tail(i, G):
            tl_ = G * NTG + i
            r_ = G_["r"].t[:, i, :]; k_ = G_["k"].t[:, i, :]; v_ = G_["v"].t[:, i, :]
            rG = [G_[n].r for n in G_]
            sw, Gm, Gi, Gp, a_, kkn, kp, t1, t2 = [X[n] for n in ("sw", "Gm", "Gi", "Gp", "a", "kkn", "kp", "t1", "t2")]
            Yt, yn = X["Y"], X["yn"]
            sfx = "" if tl_ % 2 == 0 else "2"
            Bt, Kt, Bh, Kh, GC, At_, Rt = [X[n + sfx] if (n + sfx) in X else X[n] for n in ("Bt", "Kt", "Bh", "Kh", "GC", "At", "Rt")]
            Vt = X["V0"] if tl_ % 2 == 0 else X["V2"]
            Vb = X["Vb0"] if tl_ % 2 == 0 else X["Vb2"]
            bonus = X["bonus"] if tl_ % 2 == 0 else X["bonus2"]
            glg = X["glg"] if tl_ % 2 == 0 else X["glg2"]
            t3 = X["t3"]
            v_ = Vb.t[:]
            rG = [Vb.r]
            S.op("dve", lambda e: e.tensor_reduce(out=smT.t[:, 16:24], in_=hv(Yt.t[:]), axis=AX.X, op=ALU.add), r=[Yt.r], w=[smT.r])
            S.op("act", lambda e: e.activation(out=t3.t[:], in_=Yt.t[:], func=AF.Square), r=[Yt.r, t3.r], w=[t3.r])
            S.op("dve", lambda e: e.tensor_reduce(out=smT.t[:, 24:32], in_=hv(t3.t[:]), axis=AX.X, op=ALU.add), r=[t3.r], w=[smT.r])
            S.op("dve", lambda e: e.tensor_scalar(out=smT.t[:, 16:24], in0=smT.t[:, 16:24], scalar1=1.0 / 64, scalar2=None, op0=ALU.mult), r=[smT.r], w=[smT.r])
            S.op("dve", lambda e: e.tensor_tensor(out=smT.t[:, 32:40], in0=smT.t[:, 16:24], in1=smT.t[:, 16:24], op=ALU.mult), r=[smT.r], w=[smT.r])
            S.op("dve", lambda e: e.scalar_tensor_tensor(out=smT.t[:, 24:32], in0=smT.t[:, 24:32], scalar=1.0 / 64, in1=smT.t[:, 32:40], op0=ALU.mult, op1=ALU.subtract), r=[smT.r], w=[smT.r])
            S.op("dve", lambda e: e.tensor_scalar(out=smT.t[:, 24:32], in0=smT.t[:, 24:32], scalar1=GN_EPS, scalar2=None, op0=ALU.add), r=[smT.r], w=[smT.r])
            S.op("act", lambda e: e.activation(out=smT.t[:, 24:32], in_=smT.t[:, 24:32], func=AF.Ln), r=[smT.r], w=[smT.r])
            S.op("act", lambda e: e.activation(out=smT.t[:, 24:32], in_=smT.t[:, 24:32], func=AF.Exp, scale=-0.5), r=[smT.r], w=[smT.r])
            S.op("dve", lambda e: e.tensor_tensor(out=hv(yn.t[:]), in0=hv(Yt.t[:]), in1=bc8(smT.t[:, 16:24]), op=ALU.subtract), r=[Yt.r, smT.r], w=[yn.r])
            S.op("dve", lambda e: e.tensor_tensor(out=hv(yn.t[:]), in0=hv(yn.t[:]), in1=bc8(smT.t[:, 24:32]), op=ALU.mult), r=[yn.r, smT.r], w=[yn.r])
            S.op("dve", lambda e, glg=glg: e.tensor_tensor(out=yn.t[:], in0=yn.t[:], in1=glg.t[:], op=ALU.mult), r=[yn.r, glg.r], w=[yn.r])
            S.op("dve", lambda e, bonus=bonus: e.tensor_tensor(out=ybf.t[:], in0=yn.t[:], in1=bonus.t[:], op=ALU.add), r=[yn.r, bonus.r], w=[ybf.r])
            for p in range(4):
                S.op("pe", lambda e, p=p: e.transpose(pbb.t[:, p * 128:(p + 1) * 128], ybf.t[:, p * 128:(p + 1) * 128], identb.t[:]), r=[ybf.r, identb.r], w=[pbb.r])
            yo = yTo[tl_ % 2]
            S.op("act", lambda e, yo=yo: e.copy(out=yo.t[:], in_=pbb.t[:, 0:512].rearrange("p (a t) -> p a t", t=128)), r=[pbb.r], w=[yo.r])
            tails.append(S.dma("sp", ysrc[:, :, tl_ * 128:(tl_ + 1) * 128], yo.t[:], r=[yo.r], key="yTo%d" % (tl_ % 2)))

        pend = [None, None]
        try:
          for G in range(NGR):
              if G > 0:
                  S.op("dve", lambda e: e.tensor_copy(out=xg.t[:, :, 0:1], in_=xg.t[:, :, GS:GS + 1]), r=[xg.r], w=[xg.r])
              if before_x is not None and G == 0:
                  before_x(S)
              S.dma("sp", xg.t[:, :, 1:GS + 1], xsrc_fn(G * GS, GS), r=[xg.r], w=[xg.r], key="xg")
              for kc in range(8):
                  S.op("dve", lambda e, kc=kc: e.tensor_scalar(out=xg.t[:, kc, 1:GS + 1], in0=xg.t[:, kc, 1:GS + 1], scalar1=cols.t[:, kc:kc + 1], scalar2=mods[:, kc:kc + 1], op0=ALU.mult, op1=ALU.add), r=[xg.r, cols.r, r_mods], w=[xg.r])
              S.op("dve", lambda e: e.tensor_tensor(out=dx.t[:], in0=xg.t[:, :, 0:GS], in1=xg.t[:, :, 1:GS + 1], op=ALU.subtract), r=[xg.r], w=[dx.r])
              for n, nm in enumerate(("r", "k", "v", "w", "a", "g")):
                  xb = xs[n % 2]
                  for kc in range(8):
                      eng = "dve"
                      S.op(eng, lambda e, kc=kc, n=n, xb=xb: e.scalar_tensor_tensor(out=xb.t[:, kc, :], in0=dx.t[:, kc, :], scalar=mu.t[:, n * 8 + kc:n * 8 + kc + 1], in1=xg.t[:, kc, 1:GS + 1], op0=ALU.mult, op1=ALU.add), r=[dx.r, mu.r, xg.r], w=[xb.r])
                  if n < 3:
                      wt = W[("wr", "wk", "wv")[n]]
                      for i in range(NTG):
                          pb = PB[(n * NTG + i) % 2]
                          for kc in range(8):
                              S.op("pe", lambda e, pb=pb, xb=xb, wt=wt, i=i, kc=kc: e.matmul(pb.t[:], xb.t[:, kc, i * 128:(i + 1) * 128], wt[:, kc, :], start=(kc == 0), stop=(kc == 7)), r=[xb.r, r_w], w=[pb.r])
                          S.op("act", lambda e, pb=pb, nm=nm, i=i: e.copy(out=G_[nm].t[:, i, :], in_=pb.t[:]), r=[pb.r], w=[G_[nm].r])
                  else:
                      w1n, w2n, rows = (("w1", "w2", 64), ("a1", "a2", 64), ("g1", "g2", 128))[n - 3]
                      pb = PB[2]
                      for kc in range(8):
                          S.op("pe", lambda e, pb=pb, xb=xb, w1n=w1n, rows=rows, kc=kc: e.matmul(pb.t[0:rows, 0:GS], W[w1n][:, kc, :], xb.t[:, kc, :], start=(kc == 0), stop=(kc == 7)), r=[xb.r, r_w], w=[pb.r])
                      lt = l1[n - 3]
                      if nm == "w":
                          S.op("act", lambda e, pb=pb: e.activation(out=l1f.t[0:64, :], in_=pb.t[0:64, 0:GS], func=AF.Exp, scale=-2.0), r=[pb.r, l1f.r], w=[l1f.r])
                          S.op("dve", lambda e: e.tensor_scalar(out=l1f.t[0:64, :], in0=l1f.t[0:64, :], scalar1=1.0, scalar2=None, op0=ALU.add), r=[l1f.r], w=[l1f.r])
                          S.op("dve", lambda e: e.reciprocal(out=l1f.t[0:64, :], in_=l1f.t[0:64, :]), r=[l1f.r], w=[l1f.r])
                          S.op("dve", lambda e, lt=lt: e.tensor_scalar(out=lt.t[0:64, :], in0=l1f.t[0:64, :], scalar1=2.0, scalar2=-1.0, op0=ALU.mult, op1=ALU.add), r=[l1f.r], w=[lt.r])
                      elif nm == "a":
                          S.op("act", lambda e, pb=pb, lt=lt: e.copy(out=lt.t[0:64, :], in_=pb.t[0:64, 0:GS]), r=[pb.r], w=[lt.r])
                      else:
                          S.op("act", lambda e, pb=pb: e.activation(out=l1f.t[:], in_=pb.t[:, 0:GS], func=AF.Exp, scale=-1.0), r=[pb.r, l1f.r], w=[l1f.r])
                          S.op("dve", lambda e: e.tensor_scalar(out=l1f.t[:], in0=l1f.t[:], scalar1=1.0, scalar2=None, op0=ALU.add), r=[l1f.r], w=[l1f.r])
                          S.op("dve", lambda e: e.reciprocal(out=l1f.t[:], in_=l1f.t[:]), r=[l1f.r], w=[l1f.r])
                          S.op("act", lambda e, lt=lt: e.copy(out=lt.t[:], in_=l1f.t[:]), r=[l1f.r], w=[lt.r])
              if stop == 1:
                  raise _Stop()
              for i in range(NTG):
                  tile_front_a(i, G)
                  if pend[0] is None:
                      tile_front_b(i, G)
                  else:
                      def hook(i=i, G=G, prev=pend[1]):
                          if prev is not None and not debug:
                              tile_tail(*prev)
                          tile_front_b(i, G)
                      tile_back(*pend[0], hook=hook)
                      if debug:
                          tile_tail(*pend[0])
                      pend[1] = pend[0]
                  pend[0] = (i, G)
          if pend[0] is not None:
              def hook_last(prev=pend[1]):
                  if prev is not None and not debug:
                      tile_tail(*prev)
              tile_back(*pend[0], hook=hook_last)
              tile_tail(*pend[0])
        except _Stop:
            pass
        if after is not None:
            after(C, tails)
        S.emit(st, tail_waits=tails)


PAIRS = [[0, 1], [2, 3], [4, 5], [6, 7]]
TL_KEYS = ("cT", "adaw", "adab", "lng", "lnb", "wo", "win", "wout")


def tl_decl(dt, sfx):
    return dict(cT=dt("cT" + sfx, [128, 8], F32), adaw=dt("adaw" + sfx, [D, 4 * D], F32), adab=dt("adab" + sfx, [128, 32], F32),
                lng=dt("lng" + sfx, [128, 16], F32), lnb=dt("lnb" + sfx, [128, 16], F32),
                wo=dt("wo" + sfx, [D, D], F32), win=dt("win" + sfx, [D, 2 * DFF], F32), wout=dt("wout" + sfx, [DFF, D], F32))


def build_fused(stop=99):
    nc = bass.Bass("TRN2", target_bir_lowering=False)
    dt = lambda n, s, d, k="ExternalInput": nc.dram_tensor(n, list(s), d, kind=k).ap()
    it = lambda n, s, d: nc.dram_tensor(n, list(s), d).ap()
    fox_in = fox_decl(dt, 4096, "_f")
    oT_d = dt("oT", [D, 2048], F32, "ExternalOutput")
    cin1, cout1 = it("cin1", [4, 128, 4096], BF16), it("cout1", [4, 2, 128, 4096], BF16)
    cin2, cout2 = it("cin2", [8, 128, 2048], F32), it("cout2", [8, 2, 128, 2048], F32)
    cin3, cout3 = it("cin3", [4, 128, 4096], BF16), it("cout3", [4, 2, 128, 4096], BF16)

    GS_ = {k: nc.alloc_semaphore(name="xchg_" + k) for k in ("g1", "g2", "g3")}

    def gather(cin, cout, key, nblk):
        def after(C, tails):
            prev = list(tails)
            for k in range(nblk):
                C.S.cc("AllGather", PAIRS, cin[k], cout[k].rearrange("r p t -> (r p) t"), key=key, extra=prev, ext_sem=GS_[key])
        return after

    def waiter(key, nblk):
        return lambda S: S.ext_wait("sp", GS_[key], nblk)

    cin1_v = cin1.rearrange("k p t -> (k p) t")
    cin3_v = cin3.rearrange("k p t -> (k p) t")
    cin2_v = cin2.rearrange("k p t -> (k p) t")
    y1_v = cout1
    y3_v = cout3
    fox_phase(nc, "f_", fox_in, cin1_v, 4096, after=gather(cin1, cout1, "g1", 4))
    if stop <= 1:
        return nc
    phase_end(nc)
    sel_d = dt("sel", [128, 2], F32)
    xTh_d = dt("xTh", [D, 2048], F32)
    t = tl_decl(dt, "_t0")
    tl_phase(nc, "t0_", y1_v, xTh_d, t["cT"], t["adaw"], t["adab"], t["lng"], t["lnb"], t["wo"], t["win"], t["wout"], cin2_v, 2048, 1024,
             sel_d=sel_d, after=gather(cin2, cout2, "g2", 8), before_y=waiter("g1", 4))
    if stop <= 2:
        return nc
    phase_end(nc)
    rw_in = rwkv_decl(dt, "_r")
    c2v = cout2.rearrange("kc r p t -> r p kc t")
    rwkv_phase(nc, "r_", lambda t0, n: c2v[t0 // 2048, :, :, (t0 % 2048):(t0 % 2048) + n], rw_in, cin3_v, 4096, after=gather(cin3, cout3, "g3", 4), before_x=waiter("g2", 8))
    if stop <= 3:
        return nc
    phase_end(nc)
    t = tl_decl(dt, "_t1")
    tl_phase(nc, "t1_", y3_v, cin2_v, t["cT"], t["adaw"], t["adab"], t["lng"], t["lnb"], t["wo"], t["win"], t["wout"], oT_d, 2048, 1024, sel_d=sel_d, before_y=waiter("g3", 4))
    return nc


NCORES = 8
_PROGS = {}


def _prog(name, fn):
    if name not in _PROGS:
        _PROGS[name] = fn()
    return _PROGS[name]


def _col_layout(v):
    return np.ascontiguousarray(np.asarray(v).reshape(-1, 128).T)


def _bc(v, n=512):
    v = np.asarray(v, dtype=np.float32).reshape(-1)
    return np.ascontiguousarray(np.broadcast_to(v[None, :], (128, v.shape[0])))


def _run(nc, in_maps):
    res = run_bass_kernel_spmd(nc, in_maps, core_ids=list(range(NCORES)))
    return res.results


def _tl_maps(yT_list, xT_list, c, ada_w_i, ada_b_i, ln_g_i, ln_b_i, wo, win, wout):
    maps = []
    adaw = np.ascontiguousarray(ada_w_i[:, 2 * D:])
    adab = _col_layout(ada_b_i[2 * D:])
    lng = _col_layout(ln_g_i.reshape(-1))
    lnb = _col_layout(ln_b_i.reshape(-1))
    for core in range(NCORES):
        b, th = core // 2, core % 2
        ts = slice(th * 2048, (th + 1) * 2048)
        maps.append({
            "yT": np.ascontiguousarray(yT_list[b][:, ts]),
            "xT": np.ascontiguousarray(xT_list[b][:, ts]),
            "cT": _col_layout(c[b]),
            "adaw": adaw, "adab": adab, "lng": lng, "lnb": lnb,
            "wo": wo, "win": win, "wout": wout,
        })
    return maps


def kernel_unfused(x, c, ada_w, ada_b, ln_g, ln_b, ffn_w_in, ffn_w_out,
           fox_w_in, fox_b_f, fox_q_g, fox_k_g, fox_w_o,
           rwkv_mu, rwkv_w_rkv, rwkv_w0, rwkv_w1, rwkv_w2, rwkv_a0, rwkv_a1, rwkv_a2,
           rwkv_g1, rwkv_g2, rwkv_k_k, rwkv_k_a, rwkv_r_k, rwkv_lnx_g, rwkv_lnx_b, rwkv_w_o):
    f = lambda a: np.ascontiguousarray(np.asarray(a, dtype=np.float32))
    x, c, ada_w, ada_b, ln_g, ln_b = f(x), f(c), f(ada_w), f(ada_b), f(ln_g), f(ln_b)
    ffn_w_in, ffn_w_out = f(ffn_w_in), f(ffn_w_out)
    B = x.shape[0]
    w_in = f(fox_w_in)[0]
    b_f, q_g, k_g = f(fox_b_f)[0], f(fox_q_g)[0], f(fox_k_g)[0]
    maps = []
    for core in range(NCORES):
        b, hh = core // 2, core % 2
        sl = slice(hh * 512, (hh + 1) * 512)
        maps.append({
            "x": x[b],
            "cT": _col_layout(c[b]),
            "adaw": np.ascontiguousarray(ada_w[0][:, 0:2 * D]),
            "adab": _col_layout(ada_b[0][0:2 * D]),
            "wq": np.ascontiguousarray(w_in[:, 0:D][:, sl]),
            "wk": np.ascontiguousarray(w_in[:, D:2 * D][:, sl]),
            "wv": np.ascontiguousarray(w_in[:, 2 * D:3 * D][:, sl]),
            "wf": np.ascontiguousarray(w_in[:, 3 * D + hh * 8:3 * D + hh * 8 + 8]),
            "wg": np.ascontiguousarray(w_in[:, 3 * D + 16:][:, sl]),
            "bfb": _bc(b_f[hh * 8:(hh + 1) * 8]),
            "qgb": _bc(np.tile(q_g, 8)),
            "kgb": _bc(np.tile(k_g, 8)),
        })
    r1 = _run(_prog("fox", build_fox), maps)
    yT = [np.concatenate([r1[2 * b]["ygT"], r1[2 * b + 1]["ygT"]], axis=0) for b in range(B)]
    xT = [np.ascontiguousarray(x[b].T) for b in range(B)]
    tlp = _prog("tl", build_tl)
    r2 = _run(tlp, _tl_maps(yT, xT, c, ada_w[0], ada_b[0], ln_g[0], ln_b[0], f(fox_w_o)[0], ffn_w_in[0], ffn_w_out[0]))
    x1T = [np.concatenate([r2[2 * b]["oT"], r2[2 * b + 1]["oT"]], axis=1) for b in range(B)]
    mu = f(rwkv_mu)[0]
    w_rkv = f(rwkv_w_rkv)[0]
    P = dict(w0=f(rwkv_w0)[0], w1=f(rwkv_w1)[0], w2=f(rwkv_w2)[0], a0=f(rwkv_a0)[0], a1=f(rwkv_a1)[0], a2=f(rwkv_a2)[0],
             g1=f(rwkv_g1)[0], g2=f(rwkv_g2)[0], k_k=f(rwkv_k_k)[0], k_a=f(rwkv_k_a)[0], r_k=f(rwkv_r_k)[0].reshape(-1),
             lnx_g=f(rwkv_lnx_g)[0], lnx_b=f(rwkv_lnx_b)[0])
    mu_l = np.concatenate([_col_layout(mu[n]) for n in range(6)], axis=1)
    maps = []
    for core in range(NCORES):
        b, hh = core // 2, core % 2
        sl = slice(hh * 512, (hh + 1) * 512)
        maps.append({
            "xT": x1T[b],
            "cT": _col_layout(c[b]),
            "adaw": np.ascontiguousarray(ada_w[1][:, 0:2 * D]),
            "adab": _col_layout(ada_b[1][0:2 * D]),
            "mu": mu_l,
            "wr": np.ascontiguousarray(w_rkv[0][:, sl]),
            "wk": np.ascontiguousarray(w_rkv[1][:, sl]),
            "wv": np.ascontiguousarray(w_rkv[2][:, sl]),
            "w1": P["w1"], "a1": P["a1"], "g1": P["g1"],
            "w2": np.ascontiguousarray(P["w2"][:, sl]),
            "a2": np.ascontiguousarray(P["a2"][:, sl]),
            "g2": np.ascontiguousarray(P["g2"][:, sl]),
            "w0b": _bc(P["w0"][sl]), "a0b": _bc(P["a0"][sl]), "kkb": _bc(P["k_k"][sl]), "kab": _bc(P["k_a"][sl]),
            "rkb": _bc(P["r_k"][sl]), "lngb": _bc(P["lnx_g"][sl]), "lnbb": _bc(P["lnx_b"][sl]),
        })
    r3 = _run(_prog("rwkv", build_rwkv), maps)
    y2T = [np.concatenate([r3[2 * b]["yT"], r3[2 * b + 1]["yT"]], axis=0) for b in range(B)]
    r4 = _run(tlp, _tl_maps(y2T, x1T, c, ada_w[1], ada_b[1], ln_g[1], ln_b[1], f(rwkv_w_o)[0], ffn_w_in[1], ffn_w_out[1]))
    out = np.empty(x.shape, np.float32)
    for core in range(NCORES):
        b, th = core // 2, core % 2
        out[b, th * 2048:(th + 1) * 2048, :] = r4[core]["oT"].T
    return out


def kernel(x, c, ada_w, ada_b, ln_g, ln_b, ffn_w_in, ffn_w_out,
           fox_w_in, fox_b_f, fox_q_g, fox_k_g, fox_w_o,
           rwkv_mu, rwkv_w_rkv, rwkv_w0, rwkv_w1, rwkv_w2, rwkv_a0, rwkv_a1, rwkv_a2,
           rwkv_g1, rwkv_g2, rwkv_k_k, rwkv_k_a, rwkv_r_k, rwkv_lnx_g, rwkv_lnx_b, rwkv_w_o):
    f = lambda a: np.ascontiguousarray(np.asarray(a, dtype=np.float32))
    x, c, ada_w, ada_b, ln_g, ln_b = f(x), f(c), f(ada_w), f(ada_b), f(ln_g), f(ln_b)
    ffn_w_in, ffn_w_out = f(ffn_w_in), f(ffn_w_out)
    w_in = f(fox_w_in)[0]
    b_f, q_g, k_g = f(fox_b_f)[0], f(fox_q_g)[0], f(fox_k_g)[0]
    mu = f(rwkv_mu)[0]
    w_rkv = f(rwkv_w_rkv)[0]
    P = dict(w0=f(rwkv_w0)[0], w1=f(rwkv_w1)[0], w2=f(rwkv_w2)[0], a0=f(rwkv_a0)[0], a1=f(rwkv_a1)[0], a2=f(rwkv_a2)[0],
             g1=f(rwkv_g1)[0], g2=f(rwkv_g2)[0], k_k=f(rwkv_k_k)[0], k_a=f(rwkv_k_a)[0], r_k=f(rwkv_r_k)[0].reshape(-1),
             lnx_g=f(rwkv_lnx_g)[0], lnx_b=f(rwkv_lnx_b)[0])
    mu_l = np.concatenate([_col_layout(mu[n]) for n in range(6)], axis=1)
    wos = [f(fox_w_o)[0], f(rwkv_w_o)[0]]
    tl_common = []
    for L in range(2):
        tl_common.append({
            "adaw_t%d" % L: np.ascontiguousarray(ada_w[L][:, 2 * D:]), "adab_t%d" % L: _col_layout(ada_b[L][2 * D:]),
            "lng_t%d" % L: _col_layout(ln_g[L].reshape(-1)), "lnb_t%d" % L: _col_layout(ln_b[L].reshape(-1)),
            "wo_t%d" % L: wos[L], "win_t%d" % L: ffn_w_in[L], "wout_t%d" % L: ffn_w_out[L]})
    adaw_f = np.ascontiguousarray(ada_w[0][:, 0:2 * D])
    adaw_r = np.ascontiguousarray(ada_w[1][:, 0:2 * D])
    maps = []
    for core in range(NCORES):
        b, j = core // 2, core % 2
        sl = slice(j * 512, (j + 1) * 512)
        cT = _col_layout(c[b])
        m = {
            "x_f": x[b], "cT_f": cT, "adaw_f": adaw_f, "adab_f": _col_layout(ada_b[0][0:2 * D]),
            "wq_f": np.ascontiguousarray(w_in[:, 0:D][:, sl]), "wk_f": np.ascontiguousarray(w_in[:, D:2 * D][:, sl]),
            "wv_f": np.ascontiguousarray(w_in[:, 2 * D:3 * D][:, sl]), "wf_f": np.ascontiguousarray(w_in[:, 3 * D + j * 8:3 * D + j * 8 + 8]),
            "wg_f": np.ascontiguousarray(w_in[:, 3 * D + 16:][:, sl]),
            "bfb_f": _bc(b_f[j * 8:(j + 1) * 8]), "qgb_f": _bc(np.tile(q_g, 8)), "kgb_f": _bc(np.tile(k_g, 8)),
            "cT_r": cT, "adaw_r": adaw_r, "adab_r": _col_layout(ada_b[1][0:2 * D]), "mu_r": mu_l,
            "wr_r": np.ascontiguousarray(w_rkv[0][:, sl]), "wk_r": np.ascontiguousarray(w_rkv[1][:, sl]), "wv_r": np.ascontiguousarray(w_rkv[2][:, sl]),
            "w1_r": P["w1"], "a1_r": P["a1"], "g1_r": P["g1"],
            "w2_r": np.ascontiguousarray(P["w2"][:, sl]), "a2_r": np.ascontiguousarray(P["a2"][:, sl]), "g2_r": np.ascontiguousarray(P["g2"][:, sl]),
            "w0b_r": _bc(P["w0"][sl]), "a0b_r": _bc(P["a0"][sl]), "kkb_r": _bc(P["k_k"][sl]), "kab_r": _bc(P["k_a"][sl]),
            "rkb_r": _bc(P["r_k"][sl]), "lngb_r": _bc(P["lnx_g"][sl]), "lnbb_r": _bc(P["lnx_b"][sl]),
            "cT_t0": cT, "cT_t1": cT,
            "sel": np.ascontiguousarray(np.broadcast_to(np.array([1.0 - j, float(j)], np.float32)[None, :], (128, 2))),
            "xTh": np.ascontiguousarray(x[b, j * 2048:(j + 1) * 2048, :].T),
        }
        m.update(tl_common[0])
        m.update(tl_common[1])
        maps.append(m)
    res = _run(_prog("fused", build_fused), maps)
    out = np.empty(x.shape, np.float32)
    for core in range(NCORES):
        b, j = core // 2, core % 2
        out[b, j * 2048:(j + 1) * 2048, :] = res[core]["oT"].T
    return out
```

```python
import numpy as np
from contextlib import ExitStack
import concourse.bass as bass
import concourse.mybir as mybir
from concourse.bass_utils import run_bass_kernel_spmd

F32 = mybir.dt.float32
BF16 = mybir.dt.bfloat16
AF = mybir.ActivationFunctionType
ALU = mybir.AluOpType
AX = mybir.AxisListType

ENGS = ("pe", "act", "dve", "pool", "sp")
LAST_SEMS = []
SEM_CTR = [0]


def phase_end(nc):
    nc.all_engine_barrier()
    nc.clear_and_free_semaphores(list(LAST_SEMS))
    LAST_SEMS[:] = []
    nc.all_engine_barrier()


class Res:
    __slots__ = ("name", "last_w", "readers", "excl")

    def __init__(self, name, excl=False):
        self.name = name
        self.excl = excl
        self.last_w = None
        self.readers = []


class Op:
    __slots__ = ("eng", "fn", "deps", "sig", "count", "is_dma", "key", "semval", "vc", "waits", "gid")


class Deferred:
    op = None


class Sched:
    def __init__(self, nc):
        self.nc = nc
        self.ops = []
        self.dma_counts = {}

    def res(self, name, excl=False):
        return Res(name, excl)

    def _add(self, op, r, w, extra=()):
        deps = [(d, "raw") for d in extra]
        for x in r:
            if x.last_w is not None:
                deps.append((x.last_w, "raw"))
            if x.excl:
                for rd in x.readers:
                    deps.append((rd, "war"))
        for x in w:
            if x.last_w is not None:
                deps.append((x.last_w, "waw"))
            for rd in x.readers:
                deps.append((rd, "war"))
        dd = []
        for d, kind in deps:
            if d is op:
                continue
            if (not d.is_dma) and (not op.is_dma) and d.eng == op.eng:
                if op.eng == "pe" or kind == "war":
                    continue
            dd.append(d)
        op.deps = dd
        for d in dd:
            d.sig = True
        for x in w:
            x.last_w = op
            x.readers = []
        for x in r:
            if x.last_w is not op:
                if not op.is_dma:
                    x.readers = [q for q in x.readers if q.is_dma or q.eng != op.eng]
                x.readers.append(op)
        op.gid = len(self.ops)
        self.ops.append(op)
        return op

    def begin_defer(self):
        self.flush_deferred()
        self._defer = []
        self._capturing = True

    def end_defer(self):
        self._capturing = False

    def flush_deferred(self):
        q = getattr(self, "_defer", None)
        self._defer = None
        self._capturing = False
        if q:
            for kind, ph, a, kw in q:
                real = getattr(self, kind)(*a, **kw)
                if ph is not None:
                    ph.op = real
        self._deferred_done = True

    def op(self, eng, fn, r=(), w=()):
        if getattr(self, "_capturing", False):
            self._defer.append(("op", None, (eng, fn), dict(r=list(r), w=list(w))))
            return None
        o = Op()
        o.eng = eng
        o.fn = fn
        o.is_dma = False
        o.sig = False
        o.key = eng
        return self._add(o, r, w)

    def dma(self, eng, out, in_, r=(), w=(), key=None, **kw):
        if getattr(self, "_capturing", False):
            ph = Deferred()
            self._defer.append(("dma", ph, (eng, out, in_), dict(r=list(r), w=list(w), key=key, **kw)))
            return ph
        o = Op()
        o.eng = eng
        o.is_dma = True
        o.sig = True
        assert key is not None
        o.key = "dma:" + key
        o.fn = lambda e: e.dma_start(out=out, in_=in_, **kw)
        return self._add(o, r, w)

    def ext_wait(self, eng, sem, value):
        return self.op(eng, lambda e: e.wait_ge(sem, value))

    def cc(self, kind, groups, in_ap, out_ap, r=(), w=(), key=None, extra=(), ext_sem=None):
        o = Op()
        o.eng = "pool"
        o.is_dma = True
        o.sig = True
        o.key = "cc:" + key
        if ext_sem is not None:
            self.ext_sems = getattr(self, "ext_sems", {})
            self.ext_sems[o.key] = ext_sem
        o.fn = lambda e: e.collective_compute(kind, ALU.bypass, replica_groups=groups, ins=[in_ap], outs=[out_ap])
        return self._add(o, r, w, extra)

    def finalize(self, final_wait_eng="sp"):
        nc = self.nc
        counts = {e: 0 for e in ENGS}
        dmac = {}
        for o in self.ops:
            if o.is_dma:
                dmac[o.key] = dmac.get(o.key, 0) + (1 if o.key.startswith("cc:") else 16)
                o.semval = dmac[o.key]
            elif o.sig:
                counts[o.eng] += 1
                o.semval = counts[o.eng]
        seen = {e: {} for e in ENGS}
        for o in self.ops:
            s = seen[o.eng]
            need = {}
            for d in o.deps:
                if s.get(d.key, 0) < d.semval:
                    if d.key not in need or need[d.key].semval < d.semval:
                        need[d.key] = d
            o.waits = [(d.key, d.semval) for d in need.values()]
            for d in need.values():
                for k, v in d.vc.items():
                    if s.get(k, 0) < v:
                        s[k] = v
            if o.sig:
                vc = dict(s)
                vc[o.key] = o.semval
                o.vc = vc
            else:
                o.vc = None
        return counts, dmac

    def emit(self, stack, tail_waits=()):
        nc = self.nc
        self.flush_deferred()
        tail_waits = [getattr(o, "op", None) or o for o in tail_waits]
        counts, dmac = self.finalize()
        sems = {}
        ext = getattr(self, "ext_sems", {})
        for k in list(ENGS) + list(dmac):
            if k in ext:
                continue
            SEM_CTR[0] += 1
            sems[k] = nc.alloc_semaphore(name="sem%d" % SEM_CTR[0])
        self.nsems = len(sems)
        LAST_SEMS[:] = list(sems.values())
        sems.update(ext)
        per = {e: [o for o in self.ops if o.eng == e] for e in ENGS}
        block = stack.enter_context(nc.Block())

        def run(eng_name, eng):
            for o in per[eng_name]:
                for k, v in o.waits:
                    eng.wait_ge(sems[k], v)
                ins = o.fn(eng)
                if o.sig:
                    if o.is_dma:
                        ins.then_inc(sems[o.key], 1 if o.key.startswith("cc:") else 16)
                    else:
                        ins.then_inc(sems[o.key], 1)
            if eng_name == "sp":
                done = {}
                for o in tail_waits:
                    done[o.key] = max(done.get(o.key, 0), o.semval)
                for k, v in done.items():
                    eng.wait_ge(sems[k], v)

        @block.tensor
        def _(e):
            run("pe", e)

        @block.scalar
        def _(e):
            run("act", e)

        @block.vector
        def _(e):
            run("dve", e)

        @block.gpsimd
        def _(e):
            run("pool", e)

        @block.sync
        def _(e):
            run("sp", e)


D = 1024
DFF = 2816
NFC = 22
ALPHA = 4.0 ** 0.25
LN_EPS = 1e-5
EPS_P = LN_EPS / (ALPHA * ALPHA)


class Ctx:
    def __init__(self, nc, st, pfx=""):
        self.nc = nc
        self.st = st
        self.S = Sched(nc)
        self.n = 0
        self.pfx = pfx

    def sb(self, name, shape, dt):
        return self.st.enter_context(self.nc.sbuf_tensor(self.pfx + "sb_" + name, list(shape), dt))

    def ps(self, name, shape, dt=F32):
        return self.st.enter_context(self.nc.psum_tensor(self.pfx + "ps_" + name, list(shape), dt))

    def R(self, name, excl=False):
        return self.S.res(name, excl)


def make_consts(C):
    S = C.S
    C.ones = C.sb("ones", [128, 128], F32)
    C.r_ones = C.R("ones")
    S.op("pool", lambda e: e.memset(C.ones[:], 1.0), w=[C.r_ones])
    C.ident = C.sb("ident", [128, 128], F32)
    C.r_ident = C.R("ident")
    S.op("pool", lambda e: e.memset(C.ident[:], 1.0), w=[C.r_ident])
    S.op("pool", lambda e: e.affine_select(out=C.ident[:], in_=C.ident[:], pattern=[[-1, 128]], compare_op=ALU.is_equal, fill=0.0, base=0, channel_multiplier=1), r=[C.r_ident], w=[C.r_ident])


def compute_mods(C, cT_d, adaw_d, adab_d, nm, wbufs, psm, r_psm):
    S = C.S
    cT = C.sb("cT", [128, 8], F32)
    r_cT = C.R("cT")
    S.dma("sp", cT[:], cT_d, w=[r_cT], key="cT")
    S.op("act", lambda e: e.activation(out=cT[:], in_=cT[:], func=AF.Silu), r=[r_cT], w=[r_cT])
    adab = C.sb("adab", [128, nm * 8], F32)
    r_adab = C.R("adab")
    S.dma("sp", adab[:], adab_d, w=[r_adab], key="adab")
    mods = C.sb("mods", [128, nm * 8], F32)
    r_mods = C.R("mods")
    src = adaw_d.rearrange("(kc p) n -> p kc n", p=128)
    st_tmp = None
    if wbufs is None:
        st_keep, st_tmp = C.st, ExitStack()
        C.st = st_tmp
        wbufs = [(C.sb("wA%d" % i, [128, 8, 256], F32), C.R("wA%d" % i)) for i in range(2)]
        C.st = st_keep
    for m in range(nm):
        for q in range(4):
            bi = (m * 4 + q) % 2
            wt, r_wt = wbufs[bi]
            c0 = m * 1024 + q * 256
            S.dma("sp", wt[:], src[:, :, c0:c0 + 256], w=[r_wt], key="adaw%d" % bi)
            for j in range(2):
                cc = q * 2 + j
                for kc in range(8):
                    S.op("pe", lambda e, m=m, cc=cc, kc=kc, j=j, wt=wt: e.matmul(psm[:, m * 8 + cc:m * 8 + cc + 1], wt[:, kc, j * 128:(j + 1) * 128], cT[:, kc:kc + 1], start=(kc == 0), stop=(kc == 7)), r=[r_wt, r_cT], w=[r_psm])
    S.op("dve", lambda e: e.tensor_tensor(out=mods[:], in0=psm[:, 0:nm * 8], in1=adab[:], op=ALU.add), r=[r_psm, r_adab], w=[r_mods])
    if st_tmp is not None:
        st_tmp.close()
    return mods, r_mods


def ln_group(C, zT, r_z, g, n0, colA, colB, r_cols, outs, P):
    S = C.S
    ps_s, r_ps_s = P["st0"]
    ps_q, r_ps_q = P["st1"]
    sq, r_sq = P["sq"]
    for cc in range(8):
        S.op("act", lambda e, cc=cc: e.activation(out=sq[cc % 2][:], in_=zT[:, cc, n0:n0 + 512], func=AF.Square), r=[r_z[cc]], w=[r_sq[cc % 2]])
        S.op("pe", lambda e, cc=cc: e.matmul(ps_s[:], C.ones[:], zT[:, cc, n0:n0 + 512], start=(cc == 0), stop=(cc == 7)), r=[C.r_ones, r_z[cc]], w=[r_ps_s])
        S.op("pe", lambda e, cc=cc: e.matmul(ps_q[:], C.ones[:], sq[cc % 2][:], start=(cc == 0), stop=(cc == 7)), r=[C.r_ones, r_sq[cc % 2]], w=[r_ps_q])
    mean, r_mean = P["mean"]
    rstd, r_rstd = P["rstd"]
    S.op("act", lambda e: e.mul(out=mean[:], in_=ps_s[:], mul=1.0 / D), r=[r_ps_s], w=[r_mean])
    S.op("dve", lambda e: e.tensor_tensor(out=rstd[:], in0=mean[:], in1=mean[:], op=ALU.mult), r=[r_mean], w=[r_rstd])
    S.op("dve", lambda e: e.scalar_tensor_tensor(out=rstd[:], in0=ps_q[:], scalar=1.0 / D, in1=rstd[:], op0=ALU.mult, op1=ALU.subtract), r=[r_ps_q, r_rstd], w=[r_rstd])
    S.op("dve", lambda e: e.tensor_scalar(out=rstd[:], in0=rstd[:], scalar1=P["eps"], scalar2=None, op0=ALU.add), r=[r_rstd], w=[r_rstd])
    S.op("act", lambda e: e.activation(out=rstd[:], in_=rstd[:], func=AF.Ln), r=[r_rstd], w=[r_rstd])
    S.op("act", lambda e: e.activation(out=rstd[:], in_=rstd[:], func=AF.Exp, scale=-0.5), r=[r_rstd], w=[r_rstd])
    S.op("dve", lambda e: e.scalar_tensor_tensor(out=mean[:], in0=mean[:], scalar=-1.0, in1=rstd[:], op0=ALU.mult, op1=ALU.mult), r=[r_mean, r_rstd], w=[r_mean])
    tt, r_tt = P["tt"]
    for cc in range(8):
        k = cc % 2
        S.op("dve", lambda e, cc=cc, k=k: e.tensor_tensor(out=tt[k][:], in0=zT[:, cc, n0:n0 + 512], in1=rstd[:], op=ALU.mult), r=[r_z[cc], r_rstd], w=[r_tt[k]])
        S.op("dve", lambda e, cc=cc, k=k: e.tensor_tensor(out=tt[k][:], in0=tt[k][:], in1=mean[:], op=ALU.add), r=[r_tt[k], r_mean], w=[r_tt[k]])
        for (dst, r_dst, gt, go, bt, bo, eng) in outs:
            if eng == "act":
                S.op("act", lambda e, cc=cc, k=k, dst=dst, gt=gt, go=go, bt=bt, bo=bo: e.activation(out=dst[:, cc, n0:n0 + 512], in_=tt[k][:], func=AF.Identity, scale=gt[:, go + cc:go + cc + 1], bias=bt[:, bo + cc:bo + cc + 1]), r=[r_tt[k]] + r_cols, w=[r_dst[cc]])
            else:
                S.op(eng, lambda e, cc=cc, k=k, dst=dst, gt=gt, go=go, bt=bt, bo=bo: e.tensor_scalar(out=dst[:, cc, n0:n0 + 512], in0=tt[k][:], scalar1=gt[:, go + cc:go + cc + 1], scalar2=bt[:, bo + cc:bo + cc + 1], op0=ALU.mult, op1=ALU.add), r=[r_tt[k]] + r_cols, w=[r_dst[cc]])


def build_tl(NT=2048, HALF=1024, debug=False):
    nc = bass.Bass("TRN2", target_bir_lowering=False)
    dt = lambda n, s, d, k="ExternalInput": nc.dram_tensor(n, list(s), d, kind=k).ap()
    yT_d = dt("yT", [D, NT], BF16)
    xT_d = dt("xT", [D, NT], F32)
    cT_d = dt("cT", [128, 8], F32)
    adaw_d = dt("adaw", [D, 4 * D], F32)
    adab_d = dt("adab", [128, 32], F32)
    lng_d = dt("lng", [128, 16], F32)
    lnb_d = dt("lnb", [128, 16], F32)
    wo_d = dt("wo", [D, D], F32)
    win_d = dt("win", [D, 2 * DFF], F32)
    wout_d = dt("wout", [DFF, D], F32)
    oT_d = dt("oT", [D, NT], F32, "ExternalOutput")
    tl_phase(nc, "", yT_d, xT_d, cT_d, adaw_d, adab_d, lng_d, lnb_d, wo_d, win_d, wout_d, oT_d, NT, HALF, debug=debug)
    return nc


def tl_phase(nc, pfx, yT_d, xT_d, cT_d, adaw_d, adab_d, lng_d, lnb_d, wo_d, win_d, wout_d, oT_d, NT=2048, HALF=1024, debug=False, sel_d=None, after=None, before_y=None):
    dt = lambda n, s, d, k="ExternalInput": nc.dram_tensor(n, list(s), d, kind=k).ap()
    NG = HALF // 512
    if debug:
        dbg_cols = dt("dbg_cols", [128, 48], F32, "ExternalOutput")
        dbg_mods = dt("dbg_mods", [128, 32], F32, "ExternalOutput")
        dbg_x1 = dt("dbg_x1", [D, HALF], F32, "ExternalOutput")
        dbg_h = dt("dbg_h", [D, HALF], BF16, "ExternalOutput")
        dbg_a = dt("dbg_a", [DFF, HALF], BF16, "ExternalOutput")
    with ExitStack() as st:
        C = Ctx(nc, st, pfx)
        S = C.S
        make_consts(C)
        PB = [(C.ps("pb%d" % i, [128, 512]), C.R("pb%d" % i, True)) for i in range(8)]
        if sel_d is not None:
            selt = C.sb("selt", [128, 2], F32)
            r_selt = C.R("selt")
            S.dma("sp", selt[:], sel_d, w=[r_selt], key="selt")
            ystg = [[(C.sb("ystg%d_%d" % (i, j), [128, HALF], BF16), C.R("ystg%d_%d" % (i, j))) for j in range(2)] for i in range(2)]
        xT = C.sb("xT", [128, 8, HALF], F32)
        r_x = [C.R("xT%d" % cc) for cc in range(8)]
        yT = C.sb("yT", [128, 8, HALF], BF16)
        r_y = [C.R("yT%d" % cc) for cc in range(8)]
        hT = yT
        r_h = r_y
        aT = C.sb("aT", [128, NFC, HALF], BF16)
        r_a = [C.R("aT%d" % f) for f in range(NFC)]
        wo = C.sb("wo", [128, 8, D], BF16)
        r_wo = C.R("wo")
        wA = [(C.sb("wA%d" % i, [128, 8, 256], F32), C.R("wA%d" % i)) for i in range(2)]
        wi = [(C.sb("wi%d" % i, [128, 8, 512], BF16), C.R("wi%d" % i)) for i in range(2)]
        wo2 = [(C.sb("wo2%d" % i, [128, NFC, 256], BF16), C.R("wo2%d" % i)) for i in range(2)]
        P = {
            "st0": PB[6], "st1": PB[7],
            "sq": ([C.sb("sq%d" % i, [128, 512], F32) for i in range(2)], [C.R("sq%d" % i) for i in range(2)]),
            "mean": (C.sb("mean", [128, 512], F32), C.R("mean")),
            "rstd": (C.sb("rstd", [128, 512], F32), C.R("rstd")),
            "tt": ([C.sb("tt%d" % i, [128, 512], F32) for i in range(2)], [C.R("tt%d" % i) for i in range(2)]),
            "eps": EPS_P,
        }
        sl = [C.sb("sl%d" % i, [128, 512], F32) for i in range(2)]
        r_sl = [C.R("sl%d" % i) for i in range(2)]
        mods, r_mods = compute_mods(C, cT_d, adaw_d, adab_d, 4, wA, PB[5][0], PB[5][1])
        lng = C.sb("lng", [128, 16], F32)
        lnb = C.sb("lnb", [128, 16], F32)
        r_ln = C.R("ln")
        S.dma("sp", lng[:], lng_d, w=[r_ln], key="lng")
        S.dma("sp", lnb[:], lnb_d, w=[r_ln], key="lnb")
        cols = C.sb("cols", [128, 48], F32)
        r_cols = C.R("cols")
        S.op("dve", lambda e: e.tensor_scalar(out=cols[:, 0:8], in0=mods[:, 0:8], scalar1=1.0 / ALPHA, scalar2=None, op0=ALU.mult), r=[r_mods], w=[r_cols])
        S.op("dve", lambda e: e.tensor_scalar(out=cols[:, 8:16], in0=mods[:, 24:32], scalar1=1.0 / ALPHA, scalar2=None, op0=ALU.mult), r=[r_mods], w=[r_cols])
        S.op("dve", lambda e: e.tensor_scalar(out=cols[:, 32:40], in0=mods[:, 16:24], scalar1=1.0, scalar2=None, op0=ALU.add), r=[r_mods], w=[r_cols])
        S.op("dve", lambda e: e.tensor_tensor(out=cols[:, 16:24], in0=lng[:, 0:8], in1=cols[:, 32:40], op=ALU.mult), r=[r_ln, r_cols], w=[r_cols])
        S.op("dve", lambda e: e.tensor_tensor(out=cols[:, 24:32], in0=lnb[:, 0:8], in1=cols[:, 32:40], op=ALU.mult), r=[r_ln, r_cols], w=[r_cols])
        S.op("dve", lambda e: e.tensor_tensor(out=cols[:, 24:32], in0=cols[:, 24:32], in1=mods[:, 8:16], op=ALU.add), r=[r_mods, r_cols], w=[r_cols])
        rc = [r_cols, r_ln]
        S.dma("pool", wo[:], wo_d.rearrange("(kc p) n -> p kc n", p=128), w=[r_wo], key="wo")
        xsrc = xT_d.rearrange("(kc p) t -> p kc t", p=128)
        if sel_d is None:
            ysrc_ = yT_d.rearrange("(kc p) t -> p kc t", p=128)
            ysrc_fn = lambda cc, a, n: ysrc_[:, cc, a:a + n]
        else:
            ysrc_fn = lambda cc, a, n: yT_d[cc % 4, cc // 4, :, a:a + n]
        osrc = oT_d.rearrange("(kc p) t -> p kc t", p=128)
        winr = win_d.rearrange("(kc p) n -> p kc n", p=128)
        woutr = wout_d.rearrange("(fc p) n -> p fc n", p=128)
        tails = []
        pbi = 0
        for half in range(NT // HALF):
            t0 = half * HALF
            def load_y(hf):
                ty = hf * HALF
                if before_y is not None and hf == 0:
                    before_y(S)
                for cc in range(8):
                    if sel_d is None:
                        S.dma("sp", yT[:, cc, :], ysrc_fn(cc, ty, HALF), w=[r_y[cc]], key="yT%d" % cc)
                    else:
                        (ya, r_ya), (yb, r_yb) = ystg[cc % 2]
                        S.dma("sp", ya[:], ysrc_fn(cc, ty, HALF), w=[r_ya], key="ysa%d" % (cc % 2))
                        S.dma("sp", yb[:], ysrc_fn(cc, NT + ty, HALF), w=[r_yb], key="ysb%d" % (cc % 2))
                        S.op("dve", lambda e, ya=ya: e.tensor_scalar(out=ya[:], in0=ya[:], scalar1=selt[:, 0:1], scalar2=None, op0=ALU.mult), r=[r_ya, r_selt], w=[r_ya])
                        S.op("dve", lambda e, ya=ya, yb=yb, cc=cc: e.scalar_tensor_tensor(out=yT[:, cc, :], in0=yb[:], scalar=selt[:, 1:2], in1=ya[:], op0=ALU.mult, op1=ALU.add), r=[r_ya, r_yb, r_selt], w=[r_y[cc]])

            if half == 0:
                load_y(0)
            for cc in range(8):
                S.dma("sp", xT[:, cc, :], xsrc[:, cc, t0:t0 + HALF], w=[r_x[cc]], key="xT%d" % cc)
            for g in range(NG):
                n0 = g * 512
                for cc in range(8):
                    pb, r_pb = PB[pbi % 4]
                    pbi += 1
                    for kc in range(8):
                        S.op("pe", lambda e, pb=pb, cc=cc, kc=kc, n0=n0: e.matmul(pb[:], wo[:, kc, cc * 128:(cc + 1) * 128], yT[:, kc, n0:n0 + 512], start=(kc == 0), stop=(kc == 7)), r=[r_wo] + r_y, w=[r_pb])
                    S.op("dve", lambda e, pb=pb, cc=cc, n0=n0: e.scalar_tensor_tensor(out=xT[:, cc, n0:n0 + 512], in0=pb[:], scalar=cols[:, cc:cc + 1], in1=xT[:, cc, n0:n0 + 512], op0=ALU.mult, op1=ALU.add), r=[r_pb, r_x[cc], r_cols], w=[r_x[cc]])
                ln_group(C, xT, r_x, g, n0, None, None, rc,
                         [(xT, r_x, lng, 0, lnb, 0, "act"), (hT, r_h, cols, 16, cols, 24, "act")], P)
            if debug and half == 0:
                tails.append(S.dma("sp", dbg_cols, cols[:], r=[r_cols], key="dbg0"))
                tails.append(S.dma("sp", dbg_mods, mods[:], r=[r_mods], key="dbg1"))
                tails.append(S.dma("sp", dbg_x1.rearrange("(kc p) t -> p kc t", p=128), xT[:], r=r_x, key="dbg2"))
                tails.append(S.dma("sp", dbg_h.rearrange("(kc p) t -> p kc t", p=128), hT[:], r=r_h, key="dbg3"))
            for blk in range(NFC // 2):
                wt, r_wt = wi[blk % 2]
                c0 = blk * 256
                S.dma("pool", wt[:, :, 0:256], winr[:, :, c0:c0 + 256], w=[r_wt], key="wi%d" % (blk % 2))
                S.dma("pool", wt[:, :, 256:512], winr[:, :, DFF + c0:DFF + c0 + 256], w=[r_wt], key="wi%d" % (blk % 2))
                for g in range(NG):
                    n0 = g * 512
                    for j in range(2):
                        fc = blk * 2 + j
                        pg, r_pg = PB[pbi % 4]
                        pu, r_pu = PB[(pbi + 1) % 4]
                        pbi += 2
                        for kc in range(8):
                            S.op("pe", lambda e, pg=pg, wt=wt, j=j, kc=kc, n0=n0: e.matmul(pg[:], wt[:, kc, j * 128:(j + 1) * 128], hT[:, kc, n0:n0 + 512], start=(kc == 0), stop=(kc == 7)), r=[r_wt] + r_h, w=[r_pg])
                        for kc in range(8):
                            S.op("pe", lambda e, pu=pu, wt=wt, j=j, kc=kc, n0=n0: e.matmul(pu[:], wt[:, kc, 256 + j * 128:256 + (j + 1) * 128], hT[:, kc, n0:n0 + 512], start=(kc == 0), stop=(kc == 7)), r=[r_wt] + r_h, w=[r_pu])
                        k = (fc * NG + g) % 2
                        S.op("act", lambda e, pg=pg, k=k: e.activation(out=sl[k][:], in_=pg[:], func=AF.Silu), r=[r_pg], w=[r_sl[k]])
                        S.op("dve", lambda e, pu=pu, k=k, fc=fc, n0=n0: e.tensor_tensor(out=aT[:, fc, n0:n0 + 512], in0=pu[:], in1=sl[k][:], op=ALU.mult), r=[r_pu, r_sl[k]], w=[r_a[fc]])
            if half + 1 < NT // HALF:
                load_y(half + 1)
            if debug and half == 0:
                tails.append(S.dma("sp", dbg_a.rearrange("(kc p) t -> p kc t", p=128), aT[:], r=r_a, key="dbg4"))
            for blk in range(4):
                wt, r_wt = wo2[blk % 2]
                S.dma("pool", wt[:], woutr[:, :, blk * 256:(blk + 1) * 256], w=[r_wt], key="wo2%d" % (blk % 2))
                for j in range(2):
                    cc = blk * 2 + j
                    for g in range(NG):
                        n0 = g * 512
                        pb, r_pb = PB[pbi % 4]
                        pbi += 1
                        for fc in range(NFC):
                            S.op("pe", lambda e, pb=pb, wt=wt, j=j, fc=fc, n0=n0: e.matmul(pb[:], wt[:, fc, j * 128:(j + 1) * 128], aT[:, fc, n0:n0 + 512], start=(fc == 0), stop=(fc == NFC - 1)), r=[r_wt, r_a[fc]], w=[r_pb])
                        S.op("dve", lambda e, pb=pb, cc=cc, n0=n0: e.scalar_tensor_tensor(out=xT[:, cc, n0:n0 + 512], in0=pb[:], scalar=cols[:, 8 + cc:9 + cc], in1=xT[:, cc, n0:n0 + 512], op0=ALU.mult, op1=ALU.add), r=[r_pb, r_x[cc], r_cols], w=[r_x[cc]])
            for g in range(NG):
                n0 = g * 512
                ln_group(C, xT, r_x, g, n0, None, None, rc, [(xT, r_x, lng, 8, lnb, 8, "act")], P)
            for cc in range(8):
                o = S.dma("sp", osrc[:, cc, t0:t0 + HALF], xT[:, cc, :], r=[r_x[cc]], w=[], key="oT%d" % cc)
                tails.append(o)
        if after is not None:
            after(C, tails)
        S.emit(st, tail_waits=tails)


T = 4096
NH = 8
HD = 64
QK_EPS = 1e-6
NEG = -30000.0
import os
LAG = int(os.environ.get("FOX_LAG", "2"))
FOX_DVE = int(os.environ.get("FOX_DVE", "1"))


def build_fox(TT=T, debug=False):
    nc = bass.Bass("TRN2", target_bir_lowering=False)
    dt = lambda n, s, d, k="ExternalInput": nc.dram_tensor(n, list(s), d, kind=k).ap()
    ins = fox_decl(dt, TT)
    yg_d = dt("ygT", [512, TT], BF16, "ExternalOutput")
    fox_phase(nc, "", ins, yg_d, TT, debug=debug)
    return nc


def fox_decl(dt, TT=T, sfx=""):
    return dict(
        x=dt("x" + sfx, [TT, D], F32), cT=dt("cT" + sfx, [128, 8], F32), adaw=dt("adaw" + sfx, [D, 2 * D], F32), adab=dt("adab" + sfx, [128, 16], F32),
        wq=dt("wq" + sfx, [D, 512], F32), wk=dt("wk" + sfx, [D, 512], F32), wv=dt("wv" + sfx, [D, 512], F32), wf=dt("wf" + sfx, [D, 8], F32), wg=dt("wg" + sfx, [D, 512], F32),
        bfb=dt("bfb" + sfx, [128, 8], F32), qgb=dt("qgb" + sfx, [128, 512], F32), kgb=dt("kgb" + sfx, [128, 512], F32))


def fox_phase(nc, pfx, ins, yg_d, TT=T, debug=False, after=None):
    dt = lambda n, s, d, k="ExternalInput": nc.dram_tensor(n, list(s), d, kind=k).ap()
    x_d, cT_d, adaw_d, adab_d = ins["x"], ins["cT"], ins["adaw"], ins["adab"]
    wq_d, wk_d, wv_d, wf_d, wg_d = ins["wq"], ins["wk"], ins["wv"], ins["wf"], ins["wg"]
    bf_d, qg_d, kg_d = ins["bfb"], ins["qgb"], ins["kgb"]
    NGR = TT // 512
    NTL = TT // 128
    if debug:
        dbg_h = dt("dbg_h", [D, 512], BF16, "ExternalOutput")
        dbg_q = dt("dbg_q", [128, 4, 512], BF16, "ExternalOutput")
        dbg_k = dt("dbg_k", [128, 4, TT], BF16, "ExternalOutput")
        dbg_F = dt("dbg_F", [128, NTL, 8], F32, "ExternalOutput")
        dbg_v = dt("dbg_v", [128, NTL, 8, 65], BF16, "ExternalOutput")
    with ExitStack() as st:
        C = Ctx(nc, st, pfx)
        S = C.S
        make_consts(C)
        identb = C.sb("identb", [128, 128], BF16)
        r_identb = C.R("identb")
        S.op("dve", lambda e: e.tensor_copy(out=identb[:], in_=C.ident[:]), r=[C.r_ident], w=[r_identb])
        tri = C.sb("tri", [128, 128], F32)
        r_tri = C.R("tri")
        S.op("pool", lambda e: e.memset(tri[:], 1.0), w=[r_tri])
        S.op("pool", lambda e: e.affine_select(out=tri[:], in_=tri[:], pattern=[[1, 128]], compare_op=ALU.is_ge, fill=0.0, base=0, channel_multiplier=-1), r=[r_tri], w=[r_tri])
        selA = C.sb("selA", [16, 8, 128], F32)
        selB = C.sb("selB", [16, 8, 128], F32)
        sel = C.sb("sel", [16, 8, 128], BF16)
        r_sel = C.R("sel")
        S.op("pool", lambda e: e.memset(selA[:], 1.0), w=[r_sel])
        S.op("pool", lambda e: e.memset(selB[:], 1.0), w=[r_sel])
        S.op("pool", lambda e: e.affine_select(out=selA[:], in_=selA[:], pattern=[[-1, 8], [0, 128]], compare_op=ALU.is_equal, fill=0.0, base=0, channel_multiplier=1), r=[r_sel], w=[r_sel])
        S.op("pool", lambda e: e.affine_select(out=selB[:], in_=selB[:], pattern=[[-1, 8], [0, 128]], compare_op=ALU.is_equal, fill=0.0, base=-8, channel_multiplier=1), r=[r_sel], w=[r_sel])
        S.op("dve", lambda e: e.tensor_tensor(out=sel[:], in0=selA[:], in1=selB[:], op=ALU.add), r=[r_sel], w=[r_sel])
        maskf = C.sb("maskf", [128, 4, 512], F32)
        maskb = C.sb("maskb", [128, 4, 512], BF16)
        r_mask = C.R("mask")
        S.op("pool", lambda e: e.memset(maskf[:], 0.0), w=[r_mask])
        for j in range(4):
            S.op("pool", lambda e, j=j: e.affine_select(out=maskf[:, j, :], in_=maskf[:, j, :], pattern=[[1, 512]], compare_op=ALU.is_ge, fill=NEG, base=-128 * j, channel_multiplier=-1), r=[r_mask], w=[r_mask])
        S.op("dve", lambda e: e.tensor_copy(out=maskb[:], in_=maskf[:]), r=[r_mask], w=[r_mask])
        PB = [(C.ps("pb%d" % i, [128, 512]), C.R("pb%d" % i, True)) for i in range(7)]
        pbb = C.ps("pbb", [128, 1024], BF16)
        r_pbb = C.R("pbb", True)
        wA = [(C.sb("wA%d" % i, [128, 8, 256], F32), C.R("wA%d" % i)) for i in range(2)]
        mods, r_mods = compute_mods(C, cT_d, adaw_d, adab_d, 2, wA, PB[6][0], PB[6][1])
        cols = C.sb("cols", [128, 8], F32)
        r_cols = C.R("cols")
        S.op("dve", lambda e: e.tensor_scalar(out=cols[:], in0=mods[:, 8:16], scalar1=1.0, scalar2=None, op0=ALU.add), r=[r_mods], w=[r_cols])
        wts = {}
        r_w = C.R("wts")
        for nm, d_ in (("wq", wq_d), ("wk", wk_d), ("wv", wv_d), ("wg", wg_d)):
            wts[nm] = C.sb(nm, [128, 8, 512], BF16)
            S.dma("pool", wts[nm][:], d_.rearrange("(kc p) n -> p kc n", p=128), w=[r_w], key=nm)
        wf = C.sb("wf", [128, 8, 8], BF16)
        S.dma("pool", wf[:], wf_d.rearrange("(kc p) n -> p kc n", p=128), w=[r_w], key="wf")
        bfb = C.sb("bfb", [128, 8], F32)
        qgb = C.sb("qgb", [128, 512], F32)
        kgb = C.sb("kgb", [128, 512], F32)
        r_par = C.R("par")
        S.dma("sp", bfb[:], bf_d, w=[r_par], key="bfb")
        S.dma("sp", qgb[:], qg_d, w=[r_par], key="qgb")
        S.dma("sp", kgb[:], kg_d, w=[r_par], key="kgb")
        xt = C.sb("xt", [128, 4, D], F32)
        r_xt = C.R("xt")
        hT = C.sb("hT", [128, 8, 512], BF16)
        r_hT = [C.R("hT%d" % k) for k in range(8)]
        kT = C.sb("kT", [128, 4, TT], BF16)
        r_kT = [C.R("kT%d" % g) for g in range(NGR)]
        qT = C.sb("qT", [128, 4, 512], BF16)
        r_qT = C.R("qT")
        Va = C.sb("Va", [128, NTL, NH, 65], BF16)
        r_V = [C.R("V%d" % g) for g in range(NGR)]
        r_Vones = C.R("Vones")
        S.op("pool", lambda e: e.memset(Va[:, :, :, 64:65], 1.0), w=[r_Vones])
        eg = C.sb("eg", [64, NH, 512], F32)
        r_eg = [C.R("eg%d" % h) for h in range(NH)]
        Fcol = C.sb("Fcol", [128, NTL, NH], F32)
        r_F = C.R("Fcol")
        carry = C.sb("carry", [128, NTL + 1, NH], F32)
        r_carry = C.R("carry")
        S.op("dve", lambda e: e.memset(carry[:, 0, :], 0.0), w=[r_carry])
        biasG = C.sb("biasG", [128, NTL, NH], F32)
        r_bias = C.R("biasG")
        FqT = C.sb("FqT", [16, 512], BF16)
        r_FqT = C.R("FqT")
        sq = C.sb("sq", [128, 512], F32); r_sq = C.R("sq")
        tmp = C.sb("tmp", [128, 512], F32); r_tmp = C.R("tmp")
        qtok = C.sb("qtok", [128, 512], BF16); r_qtok = C.R("qtok")
        ktok = C.sb("ktok", [128, 512], BF16); r_ktok = C.R("ktok")
        ssq = C.sb("ssq", [128, 16], F32); r_ssq = C.R("ssq")
        zf = C.sb("zf", [128, 8], F32); r_zf = C.R("zf")
        fq = C.sb("fq", [128, 8], F32); r_fq = C.R("fq")
        fq16 = C.sb("fq16", [128, 16], BF16); r_fq16 = C.R("fq16")
        PT = [(C.sb("PT%d" % i, [128, 512], BF16), C.R("PT%d" % i)) for i in range(4)]
        oT = C.sb("oT", [64, 512], F32); r_oT = C.R("oT")
        dn = C.sb("dn", [128, 512], F32); r_dn = C.R("dn")
        wv_ = C.sb("wv_", [64, 512], F32); r_wv_ = C.R("wv_")
        yg = [(C.sb("yg%d" % i, [64, 512], BF16), C.R("yg%d" % i)) for i in range(2)]
        tails = []
        xsrc = x_d.rearrange("(i p) d -> p i d", p=128)
        ps_i = 0
        for G in range(NGR):
            if G == 0:
                S.dma("sp", xt[:], xsrc[:, 0:4, :], w=[r_xt], key="xt")
            for kc in range(8):
                pb, r_pb = PB[kc % 2]
                for i in range(4):
                    S.op("pe", lambda e, pb=pb, i=i, kc=kc: e.transpose(pb[:, i * 128:(i + 1) * 128], xt[:, i, kc * 128:(kc + 1) * 128], C.ident[:]), r=[r_xt, C.r_ident], w=[r_pb])
                if kc % 2 == 0:
                    S.op("act", lambda e, pb=pb, kc=kc: e.activation(out=hT[:, kc, :], in_=pb[:], func=AF.Identity, scale=cols[:, kc:kc + 1], bias=mods[:, kc:kc + 1]), r=[r_pb, r_cols, r_mods], w=[r_hT[kc]])
                else:
                    S.op("dve", lambda e, pb=pb, kc=kc: e.tensor_scalar(out=hT[:, kc, :], in0=pb[:], scalar1=cols[:, kc:kc + 1], scalar2=mods[:, kc:kc + 1], op0=ALU.mult, op1=ALU.add), r=[r_pb, r_cols, r_mods], w=[r_hT[kc]])
            if debug and G == 0:
                tails.append(S.dma("sp", dbg_h.rearrange("(kc p) t -> p kc t", p=128), hT[:], r=r_hT, key="dbg0"))
            if G + 1 < NGR:
                S.dma("sp", xt[:], xsrc[:, 4 * (G + 1):4 * (G + 1) + 4, :], w=[r_xt], key="xt")
            for i in range(4):
                tl_ = 4 * G + i
                (pq, r_pq), (pk, r_pk), (pv, r_pv), (pf, r_pf) = PB[2], PB[3], PB[4], PB[5]
                for kc in range(8):
                    lw = hT[:, kc, i * 128:(i + 1) * 128]
                    S.op("pe", lambda e, lw=lw, kc=kc: e.matmul(pq[:], lw, wts["wq"][:, kc, :], start=(kc == 0), stop=(kc == 7)), r=[r_w, r_hT[kc]], w=[r_pq])
                    S.op("pe", lambda e, lw=lw, kc=kc: e.matmul(pk[:], lw, wts["wk"][:, kc, :], start=(kc == 0), stop=(kc == 7)), r=[r_w, r_hT[kc]], w=[r_pk])
                    S.op("pe", lambda e, lw=lw, kc=kc: e.matmul(pv[:], lw, wts["wv"][:, kc, :], start=(kc == 0), stop=(kc == 7)), r=[r_w, r_hT[kc]], w=[r_pv])
                    S.op("pe", lambda e, lw=lw, kc=kc: e.matmul(pf[:, 0:8], lw, wf[:, kc, :], start=(kc == 0), stop=(kc == 7)), r=[r_w, r_hT[kc]], w=[r_pf])
                for h in (2 * i, 2 * i + 1):
                    pb, r_pb = PB[h % 2]
                    for kc in range(8):
                        S.op("pe", lambda e, pb=pb, h=h, kc=kc: e.matmul(pb[0:64, :], wts["wg"][:, kc, h * 64:(h + 1) * 64], hT[:, kc, :], start=(kc == 0), stop=(kc == 7)), r=[r_w, r_hT[kc]], w=[r_pb])
                    S.op("act", lambda e, pb=pb, h=h: e.activation(out=eg[:, h, :], in_=pb[0:64, :], func=AF.Exp, scale=-1.0), r=[r_pb], w=[r_eg[h]])
                for which, (pp, r_pp), gb, tok, r_tok, eps_, mul_ in (("q", (pq, r_pq), qgb, qtok, r_qtok, 64.0 * QK_EPS, 1.0), ("k", (pk, r_pk), kgb, ktok, r_ktok, QK_EPS, 1.0 / 64.0)):
                    o8 = 0 if which == "q" else 8
                    S.op("act", lambda e, pp=pp: e.activation(out=sq[:], in_=pp[:], func=AF.Square), r=[r_pp], w=[r_sq])
                    S.op("dve", lambda e, o8=o8: e.tensor_reduce(out=ssq[:, o8:o8 + 8], in_=sq[:].rearrange("p (h d) -> p h d", d=64), axis=AX.X, op=ALU.add), r=[r_sq], w=[r_ssq])
                    S.op("dve", lambda e, o8=o8, eps_=eps_, mul_=mul_: e.tensor_scalar(out=ssq[:, o8:o8 + 8], in0=ssq[:, o8:o8 + 8], scalar1=mul_, scalar2=eps_, op0=ALU.mult, op1=ALU.add), r=[r_ssq], w=[r_ssq])
                    S.op("act", lambda e, o8=o8: e.activation(out=ssq[:, o8:o8 + 8], in_=ssq[:, o8:o8 + 8], func=AF.Ln), r=[r_ssq], w=[r_ssq])
                    S.op("act", lambda e, o8=o8: e.activation(out=ssq[:, o8:o8 + 8], in_=ssq[:, o8:o8 + 8], func=AF.Exp, scale=-0.5), r=[r_ssq], w=[r_ssq])
                    S.op("dve", lambda e, pp=pp, o8=o8: e.tensor_tensor(out=tmp[:].rearrange("p (h d) -> p h d", d=64), in0=pp[:].rearrange("p (h d) -> p h d", d=64), in1=ssq[:, o8:o8 + 8].unsqueeze(2).to_broadcast([128, 8, 64]), op=ALU.mult), r=[r_pp, r_ssq], w=[r_tmp])
                    S.op("dve", lambda e, gb=gb, tok=tok: e.tensor_tensor(out=tok[:], in0=tmp[:], in1=gb[:], op=ALU.mult), r=[r_tmp, r_par], w=[r_tok])
                S.op("act", lambda e, tl_=tl_: e.copy(out=Va[:, tl_, :, 0:64], in_=pv[:].rearrange("p (h d) -> p h d", d=64)), r=[r_pv, r_Vones], w=[r_V[G]])
                S.op("dve", lambda e: e.tensor_tensor(out=zf[:], in0=pf[:, 0:8], in1=bfb[:], op=ALU.add), r=[r_pf, r_par], w=[r_zf])
                S.op("act", lambda e: e.activation(out=zf[:], in_=zf[:], func=AF.Exp, scale=-1.0), r=[r_zf], w=[r_zf])
                S.op("act", lambda e: e.activation(out=zf[:], in_=zf[:], func=AF.Ln, bias=1.0), r=[r_zf], w=[r_zf])
                pc, r_pc = PB[6]
                S.op("pe", lambda e: e.matmul(pc[:, 0:8], tri[:], zf[:], start=True, stop=True), r=[r_tri, r_zf], w=[r_pc])
                S.op("pe", lambda e: e.matmul(pc[:, 8:16], C.ones[:], zf[:], start=True, stop=True), r=[C.r_ones, r_zf], w=[r_pc])
                S.op("dve", lambda e, tl_=tl_: e.tensor_tensor(out=Fcol[:, tl_, :], in0=carry[:, tl_, :], in1=pc[:, 0:8], op=ALU.subtract), r=[r_carry, r_pc], w=[r_F])
                S.op("dve", lambda e, tl_=tl_: e.tensor_tensor(out=carry[:, tl_ + 1, :], in0=carry[:, tl_, :], in1=pc[:, 8:16], op=ALU.subtract), r=[r_carry, r_pc], w=[r_carry])
                S.op("dve", lambda e, tl_=tl_, G=G: e.tensor_tensor(out=fq[:], in0=Fcol[:, tl_, :], in1=carry[:, 4 * G, :], op=ALU.subtract), r=[r_F, r_carry], w=[r_fq])
                S.op("dve", lambda e: e.tensor_copy(out=fq16[:, 0:8], in_=fq[:]), r=[r_fq], w=[r_fq16])
                S.op("dve", lambda e: e.tensor_tensor(out=fq[:], in0=fq[:], in1=fq16[:, 0:8], op=ALU.subtract), r=[r_fq, r_fq16], w=[r_fq])
                S.op("dve", lambda e: e.tensor_copy(out=fq16[:, 8:16], in_=fq[:]), r=[r_fq], w=[r_fq16])
                for p in range(4):
                    S.op("pe", lambda e, p=p: e.transpose(pbb[:, p * 128:(p + 1) * 128], qtok[:, p * 128:(p + 1) * 128], identb[:]), r=[r_qtok, r_identb], w=[r_pbb])
                for p in range(4):
                    S.op("pe", lambda e, p=p: e.transpose(pbb[:, 512 + p * 128:512 + (p + 1) * 128], ktok[:, p * 128:(p + 1) * 128], identb[:]), r=[r_ktok, r_identb], w=[r_pbb])
                S.op("act", lambda e, i=i: e.copy(out=qT[:, :, i * 128:(i + 1) * 128], in_=pbb[:, 0:512].rearrange("p (a t) -> p a t", t=128)), r=[r_pbb], w=[r_qT])
                S.op("dve", lambda e, tl_=tl_: e.tensor_copy(out=kT[:, :, tl_ * 128:(tl_ + 1) * 128], in_=pbb[:, 512:1024].rearrange("p (a t) -> p a t", t=128)), r=[r_pbb], w=[r_kT[G]])
                S.op("pe", lambda e: e.transpose(pbb[0:16, 0:128], fq16[:, 0:16], identb[:]), r=[r_fq16, r_identb], w=[r_pbb])
                S.op("dve", lambda e, i=i: e.tensor_copy(out=FqT[:, i * 128:(i + 1) * 128], in_=pbb[0:16, 0:128]), r=[r_pbb], w=[r_FqT])
            nkb = 4 * G + 4
            for h in range(NH):
                S.op("dve", lambda e, h=h, nkb=nkb, G=G: e.tensor_scalar(out=biasG[:, 0:nkb, h], in0=Fcol[:, 0:nkb, h], scalar1=-1.0, scalar2=carry[:, 4 * G, h:h + 1], op0=ALU.mult, op1=ALU.add), r=[r_F, r_carry], w=[r_bias])
            tiles = [(2 * p + e, kb) for p in range(NH // 2) for kb in range(nkb) for e in range(2)]

            def emit_pv(h, kb, PTt, r_PT, nkb=nkb, G=G):
                pO, r_pO = PB[4 + h % 2]
                gk = kb // 4
                S.op("pe", lambda e, pO=pO, PTt=PTt, kb=kb, h=h, nkb=nkb: e.matmul(pO[0:65, :], Va[:, kb, h, :], PTt[:], start=(kb == 0), stop=(kb == nkb - 1)), r=[r_V[gk], r_Vones, r_PT], w=[r_pO])
                if kb == nkb - 1:
                    pB_, r_pB = PB[6]
                    ygt, r_yg = yg[h % 2]
                    S.op("act", lambda e, pO=pO: e.copy(out=oT[:], in_=pO[0:64, :]), r=[r_pO], w=[r_oT])
                    S.op("dve", lambda e, pO=pO: e.tensor_copy(out=dn[64:65, :], in_=pO[64:65, :]), r=[r_pO], w=[r_dn])
                    S.op("pe", lambda e: e.matmul(pB_[0:64, :], C.ones[64:65, 0:64], dn[64:65, :], start=True, stop=True), r=[C.r_ones, r_dn], w=[r_pB])
                    S.op("dve", lambda e, h=h: e.scalar_tensor_tensor(out=wv_[:], in0=eg[:, h, :], scalar=1.0, in1=pB_[0:64, :], op0=ALU.add, op1=ALU.mult), r=[r_eg[h], r_pB], w=[r_wv_])
                    S.op("dve", lambda e: e.reciprocal(out=wv_[:], in_=wv_[:]), r=[r_wv_], w=[r_wv_])
                    S.op("dve", lambda e, ygt=ygt: e.tensor_tensor(out=ygt[:], in0=oT[:], in1=wv_[:], op=ALU.mult), r=[r_oT, r_wv_], w=[r_yg])
                    tails.append(S.dma("sp", yg_d[h * 64:(h + 1) * 64, G * 512:(G + 1) * 512], ygt[:], r=[r_yg], key="yg%d" % (h % 2)))

            pendq = []
            for (h, kb) in tiles:
                p_, e_ = h // 2, h % 2
                pS, r_pS = PB[ps_i % 4]
                PTt, r_PT = PT[ps_i % 4]
                ps_i += 1
                diag = kb >= 4 * G
                gk = kb // 4
                if FOX_DVE:
                    fqb, r_fqb = ((tmp, r_tmp), (sq, r_sq))[h % 2]
                    if kb == 0:
                        pB_, r_pB = PB[6]
                        S.op("pe", lambda e, h=h: e.matmul(pB_[:], sel[:, h, :], FqT[:], start=True, stop=True), r=[r_sel, r_FqT], w=[r_pB])
                        S.op("dve", lambda e, fqb=fqb: e.tensor_copy(out=fqb[:], in_=pB_[:]), r=[r_pB], w=[r_fqb])
                    S.op("pe", lambda e, pS=pS, p_=p_, e_=e_, kb=kb, diag=diag: e.matmul(pS[:], kT[e_ * 64:(e_ + 1) * 64, p_, kb * 128:(kb + 1) * 128], qT[e_ * 64:(e_ + 1) * 64, p_, :], start=True, stop=(not diag)), r=[r_kT[gk], r_qT], w=[r_pS])
                    if diag:
                        S.op("pe", lambda e, pS=pS, kb=kb, G=G: e.matmul(pS[:], identb[:], maskb[:, kb - 4 * G, :], start=False, stop=True), r=[r_identb, r_mask], w=[r_pS])
                    S.op("dve", lambda e, pS=pS, fqb=fqb: e.tensor_tensor(out=pS[:], in0=pS[:], in1=fqb[:], op=ALU.add), r=[r_pS, r_fqb], w=[r_pS])
                else:
                    S.op("pe", lambda e, pS=pS, p_=p_, e_=e_, kb=kb: e.matmul(pS[:], kT[e_ * 64:(e_ + 1) * 64, p_, kb * 128:(kb + 1) * 128], qT[e_ * 64:(e_ + 1) * 64, p_, :], start=True, stop=False), r=[r_kT[gk], r_qT], w=[r_pS])
                    S.op("pe", lambda e, pS=pS, h=h, diag=diag: e.matmul(pS[:], sel[:, h, :], FqT[:], start=False, stop=(not diag)), r=[r_sel, r_FqT], w=[r_pS])
                    if diag:
                        S.op("pe", lambda e, pS=pS, kb=kb, G=G: e.matmul(pS[:], identb[:], maskb[:, kb - 4 * G, :], start=False, stop=True), r=[r_identb, r_mask], w=[r_pS])
                S.op("act", lambda e, pS=pS, PTt=PTt, kb=kb, h=h: e.activation(out=PTt[:], in_=pS[:], func=AF.Exp, bias=biasG[:, kb, h:h + 1], scale=1.0), r=[r_pS, r_bias], w=[r_PT])
                pendq.append((h, kb, PTt, r_PT))
                if h % 2 == 1:
                    while len(pendq) > 2:
                        emit_pv(*pendq.pop(0))
            while pendq:
                emit_pv(*pendq.pop(0))
            if debug and G == 0:
                tails.append(S.dma("sp", dbg_q, qT[:], r=[r_qT], key="dbg1"))
        if debug:
            tails.append(S.dma("sp", dbg_k, kT[:], r=r_kT, key="dbg2"))
            tails.append(S.dma("sp", dbg_F, Fcol[:], r=[r_F], key="dbg3"))
            tails.append(S.dma("sp", dbg_v, Va[:], r=r_V + [r_Vones], key="dbg4"))
        if after is not None:
            after(C, tails)
        S.emit(st, tail_waits=tails)

import math, os
SEQ_VAR = 0

NH = 8
HD = 64
C0 = math.exp(-0.5)
GN_EPS = 64 * 1e-5


class TR:
    def __init__(self, C, name, shape, dt, psum=False):
        self.t = C.ps(name, shape, dt) if psum else C.sb(name, shape, dt)
        self.r = C.R(name, excl=psum)


class _Stop(Exception):
    pass


def build_rwkv(TT=4096, debug=False, stop=None):
    nc = bass.Bass("TRN2", target_bir_lowering=False)
    dt = lambda n, s, d, k="ExternalInput": nc.dram_tensor(n, list(s), d, kind=k).ap()
    xT_d = dt("xT", [D, TT], F32)
    ins = rwkv_decl(dt)
    yT_d = dt("yT", [512, TT], BF16, "ExternalOutput")
    xv = xT_d.rearrange("(kc p) t -> p kc t", p=128)
    rwkv_phase(nc, "", lambda t0, n: xv[:, :, t0:t0 + n], ins, yT_d, TT, debug=debug, stop=stop)
    return nc


RW_BC = ["w0b", "a0b", "kkb", "kab", "rkb", "lngb", "lnbb"]


def rwkv_decl(dt, sfx=""):
    d_ = dict(cT=dt("cT" + sfx, [128, 8], F32), adaw=dt("adaw" + sfx, [D, 2 * D], F32), adab=dt("adab" + sfx, [128, 16], F32), mu=dt("mu" + sfx, [128, 48], F32),
              wr=dt("wr" + sfx, [D, 512], F32), wk=dt("wk" + sfx, [D, 512], F32), wv=dt("wv" + sfx, [D, 512], F32),
              w1=dt("w1" + sfx, [D, 64], F32), a1=dt("a1" + sfx, [D, 64], F32), g1=dt("g1" + sfx, [D, 128], F32),
              w2=dt("w2" + sfx, [64, 512], F32), a2=dt("a2" + sfx, [64, 512], F32), g2=dt("g2" + sfx, [128, 512], F32))
    for n in RW_BC:
        d_[n] = dt(n + sfx, [128, 512], F32)
    return d_


def rwkv_phase(nc, pfx, xsrc_fn, ins, yT_d, TT=4096, debug=False, stop=None, after=None, before_x=None):
    dt = lambda n, s, d, k="ExternalInput": nc.dram_tensor(n, list(s), d, kind=k).ap()
    cT_d, adaw_d, adab_d, mu_d = ins["cT"], ins["adaw"], ins["adab"], ins["mu"]
    wr_d, wk_d, wv_d, w1_d, a1_d, g1_d, w2_d, a2_d, g2_d = [ins[k] for k in ("wr", "wk", "wv", "w1", "a1", "g1", "w2", "a2", "g2")]
    bc_names = RW_BC
    bc_d = {n: ins[n] for n in bc_names}
    GS = 256
    NTG = GS // 128
    NGR = TT // GS
    dbg = {}
    if debug:
        for n in ["r", "kp", "kkn", "a", "sw", "Y", "bonus", "Gm", "yn"]:
            dbg[n] = dt("dbg_" + n, [128, 512], F32, "ExternalOutput")
        dbg["S"] = dt("dbg_S", [128, 4, 64], F32, "ExternalOutput")
        for n, shp in (("Arb", [128, 8, 128]), ("Aak", [128, 8, 128]), ("NT0", [128, 8, 128]), ("Z0", [128, 8, 128]), ("Zf", [128, 8, 128]), ("Y0", [128, 512]), ("RbT", [128, 4, 128]), ("Qm", [128, 4, 128]), ("Hm", [128, 4, 128]), ("gcol", [128, 8]), ("v", [128, 512]), ("ART", [128, 4, 2, 128]), ("KT", [128, 4, 128]), ("BT", [128, 4, 128]), ("Bh", [128, 512]), ("Kh", [128, 512])):
            dbg[n] = dt("dbg_" + n, shp, F32, "ExternalOutput")
    with ExitStack() as st:
        C = Ctx(nc, st, pfx)
        S = C.S
        make_consts(C)
        ident, ones = C.ident, C.ones
        r_ident, r_ones = C.r_ident, C.r_ones
        tails = []
        identb = TR(C, "identb", [128, 128], BF16)
        S.op("dve", lambda e: e.tensor_copy(out=identb.t[:], in_=ident[:]), r=[r_ident], w=[identb.r])
        U = TR(C, "U", [128, 128], F32)
        IU = TR(C, "IUf", [128, 128], F32)
        S.op("pool", lambda e: e.memset(U.t[:], 1.0), w=[U.r])
        S.op("pool", lambda e: e.affine_select(out=U.t[:], in_=U.t[:], pattern=[[1, 128]], compare_op=ALU.is_gt, fill=0.0, base=0, channel_multiplier=-1), r=[U.r], w=[U.r])
        S.op("pool", lambda e: e.memset(IU.t[:], 1.0), w=[IU.r])
        S.op("pool", lambda e: e.affine_select(out=IU.t[:], in_=IU.t[:], pattern=[[1, 128]], compare_op=ALU.is_ge, fill=0.0, base=0, channel_multiplier=-1), r=[IU.r], w=[IU.r])
        SLf = TR(C, "SLf", [128, 128], F32)
        S.op("pool", lambda e: e.memset(SLf.t[:], 1.0), w=[SLf.r])
        S.op("pool", lambda e: e.affine_select(out=SLf.t[:], in_=SLf.t[:], pattern=[[-1, 128]], compare_op=ALU.is_gt, fill=0.0, base=0, channel_multiplier=1), r=[SLf.r], w=[SLf.r])
        mask1 = TR(C, "mask1", [128, 128], F32)
        maskL = TR(C, "maskL", [128, 64], F32)
        idl = TR(C, "idl", [128, 64], F32)
        for ch in range(2):
            ps_ = slice(ch * 64, ch * 64 + 64)
            S.op("dve", lambda e, ps_=ps_: e.tensor_copy(out=mask1.t[ps_, 0:64], in_=U.t[ps_, ps_]), r=[U.r], w=[mask1.r])
            S.op("dve", lambda e, ps_=ps_: e.tensor_copy(out=mask1.t[ps_, 64:128], in_=IU.t[ps_, ps_]), r=[IU.r], w=[mask1.r])
            S.op("dve", lambda e, ps_=ps_: e.tensor_copy(out=maskL.t[ps_, :], in_=SLf.t[ps_, ps_]), r=[SLf.r], w=[maskL.r])
            S.op("dve", lambda e, ps_=ps_: e.tensor_copy(out=idl.t[ps_, :], in_=ident[ps_, ps_]), r=[r_ident], w=[idl.r])
        triC = TR(C, "triC", [128, 128], F32)
        blkC = TR(C, "blkC", [128, 128], F32)
        S.op("dve", lambda e: e.tensor_scalar(out=triC.t[:], in0=IU.t[:], scalar1=-C0, scalar2=None, op0=ALU.mult), r=[IU.r], w=[triC.r])
        S.op("dve", lambda e: e.memset(triC.t[0:64, 64:128], 0.0), w=[triC.r])
        S.op("dve", lambda e: e.memset(blkC.t[:], -C0), w=[blkC.r])
        S.op("dve", lambda e: e.memset(blkC.t[0:64, 64:128], 0.0), w=[blkC.r])
        S.op("dve", lambda e: e.memset(blkC.t[64:128, 0:64], 0.0), w=[blkC.r])
        avg = TR(C, "avg", [128, 1], F32)
        S.op("dve", lambda e: e.memset(avg.t[:], 1.0 / 64.0), w=[avg.r])
        PB = [TR(C, "pb%d" % i, [128, 512], F32, psum=True) for i in range(7)]
        pbb = TR(C, "pbb", [128, 1024], BF16, psum=True)
        mu = TR(C, "mu", [128, 48], F32)
        S.dma("sp", mu.t[:], mu_d, w=[mu.r], key="mu")
        W = {}
        r_w = C.R("wts")
        for nm, d_, shp in (("wr", wr_d, [128, 8, 512]), ("wk", wk_d, [128, 8, 512]), ("wv", wv_d, [128, 8, 512]), ("w1", w1_d, [128, 8, 64]), ("a1", a1_d, [128, 8, 64]), ("g1", g1_d, [128, 8, 128])):
            W[nm] = C.sb(nm, shp, BF16)
            S.dma("pool", W[nm][:], d_.rearrange("(kc p) n -> p kc n", p=128), w=[r_w], key=nm)
        for nm, d_, shp in (("w2", w2_d, [64, 512]), ("a2", a2_d, [64, 512]), ("g2", g2_d, [128, 512])):
            W[nm] = C.sb(nm, shp, BF16)
            S.dma("pool", W[nm][:], d_, w=[r_w], key=nm)
        BC = {}
        r_bc = C.R("bc")
        for n in bc_names:
            BC[n] = C.sb(n, [128, 512], F32)
            S.dma("sp", BC[n][:], bc_d[n], w=[r_bc], key=n)
        xg = TR(C, "xg", [128, 8, GS + 1], F32)
        S.op("dve", lambda e: e.memset(xg.t[:, :, 0:1], 0.0), w=[xg.r])
        Nm = [TR(C, "Nm%d" % i, [128, 8, 128], BF16) for i in range(2)]
        NTm = [TR(C, "NTm%d" % i, [128, 8, 128], BF16) for i in range(2)]
        for t_ in Nm + NTm:
            S.op("pool", lambda e, t_=t_: e.memset(t_.t[:], 0.0), w=[t_.r])
        ST = [TR(C, "ST%d" % i, [128, 4, 64], F32) for i in range(2)]
        S.op("dve", lambda e: e.memset(ST[0].t[:], 0.0), w=[ST[0].r])
        mods, r_mods = compute_mods(C, cT_d, adaw_d, adab_d, 2, None, PB[6].t, PB[6].r)
        cols = TR(C, "cols", [128, 8], F32)
        S.op("dve", lambda e: e.tensor_scalar(out=cols.t[:], in0=mods[:, 8:16], scalar1=1.0, scalar2=None, op0=ALU.add), r=[r_mods], w=[cols.r])
        dx = TR(C, "dx", [128, 8, GS], F32)
        xs = [TR(C, "xs%d" % i, [128, 8, GS], BF16) for i in range(2)]
        l1 = [TR(C, "l1_%d" % i, [128, GS], BF16) for i in range(3)]
        l1f = TR(C, "l1f", [128, GS], F32)
        G_ = {n: TR(C, "G_" + n, [128, NTG, 512], F32) for n in ("r", "k", "v")}
        names = ["sw", "Gm", "Gi", "Gp", "GC", "a", "kkn", "kp", "t1", "t2", "Bt", "Kt", "Bh", "Kh", "bonus", "Y", "yn", "gg", "At", "Rt", "glg", "glg2", "bonus2", "t3", "At2", "Rt2", "Kt2", "Bt2", "Bh2", "Kh2", "GC2", "V0", "V2"]
        X = {n: TR(C, "X_" + n, [128, 512], F32) for n in names if n not in ("Bh", "Kh", "Bh2", "Kh2")}
        for n in ("Bh", "Kh", "Bh2", "Kh2", "Vb0", "Vb2"):
            X[n] = TR(C, "X_" + n, [128, 512], BF16)
        Zfb = TR(C, "Zfb", [128, 8, 128], BF16)
        Arbb = TR(C, "Arbb", [128, 8, 64], BF16)
        sm = TR(C, "sm", [128, 64], F32)
        smT = TR(C, "smT", [128, 64], F32)
        Z = [TR(C, "Z%d" % i, [128, 8, 128], BF16) for i in range(2)]
        Zf32 = TR(C, "Zf32", [128, 8, 128], F32)
        ART = TR(C, "ART", [128, 4, 2, 128], F32)
        KT = TR(C, "KT", [128, 4, 128], F32)
        BT = TR(C, "BT", [128, 4, 128], F32)
        Arb = TR(C, "Arb", [128, 8, 128], BF16)
        Aak = TR(C, "Aak", [128, 8, 128], BF16)
        RbT = TR(C, "RbT", [128, 4, 128], F32)
        Qm = TR(C, "Qm", [128, 4, 128], F32)
        Hm = TR(C, "Hm", [128, 4, 128], F32)
        gcol = TR(C, "gcol", [128, 8], F32)
        Y0 = TR(C, "Y0", [128, 512], F32)
        ybf = TR(C, "ybf", [128, 512], BF16)
        yTo = [TR(C, "yTo%d" % i, [128, 4, 128], BF16) for i in range(2)]
        ysrc = yT_d.rearrange("(a p) t -> p a t", p=128)

        def hv(t):
            return t.rearrange("p (h d) -> p h d", d=64)

        def bc8(ap):
            return ap.unsqueeze(2).to_broadcast([128, 8, 64])

        sti = [0]

        def tile_front_a(i, G):
            tl_ = G * NTG + i
            r_ = G_["r"].t[:, i, :]; k_ = G_["k"].t[:, i, :]; v_ = G_["v"].t[:, i, :]
            rG = [G_[n].r for n in G_]
            sw, Gm, Gi, Gp, a_, kkn, kp, t1, t2 = [X[n] for n in ("sw", "Gm", "Gi", "Gp", "a", "kkn", "kp", "t1", "t2")]
            Yt, yn = X["Y"], X["yn"]
            sfx = "" if tl_ % 2 == 0 else "2"
            Bt, Kt, Bh, Kh, GC, At_, Rt = [X[n + sfx] if (n + sfx) in X else X[n] for n in ("Bt", "Kt", "Bh", "Kh", "GC", "At", "Rt")]
            Vt = X["V0"] if tl_ % 2 == 0 else X["V2"]
            Vb = X["Vb0"] if tl_ % 2 == 0 else X["Vb2"]
            bonus = X["bonus"] if tl_ % 2 == 0 else X["bonus2"]
            glg = X["glg"] if tl_ % 2 == 0 else X["glg2"]
            t3 = X["t3"]
            pzw, pza, pgg = PB[4], PB[5], PB[6]
            for (pz_, lt, w2n, rows) in ((pzw, l1[0], "w2", 64), (pza, l1[1], "a2", 64), (pgg, l1[2], "g2", 128)):
                S.op("pe", lambda e, pz_=pz_, lt=lt, w2n=w2n, rows=rows, i=i: e.matmul(pz_.t[:], lt.t[0:rows, i * 128:(i + 1) * 128], W[w2n][0:rows, :], start=True, stop=True), r=[lt.r, r_w], w=[pz_.r])
            S.op("act", lambda e: e.copy(out=X["gg"].t[:], in_=pgg.t[:]), r=[pgg.r], w=[X["gg"].r])
            S.op("dve", lambda e: e.tensor_tensor(out=t1.t[:], in0=pzw.t[:], in1=BC["w0b"][:], op=ALU.add), r=[pzw.r, r_bc], w=[t1.r])
            S.op("dve", lambda e: e.tensor_tensor(out=t2.t[:], in0=pza.t[:], in1=BC["a0b"][:], op=ALU.add), r=[pza.r, r_bc, t2.r], w=[t2.r])
            S.op("act", lambda e: e.activation(out=sw.t[:], in_=t1.t[:], func=AF.Sigmoid), r=[t1.r], w=[sw.r])
            S.op("act", lambda e: e.activation(out=a_.t[:], in_=t2.t[:], func=AF.Sigmoid), r=[t2.r], w=[a_.r])

        def tile_front_b(i, G):
            tl_ = G * NTG + i
            r_ = G_["r"].t[:, i, :]; k_ = G_["k"].t[:, i, :]; v_ = G_["v"].t[:, i, :]
            rG = [G_[n].r for n in G_]
            sw, Gm, Gi, Gp, a_, kkn, kp, t1, t2 = [X[n] for n in ("sw", "Gm", "Gi", "Gp", "a", "kkn", "kp", "t1", "t2")]
            Yt, yn = X["Y"], X["yn"]
            sfx = "" if tl_ % 2 == 0 else "2"
            Bt, Kt, Bh, Kh, GC, At_, Rt = [X[n + sfx] if (n + sfx) in X else X[n] for n in ("Bt", "Kt", "Bh", "Kh", "GC", "At", "Rt")]
            Vt = X["V0"] if tl_ % 2 == 0 else X["V2"]
            Vb = X["Vb0"] if tl_ % 2 == 0 else X["Vb2"]
            bonus = X["bonus"] if tl_ % 2 == 0 else X["bonus2"]
            glg = X["glg"] if tl_ % 2 == 0 else X["glg2"]
            t3 = X["t3"]
            pL, pLC = PB[2], PB[3]
            S.op("pe", lambda e: e.matmul(pL.t[:], triC.t[:], sw.t[:], start=True, stop=True), r=[triC.r, sw.r], w=[pL.r])
            S.op("pe", lambda e: e.matmul(pLC.t[:], blkC.t[:], sw.t[:], start=True, stop=True), r=[blkC.r, sw.r], w=[pLC.r])
            S.op("act", lambda e: e.activation(out=Gm.t[:], in_=pL.t[:], func=AF.Exp), r=[pL.r], w=[Gm.r])
            S.op("act", lambda e: e.activation(out=Gi.t[:], in_=pL.t[:], func=AF.Exp, scale=-1.0), r=[pL.r], w=[Gi.r])
            S.op("dve", lambda e: e.scalar_tensor_tensor(out=Gp.t[:], in0=sw.t[:], scalar=C0, in1=pL.t[:], op0=ALU.mult, op1=ALU.add), r=[sw.r, pL.r], w=[Gp.r])
            S.op("act", lambda e: e.activation(out=Gp.t[:], in_=Gp.t[:], func=AF.Exp), r=[Gp.r], w=[Gp.r])
            S.op("act", lambda e: e.activation(out=GC.t[:], in_=pLC.t[:], func=AF.Exp), r=[pLC.r], w=[GC.r])
            S.op("dve", lambda e, k_=k_: e.tensor_tensor(out=kkn.t[:], in0=k_, in1=BC["kkb"][:], op=ALU.mult), r=rG + [r_bc], w=[kkn.r])
            S.op("act", lambda e: e.activation(out=t2.t[:], in_=kkn.t[:], func=AF.Square), r=[kkn.r], w=[t2.r])
            S.op("dve", lambda e: e.tensor_reduce(out=sm.t[:, 0:8], in_=hv(t2.t[:]), axis=AX.X, op=ALU.add), r=[t2.r], w=[sm.r])
            S.op("dve", lambda e: e.tensor_scalar(out=sm.t[:, 0:8], in0=sm.t[:, 0:8], scalar1=1e-24, scalar2=None, op0=ALU.max), r=[sm.r], w=[sm.r])
            S.op("act", lambda e: e.activation(out=sm.t[:, 0:8], in_=sm.t[:, 0:8], func=AF.Ln), r=[sm.r], w=[sm.r])
            S.op("act", lambda e: e.activation(out=sm.t[:, 0:8], in_=sm.t[:, 0:8], func=AF.Exp, scale=-0.5), r=[sm.r], w=[sm.r])
            S.op("dve", lambda e: e.tensor_tensor(out=hv(kkn.t[:]), in0=hv(kkn.t[:]), in1=bc8(sm.t[:, 0:8]), op=ALU.mult), r=[kkn.r, sm.r], w=[kkn.r])
            S.op("dve", lambda e: e.scalar_tensor_tensor(out=t2.t[:], in0=a_.t[:], scalar=-1.0, in1=BC["kab"][:], op0=ALU.add, op1=ALU.mult), r=[a_.r, r_bc, t2.r], w=[t2.r])
            S.op("dve", lambda e, k_=k_: e.scalar_tensor_tensor(out=kp.t[:], in0=t2.t[:], scalar=1.0, in1=k_, op0=ALU.add, op1=ALU.mult), r=[t2.r] + rG, w=[kp.r])
            S.op("dve", lambda e: e.scalar_tensor_tensor(out=At_.t[:], in0=kkn.t[:], scalar=-1.0, in1=Gp.t[:], op0=ALU.mult, op1=ALU.mult), r=[kkn.r, Gp.r], w=[At_.r])
            S.op("pool", lambda e: e.tensor_tensor(out=Bt.t[:], in0=kkn.t[:], in1=a_.t[:], op=ALU.mult), r=[kkn.r, a_.r], w=[Bt.r])
            S.op("pool", lambda e: e.tensor_tensor(out=Bt.t[:], in0=Bt.t[:], in1=Gi.t[:], op=ALU.mult), r=[Bt.r, Gi.r], w=[Bt.r])
            S.op("dve", lambda e: e.tensor_tensor(out=Kt.t[:], in0=kp.t[:], in1=Gi.t[:], op=ALU.mult), r=[kp.r, Gi.r], w=[Kt.r])
            S.op("dve", lambda e, r_=r_: e.tensor_tensor(out=Rt.t[:], in0=r_, in1=Gm.t[:], op=ALU.mult), r=rG + [Gm.r], w=[Rt.r])
            S.op("pool", lambda e: e.tensor_tensor(out=Bh.t[:], in0=Bt.t[:], in1=GC.t[:], op=ALU.mult), r=[Bt.r, GC.r], w=[Bh.r])
            S.op("pool", lambda e: e.tensor_tensor(out=Kh.t[:], in0=Kt.t[:], in1=GC.t[:], op=ALU.mult), r=[Kt.r, GC.r], w=[Kh.r])
            S.op("dve", lambda e, r_=r_: e.tensor_tensor(out=t2.t[:], in0=r_, in1=kp.t[:], op=ALU.mult), r=rG + [kp.r, t2.r], w=[t2.r])
            S.op("dve", lambda e: e.tensor_tensor(out=t2.t[:], in0=t2.t[:], in1=BC["rkb"][:], op=ALU.mult), r=[t2.r, r_bc], w=[t2.r])
            S.op("dve", lambda e: e.tensor_reduce(out=sm.t[:, 8:16], in_=hv(t2.t[:]), axis=AX.X, op=ALU.add), r=[t2.r], w=[sm.r])
            S.op("dve", lambda e, v_=v_, bonus=bonus: e.tensor_tensor(out=hv(bonus.t[:]), in0=hv(v_), in1=bc8(sm.t[:, 8:16]), op=ALU.mult), r=rG + [sm.r], w=[bonus.r])
            S.op("pool", lambda e, bonus=bonus: e.tensor_tensor(out=bonus.t[:], in0=bonus.t[:], in1=BC["lnbb"][:], op=ALU.add), r=[bonus.r, r_bc], w=[bonus.r])
            S.op("pool", lambda e, bonus=bonus: e.tensor_tensor(out=bonus.t[:], in0=bonus.t[:], in1=X["gg"].t[:], op=ALU.mult), r=[bonus.r, X["gg"].r], w=[bonus.r])
            S.op("pool", lambda e, glg=glg: e.tensor_tensor(out=glg.t[:], in0=X["gg"].t[:], in1=BC["lngb"][:], op=ALU.mult), r=[X["gg"].r, r_bc], w=[glg.r])
            if stop == 2:
                raise _Stop()
            S.op("pool", lambda e: e.tensor_copy(out=Vb.t[:], in_=v_), r=rG, w=[Vb.r])

        def tile_back(i, G, hook=None):
            tl_ = G * NTG + i
            r_ = G_["r"].t[:, i, :]; k_ = G_["k"].t[:, i, :]; v_ = G_["v"].t[:, i, :]
            rG = [G_[n].r for n in G_]
            sw, Gm, Gi, Gp, a_, kkn, kp, t1, t2 = [X[n] for n in ("sw", "Gm", "Gi", "Gp", "a", "kkn", "kp", "t1", "t2")]
            Yt, yn = X["Y"], X["yn"]
            sfx = "" if tl_ % 2 == 0 else "2"
            Bt, Kt, Bh, Kh, GC, At_, Rt = [X[n + sfx] if (n + sfx) in X else X[n] for n in ("Bt", "Kt", "Bh", "Kh", "GC", "At", "Rt")]
            Vt = X["V0"] if tl_ % 2 == 0 else X["V2"]
            Vb = X["Vb0"] if tl_ % 2 == 0 else X["Vb2"]
            bonus = X["bonus"] if tl_ % 2 == 0 else X["bonus2"]
            glg = X["glg"] if tl_ % 2 == 0 else X["glg2"]
            t3 = X["t3"]
            v_ = Vb.t[:]
            rG = [Vb.r]
            S.op("pool", lambda e: e.tensor_copy(out=Z[0].t[:, :, 0:64], in_=hv(At_.t[:])), r=[At_.r], w=[Z[0].r])
            for p in range(4):
                pT = PB[p % 2]
                for q_, src in enumerate((At_, Rt, Kt, Bt)):
                    S.op("pe", lambda e, pT=pT, p=p, q_=q_, src=src: e.transpose(pT.t[:, q_ * 128:(q_ + 1) * 128], src.t[:, p * 128:(p + 1) * 128], ident[:]), r=[src.r, r_ident], w=[pT.r])
                S.op("dve", lambda e, pT=pT, p=p: e.tensor_copy(out=ART.t[:, p, :, 0:64], in_=pT.t[:, 0:128].rearrange("p (c t) -> p c t", t=64)), r=[pT.r], w=[ART.r])
                S.op("act", lambda e, pT=pT, p=p: e.copy(out=ART.t[:, p, :, 64:128], in_=pT.t[:, 128:256].rearrange("p (c t) -> p c t", t=64)), r=[pT.r], w=[ART.r])
                S.op("act", lambda e, pT=pT, p=p: e.copy(out=KT.t[:, p, :], in_=pT.t[:, 256:384]), r=[pT.r], w=[KT.r])
                S.op("dve", lambda e, pT=pT, p=p: e.tensor_copy(out=BT.t[:, p, :], in_=pT.t[:, 384:512]), r=[pT.r], w=[BT.r])
            if stop == 3:
                raise _Stop()
            pS1a, pS1b, pS2a, pS2b, pS3 = PB[0], PB[1], PB[2], PB[3], PB[4]
            for h in range(8):
                p, e_ = h // 2, h % 2
                fe = slice(e_ * 64, e_ * 64 + 64)
                pa, pb_ = (pS1a, pS2a) if h < 4 else (pS1b, pS2b)
                hc = (h % 4) * 128
                for ch in range(2):
                    tc_ = slice(ch * 64, ch * 64 + 64)
                    S.op("pe", lambda e, pa=pa, fe=fe, p=p, ch=ch, tc_=tc_, hc=hc: e.matmul(pa.t[tc_, hc:hc + 128], BT.t[fe, p, tc_], ART.t[fe, p, ch, :], start=True, stop=True), r=[BT.r, ART.r], w=[pa.r])
                    S.op("pe", lambda e, pb_=pb_, fe=fe, p=p, ch=ch, tc_=tc_, hc=hc: e.matmul(pb_.t[tc_, hc:hc + 128], KT.t[fe, p, tc_], ART.t[fe, p, ch, :], start=True, stop=True), r=[KT.r, ART.r], w=[pb_.r])
                    S.op("pe", lambda e, fe=fe, p=p, ch=ch, tc_=tc_, h=h: e.matmul(pS3.t[tc_, h * 64:(h + 1) * 64], ART.t[fe, p, ch, 0:64], BT.t[fe, p, tc_], start=True, stop=True), r=[BT.r, ART.r], w=[pS3.r])
            m1b = mask1.t[:].unsqueeze(1).to_broadcast([128, 4, 128])
            for half, (pa, pb_) in enumerate(((pS1a, pS2a), (pS1b, pS2b))):
                hs = slice(half * 4, half * 4 + 4)
                S.op("dve", lambda e, pa=pa, hs=hs: e.tensor_tensor(out=Arb.t[:, hs, :], in0=pa.t[:].rearrange("p (h t) -> p h t", t=128), in1=m1b, op=ALU.mult), r=[pa.r, mask1.r], w=[Arb.r])
                S.op("dve", lambda e, pb_=pb_, hs=hs: e.tensor_tensor(out=Aak.t[:, hs, :], in0=pb_.t[:].rearrange("p (h t) -> p h t", t=128), in1=m1b, op=ALU.mult), r=[pb_.r, mask1.r], w=[Aak.r])
            for ch in range(2):
                tc_ = slice(ch * 64, ch * 64 + 64)
                S.op("dve", lambda e, tc_=tc_: e.tensor_tensor(out=NTm[0].t[tc_, :, tc_], in0=hv(pS3.t[tc_, :]), in1=maskL.t[tc_, :].unsqueeze(1).to_broadcast([64, 8, 64]), op=ALU.mult), r=[pS3.r, maskL.r], w=[NTm[0].r])
                S.op("act", lambda e, tc_=tc_: e.copy(out=Nm[0].t[tc_, :, tc_], in_=Arb.t[tc_, :, 0:64]), r=[Arb.r], w=[Nm[0].r])
            if stop == 4:
                raise _Stop()
            if debug and tl_ == 0:
                tails.append(S.dma("sp", dbg["Arb"], Arb.t[:], r=[Arb.r], key="dbg_Arb"))
            if debug and tl_ == 0:
                tails.append(S.dma("sp", dbg["Aak"], Aak.t[:], r=[Aak.r], key="dbg_Aak"))
            if debug and tl_ == 0:
                tails.append(S.dma("sp", dbg["NT0"], NTm[0].t[:], r=[NTm[0].r], key="dbg_NT0"))
            if debug and tl_ == 0:
                tails.append(S.dma("sp", dbg["ART"], ART.t[:], r=[ART.r], key="dbg_ART"))
            if debug and tl_ == 0:
                tails.append(S.dma("sp", dbg["KT"], KT.t[:], r=[KT.r], key="dbg_KT"))
            if debug and tl_ == 0:
                tails.append(S.dma("sp", dbg["BT"], BT.t[:], r=[BT.r], key="dbg_BT"))
            if debug and tl_ == 0:
                tails.append(S.dma("sp", dbg["Bh"], Bh.t[:], r=[Bh.r], key="dbg_Bh"))
            if debug and tl_ == 0:
                tails.append(S.dma("sp", dbg["Kh"], Kh.t[:], r=[Kh.r], key="dbg_Kh"))
            if hook is not None:
                hook()
            pX = PB[5]
            for h in range(8):
                for ch in range(2):
                    tc_ = slice(ch * 64, ch * 64 + 64)
                    S.op("pe", lambda e, h=h, tc_=tc_, v_=v_: e.matmul(pX.t[tc_, h * 64:(h + 1) * 64], Aak.t[tc_, h, 0:64], v_[tc_, h * 64:(h + 1) * 64], start=True, stop=True), r=[Aak.r] + rG, w=[pX.r])
            S.op("act", lambda e: e.copy(out=Z[0].t[:, :, 64:128], in_=hv(pX.t[:])), r=[pX.r], w=[Z[0].r])
            if stop == 5:
                raise _Stop()
            if debug and tl_ == 0:
                tails.append(S.dma("sp", dbg["Z0"], Z[0].t[:], r=[Z[0].r], key="dbg_Z0"))
            zi = 0
            for lev in range(6):
                Nc, NTc = Nm[lev % 2], NTm[lev % 2]
                Nn, NTn = Nm[(lev + 1) % 2], NTm[(lev + 1) % 2]
                pZa, pZb = PB[0], PB[1]
                Zc, Zn = Z[zi], Z[1 - zi]
                if lev < 5:
                    pN, pNT = (PB[2], PB[3]), (PB[4], PB[5])
                    for h in range(8):
                        hb, hc = h // 4, (h % 4) * 128
                        S.op("pe", lambda e, NTc=NTc, Nc=Nc, h=h, hb=hb, hc=hc: e.matmul(pN[hb].t[:, hc:hc + 128], NTc.t[:, h, :], Nc.t[:, h, :], start=True, stop=True), r=[Nc.r, NTc.r], w=[pN[hb].r])
                        if lev < 4:
                            S.op("pe", lambda e, NTc=NTc, Nc=Nc, h=h, hb=hb, hc=hc: e.matmul(pNT[hb].t[:, hc:hc + 128], Nc.t[:, h, :], NTc.t[:, h, :], start=True, stop=True), r=[Nc.r, NTc.r], w=[pNT[hb].r])
                for h in range(8):
                    pz = pZa if h < 4 else pZb
                    hc = (h % 4) * 128
                    S.op("pe", lambda e, pz=pz, Nc=Nc, Zc=Zc, h=h, hc=hc: e.matmul(pz.t[:, hc:hc + 128], Nc.t[:, h, :], Zc.t[:, h, :], start=True, stop=True), r=[Nc.r, Zc.r], w=[pz.r])
                if lev == 5:
                    Zn = Zf32
                S.op("dve", lambda e, Zc=Zc, Zn=Zn: e.tensor_tensor(out=Zn.t[:, 0:4, :], in0=Zc.t[:, 0:4, :], in1=pZa.t[:].rearrange("p (h t) -> p h t", t=128), op=ALU.add), r=[Zc.r, pZa.r], w=[Zn.r])
                S.op("dve", lambda e, Zc=Zc, Zn=Zn: e.tensor_tensor(out=Zn.t[:, 4:8, :], in0=Zc.t[:, 4:8, :], in1=pZb.t[:].rearrange("p (h t) -> p h t", t=128), op=ALU.add), r=[Zc.r, pZb.r], w=[Zn.r])
                zi = 1 - zi
                if lev < 5:
                    for hb in range(2):
                        S.op("act", lambda e, Nn=Nn, hb=hb: e.copy(out=Nn.t[:, hb * 4:hb * 4 + 4, :], in_=pN[hb].t[:].rearrange("p (h t) -> p h t", t=128)), r=[pN[hb].r], w=[Nn.r])
                        if lev < 4:
                            S.op("act" if hb == 0 else "dve", (lambda e, NTn=NTn, hb=hb: e.copy(out=NTn.t[:, hb * 4:hb * 4 + 4, :], in_=pNT[hb].t[:].rearrange("p (h t) -> p h t", t=128))) if hb == 0 else (lambda e, NTn=NTn, hb=hb: e.tensor_copy(out=NTn.t[:, hb * 4:hb * 4 + 4, :], in_=pNT[hb].t[:].rearrange("p (h t) -> p h t", t=128))), r=[pNT[hb].r], w=[NTn.r])
            Zf = Zf32
            S.op("act", lambda e: e.copy(out=Zfb.t[:], in_=Zf32.t[:]), r=[Zf32.r], w=[Zfb.r])
            pY0 = PB[4]
            for h in range(8):
                for ch in range(2):
                    tc_ = slice(ch * 64, ch * 64 + 64)
                    S.op("pe", lambda e, h=h, tc_=tc_: e.matmul(pY0.t[tc_, h * 64:(h + 1) * 64], Arb.t[tc_, h, 64:128], Zfb.t[tc_, h, 64:128], start=True, stop=False), r=[Arb.r, Zfb.r], w=[pY0.r])
                    S.op("pe", lambda e, h=h, tc_=tc_, v_=v_: e.matmul(pY0.t[tc_, h * 64:(h + 1) * 64], Aak.t[tc_, h, 64:128], v_[tc_, h * 64:(h + 1) * 64], start=False, stop=True), r=[Aak.r] + rG, w=[pY0.r])
            S.op("act", lambda e: e.copy(out=Y0.t[:], in_=pY0.t[:]), r=[pY0.r], w=[Y0.r])
            if stop == 7:
                raise _Stop()
            if debug and tl_ == 0:
                tails.append(S.dma("sp", dbg["Zf"], Zf.t[:], r=[Zf.r], key="dbg_Zf"))
            if debug and tl_ == 0:
                tails.append(S.dma("sp", dbg["Y0"], Y0.t[:], r=[Y0.r], key="dbg_Y0"))
            pR, pQ, pH, pg = PB[5], PB[6], PB[0], PB[1]
            for h in range(8):
                p, e_ = h // 2, h % 2
                fe = slice(e_ * 64, e_ * 64 + 64)
                for ch in range(2):
                    tc_ = slice(ch * 64, ch * 64 + 64)
                    cs = slice(p * 128 + ch * 64, p * 128 + ch * 64 + 64)
                    hs_ = slice(h * 64, h * 64 + 64)
                    S.op("pe", lambda e, fe=fe, cs=cs, tc_=tc_, h=h: e.matmul(pR.t[fe, cs], Zfb.t[tc_, h, 0:64], Arb.t[tc_, h, 64:128], start=True, stop=True), r=[Zfb.r, Arb.r], w=[pR.r])
                    S.op("pe", lambda e, fe=fe, cs=cs, tc_=tc_, h=h, hs_=hs_: e.matmul(pQ.t[fe, cs], Zfb.t[tc_, h, 0:64], Bh.t[tc_, hs_], start=True, stop=True), r=[Zfb.r, Bh.r], w=[pQ.r])
                    S.op("pe", lambda e, fe=fe, cs=cs, tc_=tc_, h=h, hs_=hs_: e.matmul(pH.t[fe, cs], Bh.t[tc_, hs_], Zfb.t[tc_, h, 64:128], start=True, stop=False), r=[Zfb.r, Bh.r], w=[pH.r])
                    S.op("pe", lambda e, fe=fe, cs=cs, tc_=tc_, hs_=hs_: e.matmul(pH.t[fe, cs], Kh.t[tc_, hs_], Vb.t[tc_, hs_], start=False, stop=True), r=[Kh.r, Vb.r], w=[pH.r])
                    S.op("pe", lambda e, fe=fe, tc_=tc_, hs_=hs_, p=p, ch=ch: e.matmul(pg.t[fe, p * 2 + ch:p * 2 + ch + 1], GC.t[tc_, hs_], avg.t[tc_, :], start=True, stop=True), r=[GC.r, avg.r], w=[pg.r])
            for ch in range(2):
                S.op("dve", lambda e, ch=ch: e.tensor_tensor(out=RbT.t[:, :, ch * 64:(ch + 1) * 64], in0=pR.t[:].rearrange("p (a t) -> p a t", t=128)[:, :, ch * 64:(ch + 1) * 64], in1=ART.t[:, :, ch, 64:128], op=ALU.add), r=[pR.r, ART.r], w=[RbT.r])
            S.op("act", lambda e: e.copy(out=Qm.t[:], in_=pQ.t[:].rearrange("p (a t) -> p a t", t=128)), r=[pQ.r], w=[Qm.r])
            S.op("act", lambda e: e.copy(out=Hm.t[:], in_=pH.t[:].rearrange("p (a t) -> p a t", t=128)), r=[pH.r], w=[Hm.r])
            S.op("act", lambda e: e.copy(out=gcol.t[:], in_=pg.t[:, 0:8]), r=[pg.r], w=[gcol.r])
            if stop == 8:
                raise _Stop()
            if debug and tl_ == 0:
                tails.append(S.dma("sp", dbg["RbT"], RbT.t[:], r=[RbT.r], key="dbg_RbT"))
            if debug and tl_ == 0:
                tails.append(S.dma("sp", dbg["Qm"], Qm.t[:], r=[Qm.r], key="dbg_Qm"))
            if debug and tl_ == 0:
                tails.append(S.dma("sp", dbg["Hm"], Hm.t[:], r=[Hm.r], key="dbg_Hm"))
            if debug and tl_ == 0:
                tails.append(S.dma("sp", dbg["gcol"], gcol.t[:], r=[gcol.r], key="dbg_gcol"))
            for ch in range(2):
                tc_ = slice(ch * 64, ch * 64 + 64)
                Sc, Sn = ST[sti[0]], ST[1 - sti[0]]
                pYe, pS_ = (PB[2], PB[4]), PB[3]
                for h in range(8):
                    p, e_ = h // 2, h % 2
                    fe = slice(e_ * 64, e_ * 64 + 64)
                    cs = slice(ch * 64, ch * 64 + 64)
                    pY = pYe[e_]
                    S.op("pe", lambda e, fe=fe, p=p, cs=cs, tc_=tc_, h=h, Sc=Sc, pY=pY: e.matmul(pY.t[tc_, h * 64:(h + 1) * 64], RbT.t[fe, p, cs], Sc.t[fe, p, :], start=True, stop=True), r=[RbT.r, Sc.r], w=[pY.r])
                    S.op("pe", lambda e, fe=fe, p=p, cs=cs, Sc=Sc: e.matmul(pS_.t[fe, p * 64:(p + 1) * 64], Qm.t[fe, p, cs], Sc.t[fe, p, :], start=True, stop=True), r=[Qm.r, Sc.r], w=[pS_.r])
                v4 = lambda ap: ap.rearrange("p (a e d) -> p a e d", e=2, d=64)
                for e_ in range(2):
                    S.op("dve", lambda e, tc_=tc_, e_=e_: e.tensor_tensor(out=v4(Yt.t[tc_, :])[:, :, e_, :], in0=v4(Y0.t[tc_, :])[:, :, e_, :], in1=v4(pYe[e_].t[tc_, :])[:, :, e_, :], op=ALU.add), r=[Y0.r, pYe[e_].r], w=[Yt.r])
                for p in range(4):
                    S.op("dve", lambda e, p=p, ch=ch, Sc=Sc, Sn=Sn: e.scalar_tensor_tensor(out=Sn.t[:, p, :], in0=Sc.t[:, p, :], scalar=gcol.t[:, p * 2 + ch:p * 2 + ch + 1], in1=pS_.t[:, p * 64:(p + 1) * 64], op0=ALU.mult, op1=ALU.add), r=[Sc.r, gcol.r, pS_.r], w=[Sn.r])
                S.op("dve", lambda e, ch=ch, Sn=Sn: e.tensor_tensor(out=Sn.t[:], in0=Sn.t[:], in1=Hm.t[:, :, ch * 64:(ch + 1) * 64], op=ALU.add), r=[Sn.r, Hm.r], w=[Sn.r])
                sti[0] = 1 - sti[0]
            if stop == 9:
                raise _Stop()
            if debug and tl_ == 0:
                S.op("act", lambda e, r_=r_: e.copy(out=t1.t[:], in_=r_), r=rG + [t1.r], w=[t1.r])
                S.op("act", lambda e, v_=v_: e.copy(out=t2.t[:], in_=v_), r=rG + [t2.r], w=[t2.r])
                tails.append(S.dma("sp", dbg["v"], t2.t[:], r=[t2.r], key="dbg_v"))
                for n, src in (("r", t1), ("kp", kp), ("kkn", kkn), ("a", a_), ("sw", sw), ("Y", Yt), ("bonus", bonus), ("Gm", Gm), ("yn", yn)):
                    tails.append(S.dma("sp", dbg[n], src.t[:], r=[src.r], key="dbg_" + n))
                tails.append(S.dma("sp", dbg["S"], ST[sti[0]].t[:], r=[ST[sti[0]].r], key="dbg_S"))


        def tile_tail(i, G):
            tl_ = G * NTG + i
            r_ = G_["r"].t[:, i, :]; k_ = G_["k"].t[:, i, :]; v_ = G_["v"].t[:, i, :]
            rG = [G_[n].r for n in G_]
            sw, Gm, Gi, Gp, a_, kkn, kp, t1, t2 = [X[n] for n in ("sw", "Gm", "Gi", "Gp", "a", "kkn", "kp", "t1", "t2")]
            Yt, yn = X["Y"], X["yn"]
            sfx = "" if tl_ % 2 == 0 else "2"
            Bt, Kt, Bh, Kh, GC, At_, Rt = [X[n + sfx] if (n + sfx) in X else X[n] for n in ("Bt", "Kt", "Bh", "Kh", "GC", "At", "Rt")]
            Vt = X["V0"] if tl_ % 2 == 0 else X["V2"]
            Vb = X["Vb0"] if tl_ % 2 == 0 else X["Vb2"]
            bonus = X["bonus"] if tl_ % 2 == 0 else X["bonus2"]
            glg = X["glg"] if tl_ % 2 == 0 else X["glg2"]
            t3 = X["t3"]
            v_ = Vb.t[:]
            rG = [Vb.r]
            S.op("dve", lambda e: e.tensor_reduce(out=smT.t[:, 16:24], in_=hv(Yt.t[:]), axis=AX.X, op=ALU.add), r=[Yt.r], w=[smT.r])
            S.op("act", lambda e: e.activation(out=t3.t[:], in_=Yt.t[:], func=AF.Square), r=[Yt.r, t3.r], w=[t3.r])
            S.op("dve", lambda e: e.tensor_reduce(out=smT.t[:, 24:32], in_=hv(t3.t[:]), axis=AX.X, op=ALU.add), r=[t3.r], w=[smT.r])
            S.op("dve", lambda e: e.tensor_scalar(out=smT.t[:, 16:24], in0=smT.t[:, 16:24], scalar1=1.0 / 64, scalar2=None, op0=ALU.mult), r=[smT.r], w=[smT.r])
            S.op("dve", lambda e: e.tensor_tensor(out=smT.t[:, 32:40], in0=smT.t[:, 16:24], in1=smT.t[:, 16:24], op=ALU.mult), r=[smT.r], w=[smT.r])
            S.op("dve", lambda e: e.scalar_tensor_tensor(out=smT.t[:, 24:32], in0=smT.t[:, 24:32], scalar=1.0 / 64, in1=smT.t[:, 32:40], op0=ALU.mult, op1=ALU.subtract), r=[smT.r], w=[smT.r])
            S.op("dve", lambda e: e.tensor_scalar(out=smT.t[:, 24:32], in0=smT.t[:, 24:32], scalar1=GN_EPS, scalar2=None, op0=ALU.add), r=[smT.r], w=[smT.r])
            S.op("act", lambda e: e.activation(out=smT.t[:, 24:32], in_=smT.t[:, 24:32], func=AF.Ln), r=[smT.r], w=[smT.r])
            S.op("act", lambda e: e.activation(out=smT.t[:, 24:32], in_=smT.t[:, 24:32], func=AF.Exp, scale=-0.5), r=[smT.r], w=[smT.r])
            S.op("dve", lambda e: e.tensor_tensor(out=hv(yn.t[:]), in0=hv(Yt.t[:]), in1=bc8(smT.t[:, 16:24]), op=ALU.subtract), r=[Yt.r, smT.r], w=[yn.r])
            S.op("dve", lambda e: e.tensor_tensor(out=hv(yn.t[:]), in0=hv(yn.t[:]), in1=bc8(smT.t[:, 24:32]), op=ALU.mult), r=[yn.r, smT.r], w=[yn.r])
            S.op("dve", lambda e, glg=glg: e.tensor_tensor(out=yn.t[:], in0=yn.t[:], in1=glg.t[:], op=ALU.mult), r=[yn.r, glg.r], w=[yn.r])
            S.op("dve", lambda e, bonus=bonus: e.tensor_tensor(out=ybf.t[:], in0=yn.t[:], in1=bonus.t[:], op=ALU.add), r=[yn.r, bonus.r], w=[ybf.r])
            for p in range(4):
                S.op("pe", lambda e, p=p: e.transpose(pbb.t[:, p * 128:(p + 1) * 128], ybf.t[:, p * 128:(p + 1) * 128], identb.t[:]), r=[ybf.r, identb.r], w=[pbb.r])
            yo = yTo[tl_ % 2]
            S.op("act", lambda e, yo=yo: e.copy(out=yo.t[:], in_=pbb.t[:, 0:512].rearrange("p (a t) -> p a t", t=128)), r=[pbb.r], w=[yo.r])
            tails.append(S.dma("sp", ysrc[:, :, tl_ * 128:(tl_ + 1) * 128], yo.t[:], r=[yo.r], key="yTo%d" % (tl_ % 2)))

        pend = [None, None]
        try:
          for G in range(NGR):
              if G > 0:
                  S.op("dve", lambda e: e.tensor_copy(out=xg.t[:, :, 0:1], in_=xg.t[:, :, GS:GS + 1]), r=[xg.r], w=[xg.r])
              if before_x is not None and G == 0:
                  before_x(S)
              S.dma("sp", xg.t[:, :, 1:GS + 1], xsrc_fn(G * GS, GS), r=[xg.r], w=[xg.r], key="xg")
              for kc in range(8):
                  S.op("dve", lambda e, kc=kc: e.tensor_scalar(out=xg.t[:, kc, 1:GS + 1], in0=xg.t[:, kc, 1:GS + 1], scalar1=cols.t[:, kc:kc + 1], scalar2=mods[:, kc:kc + 1], op0=ALU.mult, op1=ALU.add), r=[xg.r, cols.r, r_mods], w=[xg.r])
              S.op("dve", lambda e: e.tensor_tensor(out=dx.t[:], in0=xg.t[:, :, 0:GS], in1=xg.t[:, :, 1:GS + 1], op=ALU.subtract), r=[xg.r], w=[dx.r])
              for n, nm in enumerate(("r", "k", "v", "w", "a", "g")):
                  xb = xs[n % 2]
                  for kc in range(8):
                      eng = "dve"
                      S.op(eng, lambda e, kc=kc, n=n, xb=xb: e.scalar_tensor_tensor(out=xb.t[:, kc, :], in0=dx.t[:, kc, :], scalar=mu.t[:, n * 8 + kc:n * 8 + kc + 1], in1=xg.t[:, kc, 1:GS + 1], op0=ALU.mult, op1=ALU.add), r=[dx.r, mu.r, xg.r], w=[xb.r])
                  if n < 3:
                      wt = W[("wr", "wk", "wv")[n]]
                      for i in range(NTG):
                          pb = PB[(n * NTG + i) % 2]
                          for kc in range(8):
                              S.op("pe", lambda e, pb=pb, xb=xb, wt=wt, i=i, kc=kc: e.matmul(pb.t[:], xb.t[:, kc, i * 128:(i + 1) * 128], wt[:, kc, :], start=(kc == 0), stop=(kc == 7)), r=[xb.r, r_w], w=[pb.r])
                          S.op("act", lambda e, pb=pb, nm=nm, i=i: e.copy(out=G_[nm].t[:, i, :], in_=pb.t[:]), r=[pb.r], w=[G_[nm].r])
                  else:
                      w1n, w2n, rows = (("w1", "w2", 64), ("a1", "a2", 64), ("g1", "g2", 128))[n - 3]
                      pb = PB[2]
                      for kc in range(8):
                          S.op("pe", lambda e, pb=pb, xb=xb, w1n=w1n, rows=rows, kc=kc: e.matmul(pb.t[0:rows, 0:GS], W[w1n][:, kc, :], xb.t[:, kc, :], start=(kc == 0), stop=(kc == 7)), r=[xb.r, r_w], w=[pb.r])
                      lt = l1[n - 3]
                      if nm == "w":
                          S.op("act", lambda e, pb=pb, lt=lt: e.activation(out=lt.t[0:64, :], in_=pb.t[0:64, 0:GS], func=AF.Tanh), r=[pb.r], w=[lt.r])
                      elif nm == "a":
                          S.op("act", lambda e, pb=pb, lt=lt: e.copy(out=lt.t[0:64, :], in_=pb.t[0:64, 0:GS]), r=[pb.r], w=[lt.r])
                      else:
                          S.op("act", lambda e, pb=pb, lt=lt: e.activation(out=lt.t[:], in_=pb.t[:, 0:GS], func=AF.Sigmoid), r=[pb.r], w=[lt.r])
              if stop == 1:
                  raise _Stop()
              for i in range(NTG):
                  tile_front_a(i, G)
                  if pend[0] is None:
                      tile_front_b(i, G)
                  else:
                      def hook(i=i, G=G, prev=pend[1]):
                          if prev is not None and not debug:
                              tile_tail(*prev)
                          tile_front_b(i, G)
                      tile_back(*pend[0], hook=hook)
                      if debug:
                          tile_tail(*pend[0])
                      pend[1] = pend[0]
                  pend[0] = (i, G)
          if pend[0] is not None:
              def hook_last(prev=pend[1]):
                  if prev is not None and not debug:
                      tile_tail(*prev)
              tile_back(*pend[0], hook=hook_last)
              tile_tail(*pend[0])
        except _Stop:
            pass
        if after is not None:
            after(C, tails)
        S.emit(st, tail_waits=tails)


PAIRS = [[0, 1], [2, 3], [4, 5], [6, 7]]
TL_KEYS = ("cT", "adaw", "adab", "lng", "lnb", "wo", "win", "wout")


def tl_decl(dt, sfx):
    return dict(cT=dt("cT" + sfx, [128, 8], F32), adaw=dt("adaw" + sfx, [D, 4 * D], F32), adab=dt("adab" + sfx, [128, 32], F32),
                lng=dt("lng" + sfx, [128, 16], F32), lnb=dt("lnb" + sfx, [128, 16], F32),
                wo=dt("wo" + sfx, [D, D], F32), win=dt("win" + sfx, [D, 2 * DFF], F32), wout=dt("wout" + sfx, [DFF, D], F32))


def build_fused(stop=99):
    nc = bass.Bass("TRN2", target_bir_lowering=False)
    dt = lambda n, s, d, k="ExternalInput": nc.dram_tensor(n, list(s), d, kind=k).ap()
    it = lambda n, s, d: nc.dram_tensor(n, list(s), d).ap()
    fox_in = fox_decl(dt, 4096, "_f")
    oT_d = dt("oT", [D, 2048], F32, "ExternalOutput")
    cin1, cout1 = it("cin1", [4, 128, 4096], BF16), it("cout1", [4, 2, 128, 4096], BF16)
    cin2, cout2 = it("cin2", [8, 128, 2048], F32), it("cout2", [8, 2, 128, 2048], F32)
    cin3, cout3 = it("cin3", [4, 128, 4096], BF16), it("cout3", [4, 2, 128, 4096], BF16)

    GS_ = {k: nc.alloc_semaphore(name="xchg_" + k) for k in ("g1", "g2", "g3")}

    def gather(cin, cout, key, nblk):
        def after(C, tails):
            prev = list(tails)
            for k in range(nblk):
                C.S.cc("AllGather", PAIRS, cin[k], cout[k].rearrange("r p t -> (r p) t"), key=key, extra=prev, ext_sem=GS_[key])
        return after

    def waiter(key, nblk):
        return lambda S: S.ext_wait("sp", GS_[key], nblk)

    cin1_v = cin1.rearrange("k p t -> (k p) t")
    cin3_v = cin3.rearrange("k p t -> (k p) t")
    cin2_v = cin2.rearrange("k p t -> (k p) t")
    y1_v = cout1
    y3_v = cout3
    fox_phase(nc, "f_", fox_in, cin1_v, 4096, after=gather(cin1, cout1, "g1", 4))
    if stop <= 1:
        return nc
    phase_end(nc)
    sel_d = dt("sel", [128, 2], F32)
    xTh_d = dt("xTh", [D, 2048], F32)
    t = tl_decl(dt, "_t0")
    tl_phase(nc, "t0_", y1_v, xTh_d, t["cT"], t["adaw"], t["adab"], t["lng"], t["lnb"], t["wo"], t["win"], t["wout"], cin2_v, 2048, 1024,
             sel_d=sel_d, after=gather(cin2, cout2, "g2", 8), before_y=waiter("g1", 4))
    if stop <= 2:
        return nc
    phase_end(nc)
    rw_in = rwkv_decl(dt, "_r")
    c2v = cout2.rearrange("kc r p t -> r p kc t")
    rwkv_phase(nc, "r_", lambda t0, n: c2v[t0 // 2048, :, :, (t0 % 2048):(t0 % 2048) + n], rw_in, cin3_v, 4096, after=gather(cin3, cout3, "g3", 4), before_x=waiter("g2", 8))
    if stop <= 3:
        return nc
    phase_end(nc)
    t = tl_decl(dt, "_t1")
    tl_phase(nc, "t1_", y3_v, cin2_v, t["cT"], t["adaw"], t["adab"], t["lng"], t["lnb"], t["wo"], t["win"], t["wout"], oT_d, 2048, 1024, sel_d=sel_d, before_y=waiter("g3", 4))
    return nc


NCORES = 8
_PROGS = {}


def _prog(name, fn):
    if name not in _PROGS:
        _PROGS[name] = fn()
    return _PROGS[name]


def _col_layout(v):
    return np.ascontiguousarray(np.asarray(v).reshape(-1, 128).T)


def _bc(v, n=512):
    v = np.asarray(v, dtype=np.float32).reshape(-1)
    return np.ascontiguousarray(np.broadcast_to(v[None, :], (128, v.shape[0])))


def _run(nc, in_maps):
    res = run_bass_kernel_spmd(nc, in_maps, core_ids=list(range(NCORES)))
    return res.results


def _tl_maps(yT_list, xT_list, c, ada_w_i, ada_b_i, ln_g_i, ln_b_i, wo, win, wout):
    maps = []
    adaw = np.ascontiguousarray(ada_w_i[:, 2 * D:])
    adab = _col_layout(ada_b_i[2 * D:])
    lng = _col_layout(ln_g_i.reshape(-1))
    lnb = _col_layout(ln_b_i.reshape(-1))
    for core in range(NCORES):
        b, th = core // 2, core % 2
        ts = slice(th * 2048, (th + 1) * 2048)
        maps.append({
            "yT": np.ascontiguousarray(yT_list[b][:, ts]),
            "xT": np.ascontiguousarray(xT_list[b][:, ts]),
            "cT": _col_layout(c[b]),
            "adaw": adaw, "adab": adab, "lng": lng, "lnb": lnb,
            "wo": wo, "win": win, "wout": wout,
        })
    return maps


def kernel_unfused(x, c, ada_w, ada_b, ln_g, ln_b, ffn_w_in, ffn_w_out,
           fox_w_in, fox_b_f, fox_q_g, fox_k_g, fox_w_o,
           rwkv_mu, rwkv_w_rkv, rwkv_w0, rwkv_w1, rwkv_w2, rwkv_a0, rwkv_a1, rwkv_a2,
           rwkv_g1, rwkv_g2, rwkv_k_k, rwkv_k_a, rwkv_r_k, rwkv_lnx_g, rwkv_lnx_b, rwkv_w_o):
    f = lambda a: np.ascontiguousarray(np.asarray(a, dtype=np.float32))
    x, c, ada_w, ada_b, ln_g, ln_b = f(x), f(c), f(ada_w), f(ada_b), f(ln_g), f(ln_b)
    ffn_w_in, ffn_w_out = f(ffn_w_in), f(ffn_w_out)
    B = x.shape[0]
    w_in = f(fox_w_in)[0]
    b_f, q_g, k_g = f(fox_b_f)[0], f(fox_q_g)[0], f(fox_k_g)[0]
    maps = []
    for core in range(NCORES):
        b, hh = core // 2, core % 2
        sl = slice(hh * 512, (hh + 1) * 512)
        maps.append({
            "x": x[b],
            "cT": _col_layout(c[b]),
            "adaw": np.ascontiguousarray(ada_w[0][:, 0:2 * D]),
            "adab": _col_layout(ada_b[0][0:2 * D]),
            "wq": np.ascontiguousarray(w_in[:, 0:D][:, sl]),
            "wk": np.ascontiguousarray(w_in[:, D:2 * D][:, sl]),
            "wv": np.ascontiguousarray(w_in[:, 2 * D:3 * D][:, sl]),
            "wf": np.ascontiguousarray(w_in[:, 3 * D + hh * 8:3 * D + hh * 8 + 8]),
            "wg": np.ascontiguousarray(w_in[:, 3 * D + 16:][:, sl]),
            "bfb": _bc(b_f[hh * 8:(hh + 1) * 8]),
            "qgb": _bc(np.tile(q_g, 8)),
            "kgb": _bc(np.tile(k_g, 8)),
        })
    r1 = _run(_prog("fox", build_fox), maps)
    yT = [np.concatenate([r1[2 * b]["ygT"], r1[2 * b + 1]["ygT"]], axis=0) for b in range(B)]
    xT = [np.ascontiguousarray(x[b].T) for b in range(B)]
    tlp = _prog("tl", build_tl)
    r2 = _run(tlp, _tl_maps(yT, xT, c, ada_w[0], ada_b[0], ln_g[0], ln_b[0], f(fox_w_o)[0], ffn_w_in[0], ffn_w_out[0]))
    x1T = [np.concatenate([r2[2 * b]["oT"], r2[2 * b + 1]["oT"]], axis=1) for b in range(B)]
    mu = f(rwkv_mu)[0]
    w_rkv = f(rwkv_w_rkv)[0]
    P = dict(w0=f(rwkv_w0)[0], w1=f(rwkv_w1)[0], w2=f(rwkv_w2)[0], a0=f(rwkv_a0)[0], a1=f(rwkv_a1)[0], a2=f(rwkv_a2)[0],
             g1=f(rwkv_g1)[0], g2=f(rwkv_g2)[0], k_k=f(rwkv_k_k)[0], k_a=f(rwkv_k_a)[0], r_k=f(rwkv_r_k)[0].reshape(-1),
             lnx_g=f(rwkv_lnx_g)[0], lnx_b=f(rwkv_lnx_b)[0])
    mu_l = np.concatenate([_col_layout(mu[n]) for n in range(6)], axis=1)
    maps = []
    for core in range(NCORES):
        b, hh = core // 2, core % 2
        sl = slice(hh * 512, (hh + 1) * 512)
        maps.append({
            "xT": x1T[b],
            "cT": _col_layout(c[b]),
            "adaw": np.ascontiguousarray(ada_w[1][:, 0:2 * D]),
            "adab": _col_layout(ada_b[1][0:2 * D]),
            "mu": mu_l,
            "wr": np.ascontiguousarray(w_rkv[0][:, sl]),
            "wk": np.ascontiguousarray(w_rkv[1][:, sl]),
            "wv": np.ascontiguousarray(w_rkv[2][:, sl]),
            "w1": P["w1"], "a1": P["a1"], "g1": P["g1"],
            "w2": np.ascontiguousarray(P["w2"][:, sl]),
            "a2": np.ascontiguousarray(P["a2"][:, sl]),
            "g2": np.ascontiguousarray(P["g2"][:, sl]),
            "w0b": _bc(P["w0"][sl]), "a0b": _bc(P["a0"][sl]), "kkb": _bc(P["k_k"][sl]), "kab": _bc(P["k_a"][sl]),
            "rkb": _bc(P["r_k"][sl]), "lngb": _bc(P["lnx_g"][sl]), "lnbb": _bc(P["lnx_b"][sl]),
        })
    r3 = _run(_prog("rwkv", build_rwkv), maps)
    y2T = [np.concatenate([r3[2 * b]["yT"], r3[2 * b + 1]["yT"]], axis=0) for b in range(B)]
    r4 = _run(tlp, _tl_maps(y2T, x1T, c, ada_w[1], ada_b[1], ln_g[1], ln_b[1], f(rwkv_w_o)[0], ffn_w_in[1], ffn_w_out[1]))
    out = np.empty(x.shape, np.float32)
    for core in range(NCORES):
        b, th = core // 2, core % 2
        out[b, th * 2048:(th + 1) * 2048, :] = r4[core]["oT"].T
    return out


def kernel(x, c, ada_w, ada_b, ln_g, ln_b, ffn_w_in, ffn_w_out,
           fox_w_in, fox_b_f, fox_q_g, fox_k_g, fox_w_o,
           rwkv_mu, rwkv_w_rkv, rwkv_w0, rwkv_w1, rwkv_w2, rwkv_a0, rwkv_a1, rwkv_a2,
           rwkv_g1, rwkv_g2, rwkv_k_k, rwkv_k_a, rwkv_r_k, rwkv_lnx_g, rwkv_lnx_b, rwkv_w_o):
    f = lambda a: np.ascontiguousarray(np.asarray(a, dtype=np.float32))
    x, c, ada_w, ada_b, ln_g, ln_b = f(x), f(c), f(ada_w), f(ada_b), f(ln_g), f(ln_b)
    ffn_w_in, ffn_w_out = f(ffn_w_in), f(ffn_w_out)
    w_in = f(fox_w_in)[0]
    b_f, q_g, k_g = f(fox_b_f)[0], f(fox_q_g)[0], f(fox_k_g)[0]
    mu = f(rwkv_mu)[0]
    w_rkv = f(rwkv_w_rkv)[0]
    P = dict(w0=f(rwkv_w0)[0], w1=f(rwkv_w1)[0], w2=f(rwkv_w2)[0], a0=f(rwkv_a0)[0], a1=f(rwkv_a1)[0], a2=f(rwkv_a2)[0],
             g1=f(rwkv_g1)[0], g2=f(rwkv_g2)[0], k_k=f(rwkv_k_k)[0], k_a=f(rwkv_k_a)[0], r_k=f(rwkv_r_k)[0].reshape(-1),
             lnx_g=f(rwkv_lnx_g)[0], lnx_b=f(rwkv_lnx_b)[0])
    mu_l = np.concatenate([_col_layout(mu[n]) for n in range(6)], axis=1)
    wos = [f(fox_w_o)[0], f(rwkv_w_o)[0]]
    tl_common = []
    for L in range(2):
        tl_common.append({
            "adaw_t%d" % L: np.ascontiguousarray(ada_w[L][:, 2 * D:]), "adab_t%d" % L: _col_layout(ada_b[L][2 * D:]),
            "lng_t%d" % L: _col_layout(ln_g[L].reshape(-1)), "lnb_t%d" % L: _col_layout(ln_b[L].reshape(-1)),
            "wo_t%d" % L: wos[L], "win_t%d" % L: ffn_w_in[L], "wout_t%d" % L: ffn_w_out[L]})
    adaw_f = np.ascontiguousarray(ada_w[0][:, 0:2 * D])
    adaw_r = np.ascontiguousarray(ada_w[1][:, 0:2 * D])
    maps = []
    for core in range(NCORES):
        b, j = core // 2, core % 2
        sl = slice(j * 512, (j + 1) * 512)
        cT = _col_layout(c[b])
        m = {
            "x_f": x[b], "cT_f": cT, "adaw_f": adaw_f, "adab_f": _col_layout(ada_b[0][0:2 * D]),
            "wq_f": np.ascontiguousarray(w_in[:, 0:D][:, sl]), "wk_f": np.ascontiguousarray(w_in[:, D:2 * D][:, sl]),
            "wv_f": np.ascontiguousarray(w_in[:, 2 * D:3 * D][:, sl]), "wf_f": np.ascontiguousarray(w_in[:, 3 * D + j * 8:3 * D + j * 8 + 8]),
            "wg_f": np.ascontiguousarray(w_in[:, 3 * D + 16:][:, sl]),
            "bfb_f": _bc(b_f[j * 8:(j + 1) * 8]), "qgb_f": _bc(np.tile(q_g, 8)), "kgb_f": _bc(np.tile(k_g, 8)),
            "cT_r": cT, "adaw_r": adaw_r, "adab_r": _col_layout(ada_b[1][0:2 * D]), "mu_r": mu_l,
            "wr_r": np.ascontiguousarray(w_rkv[0][:, sl]), "wk_r": np.ascontiguousarray(w_rkv[1][:, sl]), "wv_r": np.ascontiguousarray(w_rkv[2][:, sl]),
            "w1_r": P["w1"], "a1_r": P["a1"], "g1_r": P["g1"],
            "w2_r": np.ascontiguousarray(P["w2"][:, sl]), "a2_r": np.ascontiguousarray(P["a2"][:, sl]), "g2_r": np.ascontiguousarray(P["g2"][:, sl]),
            "w0b_r": _bc(P["w0"][sl]), "a0b_r": _bc(P["a0"][sl]), "kkb_r": _bc(P["k_k"][sl]), "kab_r": _bc(P["k_a"][sl]),
            "rkb_r": _bc(P["r_k"][sl]), "lngb_r": _bc(P["lnx_g"][sl]), "lnbb_r": _bc(P["lnx_b"][sl]),
            "cT_t0": cT, "cT_t1": cT,
            "sel": np.ascontiguousarray(np.broadcast_to(np.array([1.0 - j, float(j)], np.float32)[None, :], (128, 2))),
            "xTh": np.ascontiguousarray(x[b, j * 2048:(j + 1) * 2048, :].T),
        }
        m.update(tl_common[0])
        m.update(tl_common[1])
        maps.append(m)
    res = _run(_prog("fused", build_fused), maps)
    out = np.empty(x.shape, np.float32)
    for core in range(NCORES):
        b, j = core // 2, core % 2
        out[b, j * 2048:(j + 1) * 2048, :] = res[core]["oT"].T
    return out
```
